# Optimizing a Trainium2 kernel written in Bass

```python
import math
import jax
import jax.numpy as jnp
from jax import lax
import numpy as np

D_MODEL = 1024
BATCH = 16
SEQ = 2048
DEPTH = 2

CTX_LEN = 256
GRID_W = 64
D_MIX = D_MODEL
N_DIR = 2
N_MOD = 6
EPS = 1e-6
M_WIDTH = 512
M_HEADDIM = 64
M_HEADS = M_WIDTH // M_HEADDIM
M_GROUPS = 2
M_STATE = 128
M_CONV = 3
M_CHUNK = 128
M_XBC = M_WIDTH + 2 * M_GROUPS * M_STATE
M_COLS = M_WIDTH + M_XBC + N_DIR * M_HEADS
R_WIDTH = D_MIX - M_WIDTH
R_HEADDIM = 64
R_HEADS = R_WIDTH // R_HEADDIM
R_DECAY_LORA = 64
R_AAA_LORA = 64
R_GATE_LORA = 160
R_LN_EPS = 64e-5
R_DECAY_SCALE = 0.6065306597126334
R_COLS = 3 * R_WIDTH + N_DIR * (R_DECAY_LORA + R_AAA_LORA) + R_GATE_LORA
IN_COLS = M_COLS + R_COLS
D_FF = 2816
FF_CONV = 3

kernel_name = 'hybrid_ssd_rwkv7_convglu_dit_block'


def rms_norm(x, g):
    xf = x.astype(jnp.float32)
    y = xf * lax.rsqrt(jnp.mean(xf * xf, axis=-1, keepdims=True) + EPS)
    return y.astype(x.dtype) * g


def modulate(h, shift, scale):
    return h * (1 + scale) + shift


def flip(t):
    return jnp.flip(t, axis=1)


def dwconv1d(u, w, b):
    y = lax.conv_general_dilated(u, w[:, None, :].astype(u.dtype), (1,), 'SAME',
                                 dimension_numbers=('NWC', 'WIO', 'NWC'),
                                 feature_group_count=u.shape[-1])
    return y + b


def dwconv_grid(u, w, b):
    bsz, length, ch = u.shape
    rows = length // GRID_W
    img = u.reshape(bsz, rows, GRID_W, ch)
    y = lax.conv_general_dilated(img, w[:, :, None, :].astype(u.dtype), (1, 1), 'SAME',
                                 dimension_numbers=('NHWC', 'HWIO', 'NHWC'),
                                 feature_group_count=ch)
    return y.reshape(bsz, length, ch) + b


def centred_shift(u, mu):
    pad = jnp.pad(u, ((0, 0), (1, 1), (0, 0)))
    nb = 0.5 * (pad[:, :-2] + pad[:, 2:])
    return u + mu * (nb - u)


def ssd_chunked(xh, dt, A, Bh, Ch, h0):
    bsz, length, nh, hp = xh.shape
    nc = length // M_CHUNK

    def chunk(t):
        return t.reshape(bsz, nc, M_CHUNK, *t.shape[2:])

    x_c, dt_c, B_c, C_c = chunk(xh), chunk(dt), chunk(Bh), chunk(Ch)
    a_cum = jnp.cumsum(dt_c * A, axis=2)
    lower = jnp.tril(jnp.ones((M_CHUNK, M_CHUNK), dtype=bool))
    seg = a_cum[:, :, :, None, :] - a_cum[:, :, None, :, :]
    decay_ij = jnp.exp(jnp.where(lower[None, None, :, :, None], seg, -jnp.inf))
    scores = jnp.einsum('bcihn,bcjhn->bcijh', C_c, B_c) * decay_ij * dt_c[:, :, None]
    y_diag = jnp.einsum('bcijh,bcjhp->bcihp', scores, x_c)
    decay_end = jnp.exp(a_cum[:, :, -1:, :] - a_cum)
    states = jnp.einsum('bcjhn,bcjh,bcjhp->bchpn', B_c, decay_end * dt_c, x_c)
    chunk_decay = jnp.exp(a_cum[:, :, -1, :])

    def step(h, inp):
        st, dec = inp
        return h * dec[:, :, None, None] + st, h

    h_last, h_in = lax.scan(step, h0, (jnp.moveaxis(states, 1, 0), jnp.moveaxis(chunk_decay, 1, 0)))
    h_in = jnp.moveaxis(h_in, 0, 1)
    y_off = jnp.einsum('bcihn,bchpn->bcihp', C_c, h_in) * jnp.exp(a_cum)[..., None]
    return (y_diag + y_off).reshape(bsz, length, nh, hp), h_last


def wkv7(r, decay, k, v, kk, a, s0):
    def step(s, inp):
        r_t, w_t, k_t, v_t, kk_t, a_t = inp
        sa = jnp.einsum('bhvk,bhk->bhv', s, -kk_t)
        s = s * w_t[:, :, None, :] + sa[..., None] * (kk_t * a_t)[:, :, None, :] + v_t[..., None] * k_t[:, :, None, :]
        return s, jnp.einsum('bhvk,bhk->bhv', s, r_t)

    xs = tuple(jnp.moveaxis(t, 1, 0) for t in (r, decay, k, v, kk, a))
    s_last, o = lax.scan(step, s0, xs)
    return jnp.moveaxis(o, 0, 1), s_last


def mamba_group(z, xbc, dt_raw, conv_w, conv_b, dt_bias, a_log, d_skip, norm_w, h0):
    bsz, length, _ = z.shape
    f32 = jnp.float32
    xbc = jax.nn.silu(dwconv1d(xbc, conv_w, conv_b))
    xs, Bm, Cm = jnp.split(xbc, [M_WIDTH, M_WIDTH + M_GROUPS * M_STATE], axis=-1)
    rep = M_HEADS // M_GROUPS
    xh = xs.reshape(bsz, length, M_HEADS, M_HEADDIM).astype(f32)
    Bh = jnp.repeat(Bm.reshape(bsz, length, M_GROUPS, M_STATE), rep, axis=2).astype(f32)
    Ch = jnp.repeat(Cm.reshape(bsz, length, M_GROUPS, M_STATE), rep, axis=2).astype(f32)
    dt = jax.nn.softplus(dt_raw.reshape(bsz, length, N_DIR, M_HEADS).astype(f32) + dt_bias.astype(f32))
    A = -jnp.exp(a_log.astype(f32))
    y_f, h_f = ssd_chunked(xh, dt[:, :, 0], A[0], Bh, Ch, h0[0])
    y_b, h_b = ssd_chunked(flip(xh), flip(dt[:, :, 1]), A[1], flip(Bh), flip(Ch), h0[1])
    y = y_f + flip(y_b) + d_skip.astype(f32)[:, None] * xh
    y = y.reshape(bsz, length, M_WIDTH) * jax.nn.silu(z.astype(f32))
    yg = y.reshape(bsz, length, M_GROUPS, M_WIDTH // M_GROUPS)
    yg = yg * lax.rsqrt(jnp.mean(yg * yg, axis=-1, keepdims=True) + EPS)
    out = yg.reshape(bsz, length, M_WIDTH).astype(z.dtype) * norm_w
    return out, (h_f, h_b)


def rwkv_group(proj_r, mu, w0, w2, a0, a2, g2, k_k, k_a, r_k, ln_w, ln_b, s0):
    bsz, length, _ = proj_r.shape
    f32 = jnp.float32
    p = centred_shift(proj_r, mu)
    splits = [R_WIDTH, 2 * R_WIDTH, 3 * R_WIDTH, 3 * R_WIDTH + N_DIR * R_DECAY_LORA,
              3 * R_WIDTH + N_DIR * (R_DECAY_LORA + R_AAA_LORA)]
    r, k, v, xw, xa, xg = jnp.split(p, splits, axis=-1)
    xw = xw.reshape(bsz, length, N_DIR, R_DECAY_LORA)
    xa = xa.reshape(bsz, length, N_DIR, R_AAA_LORA)
    wz = w0 + jnp.einsum('bldr,drc->bldc', jnp.tanh(xw), w2)
    decay = jnp.exp(-R_DECAY_SCALE * jax.nn.sigmoid(wz.astype(f32)))
    a = jax.nn.sigmoid((a0 + jnp.einsum('bldr,drc->bldc', xa, a2)).astype(f32))
    g = jax.nn.sigmoid(xg) @ g2

    def heads(t):
        return t.reshape(*t.shape[:-1], R_HEADS, R_HEADDIM).astype(f32)

    kk = heads(k * k_k)
    kk = kk / jnp.maximum(jnp.sqrt(jnp.sum(kk * kk, axis=-1, keepdims=True)), 1e-12)
    kd = heads(k[:, :, None, :] * (1 + (a - 1) * k_a))
    rh, vh, dh, ah = heads(r), heads(v), heads(decay), heads(a)
    o_f, s_f = wkv7(rh, dh[:, :, 0], kd[:, :, 0], vh, kk, ah[:, :, 0], s0[0])
    o_b, s_b = wkv7(flip(rh), flip(dh[:, :, 1]), flip(kd[:, :, 1]), flip(vh), flip(kk), flip(ah[:, :, 1]), s0[1])
    o = o_f + flip(o_b)
    mean = jnp.mean(o, axis=-1, keepdims=True)
    var = jnp.mean(jnp.square(o - mean), axis=-1, keepdims=True)
    on = ((o - mean) * lax.rsqrt(var + R_LN_EPS)).reshape(bsz, length, R_WIDTH) * ln_w + ln_b
    bonus = jnp.sum(rh[:, :, None] * kd * r_k, axis=(2, 4))[..., None] * vh
    out = (on + bonus.reshape(bsz, length, R_WIDTH)) * g
    return out.astype(proj_r.dtype), (s_f, s_b)


def token_mixer(h, w_in, mamba_p, rwkv_p, init):
    proj = h @ w_in
    z, xbc, dt_raw, pr = jnp.split(proj, [M_WIDTH, M_WIDTH + M_XBC, M_COLS], axis=-1)
    ym, (hf, hb) = mamba_group(z, xbc, dt_raw, *mamba_p, (init[0], init[1]))
    yr, (sf, sb) = rwkv_group(pr, *rwkv_p, (init[2], init[3]))
    return jnp.concatenate([ym, yr.astype(ym.dtype)], axis=-1), (hf, hb, sf, sb)


def zero_states(bsz):
    hm = jnp.zeros((bsz, M_HEADS, M_HEADDIM, M_STATE), jnp.float32)
    sr = jnp.zeros((bsz, R_HEADS, R_HEADDIM, R_HEADDIM), jnp.float32)
    return (hm, hm, sr, sr)


def conv_glu(h, w_up, conv_w, conv_b, w_down, on_grid):
    gate, val = jnp.split(h @ w_up, 2, axis=-1)
    if on_grid:
        gate = dwconv_grid(gate, conv_w, conv_b)
    else:
        gate = dwconv1d(gate, conv_w[FF_CONV // 2], conv_b)
    return (jax.nn.gelu(gate) * val) @ w_down


def setup_inputs(seed: int = 0) -> dict:
    key = jax.random.key(seed)
    ks = iter(jax.random.split(key, 48))

    def nrm(shape, s):
        return jax.random.normal(next(ks), shape, jnp.float32) * s

    def unif(shape, lo, hi):
        return jax.random.uniform(next(ks), shape, jnp.float32, lo, hi)

    dt0 = jnp.exp(unif((DEPTH, N_DIR, M_HEADS), math.log(1e-3), math.log(1e-1)))
    dt_bias = dt0 + jnp.log(-jnp.expm1(-dt0))
    return {
        'x': nrm((BATCH, SEQ, D_MODEL), 1.0),
        'c': nrm((BATCH, D_MODEL), 1.0),
        'ctx': nrm((BATCH, CTX_LEN, D_MODEL), 1.0),
        'c_ctx': nrm((D_MODEL,), 1.0),
        'w_mod': nrm((DEPTH, D_MODEL, N_MOD * D_MODEL), 0.5 * D_MODEL ** -0.5),
        'b_mod': nrm((DEPTH, N_MOD * D_MODEL), 0.02),
        'g_mix_pre': 1.0 + nrm((DEPTH, D_MODEL), 0.02),
        'g_mix_post': 1.0 + nrm((DEPTH, D_MODEL), 0.02),
        'g_ffn_pre': 1.0 + nrm((DEPTH, D_MODEL), 0.02),
        'g_ffn_post': 1.0 + nrm((DEPTH, D_MODEL), 0.02),
        'w_in': nrm((DEPTH, D_MODEL, IN_COLS), D_MODEL ** -0.5),
        'w_out': nrm((DEPTH, D_MIX, D_MODEL), D_MIX ** -0.5),
        'm_conv_w': nrm((DEPTH, M_CONV, M_XBC), M_CONV ** -0.5),
        'm_conv_b': nrm((DEPTH, M_XBC), 0.02),
        'm_dt_bias': dt_bias,
        'm_a_log': jnp.log(unif((DEPTH, N_DIR, M_HEADS), 1.0, 16.0)),
        'm_d': 1.0 + nrm((DEPTH, M_HEADS), 0.02),
        'm_norm_w': 1.0 + nrm((DEPTH, M_WIDTH), 0.02),
        'r_mu': unif((DEPTH, R_COLS), 0.0, 1.0),
        'r_w0': unif((DEPTH, N_DIR, R_WIDTH), -6.0, -0.5),
        'r_w2': nrm((DEPTH, N_DIR, R_DECAY_LORA, R_WIDTH), 0.1 * R_DECAY_LORA ** -0.5),
        'r_a0': nrm((DEPTH, N_DIR, R_WIDTH), 0.1),
        'r_a2': nrm((DEPTH, N_DIR, R_AAA_LORA, R_WIDTH), 0.5 * R_AAA_LORA ** -0.5),
        'r_g2': nrm((DEPTH, R_GATE_LORA, R_WIDTH), R_GATE_LORA ** -0.5),
        'r_k_k': 0.85 + nrm((DEPTH, R_WIDTH), 0.02),
        'r_k_a': 1.0 + nrm((DEPTH, R_WIDTH), 0.02),
        'r_r_k': nrm((DEPTH, R_HEADS, R_HEADDIM), 0.1),
        'r_ln_w': 1.0 + nrm((DEPTH, R_WIDTH), 0.02),
        'r_ln_b': nrm((DEPTH, R_WIDTH), 0.02),
        'f_w_up': nrm((DEPTH, D_MODEL, 2 * D_FF), D_MODEL ** -0.5),
        'f_conv_w': nrm((DEPTH, FF_CONV, FF_CONV, D_FF), 1.0 / FF_CONV),
        'f_conv_b': nrm((DEPTH, D_FF), 0.02),
        'f_w_down': nrm((DEPTH, D_FF, D_MODEL), D_FF ** -0.5),
    }


def reference(x, c, ctx, c_ctx, w_mod, b_mod, g_mix_pre, g_mix_post, g_ffn_pre, g_ffn_post,
              w_in, w_out, m_conv_w, m_conv_b, m_dt_bias, m_a_log, m_d, m_norm_w,
              r_mu, r_w0, r_w2, r_a0, r_a2, r_g2, r_k_k, r_k_a, r_r_k, r_ln_w, r_ln_b,
              f_w_up, f_conv_w, f_conv_b, f_w_down):
    bsz = x.shape[0]
    for l in range(DEPTH):
        mamba_p = (m_conv_w[l], m_conv_b[l], m_dt_bias[l], m_a_log[l], m_d[l], m_norm_w[l])
        rwkv_p = (r_mu[l], r_w0[l], r_w2[l], r_a0[l], r_a2[l], r_g2[l], r_k_k[l], r_k_a[l],
                  r_r_k[l], r_ln_w[l], r_ln_b[l])
        mod_x = (jax.nn.silu(c) @ w_mod[l] + b_mod[l])[:, None, :]
        mod_c = jax.nn.silu(c_ctx) @ w_mod[l] + b_mod[l]
        sx1, cx1, gx1, sx2, cx2, gx2 = jnp.split(mod_x, N_MOD, axis=-1)
        sc1, cc1, gc1, sc2, cc2, gc2 = jnp.split(mod_c, N_MOD, axis=-1)
        hc = modulate(rms_norm(ctx, g_mix_pre[l]), sc1, cc1)
        mc, ctx_states = token_mixer(hc, w_in[l], mamba_p, rwkv_p, zero_states(bsz))
        hx = modulate(rms_norm(x, g_mix_pre[l]), sx1, cx1)
        mx, _ = token_mixer(hx, w_in[l], mamba_p, rwkv_p, ctx_states)
        x = x + gx1 * rms_norm(mx @ w_out[l], g_mix_post[l])
        hx = modulate(rms_norm(x, g_ffn_pre[l]), sx2, cx2)
        x = x + gx2 * rms_norm(conv_glu(hx, f_w_up[l], f_conv_w[l], f_conv_b[l], f_w_down[l], True), g_ffn_post[l])
        if l < DEPTH - 1:
            ctx = ctx + gc1 * rms_norm(mc @ w_out[l], g_mix_post[l])
            hc = modulate(rms_norm(ctx, g_ffn_pre[l]), sc2, cc2)
            ctx = ctx + gc2 * rms_norm(conv_glu(hc, f_w_up[l], f_conv_w[l], f_conv_b[l], f_w_down[l], False), g_ffn_post[l])
    return x
```

```python
import numpy as np
from contextlib import ExitStack
import concourse.bass as bass
import concourse.mybir as mybir
from concourse.bass_utils import run_bass_kernel_spmd

F32 = mybir.dt.float32
BF16 = mybir.dt.bfloat16
AF = mybir.ActivationFunctionType
ALU = mybir.AluOpType
AX = mybir.AxisListType

L = 2
D = 1024
TX = 2048
TC = 256
NBL = 2
NCORE = 8
CH = 128
DFF = 2816
NFF = 22
EPS = 1e-6
R_LN_EPS = 64e-5
R_DECAY_SCALE = 0.6065306597126334
GELU_C = 1.5957691216057308

PV_BMOD = 0
PV_GPRE1 = 48
PV_GPOST1 = 56
PV_GPRE2 = 64
PV_GPOST2 = 72
PV_MCW = 80
PV_MCB = 104
PV_FCW = 112
PV_FCB = 310
PV_MUXG0 = 332
PV_MUXG1 = 333
NPV = 334
P6_MURKV = 0
P6_MUWA = 24
P6_W0 = 28
P6_A0 = 44
P6_KK = 60
P6_KA = 68
P6_RK = 76
NPV64 = 84
RV_MNW = 0
RV_LNW = 512
RV_LNB = 1024
RV_MD = 1536
RV_DTB = 1544
RV_ALOG = 1560
NROW = 1576
C_ID = 0
C_UTI = 128
C_LTI = 256
C_UTS = 384
C_LTS = 512
C_ONE = 640
NCST = 768


class Em:
    def __init__(self, nc, ndma=8):
        self.nc = nc
        self.engs = {'pe': nc.tensor, 'act': nc.scalar, 'dve': nc.vector, 'pool': nc.gpsimd, 'sp': nc.sync}
        self.sem = {}
        self.cnt = {}
        for k in ['pe', 'act', 'dve', 'pool']:
            self.sem[k] = nc.alloc_semaphore("sem_" + k)
            self.cnt[k] = 0
        self.dq = {}
        for q in ['sp', 'pool', 'act']:
            self.dq[q] = {'n': ndma, 'next': 0}
            for i in range(ndma):
                self.sem[f"d_{q}_{i}"] = nc.alloc_semaphore(f"dsem_{q}_{i}")
                self.cnt[f"d_{q}_{i}"] = 0
        self.seen = {k: {} for k in self.engs}
        self.lastw = {}
        self.readers = {}
        self.n = 0

    def _deps(self, reads, writes):
        deps = {}

        def add(d):
            if d is None:
                return
            k, v = d
            if deps.get(k, 0) < v:
                deps[k] = v
        for b in reads:
            add(self.lastw.get(b))
        for b in writes:
            add(self.lastw.get(b))
            for r in self.readers.get(b, ()):
                add(r)
        return deps

    def _waits(self, eng, deps):
        for k, v in deps.items():
            if k.startswith('d_'):
                v = self.cnt[k]
            if self.seen[eng].get(k, 0) >= v:
                continue
            self.seen[eng][k] = v
            self.engs[eng].wait_ge(self.sem[k], v)
            self.n += 1

    def _mark(self, me, reads, writes):
        for b in reads:
            self.readers.setdefault(b, []).append(me)
        for b in writes:
            self.lastw[b] = me
            self.readers[b] = []

    @staticmethod
    def _is_psum(k):
        return (k[0] == 'B' and (k[1:].isdigit() or k == 'BB')) or k.startswith('ps')

    def op(self, eng, fn, reads=(), writes=()):
        ex = [k for k in reads if self._is_psum(k)]
        self._waits(eng, self._deps(reads, list(writes) + ex))
        self.cnt[eng] += 1
        fn(self.engs[eng]).then_inc(self.sem[eng], 1)
        self._mark((eng, self.cnt[eng]), reads, writes)
        self.n += 1

    def dma(self, q, out, in_, reads=(), writes=(), **kw):
        self._waits(q, self._deps(reads, writes))
        d = self.dq[q]
        i = d['next']
        d['next'] = (i + 1) % d['n']
        k = f"d_{q}_{i}"
        self.cnt[k] += 16
        self.engs[q].dma_start(out=out, in_=in_, **kw).then_inc(self.sem[k], 16)
        self._mark((k, self.cnt[k]), reads, writes)
        self.n += 1

    def barrier(self):
        allv = {k: v for k, v in self.cnt.items() if v > 0}
        for e in self.engs:
            self._waits(e, dict(allv))

    def act(self, out, in_, func, r, w, bias=None, scale=None, accum=None):
        kw = {}
        if bias is not None:
            kw['bias'] = bias
        if scale is not None:
            kw['scale'] = scale
        if accum is not None:
            kw['accum_out'] = accum
        self.op('act', lambda e: e.activation(out=out, in_=in_, func=func, **kw), r, w)

    def tt(self, eng, out, a, b, op, r, w):
        self.op(eng, lambda e: e.tensor_tensor(out=out, in0=a, in1=b, op=op), r, w)

    def ts(self, eng, out, a, s1, s2, op0, op1, r, w):
        if s2 is None:
            self.op(eng, lambda e: e.tensor_scalar(out=out, in0=a, scalar1=s1, scalar2=None, op0=op0), r, w)
        else:
            self.op(eng, lambda e: e.tensor_scalar(out=out, in0=a, scalar1=s1, scalar2=s2, op0=op0, op1=op1), r, w)

    def stt(self, out, a, s, b, op0, op1, r, w):
        self.op('dve', lambda e: e.scalar_tensor_tensor(out=out, in0=a, scalar=s, in1=b, op0=op0, op1=op1), r, w)

    def mm(self, out, lhsT, rhs, start, stop, r, w):
        self.op('pe', lambda e: e.matmul(out, lhsT=lhsT, rhs=rhs, start=start, stop=stop), r, w)

    def tr(self, out, in_, ident, r, w):
        self.op('pe', lambda e: e.transpose(out, in_, ident), r, w)

    def cp(self, eng, out, in_, r, w):
        if eng == 'act':
            self.op('act', lambda e: e.activation(out=out, in_=in_, func=AF.Identity), r, w)
        else:
            self.op(eng, lambda e: e.tensor_copy(out=out, in_=in_), r, w)

    def memset(self, eng, ap, val, w):
        self.op(eng, lambda e: e.memset(ap, val), (), w)


def seqT(s):
    return TX if (s % 2) == 1 else TC


def build(debug=False, n_layers=L, stop_after=None, mixcfg=None):
    nc = bass.Bass("TRN2", target_bir_lowering=False)
    em = Em(nc)
    dbgset = debug if isinstance(debug, (set, list, tuple)) else None

    def din(name, shape, dt=F32):
        return nc.dram_tensor(name, list(shape), dt, kind="ExternalInput").ap()

    def dscr(name, shape, dt=F32):
        isdbg = (debug is True) or (dbgset is not None and name.rstrip('0123456789') in dbgset)
        return nc.dram_tensor(name, list(shape), dt, kind="ExternalOutput" if isdbg else "Internal").ap()

    xT = din("xT", [NBL, D, TX])
    ctxT = din("ctxT", [NBL, D, TC])
    cT = din("cT", [D, 3])
    w_mod = din("w_mod", [L, D, 6 * D])
    w_in = din("w_in", [L, D, 3504])
    w_out = din("w_out", [L, D, D])
    r_w2 = din("r_w2", [L, 2, 64, 512])
    r_a2 = din("r_a2", [L, 2, 64, 512])
    r_g2 = din("r_g2", [L, 160, 512])
    f_w_up = din("f_w_up", [L, D, 2 * DFF])
    f_w_down = din("f_w_down", [L, DFF, D])
    cst = din("cst", [128, NCST])
    pv = din("pv", [L, 128, NPV])
    pv64 = din("pv64", [L, 64, NPV64])
    rowv = din("rowv", [L, 1, NROW])
    outT = nc.dram_tensor("outT", [NBL, D, TX], F32, kind="ExternalOutput").ap()

    NS = 2 * NBL
    RESA = [dscr(f"resa{s}", [D, seqT(s)]) for s in range(NS)]
    RESB = [dscr(f"resb{s}", [D, seqT(s)]) for s in range(NS)]
    XBC = [dscr(f"xbc{s}", [D, seqT(s)]) for s in range(NS)]
    RKV = [dscr(f"rkv{s}", [24, 64, seqT(s)]) for s in range(NS)]
    XWA = [dscr(f"xwa{s}", [4, 64, seqT(s)]) for s in range(NS)]
    XG = [dscr(f"xg{s}", [160, seqT(s)]) for s in range(NS)]
    ZDT = [dscr(f"zdt{s}", [seqT(s), 528]) for s in range(NS)]
    YF = [dscr(f"yf{s}", [seqT(s), 512]) for s in range(NS)]
    OF = [dscr(f"of{s}", [seqT(s), 520]) for s in range(NS)]
    MIX = [dscr(f"mix{s}", [D, seqT(s)], BF16) for s in range(NS)]
    GATE = [dscr(f"gate{s}", [DFF, seqT(s)]) for s in range(NS)]
    VAL = [dscr(f"val{s}", [DFF, seqT(s)], BF16) for s in range(NS)]
    ACTV = [dscr(f"actv{s}", [DFF, seqT(s)], BF16) for s in range(NS)]

    def fm(ap):
        return ap.rearrange("(k p) t -> p k t", p=128)

    def sb(name, shape, dt=F32):
        return nc.alloc_sbuf_tensor(name, list(shape), dt).ap()

    CST = sb("CST", [128, NCST])
    CSTB = sb("CSTB", [128, NCST], BF16)
    PV = sb("PV", [128, NPV])
    PV64 = sb("PV64", [64, NPV64])
    ROWB = sb("ROWB", [128, NROW])
    MOD = sb("MOD", [128, 48, 3])
    DER = sb("DER", [128, 4, 8, 3])
    NEGA = sb("NEGA", [128, 16])
    OMMU = sb("OMMU", [64, 8])
    em.dma('sp', CST, cst, writes=['CST'])
    em.cp('dve', CSTB, CST, ['CST'], ['CSTB'])
    ident = CST[:, C_ID:C_ID + 128]
    identb = CSTB[:, C_ID:C_ID + 128]
    onesb = CSTB[:, C_ONE:C_ONE + 128]
    ones = CST[:, C_ONE:C_ONE + 128]

    def stage_scope():
        return ExitStack()

    def stage_mod(l):
        em.dma('sp', PV, pv[l], writes=['PV'])
        em.dma('sp', PV64, pv64[l], writes=['PV64'])
        em.dma('pool', ROWB, rowv[l].partition_broadcast(128), writes=['ROWB'])
        with ExitStack() as es:
            def S(name, shape, dt=F32):
                return es.enter_context(nc.sbuf_tensor(name, list(shape), dt)).ap()
            cts = S(f"cts{l}", [128, 8, 3])
            sc = S(f"sc{l}", [128, 8, 3])
            wst = [S(f"wmst{l}_{i}", [128, 8, 512]) for i in range(2)]
            ps = es.enter_context(nc.psum_tensor(f"psmod{l}", [128, 512], F32)).ap()
            em.dma('sp', cts, cT.rearrange("(k p) j -> p k j", p=128), writes=['cts'])
            em.act(sc, cts, AF.Silu, ['cts'], ['sc'])
            for g in range(12):
                w = wst[g % 2]
                wk = f"wmst{g % 2}"
                em.dma('sp' if g % 2 == 0 else 'pool', w,
                       w_mod[l][:, g * 512:(g + 1) * 512].rearrange("(k p) n -> p k n", p=128), writes=[wk])
                for mi in range(4):
                    m = g * 4 + mi
                    for k in range(8):
                        em.mm(ps[:, m * 3:(m + 1) * 3], w[:, k, mi * 128:(mi + 1) * 128], sc[:, k, :],
                              k == 0, k == 7, [wk, 'sc'], ['psmod'])
            em.tt('dve', MOD, ps[:, 0:144].rearrange("p (m j) -> p m j", j=3),
                  PV[:, PV_BMOD:PV_BMOD + 48].unsqueeze(2).to_broadcast([128, 48, 3]), ALU.add,
                  ['psmod', 'PV'], ['MOD'])
            tmp = S(f"dertmp{l}", [128, 8, 3])

            def gain(idx, goff, mlo, plus1):
                if plus1:
                    em.ts('dve', tmp, MOD[:, mlo:mlo + 8, :], 1.0, None, ALU.add, None, ['MOD'], ['dertmp'])
                    src = tmp
                    rk = ['dertmp', 'PV']
                else:
                    src = MOD[:, mlo:mlo + 8, :]
                    rk = ['MOD', 'PV']
                em.tt('dve', DER[:, idx, :, :], src,
                      PV[:, goff:goff + 8].unsqueeze(2).to_broadcast([128, 8, 3]), ALU.mult, rk, ['DER'])
            gain(0, PV_GPRE1, 8, True)
            gain(1, PV_GPOST1, 16, False)
            gain(2, PV_GPRE2, 32, True)
            gain(3, PV_GPOST2, 40, False)
            em.act(NEGA, ROWB[:, RV_ALOG:RV_ALOG + 16], AF.Exp, ['ROWB'], ['NEGA'])
            em.ts('dve', NEGA, NEGA, -1.0, None, ALU.mult, None, ['NEGA'], ['NEGA'])
            em.ts('dve', OMMU, PV64[:, P6_KA:P6_KA + 8], -1.0, 1.0, ALU.mult, ALU.add, ['PV64'], ['OMMU'])
            em.barrier()

    def prenorm(xt, TW, h, sq, rstd, ps, jmod, gidx, sidx, kx, kh, kps):
        em.act(sq[:, :, :TW], xt[:, :, :TW], AF.Square, [kx], ['sq'])
        for k in range(8):
            em.mm(ps[:, :TW], onesb, sq[:, k, :TW], k == 0, k == 7, ['sq', 'CSTB'], [kps])
        em.act(rstd[:, :TW], ps[:, :TW], AF.Sqrt, [kps], ['rstd'], bias=EPS, scale=1.0 / D)
        em.op('dve', lambda e: e.reciprocal(out=rstd[:, :TW], in_=rstd[:, :TW]), ['rstd'], ['rstd'])
        for k in range(8):
            em.stt(xt[:, k, :TW], xt[:, k, :TW], DER[:, gidx, k, jmod:jmod + 1], rstd[:, :TW], ALU.mult, ALU.mult,
                   [kx, 'rstd', 'DER'], [kx])
            em.act(h[:, k, :TW], xt[:, k, :TW], AF.Identity, [kx, 'MOD'], [kh],
                   bias=MOD[:, sidx + k, jmod:jmod + 1], scale=1.0)

    def load_wbf(es, l, wap, Kc, N, name, piece=None):
        wb = es.enter_context(nc.sbuf_tensor(f"{name}bf{l}", [128, Kc, N], BF16)).ap()
        with ExitStack() as e2:
            sts = [e2.enter_context(nc.sbuf_tensor(f"{name}st{l}_{i}", [128, N], F32)).ap() for i in range(2)]
            for k in range(Kc):
                st = sts[k % 2]
                sk = f"{name}st{k % 2}"
                em.dma('sp' if k % 2 == 0 else 'pool', st, wap[k * 128:(k + 1) * 128, :], writes=[sk])
                em.cp('act' if k % 2 == 0 else 'dve', wb[:, k, :], st, [sk], [name + 'bf'])
            em.barrier()
        return wb

    def res_src(l, s, phase):
        b = s // 2
        if phase == 0:
            if l == 0:
                return (xT[b] if s % 2 == 1 else ctxT[b]), f"in{s}"
            return RESB[s], f"resb{s}"
        return RESA[s], f"resa{s}"

    def res_dst(l, s, phase):
        b = s // 2
        if phase == 0:
            return RESA[s], f"resa{s}"
        if l == n_layers - 1 and s % 2 == 1:
            return outT[b], f"out{s}"
        return RESB[s], f"resb{s}"

    def stage_inproj(l):
        with ExitStack() as es:
            def S(name, shape, dt=F32):
                return es.enter_context(nc.sbuf_tensor(name, list(shape), dt)).ap()
            wb = load_wbf(es, l, w_in[l], 8, 3504, "win")
            xt = S(f"ipx{l}", [128, 8, 512])
            h = S(f"iph{l}", [128, 8, 512], BF16)
            sq = S(f"ipsq{l}", [128, 8, 512], BF16)
            rstd = S(f"iprs{l}", [128, 512])
            sta = [S(f"ipsta{l}_{i}", [128, 8, 512]) for i in range(2)]
            stw = S(f"ipstw{l}", [64, 4, 512])
            stg0 = S(f"ipstg0{l}", [128, 512])
            stg1 = S(f"ipstg1{l}", [32, 512])
            stz = S(f"ipstz{l}", [128, 4, 528])
            pss = [es.enter_context(nc.psum_tensor(f"ipps{l}_{i}", [128, 512], F32)).ap() for i in range(8)]
            groups = [('xbc', 512, 128, 8), ('r', 1552, 64, 8), ('k', 2064, 64, 8), ('v', 2576, 64, 8)]
            ev = 0
            for s in range(NS):
                T = seqT(s)
                TW = min(512, T)
                jmod = 2 if s % 2 == 0 else s // 2
                src, srck = res_src(l, s, 0)
                for tt_ in range(T // TW):
                    t0 = tt_ * TW
                    em.dma('sp', xt[:, :, :TW], fm(src)[:, :, t0:t0 + TW], reads=[srck], writes=['ipx'])
                    prenorm(xt, TW, h, sq, rstd, pss[7], jmod, 0, 0, 'ipx', 'iph', 'ps7')
                    pi = 0
                    for gi, (gname, c0, wdt, nb) in enumerate(groups):
                        st = sta[gi % 2]
                        stk = f"ipsta{gi % 2}"
                        for j in range(nb):
                            ps = pss[pi % 6]
                            pk = f"ps{pi % 6}"
                            pi += 1
                            cc = c0 + j * wdt
                            for k in range(8):
                                em.mm(ps[:wdt, :TW], wb[:, k, cc:cc + wdt], h[:, k, :TW], k == 0, k == 7,
                                      ['winbf', 'iph'], [pk])
                            em.cp('act' if ev % 2 == 0 else 'dve', st[:wdt, j, :TW], ps[:wdt, :TW], [pk], [stk])
                            ev += 1
                        if gname == 'xbc':
                            em.dma('pool', fm(XBC[s])[:, :, t0:t0 + TW], st[:, :, :TW], reads=[stk], writes=[f"xbc{s}"])
                        else:
                            jb = {'r': 0, 'k': 8, 'v': 16}[gname]
                            em.dma('pool', RKV[s][jb:jb + 8].rearrange("j p t -> p j t")[:, :, t0:t0 + TW],
                                   st[:64, :, :TW], reads=[stk], writes=[f"rkv{s}"])
                    for j in range(4):
                        ps = pss[pi % 6]
                        pk = f"ps{pi % 6}"
                        pi += 1
                        cc = 3088 + j * 64
                        for k in range(8):
                            em.mm(ps[:64, :TW], wb[:, k, cc:cc + 64], h[:, k, :TW], k == 0, k == 7, ['winbf', 'iph'], [pk])
                        em.cp('act' if ev % 2 == 0 else 'dve', stw[:, j, :TW], ps[:64, :TW], [pk], ['ipstw'])
                        ev += 1
                    em.dma('pool', XWA[s].rearrange("j p t -> p j t")[:, :, t0:t0 + TW], stw[:, :, :TW],
                           reads=['ipstw'], writes=[f"xwa{s}"])
                    for (cc, wdt, st, stk, r0) in [(3344, 128, stg0, 'ipstg0', 0), (3472, 32, stg1, 'ipstg1', 128)]:
                        ps = pss[pi % 6]
                        pk = f"ps{pi % 6}"
                        pi += 1
                        for k in range(8):
                            em.mm(ps[:wdt, :TW], wb[:, k, cc:cc + wdt], h[:, k, :TW], k == 0, k == 7, ['winbf', 'iph'], [pk])
                        em.cp('act' if ev % 2 == 0 else 'dve', st[:wdt, :TW], ps[:wdt, :TW], [pk], [stk])
                        ev += 1
                        em.dma('pool', XG[s][r0:r0 + wdt, t0:t0 + TW], st[:wdt, :TW], reads=[stk], writes=[f"xg{s}"])
                    for i in range(TW // 128):
                        ps = pss[pi % 6]
                        pk = f"ps{pi % 6}"
                        pi += 1
                        ps2 = pss[6]
                        for k in range(8):
                            em.mm(ps[:, 0:512], h[:, k, i * 128:(i + 1) * 128], wb[:, k, 0:512], k == 0, k == 7,
                                  ['winbf', 'iph'], [pk])
                        for k in range(8):
                            em.mm(ps2[:, 0:16], h[:, k, i * 128:(i + 1) * 128], wb[:, k, 1536:1552], k == 0, k == 7,
                                  ['winbf', 'iph'], ['ps6'])
                        em.cp('act', stz[:, i, 0:512], ps[:, 0:512], [pk], ['ipstz'])
                        em.cp('dve', stz[:, i, 512:528], ps2[:, 0:16], ['ps6'], ['ipstz'])
                    em.dma('pool', ZDT[s][t0:t0 + TW, :].rearrange("(i p) c -> p i c", p=128), stz[:, :TW // 128, :],
                           reads=['ipstz'], writes=[f"zdt{s}"])
            em.barrier()

    def stage_mixer(l, need_ctx_out):
        with ExitStack() as es:
            def S(name, shape, dt=F32):
                return es.enter_context(nc.sbuf_tensor(name, list(shape), dt)).ap()
            w2b = S(f"w2b{l}", [64, 2, 512], BF16)
            a2b = S(f"a2b{l}", [64, 2, 512], BF16)
            g2b0 = S(f"g2b0{l}", [128, 512], BF16)
            g2b1 = S(f"g2b1{l}", [32, 512], BF16)
            with ExitStack() as e2:
                t1 = e2.enter_context(nc.sbuf_tensor(f"lst1{l}", [64, 2, 512], F32)).ap()
                t2 = e2.enter_context(nc.sbuf_tensor(f"lst2{l}", [64, 2, 512], F32)).ap()
                t3 = e2.enter_context(nc.sbuf_tensor(f"lst3{l}", [128, 512], F32)).ap()
                t4 = e2.enter_context(nc.sbuf_tensor(f"lst4{l}", [32, 512], F32)).ap()
                em.dma('sp', t1, r_w2[l].rearrange("d r c -> r d c"), writes=['lst1'])
                em.dma('sp', t2, r_a2[l].rearrange("d r c -> r d c"), writes=['lst2'])
                em.dma('sp', t3, r_g2[l][0:128, :], writes=['lst3'])
                em.dma('sp', t4, r_g2[l][128:160, :], writes=['lst4'])
                em.cp('dve', w2b, t1, ['lst1'], ['w2b'])
                em.cp('dve', a2b, t2, ['lst2'], ['a2b'])
                em.cp('dve', g2b0, t3, ['lst3'], ['g2b'])
                em.cp('dve', g2b1, t4, ['lst4'], ['g2b'])
                em.barrier()
            MAR = [None, None]
            MARt = S(f"mar{l}", [128, 2, 256])
            em.cp('dve', MARt[:, 0, 0:128], CST[:, C_UTS:C_UTS + 128], ['CST'], ['MAR'])
            em.cp('dve', MARt[:, 0, 128:256], CST[:, C_UTI:C_UTI + 128], ['CST'], ['MAR'])
            em.cp('dve', MARt[:, 1, 0:128], CST[:, C_LTS:C_LTS + 128], ['CST'], ['MAR'])
            em.cp('dve', MARt[:, 1, 128:256], CST[:, C_LTI:C_LTI + 128], ['CST'], ['MAR'])
            MSO = S(f"mso{l}", [128, 2, 256])
            em.cp('dve', MSO[:, 0, 0:128], CST[:, C_LTS:C_LTS + 128], ['CST'], ['MSO'])
            em.cp('dve', MSO[:, 1, 0:128], CST[:, C_UTS:C_UTS + 128], ['CST'], ['MSO'])
            em.cp('dve', MSO[:, 0, 128:256], ones, ['CST'], ['MSO'])
            em.cp('dve', MSO[:, 1, 128:256], ones, ['CST'], ['MSO'])
            RMK = S(f"rmk{l}", [64, 8, 128])
            em.memset('pool', RMK, 1.0, ['RMK'])
            em.memset('pool', RMK[:, :, 0:1], 0.0, ['RMK'])

            def MI(d):
                return CST[:, C_UTI:C_UTI + 128] if d == 0 else CST[:, C_LTI:C_LTI + 128]

            def strictTS(d):
                return CST[:, C_LTS:C_LTS + 128] if d == 0 else CST[:, C_UTS:C_UTS + 128]

            xbc = S(f"m_xbc{l}", [128, 8, 130])
            cacc = S(f"m_cacc{l}", [128, 8, 128])
            bct = S(f"m_bct{l}", [128, 4, 128], BF16)
            xtok = S(f"m_xtok{l}", [128, 512])
            xtokb = S(f"m_xtokb{l}", [128, 512], BF16)
            btokb = S(f"m_btokb{l}", [128, 256], BF16)
            zdt = S(f"m_zdt{l}", [128, 528])
            dts = S(f"m_dts{l}", [128, 8])
            dta = S(f"m_dta{l}", [128, 8])
            sm = S(f"m_sm{l}", [128, 40])
            xw = S(f"m_xw{l}", [128, 512], BF16)
            l2 = [S(f"m_l2{l}_{i}", [128, 256]) for i in range(2)]
            Et = [S(f"m_E{l}_{i}", [128, 256]) for i in range(2)]
            LTt = [S(f"m_LT{l}_{i}", [128, 128]) for i in range(2)]
            STt = [S(f"m_ST{l}_{i}", [128, 128], BF16) for i in range(2)]
            CsT = [S(f"m_Cs{l}_{i}", [128, 128], BF16) for i in range(2)]
            hst = S(f"m_hst{l}", [128, 512])
            hstb = S(f"m_hstb{l}", [128, 512], BF16)
            yt = S(f"m_y{l}", [128, 512])
            yf = S(f"m_yf{l}", [128, 512])
            zs = S(f"m_zs{l}", [128, 512])
            ysq = S(f"m_ysq{l}", [128, 512])
            gst = S(f"m_gst{l}", [128, 4])
            mixo = S(f"mixo{l}", [128, 8, 128], BF16)
            raw = S(f"r_raw{l}", [64, 26, 130])
            ssum = S(f"r_ssum{l}", [64, 26, 128])
            pp = ssum
            xg0 = S(f"r_xg0{l}", [128, 130])
            xg1 = S(f"r_xg1{l}", [32, 130])
            sg0 = S(f"r_sg0{l}", [128, 128], BF16)
            sg1 = S(f"r_sg1{l}", [32, 128], BF16)
            xgt = S(f"r_xgt{l}", [128, 128])
            twb = S(f"r_twb{l}", [64, 2, 128], BF16)
            lw = S(f"r_lw{l}", [64, 8, 128])
            aa = S(f"r_aa{l}", [64, 8, 128])
            kk = S(f"r_kk{l}", [64, 8, 128])
            kd = S(f"r_kd{l}", [64, 8, 128])
            sqb = S(f"r_sqb{l}", [64, 8, 128], BF16)
            linc = S(f"r_linc{l}", [64, 8, 128])
            lex = S(f"r_lex{l}", [64, 8, 128])
            rinv = lex
            e1 = S(f"r_e1{l}", [64, 8, 128])
            e0 = S(f"r_e0{l}", [64, 8, 128])
            ei = S(f"r_ei{l}", [64, 8, 128])
            gC = S(f"r_gC{l}", [64, 8])
            tmpk = S(f"r_tmpk{l}", [64, 8, 128])
            AR = S(f"r_AR{l}", [64, 8, 256], BF16)
            BK = S(f"r_BK{l}", [64, 8, 2, 128], BF16)
            prodb = S(f"r_prod{l}", [64, 8, 128], BF16)
            vtok = S(f"r_vtok{l}", [128, 512])
            vtokb = S(f"r_vtokb{l}", [128, 512], BF16)
            BKtok = S(f"r_BKtok{l}", [128, 8, 2, 64], BF16)
            GB = S(f"r_GB{l}", [128, 8, 256], BF16)
            GK = S(f"r_GK{l}", [128, 8, 256], BF16)
            Pm = [S(f"r_P{l}_{i}", [128, 8, 128]) for i in range(2)]
            Qm = [S(f"r_Q{l}_{i}", [128, 8, 128]) for i in range(2)]
            Ym = [S(f"r_Y{l}_{i}", [128, 8, 128]) for i in range(2)]
            Wt = S(f"r_W{l}", [128, 512])
            Ut = S(f"r_U{l}", [128, 512], BF16)
            Hs = S(f"r_H{l}", [64, 512])
            Hb = S(f"r_Hb{l}", [64, 512], BF16)
            ot = S(f"r_o{l}", [128, 520])
            oft = S(f"r_of{l}", [128, 520])
            osq = ysq
            gn = S(f"r_gn{l}", [128, 40])
            B = [es.enter_context(nc.psum_tensor(f"mxps{l}_{i}", [128, 512], F32)).ap() for i in range(7)]
            BBp = es.enter_context(nc.psum_tensor(f"mxpsb{l}", [128, 1024], BF16)).ap()

            nbw = ROWB[:, RV_MNW:RV_MNW + 512]
            lnw = ROWB[:, RV_LNW:RV_LNW + 512]
            lnb = ROWB[:, RV_LNB:RV_LNB + 512]
            mdb = ROWB[:, RV_MD:RV_MD + 8]
            evc = [0]

            def evq():
                evc[0] += 1
                return 'act' if evc[0] % 2 == 0 else 'dve'

            def load_halo(tile, tk, srcap, srck, t0, T, nblk_dims):
                lo = max(t0 - 1, 0)
                hi = min(t0 + 129, T)
                o = lo - (t0 - 1)
                if nblk_dims:
                    em.dma('sp', tile[:, :, o:o + hi - lo], srcap[:, :, lo:hi], reads=[srck], writes=[tk])
                    if t0 == 0:
                        em.memset('pool', tile[:, :, 0:1], 0.0, [tk])
                    if t0 + 128 == T:
                        em.memset('pool', tile[:, :, 129:130], 0.0, [tk])
                else:
                    em.dma('sp', tile[:, o:o + hi - lo], srcap[:, lo:hi], reads=[srck], writes=[tk])
                    if t0 == 0:
                        em.memset('pool', tile[:, 0:1], 0.0, [tk])
                    if t0 + 128 == T:
                        em.memset('pool', tile[:, 129:130], 0.0, [tk])

            def mamba_chunk(s, t0, d, emit_out):
                T = seqT(s)
                load_halo(xbc, 'm_xbc', fm(XBC[s]), f"xbc{s}", t0, T, True)
                em.dma('pool', zdt, ZDT[s][t0:t0 + 128, :], reads=[f"zdt{s}"], writes=['m_zdt'])
                if (mixcfg or {}).get('mstop', 99) <= 1:
                    return
                for j in range(8):
                    em.ts('dve', cacc[:, j, :], xbc[:, j, 1:129], PV[:, PV_MCW + 8 + j:PV_MCW + 9 + j],
                          PV[:, PV_MCB + j:PV_MCB + j + 1], ALU.mult, ALU.add, ['m_xbc', 'PV'], ['m_cacc'])
                    em.stt(cacc[:, j, :], xbc[:, j, 0:128], PV[:, PV_MCW + j:PV_MCW + j + 1], cacc[:, j, :],
                           ALU.mult, ALU.add, ['m_xbc', 'PV', 'm_cacc'], ['m_cacc'])
                    em.stt(cacc[:, j, :], xbc[:, j, 2:130], PV[:, PV_MCW + 16 + j:PV_MCW + 17 + j], cacc[:, j, :],
                           ALU.mult, ALU.add, ['m_xbc', 'PV', 'm_cacc'], ['m_cacc'])
                em.act(cacc[:, 0:6, :], cacc[:, 0:6, :], AF.Silu, ['m_cacc'], ['m_cacc'])
                em.act(bct[:, 2:4, :], cacc[:, 6:8, :], AF.Silu, ['m_cacc'], ['m_bct'])
                em.cp('dve', bct[:, 0:2, :], cacc[:, 4:6, :], ['m_cacc'], ['m_bct'])
                if (mixcfg or {}).get('mstop', 99) <= 2:
                    return
                msub = (mixcfg or {}).get('msub', 9)
                for j in range(4):
                    em.tr(B[0][:, j * 128:(j + 1) * 128], cacc[:, j, :], ident, ['m_cacc', 'CST'], ['B0'])
                if msub >= 1:
                    for j in range(2):
                        em.tr(B[1][:, j * 128:(j + 1) * 128], cacc[:, 4 + j, :], ident, ['m_cacc', 'CST'], ['B1'])
                if msub >= 2 and msub != 33:
                    em.cp('dve', xtok, B[0], ['B0'], ['m_xtok'])
                if msub == 30:
                    em.cp('dve', xtokb, B[0], ['B0'], ['m_xtokb'])
                elif msub == 31:
                    em.cp('act', yt, B[0], ['B0'], ['m_y'])
                elif msub == 32:
                    em.cp('act', xtokb, xtok, ['m_xtok'], ['m_xtokb'])
                elif msub >= 3:
                    em.cp('act', xtokb, B[0], ['B0'], ['m_xtokb'])
                if msub >= 4:
                    em.cp('act', btokb, B[1][:, 0:256], ['B1'], ['m_btokb'])
                if (mixcfg or {}).get('mstop', 99) <= 3:
                    return
                em.tt('dve', dts, zdt[:, 512 + d * 8:520 + d * 8], ROWB[:, RV_DTB + d * 8:RV_DTB + d * 8 + 8], ALU.add,
                      ['m_zdt', 'ROWB'], ['m_dts'])
                em.act(dts, dts, AF.Exp, ['m_dts'], ['m_dts'])
                em.act(dts, dts, AF.Ln, ['m_dts'], ['m_dts'], bias=1.0, scale=1.0)
                em.tt('dve', dta, dts, NEGA[:, d * 8:d * 8 + 8], ALU.mult, ['m_dts', 'NEGA'], ['m_dta'])
                if (mixcfg or {}).get('mstop', 99) <= 4:
                    return
                em.mm(B[1][:, 256:264], MI(d), dta, True, True, ['CST', 'm_dta'], ['B1'])
                em.mm(B[1][:, 264:272], ones, dta, True, True, ['CST', 'm_dta'], ['B1'])
                em.cp('act', sm[:, 32:40], B[1][:, 264:272], ['B1'], ['m_sm'])
                em.tt('dve', sm[:, 24:32], sm[:, 32:40], B[1][:, 256:264], ALU.subtract, ['B1', 'm_sm'], ['m_sm'])
                em.act(sm[:, 0:8], sm[:, 24:32], AF.Exp, ['m_sm'], ['m_sm'])
                em.tt('dve', sm[:, 8:16], sm[:, 0:8], dts, ALU.mult, ['m_sm', 'm_dts'], ['m_sm'])
                em.act(sm[:, 16:24], B[1][:, 264:272], AF.Exp, ['B1'], ['m_sm'])
                em.tt('dve', xw.rearrange("p (h q) -> p h q", q=64), xtok.rearrange("p (h q) -> p h q", q=64),
                      sm[:, 8:16].unsqueeze(2).to_broadcast([128, 8, 64]), ALU.mult, ['m_xtok', 'm_sm'], ['m_xw'])
                if (mixcfg or {}).get('mstop', 99) <= 5:
                    return
                for g in range(2):
                    em.mm(B[2][:, g * 128:(g + 1) * 128], bct[:, g, :], bct[:, 2 + g, :], True, True, ['m_bct'], ['B2'])
                for h in range(8):
                    g = h // 4
                    i2 = h % 2
                    pe_ = B[3] if i2 == 0 else B[4]
                    pek = 'B3' if i2 == 0 else 'B4'
                    em.ts('dve', l2[i2], MSO[:, d, :], dta[:, h:h + 1], None, ALU.mult, None,
                          ['MSO', 'm_dta'], [f"m_l2{i2}"])
                    em.mm(pe_[:, 0:128], l2[i2][:, 0:128], MI(d), True, True, [f"m_l2{i2}", 'CST'], [pek])
                    em.mm(pe_[:, 128:256], l2[i2][:, 128:256], MI(d), True, True, [f"m_l2{i2}", 'CST'], [pek])
                    em.act(Et[i2], pe_[:, 0:256], AF.Exp, [pek], [f"m_E{i2}"])
                    em.stt(LTt[i2], Et[i2][:, 0:128], dts[:, h:h + 1], MI(d), ALU.mult, ALU.mult,
                           [f"m_E{i2}", 'm_dts', 'CST'], [f"m_LT{i2}"])
                    em.tt('dve', STt[i2], B[2][:, g * 128:(g + 1) * 128], LTt[i2], ALU.mult, ['B2', f"m_LT{i2}"],
                          [f"m_ST{i2}"])
                    em.tt('dve', CsT[i2], bct[:, 2 + g, :], Et[i2][:, 128:256], ALU.mult, ['m_bct', f"m_E{i2}"],
                          [f"m_Cs{i2}"])
                    em.mm(B[5][:, h * 64:(h + 1) * 64], STt[i2], xtokb[:, h * 64:(h + 1) * 64], True, False,
                          [f"m_ST{i2}", 'm_xtokb'], ['B5'])
                    em.mm(B[5][:, h * 64:(h + 1) * 64], CsT[i2], hstb[:, h * 64:(h + 1) * 64], False, True,
                          [f"m_Cs{i2}", 'm_hstb'], ['B5'])
                if (mixcfg or {}).get('mstop', 99) <= 6:
                    return
                for g in range(2):
                    em.mm(B[6][:, g * 256:(g + 1) * 256], btokb[:, g * 128:(g + 1) * 128], xw[:, g * 256:(g + 1) * 256],
                          True, True, ['m_btokb', 'm_xw'], ['B6'])
                em.tt('dve', hst.rearrange("p (h q) -> p h q", q=64), hst.rearrange("p (h q) -> p h q", q=64),
                      sm[:, 16:24].unsqueeze(2).to_broadcast([128, 8, 64]), ALU.mult, ['m_hst', 'm_sm', 'B5'], ['m_hst'])
                em.tt('dve', hst, hst, B[6], ALU.add, ['m_hst', 'B6'], ['m_hst'])
                em.cp('act', hstb, hst, ['m_hst', 'B5'], ['m_hstb'])
                if (mixcfg or {}).get('mstop', 99) <= 7:
                    return
                if d == 0:
                    if emit_out:
                        em.cp('act', yt, B[5], ['B5'], ['m_y'])
                        em.dma('pool', YF[s][t0:t0 + 128, :], yt, reads=['m_y'], writes=[f"yf{s}"])
                elif emit_out:
                    em.dma('sp', yf, YF[s][t0:t0 + 128, :], reads=[f"yf{s}"], writes=['m_yf'])
                    em.tt('dve', yt, B[5], yf, ALU.add, ['B5', 'm_yf'], ['m_y'])
                    em.tt('dve', yf.rearrange("p (h q) -> p h q", q=64), xtok.rearrange("p (h q) -> p h q", q=64),
                          mdb.unsqueeze(2).to_broadcast([128, 8, 64]), ALU.mult, ['m_xtok', 'ROWB', 'm_yf'], ['m_yf'])
                    em.tt('dve', yt, yt, yf, ALU.add, ['m_y', 'm_yf'], ['m_y'])
                    em.act(zs, zdt[:, 0:512], AF.Silu, ['m_zdt'], ['m_zs'])
                    em.tt('dve', yt, yt, zs, ALU.mult, ['m_y', 'm_zs'], ['m_y'])
                    for g in range(2):
                        em.act(ysq[:, g * 256:(g + 1) * 256], yt[:, g * 256:(g + 1) * 256], AF.Square, ['m_y'],
                               ['m_ysq', 'm_gst'], accum=gst[:, g:g + 1])
                    em.act(gst[:, 2:4], gst[:, 0:2], AF.Sqrt, ['m_gst'], ['m_gst'], bias=EPS, scale=1.0 / 256)
                    em.op('dve', lambda e: e.reciprocal(out=gst[:, 2:4], in_=gst[:, 2:4]), ['m_gst'], ['m_gst'])
                    for g in range(2):
                        em.stt(yt[:, g * 256:(g + 1) * 256], yt[:, g * 256:(g + 1) * 256], gst[:, 2 + g:3 + g],
                               nbw[:, g * 256:(g + 1) * 256], ALU.mult, ALU.mult, ['m_y', 'm_gst', 'ROWB'], ['m_y'])
                    for j in range(4):
                        em.tr(B[0][:, j * 128:(j + 1) * 128], yt[:, j * 128:(j + 1) * 128], ident, ['m_y', 'CST'], ['B0'])
                    em.cp('act', mixo[:, 0:4, :], B[0].rearrange("p (j t) -> p j t", t=128), ['B0'], ['mixo_m'])
                    em.dma('pool', fm(MIX[s])[:, 0:4, t0:t0 + 128], mixo[:, 0:4, :], reads=['mixo_m'], writes=[f"mix{s}"])

            def wkv_chunk(s, t0, d, emit_out):
                T = seqT(s)
                fin = (d == 1 and emit_out)
                rk = f"rkv{s}"
                load_halo(raw[:, 0:24, :], 'r_raw', RKV[s].rearrange("j p t -> p j t"), rk, t0, T, True)
                load_halo(raw[:, 24:25, :], 'r_raw', XWA[s][d:d + 1].rearrange("j p t -> p j t"), f"xwa{s}", t0, T, True)
                load_halo(raw[:, 25:26, :], 'r_raw', XWA[s][2 + d:3 + d].rearrange("j p t -> p j t"), f"xwa{s}", t0, T, True)
                em.tt('dve', ssum, raw[:, :, 0:128], raw[:, :, 2:130], ALU.add, ['r_raw'], ['r_ssum'])
                em.stt(ssum, ssum, 0.5, raw[:, :, 1:129], ALU.mult, ALU.subtract, ['r_ssum', 'r_raw'], ['r_ssum'])
                em.tt('dve', ssum[:, 0:24, :], ssum[:, 0:24, :],
                      PV64[:, P6_MURKV:P6_MURKV + 24].unsqueeze(2).to_broadcast([64, 24, 128]), ALU.mult,
                      ['r_ssum', 'PV64'], ['r_ssum'])
                em.ts('dve', ssum[:, 24, :], ssum[:, 24, :], PV64[:, P6_MUWA + d:P6_MUWA + d + 1], None, ALU.mult, None,
                      ['r_ssum', 'PV64'], ['r_ssum'])
                em.ts('dve', ssum[:, 25, :], ssum[:, 25, :], PV64[:, P6_MUWA + 2 + d:P6_MUWA + 3 + d], None, ALU.mult, None,
                      ['r_ssum', 'PV64'], ['r_ssum'])
                em.tt('dve', pp, raw[:, :, 1:129], ssum, ALU.add, ['r_raw', 'r_ssum'], ['r_ssum'])
                rr = pp[:, 0:8, :]
                kr = pp[:, 8:16, :]
                vr = pp[:, 16:24, :]
                em.act(twb[:, 0, :], pp[:, 24, :], AF.Tanh, ['r_ssum'], ['r_twb'])
                em.cp('dve', twb[:, 1, :], pp[:, 25, :], ['r_ssum'], ['r_twb'])
                for h in range(8):
                    bk_ = B[h // 4]
                    em.mm(bk_[0:64, (h % 4) * 128:(h % 4 + 1) * 128], w2b[:, d, h * 64:(h + 1) * 64], twb[:, 0, :], True, True,
                          ['w2b', 'r_twb'], [f"B{h // 4}"])
                for h in range(8):
                    bk_ = B[2 + h // 4]
                    em.mm(bk_[0:64, (h % 4) * 128:(h % 4 + 1) * 128], a2b[:, d, h * 64:(h + 1) * 64], twb[:, 1, :], True, True,
                          ['a2b', 'r_twb'], [f"B{2 + h // 4}"])
                for h in range(8):
                    em.act(lw[:, h, :], B[h // 4][0:64, (h % 4) * 128:(h % 4 + 1) * 128], AF.Sigmoid, [f"B{h // 4}", 'PV64'],
                           ['r_lw'], bias=PV64[:, P6_W0 + d * 8 + h:P6_W0 + d * 8 + h + 1], scale=1.0)
                    em.act(aa[:, h, :], B[2 + h // 4][0:64, (h % 4) * 128:(h % 4 + 1) * 128], AF.Sigmoid,
                           [f"B{2 + h // 4}", 'PV64'], ['r_aa'], bias=PV64[:, P6_A0 + d * 8 + h:P6_A0 + d * 8 + h + 1], scale=1.0)
                em.ts('dve', lw, lw, -R_DECAY_SCALE, None, ALU.mult, None, ['r_lw'], ['r_lw'])
                em.tt('dve', kk, kr, PV64[:, P6_KK:P6_KK + 8].unsqueeze(2).to_broadcast([64, 8, 128]), ALU.mult,
                      ['r_ssum', 'PV64'], ['r_kk'])
                em.act(sqb, kk, AF.Square, ['r_kk'], ['r_sqb'])
                for hh in range(2):
                    em.mm(B[4 + hh][0:64, :], onesb[0:64, 0:64], sqb[:, hh * 4:(hh + 1) * 4, :], True, True,
                          ['CSTB', 'r_sqb'], [f"B{4 + hh}"])
                for hh in range(2):
                    em.act(rinv[:, hh * 4:(hh + 1) * 4, :], B[4 + hh][0:64, :].rearrange("p (h t) -> p h t", t=128), AF.Sqrt,
                           [f"B{4 + hh}"], ['r_lex'])
                em.ts('dve', rinv, rinv, 1e-12, None, ALU.max, None, ['r_lex'], ['r_lex'])
                em.op('dve', lambda e: e.reciprocal(out=rinv, in_=rinv), ['r_lex'], ['r_lex'])
                em.tt('dve', kk, kk, rinv, ALU.mult, ['r_kk', 'r_lex'], ['r_kk'])
                em.tt('dve', tmpk, aa, PV64[:, P6_KA:P6_KA + 8].unsqueeze(2).to_broadcast([64, 8, 128]), ALU.mult,
                      ['r_aa', 'PV64'], ['r_tmpk'])
                em.tt('dve', tmpk, tmpk, OMMU.unsqueeze(2).to_broadcast([64, 8, 128]), ALU.add, ['r_tmpk', 'OMMU'], ['r_tmpk'])
                em.tt('dve', kd, kr, tmpk, ALU.mult, ['r_ssum', 'r_tmpk'], ['r_kd'])
                em.op('dve', lambda e: e.tensor_tensor_scan(out=linc.rearrange("p h t -> p (h t)"),
                                                            data0=RMK.rearrange("p h t -> p (h t)"),
                                                            data1=lw.rearrange("p h t -> p (h t)"), initial=0.0,
                                                            op0=ALU.mult, op1=ALU.add), ['RMK', 'r_lw'], ['r_linc'])
                if d == 0:
                    tot = linc[:, :, 127:128]
                else:
                    em.tt('dve', lex, lw, linc, ALU.subtract, ['r_lw', 'r_linc'], ['r_lex'])
                    em.cp('dve', gC, linc[:, :, 127], ['r_linc'], ['r_gC'])
                    em.tt('dve', linc, lex, gC.unsqueeze(2).to_broadcast([64, 8, 128]), ALU.add, ['r_lex', 'r_gC', 'r_linc'],
                          ['r_linc'])
                    tot = linc[:, :, 0:1]
                em.tt('dve', lex, linc, lw, ALU.subtract, ['r_linc', 'r_lw'], ['r_lex'])
                em.act(e1, linc, AF.Exp, ['r_linc'], ['r_e1'])
                em.act(e0, lex, AF.Exp, ['r_lex'], ['r_e0'])
                em.act(ei, linc, AF.Exp, ['r_linc'], ['r_ei'], scale=-1.0)
                em.act(gC, tot.rearrange("p h o -> p (h o)"), AF.Exp, ['r_linc', 'r_gC'], ['r_gC'])
                em.tt('dve', AR[:, :, 128:256], rr, e1, ALU.mult, ['r_ssum', 'r_e1'], ['r_AR'])
                em.stt(AR[:, :, 0:128], kk, -1.0, e0, ALU.mult, ALU.mult, ['r_kk', 'r_e0'], ['r_AR'])
                em.tt('dve', tmpk, kk, aa, ALU.mult, ['r_kk', 'r_aa', 'r_tmpk'], ['r_tmpk'])
                em.tt('dve', BK[:, :, 0, :], tmpk, ei, ALU.mult, ['r_tmpk', 'r_ei'], ['r_BK'])
                em.tt('dve', BK[:, :, 1, :], kd, ei, ALU.mult, ['r_kd', 'r_ei'], ['r_BK'])
                em.tt('dve', tmpk, rr, kd, ALU.mult, ['r_ssum', 'r_kd', 'r_tmpk'], ['r_tmpk'])
                em.tt('dve', prodb, tmpk, PV64[:, P6_RK:P6_RK + 8].unsqueeze(2).to_broadcast([64, 8, 128]), ALU.mult,
                      ['r_tmpk', 'PV64'], ['r_prod'])
                for h in range(8):
                    em.tr(B[6][:, h * 64:(h + 1) * 64], vr[:, h, :], ident[0:64, 0:64], ['r_ssum', 'CST'], ['B6'])
                em.cp('dve', vtok, B[6], ['B6'], ['r_vtok'])
                em.cp('act', vtokb, B[6], ['B6'], ['r_vtokb'])
                for h in range(8):
                    for q in range(2):
                        em.tr(BBp[:, (h * 2 + q) * 64:(h * 2 + q + 1) * 64], BK[:, h, q, :], identb[0:64, 0:64],
                              ['r_BK', 'CSTB'], ['BB'])
                em.cp('dve', BKtok.rearrange("p h q k -> p (h q k)"), BBp, ['BB'], ['r_BKtok'])
                for h in range(8):
                    em.mm(B[5][:, 256 + h:257 + h], prodb[:, h, :], onesb[0:64, 0:1], True, True, ['r_prod', 'CSTB'], ['B5'])
                em.cp('act', ot[:, 512:520], B[5][:, 256:264], ['B5'], ['r_ocoef'])
                for hp in range(4):
                    bb_ = B[hp % 2]
                    bbk = f"B{hp % 2}"
                    bk2 = B[2 + hp % 2]
                    bk2k = f"B{2 + hp % 2}"
                    for q in range(2):
                        h = hp * 2 + q
                        em.mm(bb_[:, q * 256:(q + 1) * 256], BK[:, h, 0, :], AR[:, h, :], True, True, ['r_BK', 'r_AR', 'r_lw', 'r_aa'], [bbk])
                        em.mm(bk2[:, q * 256:(q + 1) * 256], BK[:, h, 1, :], AR[:, h, :], True, True, ['r_BK', 'r_AR'], [bk2k])
                    em.tt('dve', GB[:, hp * 2:hp * 2 + 2, :], bb_.rearrange("p (q c) -> p q c", c=256),
                          MARt[:, d:d + 1, :].to_broadcast([128, 2, 256]), ALU.mult, [bbk, 'MAR'], ['r_GB'])
                    em.tt('dve', Qm[0][:, hp * 2:hp * 2 + 2, :], bb_.rearrange("p (q c) -> p q c", c=256)[:, :, 0:128],
                          MARt[:, d:d + 1, 0:128].to_broadcast([128, 2, 128]), ALU.mult, [bbk, 'MAR'], ['r_Q0'])
                    em.tt('dve', GK[:, hp * 2:hp * 2 + 2, :], bk2.rearrange("p (q c) -> p q c", c=256),
                          MARt[:, d:d + 1, :].to_broadcast([128, 2, 256]), ALU.mult, [bk2k, 'MAR'], ['r_GK'])
                for hh in range(2):
                    bp = B[4 + hh]
                    bpk = f"B{4 + hh}"
                    for q in range(4):
                        h = hh * 4 + q
                        em.mm(bp[:, q * 128:(q + 1) * 128], AR[:, h, 0:128], BK[:, h, 0, :], True, True, ['r_AR', 'r_BK'], [bpk])
                    em.tt('dve', Pm[0][:, hh * 4:(hh + 1) * 4, :], bp.rearrange("p (q c) -> p q c", c=128),
                          strictTS(d).unsqueeze(1).to_broadcast([128, 4, 128]), ALU.mult, [bpk, 'CST'], ['r_P0'])
                em.tt('dve', Ym[0], Qm[0], ident.unsqueeze(1).to_broadcast([128, 8, 128]), ALU.add,
                      ['r_Q0', 'CST'], ['r_Y0'])
                cur = 0
                for lev in range(1, 7):
                    nxt = 1 - cur
                    for hh in range(2):
                        bp = B[hh]
                        bpk = f"B{hh}"
                        bq = B[2 + hh]
                        bqk = f"B{2 + hh}"
                        by = B[4 + hh]
                        byk = f"B{4 + hh}"
                        for q in range(4):
                            h = hh * 4 + q
                            em.mm(bp[:, q * 128:(q + 1) * 128], Qm[cur][:, h, :], Pm[cur][:, h, :], True, True,
                                  [f"r_Q{cur}", f"r_P{cur}"], [bpk])
                        if lev < 6:
                            for q in range(4):
                                h = hh * 4 + q
                                em.mm(bq[:, q * 128:(q + 1) * 128], Pm[cur][:, h, :], Qm[cur][:, h, :], True, True,
                                      [f"r_Q{cur}", f"r_P{cur}"], [bqk])
                        em.cp('act', Pm[nxt][:, hh * 4:(hh + 1) * 4, :], bp.rearrange("p (q c) -> p q c", c=128), [bpk],
                              [f"r_P{nxt}"])
                        if lev < 6:
                            em.cp('dve', Qm[nxt][:, hh * 4:(hh + 1) * 4, :], bq.rearrange("p (q c) -> p q c", c=128), [bqk],
                                  [f"r_Q{nxt}"])
                    for hh in range(2):
                        by = B[4 + hh]
                        byk = f"B{4 + hh}"
                        for q in range(4):
                            h = hh * 4 + q
                            em.mm(by[:, q * 128:(q + 1) * 128], Pm[nxt][:, h, :], Ym[cur][:, h, :], True, True,
                                  [f"r_P{nxt}", f"r_Y{cur}"], [byk])
                        em.tt('dve', Ym[nxt][:, hh * 4:(hh + 1) * 4, :], by.rearrange("p (q c) -> p q c", c=128),
                              Ym[cur][:, hh * 4:(hh + 1) * 4, :], ALU.add, [byk, f"r_Y{cur}"], [f"r_Y{nxt}"])
                    cur = nxt
                TT_ = Ym[cur]
                ttk = f"r_Y{cur}"
                for h in range(8):
                    hs_ = slice(h * 64, (h + 1) * 64)
                    em.mm(B[0][:, hs_], AR[:, h, 0:128], Hb[:, hs_], True, False, ['r_AR', 'r_Hb'], ['B0'])
                    em.mm(B[0][:, hs_], GK[:, h, 0:128], vtokb[:, hs_], False, True, ['r_GK', 'r_vtokb'], ['B0'])
                em.cp('act', Wt, B[0], ['B0'], ['r_W'])
                for h in range(8):
                    hs_ = slice(h * 64, (h + 1) * 64)
                    em.mm(B[1][:, hs_], TT_[:, h, :], Wt[:, hs_], True, True, [ttk, 'r_W'], ['B1'])
                em.cp('dve', Ut, B[1], ['B1'], ['r_U'])
                for h in range(8):
                    hs_ = slice(h * 64, (h + 1) * 64)
                    em.mm(B[2][:, hs_], AR[:, h, 128:256], Hb[:, hs_], True, False, ['r_AR', 'r_Hb'], ['B2'])
                    em.mm(B[2][:, hs_], GB[:, h, 128:256], Ut[:, hs_], False, False, ['r_GB', 'r_U'], ['B2'])
                    em.mm(B[2][:, hs_], GK[:, h, 128:256], vtokb[:, hs_], False, True, ['r_GK', 'r_vtokb'], ['B2'])
                for h in range(8):
                    hs_ = slice(h * 64, (h + 1) * 64)
                    em.mm(B[3][0:64, hs_], BKtok[:, h, 0, :], Ut[:, hs_], True, False, ['r_BKtok', 'r_U'], ['B3'])
                    em.mm(B[3][0:64, hs_], BKtok[:, h, 1, :], vtokb[:, hs_], False, True, ['r_BKtok', 'r_vtokb'], ['B3'])
                em.tt('dve', Hs, Hs, B[3][0:64, :], ALU.add, ['r_H', 'B3'], ['r_H'])
                em.tt('dve', Hs.rearrange("p (h v) -> p h v", v=64), Hs.rearrange("p (h v) -> p h v", v=64),
                      gC.unsqueeze(2).to_broadcast([64, 8, 64]), ALU.mult, ['r_H', 'r_gC'], ['r_H'])
                em.cp('act', Hb, Hs, ['r_H', 'B0', 'B2'], ['r_Hb'])
                if d == 0:
                    if emit_out:
                        em.cp('act', ot[:, 0:512], B[2], ['B2'], ['r_o'])
                        em.dma('pool', OF[s][t0:t0 + 128, :], ot, reads=['r_o', 'r_ocoef'], writes=[f"of{s}"])
                elif emit_out:
                    em.dma('sp', oft, OF[s][t0:t0 + 128, :], reads=[f"of{s}"], writes=['r_of'])
                    em.tt('dve', ot[:, 0:512], B[2], oft[:, 0:512], ALU.add, ['B2', 'r_of'], ['r_o'])
                    o3 = ot[:, 0:512].rearrange("p (h v) -> p h v", v=64)
                    em.op('dve', lambda e: e.tensor_reduce(out=gn[:, 0:8], in_=o3, axis=AX.X, op=ALU.add), ['r_o'], ['r_gn'])
                    em.act(osq, ot[:, 0:512], AF.Square, ['r_o'], ['m_ysq'])
                    em.op('dve', lambda e: e.tensor_reduce(out=gn[:, 8:16], in_=osq.rearrange("p (h v) -> p h v", v=64),
                                                           axis=AX.X, op=ALU.add), ['m_ysq', 'r_gn'], ['r_gn'])
                    em.ts('dve', gn[:, 16:24], gn[:, 0:8], 1.0 / 64, None, ALU.mult, None, ['r_gn'], ['r_gn'])
                    em.tt('dve', gn[:, 0:8], gn[:, 16:24], gn[:, 16:24], ALU.mult, ['r_gn'], ['r_gn'])
                    em.stt(gn[:, 24:32], gn[:, 8:16], 1.0 / 64, gn[:, 0:8], ALU.mult, ALU.subtract, ['r_gn'], ['r_gn'])
                    em.act(gn[:, 24:32], gn[:, 24:32], AF.Sqrt, ['r_gn'], ['r_gn'], bias=R_LN_EPS, scale=1.0)
                    em.op('dve', lambda e: e.reciprocal(out=gn[:, 24:32], in_=gn[:, 24:32]), ['r_gn'], ['r_gn'])
                    em.tt('dve', o3, o3, gn[:, 16:24].unsqueeze(2).to_broadcast([128, 8, 64]), ALU.subtract, ['r_o', 'r_gn'], ['r_o'])
                    em.tt('dve', o3, o3, gn[:, 24:32].unsqueeze(2).to_broadcast([128, 8, 64]), ALU.mult, ['r_o', 'r_gn'], ['r_o'])
                    em.tt('dve', ot[:, 0:512], ot[:, 0:512], lnw, ALU.mult, ['r_o', 'ROWB'], ['r_o'])
                    em.tt('dve', ot[:, 0:512], ot[:, 0:512], lnb, ALU.add, ['r_o', 'ROWB'], ['r_o'])
                    em.tt('dve', gn[:, 32:40], ot[:, 512:520], oft[:, 512:520], ALU.add, ['r_ocoef', 'r_of', 'r_gn'], ['r_gn'])
                    em.tt('dve', osq.rearrange("p (h v) -> p h v", v=64), vtok.rearrange("p (h v) -> p h v", v=64),
                          gn[:, 32:40].unsqueeze(2).to_broadcast([128, 8, 64]), ALU.mult, ['r_vtok', 'r_gn', 'm_ysq'], ['m_ysq'])
                    em.tt('dve', ot[:, 0:512], ot[:, 0:512], osq, ALU.add, ['r_o', 'm_ysq'], ['r_o'])
                    load_halo(xg0, 'r_xg0', XG[s][0:128, :], f"xg{s}", t0, T, False)
                    load_halo(xg1, 'r_xg1', XG[s][128:160, :], f"xg{s}", t0, T, False)
                    for (xg_, sg_, np_, mucol, kx_, ks_) in [(xg0, sg0, 128, PV_MUXG0, 'r_xg0', 'r_sg0'),
                                                             (xg1, sg1, 32, PV_MUXG1, 'r_xg1', 'r_sg1')]:
                        em.tt('dve', xgt[:np_, :], xg_[:np_, 0:128], xg_[:np_, 2:130], ALU.add, [kx_], ['r_xgt'])
                        em.stt(xgt[:np_, :], xgt[:np_, :], 0.5, xg_[:np_, 1:129], ALU.mult, ALU.subtract, ['r_xgt', kx_], ['r_xgt'])
                        em.stt(xgt[:np_, :], xgt[:np_, :], PV[:np_, mucol:mucol + 1], xg_[:np_, 1:129], ALU.mult, ALU.add,
                               ['r_xgt', kx_, 'PV'], ['r_xgt'])
                        em.act(sg_[:np_, :], xgt[:np_, :], AF.Sigmoid, ['r_xgt'], [ks_])
                    em.mm(B[4], sg0, g2b0, True, False, ['r_sg0', 'g2b'], ['B4'])
                    em.mm(B[4], sg1, g2b1, False, True, ['r_sg1', 'g2b'], ['B4'])
                    em.tt('dve', ot[:, 0:512], ot[:, 0:512], B[4], ALU.mult, ['r_o', 'B4'], ['r_o'])
                    for j in range(4):
                        em.tr(B[5][:, j * 128:(j + 1) * 128], ot[:, j * 128:(j + 1) * 128], ident, ['r_o', 'CST'], ['B5'])
                    em.cp('act', mixo[:, 4:8, :], B[5].rearrange("p (j t) -> p j t", t=128), ['B5'], ['mixo_r'])
                    em.dma('pool', fm(MIX[s])[:, 4:8, t0:t0 + 128], mixo[:, 4:8, :], reads=['mixo_r'], writes=[f"mix{s}"])

            mc_ = mixcfg or {}
            for b in range(mc_.get('nb', NBL)):
                for d in range(mc_.get('nd', 2)):
                    em.memset('pool', hst, 0.0, ['m_hst'])
                    em.memset('pool', hstb, 0.0, ['m_hstb'])
                    em.memset('pool', Hs, 0.0, ['r_H'])
                    em.memset('pool', Hb, 0.0, ['r_Hb'])
                    for kind in range(mc_.get('nkind', 2)):
                        s = b * 2 + kind
                        T = seqT(s)
                        nch = T // CH
                        order = range(nch) if d == 0 else range(nch - 1, -1, -1)
                        emit = (kind == 1) or need_ctx_out or mc_.get('ctxout', False)
                        for c in order:
                            if mc_.get('mamba', True):
                                mamba_chunk(s, c * CH, d, emit)
                            if mc_.get('wkv', True):
                                wkv_chunk(s, c * CH, d, emit)
            em.barrier()

    def stage_proj_post(l, phase, wap, Kc, SRC, srcname, gidx, seqs):
        with ExitStack() as es:
            def S(name, shape, dt=F32):
                return es.enter_context(nc.sbuf_tensor(name, list(shape), dt)).ap()
            nm = f"pp{phase}"
            wb = load_wbf(es, l, wap, Kc, D, nm + "w")
            a = S(f"{nm}a{l}", [128, Kc, 512], BF16)
            xt = S(f"{nm}x{l}", [128, 8, 512])
            y = S(f"{nm}y{l}", [128, 8, 512])
            sq = S(f"{nm}sq{l}", [128, 8, 512], BF16)
            rstd = S(f"{nm}rs{l}", [128, 512])
            pss = [es.enter_context(nc.psum_tensor(f"{nm}ps{l}_{i}", [128, 512], F32)).ap() for i in range(5)]
            for s in seqs:
                T = seqT(s)
                TW = min(512, T)
                jmod = 2 if s % 2 == 0 else s // 2
                rsrc, rsk = res_src(l, s, phase)
                rdst, rdk = res_dst(l, s, phase)
                for tt_ in range(T // TW):
                    t0 = tt_ * TW
                    em.dma('sp', a[:, :, :TW], fm(SRC[s])[:, :, t0:t0 + TW], reads=[f"{srcname}{s}"], writes=[nm + 'a'])
                    em.dma('pool', xt[:, :, :TW], fm(rsrc)[:, :, t0:t0 + TW], reads=[rsk], writes=[nm + 'x'])
                    for m in range(8):
                        ps = pss[m % 4]
                        pk = f"ps{m % 4}"
                        for k in range(Kc):
                            em.mm(ps[:, :TW], wb[:, k, m * 128:(m + 1) * 128], a[:, k, :TW], k == 0, k == Kc - 1,
                                  [nm + 'wbf', nm + 'a'], [pk])
                        em.cp('dve', y[:, m, :TW], ps[:, :TW], [pk], [nm + 'y'])
                        em.act(sq[:, m, :TW], ps[:, :TW], AF.Square, [pk], [nm + 'sq'])
                    for m in range(8):
                        em.mm(pss[4][:, :TW], onesb, sq[:, m, :TW], m == 0, m == 7, ['CSTB', nm + 'sq'], ['ps4'])
                    em.act(rstd[:, :TW], pss[4][:, :TW], AF.Sqrt, ['ps4'], [nm + 'rs'], bias=EPS, scale=1.0 / D)
                    em.op('dve', lambda e: e.reciprocal(out=rstd[:, :TW], in_=rstd[:, :TW]), [nm + 'rs'], [nm + 'rs'])
                    for m in range(8):
                        em.stt(y[:, m, :TW], y[:, m, :TW], DER[:, gidx, m, jmod:jmod + 1], rstd[:, :TW], ALU.mult, ALU.mult,
                               [nm + 'y', nm + 'rs', 'DER'], [nm + 'y'])
                    em.tt('dve', xt[:, :, :TW], xt[:, :, :TW], y[:, :, :TW], ALU.add, [nm + 'x', nm + 'y'], [nm + 'x'])
                    em.dma('pool', fm(rdst)[:, :, t0:t0 + TW], xt[:, :, :TW], reads=[nm + 'x'], writes=[rdk])
            em.barrier()

    def stage_ffn_up(l, seqs):
        with ExitStack() as es:
            def S(name, shape, dt=F32):
                return es.enter_context(nc.sbuf_tensor(name, list(shape), dt)).ap()
            wb = load_wbf(es, l, f_w_up[l], 8, 2 * DFF, "wup")
            xt = S(f"fux{l}", [128, 8, 512])
            h = S(f"fuh{l}", [128, 8, 512], BF16)
            sq = S(f"fusq{l}", [128, 8, 512], BF16)
            rstd = S(f"furs{l}", [128, 512])
            stg = [S(f"fustg{l}_{i}", [128, 512]) for i in range(2)]
            stv = [S(f"fustv{l}_{i}", [128, 512], BF16) for i in range(2)]
            pss = [es.enter_context(nc.psum_tensor(f"fups{l}_{i}", [128, 512], F32)).ap() for i in range(8)]
            for s in seqs:
                T = seqT(s)
                TW = min(512, T)
                jmod = 2 if s % 2 == 0 else s // 2
                src, srck = res_src(l, s, 1)
                for tt_ in range(T // TW):
                    t0 = tt_ * TW
                    em.dma('sp', xt[:, :, :TW], fm(src)[:, :, t0:t0 + TW], reads=[srck], writes=['fux'])
                    prenorm(xt, TW, h, sq, rstd, pss[7], jmod, 2, 24, 'fux', 'fuh', 'ps7')
                    for j in range(NFF):
                        pg = pss[(2 * j) % 6]
                        pgk = f"ps{(2 * j) % 6}"
                        pv_ = pss[(2 * j + 1) % 6]
                        pvk = f"ps{(2 * j + 1) % 6}"
                        for k in range(8):
                            em.mm(pg[:, :TW], wb[:, k, j * 128:(j + 1) * 128], h[:, k, :TW], k == 0, k == 7, ['wupbf', 'fuh'], [pgk])
                        for k in range(8):
                            em.mm(pv_[:, :TW], wb[:, k, DFF + j * 128:DFF + (j + 1) * 128], h[:, k, :TW], k == 0, k == 7,
                                  ['wupbf', 'fuh'], [pvk])
                        sg_ = stg[j % 2]
                        sv_ = stv[j % 2]
                        em.cp('dve', sg_[:, :TW], pg[:, :TW], [pgk], [f"fustg{j % 2}"])
                        em.cp('act', sv_[:, :TW], pv_[:, :TW], [pvk], [f"fustv{j % 2}"])
                        em.dma('pool', GATE[s][j * 128:(j + 1) * 128, t0:t0 + TW], sg_[:, :TW], reads=[f"fustg{j % 2}"],
                               writes=[f"gate{s}"])
                        em.dma('sp', VAL[s][j * 128:(j + 1) * 128, t0:t0 + TW], sv_[:, :TW], reads=[f"fustv{j % 2}"],
                               writes=[f"val{s}"])
            em.barrier()

    def stage_ffn_conv(l, seqs):
        with ExitStack() as es:
            def S(name, shape, dt=F32):
                return es.enter_context(nc.sbuf_tensor(name, list(shape), dt)).ap()
            gflat = [S(f"fcg{l}_{i}", [128, 2048]) for i in range(2)]
            vflat = [S(f"fcv{l}_{i}", [128, 2048], BF16) for i in range(2)]
            gpx = S(f"fcgpx{l}", [128, 34, 66])
            gpc = S(f"fcgpc{l}", [128, 3, 258])
            acc = S(f"fcacc{l}", [128, 2048])
            acc2 = S(f"fcacc2{l}", [128, 2048])
            u = S(f"fcu{l}", [128, 2048])
            ab = [S(f"fcab{l}_{i}", [128, 2048], BF16) for i in range(2)]
            em.memset('pool', gpx, 0.0, ['fcgpx'])
            em.memset('pool', gpc, 0.0, ['fcgpc'])
            it = 0
            for s in seqs:
                T = seqT(s)
                if s % 2 == 1:
                    R, Cc, gp, gpk = 32, 64, gpx, 'fcgpx'
                else:
                    R, Cc, gp, gpk = 1, 256, gpc, 'fcgpc'
                for j in range(NFF):
                    i2 = it % 2
                    it += 1
                    gf = gflat[i2]
                    vf = vflat[i2]
                    em.dma('sp', gf[:, :T], GATE[s][j * 128:(j + 1) * 128, :], reads=[f"gate{s}"], writes=[f"fcg{i2}"])
                    em.dma('sp', vf[:, :T], VAL[s][j * 128:(j + 1) * 128, :], reads=[f"val{s}"], writes=[f"fcv{i2}"])
                    em.cp('pool', gp[:, 1:1 + R, 1:1 + Cc], gf[:, :T].rearrange("p (r c) -> p r c", c=Cc), [f"fcg{i2}"], [gpk])
                    a3 = acc[:, :T].rearrange("p (r c) -> p r c", c=Cc)
                    for tap in range(9):
                        dr, dc = tap // 3 - 1, tap % 3 - 1
                        src_ = gp[:, 1 + dr:1 + dr + R, 1 + dc:1 + dc + Cc]
                        wcol = PV[:, PV_FCW + tap * NFF + j:PV_FCW + tap * NFF + j + 1]
                        if tap == 0:
                            em.ts('dve', a3, src_, wcol, PV[:, PV_FCB + j:PV_FCB + j + 1], ALU.mult, ALU.add,
                                  [gpk, 'PV'], ['fcacc'])
                        else:
                            em.stt(a3, src_, wcol, a3, ALU.mult, ALU.add, [gpk, 'PV', 'fcacc'], ['fcacc'])
                    em.act(u[:, :T], acc[:, :T], AF.Square, ['fcacc'], ['fcu'])
                    em.ts('dve', u[:, :T], u[:, :T], 0.044715, 1.0, ALU.mult, ALU.add, ['fcu'], ['fcu'])
                    em.tt('dve', u[:, :T], u[:, :T], acc[:, :T], ALU.mult, ['fcu', 'fcacc'], ['fcu'])
                    em.act(u[:, :T], u[:, :T], AF.Sigmoid, ['fcu'], ['fcu'], scale=GELU_C)
                    em.tt('dve', u[:, :T], u[:, :T], acc[:, :T], ALU.mult, ['fcu', 'fcacc'], ['fcu'])
                    em.tt('dve', ab[i2][:, :T], u[:, :T], vf[:, :T], ALU.mult, ['fcu', f"fcv{i2}"], [f"fcab{i2}"])
                    em.dma('pool', ACTV[s][j * 128:(j + 1) * 128, :], ab[i2][:, :T], reads=[f"fcab{i2}"], writes=[f"actv{s}"])
            em.barrier()

    allseq = list(range(NS))
    xseq = [s for s in range(NS) if s % 2 == 1]
    outkeys = []
    for l in range(n_layers):
        last = (l == n_layers - 1)
        stage_mod(l)
        stage_inproj(l)
        if stop_after == 'inproj':
            break
        stage_mixer(l, need_ctx_out=not last)
        if stop_after == 'mixer':
            break
        seqs = xseq if last else allseq
        stage_proj_post(l, 0, w_out[l], 8, MIX, "mix", 1, seqs)
        if stop_after == 'outproj':
            break
        stage_ffn_up(l, seqs)
        stage_ffn_conv(l, seqs)
        stage_proj_post(l, 1, f_w_down[l], NFF, ACTV, "actv", 3, seqs)
    em.barrier()
    return nc, em


def host_prep(inp):
    f = np.float32
    idx = np.arange(128)
    cstn = np.zeros((128, NCST), f)
    cstn[:, C_ID:C_ID + 128] = np.eye(128)
    cstn[:, C_UTI:C_UTI + 128] = (idx[:, None] <= idx[None, :])
    cstn[:, C_LTI:C_LTI + 128] = (idx[:, None] >= idx[None, :])
    cstn[:, C_UTS:C_UTS + 128] = (idx[:, None] < idx[None, :])
    cstn[:, C_LTS:C_LTS + 128] = (idx[:, None] > idx[None, :])
    cstn[:, C_ONE:C_ONE + 128] = 1.0
    pvn = np.zeros((L, 128, NPV), f)
    pv6 = np.zeros((L, 64, NPV64), f)
    rwn = np.zeros((L, 1, NROW), f)
    for l in range(L):
        pvn[l, :, PV_BMOD:PV_BMOD + 48] = inp['b_mod'][l].reshape(48, 128).T
        pvn[l, :, PV_GPRE1:PV_GPRE1 + 8] = inp['g_mix_pre'][l].reshape(8, 128).T
        pvn[l, :, PV_GPOST1:PV_GPOST1 + 8] = inp['g_mix_post'][l].reshape(8, 128).T
        pvn[l, :, PV_GPRE2:PV_GPRE2 + 8] = inp['g_ffn_pre'][l].reshape(8, 128).T
        pvn[l, :, PV_GPOST2:PV_GPOST2 + 8] = inp['g_ffn_post'][l].reshape(8, 128).T
        pvn[l, :, PV_MCW:PV_MCW + 24] = inp['m_conv_w'][l].reshape(3, 8, 128).transpose(2, 0, 1).reshape(128, 24)
        pvn[l, :, PV_MCB:PV_MCB + 8] = inp['m_conv_b'][l].reshape(8, 128).T
        pvn[l, :, PV_FCW:PV_FCW + 198] = inp['f_conv_w'][l].reshape(9, NFF, 128).transpose(2, 0, 1).reshape(128, 198)
        pvn[l, :, PV_FCB:PV_FCB + NFF] = inp['f_conv_b'][l].reshape(NFF, 128).T
        mu = inp['r_mu'][l]
        pvn[l, :, PV_MUXG0] = mu[1792:1920]
        pvn[l, 0:32, PV_MUXG1] = mu[1920:1952]
        pv6[l, :, P6_MURKV:P6_MURKV + 24] = mu[0:1536].reshape(24, 64).T
        pv6[l, :, P6_MUWA:P6_MUWA + 4] = mu[1536:1792].reshape(4, 64).T
        pv6[l, :, P6_W0:P6_W0 + 16] = inp['r_w0'][l].reshape(2, 8, 64).transpose(2, 0, 1).reshape(64, 16)
        pv6[l, :, P6_A0:P6_A0 + 16] = inp['r_a0'][l].reshape(2, 8, 64).transpose(2, 0, 1).reshape(64, 16)
        pv6[l, :, P6_KK:P6_KK + 8] = inp['r_k_k'][l].reshape(8, 64).T
        pv6[l, :, P6_KA:P6_KA + 8] = inp['r_k_a'][l].reshape(8, 64).T
        pv6[l, :, P6_RK:P6_RK + 8] = inp['r_r_k'][l].T
        rwn[l, 0, RV_MNW:RV_MNW + 512] = inp['m_norm_w'][l]
        rwn[l, 0, RV_LNW:RV_LNW + 512] = inp['r_ln_w'][l]
        rwn[l, 0, RV_LNB:RV_LNB + 512] = inp['r_ln_b'][l]
        rwn[l, 0, RV_MD:RV_MD + 8] = inp['m_d'][l]
        rwn[l, 0, RV_DTB:RV_DTB + 16] = inp['m_dt_bias'][l].reshape(16)
        rwn[l, 0, RV_ALOG:RV_ALOG + 16] = inp['m_a_log'][l].reshape(16)
    return cstn, pvn, pv6, rwn


def make_in_maps(inp, cores):
    cstn, pvn, pv6, rwn = host_prep(inp)
    shared = {k: np.ascontiguousarray(np.asarray(inp[k], dtype=np.float32)) for k in
              ['w_mod', 'w_in', 'w_out', 'r_w2', 'r_a2', 'r_g2', 'f_w_up', 'f_w_down']}
    maps = []
    x = np.asarray(inp['x'], np.float32)
    ctx = np.asarray(inp['ctx'], np.float32)
    c = np.asarray(inp['c'], np.float32)
    cc = np.asarray(inp['c_ctx'], np.float32)
    for ci in cores:
        bs = [ci * NBL + i for i in range(NBL)]
        m = dict(shared)
        m['xT'] = np.ascontiguousarray(x[bs].transpose(0, 2, 1))
        m['ctxT'] = np.ascontiguousarray(ctx[bs].transpose(0, 2, 1))
        m['cT'] = np.ascontiguousarray(np.stack([c[bs[0]], c[bs[1]], cc], axis=1))
        m['cst'] = cstn
        m['pv'] = pvn
        m['pv64'] = pv6
        m['rowv'] = rwn
        maps.append(m)
    return maps


def kernel(**inputs):
    nc, em = build()
    cores = list(range(NCORE))
    maps = make_in_maps(inputs, cores)
    res = run_bass_kernel_spmd(nc, maps, core_ids=cores)
    out = np.empty((NCORE * NBL, TX, D), np.float32)
    for ci in cores:
        o = res.results[ci]["outT"]
        out[ci * NBL:(ci + 1) * NBL] = o.transpose(0, 2, 1)
    return out
```

```python
import numpy as np
from contextlib import ExitStack
import concourse.bass as bass
import concourse.mybir as mybir
from concourse.bass_utils import run_bass_kernel_spmd

F32 = mybir.dt.float32
BF16 = mybir.dt.bfloat16
AF = mybir.ActivationFunctionType
ALU = mybir.AluOpType
AX = mybir.AxisListType

L = 2
D = 1024
TX = 2048
TC = 256
NBL = 2
NCORE = 8
CH = 128
DFF = 2816
NFF = 22
EPS = 1e-6
R_LN_EPS = 64e-5
R_DECAY_SCALE = 0.6065306597126334
GELU_C = 1.5957691216057308

PV_BMOD = 0
PV_GPRE1 = 48
PV_GPOST1 = 56
PV_GPRE2 = 64
PV_GPOST2 = 72
PV_MCW = 80
PV_MCB = 104
PV_FCW = 112
PV_FCB = 310
PV_MUXG0 = 332
PV_MUXG1 = 333
NPV = 334
P6_MURKV = 0
P6_MUWA = 24
P6_W0 = 28
P6_A0 = 44
P6_KK = 60
P6_KA = 68
P6_RK = 76
NPV64 = 84
RV_MNW = 0
RV_LNW = 512
RV_LNB = 1024
RV_MD = 1536
RV_DTB = 1544
RV_ALOG = 1560
NROW = 1576
C_ID = 0
C_UTI = 128
C_LTI = 256
C_UTS = 384
C_LTS = 512
C_ONE = 640
C_MP0 = 768
C_ME1 = 1024
C_ME2 = 1280
NCST = 1536


class Em:
    def __init__(self, nc, ndma=8):
        self.nc = nc
        self.engs = {'pe': nc.tensor, 'act': nc.scalar, 'dve': nc.vector, 'pool': nc.gpsimd, 'sp': nc.sync}
        self.sem = {}
        self.cnt = {}
        for k in ['pe', 'act', 'dve', 'pool']:
            self.sem[k] = nc.alloc_semaphore("sem_" + k)
            self.cnt[k] = 0
        self.dq = {}
        for q in ['sp', 'pool', 'act']:
            self.dq[q] = {'n': ndma, 'next': 0}
            for i in range(ndma):
                self.sem[f"d_{q}_{i}"] = nc.alloc_semaphore(f"dsem_{q}_{i}")
                self.cnt[f"d_{q}_{i}"] = 0
        self.seen = {k: {} for k in self.engs}
        self.lastw = {}
        self.readers = {}
        self.n = 0

    def _deps(self, reads, writes):
        deps = {}

        def add(d):
            if d is None:
                return
            k, v = d
            if deps.get(k, 0) < v:
                deps[k] = v
        for b in reads:
            add(self.lastw.get(b))
        for b in writes:
            add(self.lastw.get(b))
            for r in self.readers.get(b, ()):
                add(r)
        return deps

    def _waits(self, eng, deps):
        for k, v in deps.items():
            if k.startswith('d_'):
                v = self.cnt[k]
            if self.seen[eng].get(k, 0) >= v:
                continue
            self.seen[eng][k] = v
            self.engs[eng].wait_ge(self.sem[k], v)
            self.n += 1

    def _mark(self, me, reads, writes):
        for b in reads:
            self.readers.setdefault(b, []).append(me)
        for b in writes:
            self.lastw[b] = me
            self.readers[b] = []

    @staticmethod
    def _is_psum(k):
        return (k[0] == 'B' and (k[1:].isdigit() or k == 'BB')) or k.startswith('ps')

    def op(self, eng, fn, reads=(), writes=()):
        ex = [k for k in reads if self._is_psum(k)]
        self._waits(eng, self._deps(reads, list(writes) + ex))
        self.cnt[eng] += 1
        fn(self.engs[eng]).then_inc(self.sem[eng], 1)
        self._mark((eng, self.cnt[eng]), reads, writes)
        self.n += 1

    def dma(self, q, out, in_, reads=(), writes=(), **kw):
        self._waits(q, self._deps(reads, writes))
        d = self.dq[q]
        i = d['next']
        d['next'] = (i + 1) % d['n']
        k = f"d_{q}_{i}"
        self.cnt[k] += 16
        self.engs[q].dma_start(out=out, in_=in_, **kw).then_inc(self.sem[k], 16)
        self._mark((k, self.cnt[k]), reads, writes)
        self.n += 1

    def barrier(self):
        allv = {k: v for k, v in self.cnt.items() if v > 0}
        for e in self.engs:
            self._waits(e, dict(allv))

    def act(self, out, in_, func, r, w, bias=None, scale=None, accum=None):
        kw = {}
        if bias is not None:
            kw['bias'] = bias
        if scale is not None:
            kw['scale'] = scale
        if accum is not None:
            kw['accum_out'] = accum
        self.op('act', lambda e: e.activation(out=out, in_=in_, func=func, **kw), r, w)

    def tt(self, eng, out, a, b, op, r, w):
        self.op(eng, lambda e: e.tensor_tensor(out=out, in0=a, in1=b, op=op), r, w)

    def ts(self, eng, out, a, s1, s2, op0, op1, r, w):
        if s2 is None:
            self.op(eng, lambda e: e.tensor_scalar(out=out, in0=a, scalar1=s1, scalar2=None, op0=op0), r, w)
        else:
            self.op(eng, lambda e: e.tensor_scalar(out=out, in0=a, scalar1=s1, scalar2=s2, op0=op0, op1=op1), r, w)

    def stt(self, out, a, s, b, op0, op1, r, w):
        self.op('dve', lambda e: e.scalar_tensor_tensor(out=out, in0=a, scalar=s, in1=b, op0=op0, op1=op1), r, w)

    def mm(self, out, lhsT, rhs, start, stop, r, w):
        self.op('pe', lambda e: e.matmul(out, lhsT=lhsT, rhs=rhs, start=start, stop=stop), r, w)

    def tr(self, out, in_, ident, r, w):
        self.op('pe', lambda e: e.transpose(out, in_, ident), r, w)

    def cp(self, eng, out, in_, r, w):
        if eng == 'act':
            self.op('act', lambda e: e.activation(out=out, in_=in_, func=AF.Identity), r, w)
        else:
            self.op(eng, lambda e: e.tensor_copy(out=out, in_=in_), r, w)

    def memset(self, eng, ap, val, w):
        self.op(eng, lambda e: e.memset(ap, val), (), w)


def seqT(s):
    return TX if (s % 2) == 1 else TC


def build(debug=False, n_layers=L, stop_after=None, mixcfg=None):
    nc = bass.Bass("TRN2", target_bir_lowering=False)
    em = Em(nc)
    dbgset = debug if isinstance(debug, (set, list, tuple)) else None

    def din(name, shape, dt=F32):
        return nc.dram_tensor(name, list(shape), dt, kind="ExternalInput").ap()

    def dscr(name, shape, dt=F32):
        isdbg = (debug is True) or (dbgset is not None and name.rstrip('0123456789') in dbgset)
        return nc.dram_tensor(name, list(shape), dt, kind="ExternalOutput" if isdbg else "Internal").ap()

    xT = din("xT", [NBL, D, TX])
    ctxT = din("ctxT", [NBL, D, TC])
    cT = din("cT", [D, 3])
    w_mod = din("w_mod", [L, D, 6 * D])
    w_in = din("w_in", [L, D, 3504])
    w_out = din("w_out", [L, D, D])
    r_w2 = din("r_w2", [L, 2, 64, 512])
    r_a2 = din("r_a2", [L, 2, 64, 512])
    r_g2 = din("r_g2", [L, 160, 512])
    f_w_up = din("f_w_up", [L, D, 2 * DFF])
    f_w_down = din("f_w_down", [L, DFF, D])
    cst = din("cst", [128, NCST])
    pv = din("pv", [L, 128, NPV])
    pv64 = din("pv64", [L, 64, NPV64])
    rowv = din("rowv", [L, 1, NROW])
    outT = nc.dram_tensor("outT", [NBL, D, TX], F32, kind="ExternalOutput").ap()

    NS = 2 * NBL
    RESA = [dscr(f"resa{s}", [D, seqT(s)]) for s in range(NS)]
    RESB = [dscr(f"resb{s}", [D, seqT(s)]) for s in range(NS)]
    XBC = [dscr(f"xbc{s}", [D, seqT(s)]) for s in range(NS)]
    RKV = [dscr(f"rkv{s}", [24, 64, seqT(s)]) for s in range(NS)]
    XWA = [dscr(f"xwa{s}", [4, 64, seqT(s)]) for s in range(NS)]
    XG = [dscr(f"xg{s}", [160, seqT(s)]) for s in range(NS)]
    ZDT = [dscr(f"zdt{s}", [seqT(s), 528]) for s in range(NS)]
    YF = [dscr(f"yf{s}", [seqT(s), 512]) for s in range(NS)]
    OF = [dscr(f"of{s}", [seqT(s), 520]) for s in range(NS)]
    MIX = [dscr(f"mix{s}", [D, seqT(s)], BF16) for s in range(NS)]
    GATE = [dscr(f"gate{s}", [DFF, seqT(s)]) for s in range(NS)]
    VAL = [dscr(f"val{s}", [DFF, seqT(s)], BF16) for s in range(NS)]
    ACTV = [dscr(f"actv{s}", [DFF, seqT(s)], BF16) for s in range(NS)]

    def fm(ap):
        return ap.rearrange("(k p) t -> p k t", p=128)

    def sb(name, shape, dt=F32):
        return nc.alloc_sbuf_tensor(name, list(shape), dt).ap()

    CST = sb("CST", [128, NCST])
    CSTB = sb("CSTB", [128, NCST], BF16)
    PV = sb("PV", [128, NPV])
    PV64 = sb("PV64", [64, NPV64])
    ROWB = sb("ROWB", [128, NROW])
    MOD = sb("MOD", [128, 48, 3])
    DER = sb("DER", [128, 4, 8, 3])
    NEGA = sb("NEGA", [128, 16])
    OMMU = sb("OMMU", [64, 8])
    em.dma('sp', CST, cst, writes=['CST'])
    em.cp('dve', CSTB, CST, ['CST'], ['CSTB'])
    ident = CST[:, C_ID:C_ID + 128]
    identb = CSTB[:, C_ID:C_ID + 128]
    onesb = CSTB[:, C_ONE:C_ONE + 128]
    ones = CST[:, C_ONE:C_ONE + 128]

    def stage_scope():
        return ExitStack()

    def stage_mod(l):
        em.dma('sp', PV, pv[l], writes=['PV'])
        em.dma('sp', PV64, pv64[l], writes=['PV64'])
        em.dma('pool', ROWB, rowv[l].partition_broadcast(128), writes=['ROWB'])
        with ExitStack() as es:
            def S(name, shape, dt=F32):
                return es.enter_context(nc.sbuf_tensor(name, list(shape), dt)).ap()
            cts = S(f"cts{l}", [128, 8, 3])
            sc = S(f"sc{l}", [128, 8, 3])
            wst = [S(f"wmst{l}_{i}", [128, 8, 512]) for i in range(2)]
            ps = es.enter_context(nc.psum_tensor(f"psmod{l}", [128, 512], F32)).ap()
            em.dma('sp', cts, cT.rearrange("(k p) j -> p k j", p=128), writes=['cts'])
            em.act(sc, cts, AF.Silu, ['cts'], ['sc'])
            for g in range(12):
                w = wst[g % 2]
                wk = f"wmst{g % 2}"
                em.dma('sp' if g % 2 == 0 else 'pool', w,
                       w_mod[l][:, g * 512:(g + 1) * 512].rearrange("(k p) n -> p k n", p=128), writes=[wk])
                for mi in range(4):
                    m = g * 4 + mi
                    for k in range(8):
                        em.mm(ps[:, m * 3:(m + 1) * 3], w[:, k, mi * 128:(mi + 1) * 128], sc[:, k, :],
                              k == 0, k == 7, [wk, 'sc'], ['psmod'])
            em.tt('dve', MOD, ps[:, 0:144].rearrange("p (m j) -> p m j", j=3),
                  PV[:, PV_BMOD:PV_BMOD + 48].unsqueeze(2).to_broadcast([128, 48, 3]), ALU.add,
                  ['psmod', 'PV'], ['MOD'])
            tmp = S(f"dertmp{l}", [128, 8, 3])

            def gain(idx, goff, mlo, plus1):
                if plus1:
                    em.ts('dve', tmp, MOD[:, mlo:mlo + 8, :], 1.0, None, ALU.add, None, ['MOD'], ['dertmp'])
                    src = tmp
                    rk = ['dertmp', 'PV']
                else:
                    src = MOD[:, mlo:mlo + 8, :]
                    rk = ['MOD', 'PV']
                em.tt('dve', DER[:, idx, :, :], src,
                      PV[:, goff:goff + 8].unsqueeze(2).to_broadcast([128, 8, 3]), ALU.mult, rk, ['DER'])
            gain(0, PV_GPRE1, 8, True)
            gain(1, PV_GPOST1, 16, False)
            gain(2, PV_GPRE2, 32, True)
            gain(3, PV_GPOST2, 40, False)
            em.act(NEGA, ROWB[:, RV_ALOG:RV_ALOG + 16], AF.Exp, ['ROWB'], ['NEGA'])
            em.ts('dve', NEGA, NEGA, -1.0, None, ALU.mult, None, ['NEGA'], ['NEGA'])
            em.ts('dve', OMMU, PV64[:, P6_KA:P6_KA + 8], -1.0, 1.0, ALU.mult, ALU.add, ['PV64'], ['OMMU'])
            em.barrier()

    def prenorm(xt, TW, h, sq, rstd, ps, jmod, gidx, sidx, kx, kh, kps):
        em.act(sq[:, :, :TW], xt[:, :, :TW], AF.Square, [kx], ['sq'])
        for k in range(8):
            em.mm(ps[:, :TW], onesb, sq[:, k, :TW], k == 0, k == 7, ['sq', 'CSTB'], [kps])
        em.act(rstd[:, :TW], ps[:, :TW], AF.Sqrt, [kps], ['rstd'], bias=EPS, scale=1.0 / D)
        em.op('dve', lambda e: e.reciprocal(out=rstd[:, :TW], in_=rstd[:, :TW]), ['rstd'], ['rstd'])
        for k in range(8):
            em.stt(xt[:, k, :TW], xt[:, k, :TW], DER[:, gidx, k, jmod:jmod + 1], rstd[:, :TW], ALU.mult, ALU.mult,
                   [kx, 'rstd', 'DER'], [kx])
            em.act(h[:, k, :TW], xt[:, k, :TW], AF.Identity, [kx, 'MOD'], [kh],
                   bias=MOD[:, sidx + k, jmod:jmod + 1], scale=1.0)

    def load_wbf(es, l, wap, Kc, N, name, piece=None):
        wb = es.enter_context(nc.sbuf_tensor(f"{name}bf{l}", [128, Kc, N], BF16)).ap()
        with ExitStack() as e2:
            sts = [e2.enter_context(nc.sbuf_tensor(f"{name}st{l}_{i}", [128, N], F32)).ap() for i in range(2)]
            for k in range(Kc):
                st = sts[k % 2]
                sk = f"{name}st{k % 2}"
                em.dma('sp' if k % 2 == 0 else 'pool', st, wap[k * 128:(k + 1) * 128, :], writes=[sk])
                em.cp('act' if k % 2 == 0 else 'dve', wb[:, k, :], st, [sk], [name + 'bf'])
            em.barrier()
        return wb

    def res_src(l, s, phase):
        b = s // 2
        if phase == 0:
            if l == 0:
                return (xT[b] if s % 2 == 1 else ctxT[b]), f"in{s}"
            return RESB[s], f"resb{s}"
        return RESA[s], f"resa{s}"

    def res_dst(l, s, phase):
        b = s // 2
        if phase == 0:
            return RESA[s], f"resa{s}"
        if l == n_layers - 1 and s % 2 == 1:
            return outT[b], f"out{s}"
        return RESB[s], f"resb{s}"

    def stage_inproj(l):
        with ExitStack() as es:
            def S(name, shape, dt=F32):
                return es.enter_context(nc.sbuf_tensor(name, list(shape), dt)).ap()
            wb = load_wbf(es, l, w_in[l], 8, 3504, "win")
            xt = S(f"ipx{l}", [128, 8, 512])
            h = S(f"iph{l}", [128, 8, 512], BF16)
            sq = S(f"ipsq{l}", [128, 8, 512], BF16)
            rstd = S(f"iprs{l}", [128, 512])
            sta = [S(f"ipsta{l}_{i}", [128, 8, 512]) for i in range(2)]
            stw = S(f"ipstw{l}", [64, 4, 512])
            stg0 = S(f"ipstg0{l}", [128, 512])
            stg1 = S(f"ipstg1{l}", [32, 512])
            stz = S(f"ipstz{l}", [128, 4, 528])
            pss = [es.enter_context(nc.psum_tensor(f"ipps{l}_{i}", [128, 512], F32)).ap() for i in range(8)]
            groups = [('xbc', 512, 128, 8), ('r', 1552, 64, 8), ('k', 2064, 64, 8), ('v', 2576, 64, 8)]
            ev = 0
            for s in range(NS):
                T = seqT(s)
                TW = min(512, T)
                jmod = 2 if s % 2 == 0 else s // 2
                src, srck = res_src(l, s, 0)
                for tt_ in range(T // TW):
                    t0 = tt_ * TW
                    em.dma('sp', xt[:, :, :TW], fm(src)[:, :, t0:t0 + TW], reads=[srck], writes=['ipx'])
                    prenorm(xt, TW, h, sq, rstd, pss[7], jmod, 0, 0, 'ipx', 'iph', 'ps7')
                    pi = 0
                    for gi, (gname, c0, wdt, nb) in enumerate(groups):
                        st = sta[gi % 2]
                        stk = f"ipsta{gi % 2}"
                        for j in range(nb):
                            ps = pss[pi % 6]
                            pk = f"ps{pi % 6}"
                            pi += 1
                            cc = c0 + j * wdt
                            for k in range(8):
                                em.mm(ps[:wdt, :TW], wb[:, k, cc:cc + wdt], h[:, k, :TW], k == 0, k == 7,
                                      ['winbf', 'iph'], [pk])
                            em.cp('act' if ev % 2 == 0 else 'dve', st[:wdt, j, :TW], ps[:wdt, :TW], [pk], [stk])
                            ev += 1
                        if gname == 'xbc':
                            em.dma('pool', fm(XBC[s])[:, :, t0:t0 + TW], st[:, :, :TW], reads=[stk], writes=[f"xbc{s}"])
                        else:
                            jb = {'r': 0, 'k': 8, 'v': 16}[gname]
                            em.dma('pool', RKV[s][jb:jb + 8].rearrange("j p t -> p j t")[:, :, t0:t0 + TW],
                                   st[:64, :, :TW], reads=[stk], writes=[f"rkv{s}"])
                    for j in range(4):
                        ps = pss[pi % 6]
                        pk = f"ps{pi % 6}"
                        pi += 1
                        cc = 3088 + j * 64
                        for k in range(8):
                            em.mm(ps[:64, :TW], wb[:, k, cc:cc + 64], h[:, k, :TW], k == 0, k == 7, ['winbf', 'iph'], [pk])
                        em.cp('act' if ev % 2 == 0 else 'dve', stw[:, j, :TW], ps[:64, :TW], [pk], ['ipstw'])
                        ev += 1
                    em.dma('pool', XWA[s].rearrange("j p t -> p j t")[:, :, t0:t0 + TW], stw[:, :, :TW],
                           reads=['ipstw'], writes=[f"xwa{s}"])
                    for (cc, wdt, st, stk, r0) in [(3344, 128, stg0, 'ipstg0', 0), (3472, 32, stg1, 'ipstg1', 128)]:
                        ps = pss[pi % 6]
                        pk = f"ps{pi % 6}"
                        pi += 1
                        for k in range(8):
                            em.mm(ps[:wdt, :TW], wb[:, k, cc:cc + wdt], h[:, k, :TW], k == 0, k == 7, ['winbf', 'iph'], [pk])
                        em.cp('act' if ev % 2 == 0 else 'dve', st[:wdt, :TW], ps[:wdt, :TW], [pk], [stk])
                        ev += 1
                        em.dma('pool', XG[s][r0:r0 + wdt, t0:t0 + TW], st[:wdt, :TW], reads=[stk], writes=[f"xg{s}"])
                    for i in range(TW // 128):
                        ps = pss[pi % 6]
                        pk = f"ps{pi % 6}"
                        pi += 1
                        ps2 = pss[6]
                        for k in range(8):
                            em.mm(ps[:, 0:512], h[:, k, i * 128:(i + 1) * 128], wb[:, k, 0:512], k == 0, k == 7,
                                  ['winbf', 'iph'], [pk])
                        for k in range(8):
                            em.mm(ps2[:, 0:16], h[:, k, i * 128:(i + 1) * 128], wb[:, k, 1536:1552], k == 0, k == 7,
                                  ['winbf', 'iph'], ['ps6'])
                        em.cp('act', stz[:, i, 0:512], ps[:, 0:512], [pk], ['ipstz'])
                        em.cp('dve', stz[:, i, 512:528], ps2[:, 0:16], ['ps6'], ['ipstz'])
                    em.dma('pool', ZDT[s][t0:t0 + TW, :].rearrange("(i p) c -> p i c", p=128), stz[:, :TW // 128, :],
                           reads=['ipstz'], writes=[f"zdt{s}"])
            em.barrier()

    def stage_mixer(l, need_ctx_out):
        with ExitStack() as es:
            def S(name, shape, dt=F32):
                return es.enter_context(nc.sbuf_tensor(name, list(shape), dt)).ap()
            w2b = S(f"w2b{l}", [64, 2, 512], BF16)
            a2b = S(f"a2b{l}", [64, 2, 512], BF16)
            g2b0 = S(f"g2b0{l}", [128, 512], BF16)
            g2b1 = S(f"g2b1{l}", [32, 512], BF16)
            with ExitStack() as e2:
                t1 = e2.enter_context(nc.sbuf_tensor(f"lst1{l}", [64, 2, 512], F32)).ap()
                t2 = e2.enter_context(nc.sbuf_tensor(f"lst2{l}", [64, 2, 512], F32)).ap()
                t3 = e2.enter_context(nc.sbuf_tensor(f"lst3{l}", [128, 512], F32)).ap()
                t4 = e2.enter_context(nc.sbuf_tensor(f"lst4{l}", [32, 512], F32)).ap()
                em.dma('sp', t1, r_w2[l].rearrange("d r c -> r d c"), writes=['lst1'])
                em.dma('sp', t2, r_a2[l].rearrange("d r c -> r d c"), writes=['lst2'])
                em.dma('sp', t3, r_g2[l][0:128, :], writes=['lst3'])
                em.dma('sp', t4, r_g2[l][128:160, :], writes=['lst4'])
                em.cp('dve', w2b, t1, ['lst1'], ['w2b'])
                em.cp('dve', a2b, t2, ['lst2'], ['a2b'])
                em.cp('dve', g2b0, t3, ['lst3'], ['g2b'])
                em.cp('dve', g2b1, t4, ['lst4'], ['g2b'])
                em.barrier()
            MAR = [None, None]
            MARt = S(f"mar{l}", [128, 2, 256])
            em.cp('dve', MARt[:, 0, 0:128], CST[:, C_UTS:C_UTS + 128], ['CST'], ['MAR'])
            em.cp('dve', MARt[:, 0, 128:256], CST[:, C_UTI:C_UTI + 128], ['CST'], ['MAR'])
            em.cp('dve', MARt[:, 1, 0:128], CST[:, C_LTS:C_LTS + 128], ['CST'], ['MAR'])
            em.cp('dve', MARt[:, 1, 128:256], CST[:, C_LTI:C_LTI + 128], ['CST'], ['MAR'])
            MSO = S(f"mso{l}", [128, 2, 256])
            em.cp('dve', MSO[:, 0, 0:128], CST[:, C_LTS:C_LTS + 128], ['CST'], ['MSO'])
            em.cp('dve', MSO[:, 1, 0:128], CST[:, C_UTS:C_UTS + 128], ['CST'], ['MSO'])
            em.cp('dve', MSO[:, 0, 128:256], ones, ['CST'], ['MSO'])
            em.cp('dve', MSO[:, 1, 128:256], ones, ['CST'], ['MSO'])
            RMK = S(f"rmk{l}", [64, 8, 128])
            em.memset('pool', RMK, 1.0, ['RMK'])
            em.memset('pool', RMK[:, :, 0:1], 0.0, ['RMK'])

            def MI(d):
                return CST[:, C_UTI:C_UTI + 128] if d == 0 else CST[:, C_LTI:C_LTI + 128]

            def strictTS(d):
                return CST[:, C_LTS:C_LTS + 128] if d == 0 else CST[:, C_UTS:C_UTS + 128]

            xbc = S(f"m_xbc{l}", [128, 8, 130])
            cacc = S(f"m_cacc{l}", [128, 8, 128])
            bct = S(f"m_bct{l}", [128, 4, 128], BF16)
            xtok = S(f"m_xtok{l}", [128, 512])
            xtokb = S(f"m_xtokb{l}", [128, 512], BF16)
            btokb = S(f"m_btokb{l}", [128, 256], BF16)
            zdt = S(f"m_zdt{l}", [128, 528])
            dts = S(f"m_dts{l}", [128, 8])
            dta = S(f"m_dta{l}", [128, 8])
            sm = S(f"m_sm{l}", [128, 40])
            xw = S(f"m_xw{l}", [128, 512], BF16)
            l2 = [S(f"m_l2{l}_{i}", [128, 256]) for i in range(2)]
            Et = [S(f"m_E{l}_{i}", [128, 256]) for i in range(2)]
            LTt = [S(f"m_LT{l}_{i}", [128, 128]) for i in range(2)]
            STt = [S(f"m_ST{l}_{i}", [128, 128], BF16) for i in range(2)]
            CsT = [S(f"m_Cs{l}_{i}", [128, 128], BF16) for i in range(2)]
            hst = S(f"m_hst{l}", [128, 512])
            hstb = S(f"m_hstb{l}", [128, 512], BF16)
            yt = S(f"m_y{l}", [128, 512])
            yf = S(f"m_yf{l}", [128, 512])
            zs = S(f"m_zs{l}", [128, 512])
            ysq = S(f"m_ysq{l}", [128, 512])
            gst = S(f"m_gst{l}", [128, 4])
            mixo = S(f"mixo{l}", [128, 8, 128], BF16)
            raw = S(f"r_raw{l}", [64, 26, 130])
            ssum = S(f"r_ssum{l}", [64, 26, 128])
            pp = ssum
            xg0 = S(f"r_xg0{l}", [128, 130])
            xg1 = S(f"r_xg1{l}", [32, 130])
            sg0 = S(f"r_sg0{l}", [128, 128], BF16)
            sg1 = S(f"r_sg1{l}", [32, 128], BF16)
            xgt = S(f"r_xgt{l}", [128, 128])
            twb = S(f"r_twb{l}", [64, 2, 128], BF16)
            lw = S(f"r_lw{l}", [64, 8, 128])
            aa = S(f"r_aa{l}", [64, 8, 128])
            kk = S(f"r_kk{l}", [64, 8, 128])
            kd = S(f"r_kd{l}", [64, 8, 128])
            sqb = S(f"r_sqb{l}", [64, 8, 128], BF16)
            linc = S(f"r_linc{l}", [64, 8, 128])
            lex = S(f"r_lex{l}", [64, 8, 128])
            rinv = lex
            e1 = S(f"r_e1{l}", [64, 8, 128])
            e0 = S(f"r_e0{l}", [64, 8, 128])
            ei = S(f"r_ei{l}", [64, 8, 128])
            gC = S(f"r_gC{l}", [64, 8])
            tmpk = S(f"r_tmpk{l}", [64, 8, 128])
            AR = S(f"r_AR{l}", [64, 8, 256], BF16)
            BK = S(f"r_BK{l}", [64, 8, 2, 128], BF16)
            prodb = S(f"r_prod{l}", [64, 8, 128], BF16)
            vtok = S(f"r_vtok{l}", [128, 512])
            vtokb = S(f"r_vtokb{l}", [128, 512], BF16)
            BKtok = S(f"r_BKtok{l}", [128, 8, 2, 64], BF16)
            GB = S(f"r_GB{l}", [128, 8, 256], BF16)
            GK = S(f"r_GK{l}", [128, 8, 256], BF16)
            Pm = [S(f"r_P{l}_{i}", [128, 8, 128], BF16) for i in range(2)]
            Qm = [S(f"r_Q{l}_{i}", [128, 8, 128], BF16) for i in range(2)]
            Ym = [S(f"r_Y{l}_{i}", [128, 8, 128], BF16) for i in range(2)]
            E1m = S(f"r_E1{l}", [128, 8, 128], BF16)
            E2m = S(f"r_E2{l}", [128, 8, 128], BF16)
            Dm = S(f"r_D{l}", [128, 8, 128], BF16)
            Zm = S(f"r_Z{l}", [128, 8, 128], BF16)
            Wt = S(f"r_W{l}", [128, 512], BF16)
            Ut = S(f"r_U{l}", [128, 512], BF16)
            Hs = S(f"r_H{l}", [64, 512])
            Hb = S(f"r_Hb{l}", [64, 512], BF16)
            ot = S(f"r_o{l}", [128, 520])
            oft = S(f"r_of{l}", [128, 520])
            osq = ysq
            gn = S(f"r_gn{l}", [128, 40])
            B = [es.enter_context(nc.psum_tensor(f"mxps{l}_{i}", [128, 512], F32)).ap() for i in range(7)]
            BBp = es.enter_context(nc.psum_tensor(f"mxpsb{l}", [128, 1024], BF16)).ap()

            nbw = ROWB[:, RV_MNW:RV_MNW + 512]
            lnw = ROWB[:, RV_LNW:RV_LNW + 512]
            lnb = ROWB[:, RV_LNB:RV_LNB + 512]
            mdb = ROWB[:, RV_MD:RV_MD + 8]
            evc = [0]

            def evq():
                evc[0] += 1
                return 'act' if evc[0] % 2 == 0 else 'dve'

            def load_halo(tile, tk, srcap, srck, t0, T, nblk_dims):
                lo = max(t0 - 1, 0)
                hi = min(t0 + 129, T)
                o = lo - (t0 - 1)
                if nblk_dims:
                    em.dma('sp', tile[:, :, o:o + hi - lo], srcap[:, :, lo:hi], reads=[srck], writes=[tk])
                    if t0 == 0:
                        em.memset('pool', tile[:, :, 0:1], 0.0, [tk])
                    if t0 + 128 == T:
                        em.memset('pool', tile[:, :, 129:130], 0.0, [tk])
                else:
                    em.dma('sp', tile[:, o:o + hi - lo], srcap[:, lo:hi], reads=[srck], writes=[tk])
                    if t0 == 0:
                        em.memset('pool', tile[:, 0:1], 0.0, [tk])
                    if t0 + 128 == T:
                        em.memset('pool', tile[:, 129:130], 0.0, [tk])

            def mamba_chunk(s, t0, d, emit_out):
                T = seqT(s)
                load_halo(xbc, 'm_xbc', fm(XBC[s]), f"xbc{s}", t0, T, True)
                em.dma('pool', zdt, ZDT[s][t0:t0 + 128, :], reads=[f"zdt{s}"], writes=['m_zdt'])
                if (mixcfg or {}).get('mstop', 99) <= 1:
                    return
                for j in range(8):
                    em.ts('dve', cacc[:, j, :], xbc[:, j, 1:129], PV[:, PV_MCW + 8 + j:PV_MCW + 9 + j],
                          PV[:, PV_MCB + j:PV_MCB + j + 1], ALU.mult, ALU.add, ['m_xbc', 'PV'], ['m_cacc'])
                    em.stt(cacc[:, j, :], xbc[:, j, 0:128], PV[:, PV_MCW + j:PV_MCW + j + 1], cacc[:, j, :],
                           ALU.mult, ALU.add, ['m_xbc', 'PV', 'm_cacc'], ['m_cacc'])
                    em.stt(cacc[:, j, :], xbc[:, j, 2:130], PV[:, PV_MCW + 16 + j:PV_MCW + 17 + j], cacc[:, j, :],
                           ALU.mult, ALU.add, ['m_xbc', 'PV', 'm_cacc'], ['m_cacc'])
                em.act(cacc[:, 0:6, :], cacc[:, 0:6, :], AF.Silu, ['m_cacc'], ['m_cacc'])
                em.act(bct[:, 2:4, :], cacc[:, 6:8, :], AF.Silu, ['m_cacc'], ['m_bct'])
                em.cp('dve', bct[:, 0:2, :], cacc[:, 4:6, :], ['m_cacc'], ['m_bct'])
                if (mixcfg or {}).get('mstop', 99) <= 2:
                    return
                msub = (mixcfg or {}).get('msub', 9)
                for j in range(4):
                    em.tr(B[0][:, j * 128:(j + 1) * 128], cacc[:, j, :], ident, ['m_cacc', 'CST'], ['B0'])
                if msub >= 1:
                    for j in range(2):
                        em.tr(B[1][:, j * 128:(j + 1) * 128], cacc[:, 4 + j, :], ident, ['m_cacc', 'CST'], ['B1'])
                if msub >= 2 and msub != 33:
                    em.cp('dve', xtok, B[0], ['B0'], ['m_xtok'])
                if msub == 30:
                    em.cp('dve', xtokb, B[0], ['B0'], ['m_xtokb'])
                elif msub == 31:
                    em.cp('act', yt, B[0], ['B0'], ['m_y'])
                elif msub == 32:
                    em.cp('act', xtokb, xtok, ['m_xtok'], ['m_xtokb'])
                elif msub >= 3:
                    em.cp('act', xtokb, B[0], ['B0'], ['m_xtokb'])
                if msub >= 4:
                    em.cp('act', btokb, B[1][:, 0:256], ['B1'], ['m_btokb'])
                if (mixcfg or {}).get('mstop', 99) <= 3:
                    return
                em.tt('dve', dts, zdt[:, 512 + d * 8:520 + d * 8], ROWB[:, RV_DTB + d * 8:RV_DTB + d * 8 + 8], ALU.add,
                      ['m_zdt', 'ROWB'], ['m_dts'])
                em.act(dts, dts, AF.Exp, ['m_dts'], ['m_dts'])
                em.act(dts, dts, AF.Ln, ['m_dts'], ['m_dts'], bias=1.0, scale=1.0)
                em.tt('dve', dta, dts, NEGA[:, d * 8:d * 8 + 8], ALU.mult, ['m_dts', 'NEGA'], ['m_dta'])
                if (mixcfg or {}).get('mstop', 99) <= 4:
                    return
                em.mm(B[1][:, 256:264], MI(d), dta, True, True, ['CST', 'm_dta'], ['B1'])
                em.mm(B[1][:, 264:272], ones, dta, True, True, ['CST', 'm_dta'], ['B1'])
                em.cp('act', sm[:, 32:40], B[1][:, 264:272], ['B1'], ['m_sm'])
                em.tt('dve', sm[:, 24:32], sm[:, 32:40], B[1][:, 256:264], ALU.subtract, ['B1', 'm_sm'], ['m_sm'])
                em.act(sm[:, 0:8], sm[:, 24:32], AF.Exp, ['m_sm'], ['m_sm'])
                em.tt('dve', sm[:, 8:16], sm[:, 0:8], dts, ALU.mult, ['m_sm', 'm_dts'], ['m_sm'])
                em.act(sm[:, 16:24], B[1][:, 264:272], AF.Exp, ['B1'], ['m_sm'])
                em.tt('dve', xw.rearrange("p (h q) -> p h q", q=64), xtok.rearrange("p (h q) -> p h q", q=64),
                      sm[:, 8:16].unsqueeze(2).to_broadcast([128, 8, 64]), ALU.mult, ['m_xtok', 'm_sm'], ['m_xw'])
                if (mixcfg or {}).get('mstop', 99) <= 5:
                    return
                for g in range(2):
                    em.mm(B[2][:, g * 128:(g + 1) * 128], bct[:, g, :], bct[:, 2 + g, :], True, True, ['m_bct'], ['B2'])
                for h in range(8):
                    g = h // 4
                    i2 = h % 2
                    pe_ = B[3] if i2 == 0 else B[4]
                    pek = 'B3' if i2 == 0 else 'B4'
                    em.ts('dve', l2[i2], MSO[:, d, :], dta[:, h:h + 1], None, ALU.mult, None,
                          ['MSO', 'm_dta'], [f"m_l2{i2}"])
                    em.mm(pe_[:, 0:128], l2[i2][:, 0:128], MI(d), True, True, [f"m_l2{i2}", 'CST'], [pek])
                    em.mm(pe_[:, 128:256], l2[i2][:, 128:256], MI(d), True, True, [f"m_l2{i2}", 'CST'], [pek])
                    em.act(Et[i2], pe_[:, 0:256], AF.Exp, [pek], [f"m_E{i2}"])
                    em.stt(LTt[i2], Et[i2][:, 0:128], dts[:, h:h + 1], MI(d), ALU.mult, ALU.mult,
                           [f"m_E{i2}", 'm_dts', 'CST'], [f"m_LT{i2}"])
                    em.tt('dve', STt[i2], B[2][:, g * 128:(g + 1) * 128], LTt[i2], ALU.mult, ['B2', f"m_LT{i2}"],
                          [f"m_ST{i2}"])
                    em.tt('dve', CsT[i2], bct[:, 2 + g, :], Et[i2][:, 128:256], ALU.mult, ['m_bct', f"m_E{i2}"],
                          [f"m_Cs{i2}"])
                    em.mm(B[5][:, h * 64:(h + 1) * 64], STt[i2], xtokb[:, h * 64:(h + 1) * 64], True, False,
                          [f"m_ST{i2}", 'm_xtokb'], ['B5'])
                    em.mm(B[5][:, h * 64:(h + 1) * 64], CsT[i2], hstb[:, h * 64:(h + 1) * 64], False, True,
                          [f"m_Cs{i2}", 'm_hstb'], ['B5'])
                if (mixcfg or {}).get('mstop', 99) <= 6:
                    return
                for g in range(2):
                    em.mm(B[6][:, g * 256:(g + 1) * 256], btokb[:, g * 128:(g + 1) * 128], xw[:, g * 256:(g + 1) * 256],
                          True, True, ['m_btokb', 'm_xw'], ['B6'])
                em.tt('dve', hst.rearrange("p (h q) -> p h q", q=64), hst.rearrange("p (h q) -> p h q", q=64),
                      sm[:, 16:24].unsqueeze(2).to_broadcast([128, 8, 64]), ALU.mult, ['m_hst', 'm_sm', 'B5'], ['m_hst'])
                em.tt('dve', hst, hst, B[6], ALU.add, ['m_hst', 'B6'], ['m_hst'])
                em.cp('act', hstb, hst, ['m_hst', 'B5'], ['m_hstb'])
                if (mixcfg or {}).get('mstop', 99) <= 7:
                    return
                if d == 0:
                    if emit_out:
                        em.cp('act', yt, B[5], ['B5'], ['m_y'])
                        em.dma('pool', YF[s][t0:t0 + 128, :], yt, reads=['m_y'], writes=[f"yf{s}"])
                elif emit_out:
                    em.dma('sp', yf, YF[s][t0:t0 + 128, :], reads=[f"yf{s}"], writes=['m_yf'])
                    em.tt('dve', yt, B[5], yf, ALU.add, ['B5', 'm_yf'], ['m_y'])
                    em.tt('dve', yf.rearrange("p (h q) -> p h q", q=64), xtok.rearrange("p (h q) -> p h q", q=64),
                          mdb.unsqueeze(2).to_broadcast([128, 8, 64]), ALU.mult, ['m_xtok', 'ROWB', 'm_yf'], ['m_yf'])
                    em.tt('dve', yt, yt, yf, ALU.add, ['m_y', 'm_yf'], ['m_y'])
                    em.act(zs, zdt[:, 0:512], AF.Silu, ['m_zdt'], ['m_zs'])
                    em.tt('dve', yt, yt, zs, ALU.mult, ['m_y', 'm_zs'], ['m_y'])
                    for g in range(2):
                        em.act(ysq[:, g * 256:(g + 1) * 256], yt[:, g * 256:(g + 1) * 256], AF.Square, ['m_y'],
                               ['m_ysq', 'm_gst'], accum=gst[:, g:g + 1])
                    em.act(gst[:, 2:4], gst[:, 0:2], AF.Sqrt, ['m_gst'], ['m_gst'], bias=EPS, scale=1.0 / 256)
                    em.op('dve', lambda e: e.reciprocal(out=gst[:, 2:4], in_=gst[:, 2:4]), ['m_gst'], ['m_gst'])
                    for g in range(2):
                        em.stt(yt[:, g * 256:(g + 1) * 256], yt[:, g * 256:(g + 1) * 256], gst[:, 2 + g:3 + g],
                               nbw[:, g * 256:(g + 1) * 256], ALU.mult, ALU.mult, ['m_y', 'm_gst', 'ROWB'], ['m_y'])
                    for j in range(4):
                        em.tr(B[0][:, j * 128:(j + 1) * 128], yt[:, j * 128:(j + 1) * 128], ident, ['m_y', 'CST'], ['B0'])
                    em.cp('act', mixo[:, 0:4, :], B[0].rearrange("p (j t) -> p j t", t=128), ['B0'], ['mixo_m'])
                    em.dma('pool', fm(MIX[s])[:, 0:4, t0:t0 + 128], mixo[:, 0:4, :], reads=['mixo_m'], writes=[f"mix{s}"])

            def wkv_chunk(s, t0, d, emit_out):
                T = seqT(s)
                fin = (d == 1 and emit_out)
                rk = f"rkv{s}"
                load_halo(raw[:, 0:24, :], 'r_raw', RKV[s].rearrange("j p t -> p j t"), rk, t0, T, True)
                load_halo(raw[:, 24:25, :], 'r_raw', XWA[s][d:d + 1].rearrange("j p t -> p j t"), f"xwa{s}", t0, T, True)
                load_halo(raw[:, 25:26, :], 'r_raw', XWA[s][2 + d:3 + d].rearrange("j p t -> p j t"), f"xwa{s}", t0, T, True)
                em.tt('dve', ssum, raw[:, :, 0:128], raw[:, :, 2:130], ALU.add, ['r_raw'], ['r_ssum'])
                em.stt(ssum, ssum, 0.5, raw[:, :, 1:129], ALU.mult, ALU.subtract, ['r_ssum', 'r_raw'], ['r_ssum'])
                em.tt('dve', ssum[:, 0:24, :], ssum[:, 0:24, :],
                      PV64[:, P6_MURKV:P6_MURKV + 24].unsqueeze(2).to_broadcast([64, 24, 128]), ALU.mult,
                      ['r_ssum', 'PV64'], ['r_ssum'])
                em.ts('dve', ssum[:, 24, :], ssum[:, 24, :], PV64[:, P6_MUWA + d:P6_MUWA + d + 1], None, ALU.mult, None,
                      ['r_ssum', 'PV64'], ['r_ssum'])
                em.ts('dve', ssum[:, 25, :], ssum[:, 25, :], PV64[:, P6_MUWA + 2 + d:P6_MUWA + 3 + d], None, ALU.mult, None,
                      ['r_ssum', 'PV64'], ['r_ssum'])
                em.tt('dve', pp, raw[:, :, 1:129], ssum, ALU.add, ['r_raw', 'r_ssum'], ['r_ssum'])
                rr = pp[:, 0:8, :]
                kr = pp[:, 8:16, :]
                vr = pp[:, 16:24, :]
                em.act(twb[:, 0, :], pp[:, 24, :], AF.Tanh, ['r_ssum'], ['r_twb'])
                em.cp('dve', twb[:, 1, :], pp[:, 25, :], ['r_ssum'], ['r_twb'])
                for h in range(8):
                    bk_ = B[h // 4]
                    em.mm(bk_[0:64, (h % 4) * 128:(h % 4 + 1) * 128], w2b[:, d, h * 64:(h + 1) * 64], twb[:, 0, :], True, True,
                          ['w2b', 'r_twb'], [f"B{h // 4}"])
                for h in range(8):
                    bk_ = B[2 + h // 4]
                    em.mm(bk_[0:64, (h % 4) * 128:(h % 4 + 1) * 128], a2b[:, d, h * 64:(h + 1) * 64], twb[:, 1, :], True, True,
                          ['a2b', 'r_twb'], [f"B{2 + h // 4}"])
                for h in range(8):
                    em.act(lw[:, h, :], B[h // 4][0:64, (h % 4) * 128:(h % 4 + 1) * 128], AF.Sigmoid, [f"B{h // 4}", 'PV64'],
                           ['r_lw'], bias=PV64[:, P6_W0 + d * 8 + h:P6_W0 + d * 8 + h + 1], scale=1.0)
                    em.act(aa[:, h, :], B[2 + h // 4][0:64, (h % 4) * 128:(h % 4 + 1) * 128], AF.Sigmoid,
                           [f"B{2 + h // 4}", 'PV64'], ['r_aa'], bias=PV64[:, P6_A0 + d * 8 + h:P6_A0 + d * 8 + h + 1], scale=1.0)
                em.ts('dve', lw, lw, -R_DECAY_SCALE, None, ALU.mult, None, ['r_lw'], ['r_lw'])
                em.tt('dve', kk, kr, PV64[:, P6_KK:P6_KK + 8].unsqueeze(2).to_broadcast([64, 8, 128]), ALU.mult,
                      ['r_ssum', 'PV64'], ['r_kk'])
                em.act(sqb, kk, AF.Square, ['r_kk'], ['r_sqb'])
                for hh in range(2):
                    em.mm(B[4 + hh][0:64, :], onesb[0:64, 0:64], sqb[:, hh * 4:(hh + 1) * 4, :], True, True,
                          ['CSTB', 'r_sqb'], [f"B{4 + hh}"])
                for hh in range(2):
                    em.act(rinv[:, hh * 4:(hh + 1) * 4, :], B[4 + hh][0:64, :].rearrange("p (h t) -> p h t", t=128), AF.Sqrt,
                           [f"B{4 + hh}"], ['r_lex'])
                em.ts('dve', rinv, rinv, 1e-12, None, ALU.max, None, ['r_lex'], ['r_lex'])
                em.op('dve', lambda e: e.reciprocal(out=rinv, in_=rinv), ['r_lex'], ['r_lex'])
                em.tt('dve', kk, kk, rinv, ALU.mult, ['r_kk', 'r_lex'], ['r_kk'])
                em.tt('dve', tmpk, aa, PV64[:, P6_KA:P6_KA + 8].unsqueeze(2).to_broadcast([64, 8, 128]), ALU.mult,
                      ['r_aa', 'PV64'], ['r_tmpk'])
                em.tt('dve', tmpk, tmpk, OMMU.unsqueeze(2).to_broadcast([64, 8, 128]), ALU.add, ['r_tmpk', 'OMMU'], ['r_tmpk'])
                em.tt('dve', kd, kr, tmpk, ALU.mult, ['r_ssum', 'r_tmpk'], ['r_kd'])
                em.op('dve', lambda e: e.tensor_tensor_scan(out=linc.rearrange("p h t -> p (h t)"),
                                                            data0=RMK.rearrange("p h t -> p (h t)"),
                                                            data1=lw.rearrange("p h t -> p (h t)"), initial=0.0,
                                                            op0=ALU.mult, op1=ALU.add), ['RMK', 'r_lw'], ['r_linc'])
                if d == 0:
                    tot = linc[:, :, 127:128]
                else:
                    em.tt('dve', lex, lw, linc, ALU.subtract, ['r_lw', 'r_linc'], ['r_lex'])
                    em.cp('dve', gC, linc[:, :, 127], ['r_linc'], ['r_gC'])
                    em.tt('dve', linc, lex, gC.unsqueeze(2).to_broadcast([64, 8, 128]), ALU.add, ['r_lex', 'r_gC', 'r_linc'],
                          ['r_linc'])
                    tot = linc[:, :, 0:1]
                em.tt('dve', lex, linc, lw, ALU.subtract, ['r_linc', 'r_lw'], ['r_lex'])
                em.act(e1, linc, AF.Exp, ['r_linc'], ['r_e1'])
                em.act(e0, lex, AF.Exp, ['r_lex'], ['r_e0'])
                em.act(ei, linc, AF.Exp, ['r_linc'], ['r_ei'], scale=-1.0)
                em.act(gC, tot.rearrange("p h o -> p (h o)"), AF.Exp, ['r_linc', 'r_gC'], ['r_gC'])
                em.tt('dve', AR[:, :, 128:256], rr, e1, ALU.mult, ['r_ssum', 'r_e1'], ['r_AR'])
                em.stt(AR[:, :, 0:128], kk, -1.0, e0, ALU.mult, ALU.mult, ['r_kk', 'r_e0'], ['r_AR'])
                em.tt('dve', tmpk, kk, aa, ALU.mult, ['r_kk', 'r_aa', 'r_tmpk'], ['r_tmpk'])
                em.tt('dve', BK[:, :, 0, :], tmpk, ei, ALU.mult, ['r_tmpk', 'r_ei'], ['r_BK'])
                em.tt('dve', BK[:, :, 1, :], kd, ei, ALU.mult, ['r_kd', 'r_ei'], ['r_BK'])
                em.tt('dve', tmpk, rr, kd, ALU.mult, ['r_ssum', 'r_kd', 'r_tmpk'], ['r_tmpk'])
                em.tt('dve', prodb, tmpk, PV64[:, P6_RK:P6_RK + 8].unsqueeze(2).to_broadcast([64, 8, 128]), ALU.mult,
                      ['r_tmpk', 'PV64'], ['r_prod'])
                for h in range(8):
                    em.tr(B[6][:, h * 64:(h + 1) * 64], vr[:, h, :], ident[0:64, 0:64], ['r_ssum', 'CST'], ['B6'])
                em.cp('dve', vtok, B[6], ['B6'], ['r_vtok'])
                em.cp('act', vtokb, B[6], ['B6'], ['r_vtokb'])
                for h in range(8):
                    for q in range(2):
                        em.tr(BBp[:, (h * 2 + q) * 64:(h * 2 + q + 1) * 64], BK[:, h, q, :], identb[0:64, 0:64],
                              ['r_BK', 'CSTB'], ['BB'])
                em.cp('dve', BKtok.rearrange("p h q k -> p (h q k)"), BBp, ['BB'], ['r_BKtok'])
                for h in range(8):
                    em.mm(B[5][:, 256 + h:257 + h], prodb[:, h, :], onesb[0:64, 0:1], True, True, ['r_prod', 'CSTB'], ['B5'])
                em.cp('act', ot[:, 512:520], B[5][:, 256:264], ['B5'], ['r_ocoef'])
                for hp in range(4):
                    bb_ = B[hp % 2]
                    bbk = f"B{hp % 2}"
                    bk2 = B[2 + hp % 2]
                    bk2k = f"B{2 + hp % 2}"
                    for q in range(2):
                        h = hp * 2 + q
                        em.mm(bb_[:, q * 256:(q + 1) * 256], BK[:, h, 0, :], AR[:, h, :], True, True, ['r_BK', 'r_AR', 'r_lw', 'r_aa'], [bbk])
                        em.mm(bk2[:, q * 256:(q + 1) * 256], BK[:, h, 1, :], AR[:, h, :], True, True, ['r_BK', 'r_AR'], [bk2k])
                    em.tt('dve', GB[:, hp * 2:hp * 2 + 2, :], bb_.rearrange("p (q c) -> p q c", c=256),
                          MARt[:, d:d + 1, :].to_broadcast([128, 2, 256]), ALU.mult, [bbk, 'MAR'], ['r_GB'])
                    em.tt('dve', Qm[0][:, hp * 2:hp * 2 + 2, :], bb_.rearrange("p (q c) -> p q c", c=256)[:, :, 0:128],
                          CST[:, C_MP0 + (1 - d) * 128:C_MP0 + (2 - d) * 128].unsqueeze(1).to_broadcast([128, 2, 128]), ALU.mult,
                          [bbk, 'CST'], ['r_Q0'])
                    em.tt('dve', GK[:, hp * 2:hp * 2 + 2, :], bk2.rearrange("p (q c) -> p q c", c=256),
                          MARt[:, d:d + 1, :].to_broadcast([128, 2, 256]), ALU.mult, [bk2k, 'MAR'], ['r_GK'])
                for hh in range(2):
                    bp = B[4 + hh]
                    bpk = f"B{4 + hh}"
                    for q in range(4):
                        h = hh * 4 + q
                        em.mm(bp[:, q * 128:(q + 1) * 128], AR[:, h, 0:128], BK[:, h, 0, :], True, True, ['r_AR', 'r_BK'], [bpk])
                    b3 = bp.rearrange("p (q c) -> p q c", c=128)
                    hsl = slice(hh * 4, (hh + 1) * 4)
                    em.tt('dve', Pm[0][:, hsl, :], b3, CST[:, C_MP0 + d * 128:C_MP0 + (d + 1) * 128].unsqueeze(1).to_broadcast([128, 4, 128]),
                          ALU.mult, [bpk, 'CST'], ['r_P0'])
                    em.tt('dve', E1m[:, hsl, :], b3, CST[:, C_ME1 + d * 128:C_ME1 + (d + 1) * 128].unsqueeze(1).to_broadcast([128, 4, 128]),
                          ALU.mult, [bpk, 'CST'], ['r_E1'])
                    em.tt('dve', E2m[:, hsl, :], b3, CST[:, C_ME2 + d * 128:C_ME2 + (d + 1) * 128].unsqueeze(1).to_broadcast([128, 4, 128]),
                          ALU.mult, [bpk, 'CST'], ['r_E2'])
                em.tt('dve', Ym[0], Qm[0], identb.unsqueeze(1).to_broadcast([128, 8, 128]), ALU.add, ['r_Q0', 'CSTB'], ['r_Y0'])
                cur = 0
                for lev in range(1, 5):
                    nxt = 1 - cur
                    for hh in range(2):
                        bp = B[hh]
                        bpk = f"B{hh}"
                        bq = B[2 + hh]
                        bqk = f"B{2 + hh}"
                        hsl = slice(hh * 4, (hh + 1) * 4)
                        for q in range(4):
                            h = hh * 4 + q
                            em.mm(bp[:, q * 128:(q + 1) * 128], Qm[cur][:, h, :], Pm[cur][:, h, :], True, True,
                                  [f"r_Q{cur}", f"r_P{cur}"], [bpk])
                        for q in range(4):
                            h = hh * 4 + q
                            em.mm(bq[:, q * 128:(q + 1) * 128], Pm[cur][:, h, :], Qm[cur][:, h, :], True, True,
                                  [f"r_Q{cur}", f"r_P{cur}"], [bqk])
                        em.cp('act', Pm[nxt][:, hsl, :], bp.rearrange("p (q c) -> p q c", c=128), [bpk], [f"r_P{nxt}"])
                        em.cp('dve', Qm[nxt][:, hsl, :], bq.rearrange("p (q c) -> p q c", c=128), [bqk], [f"r_Q{nxt}"])
                    for hh in range(2):
                        by = B[4 + hh]
                        byk = f"B{4 + hh}"
                        hsl = slice(hh * 4, (hh + 1) * 4)
                        for q in range(4):
                            h = hh * 4 + q
                            em.mm(by[:, q * 128:(q + 1) * 128], Pm[nxt][:, h, :], Ym[cur][:, h, :], True, True,
                                  [f"r_P{nxt}", f"r_Y{cur}"], [byk])
                        em.tt('dve', Ym[nxt][:, hsl, :], by.rearrange("p (q c) -> p q c", c=128), Ym[cur][:, hsl, :], ALU.add,
                              [byk, f"r_Y{cur}"], [f"r_Y{nxt}"])
                    cur = nxt
                Dt = Ym[cur]
                dtk = f"r_Y{cur}"
                for st, (Em_, ek) in enumerate([(E1m, 'r_E1'), (E2m, 'r_E2')]):
                    oth = Ym[1 - cur]
                    othk = f"r_Y{1 - cur}"
                    for h in range(8):
                        em.tr(BBp[:, h * 128:(h + 1) * 128], Dt[:, h, :], identb, [dtk, 'CSTB'], ['BB'])
                    em.cp('act', Dm, BBp.rearrange("p (h c) -> p h c", c=128), ['BB'], ['r_D'])
                    for hh in range(2):
                        bz = B[hh]
                        bzk = f"B{hh}"
                        hsl = slice(hh * 4, (hh + 1) * 4)
                        for q in range(4):
                            h = hh * 4 + q
                            em.mm(bz[:, q * 128:(q + 1) * 128], Em_[:, h, :], Dt[:, h, :], True, True, [ek, dtk], [bzk])
                        em.cp('act' if hh == 0 else 'dve', Zm[:, hsl, :], bz.rearrange("p (q c) -> p q c", c=128), [bzk], ['r_Z'])
                    for hh in range(2):
                        by = B[2 + hh]
                        byk = f"B{2 + hh}"
                        hsl = slice(hh * 4, (hh + 1) * 4)
                        for q in range(4):
                            h = hh * 4 + q
                            em.mm(by[:, q * 128:(q + 1) * 128], Dm[:, h, :], Zm[:, h, :], True, True, ['r_D', 'r_Z'], [byk])
                        em.tt('dve', oth[:, hsl, :], by.rearrange("p (q c) -> p q c", c=128), Dt[:, hsl, :], ALU.add,
                              [byk, dtk], [othk])
                    cur = 1 - cur
                    Dt = Ym[cur]
                    dtk = f"r_Y{cur}"
                TT_ = Dt
                ttk = dtk
                for h in range(8):
                    hs_ = slice(h * 64, (h + 1) * 64)
                    em.mm(B[0][:, hs_], AR[:, h, 0:128], Hb[:, hs_], True, False, ['r_AR', 'r_Hb'], ['B0'])
                    em.mm(B[0][:, hs_], GK[:, h, 0:128], vtokb[:, hs_], False, True, ['r_GK', 'r_vtokb'], ['B0'])
                em.cp('act', Wt, B[0], ['B0'], ['r_W'])
                for h in range(8):
                    hs_ = slice(h * 64, (h + 1) * 64)
                    em.mm(B[1][:, hs_], TT_[:, h, :], Wt[:, hs_], True, True, [ttk, 'r_W'], ['B1'])
                em.cp('dve', Ut, B[1], ['B1'], ['r_U'])
                for h in range(8):
                    hs_ = slice(h * 64, (h + 1) * 64)
                    em.mm(B[2][:, hs_], AR[:, h, 128:256], Hb[:, hs_], True, False, ['r_AR', 'r_Hb'], ['B2'])
                    em.mm(B[2][:, hs_], GB[:, h, 128:256], Ut[:, hs_], False, False, ['r_GB', 'r_U'], ['B2'])
                    em.mm(B[2][:, hs_], GK[:, h, 128:256], vtokb[:, hs_], False, True, ['r_GK', 'r_vtokb'], ['B2'])
                for h in range(8):
                    hs_ = slice(h * 64, (h + 1) * 64)
                    em.mm(B[3][0:64, hs_], BKtok[:, h, 0, :], Ut[:, hs_], True, False, ['r_BKtok', 'r_U'], ['B3'])
                    em.mm(B[3][0:64, hs_], BKtok[:, h, 1, :], vtokb[:, hs_], False, True, ['r_BKtok', 'r_vtokb'], ['B3'])
                em.tt('dve', Hs, Hs, B[3][0:64, :], ALU.add, ['r_H', 'B3'], ['r_H'])
                em.tt('dve', Hs.rearrange("p (h v) -> p h v", v=64), Hs.rearrange("p (h v) -> p h v", v=64),
                      gC.unsqueeze(2).to_broadcast([64, 8, 64]), ALU.mult, ['r_H', 'r_gC'], ['r_H'])
                em.cp('act', Hb, Hs, ['r_H', 'B0', 'B2'], ['r_Hb'])
                if d == 0:
                    if emit_out:
                        em.cp('act', ot[:, 0:512], B[2], ['B2'], ['r_o'])
                        em.dma('pool', OF[s][t0:t0 + 128, :], ot, reads=['r_o', 'r_ocoef'], writes=[f"of{s}"])
                elif emit_out:
                    em.dma('sp', oft, OF[s][t0:t0 + 128, :], reads=[f"of{s}"], writes=['r_of'])
                    em.tt('dve', ot[:, 0:512], B[2], oft[:, 0:512], ALU.add, ['B2', 'r_of'], ['r_o'])
                    o3 = ot[:, 0:512].rearrange("p (h v) -> p h v", v=64)
                    em.op('dve', lambda e: e.tensor_reduce(out=gn[:, 0:8], in_=o3, axis=AX.X, op=ALU.add), ['r_o'], ['r_gn'])
                    em.act(osq, ot[:, 0:512], AF.Square, ['r_o'], ['m_ysq'])
                    em.op('dve', lambda e: e.tensor_reduce(out=gn[:, 8:16], in_=osq.rearrange("p (h v) -> p h v", v=64),
                                                           axis=AX.X, op=ALU.add), ['m_ysq', 'r_gn'], ['r_gn'])
                    em.ts('dve', gn[:, 16:24], gn[:, 0:8], 1.0 / 64, None, ALU.mult, None, ['r_gn'], ['r_gn'])
                    em.tt('dve', gn[:, 0:8], gn[:, 16:24], gn[:, 16:24], ALU.mult, ['r_gn'], ['r_gn'])
                    em.stt(gn[:, 24:32], gn[:, 8:16], 1.0 / 64, gn[:, 0:8], ALU.mult, ALU.subtract, ['r_gn'], ['r_gn'])
                    em.act(gn[:, 24:32], gn[:, 24:32], AF.Sqrt, ['r_gn'], ['r_gn'], bias=R_LN_EPS, scale=1.0)
                    em.op('dve', lambda e: e.reciprocal(out=gn[:, 24:32], in_=gn[:, 24:32]), ['r_gn'], ['r_gn'])
                    em.tt('dve', o3, o3, gn[:, 16:24].unsqueeze(2).to_broadcast([128, 8, 64]), ALU.subtract, ['r_o', 'r_gn'], ['r_o'])
                    em.tt('dve', o3, o3, gn[:, 24:32].unsqueeze(2).to_broadcast([128, 8, 64]), ALU.mult, ['r_o', 'r_gn'], ['r_o'])
                    em.tt('dve', ot[:, 0:512], ot[:, 0:512], lnw, ALU.mult, ['r_o', 'ROWB'], ['r_o'])
                    em.tt('dve', ot[:, 0:512], ot[:, 0:512], lnb, ALU.add, ['r_o', 'ROWB'], ['r_o'])
                    em.tt('dve', gn[:, 32:40], ot[:, 512:520], oft[:, 512:520], ALU.add, ['r_ocoef', 'r_of', 'r_gn'], ['r_gn'])
                    em.tt('dve', osq.rearrange("p (h v) -> p h v", v=64), vtok.rearrange("p (h v) -> p h v", v=64),
                          gn[:, 32:40].unsqueeze(2).to_broadcast([128, 8, 64]), ALU.mult, ['r_vtok', 'r_gn', 'm_ysq'], ['m_ysq'])
                    em.tt('dve', ot[:, 0:512], ot[:, 0:512], osq, ALU.add, ['r_o', 'm_ysq'], ['r_o'])
                    load_halo(xg0, 'r_xg0', XG[s][0:128, :], f"xg{s}", t0, T, False)
                    load_halo(xg1, 'r_xg1', XG[s][128:160, :], f"xg{s}", t0, T, False)
                    for (xg_, sg_, np_, mucol, kx_, ks_) in [(xg0, sg0, 128, PV_MUXG0, 'r_xg0', 'r_sg0'),
                                                             (xg1, sg1, 32, PV_MUXG1, 'r_xg1', 'r_sg1')]:
                        em.tt('dve', xgt[:np_, :], xg_[:np_, 0:128], xg_[:np_, 2:130], ALU.add, [kx_], ['r_xgt'])
                        em.stt(xgt[:np_, :], xgt[:np_, :], 0.5, xg_[:np_, 1:129], ALU.mult, ALU.subtract, ['r_xgt', kx_], ['r_xgt'])
                        em.stt(xgt[:np_, :], xgt[:np_, :], PV[:np_, mucol:mucol + 1], xg_[:np_, 1:129], ALU.mult, ALU.add,
                               ['r_xgt', kx_, 'PV'], ['r_xgt'])
                        em.act(sg_[:np_, :], xgt[:np_, :], AF.Sigmoid, ['r_xgt'], [ks_])
                    em.mm(B[4], sg0, g2b0, True, False, ['r_sg0', 'g2b'], ['B4'])
                    em.mm(B[4], sg1, g2b1, False, True, ['r_sg1', 'g2b'], ['B4'])
                    em.tt('dve', ot[:, 0:512], ot[:, 0:512], B[4], ALU.mult, ['r_o', 'B4'], ['r_o'])
                    for j in range(4):
                        em.tr(B[5][:, j * 128:(j + 1) * 128], ot[:, j * 128:(j + 1) * 128], ident, ['r_o', 'CST'], ['B5'])
                    em.cp('act', mixo[:, 4:8, :], B[5].rearrange("p (j t) -> p j t", t=128), ['B5'], ['mixo_r'])
                    em.dma('pool', fm(MIX[s])[:, 4:8, t0:t0 + 128], mixo[:, 4:8, :], reads=['mixo_r'], writes=[f"mix{s}"])

            mc_ = mixcfg or {}
            for b in range(mc_.get('nb', NBL)):
                for d in range(mc_.get('nd', 2)):
                    em.memset('pool', hst, 0.0, ['m_hst'])
                    em.memset('pool', hstb, 0.0, ['m_hstb'])
                    em.memset('pool', Hs, 0.0, ['r_H'])
                    em.memset('pool', Hb, 0.0, ['r_Hb'])
                    for kind in range(mc_.get('nkind', 2)):
                        s = b * 2 + kind
                        T = seqT(s)
                        nch = T // CH
                        order = range(nch) if d == 0 else range(nch - 1, -1, -1)
                        emit = (kind == 1) or need_ctx_out or mc_.get('ctxout', False)
                        for c in order:
                            if mc_.get('mamba', True):
                                mamba_chunk(s, c * CH, d, emit)
                            if mc_.get('wkv', True):
                                wkv_chunk(s, c * CH, d, emit)
            em.barrier()

    def stage_proj_post(l, phase, wap, Kc, SRC, srcname, gidx, seqs):
        with ExitStack() as es:
            def S(name, shape, dt=F32):
                return es.enter_context(nc.sbuf_tensor(name, list(shape), dt)).ap()
            nm = f"pp{phase}"
            wb = load_wbf(es, l, wap, Kc, D, nm + "w")
            a = S(f"{nm}a{l}", [128, Kc, 512], BF16)
            xt = S(f"{nm}x{l}", [128, 8, 512])
            y = S(f"{nm}y{l}", [128, 8, 512])
            sq = S(f"{nm}sq{l}", [128, 8, 512], BF16)
            rstd = S(f"{nm}rs{l}", [128, 512])
            pss = [es.enter_context(nc.psum_tensor(f"{nm}ps{l}_{i}", [128, 512], F32)).ap() for i in range(5)]
            for s in seqs:
                T = seqT(s)
                TW = min(512, T)
                jmod = 2 if s % 2 == 0 else s // 2
                rsrc, rsk = res_src(l, s, phase)
                rdst, rdk = res_dst(l, s, phase)
                for tt_ in range(T // TW):
                    t0 = tt_ * TW
                    em.dma('sp', a[:, :, :TW], fm(SRC[s])[:, :, t0:t0 + TW], reads=[f"{srcname}{s}"], writes=[nm + 'a'])
                    em.dma('pool', xt[:, :, :TW], fm(rsrc)[:, :, t0:t0 + TW], reads=[rsk], writes=[nm + 'x'])
                    for m in range(8):
                        ps = pss[m % 4]
                        pk = f"ps{m % 4}"
                        for k in range(Kc):
                            em.mm(ps[:, :TW], wb[:, k, m * 128:(m + 1) * 128], a[:, k, :TW], k == 0, k == Kc - 1,
                                  [nm + 'wbf', nm + 'a'], [pk])
                        em.cp('dve', y[:, m, :TW], ps[:, :TW], [pk], [nm + 'y'])
                        em.act(sq[:, m, :TW], ps[:, :TW], AF.Square, [pk], [nm + 'sq'])
                    for m in range(8):
                        em.mm(pss[4][:, :TW], onesb, sq[:, m, :TW], m == 0, m == 7, ['CSTB', nm + 'sq'], ['ps4'])
                    em.act(rstd[:, :TW], pss[4][:, :TW], AF.Sqrt, ['ps4'], [nm + 'rs'], bias=EPS, scale=1.0 / D)
                    em.op('dve', lambda e: e.reciprocal(out=rstd[:, :TW], in_=rstd[:, :TW]), [nm + 'rs'], [nm + 'rs'])
                    for m in range(8):
                        em.stt(y[:, m, :TW], y[:, m, :TW], DER[:, gidx, m, jmod:jmod + 1], rstd[:, :TW], ALU.mult, ALU.mult,
                               [nm + 'y', nm + 'rs', 'DER'], [nm + 'y'])
                    em.tt('dve', xt[:, :, :TW], xt[:, :, :TW], y[:, :, :TW], ALU.add, [nm + 'x', nm + 'y'], [nm + 'x'])
                    em.dma('pool', fm(rdst)[:, :, t0:t0 + TW], xt[:, :, :TW], reads=[nm + 'x'], writes=[rdk])
            em.barrier()

    def stage_ffn_up(l, seqs):
        with ExitStack() as es:
            def S(name, shape, dt=F32):
                return es.enter_context(nc.sbuf_tensor(name, list(shape), dt)).ap()
            wb = load_wbf(es, l, f_w_up[l], 8, 2 * DFF, "wup")
            xt = S(f"fux{l}", [128, 8, 512])
            h = S(f"fuh{l}", [128, 8, 512], BF16)
            sq = S(f"fusq{l}", [128, 8, 512], BF16)
            rstd = S(f"furs{l}", [128, 512])
            stg = [S(f"fustg{l}_{i}", [128, 512]) for i in range(2)]
            stv = [S(f"fustv{l}_{i}", [128, 512], BF16) for i in range(2)]
            pss = [es.enter_context(nc.psum_tensor(f"fups{l}_{i}", [128, 512], F32)).ap() for i in range(8)]
            for s in seqs:
                T = seqT(s)
                TW = min(512, T)
                jmod = 2 if s % 2 == 0 else s // 2
                src, srck = res_src(l, s, 1)
                for tt_ in range(T // TW):
                    t0 = tt_ * TW
                    em.dma('sp', xt[:, :, :TW], fm(src)[:, :, t0:t0 + TW], reads=[srck], writes=['fux'])
                    prenorm(xt, TW, h, sq, rstd, pss[7], jmod, 2, 24, 'fux', 'fuh', 'ps7')
                    for j in range(NFF):
                        pg = pss[(2 * j) % 6]
                        pgk = f"ps{(2 * j) % 6}"
                        pv_ = pss[(2 * j + 1) % 6]
                        pvk = f"ps{(2 * j + 1) % 6}"
                        for k in range(8):
                            em.mm(pg[:, :TW], wb[:, k, j * 128:(j + 1) * 128], h[:, k, :TW], k == 0, k == 7, ['wupbf', 'fuh'], [pgk])
                        for k in range(8):
                            em.mm(pv_[:, :TW], wb[:, k, DFF + j * 128:DFF + (j + 1) * 128], h[:, k, :TW], k == 0, k == 7,
                                  ['wupbf', 'fuh'], [pvk])
                        sg_ = stg[j % 2]
                        sv_ = stv[j % 2]
                        em.cp('dve', sg_[:, :TW], pg[:, :TW], [pgk], [f"fustg{j % 2}"])
                        em.cp('act', sv_[:, :TW], pv_[:, :TW], [pvk], [f"fustv{j % 2}"])
                        em.dma('pool', GATE[s][j * 128:(j + 1) * 128, t0:t0 + TW], sg_[:, :TW], reads=[f"fustg{j % 2}"],
                               writes=[f"gate{s}"])
                        em.dma('sp', VAL[s][j * 128:(j + 1) * 128, t0:t0 + TW], sv_[:, :TW], reads=[f"fustv{j % 2}"],
                               writes=[f"val{s}"])
            em.barrier()

    def stage_ffn_conv(l, seqs):
        with ExitStack() as es:
            def S(name, shape, dt=F32):
                return es.enter_context(nc.sbuf_tensor(name, list(shape), dt)).ap()
            gflat = [S(f"fcg{l}_{i}", [128, 2048]) for i in range(2)]
            vflat = [S(f"fcv{l}_{i}", [128, 2048], BF16) for i in range(2)]
            gpx = S(f"fcgpx{l}", [128, 34, 66])
            gpc = S(f"fcgpc{l}", [128, 3, 258])
            acc = S(f"fcacc{l}", [128, 2048])
            acc2 = S(f"fcacc2{l}", [128, 2048])
            u = S(f"fcu{l}", [128, 2048])
            ab = [S(f"fcab{l}_{i}", [128, 2048], BF16) for i in range(2)]
            em.memset('pool', gpx, 0.0, ['fcgpx'])
            em.memset('pool', gpc, 0.0, ['fcgpc'])
            it = 0
            for s in seqs:
                T = seqT(s)
                if s % 2 == 1:
                    R, Cc, gp, gpk = 32, 64, gpx, 'fcgpx'
                else:
                    R, Cc, gp, gpk = 1, 256, gpc, 'fcgpc'
                for j in range(NFF):
                    i2 = it % 2
                    it += 1
                    gf = gflat[i2]
                    vf = vflat[i2]
                    em.dma('sp', gf[:, :T], GATE[s][j * 128:(j + 1) * 128, :], reads=[f"gate{s}"], writes=[f"fcg{i2}"])
                    em.dma('sp', vf[:, :T], VAL[s][j * 128:(j + 1) * 128, :], reads=[f"val{s}"], writes=[f"fcv{i2}"])
                    em.cp('pool', gp[:, 1:1 + R, 1:1 + Cc], gf[:, :T].rearrange("p (r c) -> p r c", c=Cc), [f"fcg{i2}"], [gpk])
                    a3 = acc[:, :T].rearrange("p (r c) -> p r c", c=Cc)
                    for tap in range(9):
                        dr, dc = tap // 3 - 1, tap % 3 - 1
                        src_ = gp[:, 1 + dr:1 + dr + R, 1 + dc:1 + dc + Cc]
                        wcol = PV[:, PV_FCW + tap * NFF + j:PV_FCW + tap * NFF + j + 1]
                        if tap == 0:
                            em.ts('dve', a3, src_, wcol, PV[:, PV_FCB + j:PV_FCB + j + 1], ALU.mult, ALU.add,
                                  [gpk, 'PV'], ['fcacc'])
                        else:
                            em.stt(a3, src_, wcol, a3, ALU.mult, ALU.add, [gpk, 'PV', 'fcacc'], ['fcacc'])
                    em.act(u[:, :T], acc[:, :T], AF.Square, ['fcacc'], ['fcu'])
                    em.ts('dve', u[:, :T], u[:, :T], 0.044715, 1.0, ALU.mult, ALU.add, ['fcu'], ['fcu'])
                    em.tt('dve', u[:, :T], u[:, :T], acc[:, :T], ALU.mult, ['fcu', 'fcacc'], ['fcu'])
                    em.act(u[:, :T], u[:, :T], AF.Sigmoid, ['fcu'], ['fcu'], scale=GELU_C)
                    em.tt('dve', u[:, :T], u[:, :T], acc[:, :T], ALU.mult, ['fcu', 'fcacc'], ['fcu'])
                    em.tt('dve', ab[i2][:, :T], u[:, :T], vf[:, :T], ALU.mult, ['fcu', f"fcv{i2}"], [f"fcab{i2}"])
                    em.dma('pool', ACTV[s][j * 128:(j + 1) * 128, :], ab[i2][:, :T], reads=[f"fcab{i2}"], writes=[f"actv{s}"])
            em.barrier()

    allseq = list(range(NS))
    xseq = [s for s in range(NS) if s % 2 == 1]
    outkeys = []
    for l in range(n_layers):
        last = (l == n_layers - 1)
        stage_mod(l)
        stage_inproj(l)
        if stop_after == 'inproj':
            break
        stage_mixer(l, need_ctx_out=not last)
        if stop_after == 'mixer':
            break
        seqs = xseq if last else allseq
        stage_proj_post(l, 0, w_out[l], 8, MIX, "mix", 1, seqs)
        if stop_after == 'outproj':
            break
        stage_ffn_up(l, seqs)
        stage_ffn_conv(l, seqs)
        stage_proj_post(l, 1, f_w_down[l], NFF, ACTV, "actv", 3, seqs)
    em.barrier()
    return nc, em


def host_prep(inp):
    f = np.float32
    idx = np.arange(128)
    cstn = np.zeros((128, NCST), f)
    cstn[:, C_ID:C_ID + 128] = np.eye(128)
    cstn[:, C_UTI:C_UTI + 128] = (idx[:, None] <= idx[None, :])
    cstn[:, C_LTI:C_LTI + 128] = (idx[:, None] >= idx[None, :])
    cstn[:, C_UTS:C_UTS + 128] = (idx[:, None] < idx[None, :])
    cstn[:, C_LTS:C_LTS + 128] = (idx[:, None] > idx[None, :])
    cstn[:, C_ONE:C_ONE + 128] = 1.0
    b32 = idx // 32
    b64 = idx // 64
    same32 = b32[:, None] == b32[None, :]
    same64 = b64[:, None] == b64[None, :]
    for d in range(2):
        strict = (idx[:, None] > idx[None, :]) if d == 0 else (idx[:, None] < idx[None, :])
        cstn[:, C_MP0 + d * 128:C_MP0 + (d + 1) * 128] = strict & same32
        cstn[:, C_ME1 + d * 128:C_ME1 + (d + 1) * 128] = strict & same64 & (~same32)
        cstn[:, C_ME2 + d * 128:C_ME2 + (d + 1) * 128] = strict & (~same64)
    pvn = np.zeros((L, 128, NPV), f)
    pv6 = np.zeros((L, 64, NPV64), f)
    rwn = np.zeros((L, 1, NROW), f)
    for l in range(L):
        pvn[l, :, PV_BMOD:PV_BMOD + 48] = inp['b_mod'][l].reshape(48, 128).T
        pvn[l, :, PV_GPRE1:PV_GPRE1 + 8] = inp['g_mix_pre'][l].reshape(8, 128).T
        pvn[l, :, PV_GPOST1:PV_GPOST1 + 8] = inp['g_mix_post'][l].reshape(8, 128).T
        pvn[l, :, PV_GPRE2:PV_GPRE2 + 8] = inp['g_ffn_pre'][l].reshape(8, 128).T
        pvn[l, :, PV_GPOST2:PV_GPOST2 + 8] = inp['g_ffn_post'][l].reshape(8, 128).T
        pvn[l, :, PV_MCW:PV_MCW + 24] = inp['m_conv_w'][l].reshape(3, 8, 128).transpose(2, 0, 1).reshape(128, 24)
        pvn[l, :, PV_MCB:PV_MCB + 8] = inp['m_conv_b'][l].reshape(8, 128).T
        pvn[l, :, PV_FCW:PV_FCW + 198] = inp['f_conv_w'][l].reshape(9, NFF, 128).transpose(2, 0, 1).reshape(128, 198)
        pvn[l, :, PV_FCB:PV_FCB + NFF] = inp['f_conv_b'][l].reshape(NFF, 128).T
        mu = inp['r_mu'][l]
        pvn[l, :, PV_MUXG0] = mu[1792:1920]
        pvn[l, 0:32, PV_MUXG1] = mu[1920:1952]
        pv6[l, :, P6_MURKV:P6_MURKV + 24] = mu[0:1536].reshape(24, 64).T
        pv6[l, :, P6_MUWA:P6_MUWA + 4] = mu[1536:1792].reshape(4, 64).T
        pv6[l, :, P6_W0:P6_W0 + 16] = inp['r_w0'][l].reshape(2, 8, 64).transpose(2, 0, 1).reshape(64, 16)
        pv6[l, :, P6_A0:P6_A0 + 16] = inp['r_a0'][l].reshape(2, 8, 64).transpose(2, 0, 1).reshape(64, 16)
        pv6[l, :, P6_KK:P6_KK + 8] = inp['r_k_k'][l].reshape(8, 64).T
        pv6[l, :, P6_KA:P6_KA + 8] = inp['r_k_a'][l].reshape(8, 64).T
        pv6[l, :, P6_RK:P6_RK + 8] = inp['r_r_k'][l].T
        rwn[l, 0, RV_MNW:RV_MNW + 512] = inp['m_norm_w'][l]
        rwn[l, 0, RV_LNW:RV_LNW + 512] = inp['r_ln_w'][l]
        rwn[l, 0, RV_LNB:RV_LNB + 512] = inp['r_ln_b'][l]
        rwn[l, 0, RV_MD:RV_MD + 8] = inp['m_d'][l]
        rwn[l, 0, RV_DTB:RV_DTB + 16] = inp['m_dt_bias'][l].reshape(16)
        rwn[l, 0, RV_ALOG:RV_ALOG + 16] = inp['m_a_log'][l].reshape(16)
    return cstn, pvn, pv6, rwn


def make_in_maps(inp, cores):
    cstn, pvn, pv6, rwn = host_prep(inp)
    shared = {k: np.ascontiguousarray(np.asarray(inp[k], dtype=np.float32)) for k in
              ['w_mod', 'w_in', 'w_out', 'r_w2', 'r_a2', 'r_g2', 'f_w_up', 'f_w_down']}
    maps = []
    x = np.asarray(inp['x'], np.float32)
    ctx = np.asarray(inp['ctx'], np.float32)
    c = np.asarray(inp['c'], np.float32)
    cc = np.asarray(inp['c_ctx'], np.float32)
    for ci in cores:
        bs = [ci * NBL + i for i in range(NBL)]
        m = dict(shared)
        m['xT'] = np.ascontiguousarray(x[bs].transpose(0, 2, 1))
        m['ctxT'] = np.ascontiguousarray(ctx[bs].transpose(0, 2, 1))
        m['cT'] = np.ascontiguousarray(np.stack([c[bs[0]], c[bs[1]], cc], axis=1))
        m['cst'] = cstn
        m['pv'] = pvn
        m['pv64'] = pv6
        m['rowv'] = rwn
        maps.append(m)
    return maps


def kernel(**inputs):
    nc, em = build()
    cores = list(range(NCORE))
    maps = make_in_maps(inputs, cores)
    res = run_bass_kernel_spmd(nc, maps, core_ids=cores)
    out = np.empty((NCORE * NBL, TX, D), np.float32)
    for ci in cores:
        o = res.results[ci]["outT"]
        out[ci * NBL:(ci + 1) * NBL] = o.transpose(0, 2, 1)
    return out
```

```python
import numpy as np
from contextlib import ExitStack
import concourse.bass as bass
import concourse.mybir as mybir
from concourse.bass_utils import run_bass_kernel_spmd

F32 = mybir.dt.float32
BF16 = mybir.dt.bfloat16
AF = mybir.ActivationFunctionType
ALU = mybir.AluOpType
AX = mybir.AxisListType

L = 2
D = 1024
TX = 2048
TC = 256
NBL = 2
NCORE = 8
CH = 128
DFF = 2816
NFF = 22
EPS = 1e-6
R_LN_EPS = 64e-5
R_DECAY_SCALE = 0.6065306597126334
GELU_C = 1.5957691216057308

PV_BMOD = 0
PV_GPRE1 = 48
PV_GPOST1 = 56
PV_GPRE2 = 64
PV_GPOST2 = 72
PV_MCW = 80
PV_MCB = 104
PV_FCW = 112
PV_FCB = 310
PV_MUXG0 = 332
PV_MUXG1 = 333
NPV = 334
P6_MURKV = 0
P6_MUWA = 24
P6_W0 = 28
P6_A0 = 44
P6_KK = 60
P6_KA = 68
P6_RK = 76
NPV64 = 84
RV_MNW = 0
RV_LNW = 512
RV_LNB = 1024
RV_MD = 1536
RV_DTB = 1544
RV_ALOG = 1560
NROW = 1576
C_ID = 0
C_UTI = 128
C_LTI = 256
C_UTS = 384
C_LTS = 512
C_ONE = 640
C_MP0 = 768
C_ME1 = 1024
C_ME2 = 1280
NCST = 1536


class Em:
    def __init__(self, nc, ndma=8):
        self.nc = nc
        self.engs = {'pe': nc.tensor, 'act': nc.scalar, 'dve': nc.vector, 'pool': nc.gpsimd, 'sp': nc.sync}
        self.sem = {}
        self.cnt = {}
        for k in ['pe', 'act', 'dve', 'pool']:
            self.sem[k] = nc.alloc_semaphore("sem_" + k)
            self.cnt[k] = 0
        self.dq = {}
        for q in ['sp', 'pool', 'act']:
            self.dq[q] = {'n': ndma, 'next': 0}
            for i in range(ndma):
                self.sem[f"d_{q}_{i}"] = nc.alloc_semaphore(f"dsem_{q}_{i}")
                self.cnt[f"d_{q}_{i}"] = 0
        self.seen = {k: {} for k in self.engs}
        self.lastw = {}
        self.readers = {}
        self.n = 0
        self.stream = None
        self.queues = {}

    def _deps(self, reads, writes):
        deps = {}

        def add(d):
            if d is None:
                return
            k, v = d
            if deps.get(k, 0) < v:
                deps[k] = v
        for b in reads:
            add(self.lastw.get(b))
        for b in writes:
            add(self.lastw.get(b))
            for r in self.readers.get(b, ()):
                add(r)
        return deps

    def _waits(self, eng, deps):
        for k, v in deps.items():
            if k.startswith('d_'):
                v = self.cnt[k]
            if self.seen[eng].get(k, 0) >= v:
                continue
            self.seen[eng][k] = v
            self.engs[eng].wait_ge(self.sem[k], v)
            self.n += 1

    def _mark(self, me, reads, writes):
        for b in reads:
            self.readers.setdefault(b, []).append(me)
        for b in writes:
            self.lastw[b] = me
            self.readers[b] = []

    @staticmethod
    def _is_psum(k):
        return (k[0] == 'B' and (k[1:].isdigit() or k == 'BB')) or k.startswith('ps')

    def flush(self):
        qs = {k: v for k, v in self.queues.items() if v}
        self.queues = {}
        pos = {k: 0 for k in qs}
        while qs:
            k = min(qs, key=lambda n: pos[n] / len(qs[n]))
            it = qs[k][pos[k]]
            pos[k] += 1
            if it[0] == 'op':
                self.op(*it[1:])
            else:
                self.dma(it[1], it[2], it[3], it[4], it[5], **it[6])
            if pos[k] >= len(qs[k]):
                del qs[k]

    def op(self, eng, fn, reads=(), writes=()):
        if self.stream is not None:
            self.queues.setdefault(self.stream, []).append(('op', eng, fn, tuple(reads), tuple(writes)))
            return
        ex = [k for k in reads if self._is_psum(k)]
        self._waits(eng, self._deps(reads, list(writes) + ex))
        self.cnt[eng] += 1
        fn(self.engs[eng]).then_inc(self.sem[eng], 1)
        self._mark((eng, self.cnt[eng]), reads, writes)
        self.n += 1

    def dma(self, q, out, in_, reads=(), writes=(), **kw):
        if self.stream is not None:
            self.queues.setdefault(self.stream, []).append(('dma', q, out, in_, tuple(reads), tuple(writes), kw))
            return
        self._waits(q, self._deps(reads, writes))
        d = self.dq[q]
        i = d['next']
        d['next'] = (i + 1) % d['n']
        k = f"d_{q}_{i}"
        self.cnt[k] += 16
        self.engs[q].dma_start(out=out, in_=in_, **kw).then_inc(self.sem[k], 16)
        self._mark((k, self.cnt[k]), reads, writes)
        self.n += 1

    def barrier(self):
        allv = {k: v for k, v in self.cnt.items() if v > 0}
        for e in self.engs:
            self._waits(e, dict(allv))

    def act(self, out, in_, func, r, w, bias=None, scale=None, accum=None):
        kw = {}
        if bias is not None:
            kw['bias'] = bias
        if scale is not None:
            kw['scale'] = scale
        if accum is not None:
            kw['accum_out'] = accum
        self.op('act', lambda e: e.activation(out=out, in_=in_, func=func, **kw), r, w)

    def tt(self, eng, out, a, b, op, r, w):
        self.op(eng, lambda e: e.tensor_tensor(out=out, in0=a, in1=b, op=op), r, w)

    def ts(self, eng, out, a, s1, s2, op0, op1, r, w):
        if s2 is None:
            self.op(eng, lambda e: e.tensor_scalar(out=out, in0=a, scalar1=s1, scalar2=None, op0=op0), r, w)
        else:
            self.op(eng, lambda e: e.tensor_scalar(out=out, in0=a, scalar1=s1, scalar2=s2, op0=op0, op1=op1), r, w)

    def stt(self, out, a, s, b, op0, op1, r, w):
        self.op('dve', lambda e: e.scalar_tensor_tensor(out=out, in0=a, scalar=s, in1=b, op0=op0, op1=op1), r, w)

    def mm(self, out, lhsT, rhs, start, stop, r, w):
        self.op('pe', lambda e: e.matmul(out, lhsT=lhsT, rhs=rhs, start=start, stop=stop), r, w)

    def tr(self, out, in_, ident, r, w):
        self.op('pe', lambda e: e.transpose(out, in_, ident), r, w)

    def cp(self, eng, out, in_, r, w):
        if eng == 'act':
            self.op('act', lambda e: e.activation(out=out, in_=in_, func=AF.Identity), r, w)
        else:
            self.op(eng, lambda e: e.tensor_copy(out=out, in_=in_), r, w)

    def memset(self, eng, ap, val, w):
        self.op(eng, lambda e: e.memset(ap, val), (), w)


def seqT(s):
    return TX if (s % 2) == 1 else TC


def build(debug=False, n_layers=L, stop_after=None, mixcfg=None):
    nc = bass.Bass("TRN2", target_bir_lowering=False)
    em = Em(nc)
    dbgset = debug if isinstance(debug, (set, list, tuple)) else None

    def din(name, shape, dt=F32):
        return nc.dram_tensor(name, list(shape), dt, kind="ExternalInput").ap()

    def dscr(name, shape, dt=F32):
        isdbg = (debug is True) or (dbgset is not None and name.rstrip('0123456789') in dbgset)
        return nc.dram_tensor(name, list(shape), dt, kind="ExternalOutput" if isdbg else "Internal").ap()

    xT = din("xT", [NBL, D, TX])
    ctxT = din("ctxT", [NBL, D, TC])
    cT = din("cT", [D, 3])
    w_mod = din("w_mod", [L, D, 6 * D])
    w_in = din("w_in", [L, D, 3504])
    w_out = din("w_out", [L, D, D])
    r_w2 = din("r_w2", [L, 2, 64, 512])
    r_a2 = din("r_a2", [L, 2, 64, 512])
    r_g2 = din("r_g2", [L, 160, 512])
    f_w_up = din("f_w_up", [L, D, 2 * DFF])
    f_w_down = din("f_w_down", [L, DFF, D])
    cst = din("cst", [128, NCST])
    pv = din("pv", [L, 128, NPV])
    pv64 = din("pv64", [L, 64, NPV64])
    rowv = din("rowv", [L, 1, NROW])
    outT = nc.dram_tensor("outT", [NBL, D, TX], F32, kind="ExternalOutput").ap()

    NS = 2 * NBL
    RESA = [dscr(f"resa{s}", [D, seqT(s)]) for s in range(NS)]
    RESB = [dscr(f"resb{s}", [D, seqT(s)]) for s in range(NS)]
    XBC = [dscr(f"xbc{s}", [D, seqT(s)]) for s in range(NS)]
    RKV = [dscr(f"rkv{s}", [24, 64, seqT(s)]) for s in range(NS)]
    XWA = [dscr(f"xwa{s}", [4, 64, seqT(s)]) for s in range(NS)]
    XG = [dscr(f"xg{s}", [160, seqT(s)]) for s in range(NS)]
    ZDT = [dscr(f"zdt{s}", [seqT(s), 528]) for s in range(NS)]
    YF = [dscr(f"yf{s}", [seqT(s), 512]) for s in range(NS)]
    OF = [dscr(f"of{s}", [seqT(s), 520]) for s in range(NS)]
    MIX = [dscr(f"mix{s}", [D, seqT(s)], BF16) for s in range(NS)]
    GATE = [dscr(f"gate{s}", [DFF, seqT(s)]) for s in range(NS)]
    VAL = [dscr(f"val{s}", [DFF, seqT(s)], BF16) for s in range(NS)]
    ACTV = [dscr(f"actv{s}", [DFF, seqT(s)], BF16) for s in range(NS)]

    def fm(ap):
        return ap.rearrange("(k p) t -> p k t", p=128)

    def sb(name, shape, dt=F32):
        return nc.alloc_sbuf_tensor(name, list(shape), dt).ap()

    CST = sb("CST", [128, NCST])
    CSTB = sb("CSTB", [128, NCST], BF16)
    PV = sb("PV", [128, NPV])
    PV64 = sb("PV64", [64, NPV64])
    ROWB = sb("ROWB", [128, NROW])
    MOD = sb("MOD", [128, 48, 3])
    DER = sb("DER", [128, 4, 8, 3])
    NEGA = sb("NEGA", [128, 16])
    OMMU = sb("OMMU", [64, 8])
    em.dma('sp', CST, cst, writes=['CST'])
    em.cp('dve', CSTB, CST, ['CST'], ['CSTB'])
    ident = CST[:, C_ID:C_ID + 128]
    identb = CSTB[:, C_ID:C_ID + 128]
    onesb = CSTB[:, C_ONE:C_ONE + 128]
    ones = CST[:, C_ONE:C_ONE + 128]

    def stage_scope():
        return ExitStack()

    def stage_mod(l):
        em.dma('sp', PV, pv[l], writes=['PV'])
        em.dma('sp', PV64, pv64[l], writes=['PV64'])
        em.dma('pool', ROWB, rowv[l].partition_broadcast(128), writes=['ROWB'])
        with ExitStack() as es:
            def S(name, shape, dt=F32):
                return es.enter_context(nc.sbuf_tensor(name, list(shape), dt)).ap()
            cts = S(f"cts{l}", [128, 8, 3])
            sc = S(f"sc{l}", [128, 8, 3])
            wst = [S(f"wmst{l}_{i}", [128, 8, 512]) for i in range(2)]
            ps = es.enter_context(nc.psum_tensor(f"psmod{l}", [128, 512], F32)).ap()
            em.dma('sp', cts, cT.rearrange("(k p) j -> p k j", p=128), writes=['cts'])
            em.act(sc, cts, AF.Silu, ['cts'], ['sc'])
            for g in range(12):
                w = wst[g % 2]
                wk = f"wmst{g % 2}"
                em.dma('sp' if g % 2 == 0 else 'pool', w,
                       w_mod[l][:, g * 512:(g + 1) * 512].rearrange("(k p) n -> p k n", p=128), writes=[wk])
                for mi in range(4):
                    m = g * 4 + mi
                    for k in range(8):
                        em.mm(ps[:, m * 3:(m + 1) * 3], w[:, k, mi * 128:(mi + 1) * 128], sc[:, k, :],
                              k == 0, k == 7, [wk, 'sc'], ['psmod'])
            em.tt('dve', MOD, ps[:, 0:144].rearrange("p (m j) -> p m j", j=3),
                  PV[:, PV_BMOD:PV_BMOD + 48].unsqueeze(2).to_broadcast([128, 48, 3]), ALU.add,
                  ['psmod', 'PV'], ['MOD'])
            tmp = S(f"dertmp{l}", [128, 8, 3])

            def gain(idx, goff, mlo, plus1):
                if plus1:
                    em.ts('dve', tmp, MOD[:, mlo:mlo + 8, :], 1.0, None, ALU.add, None, ['MOD'], ['dertmp'])
                    src = tmp
                    rk = ['dertmp', 'PV']
                else:
                    src = MOD[:, mlo:mlo + 8, :]
                    rk = ['MOD', 'PV']
                em.tt('dve', DER[:, idx, :, :], src,
                      PV[:, goff:goff + 8].unsqueeze(2).to_broadcast([128, 8, 3]), ALU.mult, rk, ['DER'])
            gain(0, PV_GPRE1, 8, True)
            gain(1, PV_GPOST1, 16, False)
            gain(2, PV_GPRE2, 32, True)
            gain(3, PV_GPOST2, 40, False)
            em.act(NEGA, ROWB[:, RV_ALOG:RV_ALOG + 16], AF.Exp, ['ROWB'], ['NEGA'])
            em.ts('dve', NEGA, NEGA, -1.0, None, ALU.mult, None, ['NEGA'], ['NEGA'])
            em.ts('dve', OMMU, PV64[:, P6_KA:P6_KA + 8], -1.0, 1.0, ALU.mult, ALU.add, ['PV64'], ['OMMU'])
            em.barrier()

    def prenorm(xt, TW, h, sq, rstd, ps, jmod, gidx, sidx, kx, kh, kps):
        em.act(sq[:, :, :TW], xt[:, :, :TW], AF.Square, [kx], ['sq'])
        for k in range(8):
            em.mm(ps[:, :TW], onesb, sq[:, k, :TW], k == 0, k == 7, ['sq', 'CSTB'], [kps])
        em.act(rstd[:, :TW], ps[:, :TW], AF.Sqrt, [kps], ['rstd'], bias=EPS, scale=1.0 / D)
        em.op('dve', lambda e: e.reciprocal(out=rstd[:, :TW], in_=rstd[:, :TW]), ['rstd'], ['rstd'])
        for k in range(8):
            em.stt(xt[:, k, :TW], xt[:, k, :TW], DER[:, gidx, k, jmod:jmod + 1], rstd[:, :TW], ALU.mult, ALU.mult,
                   [kx, 'rstd', 'DER'], [kx])
            em.act(h[:, k, :TW], xt[:, k, :TW], AF.Identity, [kx, 'MOD'], [kh],
                   bias=MOD[:, sidx + k, jmod:jmod + 1], scale=1.0)

    def load_wbf(es, l, wap, Kc, N, name, piece=None):
        wb = es.enter_context(nc.sbuf_tensor(f"{name}bf{l}", [128, Kc, N], BF16)).ap()
        with ExitStack() as e2:
            sts = [e2.enter_context(nc.sbuf_tensor(f"{name}st{l}_{i}", [128, N], F32)).ap() for i in range(2)]
            for k in range(Kc):
                st = sts[k % 2]
                sk = f"{name}st{k % 2}"
                em.dma('sp' if k % 2 == 0 else 'pool', st, wap[k * 128:(k + 1) * 128, :], writes=[sk])
                em.cp('act' if k % 2 == 0 else 'dve', wb[:, k, :], st, [sk], [name + 'bf'])
            em.barrier()
        return wb

    def res_src(l, s, phase):
        b = s // 2
        if phase == 0:
            if l == 0:
                return (xT[b] if s % 2 == 1 else ctxT[b]), f"in{s}"
            return RESB[s], f"resb{s}"
        return RESA[s], f"resa{s}"

    def res_dst(l, s, phase):
        b = s // 2
        if phase == 0:
            return RESA[s], f"resa{s}"
        if l == n_layers - 1 and s % 2 == 1:
            return outT[b], f"out{s}"
        return RESB[s], f"resb{s}"

    def stage_inproj(l):
        with ExitStack() as es:
            def S(name, shape, dt=F32):
                return es.enter_context(nc.sbuf_tensor(name, list(shape), dt)).ap()
            wb = load_wbf(es, l, w_in[l], 8, 3504, "win")
            xt = S(f"ipx{l}", [128, 8, 512])
            h = S(f"iph{l}", [128, 8, 512], BF16)
            sq = S(f"ipsq{l}", [128, 8, 512], BF16)
            rstd = S(f"iprs{l}", [128, 512])
            sta = [S(f"ipsta{l}_{i}", [128, 8, 512]) for i in range(2)]
            stw = S(f"ipstw{l}", [64, 4, 512])
            stg0 = S(f"ipstg0{l}", [128, 512])
            stg1 = S(f"ipstg1{l}", [32, 512])
            stz = S(f"ipstz{l}", [128, 4, 528])
            pss = [es.enter_context(nc.psum_tensor(f"ipps{l}_{i}", [128, 512], F32)).ap() for i in range(8)]
            groups = [('xbc', 512, 128, 8), ('r', 1552, 64, 8), ('k', 2064, 64, 8), ('v', 2576, 64, 8)]
            ev = 0
            for s in range(NS):
                T = seqT(s)
                TW = min(512, T)
                jmod = 2 if s % 2 == 0 else s // 2
                src, srck = res_src(l, s, 0)
                for tt_ in range(T // TW):
                    t0 = tt_ * TW
                    em.dma('sp', xt[:, :, :TW], fm(src)[:, :, t0:t0 + TW], reads=[srck], writes=['ipx'])
                    prenorm(xt, TW, h, sq, rstd, pss[7], jmod, 0, 0, 'ipx', 'iph', 'ps7')
                    pi = 0
                    for gi, (gname, c0, wdt, nb) in enumerate(groups):
                        st = sta[gi % 2]
                        stk = f"ipsta{gi % 2}"
                        for j in range(nb):
                            ps = pss[pi % 6]
                            pk = f"ps{pi % 6}"
                            pi += 1
                            cc = c0 + j * wdt
                            for k in range(8):
                                em.mm(ps[:wdt, :TW], wb[:, k, cc:cc + wdt], h[:, k, :TW], k == 0, k == 7,
                                      ['winbf', 'iph'], [pk])
                            em.cp('act' if ev % 2 == 0 else 'dve', st[:wdt, j, :TW], ps[:wdt, :TW], [pk], [stk])
                            ev += 1
                        if gname == 'xbc':
                            em.dma('pool', fm(XBC[s])[:, :, t0:t0 + TW], st[:, :, :TW], reads=[stk], writes=[f"xbc{s}"])
                        else:
                            jb = {'r': 0, 'k': 8, 'v': 16}[gname]
                            em.dma('pool', RKV[s][jb:jb + 8].rearrange("j p t -> p j t")[:, :, t0:t0 + TW],
                                   st[:64, :, :TW], reads=[stk], writes=[f"rkv{s}"])
                    for j in range(4):
                        ps = pss[pi % 6]
                        pk = f"ps{pi % 6}"
                        pi += 1
                        cc = 3088 + j * 64
                        for k in range(8):
                            em.mm(ps[:64, :TW], wb[:, k, cc:cc + 64], h[:, k, :TW], k == 0, k == 7, ['winbf', 'iph'], [pk])
                        em.cp('act' if ev % 2 == 0 else 'dve', stw[:, j, :TW], ps[:64, :TW], [pk], ['ipstw'])
                        ev += 1
                    em.dma('pool', XWA[s].rearrange("j p t -> p j t")[:, :, t0:t0 + TW], stw[:, :, :TW],
                           reads=['ipstw'], writes=[f"xwa{s}"])
                    for (cc, wdt, st, stk, r0) in [(3344, 128, stg0, 'ipstg0', 0), (3472, 32, stg1, 'ipstg1', 128)]:
                        ps = pss[pi % 6]
                        pk = f"ps{pi % 6}"
                        pi += 1
                        for k in range(8):
                            em.mm(ps[:wdt, :TW], wb[:, k, cc:cc + wdt], h[:, k, :TW], k == 0, k == 7, ['winbf', 'iph'], [pk])
                        em.cp('act' if ev % 2 == 0 else 'dve', st[:wdt, :TW], ps[:wdt, :TW], [pk], [stk])
                        ev += 1
                        em.dma('pool', XG[s][r0:r0 + wdt, t0:t0 + TW], st[:wdt, :TW], reads=[stk], writes=[f"xg{s}"])
                    for i in range(TW // 128):
                        ps = pss[pi % 6]
                        pk = f"ps{pi % 6}"
                        pi += 1
                        ps2 = pss[6]
                        for k in range(8):
                            em.mm(ps[:, 0:512], h[:, k, i * 128:(i + 1) * 128], wb[:, k, 0:512], k == 0, k == 7,
                                  ['winbf', 'iph'], [pk])
                        for k in range(8):
                            em.mm(ps2[:, 0:16], h[:, k, i * 128:(i + 1) * 128], wb[:, k, 1536:1552], k == 0, k == 7,
                                  ['winbf', 'iph'], ['ps6'])
                        em.cp('act', stz[:, i, 0:512], ps[:, 0:512], [pk], ['ipstz'])
                        em.cp('dve', stz[:, i, 512:528], ps2[:, 0:16], ['ps6'], ['ipstz'])
                    em.dma('pool', ZDT[s][t0:t0 + TW, :].rearrange("(i p) c -> p i c", p=128), stz[:, :TW // 128, :],
                           reads=['ipstz'], writes=[f"zdt{s}"])
            em.barrier()

    def stage_mixer(l, need_ctx_out):
        with ExitStack() as es:
            def S(name, shape, dt=F32):
                return es.enter_context(nc.sbuf_tensor(name, list(shape), dt)).ap()
            w2b = S(f"w2b{l}", [64, 2, 512], BF16)
            a2b = S(f"a2b{l}", [64, 2, 512], BF16)
            g2b0 = S(f"g2b0{l}", [128, 512], BF16)
            g2b1 = S(f"g2b1{l}", [32, 512], BF16)
            with ExitStack() as e2:
                t1 = e2.enter_context(nc.sbuf_tensor(f"lst1{l}", [64, 2, 512], F32)).ap()
                t2 = e2.enter_context(nc.sbuf_tensor(f"lst2{l}", [64, 2, 512], F32)).ap()
                t3 = e2.enter_context(nc.sbuf_tensor(f"lst3{l}", [128, 512], F32)).ap()
                t4 = e2.enter_context(nc.sbuf_tensor(f"lst4{l}", [32, 512], F32)).ap()
                em.dma('sp', t1, r_w2[l].rearrange("d r c -> r d c"), writes=['lst1'])
                em.dma('sp', t2, r_a2[l].rearrange("d r c -> r d c"), writes=['lst2'])
                em.dma('sp', t3, r_g2[l][0:128, :], writes=['lst3'])
                em.dma('sp', t4, r_g2[l][128:160, :], writes=['lst4'])
                em.cp('dve', w2b, t1, ['lst1'], ['w2b'])
                em.cp('dve', a2b, t2, ['lst2'], ['a2b'])
                em.cp('dve', g2b0, t3, ['lst3'], ['g2b'])
                em.cp('dve', g2b1, t4, ['lst4'], ['g2b'])
                em.barrier()
            MAR = [None, None]
            MARt = S(f"mar{l}", [128, 2, 256])
            em.cp('dve', MARt[:, 0, 0:128], CST[:, C_UTS:C_UTS + 128], ['CST'], ['MAR'])
            em.cp('dve', MARt[:, 0, 128:256], CST[:, C_UTI:C_UTI + 128], ['CST'], ['MAR'])
            em.cp('dve', MARt[:, 1, 0:128], CST[:, C_LTS:C_LTS + 128], ['CST'], ['MAR'])
            em.cp('dve', MARt[:, 1, 128:256], CST[:, C_LTI:C_LTI + 128], ['CST'], ['MAR'])
            MSO = S(f"mso{l}", [128, 2, 256])
            em.cp('dve', MSO[:, 0, 0:128], CST[:, C_LTS:C_LTS + 128], ['CST'], ['MSO'])
            em.cp('dve', MSO[:, 1, 0:128], CST[:, C_UTS:C_UTS + 128], ['CST'], ['MSO'])
            em.cp('dve', MSO[:, 0, 128:256], ones, ['CST'], ['MSO'])
            em.cp('dve', MSO[:, 1, 128:256], ones, ['CST'], ['MSO'])
            RMK = S(f"rmk{l}", [64, 8, 128])
            em.memset('pool', RMK, 1.0, ['RMK'])
            em.memset('pool', RMK[:, :, 0:1], 0.0, ['RMK'])

            def MI(d):
                return CST[:, C_UTI:C_UTI + 128] if d == 0 else CST[:, C_LTI:C_LTI + 128]

            def strictTS(d):
                return CST[:, C_LTS:C_LTS + 128] if d == 0 else CST[:, C_UTS:C_UTS + 128]

            xbc = S(f"m_xbc{l}", [128, 8, 130])
            cacc = S(f"m_cacc{l}", [128, 8, 128])
            bct = S(f"m_bct{l}", [128, 4, 128], BF16)
            xtok = S(f"m_xtok{l}", [128, 512])
            xtokb = S(f"m_xtokb{l}", [128, 512], BF16)
            btokb = S(f"m_btokb{l}", [128, 256], BF16)
            zdt = S(f"m_zdt{l}", [128, 528])
            dts = S(f"m_dts{l}", [128, 8])
            dta = S(f"m_dta{l}", [128, 8])
            sm = S(f"m_sm{l}", [128, 40])
            xw = S(f"m_xw{l}", [128, 512], BF16)
            l2 = [S(f"m_l2{l}_{i}", [128, 256]) for i in range(2)]
            Et = [S(f"m_E{l}_{i}", [128, 256]) for i in range(2)]
            LTt = [S(f"m_LT{l}_{i}", [128, 128]) for i in range(2)]
            STt = [S(f"m_ST{l}_{i}", [128, 128], BF16) for i in range(2)]
            CsT = [S(f"m_Cs{l}_{i}", [128, 128], BF16) for i in range(2)]
            hst = S(f"m_hst{l}", [128, 512])
            hstb = S(f"m_hstb{l}", [128, 512], BF16)
            yt = S(f"m_y{l}", [128, 512])
            yf = S(f"m_yf{l}", [128, 512])
            zs = S(f"m_zs{l}", [128, 512])
            ysq = S(f"m_ysq{l}", [128, 512])
            gst = S(f"m_gst{l}", [128, 4])
            mixo = S(f"mixo{l}", [128, 8, 128], BF16)
            raw = S(f"r_raw{l}", [64, 26, 130])
            ssum = S(f"r_ssum{l}", [64, 26, 128])
            pp = ssum
            xg0 = S(f"r_xg0{l}", [128, 130])
            xg1 = S(f"r_xg1{l}", [32, 130])
            sg0 = S(f"r_sg0{l}", [128, 128], BF16)
            sg1 = S(f"r_sg1{l}", [32, 128], BF16)
            xgt = S(f"r_xgt{l}", [128, 128])
            twb = S(f"r_twb{l}", [64, 2, 128], BF16)
            lw = S(f"r_lw{l}", [64, 8, 128])
            aa = S(f"r_aa{l}", [64, 8, 128])
            kk = S(f"r_kk{l}", [64, 8, 128])
            kd = S(f"r_kd{l}", [64, 8, 128])
            sqb = S(f"r_sqb{l}", [64, 8, 128], BF16)
            linc = S(f"r_linc{l}", [64, 8, 128])
            lex = S(f"r_lex{l}", [64, 8, 128])
            rinv = lex
            e1 = S(f"r_e1{l}", [64, 8, 128])
            e0 = S(f"r_e0{l}", [64, 8, 128])
            ei = S(f"r_ei{l}", [64, 8, 128])
            gC = S(f"r_gC{l}", [64, 8])
            tmpk = S(f"r_tmpk{l}", [64, 8, 128])
            AR = S(f"r_AR{l}", [64, 8, 256], BF16)
            BK = S(f"r_BK{l}", [64, 8, 2, 128], BF16)
            prodb = S(f"r_prod{l}", [64, 8, 128], BF16)
            vtok = S(f"r_vtok{l}", [128, 512])
            vtokb = S(f"r_vtokb{l}", [128, 512], BF16)
            BKtok = S(f"r_BKtok{l}", [128, 8, 2, 64], BF16)
            GB = S(f"r_GB{l}", [128, 8, 256], BF16)
            GK = S(f"r_GK{l}", [128, 8, 256], BF16)
            Pm = [S(f"r_P{l}_{i}", [128, 8, 128], BF16) for i in range(2)]
            Qm = [S(f"r_Q{l}_{i}", [128, 8, 128], BF16) for i in range(2)]
            Ym = [S(f"r_Y{l}_{i}", [128, 8, 128], BF16) for i in range(2)]
            E1m = S(f"r_E1{l}", [128, 8, 128], BF16)
            E2m = S(f"r_E2{l}", [128, 8, 128], BF16)
            Dm = S(f"r_D{l}", [128, 8, 128], BF16)
            Zm = S(f"r_Z{l}", [128, 8, 128], BF16)
            Wt = S(f"r_W{l}", [128, 512], BF16)
            Ut = S(f"r_U{l}", [128, 512], BF16)
            Hs = S(f"r_H{l}", [64, 512])
            Hb = S(f"r_Hb{l}", [64, 512], BF16)
            ot = S(f"r_o{l}", [128, 520])
            oft = S(f"r_of{l}", [128, 520])
            osq = ysq
            gn = S(f"r_gn{l}", [128, 40])
            B = [es.enter_context(nc.psum_tensor(f"mxps{l}_{i}", [128, 512], F32)).ap() for i in range(7)]
            BBp = es.enter_context(nc.psum_tensor(f"mxpsb{l}", [128, 1024], BF16)).ap()

            nbw = ROWB[:, RV_MNW:RV_MNW + 512]
            lnw = ROWB[:, RV_LNW:RV_LNW + 512]
            lnb = ROWB[:, RV_LNB:RV_LNB + 512]
            mdb = ROWB[:, RV_MD:RV_MD + 8]
            evc = [0]

            def evq():
                evc[0] += 1
                return 'act' if evc[0] % 2 == 0 else 'dve'

            def load_halo(tile, tk, srcap, srck, t0, T, nblk_dims):
                lo = max(t0 - 1, 0)
                hi = min(t0 + 129, T)
                o = lo - (t0 - 1)
                if nblk_dims:
                    em.dma('sp', tile[:, :, o:o + hi - lo], srcap[:, :, lo:hi], reads=[srck], writes=[tk])
                    if t0 == 0:
                        em.memset('pool', tile[:, :, 0:1], 0.0, [tk])
                    if t0 + 128 == T:
                        em.memset('pool', tile[:, :, 129:130], 0.0, [tk])
                else:
                    em.dma('sp', tile[:, o:o + hi - lo], srcap[:, lo:hi], reads=[srck], writes=[tk])
                    if t0 == 0:
                        em.memset('pool', tile[:, 0:1], 0.0, [tk])
                    if t0 + 128 == T:
                        em.memset('pool', tile[:, 129:130], 0.0, [tk])

            def mamba_chunk(s, t0, d, emit_out):
                T = seqT(s)
                load_halo(xbc, 'm_xbc', fm(XBC[s]), f"xbc{s}", t0, T, True)
                em.dma('pool', zdt, ZDT[s][t0:t0 + 128, :], reads=[f"zdt{s}"], writes=['m_zdt'])
                if (mixcfg or {}).get('mstop', 99) <= 1:
                    return
                for j in range(8):
                    em.ts('dve', cacc[:, j, :], xbc[:, j, 1:129], PV[:, PV_MCW + 8 + j:PV_MCW + 9 + j],
                          PV[:, PV_MCB + j:PV_MCB + j + 1], ALU.mult, ALU.add, ['m_xbc', 'PV'], ['m_cacc'])
                    em.stt(cacc[:, j, :], xbc[:, j, 0:128], PV[:, PV_MCW + j:PV_MCW + j + 1], cacc[:, j, :],
                           ALU.mult, ALU.add, ['m_xbc', 'PV', 'm_cacc'], ['m_cacc'])
                    em.stt(cacc[:, j, :], xbc[:, j, 2:130], PV[:, PV_MCW + 16 + j:PV_MCW + 17 + j], cacc[:, j, :],
                           ALU.mult, ALU.add, ['m_xbc', 'PV', 'm_cacc'], ['m_cacc'])
                em.act(cacc[:, 0:6, :], cacc[:, 0:6, :], AF.Silu, ['m_cacc'], ['m_cacc'])
                em.act(bct[:, 2:4, :], cacc[:, 6:8, :], AF.Silu, ['m_cacc'], ['m_bct'])
                em.cp('dve', bct[:, 0:2, :], cacc[:, 4:6, :], ['m_cacc'], ['m_bct'])
                if (mixcfg or {}).get('mstop', 99) <= 2:
                    return
                msub = (mixcfg or {}).get('msub', 9)
                for j in range(4):
                    em.tr(B[4][:, j * 128:(j + 1) * 128], cacc[:, j, :], ident, ['m_cacc', 'CST'], ['B4'])
                if msub >= 1:
                    for j in range(2):
                        em.tr(B[5][:, j * 128:(j + 1) * 128], cacc[:, 4 + j, :], ident, ['m_cacc', 'CST'], ['B5'])
                if msub >= 2 and msub != 33:
                    em.cp('dve', xtok, B[4], ['B4'], ['m_xtok'])
                if msub == 30:
                    em.cp('dve', xtokb, B[4], ['B4'], ['m_xtokb'])
                elif msub == 31:
                    em.cp('act', yt, B[4], ['B4'], ['m_y'])
                elif msub == 32:
                    em.cp('act', xtokb, xtok, ['m_xtok'], ['m_xtokb'])
                elif msub >= 3:
                    em.cp('act', xtokb, B[4], ['B4'], ['m_xtokb'])
                if msub >= 4:
                    em.cp('act', btokb, B[5][:, 0:256], ['B5'], ['m_btokb'])
                if (mixcfg or {}).get('mstop', 99) <= 3:
                    return
                em.tt('dve', dts, zdt[:, 512 + d * 8:520 + d * 8], ROWB[:, RV_DTB + d * 8:RV_DTB + d * 8 + 8], ALU.add,
                      ['m_zdt', 'ROWB'], ['m_dts'])
                em.act(dts, dts, AF.Exp, ['m_dts'], ['m_dts'])
                em.act(dts, dts, AF.Ln, ['m_dts'], ['m_dts'], bias=1.0, scale=1.0)
                em.tt('dve', dta, dts, NEGA[:, d * 8:d * 8 + 8], ALU.mult, ['m_dts', 'NEGA'], ['m_dta'])
                if (mixcfg or {}).get('mstop', 99) <= 4:
                    return
                em.mm(B[5][:, 256:264], MI(d), dta, True, True, ['CST', 'm_dta'], ['B5'])
                em.mm(B[5][:, 264:272], ones, dta, True, True, ['CST', 'm_dta'], ['B5'])
                em.cp('act', sm[:, 32:40], B[5][:, 264:272], ['B5'], ['m_sm'])
                em.tt('dve', sm[:, 24:32], sm[:, 32:40], B[5][:, 256:264], ALU.subtract, ['B5', 'm_sm'], ['m_sm'])
                em.act(sm[:, 0:8], sm[:, 24:32], AF.Exp, ['m_sm'], ['m_sm'])
                em.tt('dve', sm[:, 8:16], sm[:, 0:8], dts, ALU.mult, ['m_sm', 'm_dts'], ['m_sm'])
                em.act(sm[:, 16:24], B[5][:, 264:272], AF.Exp, ['B5'], ['m_sm'])
                em.tt('dve', xw.rearrange("p (h q) -> p h q", q=64), xtok.rearrange("p (h q) -> p h q", q=64),
                      sm[:, 8:16].unsqueeze(2).to_broadcast([128, 8, 64]), ALU.mult, ['m_xtok', 'm_sm'], ['m_xw'])
                if (mixcfg or {}).get('mstop', 99) <= 5:
                    return
                for g in range(2):
                    em.mm(B[6][:, g * 128:(g + 1) * 128], bct[:, g, :], bct[:, 2 + g, :], True, True, ['m_bct'], ['B6'])
                for h in range(8):
                    g = h // 4
                    i2 = h % 2
                    pe_ = B[6][:, 256:512] if i2 == 0 else B[5][:, 0:256]
                    pek = 'B6' if i2 == 0 else 'B5'
                    em.ts('dve', l2[i2], MSO[:, d, :], dta[:, h:h + 1], None, ALU.mult, None,
                          ['MSO', 'm_dta'], [f"m_l2{i2}"])
                    em.mm(pe_[:, 0:128], l2[i2][:, 0:128], MI(d), True, True, [f"m_l2{i2}", 'CST'], [pek])
                    em.mm(pe_[:, 128:256], l2[i2][:, 128:256], MI(d), True, True, [f"m_l2{i2}", 'CST'], [pek])
                    em.act(Et[i2], pe_[:, 0:256], AF.Exp, [pek], [f"m_E{i2}"])
                    em.stt(LTt[i2], Et[i2][:, 0:128], dts[:, h:h + 1], MI(d), ALU.mult, ALU.mult,
                           [f"m_E{i2}", 'm_dts', 'CST'], [f"m_LT{i2}"])
                    em.tt('dve', STt[i2], B[6][:, g * 128:(g + 1) * 128], LTt[i2], ALU.mult, ['B6', f"m_LT{i2}"],
                          [f"m_ST{i2}"])
                    em.tt('dve', CsT[i2], bct[:, 2 + g, :], Et[i2][:, 128:256], ALU.mult, ['m_bct', f"m_E{i2}"],
                          [f"m_Cs{i2}"])
                    em.mm(B[4][:, h * 64:(h + 1) * 64], STt[i2], xtokb[:, h * 64:(h + 1) * 64], True, False,
                          [f"m_ST{i2}", 'm_xtokb'], ['B4'])
                    em.mm(B[4][:, h * 64:(h + 1) * 64], CsT[i2], hstb[:, h * 64:(h + 1) * 64], False, True,
                          [f"m_Cs{i2}", 'm_hstb'], ['B4'])
                if (mixcfg or {}).get('mstop', 99) <= 6:
                    return
                for g in range(2):
                    em.mm(B[6][:, g * 256:(g + 1) * 256], btokb[:, g * 128:(g + 1) * 128], xw[:, g * 256:(g + 1) * 256],
                          True, True, ['m_btokb', 'm_xw'], ['B6'])
                em.tt('dve', hst.rearrange("p (h q) -> p h q", q=64), hst.rearrange("p (h q) -> p h q", q=64),
                      sm[:, 16:24].unsqueeze(2).to_broadcast([128, 8, 64]), ALU.mult, ['m_hst', 'm_sm', 'B4'], ['m_hst'])
                em.tt('dve', hst, hst, B[6], ALU.add, ['m_hst', 'B6'], ['m_hst'])
                em.cp('act', hstb, hst, ['m_hst', 'B4'], ['m_hstb'])
                if (mixcfg or {}).get('mstop', 99) <= 7:
                    return
                if d == 0:
                    if emit_out:
                        em.cp('act', yt, B[4], ['B4'], ['m_y'])
                        em.dma('pool', YF[s][t0:t0 + 128, :], yt, reads=['m_y'], writes=[f"yf{s}"])
                elif emit_out:
                    em.dma('sp', yf, YF[s][t0:t0 + 128, :], reads=[f"yf{s}"], writes=['m_yf'])
                    em.tt('dve', yt, B[4], yf, ALU.add, ['B4', 'm_yf'], ['m_y'])
                    em.tt('dve', yf.rearrange("p (h q) -> p h q", q=64), xtok.rearrange("p (h q) -> p h q", q=64),
                          mdb.unsqueeze(2).to_broadcast([128, 8, 64]), ALU.mult, ['m_xtok', 'ROWB', 'm_yf'], ['m_yf'])
                    em.tt('dve', yt, yt, yf, ALU.add, ['m_y', 'm_yf'], ['m_y'])
                    em.act(zs, zdt[:, 0:512], AF.Silu, ['m_zdt'], ['m_zs'])
                    em.tt('dve', yt, yt, zs, ALU.mult, ['m_y', 'm_zs'], ['m_y'])
                    for g in range(2):
                        em.act(ysq[:, g * 256:(g + 1) * 256], yt[:, g * 256:(g + 1) * 256], AF.Square, ['m_y'],
                               ['m_ysq', 'm_gst'], accum=gst[:, g:g + 1])
                    em.act(gst[:, 2:4], gst[:, 0:2], AF.Sqrt, ['m_gst'], ['m_gst'], bias=EPS, scale=1.0 / 256)
                    em.op('dve', lambda e: e.reciprocal(out=gst[:, 2:4], in_=gst[:, 2:4]), ['m_gst'], ['m_gst'])
                    for g in range(2):
                        em.stt(yt[:, g * 256:(g + 1) * 256], yt[:, g * 256:(g + 1) * 256], gst[:, 2 + g:3 + g],
                               nbw[:, g * 256:(g + 1) * 256], ALU.mult, ALU.mult, ['m_y', 'm_gst', 'ROWB'], ['m_y'])
                    for j in range(4):
                        em.tr(B[5][:, j * 128:(j + 1) * 128], yt[:, j * 128:(j + 1) * 128], ident, ['m_y', 'CST'], ['B5'])
                    em.cp('act', mixo[:, 0:4, :], B[5].rearrange("p (j t) -> p j t", t=128), ['B5'], ['mixo_m'])
                    em.dma('pool', fm(MIX[s])[:, 0:4, t0:t0 + 128], mixo[:, 0:4, :], reads=['mixo_m'], writes=[f"mixm{s}"])

            def wkv_chunk(s, t0, d, emit_out):
                T = seqT(s)
                fin = (d == 1 and emit_out)
                rk = f"rkv{s}"
                load_halo(raw[:, 0:24, :], 'r_raw', RKV[s].rearrange("j p t -> p j t"), rk, t0, T, True)
                load_halo(raw[:, 24:25, :], 'r_raw', XWA[s][d:d + 1].rearrange("j p t -> p j t"), f"xwa{s}", t0, T, True)
                load_halo(raw[:, 25:26, :], 'r_raw', XWA[s][2 + d:3 + d].rearrange("j p t -> p j t"), f"xwa{s}", t0, T, True)
                em.tt('dve', ssum, raw[:, :, 0:128], raw[:, :, 2:130], ALU.add, ['r_raw'], ['r_ssum'])
                em.stt(ssum, ssum, 0.5, raw[:, :, 1:129], ALU.mult, ALU.subtract, ['r_ssum', 'r_raw'], ['r_ssum'])
                em.tt('dve', ssum[:, 0:24, :], ssum[:, 0:24, :],
                      PV64[:, P6_MURKV:P6_MURKV + 24].unsqueeze(2).to_broadcast([64, 24, 128]), ALU.mult,
                      ['r_ssum', 'PV64'], ['r_ssum'])
                em.ts('dve', ssum[:, 24, :], ssum[:, 24, :], PV64[:, P6_MUWA + d:P6_MUWA + d + 1], None, ALU.mult, None,
                      ['r_ssum', 'PV64'], ['r_ssum'])
                em.ts('dve', ssum[:, 25, :], ssum[:, 25, :], PV64[:, P6_MUWA + 2 + d:P6_MUWA + 3 + d], None, ALU.mult, None,
                      ['r_ssum', 'PV64'], ['r_ssum'])
                em.tt('dve', pp, raw[:, :, 1:129], ssum, ALU.add, ['r_raw', 'r_ssum'], ['r_ssum'])
                rr = pp[:, 0:8, :]
                kr = pp[:, 8:16, :]
                vr = pp[:, 16:24, :]
                em.act(twb[:, 0, :], pp[:, 24, :], AF.Tanh, ['r_ssum'], ['r_twb'])
                em.cp('dve', twb[:, 1, :], pp[:, 25, :], ['r_ssum'], ['r_twb'])
                for h in range(8):
                    bk_ = B[h // 4]
                    em.mm(bk_[0:64, (h % 4) * 128:(h % 4 + 1) * 128], w2b[:, d, h * 64:(h + 1) * 64], twb[:, 0, :], True, True,
                          ['w2b', 'r_twb'], [f"B{h // 4}"])
                for h in range(8):
                    bk_ = B[2 + h // 4]
                    em.mm(bk_[0:64, (h % 4) * 128:(h % 4 + 1) * 128], a2b[:, d, h * 64:(h + 1) * 64], twb[:, 1, :], True, True,
                          ['a2b', 'r_twb'], [f"B{2 + h // 4}"])
                for h in range(8):
                    em.act(lw[:, h, :], B[h // 4][0:64, (h % 4) * 128:(h % 4 + 1) * 128], AF.Sigmoid, [f"B{h // 4}", 'PV64'],
                           ['r_lw'], bias=PV64[:, P6_W0 + d * 8 + h:P6_W0 + d * 8 + h + 1], scale=1.0)
                    em.act(aa[:, h, :], B[2 + h // 4][0:64, (h % 4) * 128:(h % 4 + 1) * 128], AF.Sigmoid,
                           [f"B{2 + h // 4}", 'PV64'], ['r_aa'], bias=PV64[:, P6_A0 + d * 8 + h:P6_A0 + d * 8 + h + 1], scale=1.0)
                em.ts('dve', lw, lw, -R_DECAY_SCALE, None, ALU.mult, None, ['r_lw'], ['r_lw'])
                em.tt('dve', kk, kr, PV64[:, P6_KK:P6_KK + 8].unsqueeze(2).to_broadcast([64, 8, 128]), ALU.mult,
                      ['r_ssum', 'PV64'], ['r_kk'])
                em.act(sqb, kk, AF.Square, ['r_kk'], ['r_sqb'])
                for hh in range(2):
                    em.mm(B[hh][0:64, :], onesb[0:64, 0:64], sqb[:, hh * 4:(hh + 1) * 4, :], True, True,
                          ['CSTB', 'r_sqb'], [f"B{hh}"])
                for hh in range(2):
                    em.act(rinv[:, hh * 4:(hh + 1) * 4, :], B[hh][0:64, :].rearrange("p (h t) -> p h t", t=128), AF.Sqrt,
                           [f"B{hh}"], ['r_lex'])
                em.ts('dve', rinv, rinv, 1e-12, None, ALU.max, None, ['r_lex'], ['r_lex'])
                em.op('dve', lambda e: e.reciprocal(out=rinv, in_=rinv), ['r_lex'], ['r_lex'])
                em.tt('dve', kk, kk, rinv, ALU.mult, ['r_kk', 'r_lex'], ['r_kk'])
                em.tt('dve', tmpk, aa, PV64[:, P6_KA:P6_KA + 8].unsqueeze(2).to_broadcast([64, 8, 128]), ALU.mult,
                      ['r_aa', 'PV64'], ['r_tmpk'])
                em.tt('dve', tmpk, tmpk, OMMU.unsqueeze(2).to_broadcast([64, 8, 128]), ALU.add, ['r_tmpk', 'OMMU'], ['r_tmpk'])
                em.tt('dve', kd, kr, tmpk, ALU.mult, ['r_ssum', 'r_tmpk'], ['r_kd'])
                em.op('dve', lambda e: e.tensor_tensor_scan(out=linc.rearrange("p h t -> p (h t)"),
                                                            data0=RMK.rearrange("p h t -> p (h t)"),
                                                            data1=lw.rearrange("p h t -> p (h t)"), initial=0.0,
                                                            op0=ALU.mult, op1=ALU.add), ['RMK', 'r_lw'], ['r_linc'])
                if d == 0:
                    tot = linc[:, :, 127:128]
                else:
                    em.tt('dve', lex, lw, linc, ALU.subtract, ['r_lw', 'r_linc'], ['r_lex'])
                    em.cp('dve', gC, linc[:, :, 127], ['r_linc'], ['r_gC'])
                    em.tt('dve', linc, lex, gC.unsqueeze(2).to_broadcast([64, 8, 128]), ALU.add, ['r_lex', 'r_gC', 'r_linc'],
                          ['r_linc'])
                    tot = linc[:, :, 0:1]
                em.tt('dve', lex, linc, lw, ALU.subtract, ['r_linc', 'r_lw'], ['r_lex'])
                em.act(e1, linc, AF.Exp, ['r_linc'], ['r_e1'])
                em.act(e0, lex, AF.Exp, ['r_lex'], ['r_e0'])
                em.act(ei, linc, AF.Exp, ['r_linc'], ['r_ei'], scale=-1.0)
                em.act(gC, tot.rearrange("p h o -> p (h o)"), AF.Exp, ['r_linc', 'r_gC'], ['r_gC'])
                em.tt('dve', AR[:, :, 128:256], rr, e1, ALU.mult, ['r_ssum', 'r_e1'], ['r_AR'])
                em.stt(AR[:, :, 0:128], kk, -1.0, e0, ALU.mult, ALU.mult, ['r_kk', 'r_e0'], ['r_AR'])
                em.tt('dve', tmpk, kk, aa, ALU.mult, ['r_kk', 'r_aa', 'r_tmpk'], ['r_tmpk'])
                em.tt('dve', BK[:, :, 0, :], tmpk, ei, ALU.mult, ['r_tmpk', 'r_ei'], ['r_BK'])
                em.tt('dve', BK[:, :, 1, :], kd, ei, ALU.mult, ['r_kd', 'r_ei'], ['r_BK'])
                em.tt('dve', tmpk, rr, kd, ALU.mult, ['r_ssum', 'r_kd', 'r_tmpk'], ['r_tmpk'])
                em.tt('dve', prodb, tmpk, PV64[:, P6_RK:P6_RK + 8].unsqueeze(2).to_broadcast([64, 8, 128]), ALU.mult,
                      ['r_tmpk', 'PV64'], ['r_prod'])
                for h in range(8):
                    em.tr(B[2][:, h * 64:(h + 1) * 64], vr[:, h, :], ident[0:64, 0:64], ['r_ssum', 'CST'], ['B2'])
                em.cp('dve', vtok, B[2], ['B2'], ['r_vtok'])
                em.cp('act', vtokb, B[2], ['B2'], ['r_vtokb'])
                for h in range(8):
                    for q in range(2):
                        em.tr(BBp[:, (h * 2 + q) * 64:(h * 2 + q + 1) * 64], BK[:, h, q, :], identb[0:64, 0:64],
                              ['r_BK', 'CSTB'], ['BB'])
                em.cp('dve', BKtok.rearrange("p h q k -> p (h q k)"), BBp, ['BB'], ['r_BKtok'])
                for h in range(8):
                    em.mm(B[3][:, 256 + h:257 + h], prodb[:, h, :], onesb[0:64, 0:1], True, True, ['r_prod', 'CSTB'], ['B3'])
                em.cp('act', ot[:, 512:520], B[3][:, 256:264], ['B3'], ['r_ocoef'])
                for hp in range(4):
                    bb_ = B[hp % 2]
                    bbk = f"B{hp % 2}"
                    bk2 = B[2 + hp % 2]
                    bk2k = f"B{2 + hp % 2}"
                    for q in range(2):
                        h = hp * 2 + q
                        em.mm(bb_[:, q * 256:(q + 1) * 256], BK[:, h, 0, :], AR[:, h, :], True, True, ['r_BK', 'r_AR', 'r_lw', 'r_aa'], [bbk])
                        em.mm(bk2[:, q * 256:(q + 1) * 256], BK[:, h, 1, :], AR[:, h, :], True, True, ['r_BK', 'r_AR'], [bk2k])
                    em.tt('dve', GB[:, hp * 2:hp * 2 + 2, :], bb_.rearrange("p (q c) -> p q c", c=256),
                          MARt[:, d:d + 1, :].to_broadcast([128, 2, 256]), ALU.mult, [bbk, 'MAR'], ['r_GB'])
                    em.tt('dve', Qm[0][:, hp * 2:hp * 2 + 2, :], bb_.rearrange("p (q c) -> p q c", c=256)[:, :, 0:128],
                          CST[:, C_MP0 + (1 - d) * 128:C_MP0 + (2 - d) * 128].unsqueeze(1).to_broadcast([128, 2, 128]), ALU.mult,
                          [bbk, 'CST'], ['r_Q0'])
                    em.tt('dve', GK[:, hp * 2:hp * 2 + 2, :], bk2.rearrange("p (q c) -> p q c", c=256),
                          MARt[:, d:d + 1, :].to_broadcast([128, 2, 256]), ALU.mult, [bk2k, 'MAR'], ['r_GK'])
                for hh in range(2):
                    bp = B[hh]
                    bpk = f"B{hh}"
                    for q in range(4):
                        h = hh * 4 + q
                        em.mm(bp[:, q * 128:(q + 1) * 128], AR[:, h, 0:128], BK[:, h, 0, :], True, True, ['r_AR', 'r_BK'], [bpk])
                    b3 = bp.rearrange("p (q c) -> p q c", c=128)
                    hsl = slice(hh * 4, (hh + 1) * 4)
                    em.tt('dve', Pm[0][:, hsl, :], b3, CST[:, C_MP0 + d * 128:C_MP0 + (d + 1) * 128].unsqueeze(1).to_broadcast([128, 4, 128]),
                          ALU.mult, [bpk, 'CST'], ['r_P0'])
                    em.tt('dve', E1m[:, hsl, :], b3, CST[:, C_ME1 + d * 128:C_ME1 + (d + 1) * 128].unsqueeze(1).to_broadcast([128, 4, 128]),
                          ALU.mult, [bpk, 'CST'], ['r_E1'])
                    em.tt('dve', E2m[:, hsl, :], b3, CST[:, C_ME2 + d * 128:C_ME2 + (d + 1) * 128].unsqueeze(1).to_broadcast([128, 4, 128]),
                          ALU.mult, [bpk, 'CST'], ['r_E2'])
                em.tt('dve', Ym[0], Qm[0], identb.unsqueeze(1).to_broadcast([128, 8, 128]), ALU.add, ['r_Q0', 'CSTB'], ['r_Y0'])
                cur = 0
                for lev in range(1, 5):
                    nxt = 1 - cur
                    for hh in range(2):
                        bp = B[hh]
                        bpk = f"B{hh}"
                        bq = B[2 + hh]
                        bqk = f"B{2 + hh}"
                        hsl = slice(hh * 4, (hh + 1) * 4)
                        for q in range(4):
                            h = hh * 4 + q
                            em.mm(bp[:, q * 128:(q + 1) * 128], Qm[cur][:, h, :], Pm[cur][:, h, :], True, True,
                                  [f"r_Q{cur}", f"r_P{cur}"], [bpk])
                        for q in range(4):
                            h = hh * 4 + q
                            em.mm(bq[:, q * 128:(q + 1) * 128], Pm[cur][:, h, :], Qm[cur][:, h, :], True, True,
                                  [f"r_Q{cur}", f"r_P{cur}"], [bqk])
                        em.cp('act', Pm[nxt][:, hsl, :], bp.rearrange("p (q c) -> p q c", c=128), [bpk], [f"r_P{nxt}"])
                        em.cp('dve', Qm[nxt][:, hsl, :], bq.rearrange("p (q c) -> p q c", c=128), [bqk], [f"r_Q{nxt}"])
                    for hh in range(2):
                        by = B[2 + hh]
                        byk = f"B{2 + hh}"
                        hsl = slice(hh * 4, (hh + 1) * 4)
                        for q in range(4):
                            h = hh * 4 + q
                            em.mm(by[:, q * 128:(q + 1) * 128], Pm[nxt][:, h, :], Ym[cur][:, h, :], True, True,
                                  [f"r_P{nxt}", f"r_Y{cur}"], [byk])
                        em.tt('dve', Ym[nxt][:, hsl, :], by.rearrange("p (q c) -> p q c", c=128), Ym[cur][:, hsl, :], ALU.add,
                              [byk, f"r_Y{cur}"], [f"r_Y{nxt}"])
                    cur = nxt
                Dt = Ym[cur]
                dtk = f"r_Y{cur}"
                for st, (Em_, ek) in enumerate([(E1m, 'r_E1'), (E2m, 'r_E2')]):
                    oth = Ym[1 - cur]
                    othk = f"r_Y{1 - cur}"
                    for h in range(8):
                        em.tr(BBp[:, h * 128:(h + 1) * 128], Dt[:, h, :], identb, [dtk, 'CSTB'], ['BB'])
                    em.cp('act', Dm, BBp.rearrange("p (h c) -> p h c", c=128), ['BB'], ['r_D'])
                    for hh in range(2):
                        bz = B[hh]
                        bzk = f"B{hh}"
                        hsl = slice(hh * 4, (hh + 1) * 4)
                        for q in range(4):
                            h = hh * 4 + q
                            em.mm(bz[:, q * 128:(q + 1) * 128], Em_[:, h, :], Dt[:, h, :], True, True, [ek, dtk], [bzk])
                        em.cp('act' if hh == 0 else 'dve', Zm[:, hsl, :], bz.rearrange("p (q c) -> p q c", c=128), [bzk], ['r_Z'])
                    for hh in range(2):
                        by = B[2 + hh]
                        byk = f"B{2 + hh}"
                        hsl = slice(hh * 4, (hh + 1) * 4)
                        for q in range(4):
                            h = hh * 4 + q
                            em.mm(by[:, q * 128:(q + 1) * 128], Dm[:, h, :], Zm[:, h, :], True, True, ['r_D', 'r_Z'], [byk])
                        em.tt('dve', oth[:, hsl, :], by.rearrange("p (q c) -> p q c", c=128), Dt[:, hsl, :], ALU.add,
                              [byk, dtk], [othk])
                    cur = 1 - cur
                    Dt = Ym[cur]
                    dtk = f"r_Y{cur}"
                TT_ = Dt
                ttk = dtk
                for h in range(8):
                    hs_ = slice(h * 64, (h + 1) * 64)
                    em.mm(B[0][:, hs_], AR[:, h, 0:128], Hb[:, hs_], True, False, ['r_AR', 'r_Hb'], ['B0'])
                    em.mm(B[0][:, hs_], GK[:, h, 0:128], vtokb[:, hs_], False, True, ['r_GK', 'r_vtokb'], ['B0'])
                em.cp('act', Wt, B[0], ['B0'], ['r_W'])
                for h in range(8):
                    hs_ = slice(h * 64, (h + 1) * 64)
                    em.mm(B[1][:, hs_], TT_[:, h, :], Wt[:, hs_], True, True, [ttk, 'r_W'], ['B1'])
                em.cp('dve', Ut, B[1], ['B1'], ['r_U'])
                for h in range(8):
                    hs_ = slice(h * 64, (h + 1) * 64)
                    em.mm(B[2][:, hs_], AR[:, h, 128:256], Hb[:, hs_], True, False, ['r_AR', 'r_Hb'], ['B2'])
                    em.mm(B[2][:, hs_], GB[:, h, 128:256], Ut[:, hs_], False, False, ['r_GB', 'r_U'], ['B2'])
                    em.mm(B[2][:, hs_], GK[:, h, 128:256], vtokb[:, hs_], False, True, ['r_GK', 'r_vtokb'], ['B2'])
                for h in range(8):
                    hs_ = slice(h * 64, (h + 1) * 64)
                    em.mm(B[3][0:64, hs_], BKtok[:, h, 0, :], Ut[:, hs_], True, False, ['r_BKtok', 'r_U'], ['B3'])
                    em.mm(B[3][0:64, hs_], BKtok[:, h, 1, :], vtokb[:, hs_], False, True, ['r_BKtok', 'r_vtokb'], ['B3'])
                em.tt('dve', Hs, Hs, B[3][0:64, :], ALU.add, ['r_H', 'B3'], ['r_H'])
                em.tt('dve', Hs.rearrange("p (h v) -> p h v", v=64), Hs.rearrange("p (h v) -> p h v", v=64),
                      gC.unsqueeze(2).to_broadcast([64, 8, 64]), ALU.mult, ['r_H', 'r_gC'], ['r_H'])
                em.cp('act', Hb, Hs, ['r_H', 'B0', 'B2'], ['r_Hb'])
                if d == 0:
                    if emit_out:
                        em.cp('act', ot[:, 0:512], B[2], ['B2'], ['r_o'])
                        em.dma('pool', OF[s][t0:t0 + 128, :], ot, reads=['r_o', 'r_ocoef'], writes=[f"of{s}"])
                elif emit_out:
                    em.dma('sp', oft, OF[s][t0:t0 + 128, :], reads=[f"of{s}"], writes=['r_of'])
                    em.tt('dve', ot[:, 0:512], B[2], oft[:, 0:512], ALU.add, ['B2', 'r_of'], ['r_o'])
                    o3 = ot[:, 0:512].rearrange("p (h v) -> p h v", v=64)
                    em.op('dve', lambda e: e.tensor_reduce(out=gn[:, 0:8], in_=o3, axis=AX.X, op=ALU.add), ['r_o'], ['r_gn'])
                    em.act(osq, ot[:, 0:512], AF.Square, ['r_o'], ['m_ysq'])
                    em.op('dve', lambda e: e.tensor_reduce(out=gn[:, 8:16], in_=osq.rearrange("p (h v) -> p h v", v=64),
                                                           axis=AX.X, op=ALU.add), ['m_ysq', 'r_gn'], ['r_gn'])
                    em.ts('dve', gn[:, 16:24], gn[:, 0:8], 1.0 / 64, None, ALU.mult, None, ['r_gn'], ['r_gn'])
                    em.tt('dve', gn[:, 0:8], gn[:, 16:24], gn[:, 16:24], ALU.mult, ['r_gn'], ['r_gn'])
                    em.stt(gn[:, 24:32], gn[:, 8:16], 1.0 / 64, gn[:, 0:8], ALU.mult, ALU.subtract, ['r_gn'], ['r_gn'])
                    em.act(gn[:, 24:32], gn[:, 24:32], AF.Sqrt, ['r_gn'], ['r_gn'], bias=R_LN_EPS, scale=1.0)
                    em.op('dve', lambda e: e.reciprocal(out=gn[:, 24:32], in_=gn[:, 24:32]), ['r_gn'], ['r_gn'])
                    em.tt('dve', o3, o3, gn[:, 16:24].unsqueeze(2).to_broadcast([128, 8, 64]), ALU.subtract, ['r_o', 'r_gn'], ['r_o'])
                    em.tt('dve', o3, o3, gn[:, 24:32].unsqueeze(2).to_broadcast([128, 8, 64]), ALU.mult, ['r_o', 'r_gn'], ['r_o'])
                    em.tt('dve', ot[:, 0:512], ot[:, 0:512], lnw, ALU.mult, ['r_o', 'ROWB'], ['r_o'])
                    em.tt('dve', ot[:, 0:512], ot[:, 0:512], lnb, ALU.add, ['r_o', 'ROWB'], ['r_o'])
                    em.tt('dve', gn[:, 32:40], ot[:, 512:520], oft[:, 512:520], ALU.add, ['r_ocoef', 'r_of', 'r_gn'], ['r_gn'])
                    em.tt('dve', osq.rearrange("p (h v) -> p h v", v=64), vtok.rearrange("p (h v) -> p h v", v=64),
                          gn[:, 32:40].unsqueeze(2).to_broadcast([128, 8, 64]), ALU.mult, ['r_vtok', 'r_gn', 'm_ysq'], ['m_ysq'])
                    em.tt('dve', ot[:, 0:512], ot[:, 0:512], osq, ALU.add, ['r_o', 'm_ysq'], ['r_o'])
                    load_halo(xg0, 'r_xg0', XG[s][0:128, :], f"xg{s}", t0, T, False)
                    load_halo(xg1, 'r_xg1', XG[s][128:160, :], f"xg{s}", t0, T, False)
                    for (xg_, sg_, np_, mucol, kx_, ks_) in [(xg0, sg0, 128, PV_MUXG0, 'r_xg0', 'r_sg0'),
                                                             (xg1, sg1, 32, PV_MUXG1, 'r_xg1', 'r_sg1')]:
                        em.tt('dve', xgt[:np_, :], xg_[:np_, 0:128], xg_[:np_, 2:130], ALU.add, [kx_], ['r_xgt'])
                        em.stt(xgt[:np_, :], xgt[:np_, :], 0.5, xg_[:np_, 1:129], ALU.mult, ALU.subtract, ['r_xgt', kx_], ['r_xgt'])
                        em.stt(xgt[:np_, :], xgt[:np_, :], PV[:np_, mucol:mucol + 1], xg_[:np_, 1:129], ALU.mult, ALU.add,
                               ['r_xgt', kx_, 'PV'], ['r_xgt'])
                        em.act(sg_[:np_, :], xgt[:np_, :], AF.Sigmoid, ['r_xgt'], [ks_])
                    em.mm(B[0], sg0, g2b0, True, False, ['r_sg0', 'g2b'], ['B0'])
                    em.mm(B[0], sg1, g2b1, False, True, ['r_sg1', 'g2b'], ['B0'])
                    em.tt('dve', ot[:, 0:512], ot[:, 0:512], B[0], ALU.mult, ['r_o', 'B0'], ['r_o'])
                    for j in range(4):
                        em.tr(B[1][:, j * 128:(j + 1) * 128], ot[:, j * 128:(j + 1) * 128], ident, ['r_o', 'CST'], ['B1'])
                    em.cp('act', mixo[:, 4:8, :], B[1].rearrange("p (j t) -> p j t", t=128), ['B1'], ['mixo_r'])
                    em.dma('pool', fm(MIX[s])[:, 4:8, t0:t0 + 128], mixo[:, 4:8, :], reads=['mixo_r'], writes=[f"mixr{s}"])

            mc_ = mixcfg or {}
            for b in range(mc_.get('nb', NBL)):
                for stream, fnc in (('m', mamba_chunk), ('r', wkv_chunk)):
                    if not mc_.get('mamba' if stream == 'm' else 'wkv', True):
                        continue
                    em.stream = stream if mc_.get('interleave', True) else None
                    for d in range(mc_.get('nd', 2)):
                        if stream == 'm':
                            em.memset('pool', hst, 0.0, ['m_hst'])
                            em.memset('pool', hstb, 0.0, ['m_hstb'])
                        else:
                            em.memset('pool', Hs, 0.0, ['r_H'])
                            em.memset('pool', Hb, 0.0, ['r_Hb'])
                        for kind in range(mc_.get('nkind', 2)):
                            s = b * 2 + kind
                            T = seqT(s)
                            nch = T // CH
                            order = range(nch) if d == 0 else range(nch - 1, -1, -1)
                            emit = (kind == 1) or need_ctx_out or mc_.get('ctxout', False)
                            for c in order:
                                fnc(s, c * CH, d, emit)
                em.stream = None
                em.flush()
            em.barrier()

    def stage_proj_post(l, phase, wap, Kc, SRC, srcname, gidx, seqs):
        with ExitStack() as es:
            def S(name, shape, dt=F32):
                return es.enter_context(nc.sbuf_tensor(name, list(shape), dt)).ap()
            nm = f"pp{phase}"
            wb = load_wbf(es, l, wap, Kc, D, nm + "w")
            a = S(f"{nm}a{l}", [128, Kc, 512], BF16)
            xt = S(f"{nm}x{l}", [128, 8, 512])
            y = S(f"{nm}y{l}", [128, 8, 512])
            sq = S(f"{nm}sq{l}", [128, 8, 512], BF16)
            rstd = S(f"{nm}rs{l}", [128, 512])
            pss = [es.enter_context(nc.psum_tensor(f"{nm}ps{l}_{i}", [128, 512], F32)).ap() for i in range(5)]
            for s in seqs:
                T = seqT(s)
                TW = min(512, T)
                jmod = 2 if s % 2 == 0 else s // 2
                rsrc, rsk = res_src(l, s, phase)
                rdst, rdk = res_dst(l, s, phase)
                for tt_ in range(T // TW):
                    t0 = tt_ * TW
                    em.dma('sp', a[:, :, :TW], fm(SRC[s])[:, :, t0:t0 + TW], reads=([f"mixm{s}", f"mixr{s}"] if srcname == 'mix' else [f"{srcname}{s}"]), writes=[nm + 'a'])
                    em.dma('pool', xt[:, :, :TW], fm(rsrc)[:, :, t0:t0 + TW], reads=[rsk], writes=[nm + 'x'])
                    for m in range(8):
                        ps = pss[m % 4]
                        pk = f"ps{m % 4}"
                        for k in range(Kc):
                            em.mm(ps[:, :TW], wb[:, k, m * 128:(m + 1) * 128], a[:, k, :TW], k == 0, k == Kc - 1,
                                  [nm + 'wbf', nm + 'a'], [pk])
                        em.cp('dve', y[:, m, :TW], ps[:, :TW], [pk], [nm + 'y'])
                        em.act(sq[:, m, :TW], ps[:, :TW], AF.Square, [pk], [nm + 'sq'])
                    for m in range(8):
                        em.mm(pss[4][:, :TW], onesb, sq[:, m, :TW], m == 0, m == 7, ['CSTB', nm + 'sq'], ['ps4'])
                    em.act(rstd[:, :TW], pss[4][:, :TW], AF.Sqrt, ['ps4'], [nm + 'rs'], bias=EPS, scale=1.0 / D)
                    em.op('dve', lambda e: e.reciprocal(out=rstd[:, :TW], in_=rstd[:, :TW]), [nm + 'rs'], [nm + 'rs'])
                    for m in range(8):
                        em.stt(y[:, m, :TW], y[:, m, :TW], DER[:, gidx, m, jmod:jmod + 1], rstd[:, :TW], ALU.mult, ALU.mult,
                               [nm + 'y', nm + 'rs', 'DER'], [nm + 'y'])
                    em.tt('dve', xt[:, :, :TW], xt[:, :, :TW], y[:, :, :TW], ALU.add, [nm + 'x', nm + 'y'], [nm + 'x'])
                    em.dma('pool', fm(rdst)[:, :, t0:t0 + TW], xt[:, :, :TW], reads=[nm + 'x'], writes=[rdk])
            em.barrier()

    def stage_ffn_up(l, seqs):
        with ExitStack() as es:
            def S(name, shape, dt=F32):
                return es.enter_context(nc.sbuf_tensor(name, list(shape), dt)).ap()
            wb = load_wbf(es, l, f_w_up[l], 8, 2 * DFF, "wup")
            xt = S(f"fux{l}", [128, 8, 512])
            h = S(f"fuh{l}", [128, 8, 512], BF16)
            sq = S(f"fusq{l}", [128, 8, 512], BF16)
            rstd = S(f"furs{l}", [128, 512])
            stg = [S(f"fustg{l}_{i}", [128, 512]) for i in range(2)]
            stv = [S(f"fustv{l}_{i}", [128, 512], BF16) for i in range(2)]
            pss = [es.enter_context(nc.psum_tensor(f"fups{l}_{i}", [128, 512], F32)).ap() for i in range(8)]
            for s in seqs:
                T = seqT(s)
                TW = min(512, T)
                jmod = 2 if s % 2 == 0 else s // 2
                src, srck = res_src(l, s, 1)
                for tt_ in range(T // TW):
                    t0 = tt_ * TW
                    em.dma('sp', xt[:, :, :TW], fm(src)[:, :, t0:t0 + TW], reads=[srck], writes=['fux'])
                    prenorm(xt, TW, h, sq, rstd, pss[7], jmod, 2, 24, 'fux', 'fuh', 'ps7')
                    for j in range(NFF):
                        pg = pss[(2 * j) % 6]
                        pgk = f"ps{(2 * j) % 6}"
                        pv_ = pss[(2 * j + 1) % 6]
                        pvk = f"ps{(2 * j + 1) % 6}"
                        for k in range(8):
                            em.mm(pg[:, :TW], wb[:, k, j * 128:(j + 1) * 128], h[:, k, :TW], k == 0, k == 7, ['wupbf', 'fuh'], [pgk])
                        for k in range(8):
                            em.mm(pv_[:, :TW], wb[:, k, DFF + j * 128:DFF + (j + 1) * 128], h[:, k, :TW], k == 0, k == 7,
                                  ['wupbf', 'fuh'], [pvk])
                        sg_ = stg[j % 2]
                        sv_ = stv[j % 2]
                        em.cp('dve', sg_[:, :TW], pg[:, :TW], [pgk], [f"fustg{j % 2}"])
                        em.cp('act', sv_[:, :TW], pv_[:, :TW], [pvk], [f"fustv{j % 2}"])
                        em.dma('pool', GATE[s][j * 128:(j + 1) * 128, t0:t0 + TW], sg_[:, :TW], reads=[f"fustg{j % 2}"],
                               writes=[f"gate{s}"])
                        em.dma('sp', VAL[s][j * 128:(j + 1) * 128, t0:t0 + TW], sv_[:, :TW], reads=[f"fustv{j % 2}"],
                               writes=[f"val{s}"])
            em.barrier()

    def stage_ffn_conv(l, seqs):
        with ExitStack() as es:
            def S(name, shape, dt=F32):
                return es.enter_context(nc.sbuf_tensor(name, list(shape), dt)).ap()
            gflat = [S(f"fcg{l}_{i}", [128, 2048]) for i in range(2)]
            vflat = [S(f"fcv{l}_{i}", [128, 2048], BF16) for i in range(2)]
            gpx = S(f"fcgpx{l}", [128, 34, 66])
            gpc = S(f"fcgpc{l}", [128, 3, 258])
            acc = S(f"fcacc{l}", [128, 2048])
            acc2 = S(f"fcacc2{l}", [128, 2048])
            u = S(f"fcu{l}", [128, 2048])
            ab = [S(f"fcab{l}_{i}", [128, 2048], BF16) for i in range(2)]
            em.memset('pool', gpx, 0.0, ['fcgpx'])
            em.memset('pool', gpc, 0.0, ['fcgpc'])
            it = 0
            for s in seqs:
                T = seqT(s)
                if s % 2 == 1:
                    R, Cc, gp, gpk = 32, 64, gpx, 'fcgpx'
                else:
                    R, Cc, gp, gpk = 1, 256, gpc, 'fcgpc'
                for j in range(NFF):
                    i2 = it % 2
                    it += 1
                    gf = gflat[i2]
                    vf = vflat[i2]
                    em.dma('sp', gf[:, :T], GATE[s][j * 128:(j + 1) * 128, :], reads=[f"gate{s}"], writes=[f"fcg{i2}"])
                    em.dma('sp', vf[:, :T], VAL[s][j * 128:(j + 1) * 128, :], reads=[f"val{s}"], writes=[f"fcv{i2}"])
                    em.cp('pool', gp[:, 1:1 + R, 1:1 + Cc], gf[:, :T].rearrange("p (r c) -> p r c", c=Cc), [f"fcg{i2}"], [gpk])
                    a3 = acc[:, :T].rearrange("p (r c) -> p r c", c=Cc)
                    for tap in range(9):
                        dr, dc = tap // 3 - 1, tap % 3 - 1
                        src_ = gp[:, 1 + dr:1 + dr + R, 1 + dc:1 + dc + Cc]
                        wcol = PV[:, PV_FCW + tap * NFF + j:PV_FCW + tap * NFF + j + 1]
                        if tap == 0:
                            em.ts('dve', a3, src_, wcol, PV[:, PV_FCB + j:PV_FCB + j + 1], ALU.mult, ALU.add,
                                  [gpk, 'PV'], ['fcacc'])
                        else:
                            em.stt(a3, src_, wcol, a3, ALU.mult, ALU.add, [gpk, 'PV', 'fcacc'], ['fcacc'])
                    em.act(u[:, :T], acc[:, :T], AF.Square, ['fcacc'], ['fcu'])
                    em.ts('dve', u[:, :T], u[:, :T], 0.044715, 1.0, ALU.mult, ALU.add, ['fcu'], ['fcu'])
                    em.tt('dve', u[:, :T], u[:, :T], acc[:, :T], ALU.mult, ['fcu', 'fcacc'], ['fcu'])
                    em.act(u[:, :T], u[:, :T], AF.Sigmoid, ['fcu'], ['fcu'], scale=GELU_C)
                    em.tt('dve', u[:, :T], u[:, :T], acc[:, :T], ALU.mult, ['fcu', 'fcacc'], ['fcu'])
                    em.tt('dve', ab[i2][:, :T], u[:, :T], vf[:, :T], ALU.mult, ['fcu', f"fcv{i2}"], [f"fcab{i2}"])
                    em.dma('pool', ACTV[s][j * 128:(j + 1) * 128, :], ab[i2][:, :T], reads=[f"fcab{i2}"], writes=[f"actv{s}"])
            em.barrier()

    allseq = list(range(NS))
    xseq = [s for s in range(NS) if s % 2 == 1]
    outkeys = []
    for l in range(n_layers):
        last = (l == n_layers - 1)
        stage_mod(l)
        stage_inproj(l)
        if stop_after == 'inproj':
            break
        stage_mixer(l, need_ctx_out=not last)
        if stop_after == 'mixer':
            break
        seqs = xseq if last else allseq
        stage_proj_post(l, 0, w_out[l], 8, MIX, "mix", 1, seqs)
        if stop_after == 'outproj':
            break
        stage_ffn_up(l, seqs)
        stage_ffn_conv(l, seqs)
        stage_proj_post(l, 1, f_w_down[l], NFF, ACTV, "actv", 3, seqs)
    em.barrier()
    return nc, em


def host_prep(inp):
    f = np.float32
    idx = np.arange(128)
    cstn = np.zeros((128, NCST), f)
    cstn[:, C_ID:C_ID + 128] = np.eye(128)
    cstn[:, C_UTI:C_UTI + 128] = (idx[:, None] <= idx[None, :])
    cstn[:, C_LTI:C_LTI + 128] = (idx[:, None] >= idx[None, :])
    cstn[:, C_UTS:C_UTS + 128] = (idx[:, None] < idx[None, :])
    cstn[:, C_LTS:C_LTS + 128] = (idx[:, None] > idx[None, :])
    cstn[:, C_ONE:C_ONE + 128] = 1.0
    b32 = idx // 32
    b64 = idx // 64
    same32 = b32[:, None] == b32[None, :]
    same64 = b64[:, None] == b64[None, :]
    for d in range(2):
        strict = (idx[:, None] > idx[None, :]) if d == 0 else (idx[:, None] < idx[None, :])
        cstn[:, C_MP0 + d * 128:C_MP0 + (d + 1) * 128] = strict & same32
        cstn[:, C_ME1 + d * 128:C_ME1 + (d + 1) * 128] = strict & same64 & (~same32)
        cstn[:, C_ME2 + d * 128:C_ME2 + (d + 1) * 128] = strict & (~same64)
    pvn = np.zeros((L, 128, NPV), f)
    pv6 = np.zeros((L, 64, NPV64), f)
    rwn = np.zeros((L, 1, NROW), f)
    for l in range(L):
        pvn[l, :, PV_BMOD:PV_BMOD + 48] = inp['b_mod'][l].reshape(48, 128).T
        pvn[l, :, PV_GPRE1:PV_GPRE1 + 8] = inp['g_mix_pre'][l].reshape(8, 128).T
        pvn[l, :, PV_GPOST1:PV_GPOST1 + 8] = inp['g_mix_post'][l].reshape(8, 128).T
        pvn[l, :, PV_GPRE2:PV_GPRE2 + 8] = inp['g_ffn_pre'][l].reshape(8, 128).T
        pvn[l, :, PV_GPOST2:PV_GPOST2 + 8] = inp['g_ffn_post'][l].reshape(8, 128).T
        pvn[l, :, PV_MCW:PV_MCW + 24] = inp['m_conv_w'][l].reshape(3, 8, 128).transpose(2, 0, 1).reshape(128, 24)
        pvn[l, :, PV_MCB:PV_MCB + 8] = inp['m_conv_b'][l].reshape(8, 128).T
        pvn[l, :, PV_FCW:PV_FCW + 198] = inp['f_conv_w'][l].reshape(9, NFF, 128).transpose(2, 0, 1).reshape(128, 198)
        pvn[l, :, PV_FCB:PV_FCB + NFF] = inp['f_conv_b'][l].reshape(NFF, 128).T
        mu = inp['r_mu'][l]
        pvn[l, :, PV_MUXG0] = mu[1792:1920]
        pvn[l, 0:32, PV_MUXG1] = mu[1920:1952]
        pv6[l, :, P6_MURKV:P6_MURKV + 24] = mu[0:1536].reshape(24, 64).T
        pv6[l, :, P6_MUWA:P6_MUWA + 4] = mu[1536:1792].reshape(4, 64).T
        pv6[l, :, P6_W0:P6_W0 + 16] = inp['r_w0'][l].reshape(2, 8, 64).transpose(2, 0, 1).reshape(64, 16)
        pv6[l, :, P6_A0:P6_A0 + 16] = inp['r_a0'][l].reshape(2, 8, 64).transpose(2, 0, 1).reshape(64, 16)
        pv6[l, :, P6_KK:P6_KK + 8] = inp['r_k_k'][l].reshape(8, 64).T
        pv6[l, :, P6_KA:P6_KA + 8] = inp['r_k_a'][l].reshape(8, 64).T
        pv6[l, :, P6_RK:P6_RK + 8] = inp['r_r_k'][l].T
        rwn[l, 0, RV_MNW:RV_MNW + 512] = inp['m_norm_w'][l]
        rwn[l, 0, RV_LNW:RV_LNW + 512] = inp['r_ln_w'][l]
        rwn[l, 0, RV_LNB:RV_LNB + 512] = inp['r_ln_b'][l]
        rwn[l, 0, RV_MD:RV_MD + 8] = inp['m_d'][l]
        rwn[l, 0, RV_DTB:RV_DTB + 16] = inp['m_dt_bias'][l].reshape(16)
        rwn[l, 0, RV_ALOG:RV_ALOG + 16] = inp['m_a_log'][l].reshape(16)
    return cstn, pvn, pv6, rwn


def make_in_maps(inp, cores):
    cstn, pvn, pv6, rwn = host_prep(inp)
    shared = {k: np.ascontiguousarray(np.asarray(inp[k], dtype=np.float32)) for k in
              ['w_mod', 'w_in', 'w_out', 'r_w2', 'r_a2', 'r_g2', 'f_w_up', 'f_w_down']}
    maps = []
    x = np.asarray(inp['x'], np.float32)
    ctx = np.asarray(inp['ctx'], np.float32)
    c = np.asarray(inp['c'], np.float32)
    cc = np.asarray(inp['c_ctx'], np.float32)
    for ci in cores:
        bs = [ci * NBL + i for i in range(NBL)]
        m = dict(shared)
        m['xT'] = np.ascontiguousarray(x[bs].transpose(0, 2, 1))
        m['ctxT'] = np.ascontiguousarray(ctx[bs].transpose(0, 2, 1))
        m['cT'] = np.ascontiguousarray(np.stack([c[bs[0]], c[bs[1]], cc], axis=1))
        m['cst'] = cstn
        m['pv'] = pvn
        m['pv64'] = pv6
        m['rowv'] = rwn
        maps.append(m)
    return maps


def kernel(**inputs):
    nc, em = build()
    cores = list(range(NCORE))
    maps = make_in_maps(inputs, cores)
    res = run_bass_kernel_spmd(nc, maps, core_ids=cores)
    out = np.empty((NCORE * NBL, TX, D), np.float32)
    for ci in cores:
        o = res.results[ci]["outT"]
        out[ci * NBL:(ci + 1) * NBL] = o.transpose(0, 2, 1)
    return out
```

```python
import numpy as np
from contextlib import ExitStack
import concourse.bass as bass
import concourse.mybir as mybir
from concourse.bass_utils import run_bass_kernel_spmd

F32 = mybir.dt.float32
BF16 = mybir.dt.bfloat16
AF = mybir.ActivationFunctionType
ALU = mybir.AluOpType
AX = mybir.AxisListType

L = 2
D = 1024
TX = 2048
TC = 256
NBL = 2
NCORE = 8
CH = 128
DFF = 2816
NFF = 22
EPS = 1e-6
R_LN_EPS = 64e-5
R_DECAY_SCALE = 0.6065306597126334
GELU_C = 1.5957691216057308

PV_BMOD = 0
PV_GPRE1 = 48
PV_GPOST1 = 56
PV_GPRE2 = 64
PV_GPOST2 = 72
PV_MCW = 80
PV_MCB = 104
PV_FCW = 112
PV_FCB = 310
PV_MUXG0 = 332
PV_MUXG1 = 333
NPV = 334
P6_MURKV = 0
P6_MUWA = 24
P6_W0 = 28
P6_A0 = 44
P6_KK = 60
P6_KA = 68
P6_RK = 76
NPV64 = 84
RV_MNW = 0
RV_LNW = 512
RV_LNB = 1024
RV_MD = 1536
RV_DTB = 1544
RV_ALOG = 1560
NROW = 1576
C_ID = 0
C_UTI = 128
C_LTI = 256
C_UTS = 384
C_LTS = 512
C_ONE = 640
C_MP0 = 768
C_ME1 = 1024
C_ME2 = 1280
NCST = 1536


class Em:
    def __init__(self, nc, ndma=8):
        self.nc = nc
        self.engs = {'pe': nc.tensor, 'act': nc.scalar, 'dve': nc.vector, 'pool': nc.gpsimd, 'sp': nc.sync}
        self.sem = {}
        self.cnt = {}
        for k in ['pe', 'act', 'dve', 'pool']:
            self.sem[k] = nc.alloc_semaphore("sem_" + k)
            self.cnt[k] = 0
        self.dq = {}
        for q in ['sp', 'pool', 'act']:
            self.dq[q] = {'n': ndma, 'next': 0}
            for i in range(ndma):
                self.sem[f"d_{q}_{i}"] = nc.alloc_semaphore(f"dsem_{q}_{i}")
                self.cnt[f"d_{q}_{i}"] = 0
        self.seen = {k: {} for k in self.engs}
        self.lastw = {}
        self.readers = {}
        self.n = 0
        self.stream = None
        self.queues = {}

    def _deps(self, reads, writes):
        deps = {}

        def add(d):
            if d is None:
                return
            k, v = d
            if deps.get(k, 0) < v:
                deps[k] = v
        for b in reads:
            add(self.lastw.get(b))
        for b in writes:
            add(self.lastw.get(b))
            for r in self.readers.get(b, ()):
                add(r)
        return deps

    def _waits(self, eng, deps):
        for k, v in deps.items():
            if k.startswith('d_'):
                v = self.cnt[k]
            if self.seen[eng].get(k, 0) >= v:
                continue
            self.seen[eng][k] = v
            self.engs[eng].wait_ge(self.sem[k], v)
            self.n += 1

    def _mark(self, me, reads, writes):
        for b in reads:
            self.readers.setdefault(b, []).append(me)
        for b in writes:
            self.lastw[b] = me
            self.readers[b] = []

    @staticmethod
    def _is_psum(k):
        return (k[0] == 'B' and (k[1:].isdigit() or k in ('BB', 'BBa', 'BBb'))) or k.startswith('ps')

    def flush(self):
        qs = {k: v for k, v in self.queues.items() if v}
        self.queues = {}
        pos = {k: 0 for k in qs}
        while qs:
            k = min(qs, key=lambda n: pos[n] / len(qs[n]))
            it = qs[k][pos[k]]
            pos[k] += 1
            if it[0] == 'op':
                self.op(*it[1:])
            else:
                self.dma(it[1], it[2], it[3], it[4], it[5], **it[6])
            if pos[k] >= len(qs[k]):
                del qs[k]

    @staticmethod
    def _merge(lists):
        lists = [l for l in lists if l]
        out = []
        pos = [0] * len(lists)
        live = list(range(len(lists)))
        while live:
            k = min(live, key=lambda n: pos[n] / len(lists[n]))
            out.append(lists[k][pos[k]])
            pos[k] += 1
            if pos[k] >= len(lists[k]):
                live.remove(k)
        return out

    def flush_mixer(self, nw):
        qs = self.queues
        self.queues = {}
        wk = list(qs.get(('rp', 0), []))
        for i in range(nw):
            wk += self._merge([qs.get(('rs', i), []), qs.get(('rp', i + 1), [])])
        allops = self._merge([wk, qs.get('m', [])])
        for it in allops:
            if it[0] == 'op':
                self.op(*it[1:])
            else:
                self.dma(it[1], it[2], it[3], it[4], it[5], **it[6])

    def op(self, eng, fn, reads=(), writes=()):
        if self.stream is not None:
            self.queues.setdefault(self.stream, []).append(('op', eng, fn, tuple(reads), tuple(writes)))
            return
        ex = [k for k in reads if self._is_psum(k)]
        self._waits(eng, self._deps(reads, list(writes) + ex))
        self.cnt[eng] += 1
        fn(self.engs[eng]).then_inc(self.sem[eng], 1)
        self._mark((eng, self.cnt[eng]), reads, writes)
        self.n += 1

    def dma(self, q, out, in_, reads=(), writes=(), **kw):
        if self.stream is not None:
            self.queues.setdefault(self.stream, []).append(('dma', q, out, in_, tuple(reads), tuple(writes), kw))
            return
        self._waits(q, self._deps(reads, writes))
        d = self.dq[q]
        i = d['next']
        d['next'] = (i + 1) % d['n']
        k = f"d_{q}_{i}"
        self.cnt[k] += 16
        self.engs[q].dma_start(out=out, in_=in_, **kw).then_inc(self.sem[k], 16)
        self._mark((k, self.cnt[k]), reads, writes)
        self.n += 1

    def barrier(self):
        allv = {k: v for k, v in self.cnt.items() if v > 0}
        for e in self.engs:
            self._waits(e, dict(allv))

    def act(self, out, in_, func, r, w, bias=None, scale=None, accum=None):
        kw = {}
        if bias is not None:
            kw['bias'] = bias
        if scale is not None:
            kw['scale'] = scale
        if accum is not None:
            kw['accum_out'] = accum
        self.op('act', lambda e: e.activation(out=out, in_=in_, func=func, **kw), r, w)

    def tt(self, eng, out, a, b, op, r, w):
        self.op(eng, lambda e: e.tensor_tensor(out=out, in0=a, in1=b, op=op), r, w)

    def ts(self, eng, out, a, s1, s2, op0, op1, r, w):
        if s2 is None:
            self.op(eng, lambda e: e.tensor_scalar(out=out, in0=a, scalar1=s1, scalar2=None, op0=op0), r, w)
        else:
            self.op(eng, lambda e: e.tensor_scalar(out=out, in0=a, scalar1=s1, scalar2=s2, op0=op0, op1=op1), r, w)

    def stt(self, out, a, s, b, op0, op1, r, w):
        self.op('dve', lambda e: e.scalar_tensor_tensor(out=out, in0=a, scalar=s, in1=b, op0=op0, op1=op1), r, w)

    def mm(self, out, lhsT, rhs, start, stop, r, w):
        self.op('pe', lambda e: e.matmul(out, lhsT=lhsT, rhs=rhs, start=start, stop=stop), r, w)

    def tr(self, out, in_, ident, r, w):
        self.op('pe', lambda e: e.transpose(out, in_, ident), r, w)

    def cp(self, eng, out, in_, r, w):
        if eng == 'act':
            self.op('act', lambda e: e.activation(out=out, in_=in_, func=AF.Identity), r, w)
        else:
            self.op(eng, lambda e: e.tensor_copy(out=out, in_=in_), r, w)

    def memset(self, eng, ap, val, w):
        self.op(eng, lambda e: e.memset(ap, val), (), w)


def seqT(s):
    return TX if (s % 2) == 1 else TC


def build(debug=False, n_layers=L, stop_after=None, mixcfg=None):
    nc = bass.Bass("TRN2", target_bir_lowering=False)
    em = Em(nc)
    dbgset = debug if isinstance(debug, (set, list, tuple)) else None

    def din(name, shape, dt=F32):
        return nc.dram_tensor(name, list(shape), dt, kind="ExternalInput").ap()

    def dscr(name, shape, dt=F32):
        isdbg = (debug is True) or (dbgset is not None and name.rstrip('0123456789') in dbgset)
        return nc.dram_tensor(name, list(shape), dt, kind="ExternalOutput" if isdbg else "Internal").ap()

    xT = din("xT", [NBL, D, TX])
    ctxT = din("ctxT", [NBL, D, TC])
    cT = din("cT", [D, 3])
    w_mod = din("w_mod", [L, D, 6 * D])
    w_in = din("w_in", [L, D, 3504])
    w_out = din("w_out", [L, D, D])
    r_w2 = din("r_w2", [L, 2, 64, 512])
    r_a2 = din("r_a2", [L, 2, 64, 512])
    r_g2 = din("r_g2", [L, 160, 512])
    f_w_up = din("f_w_up", [L, D, 2 * DFF])
    f_w_down = din("f_w_down", [L, DFF, D])
    cst = din("cst", [128, NCST])
    pv = din("pv", [L, 128, NPV])
    pv64 = din("pv64", [L, 64, NPV64])
    rowv = din("rowv", [L, 1, NROW])
    outT = nc.dram_tensor("outT", [NBL, D, TX], F32, kind="ExternalOutput").ap()

    NS = 2 * NBL
    RESA = [dscr(f"resa{s}", [D, seqT(s)]) for s in range(NS)]
    RESB = [dscr(f"resb{s}", [D, seqT(s)]) for s in range(NS)]
    XBC = [dscr(f"xbc{s}", [D, seqT(s)]) for s in range(NS)]
    RKV = [dscr(f"rkv{s}", [24, 64, seqT(s)]) for s in range(NS)]
    XWA = [dscr(f"xwa{s}", [4, 64, seqT(s)]) for s in range(NS)]
    XG = [dscr(f"xg{s}", [160, seqT(s)]) for s in range(NS)]
    ZDT = [dscr(f"zdt{s}", [seqT(s), 528]) for s in range(NS)]
    YF = [dscr(f"yf{s}", [seqT(s), 512]) for s in range(NS)]
    OF = [dscr(f"of{s}", [seqT(s), 520]) for s in range(NS)]
    MIX = [dscr(f"mix{s}", [D, seqT(s)], BF16) for s in range(NS)]
    GATE = [dscr(f"gate{s}", [DFF, seqT(s)]) for s in range(NS)]
    VAL = [dscr(f"val{s}", [DFF, seqT(s)], BF16) for s in range(NS)]
    ACTV = [dscr(f"actv{s}", [DFF, seqT(s)], BF16) for s in range(NS)]

    def fm(ap):
        return ap.rearrange("(k p) t -> p k t", p=128)

    def sb(name, shape, dt=F32):
        return nc.alloc_sbuf_tensor(name, list(shape), dt).ap()

    CST = sb("CST", [128, NCST])
    CSTB = sb("CSTB", [128, 768], BF16)
    PV = sb("PV", [128, NPV])
    PV64 = sb("PV64", [64, NPV64])
    ROWB = sb("ROWB", [128, NROW])
    MOD = sb("MOD", [128, 48, 3])
    DER = sb("DER", [128, 4, 8, 3])
    NEGA = sb("NEGA", [128, 16])
    OMMU = sb("OMMU", [64, 8])
    em.dma('sp', CST, cst, writes=['CST'])
    em.cp('dve', CSTB, CST[:, 0:768], ['CST'], ['CSTB'])
    ident = CST[:, C_ID:C_ID + 128]
    identb = CSTB[:, C_ID:C_ID + 128]
    onesb = CSTB[:, C_ONE:C_ONE + 128]
    ones = CST[:, C_ONE:C_ONE + 128]

    def stage_scope():
        return ExitStack()

    def stage_mod(l):
        em.dma('sp', PV, pv[l], writes=['PV'])
        em.dma('sp', PV64, pv64[l], writes=['PV64'])
        em.dma('pool', ROWB, rowv[l].partition_broadcast(128), writes=['ROWB'])
        with ExitStack() as es:
            def S(name, shape, dt=F32):
                return es.enter_context(nc.sbuf_tensor(name, list(shape), dt)).ap()
            cts = S(f"cts{l}", [128, 8, 3])
            sc = S(f"sc{l}", [128, 8, 3])
            wst = [S(f"wmst{l}_{i}", [128, 8, 512]) for i in range(2)]
            ps = es.enter_context(nc.psum_tensor(f"psmod{l}", [128, 512], F32)).ap()
            em.dma('sp', cts, cT.rearrange("(k p) j -> p k j", p=128), writes=['cts'])
            em.act(sc, cts, AF.Silu, ['cts'], ['sc'])
            for g in range(12):
                w = wst[g % 2]
                wk = f"wmst{g % 2}"
                em.dma('sp' if g % 2 == 0 else 'pool', w,
                       w_mod[l][:, g * 512:(g + 1) * 512].rearrange("(k p) n -> p k n", p=128), writes=[wk])
                for mi in range(4):
                    m = g * 4 + mi
                    for k in range(8):
                        em.mm(ps[:, m * 3:(m + 1) * 3], w[:, k, mi * 128:(mi + 1) * 128], sc[:, k, :],
                              k == 0, k == 7, [wk, 'sc'], ['psmod'])
            em.tt('dve', MOD, ps[:, 0:144].rearrange("p (m j) -> p m j", j=3),
                  PV[:, PV_BMOD:PV_BMOD + 48].unsqueeze(2).to_broadcast([128, 48, 3]), ALU.add,
                  ['psmod', 'PV'], ['MOD'])
            tmp = S(f"dertmp{l}", [128, 8, 3])

            def gain(idx, goff, mlo, plus1):
                if plus1:
                    em.ts('dve', tmp, MOD[:, mlo:mlo + 8, :], 1.0, None, ALU.add, None, ['MOD'], ['dertmp'])
                    src = tmp
                    rk = ['dertmp', 'PV']
                else:
                    src = MOD[:, mlo:mlo + 8, :]
                    rk = ['MOD', 'PV']
                em.tt('dve', DER[:, idx, :, :], src,
                      PV[:, goff:goff + 8].unsqueeze(2).to_broadcast([128, 8, 3]), ALU.mult, rk, ['DER'])
            gain(0, PV_GPRE1, 8, True)
            gain(1, PV_GPOST1, 16, False)
            gain(2, PV_GPRE2, 32, True)
            gain(3, PV_GPOST2, 40, False)
            em.act(NEGA, ROWB[:, RV_ALOG:RV_ALOG + 16], AF.Exp, ['ROWB'], ['NEGA'])
            em.ts('dve', NEGA, NEGA, -1.0, None, ALU.mult, None, ['NEGA'], ['NEGA'])
            em.ts('dve', OMMU, PV64[:, P6_KA:P6_KA + 8], -1.0, 1.0, ALU.mult, ALU.add, ['PV64'], ['OMMU'])
            em.barrier()

    def prenorm(xt, TW, h, sq, rstd, ps, jmod, gidx, sidx, kx, kh, kps):
        em.act(sq[:, :, :TW], xt[:, :, :TW], AF.Square, [kx], ['sq'])
        for k in range(8):
            em.mm(ps[:, :TW], onesb, sq[:, k, :TW], k == 0, k == 7, ['sq', 'CSTB'], [kps])
        em.act(rstd[:, :TW], ps[:, :TW], AF.Sqrt, [kps], ['rstd'], bias=EPS, scale=1.0 / D)
        em.op('dve', lambda e: e.reciprocal(out=rstd[:, :TW], in_=rstd[:, :TW]), ['rstd'], ['rstd'])
        for k in range(8):
            em.stt(xt[:, k, :TW], xt[:, k, :TW], DER[:, gidx, k, jmod:jmod + 1], rstd[:, :TW], ALU.mult, ALU.mult,
                   [kx, 'rstd', 'DER'], [kx])
            em.act(h[:, k, :TW], xt[:, k, :TW], AF.Identity, [kx, 'MOD'], [kh],
                   bias=MOD[:, sidx + k, jmod:jmod + 1], scale=1.0)

    def load_wbf(es, l, wap, Kc, N, name, piece=None):
        wb = es.enter_context(nc.sbuf_tensor(f"{name}bf{l}", [128, Kc, N], BF16)).ap()
        with ExitStack() as e2:
            sts = [e2.enter_context(nc.sbuf_tensor(f"{name}st{l}_{i}", [128, N], F32)).ap() for i in range(2)]
            for k in range(Kc):
                st = sts[k % 2]
                sk = f"{name}st{k % 2}"
                em.dma('sp' if k % 2 == 0 else 'pool', st, wap[k * 128:(k + 1) * 128, :], writes=[sk])
                em.cp('act' if k % 2 == 0 else 'dve', wb[:, k, :], st, [sk], [name + 'bf'])
            em.barrier()
        return wb

    def res_src(l, s, phase):
        b = s // 2
        if phase == 0:
            if l == 0:
                return (xT[b] if s % 2 == 1 else ctxT[b]), f"in{s}"
            return RESB[s], f"resb{s}"
        return RESA[s], f"resa{s}"

    def res_dst(l, s, phase):
        b = s // 2
        if phase == 0:
            return RESA[s], f"resa{s}"
        if l == n_layers - 1 and s % 2 == 1:
            return outT[b], f"out{s}"
        return RESB[s], f"resb{s}"

    def stage_inproj(l):
        with ExitStack() as es:
            def S(name, shape, dt=F32):
                return es.enter_context(nc.sbuf_tensor(name, list(shape), dt)).ap()
            wb = load_wbf(es, l, w_in[l], 8, 3504, "win")
            xt = S(f"ipx{l}", [128, 8, 512])
            h = S(f"iph{l}", [128, 8, 512], BF16)
            sq = S(f"ipsq{l}", [128, 8, 512], BF16)
            rstd = S(f"iprs{l}", [128, 512])
            sta = [S(f"ipsta{l}_{i}", [128, 8, 512]) for i in range(2)]
            stw = S(f"ipstw{l}", [64, 4, 512])
            stg0 = S(f"ipstg0{l}", [128, 512])
            stg1 = S(f"ipstg1{l}", [32, 512])
            stz = S(f"ipstz{l}", [128, 4, 528])
            pss = [es.enter_context(nc.psum_tensor(f"ipps{l}_{i}", [128, 512], F32)).ap() for i in range(8)]
            groups = [('xbc', 512, 128, 8), ('r', 1552, 64, 8), ('k', 2064, 64, 8), ('v', 2576, 64, 8)]
            ev = 0
            for s in range(NS):
                T = seqT(s)
                TW = min(512, T)
                jmod = 2 if s % 2 == 0 else s // 2
                src, srck = res_src(l, s, 0)
                for tt_ in range(T // TW):
                    t0 = tt_ * TW
                    em.dma('sp', xt[:, :, :TW], fm(src)[:, :, t0:t0 + TW], reads=[srck], writes=['ipx'])
                    prenorm(xt, TW, h, sq, rstd, pss[7], jmod, 0, 0, 'ipx', 'iph', 'ps7')
                    pi = 0
                    for gi, (gname, c0, wdt, nb) in enumerate(groups):
                        st = sta[gi % 2]
                        stk = f"ipsta{gi % 2}"
                        for j in range(nb):
                            ps = pss[pi % 6]
                            pk = f"ps{pi % 6}"
                            pi += 1
                            cc = c0 + j * wdt
                            for k in range(8):
                                em.mm(ps[:wdt, :TW], wb[:, k, cc:cc + wdt], h[:, k, :TW], k == 0, k == 7,
                                      ['winbf', 'iph'], [pk])
                            em.cp('act' if ev % 2 == 0 else 'dve', st[:wdt, j, :TW], ps[:wdt, :TW], [pk], [stk])
                            ev += 1
                        if gname == 'xbc':
                            em.dma('pool', fm(XBC[s])[:, :, t0:t0 + TW], st[:, :, :TW], reads=[stk], writes=[f"xbc{s}"])
                        else:
                            jb = {'r': 0, 'k': 8, 'v': 16}[gname]
                            em.dma('pool', RKV[s][jb:jb + 8].rearrange("j p t -> p j t")[:, :, t0:t0 + TW],
                                   st[:64, :, :TW], reads=[stk], writes=[f"rkv{s}"])
                    for j in range(4):
                        ps = pss[pi % 6]
                        pk = f"ps{pi % 6}"
                        pi += 1
                        cc = 3088 + j * 64
                        for k in range(8):
                            em.mm(ps[:64, :TW], wb[:, k, cc:cc + 64], h[:, k, :TW], k == 0, k == 7, ['winbf', 'iph'], [pk])
                        em.cp('act' if ev % 2 == 0 else 'dve', stw[:, j, :TW], ps[:64, :TW], [pk], ['ipstw'])
                        ev += 1
                    em.dma('pool', XWA[s].rearrange("j p t -> p j t")[:, :, t0:t0 + TW], stw[:, :, :TW],
                           reads=['ipstw'], writes=[f"xwa{s}"])
                    for (cc, wdt, st, stk, r0) in [(3344, 128, stg0, 'ipstg0', 0), (3472, 32, stg1, 'ipstg1', 128)]:
                        ps = pss[pi % 6]
                        pk = f"ps{pi % 6}"
                        pi += 1
                        for k in range(8):
                            em.mm(ps[:wdt, :TW], wb[:, k, cc:cc + wdt], h[:, k, :TW], k == 0, k == 7, ['winbf', 'iph'], [pk])
                        em.cp('act' if ev % 2 == 0 else 'dve', st[:wdt, :TW], ps[:wdt, :TW], [pk], [stk])
                        ev += 1
                        em.dma('pool', XG[s][r0:r0 + wdt, t0:t0 + TW], st[:wdt, :TW], reads=[stk], writes=[f"xg{s}"])
                    for i in range(TW // 128):
                        ps = pss[pi % 6]
                        pk = f"ps{pi % 6}"
                        pi += 1
                        ps2 = pss[6]
                        for k in range(8):
                            em.mm(ps[:, 0:512], h[:, k, i * 128:(i + 1) * 128], wb[:, k, 0:512], k == 0, k == 7,
                                  ['winbf', 'iph'], [pk])
                        for k in range(8):
                            em.mm(ps2[:, 0:16], h[:, k, i * 128:(i + 1) * 128], wb[:, k, 1536:1552], k == 0, k == 7,
                                  ['winbf', 'iph'], ['ps6'])
                        em.cp('act', stz[:, i, 0:512], ps[:, 0:512], [pk], ['ipstz'])
                        em.cp('dve', stz[:, i, 512:528], ps2[:, 0:16], ['ps6'], ['ipstz'])
                    em.dma('pool', ZDT[s][t0:t0 + TW, :].rearrange("(i p) c -> p i c", p=128), stz[:, :TW // 128, :],
                           reads=['ipstz'], writes=[f"zdt{s}"])
            em.barrier()

    def stage_mixer(l, need_ctx_out):
        with ExitStack() as es:
            def S(name, shape, dt=F32):
                return es.enter_context(nc.sbuf_tensor(name, list(shape), dt)).ap()
            w2b = S(f"w2b{l}", [64, 2, 512], BF16)
            a2b = S(f"a2b{l}", [64, 2, 512], BF16)
            g2b0 = S(f"g2b0{l}", [128, 512], BF16)
            g2b1 = S(f"g2b1{l}", [32, 512], BF16)
            with ExitStack() as e2:
                t1 = e2.enter_context(nc.sbuf_tensor(f"lst1{l}", [64, 2, 512], F32)).ap()
                t2 = e2.enter_context(nc.sbuf_tensor(f"lst2{l}", [64, 2, 512], F32)).ap()
                t3 = e2.enter_context(nc.sbuf_tensor(f"lst3{l}", [128, 512], F32)).ap()
                t4 = e2.enter_context(nc.sbuf_tensor(f"lst4{l}", [32, 512], F32)).ap()
                em.dma('sp', t1, r_w2[l].rearrange("d r c -> r d c"), writes=['lst1'])
                em.dma('sp', t2, r_a2[l].rearrange("d r c -> r d c"), writes=['lst2'])
                em.dma('sp', t3, r_g2[l][0:128, :], writes=['lst3'])
                em.dma('sp', t4, r_g2[l][128:160, :], writes=['lst4'])
                em.cp('dve', w2b, t1, ['lst1'], ['w2b'])
                em.cp('dve', a2b, t2, ['lst2'], ['a2b'])
                em.cp('dve', g2b0, t3, ['lst3'], ['g2b'])
                em.cp('dve', g2b1, t4, ['lst4'], ['g2b'])
                em.barrier()
            MAR = [None, None]
            MARt = S(f"mar{l}", [128, 2, 256])
            em.cp('dve', MARt[:, 0, 0:128], CST[:, C_UTS:C_UTS + 128], ['CST'], ['MAR'])
            em.cp('dve', MARt[:, 0, 128:256], CST[:, C_UTI:C_UTI + 128], ['CST'], ['MAR'])
            em.cp('dve', MARt[:, 1, 0:128], CST[:, C_LTS:C_LTS + 128], ['CST'], ['MAR'])
            em.cp('dve', MARt[:, 1, 128:256], CST[:, C_LTI:C_LTI + 128], ['CST'], ['MAR'])
            MSO = S(f"mso{l}", [128, 2, 256])
            em.cp('dve', MSO[:, 0, 0:128], CST[:, C_LTS:C_LTS + 128], ['CST'], ['MSO'])
            em.cp('dve', MSO[:, 1, 0:128], CST[:, C_UTS:C_UTS + 128], ['CST'], ['MSO'])
            em.cp('dve', MSO[:, 0, 128:256], ones, ['CST'], ['MSO'])
            em.cp('dve', MSO[:, 1, 128:256], ones, ['CST'], ['MSO'])
            RMK = S(f"rmk{l}", [64, 8, 128], BF16)
            em.memset('pool', RMK, 1.0, ['RMK'])
            em.memset('pool', RMK[:, :, 0:1], 0.0, ['RMK'])

            def MI(d):
                return CST[:, C_UTI:C_UTI + 128] if d == 0 else CST[:, C_LTI:C_LTI + 128]

            def strictTS(d):
                return CST[:, C_LTS:C_LTS + 128] if d == 0 else CST[:, C_UTS:C_UTS + 128]

            xbc = S(f"m_xbc{l}", [128, 8, 130])
            cacc = S(f"m_cacc{l}", [128, 8, 128])
            bct = S(f"m_bct{l}", [128, 4, 128], BF16)
            xtok = S(f"m_xtok{l}", [128, 512])
            xtokb = S(f"m_xtokb{l}", [128, 512], BF16)
            btokb = S(f"m_btokb{l}", [128, 256], BF16)
            zdt = S(f"m_zdt{l}", [128, 528])
            dts = S(f"m_dts{l}", [128, 8])
            dta = S(f"m_dta{l}", [128, 8])
            sm = S(f"m_sm{l}", [128, 40])
            xw = S(f"m_xw{l}", [128, 512], BF16)
            l2 = [S(f"m_l2{l}_{i}", [128, 256]) for i in range(2)]
            Et = [S(f"m_E{l}_{i}", [128, 256]) for i in range(2)]
            LTt = [S(f"m_LT{l}_{i}", [128, 128]) for i in range(2)]
            STt = [S(f"m_ST{l}_{i}", [128, 128], BF16) for i in range(2)]
            CsT = [S(f"m_Cs{l}_{i}", [128, 128], BF16) for i in range(2)]
            hst = S(f"m_hst{l}", [128, 512])
            hstb = S(f"m_hstb{l}", [128, 512], BF16)
            yt = S(f"m_y{l}", [128, 512])
            yf = S(f"m_yf{l}", [128, 512])
            zs = S(f"m_zs{l}", [128, 512])
            ysq = S(f"m_ysq{l}", [128, 512])
            gst = S(f"m_gst{l}", [128, 4])
            mixo = S(f"mixo{l}", [128, 8, 128], BF16)
            raw = S(f"r_raw{l}", [64, 26, 130])
            ssum = S(f"r_ssum{l}", [64, 26, 128])
            pp = ssum
            xg0 = S(f"r_xg0{l}", [128, 130])
            xg1 = S(f"r_xg1{l}", [32, 130])
            sg0 = S(f"r_sg0{l}", [128, 128], BF16)
            sg1 = S(f"r_sg1{l}", [32, 128], BF16)
            xgt = S(f"r_xgt{l}", [128, 128])
            twb = S(f"r_twb{l}", [64, 2, 128], BF16)
            lw = S(f"r_lw{l}", [64, 8, 128])
            aa = S(f"r_aa{l}", [64, 8, 128])
            kk = S(f"r_kk{l}", [64, 8, 128])
            kd = S(f"r_kd{l}", [64, 8, 128])
            linc = S(f"r_linc{l}", [64, 8, 128])
            lex = S(f"r_lex{l}", [64, 8, 128])
            rinv = lex
            e1 = S(f"r_e1{l}", [64, 8, 128])
            e0 = S(f"r_e0{l}", [64, 8, 128])
            ei = S(f"r_ei{l}", [64, 8, 128])
            gCs = [S(f"r_gC{l}_{i}", [64, 8]) for i in range(2)]
            coefs = [S(f"r_coef{l}_{i}", [128, 8]) for i in range(2)]
            tmpk = S(f"r_tmpk{l}", [64, 8, 128])
            ARs = [S(f"r_AR{l}_{i}", [64, 8, 256], BF16) for i in range(2)]
            BKs = [S(f"r_BK{l}_{i}", [64, 8, 2, 128], BF16) for i in range(2)]
            prodb = S(f"r_prod{l}", [64, 8, 128], BF16)
            sqb = prodb
            vtokbs = [S(f"r_vtokb{l}_{i}", [128, 512], BF16) for i in range(2)]
            BKtoks = [S(f"r_BKtok{l}_{i}", [128, 8, 2, 64], BF16) for i in range(2)]
            GBs = [S(f"r_GB{l}_{i}", [128, 8, 256], BF16) for i in range(2)]
            GKs = [S(f"r_GK{l}_{i}", [128, 8, 256], BF16) for i in range(2)]
            Q0s = [S(f"r_Q0{l}_{i}", [128, 8, 128], BF16) for i in range(2)]
            Pm = [S(f"r_P{l}_{i}", [128, 8, 128], BF16) for i in range(2)]
            Qm = [S(f"r_Q{l}_{i}", [128, 8, 128], BF16) for i in range(2)]
            Ym = [S(f"r_Y{l}_{i}", [128, 8, 128], BF16) for i in range(2)]
            E1m = S(f"r_E1{l}", [128, 8, 128], BF16)
            E2m = S(f"r_E2{l}", [128, 8, 128], BF16)
            Dm = S(f"r_D{l}", [128, 8, 128], BF16)
            Zm = S(f"r_Z{l}", [128, 8, 128], BF16)
            Wt = S(f"r_W{l}", [128, 512], BF16)
            Ut = S(f"r_U{l}", [128, 512], BF16)
            Hs = S(f"r_H{l}", [64, 512])
            Hb = S(f"r_Hb{l}", [64, 512], BF16)
            ot = S(f"r_o{l}", [128, 520])
            oft = S(f"r_of{l}", [128, 520])
            osq = S(f"r_osq{l}", [128, 512])
            gn = S(f"r_gn{l}", [128, 40])
            B = [es.enter_context(nc.psum_tensor(f"mxps{l}_{i}", [128, 512], F32)).ap() for i in range(7)]
            BBp = es.enter_context(nc.psum_tensor(f"mxpsb{l}", [128, 1024], BF16)).ap()

            nbw = ROWB[:, RV_MNW:RV_MNW + 512]
            lnw = ROWB[:, RV_LNW:RV_LNW + 512]
            lnb = ROWB[:, RV_LNB:RV_LNB + 512]
            mdb = ROWB[:, RV_MD:RV_MD + 8]
            evc = [0]

            def evq():
                evc[0] += 1
                return 'act' if evc[0] % 2 == 0 else 'dve'

            def load_halo(tile, tk, srcap, srck, t0, T, nblk_dims):
                lo = max(t0 - 1, 0)
                hi = min(t0 + 129, T)
                o = lo - (t0 - 1)
                if nblk_dims:
                    em.dma('sp', tile[:, :, o:o + hi - lo], srcap[:, :, lo:hi], reads=[srck], writes=[tk])
                    if t0 == 0:
                        em.memset('pool', tile[:, :, 0:1], 0.0, [tk])
                    if t0 + 128 == T:
                        em.memset('pool', tile[:, :, 129:130], 0.0, [tk])
                else:
                    em.dma('sp', tile[:, o:o + hi - lo], srcap[:, lo:hi], reads=[srck], writes=[tk])
                    if t0 == 0:
                        em.memset('pool', tile[:, 0:1], 0.0, [tk])
                    if t0 + 128 == T:
                        em.memset('pool', tile[:, 129:130], 0.0, [tk])

            def mamba_chunk(s, t0, d, emit_out):
                T = seqT(s)
                load_halo(xbc, 'm_xbc', fm(XBC[s]), f"xbc{s}", t0, T, True)
                em.dma('pool', zdt, ZDT[s][t0:t0 + 128, :], reads=[f"zdt{s}"], writes=['m_zdt'])
                if (mixcfg or {}).get('mstop', 99) <= 1:
                    return
                for j in range(8):
                    em.ts('dve', cacc[:, j, :], xbc[:, j, 1:129], PV[:, PV_MCW + 8 + j:PV_MCW + 9 + j],
                          PV[:, PV_MCB + j:PV_MCB + j + 1], ALU.mult, ALU.add, ['m_xbc', 'PV'], ['m_cacc'])
                    em.stt(cacc[:, j, :], xbc[:, j, 0:128], PV[:, PV_MCW + j:PV_MCW + j + 1], cacc[:, j, :],
                           ALU.mult, ALU.add, ['m_xbc', 'PV', 'm_cacc'], ['m_cacc'])
                    em.stt(cacc[:, j, :], xbc[:, j, 2:130], PV[:, PV_MCW + 16 + j:PV_MCW + 17 + j], cacc[:, j, :],
                           ALU.mult, ALU.add, ['m_xbc', 'PV', 'm_cacc'], ['m_cacc'])
                em.act(cacc[:, 0:6, :], cacc[:, 0:6, :], AF.Silu, ['m_cacc'], ['m_cacc'])
                em.act(bct[:, 2:4, :], cacc[:, 6:8, :], AF.Silu, ['m_cacc'], ['m_bct'])
                em.cp('dve', bct[:, 0:2, :], cacc[:, 4:6, :], ['m_cacc'], ['m_bct'])
                if (mixcfg or {}).get('mstop', 99) <= 2:
                    return
                msub = (mixcfg or {}).get('msub', 9)
                for j in range(4):
                    em.tr(B[4][:, j * 128:(j + 1) * 128], cacc[:, j, :], ident, ['m_cacc', 'CST'], ['B4'])
                if msub >= 1:
                    for j in range(2):
                        em.tr(B[5][:, j * 128:(j + 1) * 128], cacc[:, 4 + j, :], ident, ['m_cacc', 'CST'], ['B5'])
                if msub >= 2 and msub != 33:
                    em.cp('dve', xtok, B[4], ['B4'], ['m_xtok'])
                if msub == 30:
                    em.cp('dve', xtokb, B[4], ['B4'], ['m_xtokb'])
                elif msub == 31:
                    em.cp('act', yt, B[4], ['B4'], ['m_y'])
                elif msub == 32:
                    em.cp('act', xtokb, xtok, ['m_xtok'], ['m_xtokb'])
                elif msub >= 3:
                    em.cp('act', xtokb, B[4], ['B4'], ['m_xtokb'])
                if msub >= 4:
                    em.cp('act', btokb, B[5][:, 0:256], ['B5'], ['m_btokb'])
                if (mixcfg or {}).get('mstop', 99) <= 3:
                    return
                em.tt('dve', dts, zdt[:, 512 + d * 8:520 + d * 8], ROWB[:, RV_DTB + d * 8:RV_DTB + d * 8 + 8], ALU.add,
                      ['m_zdt', 'ROWB'], ['m_dts'])
                em.act(dts, dts, AF.Exp, ['m_dts'], ['m_dts'])
                em.act(dts, dts, AF.Ln, ['m_dts'], ['m_dts'], bias=1.0, scale=1.0)
                em.tt('dve', dta, dts, NEGA[:, d * 8:d * 8 + 8], ALU.mult, ['m_dts', 'NEGA'], ['m_dta'])
                if (mixcfg or {}).get('mstop', 99) <= 4:
                    return
                em.mm(B[5][:, 256:264], MI(d), dta, True, True, ['CST', 'm_dta'], ['B5'])
                em.mm(B[5][:, 264:272], ones, dta, True, True, ['CST', 'm_dta'], ['B5'])
                em.cp('act', sm[:, 32:40], B[5][:, 264:272], ['B5'], ['m_sm'])
                em.tt('dve', sm[:, 24:32], sm[:, 32:40], B[5][:, 256:264], ALU.subtract, ['B5', 'm_sm'], ['m_sm'])
                em.act(sm[:, 0:8], sm[:, 24:32], AF.Exp, ['m_sm'], ['m_sm'])
                em.tt('dve', sm[:, 8:16], sm[:, 0:8], dts, ALU.mult, ['m_sm', 'm_dts'], ['m_sm'])
                em.act(sm[:, 16:24], B[5][:, 264:272], AF.Exp, ['B5'], ['m_sm'])
                em.tt('dve', xw.rearrange("p (h q) -> p h q", q=64), xtok.rearrange("p (h q) -> p h q", q=64),
                      sm[:, 8:16].unsqueeze(2).to_broadcast([128, 8, 64]), ALU.mult, ['m_xtok', 'm_sm'], ['m_xw'])
                if (mixcfg or {}).get('mstop', 99) <= 5:
                    return
                for g in range(2):
                    em.mm(B[6][:, g * 128:(g + 1) * 128], bct[:, g, :], bct[:, 2 + g, :], True, True, ['m_bct'], ['B6'])
                for h in range(8):
                    g = h // 4
                    i2 = h % 2
                    pe_ = B[6][:, 256:512] if i2 == 0 else B[5][:, 0:256]
                    pek = 'B6' if i2 == 0 else 'B5'
                    em.ts('dve', l2[i2], MSO[:, d, :], dta[:, h:h + 1], None, ALU.mult, None,
                          ['MSO', 'm_dta'], [f"m_l2{i2}"])
                    em.mm(pe_[:, 0:128], l2[i2][:, 0:128], MI(d), True, True, [f"m_l2{i2}", 'CST'], [pek])
                    em.mm(pe_[:, 128:256], l2[i2][:, 128:256], MI(d), True, True, [f"m_l2{i2}", 'CST'], [pek])
                    em.act(Et[i2], pe_[:, 0:256], AF.Exp, [pek], [f"m_E{i2}"])
                    em.stt(LTt[i2], Et[i2][:, 0:128], dts[:, h:h + 1], MI(d), ALU.mult, ALU.mult,
                           [f"m_E{i2}", 'm_dts', 'CST'], [f"m_LT{i2}"])
                    em.tt('dve', STt[i2], B[6][:, g * 128:(g + 1) * 128], LTt[i2], ALU.mult, ['B6', f"m_LT{i2}"],
                          [f"m_ST{i2}"])
                    em.tt('dve', CsT[i2], bct[:, 2 + g, :], Et[i2][:, 128:256], ALU.mult, ['m_bct', f"m_E{i2}"],
                          [f"m_Cs{i2}"])
                    em.mm(B[4][:, h * 64:(h + 1) * 64], STt[i2], xtokb[:, h * 64:(h + 1) * 64], True, False,
                          [f"m_ST{i2}", 'm_xtokb'], ['B4'])
                    em.mm(B[4][:, h * 64:(h + 1) * 64], CsT[i2], hstb[:, h * 64:(h + 1) * 64], False, True,
                          [f"m_Cs{i2}", 'm_hstb'], ['B4'])
                if (mixcfg or {}).get('mstop', 99) <= 6:
                    return
                for g in range(2):
                    em.mm(B[6][:, g * 256:(g + 1) * 256], btokb[:, g * 128:(g + 1) * 128], xw[:, g * 256:(g + 1) * 256],
                          True, True, ['m_btokb', 'm_xw'], ['B6'])
                em.tt('dve', hst.rearrange("p (h q) -> p h q", q=64), hst.rearrange("p (h q) -> p h q", q=64),
                      sm[:, 16:24].unsqueeze(2).to_broadcast([128, 8, 64]), ALU.mult, ['m_hst', 'm_sm', 'B4'], ['m_hst'])
                em.tt('dve', hst, hst, B[6], ALU.add, ['m_hst', 'B6'], ['m_hst'])
                em.cp('act', hstb, hst, ['m_hst', 'B4'], ['m_hstb'])
                if (mixcfg or {}).get('mstop', 99) <= 7:
                    return
                if d == 0:
                    if emit_out:
                        em.cp('act', yt, B[4], ['B4'], ['m_y'])
                        em.dma('pool', YF[s][t0:t0 + 128, :], yt, reads=['m_y'], writes=[f"yf{s}"])
                elif emit_out:
                    em.dma('sp', yf, YF[s][t0:t0 + 128, :], reads=[f"yf{s}"], writes=['m_yf'])
                    em.tt('dve', yt, B[4], yf, ALU.add, ['B4', 'm_yf'], ['m_y'])
                    em.tt('dve', yf.rearrange("p (h q) -> p h q", q=64), xtok.rearrange("p (h q) -> p h q", q=64),
                          mdb.unsqueeze(2).to_broadcast([128, 8, 64]), ALU.mult, ['m_xtok', 'ROWB', 'm_yf'], ['m_yf'])
                    em.tt('dve', yt, yt, yf, ALU.add, ['m_y', 'm_yf'], ['m_y'])
                    em.act(zs, zdt[:, 0:512], AF.Silu, ['m_zdt'], ['m_zs'])
                    em.tt('dve', yt, yt, zs, ALU.mult, ['m_y', 'm_zs'], ['m_y'])
                    for g in range(2):
                        em.act(ysq[:, g * 256:(g + 1) * 256], yt[:, g * 256:(g + 1) * 256], AF.Square, ['m_y'],
                               ['m_ysq', 'm_gst'], accum=gst[:, g:g + 1])
                    em.act(gst[:, 2:4], gst[:, 0:2], AF.Sqrt, ['m_gst'], ['m_gst'], bias=EPS, scale=1.0 / 256)
                    em.op('dve', lambda e: e.reciprocal(out=gst[:, 2:4], in_=gst[:, 2:4]), ['m_gst'], ['m_gst'])
                    for g in range(2):
                        em.stt(yt[:, g * 256:(g + 1) * 256], yt[:, g * 256:(g + 1) * 256], gst[:, 2 + g:3 + g],
                               nbw[:, g * 256:(g + 1) * 256], ALU.mult, ALU.mult, ['m_y', 'm_gst', 'ROWB'], ['m_y'])
                    for j in range(4):
                        em.tr(B[5][:, j * 128:(j + 1) * 128], yt[:, j * 128:(j + 1) * 128], ident, ['m_y', 'CST'], ['B5'])
                    em.cp('act', mixo[:, 0:4, :], B[5].rearrange("p (j t) -> p j t", t=128), ['B5'], ['mixo_m'])
                    em.dma('pool', fm(MIX[s])[:, 0:4, t0:t0 + 128], mixo[:, 0:4, :], reads=['mixo_m'], writes=[f"mixm{s}"])

            def wkv_chunk(s, t0, d, emit_out, idx=0, first_of_d=False, split=True):
                T = seqT(s)
                pb = idx % 2
                AR, BK, GB, GK, Q0, BKtok, vtokb, gC, coef = (ARs[pb], BKs[pb], GBs[pb], GKs[pb], Q0s[pb], BKtoks[pb],
                                                             vtokbs[pb], gCs[pb], coefs[pb])
                kAR, kBK, kGB, kGK, kQ0, kBKtok, kvtokb, kgC, kcoef = [f"{n}{pb}" for n in
                                                                      ('r_AR', 'r_BK', 'r_GB', 'r_GK', 'r_Q0h', 'r_BKtok',
                                                                       'r_vtokb', 'r_gC', 'r_coef')]
                if split:
                    em.stream = ('rp', idx)
                fin = (d == 1 and emit_out)
                rk = f"rkv{s}"
                load_halo(raw[:, 0:24, :], 'r_raw', RKV[s].rearrange("j p t -> p j t"), rk, t0, T, True)
                load_halo(raw[:, 24:25, :], 'r_raw', XWA[s][d:d + 1].rearrange("j p t -> p j t"), f"xwa{s}", t0, T, True)
                load_halo(raw[:, 25:26, :], 'r_raw', XWA[s][2 + d:3 + d].rearrange("j p t -> p j t"), f"xwa{s}", t0, T, True)
                em.tt('dve', ssum, raw[:, :, 0:128], raw[:, :, 2:130], ALU.add, ['r_raw'], ['r_ssum'])
                em.stt(ssum, ssum, 0.5, raw[:, :, 1:129], ALU.mult, ALU.subtract, ['r_ssum', 'r_raw'], ['r_ssum'])
                em.tt('dve', ssum[:, 0:24, :], ssum[:, 0:24, :],
                      PV64[:, P6_MURKV:P6_MURKV + 24].unsqueeze(2).to_broadcast([64, 24, 128]), ALU.mult,
                      ['r_ssum', 'PV64'], ['r_ssum'])
                em.ts('dve', ssum[:, 24, :], ssum[:, 24, :], PV64[:, P6_MUWA + d:P6_MUWA + d + 1], None, ALU.mult, None,
                      ['r_ssum', 'PV64'], ['r_ssum'])
                em.ts('dve', ssum[:, 25, :], ssum[:, 25, :], PV64[:, P6_MUWA + 2 + d:P6_MUWA + 3 + d], None, ALU.mult, None,
                      ['r_ssum', 'PV64'], ['r_ssum'])
                em.tt('dve', pp, raw[:, :, 1:129], ssum, ALU.add, ['r_raw', 'r_ssum'], ['r_ssum'])
                rr = pp[:, 0:8, :]
                kr = pp[:, 8:16, :]
                vr = pp[:, 16:24, :]
                em.act(twb[:, 0, :], pp[:, 24, :], AF.Tanh, ['r_ssum'], ['r_twb'])
                em.cp('dve', twb[:, 1, :], pp[:, 25, :], ['r_ssum'], ['r_twb'])
                for h in range(8):
                    em.mm(B[h // 4][0:64, (h % 4) * 128:(h % 4 + 1) * 128], w2b[:, d, h * 64:(h + 1) * 64], twb[:, 0, :], True, True,
                          ['w2b', 'r_twb'], [f"B{h // 4}"])
                for h in range(8):
                    em.act(lw[:, h, :], B[h // 4][0:64, (h % 4) * 128:(h % 4 + 1) * 128], AF.Sigmoid, [f"B{h // 4}", 'PV64'],
                           ['r_lw'], bias=PV64[:, P6_W0 + d * 8 + h:P6_W0 + d * 8 + h + 1], scale=1.0)
                for h in range(8):
                    em.mm(B[h // 4][0:64, (h % 4) * 128:(h % 4 + 1) * 128], a2b[:, d, h * 64:(h + 1) * 64], twb[:, 1, :], True, True,
                          ['a2b', 'r_twb'], [f"B{h // 4}"])
                for h in range(8):
                    em.act(aa[:, h, :], B[h // 4][0:64, (h % 4) * 128:(h % 4 + 1) * 128], AF.Sigmoid,
                           [f"B{h // 4}", 'PV64'], ['r_aa'], bias=PV64[:, P6_A0 + d * 8 + h:P6_A0 + d * 8 + h + 1], scale=1.0)
                em.ts('dve', lw, lw, -R_DECAY_SCALE, None, ALU.mult, None, ['r_lw'], ['r_lw'])
                em.tt('dve', kk, kr, PV64[:, P6_KK:P6_KK + 8].unsqueeze(2).to_broadcast([64, 8, 128]), ALU.mult,
                      ['r_ssum', 'PV64'], ['r_kk'])
                em.act(sqb, kk, AF.Square, ['r_kk'], ['r_prod'])
                for hh in range(2):
                    em.mm(B[hh][0:64, :], onesb[0:64, 0:64], sqb[:, hh * 4:(hh + 1) * 4, :], True, True,
                          ['CSTB', 'r_prod'], [f"B{hh}"])
                for hh in range(2):
                    em.act(rinv[:, hh * 4:(hh + 1) * 4, :], B[hh][0:64, :].rearrange("p (h t) -> p h t", t=128), AF.Sqrt,
                           [f"B{hh}"], ['r_lex'])
                em.ts('dve', rinv, rinv, 1e-12, None, ALU.max, None, ['r_lex'], ['r_lex'])
                em.op('dve', lambda e: e.reciprocal(out=rinv, in_=rinv), ['r_lex'], ['r_lex'])
                em.tt('dve', kk, kk, rinv, ALU.mult, ['r_kk', 'r_lex'], ['r_kk'])
                em.tt('dve', tmpk, aa, PV64[:, P6_KA:P6_KA + 8].unsqueeze(2).to_broadcast([64, 8, 128]), ALU.mult,
                      ['r_aa', 'PV64'], ['r_tmpk'])
                em.tt('dve', tmpk, tmpk, OMMU.unsqueeze(2).to_broadcast([64, 8, 128]), ALU.add, ['r_tmpk', 'OMMU'], ['r_tmpk'])
                em.tt('dve', kd, kr, tmpk, ALU.mult, ['r_ssum', 'r_tmpk'], ['r_kd'])
                em.op('dve', lambda e: e.tensor_tensor_scan(out=linc.rearrange("p h t -> p (h t)"),
                                                            data0=RMK.rearrange("p h t -> p (h t)"),
                                                            data1=lw.rearrange("p h t -> p (h t)"), initial=0.0,
                                                            op0=ALU.mult, op1=ALU.add), ['RMK', 'r_lw'], ['r_linc'])
                if d == 0:
                    tot = linc[:, :, 127:128]
                else:
                    em.tt('dve', lex, lw, linc, ALU.subtract, ['r_lw', 'r_linc'], ['r_lex'])
                    em.cp('dve', gC, linc[:, :, 127], ['r_linc'], [kgC])
                    em.tt('dve', linc, lex, gC.unsqueeze(2).to_broadcast([64, 8, 128]), ALU.add, ['r_lex', kgC, 'r_linc'],
                          ['r_linc'])
                    tot = linc[:, :, 0:1]
                em.tt('dve', lex, linc, lw, ALU.subtract, ['r_linc', 'r_lw'], ['r_lex'])
                em.act(e1, linc, AF.Exp, ['r_linc'], ['r_e1'])
                em.act(e0, lex, AF.Exp, ['r_lex'], ['r_e0'])
                em.act(ei, linc, AF.Exp, ['r_linc'], ['r_ei'], scale=-1.0)
                em.act(gC, tot.rearrange("p h o -> p (h o)"), AF.Exp, ['r_linc', kgC], [kgC])
                em.tt('dve', AR[:, :, 128:256], rr, e1, ALU.mult, ['r_ssum', 'r_e1'], [kAR])
                em.stt(AR[:, :, 0:128], kk, -1.0, e0, ALU.mult, ALU.mult, ['r_kk', 'r_e0'], [kAR])
                em.tt('dve', tmpk, kk, aa, ALU.mult, ['r_kk', 'r_aa', 'r_tmpk'], ['r_tmpk'])
                em.tt('dve', BK[:, :, 0, :], tmpk, ei, ALU.mult, ['r_tmpk', 'r_ei'], [kBK])
                em.tt('dve', BK[:, :, 1, :], kd, ei, ALU.mult, ['r_kd', 'r_ei'], [kBK])
                em.tt('dve', tmpk, rr, kd, ALU.mult, ['r_ssum', 'r_kd', 'r_tmpk'], ['r_tmpk'])
                em.tt('dve', prodb, tmpk, PV64[:, P6_RK:P6_RK + 8].unsqueeze(2).to_broadcast([64, 8, 128]), ALU.mult,
                      ['r_tmpk', 'PV64'], ['r_prod'])
                for h in range(8):
                    em.tr(B[0][:, h * 64:(h + 1) * 64], vr[:, h, :], ident[0:64, 0:64], ['r_ssum', 'CST'], ['B0'])
                em.cp('act', vtokb, B[0], ['B0'], [kvtokb])
                for half in range(2):
                    for h4 in range(4):
                        for q in range(2):
                            em.tr(BBp[:, (h4 * 2 + q) * 64:(h4 * 2 + q + 1) * 64], BK[:, half * 4 + h4, q, :], identb[0:64, 0:64],
                                  [kBK, 'CSTB'], ['BB'])
                    em.cp('dve', BKtok[:, half * 4:(half + 1) * 4].rearrange("p h q k -> p (h q k)"), BBp[:, 0:512], ['BB'], [kBKtok])
                for h in range(8):
                    em.mm(B[1][:, 256 + h:257 + h], prodb[:, h, :], onesb[0:64, 0:1], True, True, ['r_prod', 'CSTB'], ['B1'])
                em.cp('act', coef, B[1][:, 256:264], ['B1'], [kcoef])
                for hp in range(4):
                    bb_ = B[0]
                    bbk = "B0"
                    bk2 = B[1]
                    bk2k = "B1"
                    for q in range(2):
                        h = hp * 2 + q
                        em.mm(bb_[:, q * 256:(q + 1) * 256], BK[:, h, 0, :], AR[:, h, :], True, True, [kBK, kAR, 'r_lw', 'r_aa'], [bbk])
                        em.mm(bk2[:, q * 256:(q + 1) * 256], BK[:, h, 1, :], AR[:, h, :], True, True, [kBK, kAR], [bk2k])
                    em.tt('dve', GB[:, hp * 2:hp * 2 + 2, :], bb_.rearrange("p (q c) -> p q c", c=256),
                          MARt[:, d:d + 1, :].to_broadcast([128, 2, 256]), ALU.mult, [bbk, 'MAR'], [kGB])
                    em.tt('dve', Q0[:, hp * 2:hp * 2 + 2, :], bb_.rearrange("p (q c) -> p q c", c=256)[:, :, 0:128],
                          CST[:, C_MP0 + (1 - d) * 128:C_MP0 + (2 - d) * 128].unsqueeze(1).to_broadcast([128, 2, 128]), ALU.mult,
                          [bbk, 'CST'], [kQ0])
                    em.tt('dve', GK[:, hp * 2:hp * 2 + 2, :], bk2.rearrange("p (q c) -> p q c", c=256),
                          MARt[:, d:d + 1, :].to_broadcast([128, 2, 256]), ALU.mult, [bk2k, 'MAR'], [kGK])
                if split:
                    em.stream = ('rs', idx)
                if first_of_d:
                    em.memset('pool', Hs, 0.0, ['r_H'])
                    em.memset('pool', Hb, 0.0, ['r_Hb'])
                for hh in range(2):
                    bp = B[2 + hh]
                    bpk = f"B{2 + hh}"
                    for q in range(4):
                        h = hh * 4 + q
                        em.mm(bp[:, q * 128:(q + 1) * 128], AR[:, h, 0:128], BK[:, h, 0, :], True, True, [kAR, kBK], [bpk])
                    b3 = bp.rearrange("p (q c) -> p q c", c=128)
                    hsl = slice(hh * 4, (hh + 1) * 4)
                    em.tt('dve', Pm[0][:, hsl, :], b3, CST[:, C_MP0 + d * 128:C_MP0 + (d + 1) * 128].unsqueeze(1).to_broadcast([128, 4, 128]),
                          ALU.mult, [bpk, 'CST'], ['r_P0'])
                    em.tt('dve', E1m[:, hsl, :], b3, CST[:, C_ME1 + d * 128:C_ME1 + (d + 1) * 128].unsqueeze(1).to_broadcast([128, 4, 128]),
                          ALU.mult, [bpk, 'CST'], ['r_E1'])
                    em.tt('dve', E2m[:, hsl, :], b3, CST[:, C_ME2 + d * 128:C_ME2 + (d + 1) * 128].unsqueeze(1).to_broadcast([128, 4, 128]),
                          ALU.mult, [bpk, 'CST'], ['r_E2'])
                em.tt('dve', Ym[0], Q0, identb.unsqueeze(1).to_broadcast([128, 8, 128]), ALU.add, [kQ0, 'CSTB'], ['r_Y0'])
                em.cp('act', Qm[0], Q0, [kQ0], ['r_Q0'])
                cur = 0
                for lev in range(1, 5):
                    nxt = 1 - cur
                    for hh in range(2):
                        bp = B[2]
                        bpk = "B2"
                        bq = B[3]
                        bqk = "B3"
                        hsl = slice(hh * 4, (hh + 1) * 4)
                        for q in range(4):
                            h = hh * 4 + q
                            em.mm(bp[:, q * 128:(q + 1) * 128], Qm[cur][:, h, :], Pm[cur][:, h, :], True, True,
                                  [f"r_Q{cur}", f"r_P{cur}"], [bpk])
                        for q in range(4):
                            h = hh * 4 + q
                            em.mm(bq[:, q * 128:(q + 1) * 128], Pm[cur][:, h, :], Qm[cur][:, h, :], True, True,
                                  [f"r_Q{cur}", f"r_P{cur}"], [bqk])
                        em.cp('act', Pm[nxt][:, hsl, :], bp.rearrange("p (q c) -> p q c", c=128), [bpk], [f"r_P{nxt}"])
                        em.cp('dve', Qm[nxt][:, hsl, :], bq.rearrange("p (q c) -> p q c", c=128), [bqk], [f"r_Q{nxt}"])
                    for hh in range(2):
                        by = B[2 + hh]
                        byk = f"B{2 + hh}"
                        hsl = slice(hh * 4, (hh + 1) * 4)
                        for q in range(4):
                            h = hh * 4 + q
                            em.mm(by[:, q * 128:(q + 1) * 128], Pm[nxt][:, h, :], Ym[cur][:, h, :], True, True,
                                  [f"r_P{nxt}", f"r_Y{cur}"], [byk])
                        em.tt('dve', Ym[nxt][:, hsl, :], by.rearrange("p (q c) -> p q c", c=128), Ym[cur][:, hsl, :], ALU.add,
                              [byk, f"r_Y{cur}"], [f"r_Y{nxt}"])
                    cur = nxt
                Dt = Ym[cur]
                dtk = f"r_Y{cur}"
                for st, (Em_, ek) in enumerate([(E1m, 'r_E1'), (E2m, 'r_E2')]):
                    oth = Ym[1 - cur]
                    othk = f"r_Y{1 - cur}"
                    for half in range(2):
                        for h4 in range(4):
                            em.tr(BBp[:, 512 + h4 * 128:512 + (h4 + 1) * 128], Dt[:, half * 4 + h4, :], identb, [dtk, 'CSTB'], ['BB'])
                        em.cp('act', Dm[:, half * 4:(half + 1) * 4, :], BBp[:, 512:1024].rearrange("p (h c) -> p h c", c=128),
                              ['BB'], ['r_D'])
                    for hh in range(2):
                        bz = B[2 + hh]
                        bzk = f"B{2 + hh}"
                        hsl = slice(hh * 4, (hh + 1) * 4)
                        for q in range(4):
                            h = hh * 4 + q
                            em.mm(bz[:, q * 128:(q + 1) * 128], Em_[:, h, :], Dt[:, h, :], True, True, [ek, dtk], [bzk])
                        em.cp('act' if hh == 0 else 'dve', Zm[:, hsl, :], bz.rearrange("p (q c) -> p q c", c=128), [bzk], ['r_Z'])
                    for hh in range(2):
                        by = B[2 + hh]
                        byk = f"B{2 + hh}"
                        hsl = slice(hh * 4, (hh + 1) * 4)
                        for q in range(4):
                            h = hh * 4 + q
                            em.mm(by[:, q * 128:(q + 1) * 128], Dm[:, h, :], Zm[:, h, :], True, True, ['r_D', 'r_Z'], [byk])
                        em.tt('dve', oth[:, hsl, :], by.rearrange("p (q c) -> p q c", c=128), Dt[:, hsl, :], ALU.add,
                              [byk, dtk], [othk])
                    cur = 1 - cur
                    Dt = Ym[cur]
                    dtk = f"r_Y{cur}"
                TT_ = Dt
                ttk = dtk
                for h in range(8):
                    hs_ = slice(h * 64, (h + 1) * 64)
                    em.mm(B[2][:, hs_], AR[:, h, 0:128], Hb[:, hs_], True, False, [kAR, 'r_Hb'], ['B2'])
                    em.mm(B[2][:, hs_], GK[:, h, 0:128], vtokb[:, hs_], False, True, [kGK, kvtokb], ['B2'])
                em.cp('act', Wt, B[2], ['B2'], ['r_W'])
                for h in range(8):
                    hs_ = slice(h * 64, (h + 1) * 64)
                    em.mm(B[3][:, hs_], TT_[:, h, :], Wt[:, hs_], True, True, [ttk, 'r_W'], ['B3'])
                em.cp('dve', Ut, B[3], ['B3'], ['r_U'])
                for h in range(8):
                    hs_ = slice(h * 64, (h + 1) * 64)
                    em.mm(B[2][:, hs_], AR[:, h, 128:256], Hb[:, hs_], True, False, [kAR, 'r_Hb'], ['B2'])
                    em.mm(B[2][:, hs_], GB[:, h, 128:256], Ut[:, hs_], False, False, [kGB, 'r_U'], ['B2'])
                    em.mm(B[2][:, hs_], GK[:, h, 128:256], vtokb[:, hs_], False, True, [kGK, kvtokb], ['B2'])
                for h in range(8):
                    hs_ = slice(h * 64, (h + 1) * 64)
                    em.mm(B[3][0:64, hs_], BKtok[:, h, 0, :], Ut[:, hs_], True, False, [kBKtok, 'r_U'], ['B3'])
                    em.mm(B[3][0:64, hs_], BKtok[:, h, 1, :], vtokb[:, hs_], False, True, [kBKtok, kvtokb], ['B3'])
                em.tt('dve', Hs, Hs, B[3][0:64, :], ALU.add, ['r_H', 'B3'], ['r_H'])
                em.tt('dve', Hs.rearrange("p (h v) -> p h v", v=64), Hs.rearrange("p (h v) -> p h v", v=64),
                      gC.unsqueeze(2).to_broadcast([64, 8, 64]), ALU.mult, ['r_H', kgC], ['r_H'])
                em.cp('act', Hb, Hs, ['r_H', 'B2'], ['r_Hb'])
                if d == 0:
                    if emit_out:
                        em.cp('act', ot[:, 0:512], B[2], ['B2'], ['r_o'])
                        em.cp('dve', ot[:, 512:520], coef, [kcoef], ['r_ocoef'])
                        em.dma('pool', OF[s][t0:t0 + 128, :], ot, reads=['r_o', 'r_ocoef'], writes=[f"of{s}"])
                elif emit_out:
                    em.dma('sp', oft, OF[s][t0:t0 + 128, :], reads=[f"of{s}"], writes=['r_of'])
                    em.tt('dve', ot[:, 0:512], B[2], oft[:, 0:512], ALU.add, ['B2', 'r_of'], ['r_o'])
                    o3 = ot[:, 0:512].rearrange("p (h v) -> p h v", v=64)
                    em.op('dve', lambda e: e.tensor_reduce(out=gn[:, 0:8], in_=o3, axis=AX.X, op=ALU.add), ['r_o'], ['r_gn'])
                    em.act(osq, ot[:, 0:512], AF.Square, ['r_o'], ['r_osq'])
                    em.op('dve', lambda e: e.tensor_reduce(out=gn[:, 8:16], in_=osq.rearrange("p (h v) -> p h v", v=64),
                                                           axis=AX.X, op=ALU.add), ['r_osq', 'r_gn'], ['r_gn'])
                    em.ts('dve', gn[:, 16:24], gn[:, 0:8], 1.0 / 64, None, ALU.mult, None, ['r_gn'], ['r_gn'])
                    em.tt('dve', gn[:, 0:8], gn[:, 16:24], gn[:, 16:24], ALU.mult, ['r_gn'], ['r_gn'])
                    em.stt(gn[:, 24:32], gn[:, 8:16], 1.0 / 64, gn[:, 0:8], ALU.mult, ALU.subtract, ['r_gn'], ['r_gn'])
                    em.act(gn[:, 24:32], gn[:, 24:32], AF.Sqrt, ['r_gn'], ['r_gn'], bias=R_LN_EPS, scale=1.0)
                    em.op('dve', lambda e: e.reciprocal(out=gn[:, 24:32], in_=gn[:, 24:32]), ['r_gn'], ['r_gn'])
                    em.tt('dve', o3, o3, gn[:, 16:24].unsqueeze(2).to_broadcast([128, 8, 64]), ALU.subtract, ['r_o', 'r_gn'], ['r_o'])
                    em.tt('dve', o3, o3, gn[:, 24:32].unsqueeze(2).to_broadcast([128, 8, 64]), ALU.mult, ['r_o', 'r_gn'], ['r_o'])
                    em.tt('dve', ot[:, 0:512], ot[:, 0:512], lnw, ALU.mult, ['r_o', 'ROWB'], ['r_o'])
                    em.tt('dve', ot[:, 0:512], ot[:, 0:512], lnb, ALU.add, ['r_o', 'ROWB'], ['r_o'])
                    em.tt('dve', gn[:, 32:40], coef, oft[:, 512:520], ALU.add, [kcoef, 'r_of', 'r_gn'], ['r_gn'])
                    em.tt('dve', osq.rearrange("p (h v) -> p h v", v=64), vtokb.rearrange("p (h v) -> p h v", v=64),
                          gn[:, 32:40].unsqueeze(2).to_broadcast([128, 8, 64]), ALU.mult, [kvtokb, 'r_gn', 'r_osq'], ['r_osq'])
                    em.tt('dve', ot[:, 0:512], ot[:, 0:512], osq, ALU.add, ['r_o', 'r_osq'], ['r_o'])
                    load_halo(xg0, 'r_xg0', XG[s][0:128, :], f"xg{s}", t0, T, False)
                    load_halo(xg1, 'r_xg1', XG[s][128:160, :], f"xg{s}", t0, T, False)
                    for (xg_, sg_, np_, mucol, kx_, ks_) in [(xg0, sg0, 128, PV_MUXG0, 'r_xg0', 'r_sg0'),
                                                             (xg1, sg1, 32, PV_MUXG1, 'r_xg1', 'r_sg1')]:
                        em.tt('dve', xgt[:np_, :], xg_[:np_, 0:128], xg_[:np_, 2:130], ALU.add, [kx_], ['r_xgt'])
                        em.stt(xgt[:np_, :], xgt[:np_, :], 0.5, xg_[:np_, 1:129], ALU.mult, ALU.subtract, ['r_xgt', kx_], ['r_xgt'])
                        em.stt(xgt[:np_, :], xgt[:np_, :], PV[:np_, mucol:mucol + 1], xg_[:np_, 1:129], ALU.mult, ALU.add,
                               ['r_xgt', kx_, 'PV'], ['r_xgt'])
                        em.act(sg_[:np_, :], xgt[:np_, :], AF.Sigmoid, ['r_xgt'], [ks_])
                    em.mm(B[3], sg0, g2b0, True, False, ['r_sg0', 'g2b'], ['B3'])
                    em.mm(B[3], sg1, g2b1, False, True, ['r_sg1', 'g2b'], ['B3'])
                    em.tt('dve', ot[:, 0:512], ot[:, 0:512], B[3], ALU.mult, ['r_o', 'B3'], ['r_o'])
                    for j in range(4):
                        em.tr(B[2][:, j * 128:(j + 1) * 128], ot[:, j * 128:(j + 1) * 128], ident, ['r_o', 'CST'], ['B2'])
                    em.cp('act', mixo[:, 4:8, :], B[2].rearrange("p (j t) -> p j t", t=128), ['B2'], ['mixo_r'])
                    em.dma('pool', fm(MIX[s])[:, 4:8, t0:t0 + 128], mixo[:, 4:8, :], reads=['mixo_r'], writes=[f"mixr{s}"])

            mc_ = mixcfg or {}
            inter = mc_.get('interleave', True)
            for b in range(mc_.get('nb', NBL)):
                nw = 0
                for stream, fnc in (('m', mamba_chunk), ('r', wkv_chunk)):
                    if not mc_.get('mamba' if stream == 'm' else 'wkv', True):
                        continue
                    for d in range(mc_.get('nd', 2)):
                        first = True
                        if stream == 'm':
                            em.stream = 'm' if inter else None
                            em.memset('pool', hst, 0.0, ['m_hst'])
                            em.memset('pool', hstb, 0.0, ['m_hstb'])
                        for kind in range(mc_.get('nkind', 2)):
                            s = b * 2 + kind
                            T = seqT(s)
                            nch = T // CH
                            order = range(nch) if d == 0 else range(nch - 1, -1, -1)
                            emit = (kind == 1) or need_ctx_out or mc_.get('ctxout', False)
                            for c in order:
                                if stream == 'm':
                                    fnc(s, c * CH, d, emit)
                                else:
                                    fnc(s, c * CH, d, emit, idx=nw, first_of_d=first, split=inter)
                                    nw += 1
                                    first = False
                em.stream = None
                if inter:
                    em.flush_mixer(nw)
            em.barrier()

    def stage_proj_post(l, phase, wap, Kc, SRC, srcname, gidx, seqs):
        with ExitStack() as es:
            def S(name, shape, dt=F32):
                return es.enter_context(nc.sbuf_tensor(name, list(shape), dt)).ap()
            nm = f"pp{phase}"
            wb = load_wbf(es, l, wap, Kc, D, nm + "w")
            a = S(f"{nm}a{l}", [128, Kc, 512], BF16)
            xt = S(f"{nm}x{l}", [128, 8, 512])
            y = S(f"{nm}y{l}", [128, 8, 512])
            sq = S(f"{nm}sq{l}", [128, 8, 512], BF16)
            rstd = S(f"{nm}rs{l}", [128, 512])
            pss = [es.enter_context(nc.psum_tensor(f"{nm}ps{l}_{i}", [128, 512], F32)).ap() for i in range(5)]
            for s in seqs:
                T = seqT(s)
                TW = min(512, T)
                jmod = 2 if s % 2 == 0 else s // 2
                rsrc, rsk = res_src(l, s, phase)
                rdst, rdk = res_dst(l, s, phase)
                for tt_ in range(T // TW):
                    t0 = tt_ * TW
                    em.dma('sp', a[:, :, :TW], fm(SRC[s])[:, :, t0:t0 + TW], reads=([f"mixm{s}", f"mixr{s}"] if srcname == 'mix' else [f"{srcname}{s}"]), writes=[nm + 'a'])
                    em.dma('pool', xt[:, :, :TW], fm(rsrc)[:, :, t0:t0 + TW], reads=[rsk], writes=[nm + 'x'])
                    for m in range(8):
                        ps = pss[m % 4]
                        pk = f"ps{m % 4}"
                        for k in range(Kc):
                            em.mm(ps[:, :TW], wb[:, k, m * 128:(m + 1) * 128], a[:, k, :TW], k == 0, k == Kc - 1,
                                  [nm + 'wbf', nm + 'a'], [pk])
                        em.cp('dve', y[:, m, :TW], ps[:, :TW], [pk], [nm + 'y'])
                        em.act(sq[:, m, :TW], ps[:, :TW], AF.Square, [pk], [nm + 'sq'])
                    for m in range(8):
                        em.mm(pss[4][:, :TW], onesb, sq[:, m, :TW], m == 0, m == 7, ['CSTB', nm + 'sq'], ['ps4'])
                    em.act(rstd[:, :TW], pss[4][:, :TW], AF.Sqrt, ['ps4'], [nm + 'rs'], bias=EPS, scale=1.0 / D)
                    em.op('dve', lambda e: e.reciprocal(out=rstd[:, :TW], in_=rstd[:, :TW]), [nm + 'rs'], [nm + 'rs'])
                    for m in range(8):
                        em.stt(y[:, m, :TW], y[:, m, :TW], DER[:, gidx, m, jmod:jmod + 1], rstd[:, :TW], ALU.mult, ALU.mult,
                               [nm + 'y', nm + 'rs', 'DER'], [nm + 'y'])
                    em.tt('dve', xt[:, :, :TW], xt[:, :, :TW], y[:, :, :TW], ALU.add, [nm + 'x', nm + 'y'], [nm + 'x'])
                    em.dma('pool', fm(rdst)[:, :, t0:t0 + TW], xt[:, :, :TW], reads=[nm + 'x'], writes=[rdk])
            em.barrier()

    def stage_ffn_up(l, seqs):
        with ExitStack() as es:
            def S(name, shape, dt=F32):
                return es.enter_context(nc.sbuf_tensor(name, list(shape), dt)).ap()
            wb = load_wbf(es, l, f_w_up[l], 8, 2 * DFF, "wup")
            xt = S(f"fux{l}", [128, 8, 512])
            h = S(f"fuh{l}", [128, 8, 512], BF16)
            sq = S(f"fusq{l}", [128, 8, 512], BF16)
            rstd = S(f"furs{l}", [128, 512])
            stg = [S(f"fustg{l}_{i}", [128, 512]) for i in range(2)]
            stv = [S(f"fustv{l}_{i}", [128, 512], BF16) for i in range(2)]
            pss = [es.enter_context(nc.psum_tensor(f"fups{l}_{i}", [128, 512], F32)).ap() for i in range(8)]
            for s in seqs:
                T = seqT(s)
                TW = min(512, T)
                jmod = 2 if s % 2 == 0 else s // 2
                src, srck = res_src(l, s, 1)
                for tt_ in range(T // TW):
                    t0 = tt_ * TW
                    em.dma('sp', xt[:, :, :TW], fm(src)[:, :, t0:t0 + TW], reads=[srck], writes=['fux'])
                    prenorm(xt, TW, h, sq, rstd, pss[7], jmod, 2, 24, 'fux', 'fuh', 'ps7')
                    for j in range(NFF):
                        pg = pss[(2 * j) % 6]
                        pgk = f"ps{(2 * j) % 6}"
                        pv_ = pss[(2 * j + 1) % 6]
                        pvk = f"ps{(2 * j + 1) % 6}"
                        for k in range(8):
                            em.mm(pg[:, :TW], wb[:, k, j * 128:(j + 1) * 128], h[:, k, :TW], k == 0, k == 7, ['wupbf', 'fuh'], [pgk])
                        for k in range(8):
                            em.mm(pv_[:, :TW], wb[:, k, DFF + j * 128:DFF + (j + 1) * 128], h[:, k, :TW], k == 0, k == 7,
                                  ['wupbf', 'fuh'], [pvk])
                        sg_ = stg[j % 2]
                        sv_ = stv[j % 2]
                        em.cp('dve', sg_[:, :TW], pg[:, :TW], [pgk], [f"fustg{j % 2}"])
                        em.cp('act', sv_[:, :TW], pv_[:, :TW], [pvk], [f"fustv{j % 2}"])
                        em.dma('pool', GATE[s][j * 128:(j + 1) * 128, t0:t0 + TW], sg_[:, :TW], reads=[f"fustg{j % 2}"],
                               writes=[f"gate{s}"])
                        em.dma('sp', VAL[s][j * 128:(j + 1) * 128, t0:t0 + TW], sv_[:, :TW], reads=[f"fustv{j % 2}"],
                               writes=[f"val{s}"])
            em.barrier()

    def stage_ffn_conv(l, seqs):
        with ExitStack() as es:
            def S(name, shape, dt=F32):
                return es.enter_context(nc.sbuf_tensor(name, list(shape), dt)).ap()
            gflat = [S(f"fcg{l}_{i}", [128, 2048]) for i in range(2)]
            vflat = [S(f"fcv{l}_{i}", [128, 2048], BF16) for i in range(2)]
            gpx = S(f"fcgpx{l}", [128, 34, 66])
            gpc = S(f"fcgpc{l}", [128, 3, 258])
            acc = S(f"fcacc{l}", [128, 2048])
            acc2 = S(f"fcacc2{l}", [128, 2048])
            u = S(f"fcu{l}", [128, 2048])
            ab = [S(f"fcab{l}_{i}", [128, 2048], BF16) for i in range(2)]
            em.memset('pool', gpx, 0.0, ['fcgpx'])
            em.memset('pool', gpc, 0.0, ['fcgpc'])
            it = 0
            for s in seqs:
                T = seqT(s)
                if s % 2 == 1:
                    R, Cc, gp, gpk = 32, 64, gpx, 'fcgpx'
                else:
                    R, Cc, gp, gpk = 1, 256, gpc, 'fcgpc'
                for j in range(NFF):
                    i2 = it % 2
                    it += 1
                    gf = gflat[i2]
                    vf = vflat[i2]
                    em.dma('sp', gf[:, :T], GATE[s][j * 128:(j + 1) * 128, :], reads=[f"gate{s}"], writes=[f"fcg{i2}"])
                    em.dma('sp', vf[:, :T], VAL[s][j * 128:(j + 1) * 128, :], reads=[f"val{s}"], writes=[f"fcv{i2}"])
                    em.cp('pool', gp[:, 1:1 + R, 1:1 + Cc], gf[:, :T].rearrange("p (r c) -> p r c", c=Cc), [f"fcg{i2}"], [gpk])
                    a3 = acc[:, :T].rearrange("p (r c) -> p r c", c=Cc)
                    for tap in range(9):
                        dr, dc = tap // 3 - 1, tap % 3 - 1
                        src_ = gp[:, 1 + dr:1 + dr + R, 1 + dc:1 + dc + Cc]
                        wcol = PV[:, PV_FCW + tap * NFF + j:PV_FCW + tap * NFF + j + 1]
                        if tap == 0:
                            em.ts('dve', a3, src_, wcol, PV[:, PV_FCB + j:PV_FCB + j + 1], ALU.mult, ALU.add,
                                  [gpk, 'PV'], ['fcacc'])
                        else:
                            em.stt(a3, src_, wcol, a3, ALU.mult, ALU.add, [gpk, 'PV', 'fcacc'], ['fcacc'])
                    em.act(u[:, :T], acc[:, :T], AF.Square, ['fcacc'], ['fcu'])
                    em.ts('dve', u[:, :T], u[:, :T], 0.044715, 1.0, ALU.mult, ALU.add, ['fcu'], ['fcu'])
                    em.tt('dve', u[:, :T], u[:, :T], acc[:, :T], ALU.mult, ['fcu', 'fcacc'], ['fcu'])
                    em.act(u[:, :T], u[:, :T], AF.Sigmoid, ['fcu'], ['fcu'], scale=GELU_C)
                    em.tt('dve', u[:, :T], u[:, :T], acc[:, :T], ALU.mult, ['fcu', 'fcacc'], ['fcu'])
                    em.tt('dve', ab[i2][:, :T], u[:, :T], vf[:, :T], ALU.mult, ['fcu', f"fcv{i2}"], [f"fcab{i2}"])
                    em.dma('pool', ACTV[s][j * 128:(j + 1) * 128, :], ab[i2][:, :T], reads=[f"fcab{i2}"], writes=[f"actv{s}"])
            em.barrier()

    allseq = list(range(NS))
    xseq = [s for s in range(NS) if s % 2 == 1]
    outkeys = []
    for l in range(n_layers):
        last = (l == n_layers - 1)
        stage_mod(l)
        stage_inproj(l)
        if stop_after == 'inproj':
            break
        stage_mixer(l, need_ctx_out=not last)
        if stop_after == 'mixer':
            break
        seqs = xseq if last else allseq
        stage_proj_post(l, 0, w_out[l], 8, MIX, "mix", 1, seqs)
        if stop_after == 'outproj':
            break
        stage_ffn_up(l, seqs)
        stage_ffn_conv(l, seqs)
        stage_proj_post(l, 1, f_w_down[l], NFF, ACTV, "actv", 3, seqs)
    em.barrier()
    return nc, em


def host_prep(inp):
    f = np.float32
    idx = np.arange(128)
    cstn = np.zeros((128, NCST), f)
    cstn[:, C_ID:C_ID + 128] = np.eye(128)
    cstn[:, C_UTI:C_UTI + 128] = (idx[:, None] <= idx[None, :])
    cstn[:, C_LTI:C_LTI + 128] = (idx[:, None] >= idx[None, :])
    cstn[:, C_UTS:C_UTS + 128] = (idx[:, None] < idx[None, :])
    cstn[:, C_LTS:C_LTS + 128] = (idx[:, None] > idx[None, :])
    cstn[:, C_ONE:C_ONE + 128] = 1.0
    b32 = idx // 32
    b64 = idx // 64
    same32 = b32[:, None] == b32[None, :]
    same64 = b64[:, None] == b64[None, :]
    for d in range(2):
        strict = (idx[:, None] > idx[None, :]) if d == 0 else (idx[:, None] < idx[None, :])
        cstn[:, C_MP0 + d * 128:C_MP0 + (d + 1) * 128] = strict & same32
        cstn[:, C_ME1 + d * 128:C_ME1 + (d + 1) * 128] = strict & same64 & (~same32)
        cstn[:, C_ME2 + d * 128:C_ME2 + (d + 1) * 128] = strict & (~same64)
    pvn = np.zeros((L, 128, NPV), f)
    pv6 = np.zeros((L, 64, NPV64), f)
    rwn = np.zeros((L, 1, NROW), f)
    for l in range(L):
        pvn[l, :, PV_BMOD:PV_BMOD + 48] = inp['b_mod'][l].reshape(48, 128).T
        pvn[l, :, PV_GPRE1:PV_GPRE1 + 8] = inp['g_mix_pre'][l].reshape(8, 128).T
        pvn[l, :, PV_GPOST1:PV_GPOST1 + 8] = inp['g_mix_post'][l].reshape(8, 128).T
        pvn[l, :, PV_GPRE2:PV_GPRE2 + 8] = inp['g_ffn_pre'][l].reshape(8, 128).T
        pvn[l, :, PV_GPOST2:PV_GPOST2 + 8] = inp['g_ffn_post'][l].reshape(8, 128).T
        pvn[l, :, PV_MCW:PV_MCW + 24] = inp['m_conv_w'][l].reshape(3, 8, 128).transpose(2, 0, 1).reshape(128, 24)
        pvn[l, :, PV_MCB:PV_MCB + 8] = inp['m_conv_b'][l].reshape(8, 128).T
        pvn[l, :, PV_FCW:PV_FCW + 198] = inp['f_conv_w'][l].reshape(9, NFF, 128).transpose(2, 0, 1).reshape(128, 198)
        pvn[l, :, PV_FCB:PV_FCB + NFF] = inp['f_conv_b'][l].reshape(NFF, 128).T
        mu = inp['r_mu'][l]
        pvn[l, :, PV_MUXG0] = mu[1792:1920]
        pvn[l, 0:32, PV_MUXG1] = mu[1920:1952]
        pv6[l, :, P6_MURKV:P6_MURKV + 24] = mu[0:1536].reshape(24, 64).T
        pv6[l, :, P6_MUWA:P6_MUWA + 4] = mu[1536:1792].reshape(4, 64).T
        pv6[l, :, P6_W0:P6_W0 + 16] = inp['r_w0'][l].reshape(2, 8, 64).transpose(2, 0, 1).reshape(64, 16)
        pv6[l, :, P6_A0:P6_A0 + 16] = inp['r_a0'][l].reshape(2, 8, 64).transpose(2, 0, 1).reshape(64, 16)
        pv6[l, :, P6_KK:P6_KK + 8] = inp['r_k_k'][l].reshape(8, 64).T
        pv6[l, :, P6_KA:P6_KA + 8] = inp['r_k_a'][l].reshape(8, 64).T
        pv6[l, :, P6_RK:P6_RK + 8] = inp['r_r_k'][l].T
        rwn[l, 0, RV_MNW:RV_MNW + 512] = inp['m_norm_w'][l]
        rwn[l, 0, RV_LNW:RV_LNW + 512] = inp['r_ln_w'][l]
        rwn[l, 0, RV_LNB:RV_LNB + 512] = inp['r_ln_b'][l]
        rwn[l, 0, RV_MD:RV_MD + 8] = inp['m_d'][l]
        rwn[l, 0, RV_DTB:RV_DTB + 16] = inp['m_dt_bias'][l].reshape(16)
        rwn[l, 0, RV_ALOG:RV_ALOG + 16] = inp['m_a_log'][l].reshape(16)
    return cstn, pvn, pv6, rwn


def make_in_maps(inp, cores):
    cstn, pvn, pv6, rwn = host_prep(inp)
    shared = {k: np.ascontiguousarray(np.asarray(inp[k], dtype=np.float32)) for k in
              ['w_mod', 'w_in', 'w_out', 'r_w2', 'r_a2', 'r_g2', 'f_w_up', 'f_w_down']}
    maps = []
    x = np.asarray(inp['x'], np.float32)
    ctx = np.asarray(inp['ctx'], np.float32)
    c = np.asarray(inp['c'], np.float32)
    cc = np.asarray(inp['c_ctx'], np.float32)
    for ci in cores:
        bs = [ci * NBL + i for i in range(NBL)]
        m = dict(shared)
        m['xT'] = np.ascontiguousarray(x[bs].transpose(0, 2, 1))
        m['ctxT'] = np.ascontiguousarray(ctx[bs].transpose(0, 2, 1))
        m['cT'] = np.ascontiguousarray(np.stack([c[bs[0]], c[bs[1]], cc], axis=1))
        m['cst'] = cstn
        m['pv'] = pvn
        m['pv64'] = pv6
        m['rowv'] = rwn
        maps.append(m)
    return maps


def kernel(**inputs):
    nc, em = build()
    cores = list(range(NCORE))
    maps = make_in_maps(inputs, cores)
    res = run_bass_kernel_spmd(nc, maps, core_ids=cores)
    out = np.empty((NCORE * NBL, TX, D), np.float32)
    for ci in cores:
        o = res.results[ci]["outT"]
        out[ci * NBL:(ci + 1) * NBL] = o.transpose(0, 2, 1)
    return out
```

```python
import numpy as np
from contextlib import ExitStack
import concourse.bass as bass
import concourse.mybir as mybir
from concourse.bass_utils import run_bass_kernel_spmd

F32 = mybir.dt.float32
BF16 = mybir.dt.bfloat16
AF = mybir.ActivationFunctionType
ALU = mybir.AluOpType
AX = mybir.AxisListType

L = 2
D = 1024
TX = 2048
TC = 256
NBL = 2
NCORE = 8
CH = 128
DFF = 2816
NFF = 22
EPS = 1e-6
R_LN_EPS = 64e-5
R_DECAY_SCALE = 0.6065306597126334
GELU_C = 1.5957691216057308

PV_BMOD = 0
PV_GPRE1 = 48
PV_GPOST1 = 56
PV_GPRE2 = 64
PV_GPOST2 = 72
PV_MCW = 80
PV_MCB = 104
PV_FCW = 112
PV_FCB = 310
PV_MUXG0 = 332
PV_MUXG1 = 333
NPV = 334
P6_MURKV = 0
P6_MUWA = 24
P6_W0 = 28
P6_A0 = 44
P6_KK = 60
P6_KA = 68
P6_RK = 76
NPV64 = 84
RV_MNW = 0
RV_LNW = 512
RV_LNB = 1024
RV_MD = 1536
RV_DTB = 1544
RV_ALOG = 1560
NROW = 1576
C_ID = 0
C_UTI = 128
C_LTI = 256
C_UTS = 384
C_LTS = 512
C_ONE = 640
C_MP0 = 768
C_ME1 = 1024
C_ME2 = 1280
NCST = 1536


class Em:
    def __init__(self, nc, ndma=8):
        self.nc = nc
        self.engs = {'pe': nc.tensor, 'act': nc.scalar, 'dve': nc.vector, 'pool': nc.gpsimd, 'sp': nc.sync}
        self.sem = {}
        self.cnt = {}
        for k in ['pe', 'act', 'dve', 'pool']:
            self.sem[k] = nc.alloc_semaphore("sem_" + k)
            self.cnt[k] = 0
        self.dq = {}
        for q in ['sp', 'pool', 'act']:
            self.dq[q] = {'n': ndma, 'next': 0}
            for i in range(ndma):
                self.sem[f"d_{q}_{i}"] = nc.alloc_semaphore(f"dsem_{q}_{i}")
                self.cnt[f"d_{q}_{i}"] = 0
        self.seen = {k: {} for k in self.engs}
        self.lastw = {}
        self.readers = {}
        self.n = 0
        self.pend = {k: False for k in self.engs}
        self.stream = None
        self.queues = {}

    def _deps(self, reads, writes):
        deps = {}

        def add(d):
            if d is None:
                return
            k, v = d
            if deps.get(k, 0) < v:
                deps[k] = v
        for b in reads:
            add(self.lastw.get(b))
        for b in writes:
            add(self.lastw.get(b))
            for r in self.readers.get(b, ()):
                add(r)
        return deps

    def _waits(self, eng, deps):
        for k, v in deps.items():
            if k == 'pe' and eng == 'pe':
                continue
            if k.startswith('d_'):
                v = self.cnt[k]
            if self.seen[eng].get(k, 0) >= v:
                continue
            self.seen[eng][k] = v
            self.engs[eng].wait_ge(self.sem[k], v)
            self.n += 1

    def _mark(self, me, reads, writes):
        for b in reads:
            self.readers.setdefault(b, []).append(me)
        for b in writes:
            self.lastw[b] = me
            self.readers[b] = []

    @staticmethod
    def _is_psum(k):
        return (k[0] == 'B' and (k[1:].isdigit() or k in ('BB', 'BBa', 'BBb'))) or k.startswith('ps')

    def flush(self):
        qs = {k: v for k, v in self.queues.items() if v}
        self.queues = {}
        pos = {k: 0 for k in qs}
        while qs:
            k = min(qs, key=lambda n: pos[n] / len(qs[n]))
            it = qs[k][pos[k]]
            pos[k] += 1
            if it[0] == 'op':
                self.op(*it[1:])
            else:
                self.dma(it[1], it[2], it[3], it[4], it[5], **it[6])
            if pos[k] >= len(qs[k]):
                del qs[k]

    @staticmethod
    def _merge(lists):
        lists = [l for l in lists if l]
        out = []
        pos = [0] * len(lists)
        live = list(range(len(lists)))
        sticky = None
        while live:
            k = sticky if sticky is not None else min(live, key=lambda n: pos[n] / len(lists[n]))
            it = lists[k][pos[k]]
            out.append(it)
            pos[k] += 1
            if it[0] == 'op' and it[1] == 'pe':
                sticky = None if it[5] else k
            if pos[k] >= len(lists[k]):
                live.remove(k)
                sticky = None
        return out

    def flush_mixer(self, nw):
        qs = self.queues
        self.queues = {}
        wk = list(qs.get(('rp', 0), []))
        for i in range(nw):
            wk += self._merge([qs.get(('rs', i), []), qs.get(('rp', i + 1), [])])
        allops = self._merge([wk, qs.get('m', [])])
        for it in allops:
            if it[0] == 'op':
                self.op(*it[1:])
            else:
                self.dma(it[1], it[2], it[3], it[4], it[5], **it[6])

    def op(self, eng, fn, reads=(), writes=(), inc=True):
        if self.stream is not None:
            self.queues.setdefault(self.stream, []).append(('op', eng, fn, tuple(reads), tuple(writes), inc))
            return
        ex = [k for k in reads if self._is_psum(k)]
        self._waits(eng, self._deps(reads, list(writes) + ex))
        if inc:
            self.cnt[eng] += 1
            fn(self.engs[eng]).then_inc(self.sem[eng], 1)
            self._mark((eng, self.cnt[eng]), reads, writes)
            self.pend[eng] = False
        else:
            fn(self.engs[eng])
            self._mark((eng, self.cnt[eng] + 1), reads, writes)
            self.pend[eng] = True
        self.n += 1

    def dma(self, q, out, in_, reads=(), writes=(), **kw):
        if self.stream is not None:
            self.queues.setdefault(self.stream, []).append(('dma', q, out, in_, tuple(reads), tuple(writes), kw))
            return
        self._waits(q, self._deps(reads, writes))
        d = self.dq[q]
        i = d['next']
        d['next'] = (i + 1) % d['n']
        k = f"d_{q}_{i}"
        self.cnt[k] += 16
        self.engs[q].dma_start(out=out, in_=in_, **kw).then_inc(self.sem[k], 16)
        self._mark((k, self.cnt[k]), reads, writes)
        self.n += 1

    def barrier(self):
        assert not any(self.pend.values()), self.pend
        allv = {k: v for k, v in self.cnt.items() if v > 0}
        for e in self.engs:
            self._waits(e, dict(allv))

    def act(self, out, in_, func, r, w, bias=None, scale=None, accum=None):
        kw = {}
        if bias is not None:
            kw['bias'] = bias
        if scale is not None:
            kw['scale'] = scale
        if accum is not None:
            kw['accum_out'] = accum
        self.op('act', lambda e: e.activation(out=out, in_=in_, func=func, **kw), r, w)

    def tt(self, eng, out, a, b, op, r, w):
        self.op(eng, lambda e: e.tensor_tensor(out=out, in0=a, in1=b, op=op), r, w)

    def ts(self, eng, out, a, s1, s2, op0, op1, r, w):
        if s2 is None:
            self.op(eng, lambda e: e.tensor_scalar(out=out, in0=a, scalar1=s1, scalar2=None, op0=op0), r, w)
        else:
            self.op(eng, lambda e: e.tensor_scalar(out=out, in0=a, scalar1=s1, scalar2=s2, op0=op0, op1=op1), r, w)

    def stt(self, out, a, s, b, op0, op1, r, w):
        self.op('dve', lambda e: e.scalar_tensor_tensor(out=out, in0=a, scalar=s, in1=b, op0=op0, op1=op1), r, w)

    def mm(self, out, lhsT, rhs, start, stop, r, w, inc=None):
        self.op('pe', lambda e: e.matmul(out, lhsT=lhsT, rhs=rhs, start=start, stop=stop), r, w,
                inc=(stop if inc is None else inc))

    def tr(self, out, in_, ident, r, w, inc=True):
        self.op('pe', lambda e: e.transpose(out, in_, ident), r, w, inc=inc)

    def cp(self, eng, out, in_, r, w):
        if eng == 'act':
            self.op('act', lambda e: e.activation(out=out, in_=in_, func=AF.Identity), r, w)
        else:
            self.op(eng, lambda e: e.tensor_copy(out=out, in_=in_), r, w)

    def memset(self, eng, ap, val, w):
        self.op(eng, lambda e: e.memset(ap, val), (), w)


def seqT(s):
    return TX if (s % 2) == 1 else TC


def build(debug=False, n_layers=L, stop_after=None, mixcfg=None):
    nc = bass.Bass("TRN2", target_bir_lowering=False)
    em = Em(nc)
    dbgset = debug if isinstance(debug, (set, list, tuple)) else None

    def din(name, shape, dt=F32):
        return nc.dram_tensor(name, list(shape), dt, kind="ExternalInput").ap()

    def dscr(name, shape, dt=F32):
        isdbg = (debug is True) or (dbgset is not None and name.rstrip('0123456789') in dbgset)
        return nc.dram_tensor(name, list(shape), dt, kind="ExternalOutput" if isdbg else "Internal").ap()

    xT = din("xT", [NBL, D, TX])
    ctxT = din("ctxT", [NBL, D, TC])
    cT = din("cT", [D, 3])
    w_mod = din("w_mod", [L, D, 6 * D])
    w_in = din("w_in", [L, D, 3504])
    w_out = din("w_out", [L, D, D])
    r_w2 = din("r_w2", [L, 2, 64, 512])
    r_a2 = din("r_a2", [L, 2, 64, 512])
    r_g2 = din("r_g2", [L, 160, 512])
    f_w_up = din("f_w_up", [L, D, 2 * DFF])
    f_w_down = din("f_w_down", [L, DFF, D])
    cst = din("cst", [128, NCST])
    pv = din("pv", [L, 128, NPV])
    pv64 = din("pv64", [L, 64, NPV64])
    rowv = din("rowv", [L, 1, NROW])
    outT = nc.dram_tensor("outT", [NBL, D, TX], F32, kind="ExternalOutput").ap()

    NS = 2 * NBL
    RESA = [dscr(f"resa{s}", [D, seqT(s)]) for s in range(NS)]
    RESB = [dscr(f"resb{s}", [D, seqT(s)]) for s in range(NS)]
    XBC = [dscr(f"xbc{s}", [D, seqT(s)]) for s in range(NS)]
    RKV = [dscr(f"rkv{s}", [24, 64, seqT(s)]) for s in range(NS)]
    XWA = [dscr(f"xwa{s}", [4, 64, seqT(s)]) for s in range(NS)]
    XG = [dscr(f"xg{s}", [160, seqT(s)]) for s in range(NS)]
    ZDT = [dscr(f"zdt{s}", [seqT(s), 528]) for s in range(NS)]
    YF = [dscr(f"yf{s}", [seqT(s), 512]) for s in range(NS)]
    OF = [dscr(f"of{s}", [seqT(s), 520]) for s in range(NS)]
    MIX = [dscr(f"mix{s}", [D, seqT(s)], BF16) for s in range(NS)]
    GATE = [dscr(f"gate{s}", [DFF, seqT(s)]) for s in range(NS)]
    VAL = [dscr(f"val{s}", [DFF, seqT(s)], BF16) for s in range(NS)]
    ACTV = [dscr(f"actv{s}", [DFF, seqT(s)], BF16) for s in range(NS)]

    def fm(ap):
        return ap.rearrange("(k p) t -> p k t", p=128)

    def sb(name, shape, dt=F32):
        return nc.alloc_sbuf_tensor(name, list(shape), dt).ap()

    CST = sb("CST", [128, NCST])
    CSTB = sb("CSTB", [128, 768], BF16)
    PV = sb("PV", [128, NPV])
    PV64 = sb("PV64", [64, NPV64])
    ROWB = sb("ROWB", [128, NROW])
    MOD = sb("MOD", [128, 48, 3])
    DER = sb("DER", [128, 4, 8, 3])
    NEGA = sb("NEGA", [128, 16])
    OMMU = sb("OMMU", [64, 8])
    em.dma('sp', CST, cst, writes=['CST'])
    em.cp('dve', CSTB, CST[:, 0:768], ['CST'], ['CSTB'])
    ident = CST[:, C_ID:C_ID + 128]
    identb = CSTB[:, C_ID:C_ID + 128]
    onesb = CSTB[:, C_ONE:C_ONE + 128]
    ones = CST[:, C_ONE:C_ONE + 128]

    def stage_scope():
        return ExitStack()

    def stage_mod(l):
        em.dma('sp', PV, pv[l], writes=['PV'])
        em.dma('sp', PV64, pv64[l], writes=['PV64'])
        em.dma('pool', ROWB, rowv[l].partition_broadcast(128), writes=['ROWB'])
        with ExitStack() as es:
            def S(name, shape, dt=F32):
                return es.enter_context(nc.sbuf_tensor(name, list(shape), dt)).ap()
            cts = S(f"cts{l}", [128, 8, 3])
            sc = S(f"sc{l}", [128, 8, 3])
            wst = [S(f"wmst{l}_{i}", [128, 8, 512]) for i in range(2)]
            ps = es.enter_context(nc.psum_tensor(f"psmod{l}", [128, 512], F32)).ap()
            em.dma('sp', cts, cT.rearrange("(k p) j -> p k j", p=128), writes=['cts'])
            em.act(sc, cts, AF.Silu, ['cts'], ['sc'])
            for g in range(12):
                w = wst[g % 2]
                wk = f"wmst{g % 2}"
                em.dma('sp' if g % 2 == 0 else 'pool', w,
                       w_mod[l][:, g * 512:(g + 1) * 512].rearrange("(k p) n -> p k n", p=128), writes=[wk])
                for mi in range(4):
                    m = g * 4 + mi
                    for k in range(8):
                        em.mm(ps[:, m * 3:(m + 1) * 3], w[:, k, mi * 128:(mi + 1) * 128], sc[:, k, :],
                              k == 0, k == 7, [wk, 'sc'], ['psmod'])
            em.tt('dve', MOD, ps[:, 0:144].rearrange("p (m j) -> p m j", j=3),
                  PV[:, PV_BMOD:PV_BMOD + 48].unsqueeze(2).to_broadcast([128, 48, 3]), ALU.add,
                  ['psmod', 'PV'], ['MOD'])
            tmp = S(f"dertmp{l}", [128, 8, 3])

            def gain(idx, goff, mlo, plus1):
                if plus1:
                    em.ts('dve', tmp, MOD[:, mlo:mlo + 8, :], 1.0, None, ALU.add, None, ['MOD'], ['dertmp'])
                    src = tmp
                    rk = ['dertmp', 'PV']
                else:
                    src = MOD[:, mlo:mlo + 8, :]
                    rk = ['MOD', 'PV']
                em.tt('dve', DER[:, idx, :, :], src,
                      PV[:, goff:goff + 8].unsqueeze(2).to_broadcast([128, 8, 3]), ALU.mult, rk, ['DER'])
            gain(0, PV_GPRE1, 8, True)
            gain(1, PV_GPOST1, 16, False)
            gain(2, PV_GPRE2, 32, True)
            gain(3, PV_GPOST2, 40, False)
            em.act(NEGA, ROWB[:, RV_ALOG:RV_ALOG + 16], AF.Exp, ['ROWB'], ['NEGA'])
            em.ts('dve', NEGA, NEGA, -1.0, None, ALU.mult, None, ['NEGA'], ['NEGA'])
            em.ts('dve', OMMU, PV64[:, P6_KA:P6_KA + 8], -1.0, 1.0, ALU.mult, ALU.add, ['PV64'], ['OMMU'])
            em.barrier()

    def prenorm(xt, TW, h, sq, rstd, ps, jmod, gidx, sidx, kx, kh, kps):
        em.act(sq[:, :, :TW], xt[:, :, :TW], AF.Square, [kx], ['sq'])
        for k in range(8):
            em.mm(ps[:, :TW], onesb, sq[:, k, :TW], k == 0, k == 7, ['sq', 'CSTB'], [kps])
        em.act(rstd[:, :TW], ps[:, :TW], AF.Sqrt, [kps], ['rstd'], bias=EPS, scale=1.0 / D)
        em.op('dve', lambda e: e.reciprocal(out=rstd[:, :TW], in_=rstd[:, :TW]), ['rstd'], ['rstd'])
        for k in range(8):
            em.stt(xt[:, k, :TW], xt[:, k, :TW], DER[:, gidx, k, jmod:jmod + 1], rstd[:, :TW], ALU.mult, ALU.mult,
                   [kx, 'rstd', 'DER'], [kx])
            em.act(h[:, k, :TW], xt[:, k, :TW], AF.Identity, [kx, 'MOD'], [kh],
                   bias=MOD[:, sidx + k, jmod:jmod + 1], scale=1.0)

    def load_wbf(es, l, wap, Kc, N, name, piece=None):
        wb = es.enter_context(nc.sbuf_tensor(f"{name}bf{l}", [128, Kc, N], BF16)).ap()
        with ExitStack() as e2:
            sts = [e2.enter_context(nc.sbuf_tensor(f"{name}st{l}_{i}", [128, N], F32)).ap() for i in range(2)]
            for k in range(Kc):
                st = sts[k % 2]
                sk = f"{name}st{k % 2}"
                em.dma('sp' if k % 2 == 0 else 'pool', st, wap[k * 128:(k + 1) * 128, :], writes=[sk])
                em.cp('act' if k % 2 == 0 else 'dve', wb[:, k, :], st, [sk], [name + 'bf'])
            em.barrier()
        return wb

    def res_src(l, s, phase):
        b = s // 2
        if phase == 0:
            if l == 0:
                return (xT[b] if s % 2 == 1 else ctxT[b]), f"in{s}"
            return RESB[s], f"resb{s}"
        return RESA[s], f"resa{s}"

    def res_dst(l, s, phase):
        b = s // 2
        if phase == 0:
            return RESA[s], f"resa{s}"
        if l == n_layers - 1 and s % 2 == 1:
            return outT[b], f"out{s}"
        return RESB[s], f"resb{s}"

    def stage_inproj(l):
        with ExitStack() as es:
            def S(name, shape, dt=F32):
                return es.enter_context(nc.sbuf_tensor(name, list(shape), dt)).ap()
            wb = load_wbf(es, l, w_in[l], 8, 3504, "win")
            xt = S(f"ipx{l}", [128, 8, 512])
            h = S(f"iph{l}", [128, 8, 512], BF16)
            sq = S(f"ipsq{l}", [128, 8, 512], BF16)
            rstd = S(f"iprs{l}", [128, 512])
            sta = [S(f"ipsta{l}_{i}", [128, 8, 512]) for i in range(2)]
            stw = S(f"ipstw{l}", [64, 4, 512])
            stg0 = S(f"ipstg0{l}", [128, 512])
            stg1 = S(f"ipstg1{l}", [32, 512])
            stz = S(f"ipstz{l}", [128, 4, 528])
            pss = [es.enter_context(nc.psum_tensor(f"ipps{l}_{i}", [128, 512], F32)).ap() for i in range(8)]
            groups = [('xbc', 512, 128, 8), ('r', 1552, 64, 8), ('k', 2064, 64, 8), ('v', 2576, 64, 8)]
            ev = 0
            for s in range(NS):
                T = seqT(s)
                TW = min(512, T)
                jmod = 2 if s % 2 == 0 else s // 2
                src, srck = res_src(l, s, 0)
                for tt_ in range(T // TW):
                    t0 = tt_ * TW
                    em.dma('sp', xt[:, :, :TW], fm(src)[:, :, t0:t0 + TW], reads=[srck], writes=['ipx'])
                    prenorm(xt, TW, h, sq, rstd, pss[7], jmod, 0, 0, 'ipx', 'iph', 'ps7')
                    pi = 0
                    for gi, (gname, c0, wdt, nb) in enumerate(groups):
                        st = sta[gi % 2]
                        stk = f"ipsta{gi % 2}"
                        for j in range(nb):
                            ps = pss[pi % 6]
                            pk = f"ps{pi % 6}"
                            pi += 1
                            cc = c0 + j * wdt
                            for k in range(8):
                                em.mm(ps[:wdt, :TW], wb[:, k, cc:cc + wdt], h[:, k, :TW], k == 0, k == 7,
                                      ['winbf', 'iph'], [pk])
                            em.cp('act' if ev % 2 == 0 else 'dve', st[:wdt, j, :TW], ps[:wdt, :TW], [pk], [stk])
                            ev += 1
                        if gname == 'xbc':
                            em.dma('pool', fm(XBC[s])[:, :, t0:t0 + TW], st[:, :, :TW], reads=[stk], writes=[f"xbc{s}"])
                        else:
                            jb = {'r': 0, 'k': 8, 'v': 16}[gname]
                            em.dma('pool', RKV[s][jb:jb + 8].rearrange("j p t -> p j t")[:, :, t0:t0 + TW],
                                   st[:64, :, :TW], reads=[stk], writes=[f"rkv{s}"])
                    for j in range(4):
                        ps = pss[pi % 6]
                        pk = f"ps{pi % 6}"
                        pi += 1
                        cc = 3088 + j * 64
                        for k in range(8):
                            em.mm(ps[:64, :TW], wb[:, k, cc:cc + 64], h[:, k, :TW], k == 0, k == 7, ['winbf', 'iph'], [pk])
                        em.cp('act' if ev % 2 == 0 else 'dve', stw[:, j, :TW], ps[:64, :TW], [pk], ['ipstw'])
                        ev += 1
                    em.dma('pool', XWA[s].rearrange("j p t -> p j t")[:, :, t0:t0 + TW], stw[:, :, :TW],
                           reads=['ipstw'], writes=[f"xwa{s}"])
                    for (cc, wdt, st, stk, r0) in [(3344, 128, stg0, 'ipstg0', 0), (3472, 32, stg1, 'ipstg1', 128)]:
                        ps = pss[pi % 6]
                        pk = f"ps{pi % 6}"
                        pi += 1
                        for k in range(8):
                            em.mm(ps[:wdt, :TW], wb[:, k, cc:cc + wdt], h[:, k, :TW], k == 0, k == 7, ['winbf', 'iph'], [pk])
                        em.cp('act' if ev % 2 == 0 else 'dve', st[:wdt, :TW], ps[:wdt, :TW], [pk], [stk])
                        ev += 1
                        em.dma('pool', XG[s][r0:r0 + wdt, t0:t0 + TW], st[:wdt, :TW], reads=[stk], writes=[f"xg{s}"])
                    for i in range(TW // 128):
                        ps = pss[pi % 6]
                        pk = f"ps{pi % 6}"
                        pi += 1
                        ps2 = pss[6]
                        for k in range(8):
                            em.mm(ps[:, 0:512], h[:, k, i * 128:(i + 1) * 128], wb[:, k, 0:512], k == 0, k == 7,
                                  ['winbf', 'iph'], [pk])
                        for k in range(8):
                            em.mm(ps2[:, 0:16], h[:, k, i * 128:(i + 1) * 128], wb[:, k, 1536:1552], k == 0, k == 7,
                                  ['winbf', 'iph'], ['ps6'])
                        em.cp('act', stz[:, i, 0:512], ps[:, 0:512], [pk], ['ipstz'])
                        em.cp('dve', stz[:, i, 512:528], ps2[:, 0:16], ['ps6'], ['ipstz'])
                    em.dma('pool', ZDT[s][t0:t0 + TW, :].rearrange("(i p) c -> p i c", p=128), stz[:, :TW // 128, :],
                           reads=['ipstz'], writes=[f"zdt{s}"])
            em.barrier()

    def stage_mixer(l, need_ctx_out):
        with ExitStack() as es:
            def S(name, shape, dt=F32):
                return es.enter_context(nc.sbuf_tensor(name, list(shape), dt)).ap()
            w2b = S(f"w2b{l}", [64, 2, 512], BF16)
            a2b = S(f"a2b{l}", [64, 2, 512], BF16)
            g2b0 = S(f"g2b0{l}", [128, 512], BF16)
            g2b1 = S(f"g2b1{l}", [32, 512], BF16)
            with ExitStack() as e2:
                t1 = e2.enter_context(nc.sbuf_tensor(f"lst1{l}", [64, 2, 512], F32)).ap()
                t2 = e2.enter_context(nc.sbuf_tensor(f"lst2{l}", [64, 2, 512], F32)).ap()
                t3 = e2.enter_context(nc.sbuf_tensor(f"lst3{l}", [128, 512], F32)).ap()
                t4 = e2.enter_context(nc.sbuf_tensor(f"lst4{l}", [32, 512], F32)).ap()
                em.dma('sp', t1, r_w2[l].rearrange("d r c -> r d c"), writes=['lst1'])
                em.dma('sp', t2, r_a2[l].rearrange("d r c -> r d c"), writes=['lst2'])
                em.dma('sp', t3, r_g2[l][0:128, :], writes=['lst3'])
                em.dma('sp', t4, r_g2[l][128:160, :], writes=['lst4'])
                em.cp('dve', w2b, t1, ['lst1'], ['w2b'])
                em.cp('dve', a2b, t2, ['lst2'], ['a2b'])
                em.cp('dve', g2b0, t3, ['lst3'], ['g2b'])
                em.cp('dve', g2b1, t4, ['lst4'], ['g2b'])
                em.barrier()
            MAR = [None, None]
            MARt = S(f"mar{l}", [128, 2, 256])
            em.cp('dve', MARt[:, 0, 0:128], CST[:, C_UTS:C_UTS + 128], ['CST'], ['MAR'])
            em.cp('dve', MARt[:, 0, 128:256], CST[:, C_UTI:C_UTI + 128], ['CST'], ['MAR'])
            em.cp('dve', MARt[:, 1, 0:128], CST[:, C_LTS:C_LTS + 128], ['CST'], ['MAR'])
            em.cp('dve', MARt[:, 1, 128:256], CST[:, C_LTI:C_LTI + 128], ['CST'], ['MAR'])
            MSO = S(f"mso{l}", [128, 2, 256])
            em.cp('dve', MSO[:, 0, 0:128], CST[:, C_LTS:C_LTS + 128], ['CST'], ['MSO'])
            em.cp('dve', MSO[:, 1, 0:128], CST[:, C_UTS:C_UTS + 128], ['CST'], ['MSO'])
            em.cp('dve', MSO[:, 0, 128:256], ones, ['CST'], ['MSO'])
            em.cp('dve', MSO[:, 1, 128:256], ones, ['CST'], ['MSO'])
            RMK = S(f"rmk{l}", [64, 8, 128], BF16)
            em.memset('pool', RMK, 1.0, ['RMK'])
            em.memset('pool', RMK[:, :, 0:1], 0.0, ['RMK'])

            def MI(d):
                return CST[:, C_UTI:C_UTI + 128] if d == 0 else CST[:, C_LTI:C_LTI + 128]

            def strictTS(d):
                return CST[:, C_LTS:C_LTS + 128] if d == 0 else CST[:, C_UTS:C_UTS + 128]

            xbc = S(f"m_xbc{l}", [128, 8, 130])
            cacc = S(f"m_cacc{l}", [128, 8, 128])
            bct = S(f"m_bct{l}", [128, 4, 128], BF16)
            xtok = S(f"m_xtok{l}", [128, 512])
            xtokb = S(f"m_xtokb{l}", [128, 512], BF16)
            btokb = S(f"m_btokb{l}", [128, 256], BF16)
            zdt = S(f"m_zdt{l}", [128, 528])
            dts = S(f"m_dts{l}", [128, 8])
            dta = S(f"m_dta{l}", [128, 8])
            sm = S(f"m_sm{l}", [128, 40])
            xw = S(f"m_xw{l}", [128, 512], BF16)
            l2 = [S(f"m_l2{l}_{i}", [128, 256]) for i in range(2)]
            Et = [S(f"m_E{l}_{i}", [128, 256]) for i in range(2)]
            LTt = [S(f"m_LT{l}_{i}", [128, 128]) for i in range(2)]
            STt = [S(f"m_ST{l}_{i}", [128, 128], BF16) for i in range(2)]
            CsT = [S(f"m_Cs{l}_{i}", [128, 128], BF16) for i in range(2)]
            hst = S(f"m_hst{l}", [128, 512])
            hstb = S(f"m_hstb{l}", [128, 512], BF16)
            yt = S(f"m_y{l}", [128, 512])
            yf = S(f"m_yf{l}", [128, 512])
            zs = S(f"m_zs{l}", [128, 512])
            ysq = S(f"m_ysq{l}", [128, 512])
            gst = S(f"m_gst{l}", [128, 4])
            mixo = S(f"mixo{l}", [128, 8, 128], BF16)
            raw = S(f"r_raw{l}", [64, 26, 130])
            ssum = S(f"r_ssum{l}", [64, 26, 128])
            pp = ssum
            xg0 = S(f"r_xg0{l}", [128, 130])
            xg1 = S(f"r_xg1{l}", [32, 130])
            sg0 = S(f"r_sg0{l}", [128, 128], BF16)
            sg1 = S(f"r_sg1{l}", [32, 128], BF16)
            xgt = S(f"r_xgt{l}", [128, 128])
            twb = S(f"r_twb{l}", [64, 2, 128], BF16)
            lw = S(f"r_lw{l}", [64, 8, 128])
            aa = S(f"r_aa{l}", [64, 8, 128])
            kk = S(f"r_kk{l}", [64, 8, 128])
            kd = S(f"r_kd{l}", [64, 8, 128])
            linc = S(f"r_linc{l}", [64, 8, 128])
            lex = S(f"r_lex{l}", [64, 8, 128])
            rinv = lex
            e1 = S(f"r_e1{l}", [64, 8, 128])
            e0 = S(f"r_e0{l}", [64, 8, 128])
            ei = S(f"r_ei{l}", [64, 8, 128])
            gCs = [S(f"r_gC{l}_{i}", [64, 8]) for i in range(2)]
            coefs = [S(f"r_coef{l}_{i}", [128, 8]) for i in range(2)]
            tmpk = S(f"r_tmpk{l}", [64, 8, 128])
            ARs = [S(f"r_AR{l}_{i}", [64, 8, 256], BF16) for i in range(2)]
            BKs = [S(f"r_BK{l}_{i}", [64, 8, 2, 128], BF16) for i in range(2)]
            prodb = S(f"r_prod{l}", [64, 8, 128], BF16)
            sqb = prodb
            vtokbs = [S(f"r_vtokb{l}_{i}", [128, 512], BF16) for i in range(2)]
            BKtoks = [S(f"r_BKtok{l}_{i}", [128, 8, 2, 64], BF16) for i in range(2)]
            GBs = [S(f"r_GB{l}_{i}", [128, 8, 256], BF16) for i in range(2)]
            GKs = [S(f"r_GK{l}_{i}", [128, 8, 256], BF16) for i in range(2)]
            Q0s = [S(f"r_Q0{l}_{i}", [128, 8, 128], BF16) for i in range(2)]
            Pm = [S(f"r_P{l}_{i}", [128, 8, 128], BF16) for i in range(2)]
            Qm = [S(f"r_Q{l}_{i}", [128, 8, 128], BF16) for i in range(2)]
            Ym = [S(f"r_Y{l}_{i}", [128, 8, 128], BF16) for i in range(2)]
            E1m = S(f"r_E1{l}", [128, 8, 128], BF16)
            E2m = S(f"r_E2{l}", [128, 8, 128], BF16)
            Dm = S(f"r_D{l}", [128, 8, 128], BF16)
            Zm = S(f"r_Z{l}", [128, 8, 128], BF16)
            Wt = S(f"r_W{l}", [128, 512], BF16)
            Ut = S(f"r_U{l}", [128, 512], BF16)
            Hs = S(f"r_H{l}", [64, 512])
            Hb = S(f"r_Hb{l}", [64, 512], BF16)
            ot = S(f"r_o{l}", [128, 520])
            oft = S(f"r_of{l}", [128, 520])
            osq = S(f"r_osq{l}", [128, 512])
            gn = S(f"r_gn{l}", [128, 40])
            B = [es.enter_context(nc.psum_tensor(f"mxps{l}_{i}", [128, 512], F32)).ap() for i in range(7)]
            BBp = es.enter_context(nc.psum_tensor(f"mxpsb{l}", [128, 1024], BF16)).ap()

            nbw = ROWB[:, RV_MNW:RV_MNW + 512]
            lnw = ROWB[:, RV_LNW:RV_LNW + 512]
            lnb = ROWB[:, RV_LNB:RV_LNB + 512]
            mdb = ROWB[:, RV_MD:RV_MD + 8]
            evc = [0]

            def evq():
                evc[0] += 1
                return 'act' if evc[0] % 2 == 0 else 'dve'

            def load_halo(tile, tk, srcap, srck, t0, T, nblk_dims):
                lo = max(t0 - 1, 0)
                hi = min(t0 + 129, T)
                o = lo - (t0 - 1)
                if nblk_dims:
                    em.dma('sp', tile[:, :, o:o + hi - lo], srcap[:, :, lo:hi], reads=[srck], writes=[tk])
                    if t0 == 0:
                        em.memset('pool', tile[:, :, 0:1], 0.0, [tk])
                    if t0 + 128 == T:
                        em.memset('pool', tile[:, :, 129:130], 0.0, [tk])
                else:
                    em.dma('sp', tile[:, o:o + hi - lo], srcap[:, lo:hi], reads=[srck], writes=[tk])
                    if t0 == 0:
                        em.memset('pool', tile[:, 0:1], 0.0, [tk])
                    if t0 + 128 == T:
                        em.memset('pool', tile[:, 129:130], 0.0, [tk])

            def mamba_chunk(s, t0, d, emit_out):
                T = seqT(s)
                load_halo(xbc, 'm_xbc', fm(XBC[s]), f"xbc{s}", t0, T, True)
                em.dma('pool', zdt, ZDT[s][t0:t0 + 128, :], reads=[f"zdt{s}"], writes=['m_zdt'])
                if (mixcfg or {}).get('mstop', 99) <= 1:
                    return
                for j in range(8):
                    em.ts('dve', cacc[:, j, :], xbc[:, j, 1:129], PV[:, PV_MCW + 8 + j:PV_MCW + 9 + j],
                          PV[:, PV_MCB + j:PV_MCB + j + 1], ALU.mult, ALU.add, ['m_xbc', 'PV'], ['m_cacc'])
                    em.stt(cacc[:, j, :], xbc[:, j, 0:128], PV[:, PV_MCW + j:PV_MCW + j + 1], cacc[:, j, :],
                           ALU.mult, ALU.add, ['m_xbc', 'PV', 'm_cacc'], ['m_cacc'])
                    em.stt(cacc[:, j, :], xbc[:, j, 2:130], PV[:, PV_MCW + 16 + j:PV_MCW + 17 + j], cacc[:, j, :],
                           ALU.mult, ALU.add, ['m_xbc', 'PV', 'm_cacc'], ['m_cacc'])
                em.act(cacc[:, 0:6, :], cacc[:, 0:6, :], AF.Silu, ['m_cacc'], ['m_cacc'])
                em.act(bct[:, 2:4, :], cacc[:, 6:8, :], AF.Silu, ['m_cacc'], ['m_bct'])
                em.cp('dve', bct[:, 0:2, :], cacc[:, 4:6, :], ['m_cacc'], ['m_bct'])
                if (mixcfg or {}).get('mstop', 99) <= 2:
                    return
                msub = (mixcfg or {}).get('msub', 9)
                for j in range(4):
                    em.tr(B[4][:, j * 128:(j + 1) * 128], cacc[:, j, :], ident, ['m_cacc', 'CST'], ['B4'], inc=(j == 3))
                if msub >= 1:
                    for j in range(2):
                        em.tr(B[5][:, j * 128:(j + 1) * 128], cacc[:, 4 + j, :], ident, ['m_cacc', 'CST'], ['B5'])
                if msub >= 2 and msub != 33:
                    em.cp('dve', xtok, B[4], ['B4'], ['m_xtok'])
                if msub == 30:
                    em.cp('dve', xtokb, B[4], ['B4'], ['m_xtokb'])
                elif msub == 31:
                    em.cp('act', yt, B[4], ['B4'], ['m_y'])
                elif msub == 32:
                    em.cp('act', xtokb, xtok, ['m_xtok'], ['m_xtokb'])
                elif msub >= 3:
                    em.cp('act', xtokb, B[4], ['B4'], ['m_xtokb'])
                if msub >= 4:
                    em.cp('act', btokb, B[5][:, 0:256], ['B5'], ['m_btokb'])
                if (mixcfg or {}).get('mstop', 99) <= 3:
                    return
                em.tt('dve', dts, zdt[:, 512 + d * 8:520 + d * 8], ROWB[:, RV_DTB + d * 8:RV_DTB + d * 8 + 8], ALU.add,
                      ['m_zdt', 'ROWB'], ['m_dts'])
                em.act(dts, dts, AF.Exp, ['m_dts'], ['m_dts'])
                em.act(dts, dts, AF.Ln, ['m_dts'], ['m_dts'], bias=1.0, scale=1.0)
                em.tt('dve', dta, dts, NEGA[:, d * 8:d * 8 + 8], ALU.mult, ['m_dts', 'NEGA'], ['m_dta'])
                if (mixcfg or {}).get('mstop', 99) <= 4:
                    return
                em.mm(B[5][:, 256:264], MI(d), dta, True, True, ['CST', 'm_dta'], ['B5'])
                em.mm(B[5][:, 264:272], ones, dta, True, True, ['CST', 'm_dta'], ['B5'])
                em.cp('act', sm[:, 32:40], B[5][:, 264:272], ['B5'], ['m_sm'])
                em.tt('dve', sm[:, 24:32], sm[:, 32:40], B[5][:, 256:264], ALU.subtract, ['B5', 'm_sm'], ['m_sm'])
                em.act(sm[:, 0:8], sm[:, 24:32], AF.Exp, ['m_sm'], ['m_sm'])
                em.tt('dve', sm[:, 8:16], sm[:, 0:8], dts, ALU.mult, ['m_sm', 'm_dts'], ['m_sm'])
                em.act(sm[:, 16:24], B[5][:, 264:272], AF.Exp, ['B5'], ['m_sm'])
                em.tt('dve', xw.rearrange("p (h q) -> p h q", q=64), xtok.rearrange("p (h q) -> p h q", q=64),
                      sm[:, 8:16].unsqueeze(2).to_broadcast([128, 8, 64]), ALU.mult, ['m_xtok', 'm_sm'], ['m_xw'])
                if (mixcfg or {}).get('mstop', 99) <= 5:
                    return
                for g in range(2):
                    em.mm(B[6][:, g * 128:(g + 1) * 128], bct[:, g, :], bct[:, 2 + g, :], True, True, ['m_bct'], ['B6'])
                for h in range(8):
                    g = h // 4
                    i2 = h % 2
                    pe_ = B[6][:, 256:512] if i2 == 0 else B[5][:, 0:256]
                    pek = 'B6' if i2 == 0 else 'B5'
                    em.ts('dve', l2[i2], MSO[:, d, :], dta[:, h:h + 1], None, ALU.mult, None,
                          ['MSO', 'm_dta'], [f"m_l2{i2}"])
                    em.mm(pe_[:, 0:128], l2[i2][:, 0:128], MI(d), True, True, [f"m_l2{i2}", 'CST'], [pek])
                    em.mm(pe_[:, 128:256], l2[i2][:, 128:256], MI(d), True, True, [f"m_l2{i2}", 'CST'], [pek])
                    em.act(Et[i2], pe_[:, 0:256], AF.Exp, [pek], [f"m_E{i2}"])
                    em.stt(LTt[i2], Et[i2][:, 0:128], dts[:, h:h + 1], MI(d), ALU.mult, ALU.mult,
                           [f"m_E{i2}", 'm_dts', 'CST'], [f"m_LT{i2}"])
                    em.tt('dve', STt[i2], B[6][:, g * 128:(g + 1) * 128], LTt[i2], ALU.mult, ['B6', f"m_LT{i2}"],
                          [f"m_ST{i2}"])
                    em.tt('dve', CsT[i2], bct[:, 2 + g, :], Et[i2][:, 128:256], ALU.mult, ['m_bct', f"m_E{i2}"],
                          [f"m_Cs{i2}"])
                    em.mm(B[4][:, h * 64:(h + 1) * 64], STt[i2], xtokb[:, h * 64:(h + 1) * 64], True, False,
                          [f"m_ST{i2}", 'm_xtokb'], ['B4'])
                    em.mm(B[4][:, h * 64:(h + 1) * 64], CsT[i2], hstb[:, h * 64:(h + 1) * 64], False, True,
                          [f"m_Cs{i2}", 'm_hstb'], ['B4'])
                if (mixcfg or {}).get('mstop', 99) <= 6:
                    return
                for g in range(2):
                    em.mm(B[6][:, g * 256:(g + 1) * 256], btokb[:, g * 128:(g + 1) * 128], xw[:, g * 256:(g + 1) * 256],
                          True, True, ['m_btokb', 'm_xw'], ['B6'])
                em.tt('dve', hst.rearrange("p (h q) -> p h q", q=64), hst.rearrange("p (h q) -> p h q", q=64),
                      sm[:, 16:24].unsqueeze(2).to_broadcast([128, 8, 64]), ALU.mult, ['m_hst', 'm_sm', 'B4'], ['m_hst'])
                em.tt('dve', hst, hst, B[6], ALU.add, ['m_hst', 'B6'], ['m_hst'])
                em.cp('act', hstb, hst, ['m_hst', 'B4'], ['m_hstb'])
                if (mixcfg or {}).get('mstop', 99) <= 7:
                    return
                if d == 0:
                    if emit_out:
                        em.cp('act', yt, B[4], ['B4'], ['m_y'])
                        em.dma('pool', YF[s][t0:t0 + 128, :], yt, reads=['m_y'], writes=[f"yf{s}"])
                elif emit_out:
                    em.dma('sp', yf, YF[s][t0:t0 + 128, :], reads=[f"yf{s}"], writes=['m_yf'])
                    em.tt('dve', yt, B[4], yf, ALU.add, ['B4', 'm_yf'], ['m_y'])
                    em.tt('dve', yf.rearrange("p (h q) -> p h q", q=64), xtok.rearrange("p (h q) -> p h q", q=64),
                          mdb.unsqueeze(2).to_broadcast([128, 8, 64]), ALU.mult, ['m_xtok', 'ROWB', 'm_yf'], ['m_yf'])
                    em.tt('dve', yt, yt, yf, ALU.add, ['m_y', 'm_yf'], ['m_y'])
                    em.act(zs, zdt[:, 0:512], AF.Silu, ['m_zdt'], ['m_zs'])
                    em.tt('dve', yt, yt, zs, ALU.mult, ['m_y', 'm_zs'], ['m_y'])
                    for g in range(2):
                        em.act(ysq[:, g * 256:(g + 1) * 256], yt[:, g * 256:(g + 1) * 256], AF.Square, ['m_y'],
                               ['m_ysq', 'm_gst'], accum=gst[:, g:g + 1])
                    em.act(gst[:, 2:4], gst[:, 0:2], AF.Sqrt, ['m_gst'], ['m_gst'], bias=EPS, scale=1.0 / 256)
                    em.op('dve', lambda e: e.reciprocal(out=gst[:, 2:4], in_=gst[:, 2:4]), ['m_gst'], ['m_gst'])
                    for g in range(2):
                        em.stt(yt[:, g * 256:(g + 1) * 256], yt[:, g * 256:(g + 1) * 256], gst[:, 2 + g:3 + g],
                               nbw[:, g * 256:(g + 1) * 256], ALU.mult, ALU.mult, ['m_y', 'm_gst', 'ROWB'], ['m_y'])
                    for j in range(4):
                        em.tr(B[5][:, j * 128:(j + 1) * 128], yt[:, j * 128:(j + 1) * 128], ident, ['m_y', 'CST'], ['B5'], inc=(j == 3))
                    em.cp('act', mixo[:, 0:4, :], B[5].rearrange("p (j t) -> p j t", t=128), ['B5'], ['mixo_m'])
                    em.dma('pool', fm(MIX[s])[:, 0:4, t0:t0 + 128], mixo[:, 0:4, :], reads=['mixo_m'], writes=[f"mixm{s}"])

            def wkv_chunk(s, t0, d, emit_out, idx=0, first_of_d=False, split=True):
                T = seqT(s)
                pb = idx % 2
                AR, BK, GB, GK, Q0, BKtok, vtokb, gC, coef = (ARs[pb], BKs[pb], GBs[pb], GKs[pb], Q0s[pb], BKtoks[pb],
                                                             vtokbs[pb], gCs[pb], coefs[pb])
                kAR, kBK, kGB, kGK, kQ0, kBKtok, kvtokb, kgC, kcoef = [f"{n}{pb}" for n in
                                                                      ('r_AR', 'r_BK', 'r_GB', 'r_GK', 'r_Q0h', 'r_BKtok',
                                                                       'r_vtokb', 'r_gC', 'r_coef')]
                if split:
                    em.stream = ('rp', idx)
                fin = (d == 1 and emit_out)
                rk = f"rkv{s}"
                load_halo(raw[:, 0:24, :], 'r_raw', RKV[s].rearrange("j p t -> p j t"), rk, t0, T, True)
                load_halo(raw[:, 24:25, :], 'r_raw', XWA[s][d:d + 1].rearrange("j p t -> p j t"), f"xwa{s}", t0, T, True)
                load_halo(raw[:, 25:26, :], 'r_raw', XWA[s][2 + d:3 + d].rearrange("j p t -> p j t"), f"xwa{s}", t0, T, True)
                em.tt('dve', ssum, raw[:, :, 0:128], raw[:, :, 2:130], ALU.add, ['r_raw'], ['r_ssum'])
                em.stt(ssum, ssum, 0.5, raw[:, :, 1:129], ALU.mult, ALU.subtract, ['r_ssum', 'r_raw'], ['r_ssum'])
                em.tt('dve', ssum[:, 0:24, :], ssum[:, 0:24, :],
                      PV64[:, P6_MURKV:P6_MURKV + 24].unsqueeze(2).to_broadcast([64, 24, 128]), ALU.mult,
                      ['r_ssum', 'PV64'], ['r_ssum'])
                em.ts('dve', ssum[:, 24, :], ssum[:, 24, :], PV64[:, P6_MUWA + d:P6_MUWA + d + 1], None, ALU.mult, None,
                      ['r_ssum', 'PV64'], ['r_ssum'])
                em.ts('dve', ssum[:, 25, :], ssum[:, 25, :], PV64[:, P6_MUWA + 2 + d:P6_MUWA + 3 + d], None, ALU.mult, None,
                      ['r_ssum', 'PV64'], ['r_ssum'])
                em.tt('dve', pp, raw[:, :, 1:129], ssum, ALU.add, ['r_raw', 'r_ssum'], ['r_ssum'])
                rr = pp[:, 0:8, :]
                kr = pp[:, 8:16, :]
                vr = pp[:, 16:24, :]
                em.act(twb[:, 0, :], pp[:, 24, :], AF.Tanh, ['r_ssum'], ['r_twb'])
                em.cp('dve', twb[:, 1, :], pp[:, 25, :], ['r_ssum'], ['r_twb'])
                for h in range(8):
                    em.mm(B[h // 4][0:64, (h % 4) * 128:(h % 4 + 1) * 128], w2b[:, d, h * 64:(h + 1) * 64], twb[:, 0, :], True, True,
                          ['w2b', 'r_twb'], [f"B{h // 4}"], inc=(h == 7))
                for h in range(8):
                    em.act(lw[:, h, :], B[h // 4][0:64, (h % 4) * 128:(h % 4 + 1) * 128], AF.Sigmoid, [f"B{h // 4}", 'PV64'],
                           ['r_lw'], bias=PV64[:, P6_W0 + d * 8 + h:P6_W0 + d * 8 + h + 1], scale=1.0)
                for h in range(8):
                    em.mm(B[h // 4][0:64, (h % 4) * 128:(h % 4 + 1) * 128], a2b[:, d, h * 64:(h + 1) * 64], twb[:, 1, :], True, True,
                          ['a2b', 'r_twb'], [f"B{h // 4}"], inc=(h == 7))
                for h in range(8):
                    em.act(aa[:, h, :], B[h // 4][0:64, (h % 4) * 128:(h % 4 + 1) * 128], AF.Sigmoid,
                           [f"B{h // 4}", 'PV64'], ['r_aa'], bias=PV64[:, P6_A0 + d * 8 + h:P6_A0 + d * 8 + h + 1], scale=1.0)
                em.ts('dve', lw, lw, -R_DECAY_SCALE, None, ALU.mult, None, ['r_lw'], ['r_lw'])
                em.tt('dve', kk, kr, PV64[:, P6_KK:P6_KK + 8].unsqueeze(2).to_broadcast([64, 8, 128]), ALU.mult,
                      ['r_ssum', 'PV64'], ['r_kk'])
                em.act(sqb, kk, AF.Square, ['r_kk'], ['r_prod'])
                for hh in range(2):
                    em.mm(B[hh][0:64, :], onesb[0:64, 0:64], sqb[:, hh * 4:(hh + 1) * 4, :], True, True,
                          ['CSTB', 'r_prod'], [f"B{hh}"])
                for hh in range(2):
                    em.act(rinv[:, hh * 4:(hh + 1) * 4, :], B[hh][0:64, :].rearrange("p (h t) -> p h t", t=128), AF.Sqrt,
                           [f"B{hh}"], ['r_lex'])
                em.ts('dve', rinv, rinv, 1e-12, None, ALU.max, None, ['r_lex'], ['r_lex'])
                em.op('dve', lambda e: e.reciprocal(out=rinv, in_=rinv), ['r_lex'], ['r_lex'])
                em.tt('dve', kk, kk, rinv, ALU.mult, ['r_kk', 'r_lex'], ['r_kk'])
                em.tt('dve', tmpk, aa, PV64[:, P6_KA:P6_KA + 8].unsqueeze(2).to_broadcast([64, 8, 128]), ALU.mult,
                      ['r_aa', 'PV64'], ['r_tmpk'])
                em.tt('dve', tmpk, tmpk, OMMU.unsqueeze(2).to_broadcast([64, 8, 128]), ALU.add, ['r_tmpk', 'OMMU'], ['r_tmpk'])
                em.tt('dve', kd, kr, tmpk, ALU.mult, ['r_ssum', 'r_tmpk'], ['r_kd'])
                em.op('dve', lambda e: e.tensor_tensor_scan(out=linc.rearrange("p h t -> p (h t)"),
                                                            data0=RMK.rearrange("p h t -> p (h t)"),
                                                            data1=lw.rearrange("p h t -> p (h t)"), initial=0.0,
                                                            op0=ALU.mult, op1=ALU.add), ['RMK', 'r_lw'], ['r_linc'])
                if d == 0:
                    tot = linc[:, :, 127:128]
                else:
                    em.tt('dve', lex, lw, linc, ALU.subtract, ['r_lw', 'r_linc'], ['r_lex'])
                    em.cp('dve', gC, linc[:, :, 127], ['r_linc'], [kgC])
                    em.tt('dve', linc, lex, gC.unsqueeze(2).to_broadcast([64, 8, 128]), ALU.add, ['r_lex', kgC, 'r_linc'],
                          ['r_linc'])
                    tot = linc[:, :, 0:1]
                em.tt('dve', lex, linc, lw, ALU.subtract, ['r_linc', 'r_lw'], ['r_lex'])
                em.act(e1, linc, AF.Exp, ['r_linc'], ['r_e1'])
                em.act(e0, lex, AF.Exp, ['r_lex'], ['r_e0'])
                em.act(ei, linc, AF.Exp, ['r_linc'], ['r_ei'], scale=-1.0)
                em.act(gC, tot.rearrange("p h o -> p (h o)"), AF.Exp, ['r_linc', kgC], [kgC])
                em.tt('dve', AR[:, :, 128:256], rr, e1, ALU.mult, ['r_ssum', 'r_e1'], [kAR])
                em.stt(AR[:, :, 0:128], kk, -1.0, e0, ALU.mult, ALU.mult, ['r_kk', 'r_e0'], [kAR])
                em.tt('dve', tmpk, kk, aa, ALU.mult, ['r_kk', 'r_aa', 'r_tmpk'], ['r_tmpk'])
                em.tt('dve', BK[:, :, 0, :], tmpk, ei, ALU.mult, ['r_tmpk', 'r_ei'], [kBK])
                em.tt('dve', BK[:, :, 1, :], kd, ei, ALU.mult, ['r_kd', 'r_ei'], [kBK])
                em.tt('dve', tmpk, rr, kd, ALU.mult, ['r_ssum', 'r_kd', 'r_tmpk'], ['r_tmpk'])
                em.tt('dve', prodb, tmpk, PV64[:, P6_RK:P6_RK + 8].unsqueeze(2).to_broadcast([64, 8, 128]), ALU.mult,
                      ['r_tmpk', 'PV64'], ['r_prod'])
                for h in range(8):
                    em.tr(B[0][:, h * 64:(h + 1) * 64], vr[:, h, :], ident[0:64, 0:64], ['r_ssum', 'CST'], ['B0'], inc=(h == 7))
                em.cp('act', vtokb, B[0], ['B0'], [kvtokb])
                for half in range(2):
                    for h4 in range(4):
                        for q in range(2):
                            em.tr(BBp[:, (h4 * 2 + q) * 64:(h4 * 2 + q + 1) * 64], BK[:, half * 4 + h4, q, :], identb[0:64, 0:64],
                                  [kBK, 'CSTB'], ['BB'], inc=(h4 == 3 and q == 1))
                    em.cp('dve', BKtok[:, half * 4:(half + 1) * 4].rearrange("p h q k -> p (h q k)"), BBp[:, 0:512], ['BB'], [kBKtok])
                for h in range(8):
                    em.mm(B[1][:, 256 + h:257 + h], prodb[:, h, :], onesb[0:64, 0:1], True, True, ['r_prod', 'CSTB'], ['B1'], inc=(h == 7))
                em.cp('act', coef, B[1][:, 256:264], ['B1'], [kcoef])
                for hp in range(4):
                    bb_ = B[0]
                    bbk = "B0"
                    bk2 = B[1]
                    bk2k = "B1"
                    for q in range(2):
                        h = hp * 2 + q
                        em.mm(bb_[:, q * 256:(q + 1) * 256], BK[:, h, 0, :], AR[:, h, :], True, True, [kBK, kAR, 'r_lw', 'r_aa'], [bbk], inc=(q == 1))
                        em.mm(bk2[:, q * 256:(q + 1) * 256], BK[:, h, 1, :], AR[:, h, :], True, True, [kBK, kAR], [bk2k], inc=(q == 1))
                    em.tt('dve', GB[:, hp * 2:hp * 2 + 2, :], bb_.rearrange("p (q c) -> p q c", c=256),
                          MARt[:, d:d + 1, :].to_broadcast([128, 2, 256]), ALU.mult, [bbk, 'MAR'], [kGB])
                    em.tt('dve', Q0[:, hp * 2:hp * 2 + 2, :], bb_.rearrange("p (q c) -> p q c", c=256)[:, :, 0:128],
                          CST[:, C_MP0 + (1 - d) * 128:C_MP0 + (2 - d) * 128].unsqueeze(1).to_broadcast([128, 2, 128]), ALU.mult,
                          [bbk, 'CST'], [kQ0])
                    em.tt('dve', GK[:, hp * 2:hp * 2 + 2, :], bk2.rearrange("p (q c) -> p q c", c=256),
                          MARt[:, d:d + 1, :].to_broadcast([128, 2, 256]), ALU.mult, [bk2k, 'MAR'], [kGK])
                if split:
                    em.stream = ('rs', idx)
                if first_of_d:
                    em.memset('pool', Hs, 0.0, ['r_H'])
                    em.memset('pool', Hb, 0.0, ['r_Hb'])
                for hh in range(2):
                    bp = B[2 + hh]
                    bpk = f"B{2 + hh}"
                    for q in range(4):
                        h = hh * 4 + q
                        em.mm(bp[:, q * 128:(q + 1) * 128], AR[:, h, 0:128], BK[:, h, 0, :], True, True, [kAR, kBK], [bpk], inc=(q == 3))
                    b3 = bp.rearrange("p (q c) -> p q c", c=128)
                    hsl = slice(hh * 4, (hh + 1) * 4)
                    em.tt('dve', Pm[0][:, hsl, :], b3, CST[:, C_MP0 + d * 128:C_MP0 + (d + 1) * 128].unsqueeze(1).to_broadcast([128, 4, 128]),
                          ALU.mult, [bpk, 'CST'], ['r_P0'])
                    em.tt('dve', E1m[:, hsl, :], b3, CST[:, C_ME1 + d * 128:C_ME1 + (d + 1) * 128].unsqueeze(1).to_broadcast([128, 4, 128]),
                          ALU.mult, [bpk, 'CST'], ['r_E1'])
                    em.tt('dve', E2m[:, hsl, :], b3, CST[:, C_ME2 + d * 128:C_ME2 + (d + 1) * 128].unsqueeze(1).to_broadcast([128, 4, 128]),
                          ALU.mult, [bpk, 'CST'], ['r_E2'])
                em.tt('dve', Ym[0], Q0, identb.unsqueeze(1).to_broadcast([128, 8, 128]), ALU.add, [kQ0, 'CSTB'], ['r_Y0'])
                em.cp('act', Qm[0], Q0, [kQ0], ['r_Q0'])
                cur = 0
                for lev in range(1, 5):
                    nxt = 1 - cur
                    for hh in range(2):
                        bp = B[2]
                        bpk = "B2"
                        bq = B[3]
                        bqk = "B3"
                        hsl = slice(hh * 4, (hh + 1) * 4)
                        for q in range(4):
                            h = hh * 4 + q
                            em.mm(bp[:, q * 128:(q + 1) * 128], Qm[cur][:, h, :], Pm[cur][:, h, :], True, True,
                                  [f"r_Q{cur}", f"r_P{cur}"], [bpk], inc=(q == 3))
                        for q in range(4):
                            h = hh * 4 + q
                            em.mm(bq[:, q * 128:(q + 1) * 128], Pm[cur][:, h, :], Qm[cur][:, h, :], True, True,
                                  [f"r_Q{cur}", f"r_P{cur}"], [bqk], inc=(q == 3))
                        em.cp('act', Pm[nxt][:, hsl, :], bp.rearrange("p (q c) -> p q c", c=128), [bpk], [f"r_P{nxt}"])
                        em.cp('dve', Qm[nxt][:, hsl, :], bq.rearrange("p (q c) -> p q c", c=128), [bqk], [f"r_Q{nxt}"])
                    for hh in range(2):
                        by = B[2 + hh]
                        byk = f"B{2 + hh}"
                        hsl = slice(hh * 4, (hh + 1) * 4)
                        for q in range(4):
                            h = hh * 4 + q
                            em.mm(by[:, q * 128:(q + 1) * 128], Pm[nxt][:, h, :], Ym[cur][:, h, :], True, True,
                                  [f"r_P{nxt}", f"r_Y{cur}"], [byk], inc=(q == 3))
                        em.tt('dve', Ym[nxt][:, hsl, :], by.rearrange("p (q c) -> p q c", c=128), Ym[cur][:, hsl, :], ALU.add,
                              [byk, f"r_Y{cur}"], [f"r_Y{nxt}"])
                    cur = nxt
                Dt = Ym[cur]
                dtk = f"r_Y{cur}"
                for st, (Em_, ek) in enumerate([(E1m, 'r_E1'), (E2m, 'r_E2')]):
                    oth = Ym[1 - cur]
                    othk = f"r_Y{1 - cur}"
                    for half in range(2):
                        for h4 in range(4):
                            em.tr(BBp[:, 512 + h4 * 128:512 + (h4 + 1) * 128], Dt[:, half * 4 + h4, :], identb, [dtk, 'CSTB'], ['BB'], inc=(h4 == 3))
                        em.cp('act', Dm[:, half * 4:(half + 1) * 4, :], BBp[:, 512:1024].rearrange("p (h c) -> p h c", c=128),
                              ['BB'], ['r_D'])
                    for hh in range(2):
                        bz = B[2 + hh]
                        bzk = f"B{2 + hh}"
                        hsl = slice(hh * 4, (hh + 1) * 4)
                        for q in range(4):
                            h = hh * 4 + q
                            em.mm(bz[:, q * 128:(q + 1) * 128], Em_[:, h, :], Dt[:, h, :], True, True, [ek, dtk], [bzk], inc=(q == 3))
                        em.cp('act' if hh == 0 else 'dve', Zm[:, hsl, :], bz.rearrange("p (q c) -> p q c", c=128), [bzk], ['r_Z'])
                    for hh in range(2):
                        by = B[2 + hh]
                        byk = f"B{2 + hh}"
                        hsl = slice(hh * 4, (hh + 1) * 4)
                        for q in range(4):
                            h = hh * 4 + q
                            em.mm(by[:, q * 128:(q + 1) * 128], Dm[:, h, :], Zm[:, h, :], True, True, ['r_D', 'r_Z'], [byk], inc=(q == 3))
                        em.tt('dve', oth[:, hsl, :], by.rearrange("p (q c) -> p q c", c=128), Dt[:, hsl, :], ALU.add,
                              [byk, dtk], [othk])
                    cur = 1 - cur
                    Dt = Ym[cur]
                    dtk = f"r_Y{cur}"
                TT_ = Dt
                ttk = dtk
                for h in range(8):
                    hs_ = slice(h * 64, (h + 1) * 64)
                    em.mm(B[2][:, hs_], AR[:, h, 0:128], Hb[:, hs_], True, False, [kAR, 'r_Hb'], ['B2'])
                    em.mm(B[2][:, hs_], GK[:, h, 0:128], vtokb[:, hs_], False, True, [kGK, kvtokb], ['B2'], inc=(h == 7))
                em.cp('act', Wt, B[2], ['B2'], ['r_W'])
                for h in range(8):
                    hs_ = slice(h * 64, (h + 1) * 64)
                    em.mm(B[3][:, hs_], TT_[:, h, :], Wt[:, hs_], True, True, [ttk, 'r_W'], ['B3'], inc=(h == 7))
                em.cp('dve', Ut, B[3], ['B3'], ['r_U'])
                for h in range(8):
                    hs_ = slice(h * 64, (h + 1) * 64)
                    em.mm(B[2][:, hs_], AR[:, h, 128:256], Hb[:, hs_], True, False, [kAR, 'r_Hb'], ['B2'])
                    em.mm(B[2][:, hs_], GB[:, h, 128:256], Ut[:, hs_], False, False, [kGB, 'r_U'], ['B2'])
                    em.mm(B[2][:, hs_], GK[:, h, 128:256], vtokb[:, hs_], False, True, [kGK, kvtokb], ['B2'], inc=(h == 7))
                for h in range(8):
                    hs_ = slice(h * 64, (h + 1) * 64)
                    em.mm(B[3][0:64, hs_], BKtok[:, h, 0, :], Ut[:, hs_], True, False, [kBKtok, 'r_U'], ['B3'])
                    em.mm(B[3][0:64, hs_], BKtok[:, h, 1, :], vtokb[:, hs_], False, True, [kBKtok, kvtokb], ['B3'], inc=(h == 7))
                em.tt('dve', Hs, Hs, B[3][0:64, :], ALU.add, ['r_H', 'B3'], ['r_H'])
                em.tt('dve', Hs.rearrange("p (h v) -> p h v", v=64), Hs.rearrange("p (h v) -> p h v", v=64),
                      gC.unsqueeze(2).to_broadcast([64, 8, 64]), ALU.mult, ['r_H', kgC], ['r_H'])
                em.cp('act', Hb, Hs, ['r_H', 'B2'], ['r_Hb'])
                if d == 0:
                    if emit_out:
                        em.cp('act', ot[:, 0:512], B[2], ['B2'], ['r_o'])
                        em.cp('dve', ot[:, 512:520], coef, [kcoef], ['r_ocoef'])
                        em.dma('pool', OF[s][t0:t0 + 128, :], ot, reads=['r_o', 'r_ocoef'], writes=[f"of{s}"])
                elif emit_out:
                    em.dma('sp', oft, OF[s][t0:t0 + 128, :], reads=[f"of{s}"], writes=['r_of'])
                    em.tt('dve', ot[:, 0:512], B[2], oft[:, 0:512], ALU.add, ['B2', 'r_of'], ['r_o'])
                    o3 = ot[:, 0:512].rearrange("p (h v) -> p h v", v=64)
                    em.op('dve', lambda e: e.tensor_reduce(out=gn[:, 0:8], in_=o3, axis=AX.X, op=ALU.add), ['r_o'], ['r_gn'])
                    em.act(osq, ot[:, 0:512], AF.Square, ['r_o'], ['r_osq'])
                    em.op('dve', lambda e: e.tensor_reduce(out=gn[:, 8:16], in_=osq.rearrange("p (h v) -> p h v", v=64),
                                                           axis=AX.X, op=ALU.add), ['r_osq', 'r_gn'], ['r_gn'])
                    em.ts('dve', gn[:, 16:24], gn[:, 0:8], 1.0 / 64, None, ALU.mult, None, ['r_gn'], ['r_gn'])
                    em.tt('dve', gn[:, 0:8], gn[:, 16:24], gn[:, 16:24], ALU.mult, ['r_gn'], ['r_gn'])
                    em.stt(gn[:, 24:32], gn[:, 8:16], 1.0 / 64, gn[:, 0:8], ALU.mult, ALU.subtract, ['r_gn'], ['r_gn'])
                    em.act(gn[:, 24:32], gn[:, 24:32], AF.Sqrt, ['r_gn'], ['r_gn'], bias=R_LN_EPS, scale=1.0)
                    em.op('dve', lambda e: e.reciprocal(out=gn[:, 24:32], in_=gn[:, 24:32]), ['r_gn'], ['r_gn'])
                    em.tt('dve', o3, o3, gn[:, 16:24].unsqueeze(2).to_broadcast([128, 8, 64]), ALU.subtract, ['r_o', 'r_gn'], ['r_o'])
                    em.tt('dve', o3, o3, gn[:, 24:32].unsqueeze(2).to_broadcast([128, 8, 64]), ALU.mult, ['r_o', 'r_gn'], ['r_o'])
                    em.tt('dve', ot[:, 0:512], ot[:, 0:512], lnw, ALU.mult, ['r_o', 'ROWB'], ['r_o'])
                    em.tt('dve', ot[:, 0:512], ot[:, 0:512], lnb, ALU.add, ['r_o', 'ROWB'], ['r_o'])
                    em.tt('dve', gn[:, 32:40], coef, oft[:, 512:520], ALU.add, [kcoef, 'r_of', 'r_gn'], ['r_gn'])
                    em.tt('dve', osq.rearrange("p (h v) -> p h v", v=64), vtokb.rearrange("p (h v) -> p h v", v=64),
                          gn[:, 32:40].unsqueeze(2).to_broadcast([128, 8, 64]), ALU.mult, [kvtokb, 'r_gn', 'r_osq'], ['r_osq'])
                    em.tt('dve', ot[:, 0:512], ot[:, 0:512], osq, ALU.add, ['r_o', 'r_osq'], ['r_o'])
                    load_halo(xg0, 'r_xg0', XG[s][0:128, :], f"xg{s}", t0, T, False)
                    load_halo(xg1, 'r_xg1', XG[s][128:160, :], f"xg{s}", t0, T, False)
                    for (xg_, sg_, np_, mucol, kx_, ks_) in [(xg0, sg0, 128, PV_MUXG0, 'r_xg0', 'r_sg0'),
                                                             (xg1, sg1, 32, PV_MUXG1, 'r_xg1', 'r_sg1')]:
                        em.tt('dve', xgt[:np_, :], xg_[:np_, 0:128], xg_[:np_, 2:130], ALU.add, [kx_], ['r_xgt'])
                        em.stt(xgt[:np_, :], xgt[:np_, :], 0.5, xg_[:np_, 1:129], ALU.mult, ALU.subtract, ['r_xgt', kx_], ['r_xgt'])
                        em.stt(xgt[:np_, :], xgt[:np_, :], PV[:np_, mucol:mucol + 1], xg_[:np_, 1:129], ALU.mult, ALU.add,
                               ['r_xgt', kx_, 'PV'], ['r_xgt'])
                        em.act(sg_[:np_, :], xgt[:np_, :], AF.Sigmoid, ['r_xgt'], [ks_])
                    em.mm(B[3], sg0, g2b0, True, False, ['r_sg0', 'g2b'], ['B3'])
                    em.mm(B[3], sg1, g2b1, False, True, ['r_sg1', 'g2b'], ['B3'])
                    em.tt('dve', ot[:, 0:512], ot[:, 0:512], B[3], ALU.mult, ['r_o', 'B3'], ['r_o'])
                    for j in range(4):
                        em.tr(B[2][:, j * 128:(j + 1) * 128], ot[:, j * 128:(j + 1) * 128], ident, ['r_o', 'CST'], ['B2'], inc=(j == 3))
                    em.cp('act', mixo[:, 4:8, :], B[2].rearrange("p (j t) -> p j t", t=128), ['B2'], ['mixo_r'])
                    em.dma('pool', fm(MIX[s])[:, 4:8, t0:t0 + 128], mixo[:, 4:8, :], reads=['mixo_r'], writes=[f"mixr{s}"])

            mc_ = mixcfg or {}
            inter = mc_.get('interleave', True)
            for b in range(mc_.get('nb', NBL)):
                nw = 0
                for stream, fnc in (('m', mamba_chunk), ('r', wkv_chunk)):
                    if not mc_.get('mamba' if stream == 'm' else 'wkv', True):
                        continue
                    for d in range(mc_.get('nd', 2)):
                        first = True
                        if stream == 'm':
                            em.stream = 'm' if inter else None
                            em.memset('pool', hst, 0.0, ['m_hst'])
                            em.memset('pool', hstb, 0.0, ['m_hstb'])
                        for kind in range(mc_.get('nkind', 2)):
                            s = b * 2 + kind
                            T = seqT(s)
                            nch = T // CH
                            order = range(nch) if d == 0 else range(nch - 1, -1, -1)
                            emit = (kind == 1) or need_ctx_out or mc_.get('ctxout', False)
                            for c in order:
                                if stream == 'm':
                                    fnc(s, c * CH, d, emit)
                                else:
                                    fnc(s, c * CH, d, emit, idx=nw, first_of_d=first, split=inter)
                                    nw += 1
                                    first = False
                em.stream = None
                if inter:
                    em.flush_mixer(nw)
            em.barrier()

    def stage_proj_post(l, phase, wap, Kc, SRC, srcname, gidx, seqs):
        with ExitStack() as es:
            def S(name, shape, dt=F32):
                return es.enter_context(nc.sbuf_tensor(name, list(shape), dt)).ap()
            nm = f"pp{phase}"
            wb = load_wbf(es, l, wap, Kc, D, nm + "w")
            a = S(f"{nm}a{l}", [128, Kc, 512], BF16)
            xt = S(f"{nm}x{l}", [128, 8, 512])
            y = S(f"{nm}y{l}", [128, 8, 512])
            sq = S(f"{nm}sq{l}", [128, 8, 512], BF16)
            rstd = S(f"{nm}rs{l}", [128, 512])
            pss = [es.enter_context(nc.psum_tensor(f"{nm}ps{l}_{i}", [128, 512], F32)).ap() for i in range(5)]
            for s in seqs:
                T = seqT(s)
                TW = min(512, T)
                jmod = 2 if s % 2 == 0 else s // 2
                rsrc, rsk = res_src(l, s, phase)
                rdst, rdk = res_dst(l, s, phase)
                for tt_ in range(T // TW):
                    t0 = tt_ * TW
                    em.dma('sp', a[:, :, :TW], fm(SRC[s])[:, :, t0:t0 + TW], reads=([f"mixm{s}", f"mixr{s}"] if srcname == 'mix' else [f"{srcname}{s}"]), writes=[nm + 'a'])
                    em.dma('pool', xt[:, :, :TW], fm(rsrc)[:, :, t0:t0 + TW], reads=[rsk], writes=[nm + 'x'])
                    for m in range(8):
                        ps = pss[m % 4]
                        pk = f"ps{m % 4}"
                        for k in range(Kc):
                            em.mm(ps[:, :TW], wb[:, k, m * 128:(m + 1) * 128], a[:, k, :TW], k == 0, k == Kc - 1,
                                  [nm + 'wbf', nm + 'a'], [pk])
                        em.cp('dve', y[:, m, :TW], ps[:, :TW], [pk], [nm + 'y'])
                        em.act(sq[:, m, :TW], ps[:, :TW], AF.Square, [pk], [nm + 'sq'])
                    for m in range(8):
                        em.mm(pss[4][:, :TW], onesb, sq[:, m, :TW], m == 0, m == 7, ['CSTB', nm + 'sq'], ['ps4'])
                    em.act(rstd[:, :TW], pss[4][:, :TW], AF.Sqrt, ['ps4'], [nm + 'rs'], bias=EPS, scale=1.0 / D)
                    em.op('dve', lambda e: e.reciprocal(out=rstd[:, :TW], in_=rstd[:, :TW]), [nm + 'rs'], [nm + 'rs'])
                    for m in range(8):
                        em.stt(y[:, m, :TW], y[:, m, :TW], DER[:, gidx, m, jmod:jmod + 1], rstd[:, :TW], ALU.mult, ALU.mult,
                               [nm + 'y', nm + 'rs', 'DER'], [nm + 'y'])
                    em.tt('dve', xt[:, :, :TW], xt[:, :, :TW], y[:, :, :TW], ALU.add, [nm + 'x', nm + 'y'], [nm + 'x'])
                    em.dma('pool', fm(rdst)[:, :, t0:t0 + TW], xt[:, :, :TW], reads=[nm + 'x'], writes=[rdk])
            em.barrier()

    def stage_ffn_up(l, seqs):
        with ExitStack() as es:
            def S(name, shape, dt=F32):
                return es.enter_context(nc.sbuf_tensor(name, list(shape), dt)).ap()
            wb = load_wbf(es, l, f_w_up[l], 8, 2 * DFF, "wup")
            xt = S(f"fux{l}", [128, 8, 512])
            h = S(f"fuh{l}", [128, 8, 512], BF16)
            sq = S(f"fusq{l}", [128, 8, 512], BF16)
            rstd = S(f"furs{l}", [128, 512])
            stg = [S(f"fustg{l}_{i}", [128, 512]) for i in range(2)]
            stv = [S(f"fustv{l}_{i}", [128, 512], BF16) for i in range(2)]
            pss = [es.enter_context(nc.psum_tensor(f"fups{l}_{i}", [128, 512], F32)).ap() for i in range(8)]
            for s in seqs:
                T = seqT(s)
                TW = min(512, T)
                jmod = 2 if s % 2 == 0 else s // 2
                src, srck = res_src(l, s, 1)
                for tt_ in range(T // TW):
                    t0 = tt_ * TW
                    em.dma('sp', xt[:, :, :TW], fm(src)[:, :, t0:t0 + TW], reads=[srck], writes=['fux'])
                    prenorm(xt, TW, h, sq, rstd, pss[7], jmod, 2, 24, 'fux', 'fuh', 'ps7')
                    for j in range(NFF):
                        pg = pss[(2 * j) % 6]
                        pgk = f"ps{(2 * j) % 6}"
                        pv_ = pss[(2 * j + 1) % 6]
                        pvk = f"ps{(2 * j + 1) % 6}"
                        for k in range(8):
                            em.mm(pg[:, :TW], wb[:, k, j * 128:(j + 1) * 128], h[:, k, :TW], k == 0, k == 7, ['wupbf', 'fuh'], [pgk])
                        for k in range(8):
                            em.mm(pv_[:, :TW], wb[:, k, DFF + j * 128:DFF + (j + 1) * 128], h[:, k, :TW], k == 0, k == 7,
                                  ['wupbf', 'fuh'], [pvk])
                        sg_ = stg[j % 2]
                        sv_ = stv[j % 2]
                        em.cp('dve', sg_[:, :TW], pg[:, :TW], [pgk], [f"fustg{j % 2}"])
                        em.cp('act', sv_[:, :TW], pv_[:, :TW], [pvk], [f"fustv{j % 2}"])
                        em.dma('pool', GATE[s][j * 128:(j + 1) * 128, t0:t0 + TW], sg_[:, :TW], reads=[f"fustg{j % 2}"],
                               writes=[f"gate{s}"])
                        em.dma('sp', VAL[s][j * 128:(j + 1) * 128, t0:t0 + TW], sv_[:, :TW], reads=[f"fustv{j % 2}"],
                               writes=[f"val{s}"])
            em.barrier()

    def stage_ffn_conv(l, seqs):
        with ExitStack() as es:
            def S(name, shape, dt=F32):
                return es.enter_context(nc.sbuf_tensor(name, list(shape), dt)).ap()
            gflat = [S(f"fcg{l}_{i}", [128, 2048]) for i in range(2)]
            vflat = [S(f"fcv{l}_{i}", [128, 2048], BF16) for i in range(2)]
            gpx = S(f"fcgpx{l}", [128, 34, 66])
            gpc = S(f"fcgpc{l}", [128, 3, 258])
            acc = S(f"fcacc{l}", [128, 2048])
            acc2 = S(f"fcacc2{l}", [128, 2048])
            u = S(f"fcu{l}", [128, 2048])
            ab = [S(f"fcab{l}_{i}", [128, 2048], BF16) for i in range(2)]
            em.memset('pool', gpx, 0.0, ['fcgpx'])
            em.memset('pool', gpc, 0.0, ['fcgpc'])
            it = 0
            for s in seqs:
                T = seqT(s)
                if s % 2 == 1:
                    R, Cc, gp, gpk = 32, 64, gpx, 'fcgpx'
                else:
                    R, Cc, gp, gpk = 1, 256, gpc, 'fcgpc'
                for j in range(NFF):
                    i2 = it % 2
                    it += 1
                    gf = gflat[i2]
                    vf = vflat[i2]
                    em.dma('sp', gf[:, :T], GATE[s][j * 128:(j + 1) * 128, :], reads=[f"gate{s}"], writes=[f"fcg{i2}"])
                    em.dma('sp', vf[:, :T], VAL[s][j * 128:(j + 1) * 128, :], reads=[f"val{s}"], writes=[f"fcv{i2}"])
                    em.cp('pool', gp[:, 1:1 + R, 1:1 + Cc], gf[:, :T].rearrange("p (r c) -> p r c", c=Cc), [f"fcg{i2}"], [gpk])
                    a3 = acc[:, :T].rearrange("p (r c) -> p r c", c=Cc)
                    for tap in range(9):
                        dr, dc = tap // 3 - 1, tap % 3 - 1
                        src_ = gp[:, 1 + dr:1 + dr + R, 1 + dc:1 + dc + Cc]
                        wcol = PV[:, PV_FCW + tap * NFF + j:PV_FCW + tap * NFF + j + 1]
                        if tap == 0:
                            em.ts('dve', a3, src_, wcol, PV[:, PV_FCB + j:PV_FCB + j + 1], ALU.mult, ALU.add,
                                  [gpk, 'PV'], ['fcacc'])
                        else:
                            em.stt(a3, src_, wcol, a3, ALU.mult, ALU.add, [gpk, 'PV', 'fcacc'], ['fcacc'])
                    em.act(u[:, :T], acc[:, :T], AF.Square, ['fcacc'], ['fcu'])
                    em.ts('dve', u[:, :T], u[:, :T], 0.044715, 1.0, ALU.mult, ALU.add, ['fcu'], ['fcu'])
                    em.tt('dve', u[:, :T], u[:, :T], acc[:, :T], ALU.mult, ['fcu', 'fcacc'], ['fcu'])
                    em.act(u[:, :T], u[:, :T], AF.Sigmoid, ['fcu'], ['fcu'], scale=GELU_C)
                    em.tt('dve', u[:, :T], u[:, :T], acc[:, :T], ALU.mult, ['fcu', 'fcacc'], ['fcu'])
                    em.tt('dve', ab[i2][:, :T], u[:, :T], vf[:, :T], ALU.mult, ['fcu', f"fcv{i2}"], [f"fcab{i2}"])
                    em.dma('pool', ACTV[s][j * 128:(j + 1) * 128, :], ab[i2][:, :T], reads=[f"fcab{i2}"], writes=[f"actv{s}"])
            em.barrier()

    allseq = list(range(NS))
    xseq = [s for s in range(NS) if s % 2 == 1]
    outkeys = []
    for l in range(n_layers):
        last = (l == n_layers - 1)
        stage_mod(l)
        if stop_after == 'mod':
            break
        stage_inproj(l)
        if stop_after == 'inproj':
            break
        stage_mixer(l, need_ctx_out=not last)
        if stop_after == 'mixer':
            break
        seqs = xseq if last else allseq
        stage_proj_post(l, 0, w_out[l], 8, MIX, "mix", 1, seqs)
        if stop_after == 'outproj':
            break
        stage_ffn_up(l, seqs)
        stage_ffn_conv(l, seqs)
        stage_proj_post(l, 1, f_w_down[l], NFF, ACTV, "actv", 3, seqs)
    em.barrier()
    return nc, em


def host_prep(inp):
    f = np.float32
    idx = np.arange(128)
    cstn = np.zeros((128, NCST), f)
    cstn[:, C_ID:C_ID + 128] = np.eye(128)
    cstn[:, C_UTI:C_UTI + 128] = (idx[:, None] <= idx[None, :])
    cstn[:, C_LTI:C_LTI + 128] = (idx[:, None] >= idx[None, :])
    cstn[:, C_UTS:C_UTS + 128] = (idx[:, None] < idx[None, :])
    cstn[:, C_LTS:C_LTS + 128] = (idx[:, None] > idx[None, :])
    cstn[:, C_ONE:C_ONE + 128] = 1.0
    b32 = idx // 32
    b64 = idx // 64
    same32 = b32[:, None] == b32[None, :]
    same64 = b64[:, None] == b64[None, :]
    for d in range(2):
        strict = (idx[:, None] > idx[None, :]) if d == 0 else (idx[:, None] < idx[None, :])
        cstn[:, C_MP0 + d * 128:C_MP0 + (d + 1) * 128] = strict & same32
        cstn[:, C_ME1 + d * 128:C_ME1 + (d + 1) * 128] = strict & same64 & (~same32)
        cstn[:, C_ME2 + d * 128:C_ME2 + (d + 1) * 128] = strict & (~same64)
    pvn = np.zeros((L, 128, NPV), f)
    pv6 = np.zeros((L, 64, NPV64), f)
    rwn = np.zeros((L, 1, NROW), f)
    for l in range(L):
        pvn[l, :, PV_BMOD:PV_BMOD + 48] = inp['b_mod'][l].reshape(48, 128).T
        pvn[l, :, PV_GPRE1:PV_GPRE1 + 8] = inp['g_mix_pre'][l].reshape(8, 128).T
        pvn[l, :, PV_GPOST1:PV_GPOST1 + 8] = inp['g_mix_post'][l].reshape(8, 128).T
        pvn[l, :, PV_GPRE2:PV_GPRE2 + 8] = inp['g_ffn_pre'][l].reshape(8, 128).T
        pvn[l, :, PV_GPOST2:PV_GPOST2 + 8] = inp['g_ffn_post'][l].reshape(8, 128).T
        pvn[l, :, PV_MCW:PV_MCW + 24] = inp['m_conv_w'][l].reshape(3, 8, 128).transpose(2, 0, 1).reshape(128, 24)
        pvn[l, :, PV_MCB:PV_MCB + 8] = inp['m_conv_b'][l].reshape(8, 128).T
        pvn[l, :, PV_FCW:PV_FCW + 198] = inp['f_conv_w'][l].reshape(9, NFF, 128).transpose(2, 0, 1).reshape(128, 198)
        pvn[l, :, PV_FCB:PV_FCB + NFF] = inp['f_conv_b'][l].reshape(NFF, 128).T
        mu = inp['r_mu'][l]
        pvn[l, :, PV_MUXG0] = mu[1792:1920]
        pvn[l, 0:32, PV_MUXG1] = mu[1920:1952]
        pv6[l, :, P6_MURKV:P6_MURKV + 24] = mu[0:1536].reshape(24, 64).T
        pv6[l, :, P6_MUWA:P6_MUWA + 4] = mu[1536:1792].reshape(4, 64).T
        pv6[l, :, P6_W0:P6_W0 + 16] = inp['r_w0'][l].reshape(2, 8, 64).transpose(2, 0, 1).reshape(64, 16)
        pv6[l, :, P6_A0:P6_A0 + 16] = inp['r_a0'][l].reshape(2, 8, 64).transpose(2, 0, 1).reshape(64, 16)
        pv6[l, :, P6_KK:P6_KK + 8] = inp['r_k_k'][l].reshape(8, 64).T
        pv6[l, :, P6_KA:P6_KA + 8] = inp['r_k_a'][l].reshape(8, 64).T
        pv6[l, :, P6_RK:P6_RK + 8] = inp['r_r_k'][l].T
        rwn[l, 0, RV_MNW:RV_MNW + 512] = inp['m_norm_w'][l]
        rwn[l, 0, RV_LNW:RV_LNW + 512] = inp['r_ln_w'][l]
        rwn[l, 0, RV_LNB:RV_LNB + 512] = inp['r_ln_b'][l]
        rwn[l, 0, RV_MD:RV_MD + 8] = inp['m_d'][l]
        rwn[l, 0, RV_DTB:RV_DTB + 16] = inp['m_dt_bias'][l].reshape(16)
        rwn[l, 0, RV_ALOG:RV_ALOG + 16] = inp['m_a_log'][l].reshape(16)
    return cstn, pvn, pv6, rwn


def make_in_maps(inp, cores):
    cstn, pvn, pv6, rwn = host_prep(inp)
    shared = {k: np.ascontiguousarray(np.asarray(inp[k], dtype=np.float32)) for k in
              ['w_mod', 'w_in', 'w_out', 'r_w2', 'r_a2', 'r_g2', 'f_w_up', 'f_w_down']}
    maps = []
    x = np.asarray(inp['x'], np.float32)
    ctx = np.asarray(inp['ctx'], np.float32)
    c = np.asarray(inp['c'], np.float32)
    cc = np.asarray(inp['c_ctx'], np.float32)
    for ci in cores:
        bs = [ci * NBL + i for i in range(NBL)]
        m = dict(shared)
        m['xT'] = np.ascontiguousarray(x[bs].transpose(0, 2, 1))
        m['ctxT'] = np.ascontiguousarray(ctx[bs].transpose(0, 2, 1))
        m['cT'] = np.ascontiguousarray(np.stack([c[bs[0]], c[bs[1]], cc], axis=1))
        m['cst'] = cstn
        m['pv'] = pvn
        m['pv64'] = pv6
        m['rowv'] = rwn
        maps.append(m)
    return maps


def kernel(**inputs):
    nc, em = build()
    cores = list(range(NCORE))
    maps = make_in_maps(inputs, cores)
    res = run_bass_kernel_spmd(nc, maps, core_ids=cores)
    out = np.empty((NCORE * NBL, TX, D), np.float32)
    for ci in cores:
        o = res.results[ci]["outT"]
        out[ci * NBL:(ci + 1) * NBL] = o.transpose(0, 2, 1)
    return out
```

```python
import numpy as np
from contextlib import ExitStack
import concourse.bass as bass
import concourse.mybir as mybir
from concourse.bass_utils import run_bass_kernel_spmd

F32 = mybir.dt.float32
BF16 = mybir.dt.bfloat16
AF = mybir.ActivationFunctionType
ALU = mybir.AluOpType
AX = mybir.AxisListType

L = 2
D = 1024
TX = 2048
TC = 256
NBL = 2
NCORE = 8
CH = 128
DFF = 2816
NFF = 22
EPS = 1e-6
R_LN_EPS = 64e-5
R_DECAY_SCALE = 0.6065306597126334
GELU_C = 1.5957691216057308

PV_BMOD = 0
PV_GPRE1 = 48
PV_GPOST1 = 56
PV_GPRE2 = 64
PV_GPOST2 = 72
PV_MCW = 80
PV_MCB = 104
PV_FCW = 112
PV_FCB = 310
PV_MUXG0 = 332
PV_MUXG1 = 333
NPV = 334
P6_MURKV = 0
P6_MUWA = 24
P6_W0 = 28
P6_A0 = 44
P6_KK = 60
P6_KA = 68
P6_RK = 76
NPV64 = 84
RV_MNW = 0
RV_LNW = 512
RV_LNB = 1024
RV_MD = 1536
RV_DTB = 1544
RV_ALOG = 1560
NROW = 1576
C_ID = 0
C_UTI = 128
C_LTI = 256
C_UTS = 384
C_LTS = 512
C_ONE = 640
C_MP0 = 768
C_ME1 = 1024
C_ME2 = 1280
NCST = 1536


class Em:
    def __init__(self, nc, ndma=8):
        self.nc = nc
        self.engs = {'pe': nc.tensor, 'act': nc.scalar, 'dve': nc.vector, 'pool': nc.gpsimd, 'sp': nc.sync}
        self.sem = {}
        self.cnt = {}
        for k in ['pe', 'act', 'dve', 'pool']:
            self.sem[k] = nc.alloc_semaphore("sem_" + k)
            self.cnt[k] = 0
        self.dq = {}
        for q in ['sp', 'pool', 'act']:
            self.dq[q] = {'n': ndma, 'next': 0}
            for i in range(ndma):
                self.sem[f"d_{q}_{i}"] = nc.alloc_semaphore(f"dsem_{q}_{i}")
                self.cnt[f"d_{q}_{i}"] = 0
        self.seen = {k: {} for k in self.engs}
        self.lastw = {}
        self.readers = {}
        self.n = 0
        self.pend = {k: False for k in self.engs}
        self.stream = None
        self.queues = {}

    def _deps(self, reads, writes):
        deps = {}

        def add(d):
            if d is None:
                return
            k, v = d
            if deps.get(k, 0) < v:
                deps[k] = v
        for b in reads:
            add(self.lastw.get(b))
        for b in writes:
            add(self.lastw.get(b))
            for r in self.readers.get(b, ()):
                add(r)
        return deps

    def _waits(self, eng, deps):
        for k, v in deps.items():
            if k == 'pe' and eng == 'pe':
                continue
            if k.startswith('d_'):
                v = self.cnt[k]
            if self.seen[eng].get(k, 0) >= v:
                continue
            self.seen[eng][k] = v
            self.engs[eng].wait_ge(self.sem[k], v)
            self.n += 1

    def _mark(self, me, reads, writes):
        for b in reads:
            self.readers.setdefault(b, []).append(me)
        for b in writes:
            self.lastw[b] = me
            self.readers[b] = []

    @staticmethod
    def _is_psum(k):
        return (k[0] == 'B' and (k[1:].isdigit() or k in ('BB', 'BBa', 'BBb'))) or k.startswith('ps')

    def flush(self):
        qs = {k: v for k, v in self.queues.items() if v}
        self.queues = {}
        pos = {k: 0 for k in qs}
        while qs:
            k = min(qs, key=lambda n: pos[n] / len(qs[n]))
            it = qs[k][pos[k]]
            pos[k] += 1
            if it[0] == 'op':
                self.op(*it[1:])
            else:
                self.dma(it[1], it[2], it[3], it[4], it[5], **it[6])
            if pos[k] >= len(qs[k]):
                del qs[k]

    @staticmethod
    def _merge(lists):
        lists = [l for l in lists if l]
        out = []
        pos = [0] * len(lists)
        live = list(range(len(lists)))
        sticky = None
        while live:
            k = sticky if sticky is not None else min(live, key=lambda n: pos[n] / len(lists[n]))
            it = lists[k][pos[k]]
            out.append(it)
            pos[k] += 1
            if it[0] == 'op' and it[1] == 'pe':
                sticky = None if it[5] else k
            if pos[k] >= len(lists[k]):
                live.remove(k)
                sticky = None
        return out

    def flush_mixer(self, nw):
        qs = self.queues
        self.queues = {}
        wk = list(qs.get(('rp', 0), []))
        for i in range(nw):
            wk += self._merge([qs.get(('rs', i), []), qs.get(('rp', i + 1), [])])
        allops = self._merge([wk, qs.get('m', [])])
        for it in allops:
            if it[0] == 'op':
                self.op(*it[1:])
            else:
                self.dma(it[1], it[2], it[3], it[4], it[5], **it[6])

    def op(self, eng, fn, reads=(), writes=(), inc=True):
        if self.stream is not None:
            self.queues.setdefault(self.stream, []).append(('op', eng, fn, tuple(reads), tuple(writes), inc))
            return
        ex = [k for k in reads if self._is_psum(k)]
        self._waits(eng, self._deps(reads, list(writes) + ex))
        if inc:
            self.cnt[eng] += 1
            fn(self.engs[eng]).then_inc(self.sem[eng], 1)
            self._mark((eng, self.cnt[eng]), reads, writes)
            self.pend[eng] = False
        else:
            fn(self.engs[eng])
            self._mark((eng, self.cnt[eng] + 1), reads, writes)
            self.pend[eng] = True
        self.n += 1

    def dma(self, q, out, in_, reads=(), writes=(), **kw):
        if self.stream is not None:
            self.queues.setdefault(self.stream, []).append(('dma', q, out, in_, tuple(reads), tuple(writes), kw))
            return
        self._waits(q, self._deps(reads, writes))
        d = self.dq[q]
        i = d['next']
        d['next'] = (i + 1) % d['n']
        k = f"d_{q}_{i}"
        self.cnt[k] += 16
        self.engs[q].dma_start(out=out, in_=in_, **kw).then_inc(self.sem[k], 16)
        self._mark((k, self.cnt[k]), reads, writes)
        self.n += 1

    def barrier(self):
        assert not any(self.pend.values()), self.pend
        allv = {k: v for k, v in self.cnt.items() if v > 0}
        for e in self.engs:
            self._waits(e, dict(allv))

    def act(self, out, in_, func, r, w, bias=None, scale=None, accum=None):
        kw = {}
        if bias is not None:
            kw['bias'] = bias
        if scale is not None:
            kw['scale'] = scale
        if accum is not None:
            kw['accum_out'] = accum
        self.op('act', lambda e: e.activation(out=out, in_=in_, func=func, **kw), r, w)

    def tt(self, eng, out, a, b, op, r, w):
        self.op(eng, lambda e: e.tensor_tensor(out=out, in0=a, in1=b, op=op), r, w)

    def ts(self, eng, out, a, s1, s2, op0, op1, r, w):
        if s2 is None:
            self.op(eng, lambda e: e.tensor_scalar(out=out, in0=a, scalar1=s1, scalar2=None, op0=op0), r, w)
        else:
            self.op(eng, lambda e: e.tensor_scalar(out=out, in0=a, scalar1=s1, scalar2=s2, op0=op0, op1=op1), r, w)

    def stt(self, out, a, s, b, op0, op1, r, w):
        self.op('dve', lambda e: e.scalar_tensor_tensor(out=out, in0=a, scalar=s, in1=b, op0=op0, op1=op1), r, w)

    def mm(self, out, lhsT, rhs, start, stop, r, w, inc=None):
        self.op('pe', lambda e: e.matmul(out, lhsT=lhsT, rhs=rhs, start=start, stop=stop), r, w,
                inc=(stop if inc is None else inc))

    def tr(self, out, in_, ident, r, w, inc=True):
        self.op('pe', lambda e: e.transpose(out, in_, ident), r, w, inc=inc)

    def cp(self, eng, out, in_, r, w):
        if eng == 'act':
            self.op('act', lambda e: e.activation(out=out, in_=in_, func=AF.Identity), r, w)
        else:
            self.op(eng, lambda e: e.tensor_copy(out=out, in_=in_), r, w)

    def memset(self, eng, ap, val, w):
        self.op(eng, lambda e: e.memset(ap, val), (), w)


def seqT(s):
    return TX if (s % 2) == 1 else TC


def build(debug=False, n_layers=L, stop_after=None, mixcfg=None):
    nc = bass.Bass("TRN2", target_bir_lowering=False)
    em = Em(nc)
    dbgset = debug if isinstance(debug, (set, list, tuple)) else None

    def din(name, shape, dt=F32):
        return nc.dram_tensor(name, list(shape), dt, kind="ExternalInput").ap()

    def dscr(name, shape, dt=F32):
        isdbg = (debug is True) or (dbgset is not None and name.rstrip('0123456789') in dbgset)
        return nc.dram_tensor(name, list(shape), dt, kind="ExternalOutput" if isdbg else "Internal").ap()

    xT = din("xT", [NBL, D, TX])
    ctxT = din("ctxT", [NBL, D, TC])
    cT = din("cT", [D, 3])
    w_mod = din("w_mod", [L, D, 6 * D])
    w_in = din("w_in", [L, D, 3504])
    w_out = din("w_out", [L, D, D])
    r_w2 = din("r_w2", [L, 2, 64, 512])
    r_a2 = din("r_a2", [L, 2, 64, 512])
    r_g2 = din("r_g2", [L, 160, 512])
    f_w_up = din("f_w_up", [L, D, 2 * DFF])
    f_w_down = din("f_w_down", [L, DFF, D])
    cst = din("cst", [128, NCST])
    pv = din("pv", [L, 128, NPV])
    pv64 = din("pv64", [L, 64, NPV64])
    rowv = din("rowv", [L, 1, NROW])
    outT = nc.dram_tensor("outT", [NBL, D, TX], F32, kind="ExternalOutput").ap()

    NS = 2 * NBL
    RESA = [dscr(f"resa{s}", [D, seqT(s)]) for s in range(NS)]
    RESB = [dscr(f"resb{s}", [D, seqT(s)]) for s in range(NS)]
    XBC = [dscr(f"xbc{s}", [D, seqT(s)]) for s in range(NS)]
    RKV = [dscr(f"rkv{s}", [24, 64, seqT(s)]) for s in range(NS)]
    XWA = [dscr(f"xwa{s}", [4, 64, seqT(s)]) for s in range(NS)]
    XG = [dscr(f"xg{s}", [160, seqT(s)]) for s in range(NS)]
    ZDT = [dscr(f"zdt{s}", [seqT(s), 528]) for s in range(NS)]
    YF = [dscr(f"yf{s}", [seqT(s), 512]) for s in range(NS)]
    OF = [dscr(f"of{s}", [seqT(s), 520]) for s in range(NS)]
    MIX = [dscr(f"mix{s}", [D, seqT(s)], BF16) for s in range(NS)]
    GATE = [dscr(f"gate{s}", [DFF, seqT(s)]) for s in range(NS)]
    VAL = [dscr(f"val{s}", [DFF, seqT(s)], BF16) for s in range(NS)]
    ACTV = [dscr(f"actv{s}", [DFF, seqT(s)], BF16) for s in range(NS)]

    def fm(ap):
        return ap.rearrange("(k p) t -> p k t", p=128)

    def sb(name, shape, dt=F32):
        return nc.alloc_sbuf_tensor(name, list(shape), dt).ap()

    CST = sb("CST", [128, NCST])
    CSTB = sb("CSTB", [128, 768], BF16)
    PV = sb("PV", [128, NPV])
    PV64 = sb("PV64", [64, NPV64])
    ROWB = sb("ROWB", [128, NROW])
    MOD = sb("MOD", [128, 48, 3])
    DER = sb("DER", [128, 4, 8, 3])
    NEGA = sb("NEGA", [128, 16])
    OMMU = sb("OMMU", [64, 8])
    em.dma('sp', CST, cst, writes=['CST'])
    em.cp('dve', CSTB, CST[:, 0:768], ['CST'], ['CSTB'])
    ident = CST[:, C_ID:C_ID + 128]
    identb = CSTB[:, C_ID:C_ID + 128]
    onesb = CSTB[:, C_ONE:C_ONE + 128]
    ones = CST[:, C_ONE:C_ONE + 128]

    def stage_scope():
        return ExitStack()

    def stage_mod(l):
        em.dma('sp', PV, pv[l], writes=['PV'])
        em.dma('sp', PV64, pv64[l], writes=['PV64'])
        em.dma('pool', ROWB, rowv[l].partition_broadcast(128), writes=['ROWB'])
        with ExitStack() as es:
            def S(name, shape, dt=F32):
                return es.enter_context(nc.sbuf_tensor(name, list(shape), dt)).ap()
            cts = S(f"cts{l}", [128, 8, 3])
            sc = S(f"sc{l}", [128, 8, 3])
            wst = [S(f"wmst{l}_{i}", [128, 8, 512]) for i in range(2)]
            ps = es.enter_context(nc.psum_tensor(f"psmod{l}", [128, 512], F32)).ap()
            em.dma('sp', cts, cT.rearrange("(k p) j -> p k j", p=128), writes=['cts'])
            em.act(sc, cts, AF.Silu, ['cts'], ['sc'])
            for g in range(12):
                w = wst[g % 2]
                wk = f"wmst{g % 2}"
                em.dma('sp' if g % 2 == 0 else 'pool', w,
                       w_mod[l][:, g * 512:(g + 1) * 512].rearrange("(k p) n -> p k n", p=128), writes=[wk])
                for mi in range(4):
                    m = g * 4 + mi
                    for k in range(8):
                        em.mm(ps[:, m * 3:(m + 1) * 3], w[:, k, mi * 128:(mi + 1) * 128], sc[:, k, :],
                              k == 0, k == 7, [wk, 'sc'], ['psmod'])
            em.tt('dve', MOD, ps[:, 0:144].rearrange("p (m j) -> p m j", j=3),
                  PV[:, PV_BMOD:PV_BMOD + 48].unsqueeze(2).to_broadcast([128, 48, 3]), ALU.add,
                  ['psmod', 'PV'], ['MOD'])
            tmp = S(f"dertmp{l}", [128, 8, 3])

            def gain(idx, goff, mlo, plus1):
                if plus1:
                    em.ts('dve', tmp, MOD[:, mlo:mlo + 8, :], 1.0, None, ALU.add, None, ['MOD'], ['dertmp'])
                    src = tmp
                    rk = ['dertmp', 'PV']
                else:
                    src = MOD[:, mlo:mlo + 8, :]
                    rk = ['MOD', 'PV']
                em.tt('dve', DER[:, idx, :, :], src,
                      PV[:, goff:goff + 8].unsqueeze(2).to_broadcast([128, 8, 3]), ALU.mult, rk, ['DER'])
            gain(0, PV_GPRE1, 8, True)
            gain(1, PV_GPOST1, 16, False)
            gain(2, PV_GPRE2, 32, True)
            gain(3, PV_GPOST2, 40, False)
            em.act(NEGA, ROWB[:, RV_ALOG:RV_ALOG + 16], AF.Exp, ['ROWB'], ['NEGA'])
            em.ts('dve', NEGA, NEGA, -1.0, None, ALU.mult, None, ['NEGA'], ['NEGA'])
            em.ts('dve', OMMU, PV64[:, P6_KA:P6_KA + 8], -1.0, 1.0, ALU.mult, ALU.add, ['PV64'], ['OMMU'])
            em.barrier()

    def prenorm(xt, TW, h, sq, rstd, ps, jmod, gidx, sidx, kx, kh, kps):
        em.act(sq[:, :, :TW], xt[:, :, :TW], AF.Square, [kx], ['sq'])
        for k in range(8):
            em.mm(ps[:, :TW], onesb, sq[:, k, :TW], k == 0, k == 7, ['sq', 'CSTB'], [kps])
        em.act(rstd[:, :TW], ps[:, :TW], AF.Sqrt, [kps], ['rstd'], bias=EPS, scale=1.0 / D)
        em.op('dve', lambda e: e.reciprocal(out=rstd[:, :TW], in_=rstd[:, :TW]), ['rstd'], ['rstd'])
        for k in range(8):
            em.stt(xt[:, k, :TW], xt[:, k, :TW], DER[:, gidx, k, jmod:jmod + 1], rstd[:, :TW], ALU.mult, ALU.mult,
                   [kx, 'rstd', 'DER'], [kx])
            em.act(h[:, k, :TW], xt[:, k, :TW], AF.Identity, [kx, 'MOD'], [kh],
                   bias=MOD[:, sidx + k, jmod:jmod + 1], scale=1.0)

    def load_wbf(es, l, wap, Kc, N, name, piece=None):
        wb = es.enter_context(nc.sbuf_tensor(f"{name}bf{l}", [128, Kc, N], BF16)).ap()
        with ExitStack() as e2:
            sts = [e2.enter_context(nc.sbuf_tensor(f"{name}st{l}_{i}", [128, N], F32)).ap() for i in range(2)]
            for k in range(Kc):
                st = sts[k % 2]
                sk = f"{name}st{k % 2}"
                em.dma('sp' if k % 2 == 0 else 'pool', st, wap[k * 128:(k + 1) * 128, :], writes=[sk])
                em.cp('act' if k % 2 == 0 else 'dve', wb[:, k, :], st, [sk], [name + 'bf'])
            em.barrier()
        return wb

    def res_src(l, s, phase):
        b = s // 2
        if phase == 0:
            if l == 0:
                return (xT[b] if s % 2 == 1 else ctxT[b]), f"in{s}"
            return RESB[s], f"resb{s}"
        return RESA[s], f"resa{s}"

    def res_dst(l, s, phase):
        b = s // 2
        if phase == 0:
            return RESA[s], f"resa{s}"
        if l == n_layers - 1 and s % 2 == 1:
            return outT[b], f"out{s}"
        return RESB[s], f"resb{s}"

    def stage_inproj(l):
        with ExitStack() as es:
            def S(name, shape, dt=F32):
                return es.enter_context(nc.sbuf_tensor(name, list(shape), dt)).ap()
            wb = load_wbf(es, l, w_in[l], 8, 3504, "win")
            xt = S(f"ipx{l}", [128, 8, 512])
            h = S(f"iph{l}", [128, 8, 512], BF16)
            sq = S(f"ipsq{l}", [128, 8, 512], BF16)
            rstd = S(f"iprs{l}", [128, 512])
            sta = [S(f"ipsta{l}_{i}", [128, 8, 512]) for i in range(2)]
            stw = S(f"ipstw{l}", [64, 4, 512])
            stg0 = S(f"ipstg0{l}", [128, 512])
            stg1 = S(f"ipstg1{l}", [32, 512])
            stz = S(f"ipstz{l}", [128, 4, 528])
            pss = [es.enter_context(nc.psum_tensor(f"ipps{l}_{i}", [128, 512], F32)).ap() for i in range(8)]
            groups = [('xbc', 512, 128, 8), ('r', 1552, 64, 8), ('k', 2064, 64, 8), ('v', 2576, 64, 8)]
            ev = 0
            for s in range(NS):
                T = seqT(s)
                TW = min(512, T)
                jmod = 2 if s % 2 == 0 else s // 2
                src, srck = res_src(l, s, 0)
                for tt_ in range(T // TW):
                    t0 = tt_ * TW
                    em.dma('sp', xt[:, :, :TW], fm(src)[:, :, t0:t0 + TW], reads=[srck], writes=['ipx'])
                    prenorm(xt, TW, h, sq, rstd, pss[7], jmod, 0, 0, 'ipx', 'iph', 'ps7')
                    pi = 0
                    for gi, (gname, c0, wdt, nb) in enumerate(groups):
                        st = sta[gi % 2]
                        stk = f"ipsta{gi % 2}"
                        for j in range(nb):
                            ps = pss[pi % 6]
                            pk = f"ps{pi % 6}"
                            pi += 1
                            cc = c0 + j * wdt
                            for k in range(8):
                                em.mm(ps[:wdt, :TW], wb[:, k, cc:cc + wdt], h[:, k, :TW], k == 0, k == 7,
                                      ['winbf', 'iph'], [pk])
                            em.cp('act' if ev % 2 == 0 else 'dve', st[:wdt, j, :TW], ps[:wdt, :TW], [pk], [stk])
                            ev += 1
                        if gname == 'xbc':
                            em.dma('pool', fm(XBC[s])[:, :, t0:t0 + TW], st[:, :, :TW], reads=[stk], writes=[f"xbc{s}"])
                        else:
                            jb = {'r': 0, 'k': 8, 'v': 16}[gname]
                            em.dma('pool', RKV[s][jb:jb + 8].rearrange("j p t -> p j t")[:, :, t0:t0 + TW],
                                   st[:64, :, :TW], reads=[stk], writes=[f"rkv{s}"])
                    for j in range(4):
                        ps = pss[pi % 6]
                        pk = f"ps{pi % 6}"
                        pi += 1
                        cc = 3088 + j * 64
                        for k in range(8):
                            em.mm(ps[:64, :TW], wb[:, k, cc:cc + 64], h[:, k, :TW], k == 0, k == 7, ['winbf', 'iph'], [pk])
                        em.cp('act' if ev % 2 == 0 else 'dve', stw[:, j, :TW], ps[:64, :TW], [pk], ['ipstw'])
                        ev += 1
                    em.dma('pool', XWA[s].rearrange("j p t -> p j t")[:, :, t0:t0 + TW], stw[:, :, :TW],
                           reads=['ipstw'], writes=[f"xwa{s}"])
                    for (cc, wdt, st, stk, r0) in [(3344, 128, stg0, 'ipstg0', 0), (3472, 32, stg1, 'ipstg1', 128)]:
                        ps = pss[pi % 6]
                        pk = f"ps{pi % 6}"
                        pi += 1
                        for k in range(8):
                            em.mm(ps[:wdt, :TW], wb[:, k, cc:cc + wdt], h[:, k, :TW], k == 0, k == 7, ['winbf', 'iph'], [pk])
                        em.cp('act' if ev % 2 == 0 else 'dve', st[:wdt, :TW], ps[:wdt, :TW], [pk], [stk])
                        ev += 1
                        em.dma('pool', XG[s][r0:r0 + wdt, t0:t0 + TW], st[:wdt, :TW], reads=[stk], writes=[f"xg{s}"])
                    for i in range(TW // 128):
                        ps = pss[pi % 6]
                        pk = f"ps{pi % 6}"
                        pi += 1
                        ps2 = pss[6]
                        for k in range(8):
                            em.mm(ps[:, 0:512], h[:, k, i * 128:(i + 1) * 128], wb[:, k, 0:512], k == 0, k == 7,
                                  ['winbf', 'iph'], [pk])
                        for k in range(8):
                            em.mm(ps2[:, 0:16], h[:, k, i * 128:(i + 1) * 128], wb[:, k, 1536:1552], k == 0, k == 7,
                                  ['winbf', 'iph'], ['ps6'])
                        em.cp('act', stz[:, i, 0:512], ps[:, 0:512], [pk], ['ipstz'])
                        em.cp('dve', stz[:, i, 512:528], ps2[:, 0:16], ['ps6'], ['ipstz'])
                    em.dma('pool', ZDT[s][t0:t0 + TW, :].rearrange("(i p) c -> p i c", p=128), stz[:, :TW // 128, :],
                           reads=['ipstz'], writes=[f"zdt{s}"])
            em.barrier()

    def stage_mixer(l, need_ctx_out):
        with ExitStack() as es:
            def S(name, shape, dt=F32):
                return es.enter_context(nc.sbuf_tensor(name, list(shape), dt)).ap()
            w2b = S(f"w2b{l}", [64, 2, 512], BF16)
            a2b = S(f"a2b{l}", [64, 2, 512], BF16)
            g2b0 = S(f"g2b0{l}", [128, 512], BF16)
            g2b1 = S(f"g2b1{l}", [32, 512], BF16)
            with ExitStack() as e2:
                t1 = e2.enter_context(nc.sbuf_tensor(f"lst1{l}", [64, 2, 512], F32)).ap()
                t2 = e2.enter_context(nc.sbuf_tensor(f"lst2{l}", [64, 2, 512], F32)).ap()
                t3 = e2.enter_context(nc.sbuf_tensor(f"lst3{l}", [128, 512], F32)).ap()
                t4 = e2.enter_context(nc.sbuf_tensor(f"lst4{l}", [32, 512], F32)).ap()
                em.dma('sp', t1, r_w2[l].rearrange("d r c -> r d c"), writes=['lst1'])
                em.dma('sp', t2, r_a2[l].rearrange("d r c -> r d c"), writes=['lst2'])
                em.dma('sp', t3, r_g2[l][0:128, :], writes=['lst3'])
                em.dma('sp', t4, r_g2[l][128:160, :], writes=['lst4'])
                em.cp('dve', w2b, t1, ['lst1'], ['w2b'])
                em.cp('dve', a2b, t2, ['lst2'], ['a2b'])
                em.cp('dve', g2b0, t3, ['lst3'], ['g2b'])
                em.cp('dve', g2b1, t4, ['lst4'], ['g2b'])
                em.barrier()
            MAR = [None, None]
            MARt = S(f"mar{l}", [128, 2, 256])
            em.cp('dve', MARt[:, 0, 0:128], CST[:, C_UTS:C_UTS + 128], ['CST'], ['MAR'])
            em.cp('dve', MARt[:, 0, 128:256], CST[:, C_UTI:C_UTI + 128], ['CST'], ['MAR'])
            em.cp('dve', MARt[:, 1, 0:128], CST[:, C_LTS:C_LTS + 128], ['CST'], ['MAR'])
            em.cp('dve', MARt[:, 1, 128:256], CST[:, C_LTI:C_LTI + 128], ['CST'], ['MAR'])
            MSO = S(f"mso{l}", [128, 2, 256])
            em.cp('dve', MSO[:, 0, 0:128], CST[:, C_LTS:C_LTS + 128], ['CST'], ['MSO'])
            em.cp('dve', MSO[:, 1, 0:128], CST[:, C_UTS:C_UTS + 128], ['CST'], ['MSO'])
            em.cp('dve', MSO[:, 0, 128:256], ones, ['CST'], ['MSO'])
            em.cp('dve', MSO[:, 1, 128:256], ones, ['CST'], ['MSO'])
            RMK = S(f"rmk{l}", [64, 8, 128], BF16)
            em.memset('pool', RMK, 1.0, ['RMK'])
            em.memset('pool', RMK[:, :, 0:1], 0.0, ['RMK'])

            def MI(d):
                return CST[:, C_UTI:C_UTI + 128] if d == 0 else CST[:, C_LTI:C_LTI + 128]

            def strictTS(d):
                return CST[:, C_LTS:C_LTS + 128] if d == 0 else CST[:, C_UTS:C_UTS + 128]

            xbc = S(f"m_xbc{l}", [128, 8, 130])
            cacc = S(f"m_cacc{l}", [128, 8, 128])
            bct = S(f"m_bct{l}", [128, 4, 128], BF16)
            xtok = S(f"m_xtok{l}", [128, 512])
            xtokb = S(f"m_xtokb{l}", [128, 512], BF16)
            btokb = S(f"m_btokb{l}", [128, 256], BF16)
            zdt = S(f"m_zdt{l}", [128, 528])
            dts = S(f"m_dts{l}", [128, 8])
            dta = S(f"m_dta{l}", [128, 8])
            sm = S(f"m_sm{l}", [128, 40])
            xw = S(f"m_xw{l}", [128, 512], BF16)
            l2 = [S(f"m_l2{l}_{i}", [128, 256]) for i in range(2)]
            Et = [S(f"m_E{l}_{i}", [128, 256]) for i in range(2)]
            LTt = [S(f"m_LT{l}_{i}", [128, 128]) for i in range(2)]
            STt = [S(f"m_ST{l}_{i}", [128, 128], BF16) for i in range(2)]
            CsT = [S(f"m_Cs{l}_{i}", [128, 128], BF16) for i in range(2)]
            hst = S(f"m_hst{l}", [128, 512])
            hstb = S(f"m_hstb{l}", [128, 512], BF16)
            yt = S(f"m_y{l}", [128, 512])
            yf = S(f"m_yf{l}", [128, 512])
            zs = S(f"m_zs{l}", [128, 512])
            ysq = S(f"m_ysq{l}", [128, 512])
            gst = S(f"m_gst{l}", [128, 4])
            mixo = S(f"mixo{l}", [128, 8, 128], BF16)
            raw = S(f"r_raw{l}", [64, 26, 130])
            ssum = S(f"r_ssum{l}", [64, 26, 128])
            pp = ssum
            xg0 = S(f"r_xg0{l}", [128, 130])
            xg1 = S(f"r_xg1{l}", [32, 130])
            sg0 = S(f"r_sg0{l}", [128, 128], BF16)
            sg1 = S(f"r_sg1{l}", [32, 128], BF16)
            xgt = S(f"r_xgt{l}", [128, 128])
            twb = S(f"r_twb{l}", [64, 2, 128], BF16)
            lw = S(f"r_lw{l}", [64, 8, 128])
            aa = S(f"r_aa{l}", [64, 8, 128])
            kk = S(f"r_kk{l}", [64, 8, 128])
            kd = S(f"r_kd{l}", [64, 8, 128])
            linc = S(f"r_linc{l}", [64, 8, 128])
            lex = S(f"r_lex{l}", [64, 8, 128])
            rinv = lex
            e1 = S(f"r_e1{l}", [64, 8, 128])
            e0 = S(f"r_e0{l}", [64, 8, 128])
            ei = S(f"r_ei{l}", [64, 8, 128])
            gCs = [S(f"r_gC{l}_{i}", [64, 8]) for i in range(2)]
            coefs = [S(f"r_coef{l}_{i}", [128, 8]) for i in range(2)]
            tmpk = S(f"r_tmpk{l}", [64, 8, 128])
            ARs = [S(f"r_AR{l}_{i}", [64, 8, 256], BF16) for i in range(2)]
            BKs = [S(f"r_BK{l}_{i}", [64, 8, 2, 128], BF16) for i in range(2)]
            prodb = S(f"r_prod{l}", [64, 8, 128], BF16)
            sqb = prodb
            vtokbs = [S(f"r_vtokb{l}_{i}", [128, 512], BF16) for i in range(2)]
            BKtoks = [S(f"r_BKtok{l}_{i}", [128, 8, 2, 64], BF16) for i in range(2)]
            GBs = [S(f"r_GB{l}_{i}", [128, 8, 256], BF16) for i in range(2)]
            GKs = [S(f"r_GK{l}_{i}", [128, 8, 256], BF16) for i in range(2)]
            Q0s = [S(f"r_Q0{l}_{i}", [128, 8, 128], BF16) for i in range(2)]
            Pm = [S(f"r_P{l}_{i}", [128, 8, 128], BF16) for i in range(2)]
            Qm = [S(f"r_Q{l}_{i}", [128, 8, 128], BF16) for i in range(2)]
            Ym = [S(f"r_Y{l}_{i}", [128, 8, 128], BF16) for i in range(2)]
            E1m = S(f"r_E1{l}", [128, 8, 128], BF16)
            E2m = S(f"r_E2{l}", [128, 8, 128], BF16)
            Dm = S(f"r_D{l}", [128, 8, 128], BF16)
            Zm = S(f"r_Z{l}", [128, 8, 128], BF16)
            Wt = S(f"r_W{l}", [128, 512], BF16)
            Ut = S(f"r_U{l}", [128, 512], BF16)
            Hs = S(f"r_H{l}", [64, 512])
            Hb = S(f"r_Hb{l}", [64, 512], BF16)
            ot = S(f"r_o{l}", [128, 520])
            oft = S(f"r_of{l}", [128, 520])
            osq = S(f"r_osq{l}", [128, 512])
            gn = S(f"r_gn{l}", [128, 40])
            B = [es.enter_context(nc.psum_tensor(f"mxps{l}_{i}", [128, 512], F32)).ap() for i in range(7)]
            BBp = es.enter_context(nc.psum_tensor(f"mxpsb{l}", [128, 1024], BF16)).ap()

            nbw = ROWB[:, RV_MNW:RV_MNW + 512]
            lnw = ROWB[:, RV_LNW:RV_LNW + 512]
            lnb = ROWB[:, RV_LNB:RV_LNB + 512]
            mdb = ROWB[:, RV_MD:RV_MD + 8]
            evc = [0]

            def evq():
                evc[0] += 1
                return 'act' if evc[0] % 2 == 0 else 'dve'

            def load_halo(tile, tk, srcap, srck, t0, T, nblk_dims):
                lo = max(t0 - 1, 0)
                hi = min(t0 + 129, T)
                o = lo - (t0 - 1)
                if nblk_dims:
                    em.dma('sp', tile[:, :, o:o + hi - lo], srcap[:, :, lo:hi], reads=[srck], writes=[tk])
                    if t0 == 0:
                        em.memset('pool', tile[:, :, 0:1], 0.0, [tk])
                    if t0 + 128 == T:
                        em.memset('pool', tile[:, :, 129:130], 0.0, [tk])
                else:
                    em.dma('sp', tile[:, o:o + hi - lo], srcap[:, lo:hi], reads=[srck], writes=[tk])
                    if t0 == 0:
                        em.memset('pool', tile[:, 0:1], 0.0, [tk])
                    if t0 + 128 == T:
                        em.memset('pool', tile[:, 129:130], 0.0, [tk])

            def mamba_chunk(s, t0, d, emit_out):
                T = seqT(s)
                load_halo(xbc, 'm_xbc', fm(XBC[s]), f"xbc{s}", t0, T, True)
                em.dma('pool', zdt, ZDT[s][t0:t0 + 128, :], reads=[f"zdt{s}"], writes=['m_zdt'])
                if (mixcfg or {}).get('mstop', 99) <= 1:
                    return
                for j in range(8):
                    em.ts('dve', cacc[:, j, :], xbc[:, j, 1:129], PV[:, PV_MCW + 8 + j:PV_MCW + 9 + j],
                          PV[:, PV_MCB + j:PV_MCB + j + 1], ALU.mult, ALU.add, ['m_xbc', 'PV'], ['m_cacc'])
                    em.stt(cacc[:, j, :], xbc[:, j, 0:128], PV[:, PV_MCW + j:PV_MCW + j + 1], cacc[:, j, :],
                           ALU.mult, ALU.add, ['m_xbc', 'PV', 'm_cacc'], ['m_cacc'])
                    em.stt(cacc[:, j, :], xbc[:, j, 2:130], PV[:, PV_MCW + 16 + j:PV_MCW + 17 + j], cacc[:, j, :],
                           ALU.mult, ALU.add, ['m_xbc', 'PV', 'm_cacc'], ['m_cacc'])
                em.act(cacc[:, 0:6, :], cacc[:, 0:6, :], AF.Silu, ['m_cacc'], ['m_cacc'])
                em.act(bct[:, 2:4, :], cacc[:, 6:8, :], AF.Silu, ['m_cacc'], ['m_bct'])
                em.cp('dve', bct[:, 0:2, :], cacc[:, 4:6, :], ['m_cacc'], ['m_bct'])
                if (mixcfg or {}).get('mstop', 99) <= 2:
                    return
                msub = (mixcfg or {}).get('msub', 9)
                for j in range(4):
                    em.tr(B[4][:, j * 128:(j + 1) * 128], cacc[:, j, :], ident, ['m_cacc', 'CST'], ['B4'], inc=(j == 3))
                if msub >= 1:
                    for j in range(2):
                        em.tr(B[5][:, j * 128:(j + 1) * 128], cacc[:, 4 + j, :], ident, ['m_cacc', 'CST'], ['B5'])
                if msub >= 2 and msub != 33:
                    em.cp('dve', xtok, B[4], ['B4'], ['m_xtok'])
                if msub == 30:
                    em.cp('dve', xtokb, B[4], ['B4'], ['m_xtokb'])
                elif msub == 31:
                    em.cp('act', yt, B[4], ['B4'], ['m_y'])
                elif msub == 32:
                    em.cp('act', xtokb, xtok, ['m_xtok'], ['m_xtokb'])
                elif msub >= 3:
                    em.cp('act', xtokb, B[4], ['B4'], ['m_xtokb'])
                if msub >= 4:
                    em.cp('act', btokb, B[5][:, 0:256], ['B5'], ['m_btokb'])
                if (mixcfg or {}).get('mstop', 99) <= 3:
                    return
                em.tt('dve', dts, zdt[:, 512 + d * 8:520 + d * 8], ROWB[:, RV_DTB + d * 8:RV_DTB + d * 8 + 8], ALU.add,
                      ['m_zdt', 'ROWB'], ['m_dts'])
                em.act(dts, dts, AF.Exp, ['m_dts'], ['m_dts'])
                em.act(dts, dts, AF.Ln, ['m_dts'], ['m_dts'], bias=1.0, scale=1.0)
                em.tt('dve', dta, dts, NEGA[:, d * 8:d * 8 + 8], ALU.mult, ['m_dts', 'NEGA'], ['m_dta'])
                if (mixcfg or {}).get('mstop', 99) <= 4:
                    return
                em.mm(B[5][:, 256:264], MI(d), dta, True, True, ['CST', 'm_dta'], ['B5'])
                em.mm(B[5][:, 264:272], ones, dta, True, True, ['CST', 'm_dta'], ['B5'])
                em.cp('act', sm[:, 32:40], B[5][:, 264:272], ['B5'], ['m_sm'])
                em.tt('dve', sm[:, 24:32], sm[:, 32:40], B[5][:, 256:264], ALU.subtract, ['B5', 'm_sm'], ['m_sm'])
                em.act(sm[:, 0:8], sm[:, 24:32], AF.Exp, ['m_sm'], ['m_sm'])
                em.tt('dve', sm[:, 8:16], sm[:, 0:8], dts, ALU.mult, ['m_sm', 'm_dts'], ['m_sm'])
                em.act(sm[:, 16:24], B[5][:, 264:272], AF.Exp, ['B5'], ['m_sm'])
                em.tt('dve', xw.rearrange("p (h q) -> p h q", q=64), xtok.rearrange("p (h q) -> p h q", q=64),
                      sm[:, 8:16].unsqueeze(2).to_broadcast([128, 8, 64]), ALU.mult, ['m_xtok', 'm_sm'], ['m_xw'])
                if (mixcfg or {}).get('mstop', 99) <= 5:
                    return
                for g in range(2):
                    em.mm(B[6][:, g * 128:(g + 1) * 128], bct[:, g, :], bct[:, 2 + g, :], True, True, ['m_bct'], ['B6'])
                for h in range(8):
                    g = h // 4
                    i2 = h % 2
                    pe_ = B[6][:, 256:512] if i2 == 0 else B[5][:, 0:256]
                    pek = 'B6' if i2 == 0 else 'B5'
                    em.ts('dve', l2[i2], MSO[:, d, :], dta[:, h:h + 1], None, ALU.mult, None,
                          ['MSO', 'm_dta'], [f"m_l2{i2}"])
                    em.mm(pe_[:, 0:128], l2[i2][:, 0:128], MI(d), True, True, [f"m_l2{i2}", 'CST'], [pek])
                    em.mm(pe_[:, 128:256], l2[i2][:, 128:256], MI(d), True, True, [f"m_l2{i2}", 'CST'], [pek])
                    em.act(Et[i2], pe_[:, 0:256], AF.Exp, [pek], [f"m_E{i2}"])
                    em.stt(LTt[i2], Et[i2][:, 0:128], dts[:, h:h + 1], MI(d), ALU.mult, ALU.mult,
                           [f"m_E{i2}", 'm_dts', 'CST'], [f"m_LT{i2}"])
                    em.tt('dve', STt[i2], B[6][:, g * 128:(g + 1) * 128], LTt[i2], ALU.mult, ['B6', f"m_LT{i2}"],
                          [f"m_ST{i2}"])
                    em.tt('dve', CsT[i2], bct[:, 2 + g, :], Et[i2][:, 128:256], ALU.mult, ['m_bct', f"m_E{i2}"],
                          [f"m_Cs{i2}"])
                    em.mm(B[4][:, h * 64:(h + 1) * 64], STt[i2], xtokb[:, h * 64:(h + 1) * 64], True, False,
                          [f"m_ST{i2}", 'm_xtokb'], ['B4'])
                    em.mm(B[4][:, h * 64:(h + 1) * 64], CsT[i2], hstb[:, h * 64:(h + 1) * 64], False, True,
                          [f"m_Cs{i2}", 'm_hstb'], ['B4'])
                if (mixcfg or {}).get('mstop', 99) <= 6:
                    return
                for g in range(2):
                    em.mm(B[6][:, g * 256:(g + 1) * 256], btokb[:, g * 128:(g + 1) * 128], xw[:, g * 256:(g + 1) * 256],
                          True, True, ['m_btokb', 'm_xw'], ['B6'])
                em.tt('dve', hst.rearrange("p (h q) -> p h q", q=64), hst.rearrange("p (h q) -> p h q", q=64),
                      sm[:, 16:24].unsqueeze(2).to_broadcast([128, 8, 64]), ALU.mult, ['m_hst', 'm_sm', 'B4'], ['m_hst'])
                em.tt('dve', hst, hst, B[6], ALU.add, ['m_hst', 'B6'], ['m_hst'])
                em.cp('act', hstb, hst, ['m_hst', 'B4'], ['m_hstb'])
                if (mixcfg or {}).get('mstop', 99) <= 7:
                    return
                if d == 0:
                    if emit_out:
                        em.cp('act', yt, B[4], ['B4'], ['m_y'])
                        em.dma('pool', YF[s][t0:t0 + 128, :], yt, reads=['m_y'], writes=[f"yf{s}"])
                elif emit_out:
                    em.dma('sp', yf, YF[s][t0:t0 + 128, :], reads=[f"yf{s}"], writes=['m_yf'])
                    em.tt('dve', yt, B[4], yf, ALU.add, ['B4', 'm_yf'], ['m_y'])
                    em.tt('dve', yf.rearrange("p (h q) -> p h q", q=64), xtok.rearrange("p (h q) -> p h q", q=64),
                          mdb.unsqueeze(2).to_broadcast([128, 8, 64]), ALU.mult, ['m_xtok', 'ROWB', 'm_yf'], ['m_yf'])
                    em.tt('dve', yt, yt, yf, ALU.add, ['m_y', 'm_yf'], ['m_y'])
                    em.act(zs, zdt[:, 0:512], AF.Silu, ['m_zdt'], ['m_zs'])
                    em.tt('dve', yt, yt, zs, ALU.mult, ['m_y', 'm_zs'], ['m_y'])
                    for g in range(2):
                        em.act(ysq[:, g * 256:(g + 1) * 256], yt[:, g * 256:(g + 1) * 256], AF.Square, ['m_y'],
                               ['m_ysq', 'm_gst'], accum=gst[:, g:g + 1])
                    em.act(gst[:, 2:4], gst[:, 0:2], AF.Sqrt, ['m_gst'], ['m_gst'], bias=EPS, scale=1.0 / 256)
                    em.op('dve', lambda e: e.reciprocal(out=gst[:, 2:4], in_=gst[:, 2:4]), ['m_gst'], ['m_gst'])
                    for g in range(2):
                        em.stt(yt[:, g * 256:(g + 1) * 256], yt[:, g * 256:(g + 1) * 256], gst[:, 2 + g:3 + g],
                               nbw[:, g * 256:(g + 1) * 256], ALU.mult, ALU.mult, ['m_y', 'm_gst', 'ROWB'], ['m_y'])
                    for j in range(4):
                        em.tr(B[5][:, j * 128:(j + 1) * 128], yt[:, j * 128:(j + 1) * 128], ident, ['m_y', 'CST'], ['B5'], inc=(j == 3))
                    em.cp('act', mixo[:, 0:4, :], B[5].rearrange("p (j t) -> p j t", t=128), ['B5'], ['mixo_m'])
                    em.dma('pool', fm(MIX[s])[:, 0:4, t0:t0 + 128], mixo[:, 0:4, :], reads=['mixo_m'], writes=[f"mixm{s}"])

            def wkv_chunk(s, t0, d, emit_out, idx=0, first_of_d=False, split=True):
                T = seqT(s)
                pb = idx % 2
                AR, BK, GB, GK, Q0, BKtok, vtokb, gC, coef = (ARs[pb], BKs[pb], GBs[pb], GKs[pb], Q0s[pb], BKtoks[pb],
                                                             vtokbs[pb], gCs[pb], coefs[pb])
                kAR, kBK, kGB, kGK, kQ0, kBKtok, kvtokb, kgC, kcoef = [f"{n}{pb}" for n in
                                                                      ('r_AR', 'r_BK', 'r_GB', 'r_GK', 'r_Q0h', 'r_BKtok',
                                                                       'r_vtokb', 'r_gC', 'r_coef')]
                if split:
                    em.stream = ('rp', idx)
                fin = (d == 1 and emit_out)
                rk = f"rkv{s}"
                load_halo(raw[:, 0:24, :], 'r_raw', RKV[s].rearrange("j p t -> p j t"), rk, t0, T, True)
                load_halo(raw[:, 24:25, :], 'r_raw', XWA[s][d:d + 1].rearrange("j p t -> p j t"), f"xwa{s}", t0, T, True)
                load_halo(raw[:, 25:26, :], 'r_raw', XWA[s][2 + d:3 + d].rearrange("j p t -> p j t"), f"xwa{s}", t0, T, True)
                em.tt('dve', ssum, raw[:, :, 0:128], raw[:, :, 2:130], ALU.add, ['r_raw'], ['r_ssum'])
                em.stt(ssum, ssum, 0.5, raw[:, :, 1:129], ALU.mult, ALU.subtract, ['r_ssum', 'r_raw'], ['r_ssum'])
                em.tt('dve', ssum[:, 0:24, :], ssum[:, 0:24, :],
                      PV64[:, P6_MURKV:P6_MURKV + 24].unsqueeze(2).to_broadcast([64, 24, 128]), ALU.mult,
                      ['r_ssum', 'PV64'], ['r_ssum'])
                em.ts('dve', ssum[:, 24, :], ssum[:, 24, :], PV64[:, P6_MUWA + d:P6_MUWA + d + 1], None, ALU.mult, None,
                      ['r_ssum', 'PV64'], ['r_ssum'])
                em.ts('dve', ssum[:, 25, :], ssum[:, 25, :], PV64[:, P6_MUWA + 2 + d:P6_MUWA + 3 + d], None, ALU.mult, None,
                      ['r_ssum', 'PV64'], ['r_ssum'])
                em.tt('dve', pp, raw[:, :, 1:129], ssum, ALU.add, ['r_raw', 'r_ssum'], ['r_ssum'])
                rr = pp[:, 0:8, :]
                kr = pp[:, 8:16, :]
                vr = pp[:, 16:24, :]
                em.act(twb[:, 0, :], pp[:, 24, :], AF.Tanh, ['r_ssum'], ['r_twb'])
                em.cp('dve', twb[:, 1, :], pp[:, 25, :], ['r_ssum'], ['r_twb'])
                for h in range(8):
                    em.mm(B[h // 4][0:64, (h % 4) * 128:(h % 4 + 1) * 128], w2b[:, d, h * 64:(h + 1) * 64], twb[:, 0, :], True, True,
                          ['w2b', 'r_twb'], [f"B{h // 4}"], inc=(h == 7))
                for h in range(8):
                    em.act(lw[:, h, :], B[h // 4][0:64, (h % 4) * 128:(h % 4 + 1) * 128], AF.Sigmoid, [f"B{h // 4}", 'PV64'],
                           ['r_lw'], bias=PV64[:, P6_W0 + d * 8 + h:P6_W0 + d * 8 + h + 1], scale=1.0)
                for h in range(8):
                    em.mm(B[h // 4][0:64, (h % 4) * 128:(h % 4 + 1) * 128], a2b[:, d, h * 64:(h + 1) * 64], twb[:, 1, :], True, True,
                          ['a2b', 'r_twb'], [f"B{h // 4}"], inc=(h == 7))
                for h in range(8):
                    em.act(aa[:, h, :], B[h // 4][0:64, (h % 4) * 128:(h % 4 + 1) * 128], AF.Sigmoid,
                           [f"B{h // 4}", 'PV64'], ['r_aa'], bias=PV64[:, P6_A0 + d * 8 + h:P6_A0 + d * 8 + h + 1], scale=1.0)
                em.ts('dve', lw, lw, -R_DECAY_SCALE, None, ALU.mult, None, ['r_lw'], ['r_lw'])
                em.tt('dve', kk, kr, PV64[:, P6_KK:P6_KK + 8].unsqueeze(2).to_broadcast([64, 8, 128]), ALU.mult,
                      ['r_ssum', 'PV64'], ['r_kk'])
                em.act(sqb, kk, AF.Square, ['r_kk'], ['r_prod'])
                for hh in range(2):
                    em.mm(B[hh][0:64, :], onesb[0:64, 0:64], sqb[:, hh * 4:(hh + 1) * 4, :], True, True,
                          ['CSTB', 'r_prod'], [f"B{hh}"])
                for hh in range(2):
                    em.act(rinv[:, hh * 4:(hh + 1) * 4, :], B[hh][0:64, :].rearrange("p (h t) -> p h t", t=128), AF.Sqrt,
                           [f"B{hh}"], ['r_lex'])
                em.ts('dve', rinv, rinv, 1e-12, None, ALU.max, None, ['r_lex'], ['r_lex'])
                em.op('dve', lambda e: e.reciprocal(out=rinv, in_=rinv), ['r_lex'], ['r_lex'])
                em.tt('dve', kk, kk, rinv, ALU.mult, ['r_kk', 'r_lex'], ['r_kk'])
                em.tt('dve', tmpk, aa, PV64[:, P6_KA:P6_KA + 8].unsqueeze(2).to_broadcast([64, 8, 128]), ALU.mult,
                      ['r_aa', 'PV64'], ['r_tmpk'])
                em.tt('dve', tmpk, tmpk, OMMU.unsqueeze(2).to_broadcast([64, 8, 128]), ALU.add, ['r_tmpk', 'OMMU'], ['r_tmpk'])
                em.tt('dve', kd, kr, tmpk, ALU.mult, ['r_ssum', 'r_tmpk'], ['r_kd'])
                em.op('dve', lambda e: e.tensor_tensor_scan(out=linc.rearrange("p h t -> p (h t)"),
                                                            data0=RMK.rearrange("p h t -> p (h t)"),
                                                            data1=lw.rearrange("p h t -> p (h t)"), initial=0.0,
                                                            op0=ALU.mult, op1=ALU.add), ['RMK', 'r_lw'], ['r_linc'])
                if d == 0:
                    tot = linc[:, :, 127:128]
                else:
                    em.tt('dve', lex, lw, linc, ALU.subtract, ['r_lw', 'r_linc'], ['r_lex'])
                    em.cp('dve', gC, linc[:, :, 127], ['r_linc'], [kgC])
                    em.tt('dve', linc, lex, gC.unsqueeze(2).to_broadcast([64, 8, 128]), ALU.add, ['r_lex', kgC, 'r_linc'],
                          ['r_linc'])
                    tot = linc[:, :, 0:1]
                em.tt('dve', lex, linc, lw, ALU.subtract, ['r_linc', 'r_lw'], ['r_lex'])
                em.act(e1, linc, AF.Exp, ['r_linc'], ['r_e1'])
                em.act(e0, lex, AF.Exp, ['r_lex'], ['r_e0'])
                em.act(ei, linc, AF.Exp, ['r_linc'], ['r_ei'], scale=-1.0)
                em.act(gC, tot.rearrange("p h o -> p (h o)"), AF.Exp, ['r_linc', kgC], [kgC])
                em.tt('dve', AR[:, :, 128:256], rr, e1, ALU.mult, ['r_ssum', 'r_e1'], [kAR])
                em.stt(AR[:, :, 0:128], kk, -1.0, e0, ALU.mult, ALU.mult, ['r_kk', 'r_e0'], [kAR])
                em.tt('dve', tmpk, kk, aa, ALU.mult, ['r_kk', 'r_aa', 'r_tmpk'], ['r_tmpk'])
                em.tt('dve', BK[:, :, 0, :], tmpk, ei, ALU.mult, ['r_tmpk', 'r_ei'], [kBK])
                em.tt('dve', BK[:, :, 1, :], kd, ei, ALU.mult, ['r_kd', 'r_ei'], [kBK])
                em.tt('dve', tmpk, rr, kd, ALU.mult, ['r_ssum', 'r_kd', 'r_tmpk'], ['r_tmpk'])
                em.tt('dve', prodb, tmpk, PV64[:, P6_RK:P6_RK + 8].unsqueeze(2).to_broadcast([64, 8, 128]), ALU.mult,
                      ['r_tmpk', 'PV64'], ['r_prod'])
                for h in range(8):
                    em.tr(B[0][:, h * 64:(h + 1) * 64], vr[:, h, :], ident[0:64, 0:64], ['r_ssum', 'CST'], ['B0'], inc=(h == 7))
                em.cp('act', vtokb, B[0], ['B0'], [kvtokb])
                for half in range(2):
                    for h4 in range(4):
                        for q in range(2):
                            em.tr(BBp[:, (h4 * 2 + q) * 64:(h4 * 2 + q + 1) * 64], BK[:, half * 4 + h4, q, :], identb[0:64, 0:64],
                                  [kBK, 'CSTB'], ['BB'], inc=(h4 == 3 and q == 1))
                    em.cp('dve', BKtok[:, half * 4:(half + 1) * 4].rearrange("p h q k -> p (h q k)"), BBp[:, 0:512], ['BB'], [kBKtok])
                for h in range(8):
                    em.mm(B[1][:, 256 + h:257 + h], prodb[:, h, :], onesb[0:64, 0:1], True, True, ['r_prod', 'CSTB'], ['B1'], inc=(h == 7))
                em.cp('act', coef, B[1][:, 256:264], ['B1'], [kcoef])
                for hp in range(4):
                    bb_ = B[0]
                    bbk = "B0"
                    bk2 = B[1]
                    bk2k = "B1"
                    for q in range(2):
                        h = hp * 2 + q
                        em.mm(bb_[:, q * 256:(q + 1) * 256], BK[:, h, 0, :], AR[:, h, :], True, True, [kBK, kAR, 'r_lw', 'r_aa'], [bbk], inc=(q == 1))
                        em.mm(bk2[:, q * 256:(q + 1) * 256], BK[:, h, 1, :], AR[:, h, :], True, True, [kBK, kAR], [bk2k], inc=(q == 1))
                    em.tt('dve', GB[:, hp * 2:hp * 2 + 2, :], bb_.rearrange("p (q c) -> p q c", c=256),
                          MARt[:, d:d + 1, :].to_broadcast([128, 2, 256]), ALU.mult, [bbk, 'MAR'], [kGB])
                    em.tt('dve', Q0[:, hp * 2:hp * 2 + 2, :], bb_.rearrange("p (q c) -> p q c", c=256)[:, :, 0:128],
                          CST[:, C_MP0 + (1 - d) * 128:C_MP0 + (2 - d) * 128].unsqueeze(1).to_broadcast([128, 2, 128]), ALU.mult,
                          [bbk, 'CST'], [kQ0])
                    em.tt('dve', GK[:, hp * 2:hp * 2 + 2, :], bk2.rearrange("p (q c) -> p q c", c=256),
                          MARt[:, d:d + 1, :].to_broadcast([128, 2, 256]), ALU.mult, [bk2k, 'MAR'], [kGK])
                if split:
                    em.stream = ('rs', idx)
                if first_of_d:
                    em.memset('pool', Hs, 0.0, ['r_H'])
                    em.memset('pool', Hb, 0.0, ['r_Hb'])
                for hh in range(2):
                    bp = B[2 + hh]
                    bpk = f"B{2 + hh}"
                    for q in range(4):
                        h = hh * 4 + q
                        em.mm(bp[:, q * 128:(q + 1) * 128], AR[:, h, 0:128], BK[:, h, 0, :], True, True, [kAR, kBK], [bpk], inc=(q == 3))
                    b3 = bp.rearrange("p (q c) -> p q c", c=128)
                    hsl = slice(hh * 4, (hh + 1) * 4)
                    em.tt('dve', Pm[0][:, hsl, :], b3, CST[:, C_MP0 + d * 128:C_MP0 + (d + 1) * 128].unsqueeze(1).to_broadcast([128, 4, 128]),
                          ALU.mult, [bpk, 'CST'], ['r_P0'])
                    em.tt('dve', E1m[:, hsl, :], b3, CST[:, C_ME1 + d * 128:C_ME1 + (d + 1) * 128].unsqueeze(1).to_broadcast([128, 4, 128]),
                          ALU.mult, [bpk, 'CST'], ['r_E1'])
                    em.tt('dve', E2m[:, hsl, :], b3, CST[:, C_ME2 + d * 128:C_ME2 + (d + 1) * 128].unsqueeze(1).to_broadcast([128, 4, 128]),
                          ALU.mult, [bpk, 'CST'], ['r_E2'])
                em.tt('dve', Ym[0], Q0, identb.unsqueeze(1).to_broadcast([128, 8, 128]), ALU.add, [kQ0, 'CSTB'], ['r_Y0'])
                em.cp('act', Qm[0], Q0, [kQ0], ['r_Q0'])
                cur = 0
                for lev in range(1, 5):
                    nxt = 1 - cur
                    for hh in range(2):
                        bp = B[2]
                        bpk = "B2"
                        bq = B[3]
                        bqk = "B3"
                        hsl = slice(hh * 4, (hh + 1) * 4)
                        for q in range(4):
                            h = hh * 4 + q
                            em.mm(bp[:, q * 128:(q + 1) * 128], Qm[cur][:, h, :], Pm[cur][:, h, :], True, True,
                                  [f"r_Q{cur}", f"r_P{cur}"], [bpk], inc=(q == 3))
                        for q in range(4):
                            h = hh * 4 + q
                            em.mm(bq[:, q * 128:(q + 1) * 128], Pm[cur][:, h, :], Qm[cur][:, h, :], True, True,
                                  [f"r_Q{cur}", f"r_P{cur}"], [bqk], inc=(q == 3))
                        em.cp('act', Pm[nxt][:, hsl, :], bp.rearrange("p (q c) -> p q c", c=128), [bpk], [f"r_P{nxt}"])
                        em.cp('dve', Qm[nxt][:, hsl, :], bq.rearrange("p (q c) -> p q c", c=128), [bqk], [f"r_Q{nxt}"])
                    for hh in range(2):
                        by = B[2 + hh]
                        byk = f"B{2 + hh}"
                        hsl = slice(hh * 4, (hh + 1) * 4)
                        for q in range(4):
                            h = hh * 4 + q
                            em.mm(by[:, q * 128:(q + 1) * 128], Pm[nxt][:, h, :], Ym[cur][:, h, :], True, True,
                                  [f"r_P{nxt}", f"r_Y{cur}"], [byk], inc=(q == 3))
                        em.tt('dve', Ym[nxt][:, hsl, :], by.rearrange("p (q c) -> p q c", c=128), Ym[cur][:, hsl, :], ALU.add,
                              [byk, f"r_Y{cur}"], [f"r_Y{nxt}"])
                    cur = nxt
                Dt = Ym[cur]
                dtk = f"r_Y{cur}"
                for st, (Em_, ek) in enumerate([(E1m, 'r_E1'), (E2m, 'r_E2')]):
                    oth = Ym[1 - cur]
                    othk = f"r_Y{1 - cur}"
                    for half in range(2):
                        for h4 in range(4):
                            em.tr(BBp[:, 512 + h4 * 128:512 + (h4 + 1) * 128], Dt[:, half * 4 + h4, :], identb, [dtk, 'CSTB'], ['BB'], inc=(h4 == 3))
                        em.cp('act', Dm[:, half * 4:(half + 1) * 4, :], BBp[:, 512:1024].rearrange("p (h c) -> p h c", c=128),
                              ['BB'], ['r_D'])
                    for hh in range(2):
                        bz = B[2 + hh]
                        bzk = f"B{2 + hh}"
                        hsl = slice(hh * 4, (hh + 1) * 4)
                        for q in range(4):
                            h = hh * 4 + q
                            em.mm(bz[:, q * 128:(q + 1) * 128], Em_[:, h, :], Dt[:, h, :], True, True, [ek, dtk], [bzk], inc=(q == 3))
                        em.cp('act' if hh == 0 else 'dve', Zm[:, hsl, :], bz.rearrange("p (q c) -> p q c", c=128), [bzk], ['r_Z'])
                    for hh in range(2):
                        by = B[2 + hh]
                        byk = f"B{2 + hh}"
                        hsl = slice(hh * 4, (hh + 1) * 4)
                        for q in range(4):
                            h = hh * 4 + q
                            em.mm(by[:, q * 128:(q + 1) * 128], Dm[:, h, :], Zm[:, h, :], True, True, ['r_D', 'r_Z'], [byk], inc=(q == 3))
                        em.tt('dve', oth[:, hsl, :], by.rearrange("p (q c) -> p q c", c=128), Dt[:, hsl, :], ALU.add,
                              [byk, dtk], [othk])
                    cur = 1 - cur
                    Dt = Ym[cur]
                    dtk = f"r_Y{cur}"
                TT_ = Dt
                ttk = dtk
                for h in range(8):
                    hs_ = slice(h * 64, (h + 1) * 64)
                    em.mm(B[2][:, hs_], AR[:, h, 0:128], Hb[:, hs_], True, False, [kAR, 'r_Hb'], ['B2'])
                    em.mm(B[2][:, hs_], GK[:, h, 0:128], vtokb[:, hs_], False, True, [kGK, kvtokb], ['B2'], inc=(h == 7))
                em.cp('act', Wt, B[2], ['B2'], ['r_W'])
                for h in range(8):
                    hs_ = slice(h * 64, (h + 1) * 64)
                    em.mm(B[3][:, hs_], TT_[:, h, :], Wt[:, hs_], True, True, [ttk, 'r_W'], ['B3'], inc=(h == 7))
                em.cp('dve', Ut, B[3], ['B3'], ['r_U'])
                for h in range(8):
                    hs_ = slice(h * 64, (h + 1) * 64)
                    em.mm(B[2][:, hs_], AR[:, h, 128:256], Hb[:, hs_], True, False, [kAR, 'r_Hb'], ['B2'])
                    em.mm(B[2][:, hs_], GB[:, h, 128:256], Ut[:, hs_], False, False, [kGB, 'r_U'], ['B2'])
                    em.mm(B[2][:, hs_], GK[:, h, 128:256], vtokb[:, hs_], False, True, [kGK, kvtokb], ['B2'], inc=(h == 7))
                for h in range(8):
                    hs_ = slice(h * 64, (h + 1) * 64)
                    em.mm(B[3][0:64, hs_], BKtok[:, h, 0, :], Ut[:, hs_], True, False, [kBKtok, 'r_U'], ['B3'])
                    em.mm(B[3][0:64, hs_], BKtok[:, h, 1, :], vtokb[:, hs_], False, True, [kBKtok, kvtokb], ['B3'], inc=(h == 7))
                em.tt('dve', Hs, Hs, B[3][0:64, :], ALU.add, ['r_H', 'B3'], ['r_H'])
                em.tt('dve', Hs.rearrange("p (h v) -> p h v", v=64), Hs.rearrange("p (h v) -> p h v", v=64),
                      gC.unsqueeze(2).to_broadcast([64, 8, 64]), ALU.mult, ['r_H', kgC], ['r_H'])
                em.cp('act', Hb, Hs, ['r_H', 'B2'], ['r_Hb'])
                if d == 0:
                    if emit_out:
                        em.cp('act', ot[:, 0:512], B[2], ['B2'], ['r_o'])
                        em.cp('dve', ot[:, 512:520], coef, [kcoef], ['r_ocoef'])
                        em.dma('pool', OF[s][t0:t0 + 128, :], ot, reads=['r_o', 'r_ocoef'], writes=[f"of{s}"])
                elif emit_out:
                    em.dma('sp', oft, OF[s][t0:t0 + 128, :], reads=[f"of{s}"], writes=['r_of'])
                    em.tt('dve', ot[:, 0:512], B[2], oft[:, 0:512], ALU.add, ['B2', 'r_of'], ['r_o'])
                    o3 = ot[:, 0:512].rearrange("p (h v) -> p h v", v=64)
                    em.op('dve', lambda e: e.tensor_reduce(out=gn[:, 0:8], in_=o3, axis=AX.X, op=ALU.add), ['r_o'], ['r_gn'])
                    em.act(osq, ot[:, 0:512], AF.Square, ['r_o'], ['r_osq'])
                    em.op('dve', lambda e: e.tensor_reduce(out=gn[:, 8:16], in_=osq.rearrange("p (h v) -> p h v", v=64),
                                                           axis=AX.X, op=ALU.add), ['r_osq', 'r_gn'], ['r_gn'])
                    em.ts('dve', gn[:, 16:24], gn[:, 0:8], 1.0 / 64, None, ALU.mult, None, ['r_gn'], ['r_gn'])
                    em.tt('dve', gn[:, 0:8], gn[:, 16:24], gn[:, 16:24], ALU.mult, ['r_gn'], ['r_gn'])
                    em.stt(gn[:, 24:32], gn[:, 8:16], 1.0 / 64, gn[:, 0:8], ALU.mult, ALU.subtract, ['r_gn'], ['r_gn'])
                    em.act(gn[:, 24:32], gn[:, 24:32], AF.Sqrt, ['r_gn'], ['r_gn'], bias=R_LN_EPS, scale=1.0)
                    em.op('dve', lambda e: e.reciprocal(out=gn[:, 24:32], in_=gn[:, 24:32]), ['r_gn'], ['r_gn'])
                    em.tt('dve', o3, o3, gn[:, 16:24].unsqueeze(2).to_broadcast([128, 8, 64]), ALU.subtract, ['r_o', 'r_gn'], ['r_o'])
                    em.tt('dve', o3, o3, gn[:, 24:32].unsqueeze(2).to_broadcast([128, 8, 64]), ALU.mult, ['r_o', 'r_gn'], ['r_o'])
                    em.tt('dve', ot[:, 0:512], ot[:, 0:512], lnw, ALU.mult, ['r_o', 'ROWB'], ['r_o'])
                    em.tt('dve', ot[:, 0:512], ot[:, 0:512], lnb, ALU.add, ['r_o', 'ROWB'], ['r_o'])
                    em.tt('dve', gn[:, 32:40], coef, oft[:, 512:520], ALU.add, [kcoef, 'r_of', 'r_gn'], ['r_gn'])
                    em.tt('dve', osq.rearrange("p (h v) -> p h v", v=64), vtokb.rearrange("p (h v) -> p h v", v=64),
                          gn[:, 32:40].unsqueeze(2).to_broadcast([128, 8, 64]), ALU.mult, [kvtokb, 'r_gn', 'r_osq'], ['r_osq'])
                    em.tt('dve', ot[:, 0:512], ot[:, 0:512], osq, ALU.add, ['r_o', 'r_osq'], ['r_o'])
                    load_halo(xg0, 'r_xg0', XG[s][0:128, :], f"xg{s}", t0, T, False)
                    load_halo(xg1, 'r_xg1', XG[s][128:160, :], f"xg{s}", t0, T, False)
                    for (xg_, sg_, np_, mucol, kx_, ks_) in [(xg0, sg0, 128, PV_MUXG0, 'r_xg0', 'r_sg0'),
                                                             (xg1, sg1, 32, PV_MUXG1, 'r_xg1', 'r_sg1')]:
                        em.tt('dve', xgt[:np_, :], xg_[:np_, 0:128], xg_[:np_, 2:130], ALU.add, [kx_], ['r_xgt'])
                        em.stt(xgt[:np_, :], xgt[:np_, :], 0.5, xg_[:np_, 1:129], ALU.mult, ALU.subtract, ['r_xgt', kx_], ['r_xgt'])
                        em.stt(xgt[:np_, :], xgt[:np_, :], PV[:np_, mucol:mucol + 1], xg_[:np_, 1:129], ALU.mult, ALU.add,
                               ['r_xgt', kx_, 'PV'], ['r_xgt'])
                        em.act(sg_[:np_, :], xgt[:np_, :], AF.Sigmoid, ['r_xgt'], [ks_])
                    em.mm(B[3], sg0, g2b0, True, False, ['r_sg0', 'g2b'], ['B3'])
                    em.mm(B[3], sg1, g2b1, False, True, ['r_sg1', 'g2b'], ['B3'])
                    em.tt('dve', ot[:, 0:512], ot[:, 0:512], B[3], ALU.mult, ['r_o', 'B3'], ['r_o'])
                    for j in range(4):
                        em.tr(B[2][:, j * 128:(j + 1) * 128], ot[:, j * 128:(j + 1) * 128], ident, ['r_o', 'CST'], ['B2'], inc=(j == 3))
                    em.cp('act', mixo[:, 4:8, :], B[2].rearrange("p (j t) -> p j t", t=128), ['B2'], ['mixo_r'])
                    em.dma('pool', fm(MIX[s])[:, 4:8, t0:t0 + 128], mixo[:, 4:8, :], reads=['mixo_r'], writes=[f"mixr{s}"])

            mc_ = mixcfg or {}
            inter = mc_.get('interleave', True)
            for b in range(mc_.get('nb', NBL)):
                nw = 0
                for stream, fnc in (('m', mamba_chunk), ('r', wkv_chunk)):
                    if not mc_.get('mamba' if stream == 'm' else 'wkv', True):
                        continue
                    for d in range(mc_.get('nd', 2)):
                        first = True
                        if stream == 'm':
                            em.stream = 'm' if inter else None
                            em.memset('pool', hst, 0.0, ['m_hst'])
                            em.memset('pool', hstb, 0.0, ['m_hstb'])
                        for kind in range(mc_.get('nkind', 2)):
                            s = b * 2 + kind
                            T = seqT(s)
                            nch = T // CH
                            order = range(nch) if d == 0 else range(nch - 1, -1, -1)
                            emit = (kind == 1) or need_ctx_out or mc_.get('ctxout', False)
                            for c in order:
                                if stream == 'm':
                                    fnc(s, c * CH, d, emit)
                                else:
                                    fnc(s, c * CH, d, emit, idx=nw, first_of_d=first, split=inter)
                                    nw += 1
                                    first = False
                em.stream = None
                if inter:
                    em.flush_mixer(nw)
            em.barrier()

    def stage_proj_post(l, phase, wap, Kc, SRC, srcname, gidx, seqs):
        with ExitStack() as es:
            def S(name, shape, dt=F32):
                return es.enter_context(nc.sbuf_tensor(name, list(shape), dt)).ap()
            nm = f"pp{phase}"
            wb = load_wbf(es, l, wap, Kc, D, nm + "w")
            a = S(f"{nm}a{l}", [128, Kc, 512], BF16)
            xt = S(f"{nm}x{l}", [128, 8, 512])
            y = S(f"{nm}y{l}", [128, 8, 512])
            sq = S(f"{nm}sq{l}", [128, 8, 512], BF16)
            rstd = S(f"{nm}rs{l}", [128, 512])
            pss = [es.enter_context(nc.psum_tensor(f"{nm}ps{l}_{i}", [128, 512], F32)).ap() for i in range(5)]
            for s in seqs:
                T = seqT(s)
                TW = min(512, T)
                jmod = 2 if s % 2 == 0 else s // 2
                rsrc, rsk = res_src(l, s, phase)
                rdst, rdk = res_dst(l, s, phase)
                for tt_ in range(T // TW):
                    t0 = tt_ * TW
                    em.dma('sp', a[:, :, :TW], fm(SRC[s])[:, :, t0:t0 + TW], reads=([f"mixm{s}", f"mixr{s}"] if srcname == 'mix' else [f"{srcname}{s}"]), writes=[nm + 'a'])
                    em.dma('pool', xt[:, :, :TW], fm(rsrc)[:, :, t0:t0 + TW], reads=[rsk], writes=[nm + 'x'])
                    for m in range(8):
                        ps = pss[m % 4]
                        pk = f"ps{m % 4}"
                        for k in range(Kc):
                            em.mm(ps[:, :TW], wb[:, k, m * 128:(m + 1) * 128], a[:, k, :TW], k == 0, k == Kc - 1,
                                  [nm + 'wbf', nm + 'a'], [pk])
                        em.cp('dve', y[:, m, :TW], ps[:, :TW], [pk], [nm + 'y'])
                        em.act(sq[:, m, :TW], ps[:, :TW], AF.Square, [pk], [nm + 'sq'])
                    for m in range(8):
                        em.mm(pss[4][:, :TW], onesb, sq[:, m, :TW], m == 0, m == 7, ['CSTB', nm + 'sq'], ['ps4'])
                    em.act(rstd[:, :TW], pss[4][:, :TW], AF.Sqrt, ['ps4'], [nm + 'rs'], bias=EPS, scale=1.0 / D)
                    em.op('dve', lambda e: e.reciprocal(out=rstd[:, :TW], in_=rstd[:, :TW]), [nm + 'rs'], [nm + 'rs'])
                    for m in range(8):
                        em.stt(y[:, m, :TW], y[:, m, :TW], DER[:, gidx, m, jmod:jmod + 1], rstd[:, :TW], ALU.mult, ALU.mult,
                               [nm + 'y', nm + 'rs', 'DER'], [nm + 'y'])
                    em.tt('dve', xt[:, :, :TW], xt[:, :, :TW], y[:, :, :TW], ALU.add, [nm + 'x', nm + 'y'], [nm + 'x'])
                    em.dma('pool', fm(rdst)[:, :, t0:t0 + TW], xt[:, :, :TW], reads=[nm + 'x'], writes=[rdk])
            em.barrier()

    def stage_ffn_up(l, seqs):
        with ExitStack() as es:
            def S(name, shape, dt=F32):
                return es.enter_context(nc.sbuf_tensor(name, list(shape), dt)).ap()
            wb = load_wbf(es, l, f_w_up[l], 8, 2 * DFF, "wup")
            xt = S(f"fux{l}", [128, 8, 512])
            h = S(f"fuh{l}", [128, 8, 512], BF16)
            sq = S(f"fusq{l}", [128, 8, 512], BF16)
            rstd = S(f"furs{l}", [128, 512])
            stg = [S(f"fustg{l}_{i}", [128, 512]) for i in range(2)]
            stv = [S(f"fustv{l}_{i}", [128, 512], BF16) for i in range(2)]
            pss = [es.enter_context(nc.psum_tensor(f"fups{l}_{i}", [128, 512], F32)).ap() for i in range(8)]
            for s in seqs:
                T = seqT(s)
                TW = min(512, T)
                jmod = 2 if s % 2 == 0 else s // 2
                src, srck = res_src(l, s, 1)
                for tt_ in range(T // TW):
                    t0 = tt_ * TW
                    em.dma('sp', xt[:, :, :TW], fm(src)[:, :, t0:t0 + TW], reads=[srck], writes=['fux'])
                    prenorm(xt, TW, h, sq, rstd, pss[7], jmod, 2, 24, 'fux', 'fuh', 'ps7')
                    for j in range(NFF):
                        pg = pss[(2 * j) % 6]
                        pgk = f"ps{(2 * j) % 6}"
                        pv_ = pss[(2 * j + 1) % 6]
                        pvk = f"ps{(2 * j + 1) % 6}"
                        for k in range(8):
                            em.mm(pg[:, :TW], wb[:, k, j * 128:(j + 1) * 128], h[:, k, :TW], k == 0, k == 7, ['wupbf', 'fuh'], [pgk])
                        for k in range(8):
                            em.mm(pv_[:, :TW], wb[:, k, DFF + j * 128:DFF + (j + 1) * 128], h[:, k, :TW], k == 0, k == 7,
                                  ['wupbf', 'fuh'], [pvk])
                        sg_ = stg[j % 2]
                        sv_ = stv[j % 2]
                        em.cp('dve', sg_[:, :TW], pg[:, :TW], [pgk], [f"fustg{j % 2}"])
                        em.cp('act', sv_[:, :TW], pv_[:, :TW], [pvk], [f"fustv{j % 2}"])
                        em.dma('pool', GATE[s][j * 128:(j + 1) * 128, t0:t0 + TW], sg_[:, :TW], reads=[f"fustg{j % 2}"],
                               writes=[f"gate{s}"])
                        em.dma('sp', VAL[s][j * 128:(j + 1) * 128, t0:t0 + TW], sv_[:, :TW], reads=[f"fustv{j % 2}"],
                               writes=[f"val{s}"])
            em.barrier()

    def stage_ffn_conv(l, seqs):
        with ExitStack() as es:
            def S(name, shape, dt=F32):
                return es.enter_context(nc.sbuf_tensor(name, list(shape), dt)).ap()
            gflat = [S(f"fcg{l}_{i}", [128, 2048]) for i in range(2)]
            vflat = [S(f"fcv{l}_{i}", [128, 2048], BF16) for i in range(2)]
            gpx = [S(f"fcgpx{l}_{i}", [128, 34, 66], BF16) for i in range(2)]
            gpc = [S(f"fcgpc{l}_{i}", [128, 3, 258], BF16) for i in range(2)]
            dg = [S(f"fcdg{l}_{i}", [128, 9, 128], BF16) for i in range(2)]
            acc = S(f"fcacc{l}", [128, 2048])
            u = S(f"fcu{l}", [128, 2048])
            ab = [S(f"fcab{l}_{i}", [128, 2048], BF16) for i in range(2)]
            pss = [es.enter_context(nc.psum_tensor(f"fcps{l}_{i}", [128, 512], F32)).ap() for i in range(4)]
            for i in range(2):
                em.memset('pool', gpx[i], 0.0, [f"fcgpx{i}"])
                em.memset('pool', gpc[i], 0.0, [f"fcgpc{i}"])
            it = 0
            pi = 0
            for s in seqs:
                T = seqT(s)
                for j in range(NFF):
                    i2 = it % 2
                    it += 1
                    if s % 2 == 1:
                        R, Cc, gp, gpk = 32, 64, gpx[i2], f"fcgpx{i2}"
                    else:
                        R, Cc, gp, gpk = 1, 256, gpc[i2], f"fcgpc{i2}"
                    gf = gflat[i2]
                    vf = vflat[i2]
                    em.dma('sp', gf[:, :T], GATE[s][j * 128:(j + 1) * 128, :], reads=[f"gate{s}"], writes=[f"fcg{i2}"])
                    em.dma('sp', vf[:, :T], VAL[s][j * 128:(j + 1) * 128, :], reads=[f"val{s}"], writes=[f"fcv{i2}"])
                    em.cp('act', gp[:, 1:1 + R, 1:1 + Cc], gf[:, :T].rearrange("p (r c) -> p r c", c=Cc), [f"fcg{i2}"], [gpk])
                    taps = list(range(9)) if s % 2 == 1 else [3, 4, 5]
                    for tap in taps:
                        em.ts('dve', dg[i2][:, tap, :], identb, PV[:, PV_FCW + tap * NFF + j:PV_FCW + tap * NFF + j + 1], None,
                              ALU.mult, None, ['CSTB', 'PV'], [f"fcdg{i2}"])
                    nblk = T // 512 if s % 2 == 1 else 1
                    for blk in range(nblk):
                        ps = pss[pi % 4]
                        pk = f"ps{pi % 4}"
                        pi += 1
                        for ti, tap in enumerate(taps):
                            dr, dc = tap // 3 - 1, tap % 3 - 1
                            if s % 2 == 1:
                                rhs = gp[:, 1 + dr + blk * 8:1 + dr + blk * 8 + 8, 1 + dc:1 + dc + 64]
                                out = ps.rearrange("p (r c) -> p r c", c=64)
                                wdt = 512
                            else:
                                rhs = gp[:, 1, 1 + dc:1 + dc + 256]
                                out = ps[:, 0:256]
                                wdt = 256
                            em.mm(out, dg[i2][:, tap, :], rhs, ti == 0, ti == len(taps) - 1, [f"fcdg{i2}", gpk], [pk])
                        em.act(acc[:, blk * 512:blk * 512 + wdt], ps[:, 0:wdt], AF.Identity, [pk, 'PV'], ['fcacc'],
                               bias=PV[:, PV_FCB + j:PV_FCB + j + 1], scale=1.0)
                    em.act(u[:, :T], acc[:, :T], AF.Square, ['fcacc'], ['fcu'])
                    em.ts('dve', u[:, :T], u[:, :T], 0.044715, 1.0, ALU.mult, ALU.add, ['fcu'], ['fcu'])
                    em.tt('dve', u[:, :T], u[:, :T], acc[:, :T], ALU.mult, ['fcu', 'fcacc'], ['fcu'])
                    em.act(u[:, :T], u[:, :T], AF.Sigmoid, ['fcu'], ['fcu'], scale=GELU_C)
                    em.tt('dve', u[:, :T], u[:, :T], acc[:, :T], ALU.mult, ['fcu', 'fcacc'], ['fcu'])
                    em.tt('dve', ab[i2][:, :T], u[:, :T], vf[:, :T], ALU.mult, ['fcu', f"fcv{i2}"], [f"fcab{i2}"])
                    em.dma('pool', ACTV[s][j * 128:(j + 1) * 128, :], ab[i2][:, :T], reads=[f"fcab{i2}"], writes=[f"actv{s}"])
            em.barrier()

    allseq = list(range(NS))
    xseq = [s for s in range(NS) if s % 2 == 1]
    outkeys = []
    for l in range(n_layers):
        last = (l == n_layers - 1)
        stage_mod(l)
        if stop_after == 'mod':
            break
        stage_inproj(l)
        if stop_after == 'inproj':
            break
        stage_mixer(l, need_ctx_out=not last)
        if stop_after == 'mixer':
            break
        seqs = xseq if last else allseq
        stage_proj_post(l, 0, w_out[l], 8, MIX, "mix", 1, seqs)
        if stop_after == 'outproj':
            break
        stage_ffn_up(l, seqs)
        stage_ffn_conv(l, seqs)
        stage_proj_post(l, 1, f_w_down[l], NFF, ACTV, "actv", 3, seqs)
    em.barrier()
    return nc, em


def host_prep(inp):
    f = np.float32
    idx = np.arange(128)
    cstn = np.zeros((128, NCST), f)
    cstn[:, C_ID:C_ID + 128] = np.eye(128)
    cstn[:, C_UTI:C_UTI + 128] = (idx[:, None] <= idx[None, :])
    cstn[:, C_LTI:C_LTI + 128] = (idx[:, None] >= idx[None, :])
    cstn[:, C_UTS:C_UTS + 128] = (idx[:, None] < idx[None, :])
    cstn[:, C_LTS:C_LTS + 128] = (idx[:, None] > idx[None, :])
    cstn[:, C_ONE:C_ONE + 128] = 1.0
    b32 = idx // 32
    b64 = idx // 64
    same32 = b32[:, None] == b32[None, :]
    same64 = b64[:, None] == b64[None, :]
    for d in range(2):
        strict = (idx[:, None] > idx[None, :]) if d == 0 else (idx[:, None] < idx[None, :])
        cstn[:, C_MP0 + d * 128:C_MP0 + (d + 1) * 128] = strict & same32
        cstn[:, C_ME1 + d * 128:C_ME1 + (d + 1) * 128] = strict & same64 & (~same32)
        cstn[:, C_ME2 + d * 128:C_ME2 + (d + 1) * 128] = strict & (~same64)
    pvn = np.zeros((L, 128, NPV), f)
    pv6 = np.zeros((L, 64, NPV64), f)
    rwn = np.zeros((L, 1, NROW), f)
    for l in range(L):
        pvn[l, :, PV_BMOD:PV_BMOD + 48] = inp['b_mod'][l].reshape(48, 128).T
        pvn[l, :, PV_GPRE1:PV_GPRE1 + 8] = inp['g_mix_pre'][l].reshape(8, 128).T
        pvn[l, :, PV_GPOST1:PV_GPOST1 + 8] = inp['g_mix_post'][l].reshape(8, 128).T
        pvn[l, :, PV_GPRE2:PV_GPRE2 + 8] = inp['g_ffn_pre'][l].reshape(8, 128).T
        pvn[l, :, PV_GPOST2:PV_GPOST2 + 8] = inp['g_ffn_post'][l].reshape(8, 128).T
        pvn[l, :, PV_MCW:PV_MCW + 24] = inp['m_conv_w'][l].reshape(3, 8, 128).transpose(2, 0, 1).reshape(128, 24)
        pvn[l, :, PV_MCB:PV_MCB + 8] = inp['m_conv_b'][l].reshape(8, 128).T
        pvn[l, :, PV_FCW:PV_FCW + 198] = inp['f_conv_w'][l].reshape(9, NFF, 128).transpose(2, 0, 1).reshape(128, 198)
        pvn[l, :, PV_FCB:PV_FCB + NFF] = inp['f_conv_b'][l].reshape(NFF, 128).T
        mu = inp['r_mu'][l]
        pvn[l, :, PV_MUXG0] = mu[1792:1920]
        pvn[l, 0:32, PV_MUXG1] = mu[1920:1952]
        pv6[l, :, P6_MURKV:P6_MURKV + 24] = mu[0:1536].reshape(24, 64).T
        pv6[l, :, P6_MUWA:P6_MUWA + 4] = mu[1536:1792].reshape(4, 64).T
        pv6[l, :, P6_W0:P6_W0 + 16] = inp['r_w0'][l].reshape(2, 8, 64).transpose(2, 0, 1).reshape(64, 16)
        pv6[l, :, P6_A0:P6_A0 + 16] = inp['r_a0'][l].reshape(2, 8, 64).transpose(2, 0, 1).reshape(64, 16)
        pv6[l, :, P6_KK:P6_KK + 8] = inp['r_k_k'][l].reshape(8, 64).T
        pv6[l, :, P6_KA:P6_KA + 8] = inp['r_k_a'][l].reshape(8, 64).T
        pv6[l, :, P6_RK:P6_RK + 8] = inp['r_r_k'][l].T
        rwn[l, 0, RV_MNW:RV_MNW + 512] = inp['m_norm_w'][l]
        rwn[l, 0, RV_LNW:RV_LNW + 512] = inp['r_ln_w'][l]
        rwn[l, 0, RV_LNB:RV_LNB + 512] = inp['r_ln_b'][l]
        rwn[l, 0, RV_MD:RV_MD + 8] = inp['m_d'][l]
        rwn[l, 0, RV_DTB:RV_DTB + 16] = inp['m_dt_bias'][l].reshape(16)
        rwn[l, 0, RV_ALOG:RV_ALOG + 16] = inp['m_a_log'][l].reshape(16)
    return cstn, pvn, pv6, rwn


def make_in_maps(inp, cores):
    cstn, pvn, pv6, rwn = host_prep(inp)
    shared = {k: np.ascontiguousarray(np.asarray(inp[k], dtype=np.float32)) for k in
              ['w_mod', 'w_in', 'w_out', 'r_w2', 'r_a2', 'r_g2', 'f_w_up', 'f_w_down']}
    maps = []
    x = np.asarray(inp['x'], np.float32)
    ctx = np.asarray(inp['ctx'], np.float32)
    c = np.asarray(inp['c'], np.float32)
    cc = np.asarray(inp['c_ctx'], np.float32)
    for ci in cores:
        bs = [ci * NBL + i for i in range(NBL)]
        m = dict(shared)
        m['xT'] = np.ascontiguousarray(x[bs].transpose(0, 2, 1))
        m['ctxT'] = np.ascontiguousarray(ctx[bs].transpose(0, 2, 1))
        m['cT'] = np.ascontiguousarray(np.stack([c[bs[0]], c[bs[1]], cc], axis=1))
        m['cst'] = cstn
        m['pv'] = pvn
        m['pv64'] = pv6
        m['rowv'] = rwn
        maps.append(m)
    return maps


def kernel(**inputs):
    nc, em = build()
    cores = list(range(NCORE))
    maps = make_in_maps(inputs, cores)
    res = run_bass_kernel_spmd(nc, maps, core_ids=cores)
    out = np.empty((NCORE * NBL, TX, D), np.float32)
    for ci in cores:
        o = res.results[ci]["outT"]
        out[ci * NBL:(ci + 1) * NBL] = o.transpose(0, 2, 1)
    return out
```

```python
import numpy as np
from contextlib import ExitStack
import concourse.bass as bass
import concourse.mybir as mybir
from concourse.bass_utils import run_bass_kernel_spmd

F32 = mybir.dt.float32
BF16 = mybir.dt.bfloat16
AF = mybir.ActivationFunctionType
ALU = mybir.AluOpType
AX = mybir.AxisListType

L = 2
D = 1024
TX = 2048
TC = 256
NBL = 2
NCORE = 8
CH = 128
DFF = 2816
NFF = 22
EPS = 1e-6
R_LN_EPS = 64e-5
R_DECAY_SCALE = 0.6065306597126334
GELU_C = 1.5957691216057308

PV_BMOD = 0
PV_GPRE1 = 48
PV_GPOST1 = 56
PV_GPRE2 = 64
PV_GPOST2 = 72
PV_MCW = 80
PV_MCB = 104
PV_FCW = 112
PV_FCB = 310
PV_MUXG0 = 332
PV_MUXG1 = 333
NPV = 334
P6_MURKV = 0
P6_MUWA = 24
P6_W0 = 28
P6_A0 = 44
P6_KK = 60
P6_KA = 68
P6_RK = 76
NPV64 = 84
RV_MNW = 0
RV_LNW = 512
RV_LNB = 1024
RV_MD = 1536
RV_DTB = 1544
RV_ALOG = 1560
NROW = 1576
C_ID = 0
C_UTI = 128
C_LTI = 256
C_UTS = 384
C_LTS = 512
C_ONE = 640
C_MP0 = 768
C_ME1 = 1024
C_ME2 = 1280
NCST = 1536


class Em:
    def __init__(self, nc, ndma=8):
        self.nc = nc
        self.engs = {'pe': nc.tensor, 'act': nc.scalar, 'dve': nc.vector, 'pool': nc.gpsimd, 'sp': nc.sync}
        self.sem = {}
        self.cnt = {}
        for k in ['pe', 'act', 'dve', 'pool']:
            self.sem[k] = nc.alloc_semaphore("sem_" + k)
            self.cnt[k] = 0
        self.dq = {}
        for q in ['sp', 'pool', 'act']:
            self.dq[q] = {'n': ndma, 'next': 0}
            for i in range(ndma):
                self.sem[f"d_{q}_{i}"] = nc.alloc_semaphore(f"dsem_{q}_{i}")
                self.cnt[f"d_{q}_{i}"] = 0
        self.seen = {k: {} for k in self.engs}
        self.lastw = {}
        self.readers = {}
        self.n = 0
        self.pend = {k: False for k in self.engs}
        self.stream = None
        self.queues = {}

    def _deps(self, reads, writes):
        deps = {}

        def add(d):
            if d is None:
                return
            k, v = d
            if deps.get(k, 0) < v:
                deps[k] = v
        for b in reads:
            add(self.lastw.get(b))
        for b in writes:
            add(self.lastw.get(b))
            for r in self.readers.get(b, ()):
                add(r)
        return deps

    def _waits(self, eng, deps):
        for k, v in deps.items():
            if k == 'pe' and eng == 'pe':
                continue
            if k.startswith('d_'):
                v = self.cnt[k]
            if self.seen[eng].get(k, 0) >= v:
                continue
            self.seen[eng][k] = v
            self.engs[eng].wait_ge(self.sem[k], v)
            self.n += 1

    def _mark(self, me, reads, writes):
        for b in reads:
            self.readers.setdefault(b, []).append(me)
        for b in writes:
            self.lastw[b] = me
            self.readers[b] = []

    @staticmethod
    def _is_psum(k):
        return (k[0] == 'B' and (k[1:].isdigit() or k in ('BB', 'BBa', 'BBb'))) or k.startswith('ps')

    def flush(self):
        qs = {k: v for k, v in self.queues.items() if v}
        self.queues = {}
        pos = {k: 0 for k in qs}
        while qs:
            k = min(qs, key=lambda n: pos[n] / len(qs[n]))
            it = qs[k][pos[k]]
            pos[k] += 1
            if it[0] == 'op':
                self.op(*it[1:])
            else:
                self.dma(it[1], it[2], it[3], it[4], it[5], **it[6])
            if pos[k] >= len(qs[k]):
                del qs[k]

    @staticmethod
    def _merge(lists):
        lists = [l for l in lists if l]
        out = []
        pos = [0] * len(lists)
        live = list(range(len(lists)))
        sticky = None
        while live:
            k = sticky if sticky is not None else min(live, key=lambda n: pos[n] / len(lists[n]))
            it = lists[k][pos[k]]
            out.append(it)
            pos[k] += 1
            if it[0] == 'op' and it[1] == 'pe':
                sticky = None if it[5] else k
            if pos[k] >= len(lists[k]):
                live.remove(k)
                sticky = None
        return out

    def flush_mixer(self, nw):
        qs = self.queues
        self.queues = {}
        wk = list(qs.get(('rp', 0), []))
        for i in range(nw):
            wk += self._merge([qs.get(('rs', i), []), qs.get(('rp', i + 1), [])])
        allops = self._merge([wk, qs.get('m', [])])
        for it in allops:
            if it[0] == 'op':
                self.op(*it[1:])
            else:
                self.dma(it[1], it[2], it[3], it[4], it[5], **it[6])

    def op(self, eng, fn, reads=(), writes=(), inc=True):
        if self.stream is not None:
            self.queues.setdefault(self.stream, []).append(('op', eng, fn, tuple(reads), tuple(writes), inc))
            return
        ex = [k for k in reads if self._is_psum(k)]
        self._waits(eng, self._deps(reads, list(writes) + ex))
        if inc:
            self.cnt[eng] += 1
            fn(self.engs[eng]).then_inc(self.sem[eng], 1)
            self._mark((eng, self.cnt[eng]), reads, writes)
            self.pend[eng] = False
        else:
            fn(self.engs[eng])
            self._mark((eng, self.cnt[eng] + 1), reads, writes)
            self.pend[eng] = True
        self.n += 1

    def dma(self, q, out, in_, reads=(), writes=(), **kw):
        if self.stream is not None:
            self.queues.setdefault(self.stream, []).append(('dma', q, out, in_, tuple(reads), tuple(writes), kw))
            return
        self._waits(q, self._deps(reads, writes))
        d = self.dq[q]
        i = d['next']
        d['next'] = (i + 1) % d['n']
        k = f"d_{q}_{i}"
        self.cnt[k] += 16
        self.engs[q].dma_start(out=out, in_=in_, **kw).then_inc(self.sem[k], 16)
        self._mark((k, self.cnt[k]), reads, writes)
        self.n += 1

    def barrier(self):
        assert not any(self.pend.values()), self.pend
        allv = {k: v for k, v in self.cnt.items() if v > 0}
        for e in self.engs:
            self._waits(e, dict(allv))

    def act(self, out, in_, func, r, w, bias=None, scale=None, accum=None):
        kw = {}
        if bias is not None:
            kw['bias'] = bias
        if scale is not None:
            kw['scale'] = scale
        if accum is not None:
            kw['accum_out'] = accum
        self.op('act', lambda e: e.activation(out=out, in_=in_, func=func, **kw), r, w)

    def tt(self, eng, out, a, b, op, r, w):
        self.op(eng, lambda e: e.tensor_tensor(out=out, in0=a, in1=b, op=op), r, w)

    def ts(self, eng, out, a, s1, s2, op0, op1, r, w):
        if s2 is None:
            self.op(eng, lambda e: e.tensor_scalar(out=out, in0=a, scalar1=s1, scalar2=None, op0=op0), r, w)
        else:
            self.op(eng, lambda e: e.tensor_scalar(out=out, in0=a, scalar1=s1, scalar2=s2, op0=op0, op1=op1), r, w)

    def stt(self, out, a, s, b, op0, op1, r, w):
        self.op('dve', lambda e: e.scalar_tensor_tensor(out=out, in0=a, scalar=s, in1=b, op0=op0, op1=op1), r, w)

    def mm(self, out, lhsT, rhs, start, stop, r, w, inc=None):
        self.op('pe', lambda e: e.matmul(out, lhsT=lhsT, rhs=rhs, start=start, stop=stop), r, w,
                inc=(stop if inc is None else inc))

    def tr(self, out, in_, ident, r, w, inc=True):
        self.op('pe', lambda e: e.transpose(out, in_, ident), r, w, inc=inc)

    def cp(self, eng, out, in_, r, w):
        if eng == 'act':
            self.op('act', lambda e: e.activation(out=out, in_=in_, func=AF.Identity), r, w)
        else:
            self.op(eng, lambda e: e.tensor_copy(out=out, in_=in_), r, w)

    def memset(self, eng, ap, val, w):
        self.op(eng, lambda e: e.memset(ap, val), (), w)


def seqT(s):
    return TX if (s % 2) == 1 else TC


def build(debug=False, n_layers=L, stop_after=None, mixcfg=None):
    nc = bass.Bass("TRN2", target_bir_lowering=False)
    em = Em(nc)
    dbgset = debug if isinstance(debug, (set, list, tuple)) else None

    def din(name, shape, dt=F32):
        return nc.dram_tensor(name, list(shape), dt, kind="ExternalInput").ap()

    def dscr(name, shape, dt=F32):
        isdbg = (debug is True) or (dbgset is not None and name.rstrip('0123456789') in dbgset)
        return nc.dram_tensor(name, list(shape), dt, kind="ExternalOutput" if isdbg else "Internal").ap()

    xT = din("xT", [NBL, D, TX])
    ctxT = din("ctxT", [NBL, D, TC])
    cT = din("cT", [D, 3])
    w_mod = din("w_mod", [L, D, 6 * D])
    w_in = din("w_in", [L, D, 3504])
    w_out = din("w_out", [L, D, D])
    r_w2 = din("r_w2", [L, 2, 64, 512])
    r_a2 = din("r_a2", [L, 2, 64, 512])
    r_g2 = din("r_g2", [L, 160, 512])
    f_w_up = din("f_w_up", [L, D, 2 * DFF])
    f_w_down = din("f_w_down", [L, DFF, D])
    cst = din("cst", [128, NCST])
    pv = din("pv", [L, 128, NPV])
    pv64 = din("pv64", [L, 64, NPV64])
    rowv = din("rowv", [L, 1, NROW])
    rmu = din("rmu", [L, 1, 1952])
    outT = nc.dram_tensor("outT", [NBL, D, TX], F32, kind="ExternalOutput").ap()

    NS = 2 * NBL
    RESA = [dscr(f"resa{s}", [D, seqT(s)]) for s in range(NS)]
    RESB = [dscr(f"resb{s}", [D, seqT(s)]) for s in range(NS)]
    XBC = [dscr(f"xbc{s}", [D, seqT(s)]) for s in range(NS)]
    RKV = [dscr(f"rkv{s}", [24, 64, seqT(s)]) for s in range(NS)]
    XWA = [dscr(f"xwa{s}", [4, 64, seqT(s)]) for s in range(NS)]
    XG = [dscr(f"xg{s}", [160, seqT(s)]) for s in range(NS)]
    ZDT = [dscr(f"zdt{s}", [seqT(s), 528]) for s in range(NS)]
    YF = [dscr(f"yf{s}", [seqT(s), 512]) for s in range(NS)]
    OF = [dscr(f"of{s}", [seqT(s), 520]) for s in range(NS)]
    MIX = [dscr(f"mix{s}", [D, seqT(s)], BF16) for s in range(NS)]
    GATE = [dscr(f"gate{s}", [DFF, seqT(s)]) for s in range(NS)]
    VAL = [dscr(f"val{s}", [DFF, seqT(s)], BF16) for s in range(NS)]
    ACTV = [dscr(f"actv{s}", [DFF, seqT(s)], BF16) for s in range(NS)]

    def fm(ap):
        return ap.rearrange("(k p) t -> p k t", p=128)

    def sb(name, shape, dt=F32):
        return nc.alloc_sbuf_tensor(name, list(shape), dt).ap()

    CST = sb("CST", [128, NCST])
    CSTB = sb("CSTB", [128, 768], BF16)
    PV = sb("PV", [128, NPV])
    PV64 = sb("PV64", [64, NPV64])
    ROWB = sb("ROWB", [128, NROW])
    MOD = sb("MOD", [128, 48, 3])
    DER = sb("DER", [128, 4, 8, 3])
    NEGA = sb("NEGA", [128, 16])
    OMMU = sb("OMMU", [64, 8])
    em.dma('sp', CST, cst, writes=['CST'])
    em.cp('dve', CSTB, CST[:, 0:768], ['CST'], ['CSTB'])
    ident = CST[:, C_ID:C_ID + 128]
    identb = CSTB[:, C_ID:C_ID + 128]
    onesb = CSTB[:, C_ONE:C_ONE + 128]
    ones = CST[:, C_ONE:C_ONE + 128]

    def stage_scope():
        return ExitStack()

    def stage_mod(l):
        em.dma('sp', PV, pv[l], writes=['PV'])
        em.dma('sp', PV64, pv64[l], writes=['PV64'])
        em.dma('pool', ROWB, rowv[l].partition_broadcast(128), writes=['ROWB'])
        with ExitStack() as es:
            def S(name, shape, dt=F32):
                return es.enter_context(nc.sbuf_tensor(name, list(shape), dt)).ap()
            cts = S(f"cts{l}", [128, 8, 3])
            sc = S(f"sc{l}", [128, 8, 3])
            wst = [S(f"wmst{l}_{i}", [128, 8, 512]) for i in range(2)]
            ps = es.enter_context(nc.psum_tensor(f"psmod{l}", [128, 512], F32)).ap()
            em.dma('sp', cts, cT.rearrange("(k p) j -> p k j", p=128), writes=['cts'])
            em.act(sc, cts, AF.Silu, ['cts'], ['sc'])
            for g in range(12):
                w = wst[g % 2]
                wk = f"wmst{g % 2}"
                em.dma('sp' if g % 2 == 0 else 'pool', w,
                       w_mod[l][:, g * 512:(g + 1) * 512].rearrange("(k p) n -> p k n", p=128), writes=[wk])
                for mi in range(4):
                    m = g * 4 + mi
                    for k in range(8):
                        em.mm(ps[:, m * 3:(m + 1) * 3], w[:, k, mi * 128:(mi + 1) * 128], sc[:, k, :],
                              k == 0, k == 7, [wk, 'sc'], ['psmod'])
            em.tt('dve', MOD, ps[:, 0:144].rearrange("p (m j) -> p m j", j=3),
                  PV[:, PV_BMOD:PV_BMOD + 48].unsqueeze(2).to_broadcast([128, 48, 3]), ALU.add,
                  ['psmod', 'PV'], ['MOD'])
            tmp = S(f"dertmp{l}", [128, 8, 3])

            def gain(idx, goff, mlo, plus1):
                if plus1:
                    em.ts('dve', tmp, MOD[:, mlo:mlo + 8, :], 1.0, None, ALU.add, None, ['MOD'], ['dertmp'])
                    src = tmp
                    rk = ['dertmp', 'PV']
                else:
                    src = MOD[:, mlo:mlo + 8, :]
                    rk = ['MOD', 'PV']
                em.tt('dve', DER[:, idx, :, :], src,
                      PV[:, goff:goff + 8].unsqueeze(2).to_broadcast([128, 8, 3]), ALU.mult, rk, ['DER'])
            gain(0, PV_GPRE1, 8, True)
            gain(1, PV_GPOST1, 16, False)
            gain(2, PV_GPRE2, 32, True)
            gain(3, PV_GPOST2, 40, False)
            em.act(NEGA, ROWB[:, RV_ALOG:RV_ALOG + 16], AF.Exp, ['ROWB'], ['NEGA'])
            em.ts('dve', NEGA, NEGA, -1.0, None, ALU.mult, None, ['NEGA'], ['NEGA'])
            em.ts('dve', OMMU, PV64[:, P6_KA:P6_KA + 8], -1.0, 1.0, ALU.mult, ALU.add, ['PV64'], ['OMMU'])
            em.barrier()

    def prenorm(xt, TW, h, sq, rstd, ps, jmod, gidx, sidx, kx, kh, kps, sqk='sq', rsk='rstd'):
        em.act(sq[:, :, :TW], xt[:, :, :TW], AF.Square, [kx], [sqk])
        for k in range(8):
            em.mm(ps[:, :TW], onesb, sq[:, k, :TW], k == 0, k == 7, [sqk, 'CSTB'], [kps])
        em.act(rstd[:, :TW], ps[:, :TW], AF.Sqrt, [kps], [rsk], bias=EPS, scale=1.0 / D)
        em.op('dve', lambda e: e.reciprocal(out=rstd[:, :TW], in_=rstd[:, :TW]), [rsk], [rsk])
        for k in range(8):
            em.stt(xt[:, k, :TW], xt[:, k, :TW], DER[:, gidx, k, jmod:jmod + 1], rstd[:, :TW], ALU.mult, ALU.mult,
                   [kx, rsk, 'DER'], [kx])
            em.act(h[:, k, :TW], xt[:, k, :TW], AF.Identity, [kx, 'MOD'], [kh],
                   bias=MOD[:, sidx + k, jmod:jmod + 1], scale=1.0)

    def load_wbf(es, l, wap, Kc, N, name, piece=None):
        wb = es.enter_context(nc.sbuf_tensor(f"{name}bf{l}", [128, Kc, N], BF16)).ap()
        with ExitStack() as e2:
            sts = [e2.enter_context(nc.sbuf_tensor(f"{name}st{l}_{i}", [128, N], F32)).ap() for i in range(2)]
            for k in range(Kc):
                st = sts[k % 2]
                sk = f"{name}st{k % 2}"
                em.dma('sp' if k % 2 == 0 else 'pool', st, wap[k * 128:(k + 1) * 128, :], writes=[sk])
                em.cp('act' if k % 2 == 0 else 'dve', wb[:, k, :], st, [sk], [name + 'bf'])
            em.barrier()
        return wb

    def res_src(l, s, phase):
        b = s // 2
        if phase == 0:
            if l == 0:
                return (xT[b] if s % 2 == 1 else ctxT[b]), f"in{s}"
            return RESB[s], f"resb{s}"
        return RESA[s], f"resa{s}"

    def res_dst(l, s, phase):
        b = s // 2
        if phase == 0:
            return RESA[s], f"resa{s}"
        if l == n_layers - 1 and s % 2 == 1:
            return outT[b], f"out{s}"
        return RESB[s], f"resb{s}"

    def stage_inproj(l):
        with ExitStack() as es:
            def S(name, shape, dt=F32):
                return es.enter_context(nc.sbuf_tensor(name, list(shape), dt)).ap()
            wb = load_wbf(es, l, w_in[l], 8, 3504, "win")
            w2 = S(f"ipw2{l}", [128, 8, 1952], BF16)
            with ExitStack() as e2:
                mub = e2.enter_context(nc.sbuf_tensor(f"ipmub{l}", [128, 1952], F32)).ap()
                em.dma('sp', mub, rmu[l].partition_broadcast(128), writes=['ipmub'])
                for k in range(8):
                    em.stt(w2[:, k, :], mub, 0.5, wb[:, k, 1552:3504], ALU.mult, ALU.mult, ['ipmub', 'winbf'], ['ipw2'])
                em.ts('dve', mub, mub, -1.0, 1.0, ALU.mult, ALU.add, ['ipmub'], ['ipmub'])
                for k in range(8):
                    em.tt('dve', wb[:, k, 1552:3504], wb[:, k, 1552:3504], mub, ALU.mult, ['ipmub', 'winbf', 'ipw2'], ['winbf'])
                em.barrier()
            xt = S(f"ipx{l}", [128, 8, 512])
            h = S(f"iph{l}", [128, 8, 512], BF16)
            sq = S(f"ipsq{l}", [128, 8, 512], BF16)
            hs = sq
            rstd = S(f"iprs{l}", [128, 512])
            xh = S(f"ipxh{l}", [128, 8, 2])
            hh = S(f"iphh{l}", [128, 8, 2], BF16)
            sqh = S(f"ipsqh{l}", [128, 8, 2], BF16)
            rstdh = S(f"iprsh{l}", [128, 2])
            sta = [S(f"ipsta{l}_{i}", [128, 8, 512]) for i in range(2)]
            stw = S(f"ipstw{l}", [64, 4, 512])
            stg0 = S(f"ipstg0{l}", [128, 512])
            stg1 = S(f"ipstg1{l}", [32, 512])
            stz = S(f"ipstz{l}", [128, 4, 528])
            pss = [es.enter_context(nc.psum_tensor(f"ipps{l}_{i}", [128, 512], F32)).ap() for i in range(8)]
            groups = [('xbc', 512, 128, 8), ('r', 1552, 64, 8), ('k', 2064, 64, 8), ('v', 2576, 64, 8)]
            ev = 0
            for s in range(NS):
                T = seqT(s)
                TW = min(512, T)
                jmod = 2 if s % 2 == 0 else s // 2
                src, srck = res_src(l, s, 0)
                for tt_ in range(T // TW):
                    t0 = tt_ * TW
                    em.dma('sp', xt[:, :, :TW], fm(src)[:, :, t0:t0 + TW], reads=[srck], writes=['ipx'])
                    cl = max(t0 - 1, 0)
                    cr = min(t0 + TW, T - 1)
                    em.dma('sp', xh[:, :, 0:1], fm(src)[:, :, cl:cl + 1], reads=[srck], writes=['ipxh'], allow_slow_non_contiguous=True)
                    em.dma('sp', xh[:, :, 1:2], fm(src)[:, :, cr:cr + 1], reads=[srck], writes=['ipxh'], allow_slow_non_contiguous=True)
                    prenorm(xt, TW, h, sq, rstd, pss[7], jmod, 0, 0, 'ipx', 'iph', 'ps7')
                    prenorm(xh, 2, hh, sqh, rstdh, pss[6], jmod, 0, 0, 'ipxh', 'iphh', 'ps6', sqk='ipsqh', rsk='iprsh')
                    if t0 == 0:
                        em.memset('pool', hh[:, :, 0:1], 0.0, ['iphh'])
                    if t0 + TW == T:
                        em.memset('pool', hh[:, :, 1:2], 0.0, ['iphh'])
                    em.tt('dve', hs[:, :, 1:TW - 1], h[:, :, 0:TW - 2], h[:, :, 2:TW], ALU.add, ['iph', 'sq'], ['sq'])
                    em.tt('dve', hs[:, :, 0:1], hh[:, :, 0:1], h[:, :, 1:2], ALU.add, ['iph', 'iphh', 'sq'], ['sq'])
                    em.tt('dve', hs[:, :, TW - 1:TW], h[:, :, TW - 2:TW - 1], hh[:, :, 1:2], ALU.add, ['iph', 'iphh', 'sq'], ['sq'])
                    pi = 0
                    for gi, (gname, c0, wdt, nb) in enumerate(groups):
                        st = sta[gi % 2]
                        stk = f"ipsta{gi % 2}"
                        for j in range(nb):
                            ps = pss[pi % 6]
                            pk = f"ps{pi % 6}"
                            pi += 1
                            cc = c0 + j * wdt
                            shifted = (gname != 'xbc')
                            for k in range(8):
                                em.mm(ps[:wdt, :TW], wb[:, k, cc:cc + wdt], h[:, k, :TW], k == 0, (k == 7) and not shifted,
                                      ['winbf', 'iph'], [pk])
                            if shifted:
                                for k in range(8):
                                    em.mm(ps[:wdt, :TW], w2[:, k, cc - 1552:cc - 1552 + wdt], hs[:, k, :TW], False, k == 7,
                                          ['ipw2', 'sq'], [pk])
                            em.cp('act' if ev % 2 == 0 else 'dve', st[:wdt, j, :TW], ps[:wdt, :TW], [pk], [stk])
                            ev += 1
                        if gname == 'xbc':
                            em.dma('pool', fm(XBC[s])[:, :, t0:t0 + TW], st[:, :, :TW], reads=[stk], writes=[f"xbc{s}"])
                        else:
                            jb = {'r': 0, 'k': 8, 'v': 16}[gname]
                            em.dma('pool', RKV[s][jb:jb + 8].rearrange("j p t -> p j t")[:, :, t0:t0 + TW],
                                   st[:64, :, :TW], reads=[stk], writes=[f"rkv{s}"])
                    for j in range(4):
                        ps = pss[pi % 6]
                        pk = f"ps{pi % 6}"
                        pi += 1
                        cc = 3088 + j * 64
                        for k in range(8):
                            em.mm(ps[:64, :TW], wb[:, k, cc:cc + 64], h[:, k, :TW], k == 0, False, ['winbf', 'iph'], [pk])
                        for k in range(8):
                            em.mm(ps[:64, :TW], w2[:, k, cc - 1552:cc - 1552 + 64], hs[:, k, :TW], False, k == 7, ['ipw2', 'sq'], [pk])
                        em.cp('act' if ev % 2 == 0 else 'dve', stw[:, j, :TW], ps[:64, :TW], [pk], ['ipstw'])
                        ev += 1
                    em.dma('pool', XWA[s].rearrange("j p t -> p j t")[:, :, t0:t0 + TW], stw[:, :, :TW],
                           reads=['ipstw'], writes=[f"xwa{s}"])
                    for (cc, wdt, st, stk, r0) in [(3344, 128, stg0, 'ipstg0', 0), (3472, 32, stg1, 'ipstg1', 128)]:
                        ps = pss[pi % 6]
                        pk = f"ps{pi % 6}"
                        pi += 1
                        for k in range(8):
                            em.mm(ps[:wdt, :TW], wb[:, k, cc:cc + wdt], h[:, k, :TW], k == 0, False, ['winbf', 'iph'], [pk])
                        for k in range(8):
                            em.mm(ps[:wdt, :TW], w2[:, k, cc - 1552:cc - 1552 + wdt], hs[:, k, :TW], False, k == 7, ['ipw2', 'sq'], [pk])
                        em.cp('act' if ev % 2 == 0 else 'dve', st[:wdt, :TW], ps[:wdt, :TW], [pk], [stk])
                        ev += 1
                        em.dma('pool', XG[s][r0:r0 + wdt, t0:t0 + TW], st[:wdt, :TW], reads=[stk], writes=[f"xg{s}"])
                    for i in range(TW // 128):
                        ps = pss[pi % 6]
                        pk = f"ps{pi % 6}"
                        pi += 1
                        ps2 = pss[6]
                        for k in range(8):
                            em.mm(ps[:, 0:512], h[:, k, i * 128:(i + 1) * 128], wb[:, k, 0:512], k == 0, k == 7,
                                  ['winbf', 'iph'], [pk])
                        for k in range(8):
                            em.mm(ps2[:, 0:16], h[:, k, i * 128:(i + 1) * 128], wb[:, k, 1536:1552], k == 0, k == 7,
                                  ['winbf', 'iph'], ['ps6'])
                        em.cp('act', stz[:, i, 0:512], ps[:, 0:512], [pk], ['ipstz'])
                        em.cp('dve', stz[:, i, 512:528], ps2[:, 0:16], ['ps6'], ['ipstz'])
                    em.dma('pool', ZDT[s][t0:t0 + TW, :].rearrange("(i p) c -> p i c", p=128), stz[:, :TW // 128, :],
                           reads=['ipstz'], writes=[f"zdt{s}"])
            em.barrier()

    def stage_mixer(l, need_ctx_out):
        with ExitStack() as es:
            def S(name, shape, dt=F32):
                return es.enter_context(nc.sbuf_tensor(name, list(shape), dt)).ap()
            w2b = S(f"w2b{l}", [64, 2, 512], BF16)
            a2b = S(f"a2b{l}", [64, 2, 512], BF16)
            g2b0 = S(f"g2b0{l}", [128, 512], BF16)
            g2b1 = S(f"g2b1{l}", [32, 512], BF16)
            with ExitStack() as e2:
                t1 = e2.enter_context(nc.sbuf_tensor(f"lst1{l}", [64, 2, 512], F32)).ap()
                t2 = e2.enter_context(nc.sbuf_tensor(f"lst2{l}", [64, 2, 512], F32)).ap()
                t3 = e2.enter_context(nc.sbuf_tensor(f"lst3{l}", [128, 512], F32)).ap()
                t4 = e2.enter_context(nc.sbuf_tensor(f"lst4{l}", [32, 512], F32)).ap()
                em.dma('sp', t1, r_w2[l].rearrange("d r c -> r d c"), writes=['lst1'])
                em.dma('sp', t2, r_a2[l].rearrange("d r c -> r d c"), writes=['lst2'])
                em.dma('sp', t3, r_g2[l][0:128, :], writes=['lst3'])
                em.dma('sp', t4, r_g2[l][128:160, :], writes=['lst4'])
                em.cp('dve', w2b, t1, ['lst1'], ['w2b'])
                em.cp('dve', a2b, t2, ['lst2'], ['a2b'])
                em.cp('dve', g2b0, t3, ['lst3'], ['g2b'])
                em.cp('dve', g2b1, t4, ['lst4'], ['g2b'])
                em.barrier()
            MAR = [None, None]
            MARt = S(f"mar{l}", [128, 2, 256])
            em.cp('dve', MARt[:, 0, 0:128], CST[:, C_UTS:C_UTS + 128], ['CST'], ['MAR'])
            em.cp('dve', MARt[:, 0, 128:256], CST[:, C_UTI:C_UTI + 128], ['CST'], ['MAR'])
            em.cp('dve', MARt[:, 1, 0:128], CST[:, C_LTS:C_LTS + 128], ['CST'], ['MAR'])
            em.cp('dve', MARt[:, 1, 128:256], CST[:, C_LTI:C_LTI + 128], ['CST'], ['MAR'])
            MSO = S(f"mso{l}", [128, 2, 256])
            em.cp('dve', MSO[:, 0, 0:128], CST[:, C_LTS:C_LTS + 128], ['CST'], ['MSO'])
            em.cp('dve', MSO[:, 1, 0:128], CST[:, C_UTS:C_UTS + 128], ['CST'], ['MSO'])
            em.cp('dve', MSO[:, 0, 128:256], ones, ['CST'], ['MSO'])
            em.cp('dve', MSO[:, 1, 128:256], ones, ['CST'], ['MSO'])
            RMK = S(f"rmk{l}", [64, 8, 128], BF16)
            em.memset('pool', RMK, 1.0, ['RMK'])
            em.memset('pool', RMK[:, :, 0:1], 0.0, ['RMK'])

            def MI(d):
                return CST[:, C_UTI:C_UTI + 128] if d == 0 else CST[:, C_LTI:C_LTI + 128]

            def strictTS(d):
                return CST[:, C_LTS:C_LTS + 128] if d == 0 else CST[:, C_UTS:C_UTS + 128]

            xbc = S(f"m_xbc{l}", [128, 8, 130])
            cacc = S(f"m_cacc{l}", [128, 8, 128])
            bct = S(f"m_bct{l}", [128, 4, 128], BF16)
            xtok = S(f"m_xtok{l}", [128, 512])
            xtokb = S(f"m_xtokb{l}", [128, 512], BF16)
            btokb = S(f"m_btokb{l}", [128, 256], BF16)
            zdt = S(f"m_zdt{l}", [128, 528])
            dts = S(f"m_dts{l}", [128, 8])
            dta = S(f"m_dta{l}", [128, 8])
            sm = S(f"m_sm{l}", [128, 40])
            xw = S(f"m_xw{l}", [128, 512], BF16)
            l2 = [S(f"m_l2{l}_{i}", [128, 256]) for i in range(2)]
            Et = [S(f"m_E{l}_{i}", [128, 256]) for i in range(2)]
            LTt = [S(f"m_LT{l}_{i}", [128, 128]) for i in range(2)]
            STt = [S(f"m_ST{l}_{i}", [128, 128], BF16) for i in range(2)]
            CsT = [S(f"m_Cs{l}_{i}", [128, 128], BF16) for i in range(2)]
            hst = S(f"m_hst{l}", [128, 512])
            hstb = S(f"m_hstb{l}", [128, 512], BF16)
            yt = S(f"m_y{l}", [128, 512])
            yf = S(f"m_yf{l}", [128, 512])
            zs = S(f"m_zs{l}", [128, 512])
            ysq = S(f"m_ysq{l}", [128, 512])
            gst = S(f"m_gst{l}", [128, 4])
            mixo = S(f"mixo{l}", [128, 8, 128], BF16)
            ssum = S(f"r_ssum{l}", [64, 26, 128])
            pp = ssum
            xg0 = S(f"r_xg0{l}", [128, 130])
            xg1 = S(f"r_xg1{l}", [32, 130])
            sg0 = S(f"r_sg0{l}", [128, 128], BF16)
            sg1 = S(f"r_sg1{l}", [32, 128], BF16)
            xgt = S(f"r_xgt{l}", [128, 128])
            twb = S(f"r_twb{l}", [64, 2, 128], BF16)
            lw = S(f"r_lw{l}", [64, 8, 128])
            aa = S(f"r_aa{l}", [64, 8, 128])
            kk = S(f"r_kk{l}", [64, 8, 128])
            kd = S(f"r_kd{l}", [64, 8, 128])
            linc = S(f"r_linc{l}", [64, 8, 128])
            lex = S(f"r_lex{l}", [64, 8, 128])
            rinv = lex
            e1 = S(f"r_e1{l}", [64, 8, 128])
            e0 = S(f"r_e0{l}", [64, 8, 128])
            ei = S(f"r_ei{l}", [64, 8, 128])
            gCs = [S(f"r_gC{l}_{i}", [64, 8]) for i in range(2)]
            coefs = [S(f"r_coef{l}_{i}", [128, 8]) for i in range(2)]
            tmpk = S(f"r_tmpk{l}", [64, 8, 128])
            ARs = [S(f"r_AR{l}_{i}", [64, 8, 256], BF16) for i in range(2)]
            BKs = [S(f"r_BK{l}_{i}", [64, 8, 2, 128], BF16) for i in range(2)]
            prodb = S(f"r_prod{l}", [64, 8, 128], BF16)
            sqb = prodb
            vtokbs = [S(f"r_vtokb{l}_{i}", [128, 512], BF16) for i in range(2)]
            BKtoks = [S(f"r_BKtok{l}_{i}", [128, 8, 2, 64], BF16) for i in range(2)]
            GBs = [S(f"r_GB{l}_{i}", [128, 8, 256], BF16) for i in range(2)]
            GKs = [S(f"r_GK{l}_{i}", [128, 8, 256], BF16) for i in range(2)]
            Q0s = [S(f"r_Q0{l}_{i}", [128, 8, 128], BF16) for i in range(2)]
            Pm = [S(f"r_P{l}_{i}", [128, 8, 128], BF16) for i in range(2)]
            Qm = [S(f"r_Q{l}_{i}", [128, 8, 128], BF16) for i in range(2)]
            Ym = [S(f"r_Y{l}_{i}", [128, 8, 128], BF16) for i in range(2)]
            E1m = S(f"r_E1{l}", [128, 8, 128], BF16)
            E2m = S(f"r_E2{l}", [128, 8, 128], BF16)
            Dm = S(f"r_D{l}", [128, 8, 128], BF16)
            Zm = S(f"r_Z{l}", [128, 8, 128], BF16)
            Wt = S(f"r_W{l}", [128, 512], BF16)
            Ut = S(f"r_U{l}", [128, 512], BF16)
            Hs = S(f"r_H{l}", [64, 512])
            Hb = S(f"r_Hb{l}", [64, 512], BF16)
            ot = S(f"r_o{l}", [128, 520])
            oft = S(f"r_of{l}", [128, 520])
            osq = S(f"r_osq{l}", [128, 512])
            gn = S(f"r_gn{l}", [128, 40])
            B = [es.enter_context(nc.psum_tensor(f"mxps{l}_{i}", [128, 512], F32)).ap() for i in range(7)]
            BBp = es.enter_context(nc.psum_tensor(f"mxpsb{l}", [128, 1024], BF16)).ap()

            nbw = ROWB[:, RV_MNW:RV_MNW + 512]
            lnw = ROWB[:, RV_LNW:RV_LNW + 512]
            lnb = ROWB[:, RV_LNB:RV_LNB + 512]
            mdb = ROWB[:, RV_MD:RV_MD + 8]
            evc = [0]

            def evq():
                evc[0] += 1
                return 'act' if evc[0] % 2 == 0 else 'dve'

            def load_halo(tile, tk, srcap, srck, t0, T, nblk_dims):
                lo = max(t0 - 1, 0)
                hi = min(t0 + 129, T)
                o = lo - (t0 - 1)
                if nblk_dims:
                    em.dma('sp', tile[:, :, o:o + hi - lo], srcap[:, :, lo:hi], reads=[srck], writes=[tk])
                    if t0 == 0:
                        em.memset('pool', tile[:, :, 0:1], 0.0, [tk])
                    if t0 + 128 == T:
                        em.memset('pool', tile[:, :, 129:130], 0.0, [tk])
                else:
                    em.dma('sp', tile[:, o:o + hi - lo], srcap[:, lo:hi], reads=[srck], writes=[tk])
                    if t0 == 0:
                        em.memset('pool', tile[:, 0:1], 0.0, [tk])
                    if t0 + 128 == T:
                        em.memset('pool', tile[:, 129:130], 0.0, [tk])

            def mamba_chunk(s, t0, d, emit_out):
                T = seqT(s)
                load_halo(xbc, 'm_xbc', fm(XBC[s]), f"xbc{s}", t0, T, True)
                em.dma('pool', zdt, ZDT[s][t0:t0 + 128, :], reads=[f"zdt{s}"], writes=['m_zdt'])
                if (mixcfg or {}).get('mstop', 99) <= 1:
                    return
                for j in range(8):
                    em.ts('dve', cacc[:, j, :], xbc[:, j, 1:129], PV[:, PV_MCW + 8 + j:PV_MCW + 9 + j],
                          PV[:, PV_MCB + j:PV_MCB + j + 1], ALU.mult, ALU.add, ['m_xbc', 'PV'], ['m_cacc'])
                    em.stt(cacc[:, j, :], xbc[:, j, 0:128], PV[:, PV_MCW + j:PV_MCW + j + 1], cacc[:, j, :],
                           ALU.mult, ALU.add, ['m_xbc', 'PV', 'm_cacc'], ['m_cacc'])
                    em.stt(cacc[:, j, :], xbc[:, j, 2:130], PV[:, PV_MCW + 16 + j:PV_MCW + 17 + j], cacc[:, j, :],
                           ALU.mult, ALU.add, ['m_xbc', 'PV', 'm_cacc'], ['m_cacc'])
                em.act(cacc[:, 0:6, :], cacc[:, 0:6, :], AF.Silu, ['m_cacc'], ['m_cacc'])
                em.act(bct[:, 2:4, :], cacc[:, 6:8, :], AF.Silu, ['m_cacc'], ['m_bct'])
                em.cp('dve', bct[:, 0:2, :], cacc[:, 4:6, :], ['m_cacc'], ['m_bct'])
                if (mixcfg or {}).get('mstop', 99) <= 2:
                    return
                msub = (mixcfg or {}).get('msub', 9)
                for j in range(4):
                    em.tr(B[4][:, j * 128:(j + 1) * 128], cacc[:, j, :], ident, ['m_cacc', 'CST'], ['B4'], inc=(j == 3))
                if msub >= 1:
                    for j in range(2):
                        em.tr(B[5][:, j * 128:(j + 1) * 128], cacc[:, 4 + j, :], ident, ['m_cacc', 'CST'], ['B5'])
                if msub >= 2 and msub != 33:
                    em.cp('dve', xtok, B[4], ['B4'], ['m_xtok'])
                if msub == 30:
                    em.cp('dve', xtokb, B[4], ['B4'], ['m_xtokb'])
                elif msub == 31:
                    em.cp('act', yt, B[4], ['B4'], ['m_y'])
                elif msub == 32:
                    em.cp('act', xtokb, xtok, ['m_xtok'], ['m_xtokb'])
                elif msub >= 3:
                    em.cp('act', xtokb, B[4], ['B4'], ['m_xtokb'])
                if msub >= 4:
                    em.cp('act', btokb, B[5][:, 0:256], ['B5'], ['m_btokb'])
                if (mixcfg or {}).get('mstop', 99) <= 3:
                    return
                em.tt('dve', dts, zdt[:, 512 + d * 8:520 + d * 8], ROWB[:, RV_DTB + d * 8:RV_DTB + d * 8 + 8], ALU.add,
                      ['m_zdt', 'ROWB'], ['m_dts'])
                em.act(dts, dts, AF.Exp, ['m_dts'], ['m_dts'])
                em.act(dts, dts, AF.Ln, ['m_dts'], ['m_dts'], bias=1.0, scale=1.0)
                em.tt('dve', dta, dts, NEGA[:, d * 8:d * 8 + 8], ALU.mult, ['m_dts', 'NEGA'], ['m_dta'])
                if (mixcfg or {}).get('mstop', 99) <= 4:
                    return
                em.mm(B[5][:, 256:264], MI(d), dta, True, True, ['CST', 'm_dta'], ['B5'])
                em.mm(B[5][:, 264:272], ones, dta, True, True, ['CST', 'm_dta'], ['B5'])
                em.cp('act', sm[:, 32:40], B[5][:, 264:272], ['B5'], ['m_sm'])
                em.tt('dve', sm[:, 24:32], sm[:, 32:40], B[5][:, 256:264], ALU.subtract, ['B5', 'm_sm'], ['m_sm'])
                em.act(sm[:, 0:8], sm[:, 24:32], AF.Exp, ['m_sm'], ['m_sm'])
                em.tt('dve', sm[:, 8:16], sm[:, 0:8], dts, ALU.mult, ['m_sm', 'm_dts'], ['m_sm'])
                em.act(sm[:, 16:24], B[5][:, 264:272], AF.Exp, ['B5'], ['m_sm'])
                em.tt('dve', xw.rearrange("p (h q) -> p h q", q=64), xtok.rearrange("p (h q) -> p h q", q=64),
                      sm[:, 8:16].unsqueeze(2).to_broadcast([128, 8, 64]), ALU.mult, ['m_xtok', 'm_sm'], ['m_xw'])
                if (mixcfg or {}).get('mstop', 99) <= 5:
                    return
                for g in range(2):
                    em.mm(B[6][:, g * 128:(g + 1) * 128], bct[:, g, :], bct[:, 2 + g, :], True, True, ['m_bct'], ['B6'])
                for h in range(8):
                    g = h // 4
                    i2 = h % 2
                    pe_ = B[6][:, 256:512] if i2 == 0 else B[5][:, 0:256]
                    pek = 'B6' if i2 == 0 else 'B5'
                    em.ts('dve', l2[i2], MSO[:, d, :], dta[:, h:h + 1], None, ALU.mult, None,
                          ['MSO', 'm_dta'], [f"m_l2{i2}"])
                    em.mm(pe_[:, 0:128], l2[i2][:, 0:128], MI(d), True, True, [f"m_l2{i2}", 'CST'], [pek])
                    em.mm(pe_[:, 128:256], l2[i2][:, 128:256], MI(d), True, True, [f"m_l2{i2}", 'CST'], [pek])
                    em.act(Et[i2], pe_[:, 0:256], AF.Exp, [pek], [f"m_E{i2}"])
                    em.stt(LTt[i2], Et[i2][:, 0:128], dts[:, h:h + 1], MI(d), ALU.mult, ALU.mult,
                           [f"m_E{i2}", 'm_dts', 'CST'], [f"m_LT{i2}"])
                    em.tt('dve', STt[i2], B[6][:, g * 128:(g + 1) * 128], LTt[i2], ALU.mult, ['B6', f"m_LT{i2}"],
                          [f"m_ST{i2}"])
                    em.tt('dve', CsT[i2], bct[:, 2 + g, :], Et[i2][:, 128:256], ALU.mult, ['m_bct', f"m_E{i2}"],
                          [f"m_Cs{i2}"])
                    em.mm(B[4][:, h * 64:(h + 1) * 64], STt[i2], xtokb[:, h * 64:(h + 1) * 64], True, False,
                          [f"m_ST{i2}", 'm_xtokb'], ['B4'])
                    em.mm(B[4][:, h * 64:(h + 1) * 64], CsT[i2], hstb[:, h * 64:(h + 1) * 64], False, True,
                          [f"m_Cs{i2}", 'm_hstb'], ['B4'])
                if (mixcfg or {}).get('mstop', 99) <= 6:
                    return
                for g in range(2):
                    em.mm(B[6][:, g * 256:(g + 1) * 256], btokb[:, g * 128:(g + 1) * 128], xw[:, g * 256:(g + 1) * 256],
                          True, True, ['m_btokb', 'm_xw'], ['B6'])
                em.tt('dve', hst.rearrange("p (h q) -> p h q", q=64), hst.rearrange("p (h q) -> p h q", q=64),
                      sm[:, 16:24].unsqueeze(2).to_broadcast([128, 8, 64]), ALU.mult, ['m_hst', 'm_sm', 'B4'], ['m_hst'])
                em.tt('dve', hst, hst, B[6], ALU.add, ['m_hst', 'B6'], ['m_hst'])
                em.cp('act', hstb, hst, ['m_hst', 'B4'], ['m_hstb'])
                if (mixcfg or {}).get('mstop', 99) <= 7:
                    return
                if d == 0:
                    if emit_out:
                        em.cp('act', yt, B[4], ['B4'], ['m_y'])
                        em.dma('pool', YF[s][t0:t0 + 128, :], yt, reads=['m_y'], writes=[f"yf{s}"])
                elif emit_out:
                    em.dma('sp', yf, YF[s][t0:t0 + 128, :], reads=[f"yf{s}"], writes=['m_yf'])
                    em.tt('dve', yt, B[4], yf, ALU.add, ['B4', 'm_yf'], ['m_y'])
                    em.tt('dve', yf.rearrange("p (h q) -> p h q", q=64), xtok.rearrange("p (h q) -> p h q", q=64),
                          mdb.unsqueeze(2).to_broadcast([128, 8, 64]), ALU.mult, ['m_xtok', 'ROWB', 'm_yf'], ['m_yf'])
                    em.tt('dve', yt, yt, yf, ALU.add, ['m_y', 'm_yf'], ['m_y'])
                    em.act(zs, zdt[:, 0:512], AF.Silu, ['m_zdt'], ['m_zs'])
                    em.tt('dve', yt, yt, zs, ALU.mult, ['m_y', 'm_zs'], ['m_y'])
                    for g in range(2):
                        em.act(ysq[:, g * 256:(g + 1) * 256], yt[:, g * 256:(g + 1) * 256], AF.Square, ['m_y'],
                               ['m_ysq', 'm_gst'], accum=gst[:, g:g + 1])
                    em.act(gst[:, 2:4], gst[:, 0:2], AF.Sqrt, ['m_gst'], ['m_gst'], bias=EPS, scale=1.0 / 256)
                    em.op('dve', lambda e: e.reciprocal(out=gst[:, 2:4], in_=gst[:, 2:4]), ['m_gst'], ['m_gst'])
                    for g in range(2):
                        em.stt(yt[:, g * 256:(g + 1) * 256], yt[:, g * 256:(g + 1) * 256], gst[:, 2 + g:3 + g],
                               nbw[:, g * 256:(g + 1) * 256], ALU.mult, ALU.mult, ['m_y', 'm_gst', 'ROWB'], ['m_y'])
                    for j in range(4):
                        em.tr(B[5][:, j * 128:(j + 1) * 128], yt[:, j * 128:(j + 1) * 128], ident, ['m_y', 'CST'], ['B5'], inc=(j == 3))
                    em.cp('act', mixo[:, 0:4, :], B[5].rearrange("p (j t) -> p j t", t=128), ['B5'], ['mixo_m'])
                    em.dma('pool', fm(MIX[s])[:, 0:4, t0:t0 + 128], mixo[:, 0:4, :], reads=['mixo_m'], writes=[f"mixm{s}"])

            def wkv_chunk(s, t0, d, emit_out, idx=0, first_of_d=False, split=True):
                T = seqT(s)
                pb = idx % 2
                AR, BK, GB, GK, Q0, BKtok, vtokb, gC, coef = (ARs[pb], BKs[pb], GBs[pb], GKs[pb], Q0s[pb], BKtoks[pb],
                                                             vtokbs[pb], gCs[pb], coefs[pb])
                kAR, kBK, kGB, kGK, kQ0, kBKtok, kvtokb, kgC, kcoef = [f"{n}{pb}" for n in
                                                                      ('r_AR', 'r_BK', 'r_GB', 'r_GK', 'r_Q0h', 'r_BKtok',
                                                                       'r_vtokb', 'r_gC', 'r_coef')]
                if split:
                    em.stream = ('rp', idx)
                fin = (d == 1 and emit_out)
                em.dma('sp', pp[:, 0:24, :], RKV[s].rearrange("j p t -> p j t")[:, :, t0:t0 + 128], reads=[f"rkv{s}"], writes=['r_ssum'])
                em.dma('sp', pp[:, 24:25, :], XWA[s][d:d + 1].rearrange("j p t -> p j t")[:, :, t0:t0 + 128], reads=[f"xwa{s}"],
                       writes=['r_ssum'])
                em.dma('sp', pp[:, 25:26, :], XWA[s][2 + d:3 + d].rearrange("j p t -> p j t")[:, :, t0:t0 + 128], reads=[f"xwa{s}"],
                       writes=['r_ssum'])
                rr = pp[:, 0:8, :]
                kr = pp[:, 8:16, :]
                vr = pp[:, 16:24, :]
                em.act(twb[:, 0, :], pp[:, 24, :], AF.Tanh, ['r_ssum'], ['r_twb'])
                em.cp('dve', twb[:, 1, :], pp[:, 25, :], ['r_ssum'], ['r_twb'])
                for h in range(8):
                    em.mm(B[h // 4][0:64, (h % 4) * 128:(h % 4 + 1) * 128], w2b[:, d, h * 64:(h + 1) * 64], twb[:, 0, :], True, True,
                          ['w2b', 'r_twb'], [f"B{h // 4}"], inc=(h == 7))
                for h in range(8):
                    em.act(lw[:, h, :], B[h // 4][0:64, (h % 4) * 128:(h % 4 + 1) * 128], AF.Sigmoid, [f"B{h // 4}", 'PV64'],
                           ['r_lw'], bias=PV64[:, P6_W0 + d * 8 + h:P6_W0 + d * 8 + h + 1], scale=1.0)
                for h in range(8):
                    em.mm(B[h // 4][0:64, (h % 4) * 128:(h % 4 + 1) * 128], a2b[:, d, h * 64:(h + 1) * 64], twb[:, 1, :], True, True,
                          ['a2b', 'r_twb'], [f"B{h // 4}"], inc=(h == 7))
                for h in range(8):
                    em.act(aa[:, h, :], B[h // 4][0:64, (h % 4) * 128:(h % 4 + 1) * 128], AF.Sigmoid,
                           [f"B{h // 4}", 'PV64'], ['r_aa'], bias=PV64[:, P6_A0 + d * 8 + h:P6_A0 + d * 8 + h + 1], scale=1.0)
                em.ts('dve', lw, lw, -R_DECAY_SCALE, None, ALU.mult, None, ['r_lw'], ['r_lw'])
                em.tt('dve', kk, kr, PV64[:, P6_KK:P6_KK + 8].unsqueeze(2).to_broadcast([64, 8, 128]), ALU.mult,
                      ['r_ssum', 'PV64'], ['r_kk'])
                em.act(sqb, kk, AF.Square, ['r_kk'], ['r_prod'])
                for hh in range(2):
                    em.mm(B[hh][0:64, :], onesb[0:64, 0:64], sqb[:, hh * 4:(hh + 1) * 4, :], True, True,
                          ['CSTB', 'r_prod'], [f"B{hh}"])
                for hh in range(2):
                    em.act(rinv[:, hh * 4:(hh + 1) * 4, :], B[hh][0:64, :].rearrange("p (h t) -> p h t", t=128), AF.Sqrt,
                           [f"B{hh}"], ['r_lex'])
                em.ts('dve', rinv, rinv, 1e-12, None, ALU.max, None, ['r_lex'], ['r_lex'])
                em.op('dve', lambda e: e.reciprocal(out=rinv, in_=rinv), ['r_lex'], ['r_lex'])
                em.tt('dve', kk, kk, rinv, ALU.mult, ['r_kk', 'r_lex'], ['r_kk'])
                em.tt('dve', tmpk, aa, PV64[:, P6_KA:P6_KA + 8].unsqueeze(2).to_broadcast([64, 8, 128]), ALU.mult,
                      ['r_aa', 'PV64'], ['r_tmpk'])
                em.tt('dve', tmpk, tmpk, OMMU.unsqueeze(2).to_broadcast([64, 8, 128]), ALU.add, ['r_tmpk', 'OMMU'], ['r_tmpk'])
                em.tt('dve', kd, kr, tmpk, ALU.mult, ['r_ssum', 'r_tmpk'], ['r_kd'])
                em.op('dve', lambda e: e.tensor_tensor_scan(out=linc.rearrange("p h t -> p (h t)"),
                                                            data0=RMK.rearrange("p h t -> p (h t)"),
                                                            data1=lw.rearrange("p h t -> p (h t)"), initial=0.0,
                                                            op0=ALU.mult, op1=ALU.add), ['RMK', 'r_lw'], ['r_linc'])
                if d == 0:
                    tot = linc[:, :, 127:128]
                else:
                    em.tt('dve', lex, lw, linc, ALU.subtract, ['r_lw', 'r_linc'], ['r_lex'])
                    em.cp('dve', gC, linc[:, :, 127], ['r_linc'], [kgC])
                    em.tt('dve', linc, lex, gC.unsqueeze(2).to_broadcast([64, 8, 128]), ALU.add, ['r_lex', kgC, 'r_linc'],
                          ['r_linc'])
                    tot = linc[:, :, 0:1]
                em.tt('dve', lex, linc, lw, ALU.subtract, ['r_linc', 'r_lw'], ['r_lex'])
                em.act(e1, linc, AF.Exp, ['r_linc'], ['r_e1'])
                em.act(e0, lex, AF.Exp, ['r_lex'], ['r_e0'])
                em.act(ei, linc, AF.Exp, ['r_linc'], ['r_ei'], scale=-1.0)
                em.act(gC, tot.rearrange("p h o -> p (h o)"), AF.Exp, ['r_linc', kgC], [kgC])
                em.tt('dve', AR[:, :, 128:256], rr, e1, ALU.mult, ['r_ssum', 'r_e1'], [kAR])
                em.stt(AR[:, :, 0:128], kk, -1.0, e0, ALU.mult, ALU.mult, ['r_kk', 'r_e0'], [kAR])
                em.tt('dve', tmpk, kk, aa, ALU.mult, ['r_kk', 'r_aa', 'r_tmpk'], ['r_tmpk'])
                em.tt('dve', BK[:, :, 0, :], tmpk, ei, ALU.mult, ['r_tmpk', 'r_ei'], [kBK])
                em.tt('dve', BK[:, :, 1, :], kd, ei, ALU.mult, ['r_kd', 'r_ei'], [kBK])
                em.tt('dve', tmpk, rr, kd, ALU.mult, ['r_ssum', 'r_kd', 'r_tmpk'], ['r_tmpk'])
                em.tt('dve', prodb, tmpk, PV64[:, P6_RK:P6_RK + 8].unsqueeze(2).to_broadcast([64, 8, 128]), ALU.mult,
                      ['r_tmpk', 'PV64'], ['r_prod'])
                for h in range(8):
                    em.tr(B[0][:, h * 64:(h + 1) * 64], vr[:, h, :], ident[0:64, 0:64], ['r_ssum', 'CST'], ['B0'], inc=(h == 7))
                em.cp('act', vtokb, B[0], ['B0'], [kvtokb])
                for half in range(2):
                    for h4 in range(4):
                        for q in range(2):
                            em.tr(BBp[:, (h4 * 2 + q) * 64:(h4 * 2 + q + 1) * 64], BK[:, half * 4 + h4, q, :], identb[0:64, 0:64],
                                  [kBK, 'CSTB'], ['BB'], inc=(h4 == 3 and q == 1))
                    em.cp('dve', BKtok[:, half * 4:(half + 1) * 4].rearrange("p h q k -> p (h q k)"), BBp[:, 0:512], ['BB'], [kBKtok])
                for h in range(8):
                    em.mm(B[1][:, 256 + h:257 + h], prodb[:, h, :], onesb[0:64, 0:1], True, True, ['r_prod', 'CSTB'], ['B1'], inc=(h == 7))
                em.cp('act', coef, B[1][:, 256:264], ['B1'], [kcoef])
                for hp in range(4):
                    bb_ = B[0]
                    bbk = "B0"
                    bk2 = B[1]
                    bk2k = "B1"
                    for q in range(2):
                        h = hp * 2 + q
                        em.mm(bb_[:, q * 256:(q + 1) * 256], BK[:, h, 0, :], AR[:, h, :], True, True, [kBK, kAR, 'r_lw', 'r_aa'], [bbk], inc=(q == 1))
                        em.mm(bk2[:, q * 256:(q + 1) * 256], BK[:, h, 1, :], AR[:, h, :], True, True, [kBK, kAR], [bk2k], inc=(q == 1))
                    em.tt('dve', GB[:, hp * 2:hp * 2 + 2, :], bb_.rearrange("p (q c) -> p q c", c=256),
                          MARt[:, d:d + 1, :].to_broadcast([128, 2, 256]), ALU.mult, [bbk, 'MAR'], [kGB])
                    em.tt('dve', Q0[:, hp * 2:hp * 2 + 2, :], bb_.rearrange("p (q c) -> p q c", c=256)[:, :, 0:128],
                          CST[:, C_MP0 + (1 - d) * 128:C_MP0 + (2 - d) * 128].unsqueeze(1).to_broadcast([128, 2, 128]), ALU.mult,
                          [bbk, 'CST'], [kQ0])
                    em.tt('dve', GK[:, hp * 2:hp * 2 + 2, :], bk2.rearrange("p (q c) -> p q c", c=256),
                          MARt[:, d:d + 1, :].to_broadcast([128, 2, 256]), ALU.mult, [bk2k, 'MAR'], [kGK])
                if split:
                    em.stream = ('rs', idx)
                if first_of_d:
                    em.memset('pool', Hs, 0.0, ['r_H'])
                    em.memset('pool', Hb, 0.0, ['r_Hb'])
                for hh in range(2):
                    bp = B[2 + hh]
                    bpk = f"B{2 + hh}"
                    for q in range(4):
                        h = hh * 4 + q
                        em.mm(bp[:, q * 128:(q + 1) * 128], AR[:, h, 0:128], BK[:, h, 0, :], True, True, [kAR, kBK], [bpk], inc=(q == 3))
                    b3 = bp.rearrange("p (q c) -> p q c", c=128)
                    hsl = slice(hh * 4, (hh + 1) * 4)
                    em.tt('dve', Pm[0][:, hsl, :], b3, CST[:, C_MP0 + d * 128:C_MP0 + (d + 1) * 128].unsqueeze(1).to_broadcast([128, 4, 128]),
                          ALU.mult, [bpk, 'CST'], ['r_P0'])
                    em.tt('dve', E1m[:, hsl, :], b3, CST[:, C_ME1 + d * 128:C_ME1 + (d + 1) * 128].unsqueeze(1).to_broadcast([128, 4, 128]),
                          ALU.mult, [bpk, 'CST'], ['r_E1'])
                    em.tt('dve', E2m[:, hsl, :], b3, CST[:, C_ME2 + d * 128:C_ME2 + (d + 1) * 128].unsqueeze(1).to_broadcast([128, 4, 128]),
                          ALU.mult, [bpk, 'CST'], ['r_E2'])
                em.tt('dve', Ym[0], Q0, identb.unsqueeze(1).to_broadcast([128, 8, 128]), ALU.add, [kQ0, 'CSTB'], ['r_Y0'])
                em.cp('act', Qm[0], Q0, [kQ0], ['r_Q0'])
                cur = 0
                for lev in range(1, 5):
                    nxt = 1 - cur
                    for hh in range(2):
                        bp = B[2]
                        bpk = "B2"
                        bq = B[3]
                        bqk = "B3"
                        hsl = slice(hh * 4, (hh + 1) * 4)
                        for q in range(4):
                            h = hh * 4 + q
                            em.mm(bp[:, q * 128:(q + 1) * 128], Qm[cur][:, h, :], Pm[cur][:, h, :], True, True,
                                  [f"r_Q{cur}", f"r_P{cur}"], [bpk], inc=(q == 3))
                        for q in range(4):
                            h = hh * 4 + q
                            em.mm(bq[:, q * 128:(q + 1) * 128], Pm[cur][:, h, :], Qm[cur][:, h, :], True, True,
                                  [f"r_Q{cur}", f"r_P{cur}"], [bqk], inc=(q == 3))
                        em.cp('act', Pm[nxt][:, hsl, :], bp.rearrange("p (q c) -> p q c", c=128), [bpk], [f"r_P{nxt}"])
                        em.cp('dve', Qm[nxt][:, hsl, :], bq.rearrange("p (q c) -> p q c", c=128), [bqk], [f"r_Q{nxt}"])
                    for hh in range(2):
                        by = B[2 + hh]
                        byk = f"B{2 + hh}"
                        hsl = slice(hh * 4, (hh + 1) * 4)
                        for q in range(4):
                            h = hh * 4 + q
                            em.mm(by[:, q * 128:(q + 1) * 128], Pm[nxt][:, h, :], Ym[cur][:, h, :], True, True,
                                  [f"r_P{nxt}", f"r_Y{cur}"], [byk], inc=(q == 3))
                        em.tt('dve', Ym[nxt][:, hsl, :], by.rearrange("p (q c) -> p q c", c=128), Ym[cur][:, hsl, :], ALU.add,
                              [byk, f"r_Y{cur}"], [f"r_Y{nxt}"])
                    cur = nxt
                Dt = Ym[cur]
                dtk = f"r_Y{cur}"
                for st, (Em_, ek) in enumerate([(E1m, 'r_E1'), (E2m, 'r_E2')]):
                    oth = Ym[1 - cur]
                    othk = f"r_Y{1 - cur}"
                    for half in range(2):
                        for h4 in range(4):
                            em.tr(BBp[:, 512 + h4 * 128:512 + (h4 + 1) * 128], Dt[:, half * 4 + h4, :], identb, [dtk, 'CSTB'], ['BB'], inc=(h4 == 3))
                        em.cp('act', Dm[:, half * 4:(half + 1) * 4, :], BBp[:, 512:1024].rearrange("p (h c) -> p h c", c=128),
                              ['BB'], ['r_D'])
                    for hh in range(2):
                        bz = B[2 + hh]
                        bzk = f"B{2 + hh}"
                        hsl = slice(hh * 4, (hh + 1) * 4)
                        for q in range(4):
                            h = hh * 4 + q
                            em.mm(bz[:, q * 128:(q + 1) * 128], Em_[:, h, :], Dt[:, h, :], True, True, [ek, dtk], [bzk], inc=(q == 3))
                        em.cp('act' if hh == 0 else 'dve', Zm[:, hsl, :], bz.rearrange("p (q c) -> p q c", c=128), [bzk], ['r_Z'])
                    for hh in range(2):
                        by = B[2 + hh]
                        byk = f"B{2 + hh}"
                        hsl = slice(hh * 4, (hh + 1) * 4)
                        for q in range(4):
                            h = hh * 4 + q
                            em.mm(by[:, q * 128:(q + 1) * 128], Dm[:, h, :], Zm[:, h, :], True, True, ['r_D', 'r_Z'], [byk], inc=(q == 3))
                        em.tt('dve', oth[:, hsl, :], by.rearrange("p (q c) -> p q c", c=128), Dt[:, hsl, :], ALU.add,
                              [byk, dtk], [othk])
                    cur = 1 - cur
                    Dt = Ym[cur]
                    dtk = f"r_Y{cur}"
                TT_ = Dt
                ttk = dtk
                for h in range(8):
                    hs_ = slice(h * 64, (h + 1) * 64)
                    em.mm(B[2][:, hs_], AR[:, h, 0:128], Hb[:, hs_], True, False, [kAR, 'r_Hb'], ['B2'])
                    em.mm(B[2][:, hs_], GK[:, h, 0:128], vtokb[:, hs_], False, True, [kGK, kvtokb], ['B2'], inc=(h == 7))
                em.cp('act', Wt, B[2], ['B2'], ['r_W'])
                for h in range(8):
                    hs_ = slice(h * 64, (h + 1) * 64)
                    em.mm(B[3][:, hs_], TT_[:, h, :], Wt[:, hs_], True, True, [ttk, 'r_W'], ['B3'], inc=(h == 7))
                em.cp('dve', Ut, B[3], ['B3'], ['r_U'])
                for h in range(8):
                    hs_ = slice(h * 64, (h + 1) * 64)
                    em.mm(B[2][:, hs_], AR[:, h, 128:256], Hb[:, hs_], True, False, [kAR, 'r_Hb'], ['B2'])
                    em.mm(B[2][:, hs_], GB[:, h, 128:256], Ut[:, hs_], False, False, [kGB, 'r_U'], ['B2'])
                    em.mm(B[2][:, hs_], GK[:, h, 128:256], vtokb[:, hs_], False, True, [kGK, kvtokb], ['B2'], inc=(h == 7))
                for h in range(8):
                    hs_ = slice(h * 64, (h + 1) * 64)
                    em.mm(B[3][0:64, hs_], BKtok[:, h, 0, :], Ut[:, hs_], True, False, [kBKtok, 'r_U'], ['B3'])
                    em.mm(B[3][0:64, hs_], BKtok[:, h, 1, :], vtokb[:, hs_], False, True, [kBKtok, kvtokb], ['B3'], inc=(h == 7))
                em.tt('dve', Hs, Hs, B[3][0:64, :], ALU.add, ['r_H', 'B3'], ['r_H'])
                em.tt('dve', Hs.rearrange("p (h v) -> p h v", v=64), Hs.rearrange("p (h v) -> p h v", v=64),
                      gC.unsqueeze(2).to_broadcast([64, 8, 64]), ALU.mult, ['r_H', kgC], ['r_H'])
                em.cp('act', Hb, Hs, ['r_H', 'B2'], ['r_Hb'])
                if d == 0:
                    if emit_out:
                        em.cp('act', ot[:, 0:512], B[2], ['B2'], ['r_o'])
                        em.cp('dve', ot[:, 512:520], coef, [kcoef], ['r_ocoef'])
                        em.dma('pool', OF[s][t0:t0 + 128, :], ot, reads=['r_o', 'r_ocoef'], writes=[f"of{s}"])
                elif emit_out:
                    em.dma('sp', oft, OF[s][t0:t0 + 128, :], reads=[f"of{s}"], writes=['r_of'])
                    em.tt('dve', ot[:, 0:512], B[2], oft[:, 0:512], ALU.add, ['B2', 'r_of'], ['r_o'])
                    o3 = ot[:, 0:512].rearrange("p (h v) -> p h v", v=64)
                    em.op('dve', lambda e: e.tensor_reduce(out=gn[:, 0:8], in_=o3, axis=AX.X, op=ALU.add), ['r_o'], ['r_gn'])
                    em.act(osq, ot[:, 0:512], AF.Square, ['r_o'], ['r_osq'])
                    em.op('dve', lambda e: e.tensor_reduce(out=gn[:, 8:16], in_=osq.rearrange("p (h v) -> p h v", v=64),
                                                           axis=AX.X, op=ALU.add), ['r_osq', 'r_gn'], ['r_gn'])
                    em.ts('dve', gn[:, 16:24], gn[:, 0:8], 1.0 / 64, None, ALU.mult, None, ['r_gn'], ['r_gn'])
                    em.tt('dve', gn[:, 0:8], gn[:, 16:24], gn[:, 16:24], ALU.mult, ['r_gn'], ['r_gn'])
                    em.stt(gn[:, 24:32], gn[:, 8:16], 1.0 / 64, gn[:, 0:8], ALU.mult, ALU.subtract, ['r_gn'], ['r_gn'])
                    em.act(gn[:, 24:32], gn[:, 24:32], AF.Sqrt, ['r_gn'], ['r_gn'], bias=R_LN_EPS, scale=1.0)
                    em.op('dve', lambda e: e.reciprocal(out=gn[:, 24:32], in_=gn[:, 24:32]), ['r_gn'], ['r_gn'])
                    em.tt('dve', o3, o3, gn[:, 16:24].unsqueeze(2).to_broadcast([128, 8, 64]), ALU.subtract, ['r_o', 'r_gn'], ['r_o'])
                    em.tt('dve', o3, o3, gn[:, 24:32].unsqueeze(2).to_broadcast([128, 8, 64]), ALU.mult, ['r_o', 'r_gn'], ['r_o'])
                    em.tt('dve', ot[:, 0:512], ot[:, 0:512], lnw, ALU.mult, ['r_o', 'ROWB'], ['r_o'])
                    em.tt('dve', ot[:, 0:512], ot[:, 0:512], lnb, ALU.add, ['r_o', 'ROWB'], ['r_o'])
                    em.tt('dve', gn[:, 32:40], coef, oft[:, 512:520], ALU.add, [kcoef, 'r_of', 'r_gn'], ['r_gn'])
                    em.tt('dve', osq.rearrange("p (h v) -> p h v", v=64), vtokb.rearrange("p (h v) -> p h v", v=64),
                          gn[:, 32:40].unsqueeze(2).to_broadcast([128, 8, 64]), ALU.mult, [kvtokb, 'r_gn', 'r_osq'], ['r_osq'])
                    em.tt('dve', ot[:, 0:512], ot[:, 0:512], osq, ALU.add, ['r_o', 'r_osq'], ['r_o'])
                    em.dma('sp', xg0[:, 0:128], XG[s][0:128, t0:t0 + 128], reads=[f"xg{s}"], writes=['r_xg0'])
                    em.dma('sp', xg1[:, 0:128], XG[s][128:160, t0:t0 + 128], reads=[f"xg{s}"], writes=['r_xg1'])
                    em.act(sg0, xg0[:, 0:128], AF.Sigmoid, ['r_xg0'], ['r_sg0'])
                    em.act(sg1, xg1[:, 0:128], AF.Sigmoid, ['r_xg1'], ['r_sg1'])
                    em.mm(B[3], sg0, g2b0, True, False, ['r_sg0', 'g2b'], ['B3'])
                    em.mm(B[3], sg1, g2b1, False, True, ['r_sg1', 'g2b'], ['B3'])
                    em.tt('dve', ot[:, 0:512], ot[:, 0:512], B[3], ALU.mult, ['r_o', 'B3'], ['r_o'])
                    for j in range(4):
                        em.tr(B[2][:, j * 128:(j + 1) * 128], ot[:, j * 128:(j + 1) * 128], ident, ['r_o', 'CST'], ['B2'], inc=(j == 3))
                    em.cp('act', mixo[:, 4:8, :], B[2].rearrange("p (j t) -> p j t", t=128), ['B2'], ['mixo_r'])
                    em.dma('pool', fm(MIX[s])[:, 4:8, t0:t0 + 128], mixo[:, 4:8, :], reads=['mixo_r'], writes=[f"mixr{s}"])

            mc_ = mixcfg or {}
            inter = mc_.get('interleave', True)
            for b in range(mc_.get('nb', NBL)):
                nw = 0
                for stream, fnc in (('m', mamba_chunk), ('r', wkv_chunk)):
                    if not mc_.get('mamba' if stream == 'm' else 'wkv', True):
                        continue
                    for d in range(mc_.get('nd', 2)):
                        first = True
                        if stream == 'm':
                            em.stream = 'm' if inter else None
                            em.memset('pool', hst, 0.0, ['m_hst'])
                            em.memset('pool', hstb, 0.0, ['m_hstb'])
                        for kind in range(mc_.get('nkind', 2)):
                            s = b * 2 + kind
                            T = seqT(s)
                            nch = T // CH
                            order = range(nch) if d == 0 else range(nch - 1, -1, -1)
                            emit = (kind == 1) or need_ctx_out or mc_.get('ctxout', False)
                            for c in order:
                                if stream == 'm':
                                    fnc(s, c * CH, d, emit)
                                else:
                                    fnc(s, c * CH, d, emit, idx=nw, first_of_d=first, split=inter)
                                    nw += 1
                                    first = False
                em.stream = None
                if inter:
                    em.flush_mixer(nw)
            em.barrier()

    def stage_proj_post(l, phase, wap, Kc, SRC, srcname, gidx, seqs):
        with ExitStack() as es:
            def S(name, shape, dt=F32):
                return es.enter_context(nc.sbuf_tensor(name, list(shape), dt)).ap()
            nm = f"pp{phase}"
            wb = load_wbf(es, l, wap, Kc, D, nm + "w")
            a = S(f"{nm}a{l}", [128, Kc, 512], BF16)
            xt = S(f"{nm}x{l}", [128, 8, 512])
            y = S(f"{nm}y{l}", [128, 8, 512])
            sq = S(f"{nm}sq{l}", [128, 8, 512], BF16)
            rstd = S(f"{nm}rs{l}", [128, 512])
            pss = [es.enter_context(nc.psum_tensor(f"{nm}ps{l}_{i}", [128, 512], F32)).ap() for i in range(5)]
            for s in seqs:
                T = seqT(s)
                TW = min(512, T)
                jmod = 2 if s % 2 == 0 else s // 2
                rsrc, rsk = res_src(l, s, phase)
                rdst, rdk = res_dst(l, s, phase)
                for tt_ in range(T // TW):
                    t0 = tt_ * TW
                    em.dma('sp', a[:, :, :TW], fm(SRC[s])[:, :, t0:t0 + TW], reads=([f"mixm{s}", f"mixr{s}"] if srcname == 'mix' else [f"{srcname}{s}"]), writes=[nm + 'a'])
                    em.dma('pool', xt[:, :, :TW], fm(rsrc)[:, :, t0:t0 + TW], reads=[rsk], writes=[nm + 'x'])
                    for m in range(8):
                        ps = pss[m % 4]
                        pk = f"ps{m % 4}"
                        for k in range(Kc):
                            em.mm(ps[:, :TW], wb[:, k, m * 128:(m + 1) * 128], a[:, k, :TW], k == 0, k == Kc - 1,
                                  [nm + 'wbf', nm + 'a'], [pk])
                        em.cp('dve', y[:, m, :TW], ps[:, :TW], [pk], [nm + 'y'])
                        em.act(sq[:, m, :TW], ps[:, :TW], AF.Square, [pk], [nm + 'sq'])
                    for m in range(8):
                        em.mm(pss[4][:, :TW], onesb, sq[:, m, :TW], m == 0, m == 7, ['CSTB', nm + 'sq'], ['ps4'])
                    em.act(rstd[:, :TW], pss[4][:, :TW], AF.Sqrt, ['ps4'], [nm + 'rs'], bias=EPS, scale=1.0 / D)
                    em.op('dve', lambda e: e.reciprocal(out=rstd[:, :TW], in_=rstd[:, :TW]), [nm + 'rs'], [nm + 'rs'])
                    for m in range(8):
                        em.stt(y[:, m, :TW], y[:, m, :TW], DER[:, gidx, m, jmod:jmod + 1], rstd[:, :TW], ALU.mult, ALU.mult,
                               [nm + 'y', nm + 'rs', 'DER'], [nm + 'y'])
                    em.tt('dve', xt[:, :, :TW], xt[:, :, :TW], y[:, :, :TW], ALU.add, [nm + 'x', nm + 'y'], [nm + 'x'])
                    em.dma('pool', fm(rdst)[:, :, t0:t0 + TW], xt[:, :, :TW], reads=[nm + 'x'], writes=[rdk])
            em.barrier()

    def stage_ffn_up(l, seqs):
        with ExitStack() as es:
            def S(name, shape, dt=F32):
                return es.enter_context(nc.sbuf_tensor(name, list(shape), dt)).ap()
            wb = load_wbf(es, l, f_w_up[l], 8, 2 * DFF, "wup")
            xt = S(f"fux{l}", [128, 8, 512])
            h = S(f"fuh{l}", [128, 8, 512], BF16)
            sq = S(f"fusq{l}", [128, 8, 512], BF16)
            rstd = S(f"furs{l}", [128, 512])
            stg = [S(f"fustg{l}_{i}", [128, 512]) for i in range(2)]
            stv = [S(f"fustv{l}_{i}", [128, 512], BF16) for i in range(2)]
            pss = [es.enter_context(nc.psum_tensor(f"fups{l}_{i}", [128, 512], F32)).ap() for i in range(8)]
            for s in seqs:
                T = seqT(s)
                TW = min(512, T)
                jmod = 2 if s % 2 == 0 else s // 2
                src, srck = res_src(l, s, 1)
                for tt_ in range(T // TW):
                    t0 = tt_ * TW
                    em.dma('sp', xt[:, :, :TW], fm(src)[:, :, t0:t0 + TW], reads=[srck], writes=['fux'])
                    prenorm(xt, TW, h, sq, rstd, pss[7], jmod, 2, 24, 'fux', 'fuh', 'ps7')
                    for j in range(NFF):
                        pg = pss[(2 * j) % 6]
                        pgk = f"ps{(2 * j) % 6}"
                        pv_ = pss[(2 * j + 1) % 6]
                        pvk = f"ps{(2 * j + 1) % 6}"
                        for k in range(8):
                            em.mm(pg[:, :TW], wb[:, k, j * 128:(j + 1) * 128], h[:, k, :TW], k == 0, k == 7, ['wupbf', 'fuh'], [pgk])
                        for k in range(8):
                            em.mm(pv_[:, :TW], wb[:, k, DFF + j * 128:DFF + (j + 1) * 128], h[:, k, :TW], k == 0, k == 7,
                                  ['wupbf', 'fuh'], [pvk])
                        sg_ = stg[j % 2]
                        sv_ = stv[j % 2]
                        em.cp('dve', sg_[:, :TW], pg[:, :TW], [pgk], [f"fustg{j % 2}"])
                        em.cp('act', sv_[:, :TW], pv_[:, :TW], [pvk], [f"fustv{j % 2}"])
                        em.dma('pool', GATE[s][j * 128:(j + 1) * 128, t0:t0 + TW], sg_[:, :TW], reads=[f"fustg{j % 2}"],
                               writes=[f"gate{s}"])
                        em.dma('sp', VAL[s][j * 128:(j + 1) * 128, t0:t0 + TW], sv_[:, :TW], reads=[f"fustv{j % 2}"],
                               writes=[f"val{s}"])
            em.barrier()

    def stage_ffn_conv(l, seqs):
        with ExitStack() as es:
            def S(name, shape, dt=F32):
                return es.enter_context(nc.sbuf_tensor(name, list(shape), dt)).ap()
            gflat = [S(f"fcg{l}_{i}", [128, 2048]) for i in range(2)]
            vflat = [S(f"fcv{l}_{i}", [128, 2048], BF16) for i in range(2)]
            gpx = [S(f"fcgpx{l}_{i}", [128, 34, 66], BF16) for i in range(2)]
            gpc = [S(f"fcgpc{l}_{i}", [128, 3, 258], BF16) for i in range(2)]
            dg = [S(f"fcdg{l}_{i}", [128, 9, 128], BF16) for i in range(2)]
            acc = S(f"fcacc{l}", [128, 2048])
            u = S(f"fcu{l}", [128, 2048])
            ab = [S(f"fcab{l}_{i}", [128, 2048], BF16) for i in range(2)]
            pss = [es.enter_context(nc.psum_tensor(f"fcps{l}_{i}", [128, 512], F32)).ap() for i in range(4)]
            for i in range(2):
                em.memset('pool', gpx[i], 0.0, [f"fcgpx{i}"])
                em.memset('pool', gpc[i], 0.0, [f"fcgpc{i}"])
            it = 0
            pi = 0
            for s in seqs:
                T = seqT(s)
                for j in range(NFF):
                    i2 = it % 2
                    it += 1
                    if s % 2 == 1:
                        R, Cc, gp, gpk = 32, 64, gpx[i2], f"fcgpx{i2}"
                    else:
                        R, Cc, gp, gpk = 1, 256, gpc[i2], f"fcgpc{i2}"
                    gf = gflat[i2]
                    vf = vflat[i2]
                    em.dma('sp', gf[:, :T], GATE[s][j * 128:(j + 1) * 128, :], reads=[f"gate{s}"], writes=[f"fcg{i2}"])
                    em.dma('sp', vf[:, :T], VAL[s][j * 128:(j + 1) * 128, :], reads=[f"val{s}"], writes=[f"fcv{i2}"])
                    em.cp('act', gp[:, 1:1 + R, 1:1 + Cc], gf[:, :T].rearrange("p (r c) -> p r c", c=Cc), [f"fcg{i2}"], [gpk])
                    taps = list(range(9)) if s % 2 == 1 else [3, 4, 5]
                    for tap in taps:
                        em.ts('dve', dg[i2][:, tap, :], identb, PV[:, PV_FCW + tap * NFF + j:PV_FCW + tap * NFF + j + 1], None,
                              ALU.mult, None, ['CSTB', 'PV'], [f"fcdg{i2}"])
                    nblk = T // 512 if s % 2 == 1 else 1
                    for blk in range(nblk):
                        ps = pss[pi % 4]
                        pk = f"ps{pi % 4}"
                        pi += 1
                        for ti, tap in enumerate(taps):
                            dr, dc = tap // 3 - 1, tap % 3 - 1
                            if s % 2 == 1:
                                rhs = gp[:, 1 + dr + blk * 8:1 + dr + blk * 8 + 8, 1 + dc:1 + dc + 64]
                                out = ps.rearrange("p (r c) -> p r c", c=64)
                                wdt = 512
                            else:
                                rhs = gp[:, 1, 1 + dc:1 + dc + 256]
                                out = ps[:, 0:256]
                                wdt = 256
                            em.mm(out, dg[i2][:, tap, :], rhs, ti == 0, ti == len(taps) - 1, [f"fcdg{i2}", gpk], [pk])
                        em.act(acc[:, blk * 512:blk * 512 + wdt], ps[:, 0:wdt], AF.Identity, [pk, 'PV'], ['fcacc'],
                               bias=PV[:, PV_FCB + j:PV_FCB + j + 1], scale=1.0)
                    em.act(u[:, :T], acc[:, :T], AF.Square, ['fcacc'], ['fcu'])
                    em.ts('dve', u[:, :T], u[:, :T], 0.044715, 1.0, ALU.mult, ALU.add, ['fcu'], ['fcu'])
                    em.tt('dve', u[:, :T], u[:, :T], acc[:, :T], ALU.mult, ['fcu', 'fcacc'], ['fcu'])
                    em.act(u[:, :T], u[:, :T], AF.Sigmoid, ['fcu'], ['fcu'], scale=GELU_C)
                    em.tt('dve', u[:, :T], u[:, :T], acc[:, :T], ALU.mult, ['fcu', 'fcacc'], ['fcu'])
                    em.tt('dve', ab[i2][:, :T], u[:, :T], vf[:, :T], ALU.mult, ['fcu', f"fcv{i2}"], [f"fcab{i2}"])
                    em.dma('pool', ACTV[s][j * 128:(j + 1) * 128, :], ab[i2][:, :T], reads=[f"fcab{i2}"], writes=[f"actv{s}"])
            em.barrier()

    allseq = list(range(NS))
    xseq = [s for s in range(NS) if s % 2 == 1]
    outkeys = []
    for l in range(n_layers):
        last = (l == n_layers - 1)
        stage_mod(l)
        if stop_after == 'mod':
            break
        stage_inproj(l)
        if stop_after == 'inproj':
            break
        stage_mixer(l, need_ctx_out=not last)
        if stop_after == 'mixer':
            break
        seqs = xseq if last else allseq
        stage_proj_post(l, 0, w_out[l], 8, MIX, "mix", 1, seqs)
        if stop_after == 'outproj':
            break
        stage_ffn_up(l, seqs)
        stage_ffn_conv(l, seqs)
        stage_proj_post(l, 1, f_w_down[l], NFF, ACTV, "actv", 3, seqs)
    em.barrier()
    return nc, em


def host_prep(inp):
    f = np.float32
    idx = np.arange(128)
    cstn = np.zeros((128, NCST), f)
    cstn[:, C_ID:C_ID + 128] = np.eye(128)
    cstn[:, C_UTI:C_UTI + 128] = (idx[:, None] <= idx[None, :])
    cstn[:, C_LTI:C_LTI + 128] = (idx[:, None] >= idx[None, :])
    cstn[:, C_UTS:C_UTS + 128] = (idx[:, None] < idx[None, :])
    cstn[:, C_LTS:C_LTS + 128] = (idx[:, None] > idx[None, :])
    cstn[:, C_ONE:C_ONE + 128] = 1.0
    b32 = idx // 32
    b64 = idx // 64
    same32 = b32[:, None] == b32[None, :]
    same64 = b64[:, None] == b64[None, :]
    for d in range(2):
        strict = (idx[:, None] > idx[None, :]) if d == 0 else (idx[:, None] < idx[None, :])
        cstn[:, C_MP0 + d * 128:C_MP0 + (d + 1) * 128] = strict & same32
        cstn[:, C_ME1 + d * 128:C_ME1 + (d + 1) * 128] = strict & same64 & (~same32)
        cstn[:, C_ME2 + d * 128:C_ME2 + (d + 1) * 128] = strict & (~same64)
    pvn = np.zeros((L, 128, NPV), f)
    pv6 = np.zeros((L, 64, NPV64), f)
    rwn = np.zeros((L, 1, NROW), f)
    for l in range(L):
        pvn[l, :, PV_BMOD:PV_BMOD + 48] = inp['b_mod'][l].reshape(48, 128).T
        pvn[l, :, PV_GPRE1:PV_GPRE1 + 8] = inp['g_mix_pre'][l].reshape(8, 128).T
        pvn[l, :, PV_GPOST1:PV_GPOST1 + 8] = inp['g_mix_post'][l].reshape(8, 128).T
        pvn[l, :, PV_GPRE2:PV_GPRE2 + 8] = inp['g_ffn_pre'][l].reshape(8, 128).T
        pvn[l, :, PV_GPOST2:PV_GPOST2 + 8] = inp['g_ffn_post'][l].reshape(8, 128).T
        pvn[l, :, PV_MCW:PV_MCW + 24] = inp['m_conv_w'][l].reshape(3, 8, 128).transpose(2, 0, 1).reshape(128, 24)
        pvn[l, :, PV_MCB:PV_MCB + 8] = inp['m_conv_b'][l].reshape(8, 128).T
        pvn[l, :, PV_FCW:PV_FCW + 198] = inp['f_conv_w'][l].reshape(9, NFF, 128).transpose(2, 0, 1).reshape(128, 198)
        pvn[l, :, PV_FCB:PV_FCB + NFF] = inp['f_conv_b'][l].reshape(NFF, 128).T
        mu = inp['r_mu'][l]
        pvn[l, :, PV_MUXG0] = mu[1792:1920]
        pvn[l, 0:32, PV_MUXG1] = mu[1920:1952]
        pv6[l, :, P6_MURKV:P6_MURKV + 24] = mu[0:1536].reshape(24, 64).T
        pv6[l, :, P6_MUWA:P6_MUWA + 4] = mu[1536:1792].reshape(4, 64).T
        pv6[l, :, P6_W0:P6_W0 + 16] = inp['r_w0'][l].reshape(2, 8, 64).transpose(2, 0, 1).reshape(64, 16)
        pv6[l, :, P6_A0:P6_A0 + 16] = inp['r_a0'][l].reshape(2, 8, 64).transpose(2, 0, 1).reshape(64, 16)
        pv6[l, :, P6_KK:P6_KK + 8] = inp['r_k_k'][l].reshape(8, 64).T
        pv6[l, :, P6_KA:P6_KA + 8] = inp['r_k_a'][l].reshape(8, 64).T
        pv6[l, :, P6_RK:P6_RK + 8] = inp['r_r_k'][l].T
        rwn[l, 0, RV_MNW:RV_MNW + 512] = inp['m_norm_w'][l]
        rwn[l, 0, RV_LNW:RV_LNW + 512] = inp['r_ln_w'][l]
        rwn[l, 0, RV_LNB:RV_LNB + 512] = inp['r_ln_b'][l]
        rwn[l, 0, RV_MD:RV_MD + 8] = inp['m_d'][l]
        rwn[l, 0, RV_DTB:RV_DTB + 16] = inp['m_dt_bias'][l].reshape(16)
        rwn[l, 0, RV_ALOG:RV_ALOG + 16] = inp['m_a_log'][l].reshape(16)
    return cstn, pvn, pv6, rwn


def make_in_maps(inp, cores):
    cstn, pvn, pv6, rwn = host_prep(inp)
    shared = {k: np.ascontiguousarray(np.asarray(inp[k], dtype=np.float32)) for k in
              ['w_mod', 'w_in', 'w_out', 'r_w2', 'r_a2', 'r_g2', 'f_w_up', 'f_w_down']}
    maps = []
    x = np.asarray(inp['x'], np.float32)
    ctx = np.asarray(inp['ctx'], np.float32)
    c = np.asarray(inp['c'], np.float32)
    cc = np.asarray(inp['c_ctx'], np.float32)
    for ci in cores:
        bs = [ci * NBL + i for i in range(NBL)]
        m = dict(shared)
        m['xT'] = np.ascontiguousarray(x[bs].transpose(0, 2, 1))
        m['ctxT'] = np.ascontiguousarray(ctx[bs].transpose(0, 2, 1))
        m['cT'] = np.ascontiguousarray(np.stack([c[bs[0]], c[bs[1]], cc], axis=1))
        m['cst'] = cstn
        m['pv'] = pvn
        m['pv64'] = pv6
        m['rowv'] = rwn
        m['rmu'] = np.ascontiguousarray(np.asarray(inp['r_mu'], np.float32).reshape(L, 1, 1952))
        maps.append(m)
    return maps


def kernel(**inputs):
    nc, em = build()
    cores = list(range(NCORE))
    maps = make_in_maps(inputs, cores)
    res = run_bass_kernel_spmd(nc, maps, core_ids=cores)
    out = np.empty((NCORE * NBL, TX, D), np.float32)
    for ci in cores:
        o = res.results[ci]["outT"]
        out[ci * NBL:(ci + 1) * NBL] = o.transpose(0, 2, 1)
    return out
```

```python
import numpy as np
from contextlib import ExitStack
import concourse.bass as bass
import concourse.mybir as mybir
from concourse.bass_utils import run_bass_kernel_spmd

F32 = mybir.dt.float32
BF16 = mybir.dt.bfloat16
AF = mybir.ActivationFunctionType
ALU = mybir.AluOpType
AX = mybir.AxisListType

L = 2
D = 1024
TX = 2048
TC = 256
NBL = 2
NCORE = 8
CH = 128
DFF = 2816
NFF = 22
EPS = 1e-6
R_LN_EPS = 64e-5
R_DECAY_SCALE = 0.6065306597126334
GELU_C = 1.5957691216057308

PV_BMOD = 0
PV_GPRE1 = 48
PV_GPOST1 = 56
PV_GPRE2 = 64
PV_GPOST2 = 72
PV_MCW = 80
PV_MCB = 104
PV_FCW = 112
PV_FCB = 310
PV_MUXG0 = 332
PV_MUXG1 = 333
NPV = 334
P6_MURKV = 0
P6_MUWA = 24
P6_W0 = 28
P6_A0 = 44
P6_KK = 60
P6_KA = 68
P6_RK = 76
NPV64 = 84
RV_MNW = 0
RV_LNW = 512
RV_LNB = 1024
RV_MD = 1536
RV_DTB = 1544
RV_ALOG = 1560
NROW = 1576
C_ID = 0
C_UTI = 128
C_LTI = 256
C_UTS = 384
C_LTS = 512
C_ONE = 640
C_MP0 = 768
C_ME1 = 1024
C_ME2 = 1280
NCST = 1536


class Em:
    def __init__(self, nc, ndma=8):
        self.nc = nc
        self.engs = {'pe': nc.tensor, 'act': nc.scalar, 'dve': nc.vector, 'pool': nc.gpsimd, 'sp': nc.sync}
        self.sem = {}
        self.cnt = {}
        for k in ['pe', 'act', 'dve', 'pool']:
            self.sem[k] = nc.alloc_semaphore("sem_" + k)
            self.cnt[k] = 0
        self.dq = {}
        for q in ['sp', 'pool', 'act']:
            self.dq[q] = {'n': ndma, 'next': 0}
            for i in range(ndma):
                self.sem[f"d_{q}_{i}"] = nc.alloc_semaphore(f"dsem_{q}_{i}")
                self.cnt[f"d_{q}_{i}"] = 0
        self.seen = {k: {} for k in self.engs}
        self.lastw = {}
        self.readers = {}
        self.n = 0
        self.pend = {k: False for k in self.engs}
        self.stream = None
        self.queues = {}

    def _deps(self, reads, writes):
        deps = {}

        def add(d):
            if d is None:
                return
            k, v = d
            if deps.get(k, 0) < v:
                deps[k] = v
        for b in reads:
            add(self.lastw.get(b))
        for b in writes:
            add(self.lastw.get(b))
            for r in self.readers.get(b, ()):
                add(r)
        return deps

    def _waits(self, eng, deps):
        for k, v in deps.items():
            if k == 'pe' and eng == 'pe':
                continue
            if k.startswith('d_'):
                v = self.cnt[k]
            if self.seen[eng].get(k, 0) >= v:
                continue
            self.seen[eng][k] = v
            self.engs[eng].wait_ge(self.sem[k], v)
            self.n += 1

    def _mark(self, me, reads, writes):
        for b in reads:
            self.readers.setdefault(b, []).append(me)
        for b in writes:
            self.lastw[b] = me
            self.readers[b] = []

    @staticmethod
    def _is_psum(k):
        return (k[0] == 'B' and (k[1:].isdigit() or k in ('BB', 'BBa', 'BBb'))) or k.startswith('ps')

    def flush(self):
        qs = {k: v for k, v in self.queues.items() if v}
        self.queues = {}
        pos = {k: 0 for k in qs}
        while qs:
            k = min(qs, key=lambda n: pos[n] / len(qs[n]))
            it = qs[k][pos[k]]
            pos[k] += 1
            if it[0] == 'op':
                self.op(*it[1:])
            else:
                self.dma(it[1], it[2], it[3], it[4], it[5], **it[6])
            if pos[k] >= len(qs[k]):
                del qs[k]

    @staticmethod
    def _merge(lists):
        lists = [l for l in lists if l]
        out = []
        pos = [0] * len(lists)
        live = list(range(len(lists)))
        sticky = None
        while live:
            k = sticky if sticky is not None else min(live, key=lambda n: pos[n] / len(lists[n]))
            it = lists[k][pos[k]]
            out.append(it)
            pos[k] += 1
            if it[0] == 'op' and it[1] == 'pe':
                sticky = None if it[5] else k
            if pos[k] >= len(lists[k]):
                live.remove(k)
                sticky = None
        return out

    @staticmethod
    def _ec(eng, out):
        n = max(out.free_size(), 64)
        return 0.12 + n / {'dve': 960.0, 'act': 1400.0, 'pool': 700.0}.get(eng, 960.0)

    def flush_mixer(self, nw):
        qs = self.queues
        self.queues = {}
        HOP = 0.6
        eng_free = {}
        kw_t = {}
        kr_t = {}
        out = []

        def engine_of(it):
            return it[1]

        def start_time(it):
            reads, writes = (it[3], it[4]) if it[0] == 'op' else (it[4], it[5])
            t = 0.0
            for k in reads:
                t = max(t, kw_t.get(k, 0.0))
                if self._is_psum(k):
                    t = max(t, kr_t.get(k, 0.0))
            for k in writes:
                t = max(t, kw_t.get(k, 0.0), kr_t.get(k, 0.0))
            return max(eng_free.get(engine_of(it), 0.0), t + HOP)

        def commit(it):
            st = start_time(it)
            e = engine_of(it)
            if it[0] == 'op':
                c = it[6] if it[6] is not None else 0.6
                fin = st + c
                eng_free[e] = fin
                reads, writes = it[3], it[4]
            else:
                eng_free[e] = st + 0.15
                fin = st + 2.5
                reads, writes = it[4], it[5]
            for k in reads:
                kr_t[k] = max(kr_t.get(k, 0.0), fin)
            for k in writes:
                kw_t[k] = fin
                kr_t[k] = 0.0
            out.append(it)

        m = qs.get('m', [])
        mpos = [0]
        carry = [None]

        def run_group(streams):
            pos = [0] * len(streams)
            sticky = carry[0]
            while True:
                cands = [(si, streams[si][pos[si]]) for si in range(len(streams)) if pos[si] < len(streams[si])]
                if not cands:
                    break
                if mpos[0] < len(m):
                    cands.append(('m', m[mpos[0]]))
                if sticky is not None and any(c[0] == sticky for c in cands):
                    pick = [c for c in cands if c[0] == sticky][0]
                else:
                    pick = min(cands, key=lambda c: (start_time(c[1]), 9 if c[0] == 'm' else c[0]))
                si, it = pick
                commit(it)
                if si == 'm':
                    mpos[0] += 1
                else:
                    pos[si] += 1
                if it[0] == 'op' and it[1] == 'pe':
                    sticky = None if it[5] else si
                if sticky is not None:
                    ended = (mpos[0] >= len(m)) if sticky == 'm' else (pos[sticky] >= len(streams[sticky]))
                    if ended:
                        sticky = None
            carry[0] = 'm' if sticky == 'm' else None

        if nw > 0:
            run_group([qs.get(('rp', 0), [])])
            for i in range(nw):
                run_group([qs.get(('rs', i), []), qs.get(('rp', i + 1), [])])
        while mpos[0] < len(m):
            commit(m[mpos[0]])
            mpos[0] += 1
        for it in out:
            if it[0] == 'op':
                self.op(*it[1:])
            else:
                self.dma(it[1], it[2], it[3], it[4], it[5], **it[6])

    def op(self, eng, fn, reads=(), writes=(), inc=True, cost=None):
        if self.stream is not None:
            self.queues.setdefault(self.stream, []).append(('op', eng, fn, tuple(reads), tuple(writes), inc, cost))
            return
        ex = [k for k in reads if self._is_psum(k)]
        self._waits(eng, self._deps(reads, list(writes) + ex))
        if inc:
            self.cnt[eng] += 1
            fn(self.engs[eng]).then_inc(self.sem[eng], 1)
            self._mark((eng, self.cnt[eng]), reads, writes)
            self.pend[eng] = False
        else:
            fn(self.engs[eng])
            self._mark((eng, self.cnt[eng] + 1), reads, writes)
            self.pend[eng] = True
        self.n += 1

    def dma(self, q, out, in_, reads=(), writes=(), **kw):
        if self.stream is not None:
            self.queues.setdefault(self.stream, []).append(('dma', q, out, in_, tuple(reads), tuple(writes), kw))
            return
        self._waits(q, self._deps(reads, writes))
        d = self.dq[q]
        i = d['next']
        d['next'] = (i + 1) % d['n']
        k = f"d_{q}_{i}"
        self.cnt[k] += 16
        self.engs[q].dma_start(out=out, in_=in_, **kw).then_inc(self.sem[k], 16)
        self._mark((k, self.cnt[k]), reads, writes)
        self.n += 1

    def barrier(self):
        assert not any(self.pend.values()), self.pend
        allv = {k: v for k, v in self.cnt.items() if v > 0}
        for e in self.engs:
            self._waits(e, dict(allv))

    def act(self, out, in_, func, r, w, bias=None, scale=None, accum=None):
        kw = {}
        if bias is not None:
            kw['bias'] = bias
        if scale is not None:
            kw['scale'] = scale
        if accum is not None:
            kw['accum_out'] = accum
        self.op('act', lambda e: e.activation(out=out, in_=in_, func=func, **kw), r, w, cost=self._ec('act', out))

    def tt(self, eng, out, a, b, op, r, w):
        self.op(eng, lambda e: e.tensor_tensor(out=out, in0=a, in1=b, op=op), r, w, cost=self._ec(eng, out))

    def ts(self, eng, out, a, s1, s2, op0, op1, r, w):
        if s2 is None:
            self.op(eng, lambda e: e.tensor_scalar(out=out, in0=a, scalar1=s1, scalar2=None, op0=op0), r, w,
                    cost=self._ec(eng, out))
        else:
            self.op(eng, lambda e: e.tensor_scalar(out=out, in0=a, scalar1=s1, scalar2=s2, op0=op0, op1=op1), r, w,
                    cost=self._ec(eng, out))

    def stt(self, out, a, s, b, op0, op1, r, w):
        self.op('dve', lambda e: e.scalar_tensor_tensor(out=out, in0=a, scalar=s, in1=b, op0=op0, op1=op1), r, w,
                cost=self._ec('dve', out))

    def mm(self, out, lhsT, rhs, start, stop, r, w, inc=None):
        self.op('pe', lambda e: e.matmul(out, lhsT=lhsT, rhs=rhs, start=start, stop=stop), r, w,
                inc=(stop if inc is None else inc), cost=0.06 + max(rhs.free_size(), 32) / 1200.0 * (4 if rhs.dtype == F32 else 1))

    def tr(self, out, in_, ident, r, w, inc=True):
        self.op('pe', lambda e: e.transpose(out, in_, ident), r, w, inc=inc,
                cost=0.06 + max(ident.free_size(), 32) / 1200.0 * (4 if in_.dtype == F32 else 1))

    def cp(self, eng, out, in_, r, w):
        if eng == 'act':
            self.op('act', lambda e: e.activation(out=out, in_=in_, func=AF.Identity), r, w, cost=self._ec('act', out))
        else:
            self.op(eng, lambda e: e.tensor_copy(out=out, in_=in_), r, w, cost=self._ec(eng, out))

    def memset(self, eng, ap, val, w):
        self.op(eng, lambda e: e.memset(ap, val), (), w, cost=self._ec(eng, ap))


def seqT(s):
    return TX if (s % 2) == 1 else TC


def build(debug=False, n_layers=L, stop_after=None, mixcfg=None):
    nc = bass.Bass("TRN2", target_bir_lowering=False)
    em = Em(nc)
    dbgset = debug if isinstance(debug, (set, list, tuple)) else None

    def din(name, shape, dt=F32):
        return nc.dram_tensor(name, list(shape), dt, kind="ExternalInput").ap()

    def dscr(name, shape, dt=F32):
        isdbg = (debug is True) or (dbgset is not None and name.rstrip('0123456789') in dbgset)
        return nc.dram_tensor(name, list(shape), dt, kind="ExternalOutput" if isdbg else "Internal").ap()

    xT = din("xT", [NBL, D, TX])
    ctxT = din("ctxT", [NBL, D, TC])
    cT = din("cT", [D, 3])
    w_mod = din("w_mod", [L, D, 6 * D])
    w_in = din("w_in", [L, D, 3504])
    w_out = din("w_out", [L, D, D])
    r_w2 = din("r_w2", [L, 2, 64, 512])
    r_a2 = din("r_a2", [L, 2, 64, 512])
    r_g2 = din("r_g2", [L, 160, 512])
    f_w_up = din("f_w_up", [L, D, 2 * DFF])
    f_w_down = din("f_w_down", [L, DFF, D])
    cst = din("cst", [128, NCST])
    pv = din("pv", [L, 128, NPV])
    pv64 = din("pv64", [L, 64, NPV64])
    rowv = din("rowv", [L, 1, NROW])
    rmu = din("rmu", [L, 1, 1952])
    outT = nc.dram_tensor("outT", [NBL, D, TX], F32, kind="ExternalOutput").ap()

    NS = 2 * NBL
    RESA = [dscr(f"resa{s}", [D, seqT(s)]) for s in range(NS)]
    RESB = [dscr(f"resb{s}", [D, seqT(s)]) for s in range(NS)]
    XBC = [dscr(f"xbc{s}", [D, seqT(s)]) for s in range(NS)]
    RKV = [dscr(f"rkv{s}", [24, 64, seqT(s)]) for s in range(NS)]
    XWA = [dscr(f"xwa{s}", [4, 64, seqT(s)]) for s in range(NS)]
    XG = [dscr(f"xg{s}", [160, seqT(s)]) for s in range(NS)]
    ZDT = [dscr(f"zdt{s}", [seqT(s), 528]) for s in range(NS)]
    YF = [dscr(f"yf{s}", [seqT(s), 512]) for s in range(NS)]
    OF = [dscr(f"of{s}", [seqT(s), 520]) for s in range(NS)]
    MIX = [dscr(f"mix{s}", [D, seqT(s)], BF16) for s in range(NS)]
    GATE = [dscr(f"gate{s}", [DFF, seqT(s)]) for s in range(NS)]
    VAL = [dscr(f"val{s}", [DFF, seqT(s)], BF16) for s in range(NS)]
    ACTV = [dscr(f"actv{s}", [DFF, seqT(s)], BF16) for s in range(NS)]

    def fm(ap):
        return ap.rearrange("(k p) t -> p k t", p=128)

    def sb(name, shape, dt=F32):
        return nc.alloc_sbuf_tensor(name, list(shape), dt).ap()

    CST = sb("CST", [128, NCST])
    CSTB = sb("CSTB", [128, 768], BF16)
    PV = sb("PV", [128, NPV])
    PV64 = sb("PV64", [64, NPV64])
    ROWB = sb("ROWB", [128, NROW])
    MOD = sb("MOD", [128, 48, 3])
    DER = sb("DER", [128, 4, 8, 3])
    NEGA = sb("NEGA", [128, 16])
    OMMU = sb("OMMU", [64, 8])
    em.dma('sp', CST, cst, writes=['CST'])
    em.cp('dve', CSTB, CST[:, 0:768], ['CST'], ['CSTB'])
    ident = CST[:, C_ID:C_ID + 128]
    identb = CSTB[:, C_ID:C_ID + 128]
    onesb = CSTB[:, C_ONE:C_ONE + 128]
    ones = CST[:, C_ONE:C_ONE + 128]

    def stage_scope():
        return ExitStack()

    def stage_mod(l):
        em.dma('sp', PV, pv[l], writes=['PV'])
        em.dma('sp', PV64, pv64[l], writes=['PV64'])
        em.dma('pool', ROWB, rowv[l].partition_broadcast(128), writes=['ROWB'])
        with ExitStack() as es:
            def S(name, shape, dt=F32):
                return es.enter_context(nc.sbuf_tensor(name, list(shape), dt)).ap()
            cts = S(f"cts{l}", [128, 8, 3])
            sc = S(f"sc{l}", [128, 8, 3])
            wst = [S(f"wmst{l}_{i}", [128, 8, 512]) for i in range(2)]
            ps = es.enter_context(nc.psum_tensor(f"psmod{l}", [128, 512], F32)).ap()
            em.dma('sp', cts, cT.rearrange("(k p) j -> p k j", p=128), writes=['cts'])
            em.act(sc, cts, AF.Silu, ['cts'], ['sc'])
            for g in range(12):
                w = wst[g % 2]
                wk = f"wmst{g % 2}"
                em.dma('sp' if g % 2 == 0 else 'pool', w,
                       w_mod[l][:, g * 512:(g + 1) * 512].rearrange("(k p) n -> p k n", p=128), writes=[wk])
                for mi in range(4):
                    m = g * 4 + mi
                    for k in range(8):
                        em.mm(ps[:, m * 3:(m + 1) * 3], w[:, k, mi * 128:(mi + 1) * 128], sc[:, k, :],
                              k == 0, k == 7, [wk, 'sc'], ['psmod'])
            em.tt('dve', MOD, ps[:, 0:144].rearrange("p (m j) -> p m j", j=3),
                  PV[:, PV_BMOD:PV_BMOD + 48].unsqueeze(2).to_broadcast([128, 48, 3]), ALU.add,
                  ['psmod', 'PV'], ['MOD'])
            tmp = S(f"dertmp{l}", [128, 8, 3])

            def gain(idx, goff, mlo, plus1):
                if plus1:
                    em.ts('dve', tmp, MOD[:, mlo:mlo + 8, :], 1.0, None, ALU.add, None, ['MOD'], ['dertmp'])
                    src = tmp
                    rk = ['dertmp', 'PV']
                else:
                    src = MOD[:, mlo:mlo + 8, :]
                    rk = ['MOD', 'PV']
                em.tt('dve', DER[:, idx, :, :], src,
                      PV[:, goff:goff + 8].unsqueeze(2).to_broadcast([128, 8, 3]), ALU.mult, rk, ['DER'])
            gain(0, PV_GPRE1, 8, True)
            gain(1, PV_GPOST1, 16, False)
            gain(2, PV_GPRE2, 32, True)
            gain(3, PV_GPOST2, 40, False)
            em.act(NEGA, ROWB[:, RV_ALOG:RV_ALOG + 16], AF.Exp, ['ROWB'], ['NEGA'])
            em.ts('dve', NEGA, NEGA, -1.0, None, ALU.mult, None, ['NEGA'], ['NEGA'])
            em.ts('dve', OMMU, PV64[:, P6_KA:P6_KA + 8], -1.0, 1.0, ALU.mult, ALU.add, ['PV64'], ['OMMU'])
            em.barrier()

    def prenorm(xt, TW, h, sq, rstd, ps, jmod, gidx, sidx, kx, kh, kps, sqk='sq', rsk='rstd'):
        em.act(sq[:, :, :TW], xt[:, :, :TW], AF.Square, [kx], [sqk])
        for k in range(8):
            em.mm(ps[:, :TW], onesb, sq[:, k, :TW], k == 0, k == 7, [sqk, 'CSTB'], [kps])
        em.act(rstd[:, :TW], ps[:, :TW], AF.Sqrt, [kps], [rsk], bias=EPS, scale=1.0 / D)
        em.op('dve', lambda e: e.reciprocal(out=rstd[:, :TW], in_=rstd[:, :TW]), [rsk], [rsk])
        for k in range(8):
            em.stt(xt[:, k, :TW], xt[:, k, :TW], DER[:, gidx, k, jmod:jmod + 1], rstd[:, :TW], ALU.mult, ALU.mult,
                   [kx, rsk, 'DER'], [kx])
            em.act(h[:, k, :TW], xt[:, k, :TW], AF.Identity, [kx, 'MOD'], [kh],
                   bias=MOD[:, sidx + k, jmod:jmod + 1], scale=1.0)

    def load_wbf(es, l, wap, Kc, N, name, piece=None):
        wb = es.enter_context(nc.sbuf_tensor(f"{name}bf{l}", [128, Kc, N], BF16)).ap()
        with ExitStack() as e2:
            sts = [e2.enter_context(nc.sbuf_tensor(f"{name}st{l}_{i}", [128, N], F32)).ap() for i in range(2)]
            for k in range(Kc):
                st = sts[k % 2]
                sk = f"{name}st{k % 2}"
                em.dma('sp' if k % 2 == 0 else 'pool', st, wap[k * 128:(k + 1) * 128, :], writes=[sk])
                em.cp('act' if k % 2 == 0 else 'dve', wb[:, k, :], st, [sk], [name + 'bf'])
            em.barrier()
        return wb

    def res_src(l, s, phase):
        b = s // 2
        if phase == 0:
            if l == 0:
                return (xT[b] if s % 2 == 1 else ctxT[b]), f"in{s}"
            return RESB[s], f"resb{s}"
        return RESA[s], f"resa{s}"

    def res_dst(l, s, phase):
        b = s // 2
        if phase == 0:
            return RESA[s], f"resa{s}"
        if l == n_layers - 1 and s % 2 == 1:
            return outT[b], f"out{s}"
        return RESB[s], f"resb{s}"

    def stage_inproj(l):
        with ExitStack() as es:
            def S(name, shape, dt=F32):
                return es.enter_context(nc.sbuf_tensor(name, list(shape), dt)).ap()
            wb = load_wbf(es, l, w_in[l], 8, 3504, "win")
            w2 = S(f"ipw2{l}", [128, 8, 1952], BF16)
            with ExitStack() as e2:
                mub = e2.enter_context(nc.sbuf_tensor(f"ipmub{l}", [128, 1952], F32)).ap()
                em.dma('sp', mub, rmu[l].partition_broadcast(128), writes=['ipmub'])
                for k in range(8):
                    em.stt(w2[:, k, :], mub, 0.5, wb[:, k, 1552:3504], ALU.mult, ALU.mult, ['ipmub', 'winbf'], ['ipw2'])
                em.ts('dve', mub, mub, -1.0, 1.0, ALU.mult, ALU.add, ['ipmub'], ['ipmub'])
                for k in range(8):
                    em.tt('dve', wb[:, k, 1552:3504], wb[:, k, 1552:3504], mub, ALU.mult, ['ipmub', 'winbf', 'ipw2'], ['winbf'])
                em.barrier()
            xt = S(f"ipx{l}", [128, 8, 512])
            h = S(f"iph{l}", [128, 8, 512], BF16)
            sq = S(f"ipsq{l}", [128, 8, 512], BF16)
            hs = sq
            rstd = S(f"iprs{l}", [128, 512])
            xh = S(f"ipxh{l}", [128, 8, 2])
            hh = S(f"iphh{l}", [128, 8, 2], BF16)
            sqh = S(f"ipsqh{l}", [128, 8, 2], BF16)
            rstdh = S(f"iprsh{l}", [128, 2])
            sta = [S(f"ipsta{l}_{i}", [128, 8, 512]) for i in range(2)]
            stw = S(f"ipstw{l}", [64, 4, 512])
            stg0 = S(f"ipstg0{l}", [128, 512])
            stg1 = S(f"ipstg1{l}", [32, 512])
            stz = S(f"ipstz{l}", [128, 4, 528])
            pss = [es.enter_context(nc.psum_tensor(f"ipps{l}_{i}", [128, 512], F32)).ap() for i in range(8)]
            groups = [('xbc', 512, 128, 8), ('r', 1552, 64, 8), ('k', 2064, 64, 8), ('v', 2576, 64, 8)]
            ev = 0
            for s in range(NS):
                T = seqT(s)
                TW = min(512, T)
                jmod = 2 if s % 2 == 0 else s // 2
                src, srck = res_src(l, s, 0)
                for tt_ in range(T // TW):
                    t0 = tt_ * TW
                    em.dma('sp', xt[:, :, :TW], fm(src)[:, :, t0:t0 + TW], reads=[srck], writes=['ipx'])
                    cl = max(t0 - 1, 0)
                    cr = min(t0 + TW, T - 1)
                    em.dma('sp', xh[:, :, 0:1], fm(src)[:, :, cl:cl + 1], reads=[srck], writes=['ipxh'], allow_slow_non_contiguous=True)
                    em.dma('sp', xh[:, :, 1:2], fm(src)[:, :, cr:cr + 1], reads=[srck], writes=['ipxh'], allow_slow_non_contiguous=True)
                    prenorm(xt, TW, h, sq, rstd, pss[7], jmod, 0, 0, 'ipx', 'iph', 'ps7')
                    prenorm(xh, 2, hh, sqh, rstdh, pss[6], jmod, 0, 0, 'ipxh', 'iphh', 'ps6', sqk='ipsqh', rsk='iprsh')
                    if t0 == 0:
                        em.memset('pool', hh[:, :, 0:1], 0.0, ['iphh'])
                    if t0 + TW == T:
                        em.memset('pool', hh[:, :, 1:2], 0.0, ['iphh'])
                    em.tt('dve', hs[:, :, 1:TW - 1], h[:, :, 0:TW - 2], h[:, :, 2:TW], ALU.add, ['iph', 'sq'], ['sq'])
                    em.tt('dve', hs[:, :, 0:1], hh[:, :, 0:1], h[:, :, 1:2], ALU.add, ['iph', 'iphh', 'sq'], ['sq'])
                    em.tt('dve', hs[:, :, TW - 1:TW], h[:, :, TW - 2:TW - 1], hh[:, :, 1:2], ALU.add, ['iph', 'iphh', 'sq'], ['sq'])
                    pi = 0
                    for gi, (gname, c0, wdt, nb) in enumerate(groups):
                        st = sta[gi % 2]
                        stk = f"ipsta{gi % 2}"
                        for j in range(nb):
                            ps = pss[pi % 6]
                            pk = f"ps{pi % 6}"
                            pi += 1
                            cc = c0 + j * wdt
                            shifted = (gname != 'xbc')
                            for k in range(8):
                                em.mm(ps[:wdt, :TW], wb[:, k, cc:cc + wdt], h[:, k, :TW], k == 0, (k == 7) and not shifted,
                                      ['winbf', 'iph'], [pk])
                            if shifted:
                                for k in range(8):
                                    em.mm(ps[:wdt, :TW], w2[:, k, cc - 1552:cc - 1552 + wdt], hs[:, k, :TW], False, k == 7,
                                          ['ipw2', 'sq'], [pk])
                            em.cp('act' if ev % 2 == 0 else 'dve', st[:wdt, j, :TW], ps[:wdt, :TW], [pk], [stk])
                            ev += 1
                        if gname == 'xbc':
                            em.dma('pool', fm(XBC[s])[:, :, t0:t0 + TW], st[:, :, :TW], reads=[stk], writes=[f"xbc{s}"])
                        else:
                            jb = {'r': 0, 'k': 8, 'v': 16}[gname]
                            em.dma('pool', RKV[s][jb:jb + 8].rearrange("j p t -> p j t")[:, :, t0:t0 + TW],
                                   st[:64, :, :TW], reads=[stk], writes=[f"rkv{s}"])
                    for j in range(4):
                        ps = pss[pi % 6]
                        pk = f"ps{pi % 6}"
                        pi += 1
                        cc = 3088 + j * 64
                        for k in range(8):
                            em.mm(ps[:64, :TW], wb[:, k, cc:cc + 64], h[:, k, :TW], k == 0, False, ['winbf', 'iph'], [pk])
                        for k in range(8):
                            em.mm(ps[:64, :TW], w2[:, k, cc - 1552:cc - 1552 + 64], hs[:, k, :TW], False, k == 7, ['ipw2', 'sq'], [pk])
                        em.cp('act' if ev % 2 == 0 else 'dve', stw[:, j, :TW], ps[:64, :TW], [pk], ['ipstw'])
                        ev += 1
                    em.dma('pool', XWA[s].rearrange("j p t -> p j t")[:, :, t0:t0 + TW], stw[:, :, :TW],
                           reads=['ipstw'], writes=[f"xwa{s}"])
                    for (cc, wdt, st, stk, r0) in [(3344, 128, stg0, 'ipstg0', 0), (3472, 32, stg1, 'ipstg1', 128)]:
                        ps = pss[pi % 6]
                        pk = f"ps{pi % 6}"
                        pi += 1
                        for k in range(8):
                            em.mm(ps[:wdt, :TW], wb[:, k, cc:cc + wdt], h[:, k, :TW], k == 0, False, ['winbf', 'iph'], [pk])
                        for k in range(8):
                            em.mm(ps[:wdt, :TW], w2[:, k, cc - 1552:cc - 1552 + wdt], hs[:, k, :TW], False, k == 7, ['ipw2', 'sq'], [pk])
                        em.cp('act' if ev % 2 == 0 else 'dve', st[:wdt, :TW], ps[:wdt, :TW], [pk], [stk])
                        ev += 1
                        em.dma('pool', XG[s][r0:r0 + wdt, t0:t0 + TW], st[:wdt, :TW], reads=[stk], writes=[f"xg{s}"])
                    for i in range(TW // 128):
                        ps = pss[pi % 6]
                        pk = f"ps{pi % 6}"
                        pi += 1
                        ps2 = pss[6]
                        for k in range(8):
                            em.mm(ps[:, 0:512], h[:, k, i * 128:(i + 1) * 128], wb[:, k, 0:512], k == 0, k == 7,
                                  ['winbf', 'iph'], [pk])
                        for k in range(8):
                            em.mm(ps2[:, 0:16], h[:, k, i * 128:(i + 1) * 128], wb[:, k, 1536:1552], k == 0, k == 7,
                                  ['winbf', 'iph'], ['ps6'])
                        em.cp('act', stz[:, i, 0:512], ps[:, 0:512], [pk], ['ipstz'])
                        em.cp('dve', stz[:, i, 512:528], ps2[:, 0:16], ['ps6'], ['ipstz'])
                    em.dma('pool', ZDT[s][t0:t0 + TW, :].rearrange("(i p) c -> p i c", p=128), stz[:, :TW // 128, :],
                           reads=['ipstz'], writes=[f"zdt{s}"])
            em.barrier()

    def stage_mixer(l, need_ctx_out):
        with ExitStack() as es:
            def S(name, shape, dt=F32):
                return es.enter_context(nc.sbuf_tensor(name, list(shape), dt)).ap()
            w2b = S(f"w2b{l}", [64, 2, 512], BF16)
            a2b = S(f"a2b{l}", [64, 2, 512], BF16)
            g2b0 = S(f"g2b0{l}", [128, 512], BF16)
            g2b1 = S(f"g2b1{l}", [32, 512], BF16)
            with ExitStack() as e2:
                t1 = e2.enter_context(nc.sbuf_tensor(f"lst1{l}", [64, 2, 512], F32)).ap()
                t2 = e2.enter_context(nc.sbuf_tensor(f"lst2{l}", [64, 2, 512], F32)).ap()
                t3 = e2.enter_context(nc.sbuf_tensor(f"lst3{l}", [128, 512], F32)).ap()
                t4 = e2.enter_context(nc.sbuf_tensor(f"lst4{l}", [32, 512], F32)).ap()
                em.dma('sp', t1, r_w2[l].rearrange("d r c -> r d c"), writes=['lst1'])
                em.dma('sp', t2, r_a2[l].rearrange("d r c -> r d c"), writes=['lst2'])
                em.dma('sp', t3, r_g2[l][0:128, :], writes=['lst3'])
                em.dma('sp', t4, r_g2[l][128:160, :], writes=['lst4'])
                em.cp('dve', w2b, t1, ['lst1'], ['w2b'])
                em.cp('dve', a2b, t2, ['lst2'], ['a2b'])
                em.cp('dve', g2b0, t3, ['lst3'], ['g2b'])
                em.cp('dve', g2b1, t4, ['lst4'], ['g2b'])
                em.barrier()
            MAR = [None, None]
            MARt = S(f"mar{l}", [128, 2, 256])
            em.cp('dve', MARt[:, 0, 0:128], CST[:, C_UTS:C_UTS + 128], ['CST'], ['MAR'])
            em.cp('dve', MARt[:, 0, 128:256], CST[:, C_UTI:C_UTI + 128], ['CST'], ['MAR'])
            em.cp('dve', MARt[:, 1, 0:128], CST[:, C_LTS:C_LTS + 128], ['CST'], ['MAR'])
            em.cp('dve', MARt[:, 1, 128:256], CST[:, C_LTI:C_LTI + 128], ['CST'], ['MAR'])
            MSO = S(f"mso{l}", [128, 2, 256])
            em.cp('dve', MSO[:, 0, 0:128], CST[:, C_LTS:C_LTS + 128], ['CST'], ['MSO'])
            em.cp('dve', MSO[:, 1, 0:128], CST[:, C_UTS:C_UTS + 128], ['CST'], ['MSO'])
            em.cp('dve', MSO[:, 0, 128:256], ones, ['CST'], ['MSO'])
            em.cp('dve', MSO[:, 1, 128:256], ones, ['CST'], ['MSO'])
            RMK = S(f"rmk{l}", [64, 8, 128], BF16)
            em.memset('pool', RMK, 1.0, ['RMK'])
            em.memset('pool', RMK[:, :, 0:1], 0.0, ['RMK'])

            def MI(d):
                return CST[:, C_UTI:C_UTI + 128] if d == 0 else CST[:, C_LTI:C_LTI + 128]

            def strictTS(d):
                return CST[:, C_LTS:C_LTS + 128] if d == 0 else CST[:, C_UTS:C_UTS + 128]

            xbc = S(f"m_xbc{l}", [128, 8, 130])
            cacc = S(f"m_cacc{l}", [128, 8, 128])
            bct = S(f"m_bct{l}", [128, 4, 128], BF16)
            xtok = S(f"m_xtok{l}", [128, 512])
            xtokb = S(f"m_xtokb{l}", [128, 512], BF16)
            btokb = S(f"m_btokb{l}", [128, 256], BF16)
            zdt = S(f"m_zdt{l}", [128, 528])
            dts = S(f"m_dts{l}", [128, 8])
            dta = S(f"m_dta{l}", [128, 8])
            sm = S(f"m_sm{l}", [128, 40])
            xw = S(f"m_xw{l}", [128, 512], BF16)
            l2 = [S(f"m_l2{l}_{i}", [128, 256]) for i in range(2)]
            Et = [S(f"m_E{l}_{i}", [128, 256]) for i in range(2)]
            LTt = [S(f"m_LT{l}_{i}", [128, 128]) for i in range(2)]
            STt = [S(f"m_ST{l}_{i}", [128, 128], BF16) for i in range(2)]
            CsT = [S(f"m_Cs{l}_{i}", [128, 128], BF16) for i in range(2)]
            hst = S(f"m_hst{l}", [128, 512])
            hstb = S(f"m_hstb{l}", [128, 512], BF16)
            yt = S(f"m_y{l}", [128, 512])
            yf = S(f"m_yf{l}", [128, 512])
            zs = S(f"m_zs{l}", [128, 512])
            ysq = S(f"m_ysq{l}", [128, 512])
            gst = S(f"m_gst{l}", [128, 4])
            mixo = S(f"mixo{l}", [128, 8, 128], BF16)
            ssum = S(f"r_ssum{l}", [64, 26, 128])
            pp = ssum
            xg0 = S(f"r_xg0{l}", [128, 130])
            xg1 = S(f"r_xg1{l}", [32, 130])
            sg0 = S(f"r_sg0{l}", [128, 128], BF16)
            sg1 = S(f"r_sg1{l}", [32, 128], BF16)
            xgt = S(f"r_xgt{l}", [128, 128])
            twb = S(f"r_twb{l}", [64, 2, 128], BF16)
            lw = S(f"r_lw{l}", [64, 8, 128])
            aa = S(f"r_aa{l}", [64, 8, 128])
            kk = S(f"r_kk{l}", [64, 8, 128])
            kd = S(f"r_kd{l}", [64, 8, 128])
            linc = S(f"r_linc{l}", [64, 8, 128])
            lex = S(f"r_lex{l}", [64, 8, 128])
            rinv = lex
            e1 = S(f"r_e1{l}", [64, 8, 128])
            e0 = S(f"r_e0{l}", [64, 8, 128])
            ei = S(f"r_ei{l}", [64, 8, 128])
            gCs = [S(f"r_gC{l}_{i}", [64, 8]) for i in range(2)]
            coefs = [S(f"r_coef{l}_{i}", [128, 8]) for i in range(2)]
            tmpk = S(f"r_tmpk{l}", [64, 8, 128])
            ARs = [S(f"r_AR{l}_{i}", [64, 8, 256], BF16) for i in range(2)]
            BKs = [S(f"r_BK{l}_{i}", [64, 8, 2, 128], BF16) for i in range(2)]
            prodb = S(f"r_prod{l}", [64, 8, 128], BF16)
            sqb = prodb
            vtokbs = [S(f"r_vtokb{l}_{i}", [128, 512], BF16) for i in range(2)]
            BKtoks = [S(f"r_BKtok{l}_{i}", [128, 8, 2, 64], BF16) for i in range(2)]
            GBs = [S(f"r_GB{l}_{i}", [128, 8, 256], BF16) for i in range(2)]
            GKs = [S(f"r_GK{l}_{i}", [128, 8, 256], BF16) for i in range(2)]
            Q0s = [S(f"r_Q0{l}_{i}", [128, 8, 128], BF16) for i in range(2)]
            Pm = [S(f"r_P{l}_{i}", [128, 8, 128], BF16) for i in range(2)]
            Qm = [S(f"r_Q{l}_{i}", [128, 8, 128], BF16) for i in range(2)]
            Ym = [S(f"r_Y{l}_{i}", [128, 8, 128], BF16) for i in range(2)]
            E1m = S(f"r_E1{l}", [128, 8, 128], BF16)
            E2m = S(f"r_E2{l}", [128, 8, 128], BF16)
            Dm = S(f"r_D{l}", [128, 8, 128], BF16)
            Zm = S(f"r_Z{l}", [128, 8, 128], BF16)
            Wt = S(f"r_W{l}", [128, 512], BF16)
            Ut = S(f"r_U{l}", [128, 512], BF16)
            Hs = S(f"r_H{l}", [64, 512])
            Hb = S(f"r_Hb{l}", [64, 512], BF16)
            ot = S(f"r_o{l}", [128, 520])
            oft = S(f"r_of{l}", [128, 520])
            osq = S(f"r_osq{l}", [128, 512])
            gn = S(f"r_gn{l}", [128, 40])
            B = [es.enter_context(nc.psum_tensor(f"mxps{l}_{i}", [128, 512], F32)).ap() for i in range(7)]
            BBp = es.enter_context(nc.psum_tensor(f"mxpsb{l}", [128, 1024], BF16)).ap()

            nbw = ROWB[:, RV_MNW:RV_MNW + 512]
            lnw = ROWB[:, RV_LNW:RV_LNW + 512]
            lnb = ROWB[:, RV_LNB:RV_LNB + 512]
            mdb = ROWB[:, RV_MD:RV_MD + 8]
            evc = [0]

            def evq():
                evc[0] += 1
                return 'act' if evc[0] % 2 == 0 else 'dve'

            def load_halo(tile, tk, srcap, srck, t0, T, nblk_dims):
                lo = max(t0 - 1, 0)
                hi = min(t0 + 129, T)
                o = lo - (t0 - 1)
                if nblk_dims:
                    em.dma('sp', tile[:, :, o:o + hi - lo], srcap[:, :, lo:hi], reads=[srck], writes=[tk])
                    if t0 == 0:
                        em.memset('pool', tile[:, :, 0:1], 0.0, [tk])
                    if t0 + 128 == T:
                        em.memset('pool', tile[:, :, 129:130], 0.0, [tk])
                else:
                    em.dma('sp', tile[:, o:o + hi - lo], srcap[:, lo:hi], reads=[srck], writes=[tk])
                    if t0 == 0:
                        em.memset('pool', tile[:, 0:1], 0.0, [tk])
                    if t0 + 128 == T:
                        em.memset('pool', tile[:, 129:130], 0.0, [tk])

            def mamba_chunk(s, t0, d, emit_out):
                T = seqT(s)
                load_halo(xbc, 'm_xbc', fm(XBC[s]), f"xbc{s}", t0, T, True)
                em.dma('pool', zdt, ZDT[s][t0:t0 + 128, :], reads=[f"zdt{s}"], writes=['m_zdt'])
                if (mixcfg or {}).get('mstop', 99) <= 1:
                    return
                for j in range(8):
                    em.ts('dve', cacc[:, j, :], xbc[:, j, 1:129], PV[:, PV_MCW + 8 + j:PV_MCW + 9 + j],
                          PV[:, PV_MCB + j:PV_MCB + j + 1], ALU.mult, ALU.add, ['m_xbc', 'PV'], ['m_cacc'])
                    em.stt(cacc[:, j, :], xbc[:, j, 0:128], PV[:, PV_MCW + j:PV_MCW + j + 1], cacc[:, j, :],
                           ALU.mult, ALU.add, ['m_xbc', 'PV', 'm_cacc'], ['m_cacc'])
                    em.stt(cacc[:, j, :], xbc[:, j, 2:130], PV[:, PV_MCW + 16 + j:PV_MCW + 17 + j], cacc[:, j, :],
                           ALU.mult, ALU.add, ['m_xbc', 'PV', 'm_cacc'], ['m_cacc'])
                em.act(cacc[:, 0:6, :], cacc[:, 0:6, :], AF.Silu, ['m_cacc'], ['m_cacc'])
                em.act(bct[:, 2:4, :], cacc[:, 6:8, :], AF.Silu, ['m_cacc'], ['m_bct'])
                em.cp('dve', bct[:, 0:2, :], cacc[:, 4:6, :], ['m_cacc'], ['m_bct'])
                if (mixcfg or {}).get('mstop', 99) <= 2:
                    return
                msub = (mixcfg or {}).get('msub', 9)
                for j in range(4):
                    em.tr(B[4][:, j * 128:(j + 1) * 128], cacc[:, j, :], ident, ['m_cacc', 'CST'], ['B4'], inc=(j == 3))
                if msub >= 1:
                    for j in range(2):
                        em.tr(B[5][:, j * 128:(j + 1) * 128], cacc[:, 4 + j, :], ident, ['m_cacc', 'CST'], ['B5'])
                if msub >= 2 and msub != 33:
                    em.cp('dve', xtok, B[4], ['B4'], ['m_xtok'])
                if msub == 30:
                    em.cp('dve', xtokb, B[4], ['B4'], ['m_xtokb'])
                elif msub == 31:
                    em.cp('act', yt, B[4], ['B4'], ['m_y'])
                elif msub == 32:
                    em.cp('act', xtokb, xtok, ['m_xtok'], ['m_xtokb'])
                elif msub >= 3:
                    em.cp('act', xtokb, B[4], ['B4'], ['m_xtokb'])
                if msub >= 4:
                    em.cp('act', btokb, B[5][:, 0:256], ['B5'], ['m_btokb'])
                if (mixcfg or {}).get('mstop', 99) <= 3:
                    return
                em.tt('dve', dts, zdt[:, 512 + d * 8:520 + d * 8], ROWB[:, RV_DTB + d * 8:RV_DTB + d * 8 + 8], ALU.add,
                      ['m_zdt', 'ROWB'], ['m_dts'])
                em.act(dts, dts, AF.Exp, ['m_dts'], ['m_dts'])
                em.act(dts, dts, AF.Ln, ['m_dts'], ['m_dts'], bias=1.0, scale=1.0)
                em.tt('dve', dta, dts, NEGA[:, d * 8:d * 8 + 8], ALU.mult, ['m_dts', 'NEGA'], ['m_dta'])
                if (mixcfg or {}).get('mstop', 99) <= 4:
                    return
                em.mm(B[5][:, 256:264], MI(d), dta, True, True, ['CST', 'm_dta'], ['B5'])
                em.mm(B[5][:, 264:272], ones, dta, True, True, ['CST', 'm_dta'], ['B5'])
                em.cp('act', sm[:, 32:40], B[5][:, 264:272], ['B5'], ['m_sm'])
                em.tt('dve', sm[:, 24:32], sm[:, 32:40], B[5][:, 256:264], ALU.subtract, ['B5', 'm_sm'], ['m_sm'])
                em.act(sm[:, 0:8], sm[:, 24:32], AF.Exp, ['m_sm'], ['m_sm'])
                em.tt('dve', sm[:, 8:16], sm[:, 0:8], dts, ALU.mult, ['m_sm', 'm_dts'], ['m_sm'])
                em.act(sm[:, 16:24], B[5][:, 264:272], AF.Exp, ['B5'], ['m_sm'])
                em.tt('dve', xw.rearrange("p (h q) -> p h q", q=64), xtok.rearrange("p (h q) -> p h q", q=64),
                      sm[:, 8:16].unsqueeze(2).to_broadcast([128, 8, 64]), ALU.mult, ['m_xtok', 'm_sm'], ['m_xw'])
                if (mixcfg or {}).get('mstop', 99) <= 5:
                    return
                for g in range(2):
                    em.mm(B[6][:, g * 128:(g + 1) * 128], bct[:, g, :], bct[:, 2 + g, :], True, True, ['m_bct'], ['B6'])
                for h in range(8):
                    g = h // 4
                    i2 = h % 2
                    pe_ = B[6][:, 256:512] if i2 == 0 else B[5][:, 0:256]
                    pek = 'B6' if i2 == 0 else 'B5'
                    em.ts('dve', l2[i2], MSO[:, d, :], dta[:, h:h + 1], None, ALU.mult, None,
                          ['MSO', 'm_dta'], [f"m_l2{i2}"])
                    em.mm(pe_[:, 0:128], l2[i2][:, 0:128], MI(d), True, True, [f"m_l2{i2}", 'CST'], [pek])
                    em.mm(pe_[:, 128:256], l2[i2][:, 128:256], MI(d), True, True, [f"m_l2{i2}", 'CST'], [pek])
                    em.act(Et[i2], pe_[:, 0:256], AF.Exp, [pek], [f"m_E{i2}"])
                    em.stt(LTt[i2], Et[i2][:, 0:128], dts[:, h:h + 1], MI(d), ALU.mult, ALU.mult,
                           [f"m_E{i2}", 'm_dts', 'CST'], [f"m_LT{i2}"])
                    em.tt('dve', STt[i2], B[6][:, g * 128:(g + 1) * 128], LTt[i2], ALU.mult, ['B6', f"m_LT{i2}"],
                          [f"m_ST{i2}"])
                    em.tt('dve', CsT[i2], bct[:, 2 + g, :], Et[i2][:, 128:256], ALU.mult, ['m_bct', f"m_E{i2}"],
                          [f"m_Cs{i2}"])
                    em.mm(B[4][:, h * 64:(h + 1) * 64], STt[i2], xtokb[:, h * 64:(h + 1) * 64], True, False,
                          [f"m_ST{i2}", 'm_xtokb'], ['B4'])
                    em.mm(B[4][:, h * 64:(h + 1) * 64], CsT[i2], hstb[:, h * 64:(h + 1) * 64], False, True,
                          [f"m_Cs{i2}", 'm_hstb'], ['B4'])
                if (mixcfg or {}).get('mstop', 99) <= 6:
                    return
                for g in range(2):
                    em.mm(B[6][:, g * 256:(g + 1) * 256], btokb[:, g * 128:(g + 1) * 128], xw[:, g * 256:(g + 1) * 256],
                          True, True, ['m_btokb', 'm_xw'], ['B6'])
                em.tt('dve', hst.rearrange("p (h q) -> p h q", q=64), hst.rearrange("p (h q) -> p h q", q=64),
                      sm[:, 16:24].unsqueeze(2).to_broadcast([128, 8, 64]), ALU.mult, ['m_hst', 'm_sm', 'B4'], ['m_hst'])
                em.tt('dve', hst, hst, B[6], ALU.add, ['m_hst', 'B6'], ['m_hst'])
                em.cp('act', hstb, hst, ['m_hst', 'B4'], ['m_hstb'])
                if (mixcfg or {}).get('mstop', 99) <= 7:
                    return
                if d == 0:
                    if emit_out:
                        em.cp('act', yt, B[4], ['B4'], ['m_y'])
                        em.dma('pool', YF[s][t0:t0 + 128, :], yt, reads=['m_y'], writes=[f"yf{s}"])
                elif emit_out:
                    em.dma('sp', yf, YF[s][t0:t0 + 128, :], reads=[f"yf{s}"], writes=['m_yf'])
                    em.tt('dve', yt, B[4], yf, ALU.add, ['B4', 'm_yf'], ['m_y'])
                    em.tt('dve', yf.rearrange("p (h q) -> p h q", q=64), xtok.rearrange("p (h q) -> p h q", q=64),
                          mdb.unsqueeze(2).to_broadcast([128, 8, 64]), ALU.mult, ['m_xtok', 'ROWB', 'm_yf'], ['m_yf'])
                    em.tt('dve', yt, yt, yf, ALU.add, ['m_y', 'm_yf'], ['m_y'])
                    em.act(zs, zdt[:, 0:512], AF.Silu, ['m_zdt'], ['m_zs'])
                    em.tt('dve', yt, yt, zs, ALU.mult, ['m_y', 'm_zs'], ['m_y'])
                    for g in range(2):
                        em.act(ysq[:, g * 256:(g + 1) * 256], yt[:, g * 256:(g + 1) * 256], AF.Square, ['m_y'],
                               ['m_ysq', 'm_gst'], accum=gst[:, g:g + 1])
                    em.act(gst[:, 2:4], gst[:, 0:2], AF.Sqrt, ['m_gst'], ['m_gst'], bias=EPS, scale=1.0 / 256)
                    em.op('dve', lambda e: e.reciprocal(out=gst[:, 2:4], in_=gst[:, 2:4]), ['m_gst'], ['m_gst'])
                    for g in range(2):
                        em.stt(yt[:, g * 256:(g + 1) * 256], yt[:, g * 256:(g + 1) * 256], gst[:, 2 + g:3 + g],
                               nbw[:, g * 256:(g + 1) * 256], ALU.mult, ALU.mult, ['m_y', 'm_gst', 'ROWB'], ['m_y'])
                    for j in range(4):
                        em.tr(B[5][:, j * 128:(j + 1) * 128], yt[:, j * 128:(j + 1) * 128], ident, ['m_y', 'CST'], ['B5'], inc=(j == 3))
                    em.cp('act', mixo[:, 0:4, :], B[5].rearrange("p (j t) -> p j t", t=128), ['B5'], ['mixo_m'])
                    em.dma('pool', fm(MIX[s])[:, 0:4, t0:t0 + 128], mixo[:, 0:4, :], reads=['mixo_m'], writes=[f"mixm{s}"])

            def wkv_chunk(s, t0, d, emit_out, idx=0, first_of_d=False, split=True):
                T = seqT(s)
                pb = idx % 2
                AR, BK, GB, GK, Q0, BKtok, vtokb, gC, coef = (ARs[pb], BKs[pb], GBs[pb], GKs[pb], Q0s[pb], BKtoks[pb],
                                                             vtokbs[pb], gCs[pb], coefs[pb])
                kAR, kBK, kGB, kGK, kQ0, kBKtok, kvtokb, kgC, kcoef = [f"{n}{pb}" for n in
                                                                      ('r_AR', 'r_BK', 'r_GB', 'r_GK', 'r_Q0h', 'r_BKtok',
                                                                       'r_vtokb', 'r_gC', 'r_coef')]
                if split:
                    em.stream = ('rp', idx)
                fin = (d == 1 and emit_out)
                em.dma('sp', pp[:, 0:24, :], RKV[s].rearrange("j p t -> p j t")[:, :, t0:t0 + 128], reads=[f"rkv{s}"], writes=['r_ssum'])
                em.dma('sp', pp[:, 24:25, :], XWA[s][d:d + 1].rearrange("j p t -> p j t")[:, :, t0:t0 + 128], reads=[f"xwa{s}"],
                       writes=['r_ssum'])
                em.dma('sp', pp[:, 25:26, :], XWA[s][2 + d:3 + d].rearrange("j p t -> p j t")[:, :, t0:t0 + 128], reads=[f"xwa{s}"],
                       writes=['r_ssum'])
                rr = pp[:, 0:8, :]
                kr = pp[:, 8:16, :]
                vr = pp[:, 16:24, :]
                em.act(twb[:, 0, :], pp[:, 24, :], AF.Tanh, ['r_ssum'], ['r_twb'])
                em.cp('dve', twb[:, 1, :], pp[:, 25, :], ['r_ssum'], ['r_twb'])
                for h in range(8):
                    em.mm(B[h // 4][0:64, (h % 4) * 128:(h % 4 + 1) * 128], w2b[:, d, h * 64:(h + 1) * 64], twb[:, 0, :], True, True,
                          ['w2b', 'r_twb'], [f"B{h // 4}"], inc=(h == 7))
                for h in range(8):
                    em.act(lw[:, h, :], B[h // 4][0:64, (h % 4) * 128:(h % 4 + 1) * 128], AF.Sigmoid, [f"B{h // 4}", 'PV64'],
                           ['r_lw'], bias=PV64[:, P6_W0 + d * 8 + h:P6_W0 + d * 8 + h + 1], scale=1.0)
                for h in range(8):
                    em.mm(B[h // 4][0:64, (h % 4) * 128:(h % 4 + 1) * 128], a2b[:, d, h * 64:(h + 1) * 64], twb[:, 1, :], True, True,
                          ['a2b', 'r_twb'], [f"B{h // 4}"], inc=(h == 7))
                for h in range(8):
                    em.act(aa[:, h, :], B[h // 4][0:64, (h % 4) * 128:(h % 4 + 1) * 128], AF.Sigmoid,
                           [f"B{h // 4}", 'PV64'], ['r_aa'], bias=PV64[:, P6_A0 + d * 8 + h:P6_A0 + d * 8 + h + 1], scale=1.0)
                em.ts('dve', lw, lw, -R_DECAY_SCALE, None, ALU.mult, None, ['r_lw'], ['r_lw'])
                em.tt('dve', kk, kr, PV64[:, P6_KK:P6_KK + 8].unsqueeze(2).to_broadcast([64, 8, 128]), ALU.mult,
                      ['r_ssum', 'PV64'], ['r_kk'])
                em.act(sqb, kk, AF.Square, ['r_kk'], ['r_prod'])
                for hh in range(2):
                    em.mm(B[hh][0:64, :], onesb[0:64, 0:64], sqb[:, hh * 4:(hh + 1) * 4, :], True, True,
                          ['CSTB', 'r_prod'], [f"B{hh}"])
                for hh in range(2):
                    em.act(rinv[:, hh * 4:(hh + 1) * 4, :], B[hh][0:64, :].rearrange("p (h t) -> p h t", t=128), AF.Sqrt,
                           [f"B{hh}"], ['r_lex'])
                em.ts('dve', rinv, rinv, 1e-12, None, ALU.max, None, ['r_lex'], ['r_lex'])
                em.op('dve', lambda e: e.reciprocal(out=rinv, in_=rinv), ['r_lex'], ['r_lex'])
                em.tt('dve', kk, kk, rinv, ALU.mult, ['r_kk', 'r_lex'], ['r_kk'])
                em.tt('dve', tmpk, aa, PV64[:, P6_KA:P6_KA + 8].unsqueeze(2).to_broadcast([64, 8, 128]), ALU.mult,
                      ['r_aa', 'PV64'], ['r_tmpk'])
                em.tt('dve', tmpk, tmpk, OMMU.unsqueeze(2).to_broadcast([64, 8, 128]), ALU.add, ['r_tmpk', 'OMMU'], ['r_tmpk'])
                em.tt('dve', kd, kr, tmpk, ALU.mult, ['r_ssum', 'r_tmpk'], ['r_kd'])
                em.op('dve', lambda e: e.tensor_tensor_scan(out=linc.rearrange("p h t -> p (h t)"),
                                                            data0=RMK.rearrange("p h t -> p (h t)"),
                                                            data1=lw.rearrange("p h t -> p (h t)"), initial=0.0,
                                                            op0=ALU.mult, op1=ALU.add), ['RMK', 'r_lw'], ['r_linc'])
                if d == 0:
                    tot = linc[:, :, 127:128]
                else:
                    em.tt('dve', lex, lw, linc, ALU.subtract, ['r_lw', 'r_linc'], ['r_lex'])
                    em.cp('dve', gC, linc[:, :, 127], ['r_linc'], [kgC])
                    em.tt('dve', linc, lex, gC.unsqueeze(2).to_broadcast([64, 8, 128]), ALU.add, ['r_lex', kgC, 'r_linc'],
                          ['r_linc'])
                    tot = linc[:, :, 0:1]
                em.tt('dve', lex, linc, lw, ALU.subtract, ['r_linc', 'r_lw'], ['r_lex'])
                em.act(e1, linc, AF.Exp, ['r_linc'], ['r_e1'])
                em.act(e0, lex, AF.Exp, ['r_lex'], ['r_e0'])
                em.act(ei, linc, AF.Exp, ['r_linc'], ['r_ei'], scale=-1.0)
                em.act(gC, tot.rearrange("p h o -> p (h o)"), AF.Exp, ['r_linc', kgC], [kgC])
                em.tt('dve', AR[:, :, 128:256], rr, e1, ALU.mult, ['r_ssum', 'r_e1'], [kAR])
                em.stt(AR[:, :, 0:128], kk, -1.0, e0, ALU.mult, ALU.mult, ['r_kk', 'r_e0'], [kAR])
                em.tt('dve', tmpk, kk, aa, ALU.mult, ['r_kk', 'r_aa', 'r_tmpk'], ['r_tmpk'])
                em.tt('dve', BK[:, :, 0, :], tmpk, ei, ALU.mult, ['r_tmpk', 'r_ei'], [kBK])
                em.tt('dve', BK[:, :, 1, :], kd, ei, ALU.mult, ['r_kd', 'r_ei'], [kBK])
                em.tt('dve', tmpk, rr, kd, ALU.mult, ['r_ssum', 'r_kd', 'r_tmpk'], ['r_tmpk'])
                em.tt('dve', prodb, tmpk, PV64[:, P6_RK:P6_RK + 8].unsqueeze(2).to_broadcast([64, 8, 128]), ALU.mult,
                      ['r_tmpk', 'PV64'], ['r_prod'])
                for h in range(8):
                    em.tr(B[0][:, h * 64:(h + 1) * 64], vr[:, h, :], ident[0:64, 0:64], ['r_ssum', 'CST'], ['B0'], inc=(h == 7))
                em.cp('act', vtokb, B[0], ['B0'], [kvtokb])
                for half in range(2):
                    for h4 in range(4):
                        for q in range(2):
                            em.tr(BBp[:, (h4 * 2 + q) * 64:(h4 * 2 + q + 1) * 64], BK[:, half * 4 + h4, q, :], identb[0:64, 0:64],
                                  [kBK, 'CSTB'], ['BB'], inc=(h4 == 3 and q == 1))
                    em.cp('dve', BKtok[:, half * 4:(half + 1) * 4].rearrange("p h q k -> p (h q k)"), BBp[:, 0:512], ['BB'], [kBKtok])
                for h in range(8):
                    em.mm(B[1][:, 256 + h:257 + h], prodb[:, h, :], onesb[0:64, 0:1], True, True, ['r_prod', 'CSTB'], ['B1'], inc=(h == 7))
                em.cp('act', coef, B[1][:, 256:264], ['B1'], [kcoef])
                for hp in range(4):
                    bb_ = B[0]
                    bbk = "B0"
                    bk2 = B[1]
                    bk2k = "B1"
                    for q in range(2):
                        h = hp * 2 + q
                        em.mm(bb_[:, q * 256:(q + 1) * 256], BK[:, h, 0, :], AR[:, h, :], True, True, [kBK, kAR, 'r_lw', 'r_aa'], [bbk], inc=(q == 1))
                        em.mm(bk2[:, q * 256:(q + 1) * 256], BK[:, h, 1, :], AR[:, h, :], True, True, [kBK, kAR], [bk2k], inc=(q == 1))
                    em.tt('dve', GB[:, hp * 2:hp * 2 + 2, :], bb_.rearrange("p (q c) -> p q c", c=256),
                          MARt[:, d:d + 1, :].to_broadcast([128, 2, 256]), ALU.mult, [bbk, 'MAR'], [kGB])
                    em.tt('dve', Q0[:, hp * 2:hp * 2 + 2, :], bb_.rearrange("p (q c) -> p q c", c=256)[:, :, 0:128],
                          CST[:, C_MP0 + (1 - d) * 128:C_MP0 + (2 - d) * 128].unsqueeze(1).to_broadcast([128, 2, 128]), ALU.mult,
                          [bbk, 'CST'], [kQ0])
                    em.tt('dve', GK[:, hp * 2:hp * 2 + 2, :], bk2.rearrange("p (q c) -> p q c", c=256),
                          MARt[:, d:d + 1, :].to_broadcast([128, 2, 256]), ALU.mult, [bk2k, 'MAR'], [kGK])
                if split:
                    em.stream = ('rs', idx)
                if first_of_d:
                    em.memset('pool', Hs, 0.0, ['r_H'])
                    em.memset('pool', Hb, 0.0, ['r_Hb'])
                for hh in range(2):
                    bp = B[2 + hh]
                    bpk = f"B{2 + hh}"
                    for q in range(4):
                        h = hh * 4 + q
                        em.mm(bp[:, q * 128:(q + 1) * 128], AR[:, h, 0:128], BK[:, h, 0, :], True, True, [kAR, kBK], [bpk], inc=(q == 3))
                    b3 = bp.rearrange("p (q c) -> p q c", c=128)
                    hsl = slice(hh * 4, (hh + 1) * 4)
                    em.tt('dve', Pm[0][:, hsl, :], b3, CST[:, C_MP0 + d * 128:C_MP0 + (d + 1) * 128].unsqueeze(1).to_broadcast([128, 4, 128]),
                          ALU.mult, [bpk, 'CST'], ['r_P0'])
                    em.tt('dve', E1m[:, hsl, :], b3, CST[:, C_ME1 + d * 128:C_ME1 + (d + 1) * 128].unsqueeze(1).to_broadcast([128, 4, 128]),
                          ALU.mult, [bpk, 'CST'], ['r_E1'])
                    em.tt('dve', E2m[:, hsl, :], b3, CST[:, C_ME2 + d * 128:C_ME2 + (d + 1) * 128].unsqueeze(1).to_broadcast([128, 4, 128]),
                          ALU.mult, [bpk, 'CST'], ['r_E2'])
                em.tt('dve', Ym[0], Q0, identb.unsqueeze(1).to_broadcast([128, 8, 128]), ALU.add, [kQ0, 'CSTB'], ['r_Y0'])
                em.cp('act', Qm[0], Q0, [kQ0], ['r_Q0'])
                cur = 0
                for lev in range(1, 5):
                    nxt = 1 - cur
                    for hh in range(2):
                        bp = B[2]
                        bpk = "B2"
                        bq = B[3]
                        bqk = "B3"
                        hsl = slice(hh * 4, (hh + 1) * 4)
                        for q in range(4):
                            h = hh * 4 + q
                            em.mm(bp[:, q * 128:(q + 1) * 128], Qm[cur][:, h, :], Pm[cur][:, h, :], True, True,
                                  [f"r_Q{cur}", f"r_P{cur}"], [bpk], inc=(q == 3))
                        for q in range(4):
                            h = hh * 4 + q
                            em.mm(bq[:, q * 128:(q + 1) * 128], Pm[cur][:, h, :], Qm[cur][:, h, :], True, True,
                                  [f"r_Q{cur}", f"r_P{cur}"], [bqk], inc=(q == 3))
                        em.cp('act', Pm[nxt][:, hsl, :], bp.rearrange("p (q c) -> p q c", c=128), [bpk], [f"r_P{nxt}"])
                        em.cp('dve', Qm[nxt][:, hsl, :], bq.rearrange("p (q c) -> p q c", c=128), [bqk], [f"r_Q{nxt}"])
                    for hh in range(2):
                        by = B[2 + hh]
                        byk = f"B{2 + hh}"
                        hsl = slice(hh * 4, (hh + 1) * 4)
                        for q in range(4):
                            h = hh * 4 + q
                            em.mm(by[:, q * 128:(q + 1) * 128], Pm[nxt][:, h, :], Ym[cur][:, h, :], True, True,
                                  [f"r_P{nxt}", f"r_Y{cur}"], [byk], inc=(q == 3))
                        em.tt('dve', Ym[nxt][:, hsl, :], by.rearrange("p (q c) -> p q c", c=128), Ym[cur][:, hsl, :], ALU.add,
                              [byk, f"r_Y{cur}"], [f"r_Y{nxt}"])
                    cur = nxt
                Dt = Ym[cur]
                dtk = f"r_Y{cur}"
                for st, (Em_, ek) in enumerate([(E1m, 'r_E1'), (E2m, 'r_E2')]):
                    oth = Ym[1 - cur]
                    othk = f"r_Y{1 - cur}"
                    for half in range(2):
                        for h4 in range(4):
                            em.tr(BBp[:, 512 + h4 * 128:512 + (h4 + 1) * 128], Dt[:, half * 4 + h4, :], identb, [dtk, 'CSTB'], ['BB'], inc=(h4 == 3))
                        em.cp('act', Dm[:, half * 4:(half + 1) * 4, :], BBp[:, 512:1024].rearrange("p (h c) -> p h c", c=128),
                              ['BB'], ['r_D'])
                    for hh in range(2):
                        bz = B[2 + hh]
                        bzk = f"B{2 + hh}"
                        hsl = slice(hh * 4, (hh + 1) * 4)
                        for q in range(4):
                            h = hh * 4 + q
                            em.mm(bz[:, q * 128:(q + 1) * 128], Em_[:, h, :], Dt[:, h, :], True, True, [ek, dtk], [bzk], inc=(q == 3))
                        em.cp('act' if hh == 0 else 'dve', Zm[:, hsl, :], bz.rearrange("p (q c) -> p q c", c=128), [bzk], ['r_Z'])
                    for hh in range(2):
                        by = B[2 + hh]
                        byk = f"B{2 + hh}"
                        hsl = slice(hh * 4, (hh + 1) * 4)
                        for q in range(4):
                            h = hh * 4 + q
                            em.mm(by[:, q * 128:(q + 1) * 128], Dm[:, h, :], Zm[:, h, :], True, True, ['r_D', 'r_Z'], [byk], inc=(q == 3))
                        em.tt('dve', oth[:, hsl, :], by.rearrange("p (q c) -> p q c", c=128), Dt[:, hsl, :], ALU.add,
                              [byk, dtk], [othk])
                    cur = 1 - cur
                    Dt = Ym[cur]
                    dtk = f"r_Y{cur}"
                TT_ = Dt
                ttk = dtk
                for h in range(8):
                    hs_ = slice(h * 64, (h + 1) * 64)
                    em.mm(B[2][:, hs_], AR[:, h, 0:128], Hb[:, hs_], True, False, [kAR, 'r_Hb'], ['B2'])
                    em.mm(B[2][:, hs_], GK[:, h, 0:128], vtokb[:, hs_], False, True, [kGK, kvtokb], ['B2'], inc=(h == 7))
                em.cp('act', Wt, B[2], ['B2'], ['r_W'])
                for h in range(8):
                    hs_ = slice(h * 64, (h + 1) * 64)
                    em.mm(B[3][:, hs_], TT_[:, h, :], Wt[:, hs_], True, True, [ttk, 'r_W'], ['B3'], inc=(h == 7))
                em.cp('dve', Ut, B[3], ['B3'], ['r_U'])
                for h in range(8):
                    hs_ = slice(h * 64, (h + 1) * 64)
                    em.mm(B[2][:, hs_], AR[:, h, 128:256], Hb[:, hs_], True, False, [kAR, 'r_Hb'], ['B2'])
                    em.mm(B[2][:, hs_], GB[:, h, 128:256], Ut[:, hs_], False, False, [kGB, 'r_U'], ['B2'])
                    em.mm(B[2][:, hs_], GK[:, h, 128:256], vtokb[:, hs_], False, True, [kGK, kvtokb], ['B2'], inc=(h == 7))
                for h in range(8):
                    hs_ = slice(h * 64, (h + 1) * 64)
                    em.mm(B[3][0:64, hs_], BKtok[:, h, 0, :], Ut[:, hs_], True, False, [kBKtok, 'r_U'], ['B3'])
                    em.mm(B[3][0:64, hs_], BKtok[:, h, 1, :], vtokb[:, hs_], False, True, [kBKtok, kvtokb], ['B3'], inc=(h == 7))
                em.tt('dve', Hs, Hs, B[3][0:64, :], ALU.add, ['r_H', 'B3'], ['r_H'])
                em.tt('dve', Hs.rearrange("p (h v) -> p h v", v=64), Hs.rearrange("p (h v) -> p h v", v=64),
                      gC.unsqueeze(2).to_broadcast([64, 8, 64]), ALU.mult, ['r_H', kgC], ['r_H'])
                em.cp('act', Hb, Hs, ['r_H', 'B2'], ['r_Hb'])
                if d == 0:
                    if emit_out:
                        em.cp('act', ot[:, 0:512], B[2], ['B2'], ['r_o'])
                        em.cp('dve', ot[:, 512:520], coef, [kcoef], ['r_ocoef'])
                        em.dma('pool', OF[s][t0:t0 + 128, :], ot, reads=['r_o', 'r_ocoef'], writes=[f"of{s}"])
                elif emit_out:
                    em.dma('sp', oft, OF[s][t0:t0 + 128, :], reads=[f"of{s}"], writes=['r_of'])
                    em.tt('dve', ot[:, 0:512], B[2], oft[:, 0:512], ALU.add, ['B2', 'r_of'], ['r_o'])
                    o3 = ot[:, 0:512].rearrange("p (h v) -> p h v", v=64)
                    em.op('dve', lambda e: e.tensor_reduce(out=gn[:, 0:8], in_=o3, axis=AX.X, op=ALU.add), ['r_o'], ['r_gn'])
                    em.act(osq, ot[:, 0:512], AF.Square, ['r_o'], ['r_osq'])
                    em.op('dve', lambda e: e.tensor_reduce(out=gn[:, 8:16], in_=osq.rearrange("p (h v) -> p h v", v=64),
                                                           axis=AX.X, op=ALU.add), ['r_osq', 'r_gn'], ['r_gn'])
                    em.ts('dve', gn[:, 16:24], gn[:, 0:8], 1.0 / 64, None, ALU.mult, None, ['r_gn'], ['r_gn'])
                    em.tt('dve', gn[:, 0:8], gn[:, 16:24], gn[:, 16:24], ALU.mult, ['r_gn'], ['r_gn'])
                    em.stt(gn[:, 24:32], gn[:, 8:16], 1.0 / 64, gn[:, 0:8], ALU.mult, ALU.subtract, ['r_gn'], ['r_gn'])
                    em.act(gn[:, 24:32], gn[:, 24:32], AF.Sqrt, ['r_gn'], ['r_gn'], bias=R_LN_EPS, scale=1.0)
                    em.op('dve', lambda e: e.reciprocal(out=gn[:, 24:32], in_=gn[:, 24:32]), ['r_gn'], ['r_gn'])
                    em.tt('dve', o3, o3, gn[:, 16:24].unsqueeze(2).to_broadcast([128, 8, 64]), ALU.subtract, ['r_o', 'r_gn'], ['r_o'])
                    em.tt('dve', o3, o3, gn[:, 24:32].unsqueeze(2).to_broadcast([128, 8, 64]), ALU.mult, ['r_o', 'r_gn'], ['r_o'])
                    em.tt('dve', ot[:, 0:512], ot[:, 0:512], lnw, ALU.mult, ['r_o', 'ROWB'], ['r_o'])
                    em.tt('dve', ot[:, 0:512], ot[:, 0:512], lnb, ALU.add, ['r_o', 'ROWB'], ['r_o'])
                    em.tt('dve', gn[:, 32:40], coef, oft[:, 512:520], ALU.add, [kcoef, 'r_of', 'r_gn'], ['r_gn'])
                    em.tt('dve', osq.rearrange("p (h v) -> p h v", v=64), vtokb.rearrange("p (h v) -> p h v", v=64),
                          gn[:, 32:40].unsqueeze(2).to_broadcast([128, 8, 64]), ALU.mult, [kvtokb, 'r_gn', 'r_osq'], ['r_osq'])
                    em.tt('dve', ot[:, 0:512], ot[:, 0:512], osq, ALU.add, ['r_o', 'r_osq'], ['r_o'])
                    em.dma('sp', xg0[:, 0:128], XG[s][0:128, t0:t0 + 128], reads=[f"xg{s}"], writes=['r_xg0'])
                    em.dma('sp', xg1[:, 0:128], XG[s][128:160, t0:t0 + 128], reads=[f"xg{s}"], writes=['r_xg1'])
                    em.act(sg0, xg0[:, 0:128], AF.Sigmoid, ['r_xg0'], ['r_sg0'])
                    em.act(sg1, xg1[:, 0:128], AF.Sigmoid, ['r_xg1'], ['r_sg1'])
                    em.mm(B[3], sg0, g2b0, True, False, ['r_sg0', 'g2b'], ['B3'])
                    em.mm(B[3], sg1, g2b1, False, True, ['r_sg1', 'g2b'], ['B3'])
                    em.tt('dve', ot[:, 0:512], ot[:, 0:512], B[3], ALU.mult, ['r_o', 'B3'], ['r_o'])
                    for j in range(4):
                        em.tr(B[2][:, j * 128:(j + 1) * 128], ot[:, j * 128:(j + 1) * 128], ident, ['r_o', 'CST'], ['B2'], inc=(j == 3))
                    em.cp('act', mixo[:, 4:8, :], B[2].rearrange("p (j t) -> p j t", t=128), ['B2'], ['mixo_r'])
                    em.dma('pool', fm(MIX[s])[:, 4:8, t0:t0 + 128], mixo[:, 4:8, :], reads=['mixo_r'], writes=[f"mixr{s}"])

            mc_ = mixcfg or {}
            inter = mc_.get('interleave', True)
            for b in range(mc_.get('nb', NBL)):
                nw = 0
                for stream, fnc in (('m', mamba_chunk), ('r', wkv_chunk)):
                    if not mc_.get('mamba' if stream == 'm' else 'wkv', True):
                        continue
                    for d in range(mc_.get('nd', 2)):
                        first = True
                        if stream == 'm':
                            em.stream = 'm' if inter else None
                            em.memset('pool', hst, 0.0, ['m_hst'])
                            em.memset('pool', hstb, 0.0, ['m_hstb'])
                        for kind in range(mc_.get('nkind', 2)):
                            s = b * 2 + kind
                            T = seqT(s)
                            nch = T // CH
                            order = range(nch) if d == 0 else range(nch - 1, -1, -1)
                            emit = (kind == 1) or need_ctx_out or mc_.get('ctxout', False)
                            for c in order:
                                if stream == 'm':
                                    fnc(s, c * CH, d, emit)
                                else:
                                    fnc(s, c * CH, d, emit, idx=nw, first_of_d=first, split=inter)
                                    nw += 1
                                    first = False
                em.stream = None
                if inter:
                    em.flush_mixer(nw)
            em.barrier()

    def stage_proj_post(l, phase, wap, Kc, SRC, srcname, gidx, seqs):
        with ExitStack() as es:
            def S(name, shape, dt=F32):
                return es.enter_context(nc.sbuf_tensor(name, list(shape), dt)).ap()
            nm = f"pp{phase}"
            wb = load_wbf(es, l, wap, Kc, D, nm + "w")
            a = S(f"{nm}a{l}", [128, Kc, 512], BF16)
            xt = S(f"{nm}x{l}", [128, 8, 512])
            y = S(f"{nm}y{l}", [128, 8, 512])
            sq = S(f"{nm}sq{l}", [128, 8, 512], BF16)
            rstd = S(f"{nm}rs{l}", [128, 512])
            pss = [es.enter_context(nc.psum_tensor(f"{nm}ps{l}_{i}", [128, 512], F32)).ap() for i in range(5)]
            for s in seqs:
                T = seqT(s)
                TW = min(512, T)
                jmod = 2 if s % 2 == 0 else s // 2
                rsrc, rsk = res_src(l, s, phase)
                rdst, rdk = res_dst(l, s, phase)
                for tt_ in range(T // TW):
                    t0 = tt_ * TW
                    em.dma('sp', a[:, :, :TW], fm(SRC[s])[:, :, t0:t0 + TW], reads=([f"mixm{s}", f"mixr{s}"] if srcname == 'mix' else [f"{srcname}{s}"]), writes=[nm + 'a'])
                    em.dma('pool', xt[:, :, :TW], fm(rsrc)[:, :, t0:t0 + TW], reads=[rsk], writes=[nm + 'x'])
                    for m in range(8):
                        ps = pss[m % 4]
                        pk = f"ps{m % 4}"
                        for k in range(Kc):
                            em.mm(ps[:, :TW], wb[:, k, m * 128:(m + 1) * 128], a[:, k, :TW], k == 0, k == Kc - 1,
                                  [nm + 'wbf', nm + 'a'], [pk])
                        em.cp('dve', y[:, m, :TW], ps[:, :TW], [pk], [nm + 'y'])
                        em.act(sq[:, m, :TW], ps[:, :TW], AF.Square, [pk], [nm + 'sq'])
                    for m in range(8):
                        em.mm(pss[4][:, :TW], onesb, sq[:, m, :TW], m == 0, m == 7, ['CSTB', nm + 'sq'], ['ps4'])
                    em.act(rstd[:, :TW], pss[4][:, :TW], AF.Sqrt, ['ps4'], [nm + 'rs'], bias=EPS, scale=1.0 / D)
                    em.op('dve', lambda e: e.reciprocal(out=rstd[:, :TW], in_=rstd[:, :TW]), [nm + 'rs'], [nm + 'rs'])
                    for m in range(8):
                        em.stt(y[:, m, :TW], y[:, m, :TW], DER[:, gidx, m, jmod:jmod + 1], rstd[:, :TW], ALU.mult, ALU.mult,
                               [nm + 'y', nm + 'rs', 'DER'], [nm + 'y'])
                    em.tt('dve', xt[:, :, :TW], xt[:, :, :TW], y[:, :, :TW], ALU.add, [nm + 'x', nm + 'y'], [nm + 'x'])
                    em.dma('pool', fm(rdst)[:, :, t0:t0 + TW], xt[:, :, :TW], reads=[nm + 'x'], writes=[rdk])
            em.barrier()

    def stage_ffn_up(l, seqs):
        with ExitStack() as es:
            def S(name, shape, dt=F32):
                return es.enter_context(nc.sbuf_tensor(name, list(shape), dt)).ap()
            wb = load_wbf(es, l, f_w_up[l], 8, 2 * DFF, "wup")
            xt = S(f"fux{l}", [128, 8, 512])
            h = S(f"fuh{l}", [128, 8, 512], BF16)
            sq = S(f"fusq{l}", [128, 8, 512], BF16)
            rstd = S(f"furs{l}", [128, 512])
            stg = [S(f"fustg{l}_{i}", [128, 512]) for i in range(2)]
            stv = [S(f"fustv{l}_{i}", [128, 512], BF16) for i in range(2)]
            pss = [es.enter_context(nc.psum_tensor(f"fups{l}_{i}", [128, 512], F32)).ap() for i in range(8)]
            for s in seqs:
                T = seqT(s)
                TW = min(512, T)
                jmod = 2 if s % 2 == 0 else s // 2
                src, srck = res_src(l, s, 1)
                for tt_ in range(T // TW):
                    t0 = tt_ * TW
                    em.dma('sp', xt[:, :, :TW], fm(src)[:, :, t0:t0 + TW], reads=[srck], writes=['fux'])
                    prenorm(xt, TW, h, sq, rstd, pss[7], jmod, 2, 24, 'fux', 'fuh', 'ps7')
                    for j in range(NFF):
                        pg = pss[(2 * j) % 6]
                        pgk = f"ps{(2 * j) % 6}"
                        pv_ = pss[(2 * j + 1) % 6]
                        pvk = f"ps{(2 * j + 1) % 6}"
                        for k in range(8):
                            em.mm(pg[:, :TW], wb[:, k, j * 128:(j + 1) * 128], h[:, k, :TW], k == 0, k == 7, ['wupbf', 'fuh'], [pgk])
                        for k in range(8):
                            em.mm(pv_[:, :TW], wb[:, k, DFF + j * 128:DFF + (j + 1) * 128], h[:, k, :TW], k == 0, k == 7,
                                  ['wupbf', 'fuh'], [pvk])
                        sg_ = stg[j % 2]
                        sv_ = stv[j % 2]
                        em.cp('dve', sg_[:, :TW], pg[:, :TW], [pgk], [f"fustg{j % 2}"])
                        em.cp('act', sv_[:, :TW], pv_[:, :TW], [pvk], [f"fustv{j % 2}"])
                        em.dma('pool', GATE[s][j * 128:(j + 1) * 128, t0:t0 + TW], sg_[:, :TW], reads=[f"fustg{j % 2}"],
                               writes=[f"gate{s}"])
                        em.dma('sp', VAL[s][j * 128:(j + 1) * 128, t0:t0 + TW], sv_[:, :TW], reads=[f"fustv{j % 2}"],
                               writes=[f"val{s}"])
            em.barrier()

    def stage_ffn_conv(l, seqs):
        with ExitStack() as es:
            def S(name, shape, dt=F32):
                return es.enter_context(nc.sbuf_tensor(name, list(shape), dt)).ap()
            gflat = [S(f"fcg{l}_{i}", [128, 2048]) for i in range(2)]
            vflat = [S(f"fcv{l}_{i}", [128, 2048], BF16) for i in range(2)]
            gpx = [S(f"fcgpx{l}_{i}", [128, 34, 66], BF16) for i in range(2)]
            gpc = [S(f"fcgpc{l}_{i}", [128, 3, 258], BF16) for i in range(2)]
            dg = [S(f"fcdg{l}_{i}", [128, 9, 128], BF16) for i in range(2)]
            acc = S(f"fcacc{l}", [128, 2048])
            u = S(f"fcu{l}", [128, 2048])
            ab = [S(f"fcab{l}_{i}", [128, 2048], BF16) for i in range(2)]
            pss = [es.enter_context(nc.psum_tensor(f"fcps{l}_{i}", [128, 512], F32)).ap() for i in range(4)]
            for i in range(2):
                em.memset('pool', gpx[i], 0.0, [f"fcgpx{i}"])
                em.memset('pool', gpc[i], 0.0, [f"fcgpc{i}"])
            it = 0
            pi = 0
            for s in seqs:
                T = seqT(s)
                for j in range(NFF):
                    i2 = it % 2
                    it += 1
                    if s % 2 == 1:
                        R, Cc, gp, gpk = 32, 64, gpx[i2], f"fcgpx{i2}"
                    else:
                        R, Cc, gp, gpk = 1, 256, gpc[i2], f"fcgpc{i2}"
                    gf = gflat[i2]
                    vf = vflat[i2]
                    em.dma('sp', gf[:, :T], GATE[s][j * 128:(j + 1) * 128, :], reads=[f"gate{s}"], writes=[f"fcg{i2}"])
                    em.dma('sp', vf[:, :T], VAL[s][j * 128:(j + 1) * 128, :], reads=[f"val{s}"], writes=[f"fcv{i2}"])
                    em.cp('act', gp[:, 1:1 + R, 1:1 + Cc], gf[:, :T].rearrange("p (r c) -> p r c", c=Cc), [f"fcg{i2}"], [gpk])
                    taps = list(range(9)) if s % 2 == 1 else [3, 4, 5]
                    for tap in taps:
                        em.ts('dve', dg[i2][:, tap, :], identb, PV[:, PV_FCW + tap * NFF + j:PV_FCW + tap * NFF + j + 1], None,
                              ALU.mult, None, ['CSTB', 'PV'], [f"fcdg{i2}"])
                    nblk = T // 512 if s % 2 == 1 else 1
                    for blk in range(nblk):
                        ps = pss[pi % 4]
                        pk = f"ps{pi % 4}"
                        pi += 1
                        for ti, tap in enumerate(taps):
                            dr, dc = tap // 3 - 1, tap % 3 - 1
                            if s % 2 == 1:
                                rhs = gp[:, 1 + dr + blk * 8:1 + dr + blk * 8 + 8, 1 + dc:1 + dc + 64]
                                out = ps.rearrange("p (r c) -> p r c", c=64)
                                wdt = 512
                            else:
                                rhs = gp[:, 1, 1 + dc:1 + dc + 256]
                                out = ps[:, 0:256]
                                wdt = 256
                            em.mm(out, dg[i2][:, tap, :], rhs, ti == 0, ti == len(taps) - 1, [f"fcdg{i2}", gpk], [pk])
                        em.act(acc[:, blk * 512:blk * 512 + wdt], ps[:, 0:wdt], AF.Identity, [pk, 'PV'], ['fcacc'],
                               bias=PV[:, PV_FCB + j:PV_FCB + j + 1], scale=1.0)
                    em.act(u[:, :T], acc[:, :T], AF.Square, ['fcacc'], ['fcu'])
                    em.ts('dve', u[:, :T], u[:, :T], 0.044715, 1.0, ALU.mult, ALU.add, ['fcu'], ['fcu'])
                    em.tt('dve', u[:, :T], u[:, :T], acc[:, :T], ALU.mult, ['fcu', 'fcacc'], ['fcu'])
                    em.act(u[:, :T], u[:, :T], AF.Sigmoid, ['fcu'], ['fcu'], scale=GELU_C)
                    em.tt('dve', u[:, :T], u[:, :T], acc[:, :T], ALU.mult, ['fcu', 'fcacc'], ['fcu'])
                    em.tt('dve', ab[i2][:, :T], u[:, :T], vf[:, :T], ALU.mult, ['fcu', f"fcv{i2}"], [f"fcab{i2}"])
                    em.dma('pool', ACTV[s][j * 128:(j + 1) * 128, :], ab[i2][:, :T], reads=[f"fcab{i2}"], writes=[f"actv{s}"])
            em.barrier()

    allseq = list(range(NS))
    xseq = [s for s in range(NS) if s % 2 == 1]
    outkeys = []
    for l in range(n_layers):
        last = (l == n_layers - 1)
        stage_mod(l)
        if stop_after == 'mod':
            break
        stage_inproj(l)
        if stop_after == 'inproj':
            break
        stage_mixer(l, need_ctx_out=not last)
        if stop_after == 'mixer':
            break
        seqs = xseq if last else allseq
        stage_proj_post(l, 0, w_out[l], 8, MIX, "mix", 1, seqs)
        if stop_after == 'outproj':
            break
        stage_ffn_up(l, seqs)
        stage_ffn_conv(l, seqs)
        stage_proj_post(l, 1, f_w_down[l], NFF, ACTV, "actv", 3, seqs)
    em.barrier()
    return nc, em


def host_prep(inp):
    f = np.float32
    idx = np.arange(128)
    cstn = np.zeros((128, NCST), f)
    cstn[:, C_ID:C_ID + 128] = np.eye(128)
    cstn[:, C_UTI:C_UTI + 128] = (idx[:, None] <= idx[None, :])
    cstn[:, C_LTI:C_LTI + 128] = (idx[:, None] >= idx[None, :])
    cstn[:, C_UTS:C_UTS + 128] = (idx[:, None] < idx[None, :])
    cstn[:, C_LTS:C_LTS + 128] = (idx[:, None] > idx[None, :])
    cstn[:, C_ONE:C_ONE + 128] = 1.0
    b32 = idx // 32
    b64 = idx // 64
    same32 = b32[:, None] == b32[None, :]
    same64 = b64[:, None] == b64[None, :]
    for d in range(2):
        strict = (idx[:, None] > idx[None, :]) if d == 0 else (idx[:, None] < idx[None, :])
        cstn[:, C_MP0 + d * 128:C_MP0 + (d + 1) * 128] = strict & same32
        cstn[:, C_ME1 + d * 128:C_ME1 + (d + 1) * 128] = strict & same64 & (~same32)
        cstn[:, C_ME2 + d * 128:C_ME2 + (d + 1) * 128] = strict & (~same64)
    pvn = np.zeros((L, 128, NPV), f)
    pv6 = np.zeros((L, 64, NPV64), f)
    rwn = np.zeros((L, 1, NROW), f)
    for l in range(L):
        pvn[l, :, PV_BMOD:PV_BMOD + 48] = inp['b_mod'][l].reshape(48, 128).T
        pvn[l, :, PV_GPRE1:PV_GPRE1 + 8] = inp['g_mix_pre'][l].reshape(8, 128).T
        pvn[l, :, PV_GPOST1:PV_GPOST1 + 8] = inp['g_mix_post'][l].reshape(8, 128).T
        pvn[l, :, PV_GPRE2:PV_GPRE2 + 8] = inp['g_ffn_pre'][l].reshape(8, 128).T
        pvn[l, :, PV_GPOST2:PV_GPOST2 + 8] = inp['g_ffn_post'][l].reshape(8, 128).T
        pvn[l, :, PV_MCW:PV_MCW + 24] = inp['m_conv_w'][l].reshape(3, 8, 128).transpose(2, 0, 1).reshape(128, 24)
        pvn[l, :, PV_MCB:PV_MCB + 8] = inp['m_conv_b'][l].reshape(8, 128).T
        pvn[l, :, PV_FCW:PV_FCW + 198] = inp['f_conv_w'][l].reshape(9, NFF, 128).transpose(2, 0, 1).reshape(128, 198)
        pvn[l, :, PV_FCB:PV_FCB + NFF] = inp['f_conv_b'][l].reshape(NFF, 128).T
        mu = inp['r_mu'][l]
        pvn[l, :, PV_MUXG0] = mu[1792:1920]
        pvn[l, 0:32, PV_MUXG1] = mu[1920:1952]
        pv6[l, :, P6_MURKV:P6_MURKV + 24] = mu[0:1536].reshape(24, 64).T
        pv6[l, :, P6_MUWA:P6_MUWA + 4] = mu[1536:1792].reshape(4, 64).T
        pv6[l, :, P6_W0:P6_W0 + 16] = inp['r_w0'][l].reshape(2, 8, 64).transpose(2, 0, 1).reshape(64, 16)
        pv6[l, :, P6_A0:P6_A0 + 16] = inp['r_a0'][l].reshape(2, 8, 64).transpose(2, 0, 1).reshape(64, 16)
        pv6[l, :, P6_KK:P6_KK + 8] = inp['r_k_k'][l].reshape(8, 64).T
        pv6[l, :, P6_KA:P6_KA + 8] = inp['r_k_a'][l].reshape(8, 64).T
        pv6[l, :, P6_RK:P6_RK + 8] = inp['r_r_k'][l].T
        rwn[l, 0, RV_MNW:RV_MNW + 512] = inp['m_norm_w'][l]
        rwn[l, 0, RV_LNW:RV_LNW + 512] = inp['r_ln_w'][l]
        rwn[l, 0, RV_LNB:RV_LNB + 512] = inp['r_ln_b'][l]
        rwn[l, 0, RV_MD:RV_MD + 8] = inp['m_d'][l]
        rwn[l, 0, RV_DTB:RV_DTB + 16] = inp['m_dt_bias'][l].reshape(16)
        rwn[l, 0, RV_ALOG:RV_ALOG + 16] = inp['m_a_log'][l].reshape(16)
    return cstn, pvn, pv6, rwn


def make_in_maps(inp, cores):
    cstn, pvn, pv6, rwn = host_prep(inp)
    shared = {k: np.ascontiguousarray(np.asarray(inp[k], dtype=np.float32)) for k in
              ['w_mod', 'w_in', 'w_out', 'r_w2', 'r_a2', 'r_g2', 'f_w_up', 'f_w_down']}
    maps = []
    x = np.asarray(inp['x'], np.float32)
    ctx = np.asarray(inp['ctx'], np.float32)
    c = np.asarray(inp['c'], np.float32)
    cc = np.asarray(inp['c_ctx'], np.float32)
    for ci in cores:
        bs = [ci * NBL + i for i in range(NBL)]
        m = dict(shared)
        m['xT'] = np.ascontiguousarray(x[bs].transpose(0, 2, 1))
        m['ctxT'] = np.ascontiguousarray(ctx[bs].transpose(0, 2, 1))
        m['cT'] = np.ascontiguousarray(np.stack([c[bs[0]], c[bs[1]], cc], axis=1))
        m['cst'] = cstn
        m['pv'] = pvn
        m['pv64'] = pv6
        m['rowv'] = rwn
        m['rmu'] = np.ascontiguousarray(np.asarray(inp['r_mu'], np.float32).reshape(L, 1, 1952))
        maps.append(m)
    return maps


def kernel(**inputs):
    nc, em = build()
    cores = list(range(NCORE))
    maps = make_in_maps(inputs, cores)
    res = run_bass_kernel_spmd(nc, maps, core_ids=cores)
    out = np.empty((NCORE * NBL, TX, D), np.float32)
    for ci in cores:
        o = res.results[ci]["outT"]
        out[ci * NBL:(ci + 1) * NBL] = o.transpose(0, 2, 1)
    return out
```

```python
import numpy as np
from contextlib import ExitStack
import concourse.bass as bass
import concourse.mybir as mybir
from concourse.bass_utils import run_bass_kernel_spmd

F32 = mybir.dt.float32
BF16 = mybir.dt.bfloat16
AF = mybir.ActivationFunctionType
ALU = mybir.AluOpType
AX = mybir.AxisListType

L = 2
D = 1024
TX = 2048
TC = 256
NBL = 2
NCORE = 8
CH = 128
DFF = 2816
NFF = 22
EPS = 1e-6
R_LN_EPS = 64e-5
R_DECAY_SCALE = 0.6065306597126334
GELU_C = 1.5957691216057308

PV_BMOD = 0
PV_GPRE1 = 48
PV_GPOST1 = 56
PV_GPRE2 = 64
PV_GPOST2 = 72
PV_MCW = 80
PV_MCB = 104
PV_FCW = 112
PV_FCB = 310
PV_MUXG0 = 332
PV_MUXG1 = 333
NPV = 334
P6_MURKV = 0
P6_MUWA = 24
P6_W0 = 28
P6_A0 = 44
P6_KK = 60
P6_KA = 68
P6_RK = 76
NPV64 = 84
RV_MNW = 0
RV_LNW = 512
RV_LNB = 1024
RV_MD = 1536
RV_DTB = 1544
RV_ALOG = 1560
NROW = 1576
C_ID = 0
C_UTI = 128
C_LTI = 256
C_UTS = 384
C_LTS = 512
C_ONE = 640
C_MP0 = 768
C_ME1 = 1024
C_ME2 = 1280
NCST = 1536


class Em:
    def __init__(self, nc, ndma=8):
        self.nc = nc
        self.engs = {'pe': nc.tensor, 'act': nc.scalar, 'dve': nc.vector, 'pool': nc.gpsimd, 'sp': nc.sync}
        self.sem = {}
        self.cnt = {}
        for k in ['pe', 'act', 'dve', 'pool']:
            self.sem[k] = nc.alloc_semaphore("sem_" + k)
            self.cnt[k] = 0
        self.dq = {}
        for q in ['sp', 'pool', 'act']:
            self.dq[q] = {'n': ndma, 'next': 0}
            for i in range(ndma):
                self.sem[f"d_{q}_{i}"] = nc.alloc_semaphore(f"dsem_{q}_{i}")
                self.cnt[f"d_{q}_{i}"] = 0
        self.seen = {k: {} for k in self.engs}
        self.lastw = {}
        self.readers = {}
        self.n = 0
        self.pend = {k: False for k in self.engs}
        self.stream = None
        self.queues = {}

    def _deps(self, reads, writes):
        deps = {}

        def add(d):
            if d is None:
                return
            k, v = d
            if deps.get(k, 0) < v:
                deps[k] = v
        for b in reads:
            add(self.lastw.get(b))
        for b in writes:
            add(self.lastw.get(b))
            for r in self.readers.get(b, ()):
                add(r)
        return deps

    def _waits(self, eng, deps):
        for k, v in deps.items():
            if k == 'pe' and eng == 'pe':
                continue
            if k.startswith('d_'):
                v = self.cnt[k]
            if self.seen[eng].get(k, 0) >= v:
                continue
            self.seen[eng][k] = v
            self.engs[eng].wait_ge(self.sem[k], v)
            self.n += 1

    def _mark(self, me, reads, writes):
        for b in reads:
            self.readers.setdefault(b, []).append(me)
        for b in writes:
            self.lastw[b] = me
            self.readers[b] = []

    @staticmethod
    def _is_psum(k):
        return (k[0] == 'B' and (k[1:].isdigit() or k in ('BB', 'BBa', 'BBb'))) or k.startswith('ps')

    def flush(self):
        qs = {k: v for k, v in self.queues.items() if v}
        self.queues = {}
        pos = {k: 0 for k in qs}
        while qs:
            k = min(qs, key=lambda n: pos[n] / len(qs[n]))
            it = qs[k][pos[k]]
            pos[k] += 1
            if it[0] == 'op':
                self.op(*it[1:])
            else:
                self.dma(it[1], it[2], it[3], it[4], it[5], **it[6])
            if pos[k] >= len(qs[k]):
                del qs[k]

    @staticmethod
    def _merge(lists):
        lists = [l for l in lists if l]
        out = []
        pos = [0] * len(lists)
        live = list(range(len(lists)))
        sticky = None
        while live:
            k = sticky if sticky is not None else min(live, key=lambda n: pos[n] / len(lists[n]))
            it = lists[k][pos[k]]
            out.append(it)
            pos[k] += 1
            if it[0] == 'op' and it[1] == 'pe':
                sticky = None if it[5] else k
            if pos[k] >= len(lists[k]):
                live.remove(k)
                sticky = None
        return out

    @staticmethod
    def _ec(eng, out):
        n = max(out.free_size(), 64)
        return 0.12 + n / {'dve': 960.0, 'act': 1400.0, 'pool': 700.0}.get(eng, 960.0)

    def flush_mixer(self, nw):
        qs = self.queues
        self.queues = {}
        HOP = 0.6
        eng_free = {}
        kw_t = {}
        kr_t = {}
        out = []

        def engine_of(it):
            return it[1]

        def start_time(it):
            reads, writes = (it[3], it[4]) if it[0] == 'op' else (it[4], it[5])
            t = 0.0
            for k in reads:
                t = max(t, kw_t.get(k, 0.0))
                if self._is_psum(k):
                    t = max(t, kr_t.get(k, 0.0))
            for k in writes:
                t = max(t, kw_t.get(k, 0.0), kr_t.get(k, 0.0))
            return max(eng_free.get(engine_of(it), 0.0), t + HOP)

        def commit(it):
            st = start_time(it)
            e = engine_of(it)
            if it[0] == 'op':
                c = it[6] if it[6] is not None else 0.6
                fin = st + c
                eng_free[e] = fin
                reads, writes = it[3], it[4]
            else:
                eng_free[e] = st + 0.15
                fin = st + 2.5
                reads, writes = it[4], it[5]
            for k in reads:
                kr_t[k] = max(kr_t.get(k, 0.0), fin)
            for k in writes:
                kw_t[k] = fin
                kr_t[k] = 0.0
            out.append(it)

        m = qs.get('m', [])
        mpos = [0]
        carry = [None]

        def run_group(streams):
            pos = [0] * len(streams)
            sticky = carry[0]
            while True:
                cands = [(si, streams[si][pos[si]]) for si in range(len(streams)) if pos[si] < len(streams[si])]
                if not cands:
                    break
                if mpos[0] < len(m):
                    cands.append(('m', m[mpos[0]]))
                if sticky is not None and any(c[0] == sticky for c in cands):
                    pick = [c for c in cands if c[0] == sticky][0]
                else:
                    pick = min(cands, key=lambda c: (start_time(c[1]), 9 if c[0] == 'm' else c[0]))
                si, it = pick
                commit(it)
                if si == 'm':
                    mpos[0] += 1
                else:
                    pos[si] += 1
                if it[0] == 'op' and it[1] == 'pe':
                    sticky = None if it[5] else si
                if sticky is not None:
                    ended = (mpos[0] >= len(m)) if sticky == 'm' else (pos[sticky] >= len(streams[sticky]))
                    if ended:
                        sticky = None
            carry[0] = 'm' if sticky == 'm' else None

        if nw > 0:
            run_group([qs.get(('rp', 0), [])])
            for i in range(nw):
                run_group([qs.get(('rs', i), []), qs.get(('rp', i + 1), [])])
        while mpos[0] < len(m):
            commit(m[mpos[0]])
            mpos[0] += 1
        for it in out:
            if it[0] == 'op':
                self.op(*it[1:])
            else:
                self.dma(it[1], it[2], it[3], it[4], it[5], **it[6])

    def op(self, eng, fn, reads=(), writes=(), inc=True, cost=None):
        if self.stream is not None:
            self.queues.setdefault(self.stream, []).append(('op', eng, fn, tuple(reads), tuple(writes), inc, cost))
            return
        ex = [k for k in reads if self._is_psum(k)]
        self._waits(eng, self._deps(reads, list(writes) + ex))
        if inc:
            self.cnt[eng] += 1
            fn(self.engs[eng]).then_inc(self.sem[eng], 1)
            self._mark((eng, self.cnt[eng]), reads, writes)
            self.pend[eng] = False
        else:
            fn(self.engs[eng])
            self._mark((eng, self.cnt[eng] + 1), reads, writes)
            self.pend[eng] = True
        self.n += 1

    def dma(self, q, out, in_, reads=(), writes=(), **kw):
        if self.stream is not None:
            self.queues.setdefault(self.stream, []).append(('dma', q, out, in_, tuple(reads), tuple(writes), kw))
            return
        self._waits(q, self._deps(reads, writes))
        d = self.dq[q]
        i = d['next']
        d['next'] = (i + 1) % d['n']
        k = f"d_{q}_{i}"
        self.cnt[k] += 16
        self.engs[q].dma_start(out=out, in_=in_, **kw).then_inc(self.sem[k], 16)
        self._mark((k, self.cnt[k]), reads, writes)
        self.n += 1

    def barrier(self):
        assert not any(self.pend.values()), self.pend
        allv = {k: v for k, v in self.cnt.items() if v > 0}
        for e in self.engs:
            self._waits(e, dict(allv))

    def act(self, out, in_, func, r, w, bias=None, scale=None, accum=None):
        kw = {}
        if bias is not None:
            kw['bias'] = bias
        if scale is not None:
            kw['scale'] = scale
        if accum is not None:
            kw['accum_out'] = accum
        self.op('act', lambda e: e.activation(out=out, in_=in_, func=func, **kw), r, w, cost=self._ec('act', out))

    def tt(self, eng, out, a, b, op, r, w):
        self.op(eng, lambda e: e.tensor_tensor(out=out, in0=a, in1=b, op=op), r, w, cost=self._ec(eng, out))

    def ts(self, eng, out, a, s1, s2, op0, op1, r, w):
        if s2 is None:
            self.op(eng, lambda e: e.tensor_scalar(out=out, in0=a, scalar1=s1, scalar2=None, op0=op0), r, w,
                    cost=self._ec(eng, out))
        else:
            self.op(eng, lambda e: e.tensor_scalar(out=out, in0=a, scalar1=s1, scalar2=s2, op0=op0, op1=op1), r, w,
                    cost=self._ec(eng, out))

    def stt(self, out, a, s, b, op0, op1, r, w):
        self.op('dve', lambda e: e.scalar_tensor_tensor(out=out, in0=a, scalar=s, in1=b, op0=op0, op1=op1), r, w,
                cost=self._ec('dve', out))

    def mm(self, out, lhsT, rhs, start, stop, r, w, inc=None):
        self.op('pe', lambda e: e.matmul(out, lhsT=lhsT, rhs=rhs, start=start, stop=stop), r, w,
                inc=(stop if inc is None else inc), cost=0.06 + max(rhs.free_size(), 32) / 1200.0 * (4 if rhs.dtype == F32 else 1))

    def tr(self, out, in_, ident, r, w, inc=True):
        self.op('pe', lambda e: e.transpose(out, in_, ident), r, w, inc=inc,
                cost=0.06 + max(ident.free_size(), 32) / 1200.0 * (4 if in_.dtype == F32 else 1))

    def cp(self, eng, out, in_, r, w):
        if eng == 'act':
            self.op('act', lambda e: e.activation(out=out, in_=in_, func=AF.Identity), r, w, cost=self._ec('act', out))
        else:
            self.op(eng, lambda e: e.tensor_copy(out=out, in_=in_), r, w, cost=self._ec(eng, out))

    def memset(self, eng, ap, val, w):
        self.op(eng, lambda e: e.memset(ap, val), (), w, cost=self._ec(eng, ap))


def seqT(s):
    return TX if (s % 2) == 1 else TC


def build(debug=False, n_layers=L, stop_after=None, mixcfg=None):
    nc = bass.Bass("TRN2", target_bir_lowering=False)
    em = Em(nc)
    dbgset = debug if isinstance(debug, (set, list, tuple)) else None

    def din(name, shape, dt=F32):
        return nc.dram_tensor(name, list(shape), dt, kind="ExternalInput").ap()

    def dscr(name, shape, dt=F32):
        isdbg = (debug is True) or (dbgset is not None and name.rstrip('0123456789') in dbgset)
        return nc.dram_tensor(name, list(shape), dt, kind="ExternalOutput" if isdbg else "Internal").ap()

    xT = din("xT", [NBL, D, TX])
    ctxT = din("ctxT", [NBL, D, TC])
    cT = din("cT", [D, 3])
    w_mod = din("w_mod", [L, D, 6 * D])
    w_in = din("w_in", [L, D, 3504])
    w_out = din("w_out", [L, D, D])
    r_w2 = din("r_w2", [L, 2, 64, 512])
    r_a2 = din("r_a2", [L, 2, 64, 512])
    r_g2 = din("r_g2", [L, 160, 512])
    f_w_up = din("f_w_up", [L, D, 2 * DFF])
    f_w_down = din("f_w_down", [L, DFF, D])
    cst = din("cst", [128, NCST])
    pv = din("pv", [L, 128, NPV])
    pv64 = din("pv64", [L, 64, NPV64])
    rowv = din("rowv", [L, 1, NROW])
    rmu = din("rmu", [L, 1, 1952])
    outT = nc.dram_tensor("outT", [NBL, D, TX], F32, kind="ExternalOutput").ap()

    NS = 2 * NBL
    RESA = [dscr(f"resa{s}", [D, seqT(s)]) for s in range(NS)]
    RESB = [dscr(f"resb{s}", [D, seqT(s)]) for s in range(NS)]
    XBC = [dscr(f"xbc{s}", [D, seqT(s)]) for s in range(NS)]
    RKV = [dscr(f"rkv{s}", [24, 64, seqT(s)]) for s in range(NS)]
    XWA = [dscr(f"xwa{s}", [4, 64, seqT(s)]) for s in range(NS)]
    XG = [dscr(f"xg{s}", [160, seqT(s)]) for s in range(NS)]
    ZDT = [dscr(f"zdt{s}", [seqT(s), 528]) for s in range(NS)]
    YF = [dscr(f"yf{s}", [seqT(s), 512]) for s in range(NS)]
    OF = [dscr(f"of{s}", [seqT(s), 520]) for s in range(NS)]
    MIX = [dscr(f"mix{s}", [D, seqT(s)], BF16) for s in range(NS)]
    GATE = [dscr(f"gate{s}", [DFF, seqT(s)]) for s in range(NS)]
    VAL = [dscr(f"val{s}", [DFF, seqT(s)], BF16) for s in range(NS)]
    ACTV = [dscr(f"actv{s}", [DFF, seqT(s)], BF16) for s in range(NS)]

    def fm(ap):
        return ap.rearrange("(k p) t -> p k t", p=128)

    def sb(name, shape, dt=F32):
        return nc.alloc_sbuf_tensor(name, list(shape), dt).ap()

    CST = sb("CST", [128, NCST])
    CSTB = sb("CSTB", [128, 768], BF16)
    PV = sb("PV", [128, NPV])
    PV64 = sb("PV64", [64, NPV64])
    ROWB = sb("ROWB", [128, NROW])
    MOD = sb("MOD", [128, 48, 3])
    DER = sb("DER", [128, 4, 8, 3])
    NEGA = sb("NEGA", [128, 16])
    OMMU = sb("OMMU", [64, 8])
    em.dma('sp', CST, cst, writes=['CST'])
    em.cp('dve', CSTB, CST[:, 0:768], ['CST'], ['CSTB'])
    ident = CST[:, C_ID:C_ID + 128]
    identb = CSTB[:, C_ID:C_ID + 128]
    onesb = CSTB[:, C_ONE:C_ONE + 128]
    ones = CST[:, C_ONE:C_ONE + 128]

    def stage_scope():
        return ExitStack()

    def stage_mod(l):
        em.dma('sp', PV, pv[l], writes=['PV'])
        em.dma('sp', PV64, pv64[l], writes=['PV64'])
        em.dma('pool', ROWB, rowv[l].partition_broadcast(128), writes=['ROWB'])
        with ExitStack() as es:
            def S(name, shape, dt=F32):
                return es.enter_context(nc.sbuf_tensor(name, list(shape), dt)).ap()
            cts = S(f"cts{l}", [128, 8, 3])
            sc = S(f"sc{l}", [128, 8, 3])
            wst = [S(f"wmst{l}_{i}", [128, 8, 512]) for i in range(2)]
            ps = es.enter_context(nc.psum_tensor(f"psmod{l}", [128, 512], F32)).ap()
            em.dma('sp', cts, cT.rearrange("(k p) j -> p k j", p=128), writes=['cts'])
            em.act(sc, cts, AF.Silu, ['cts'], ['sc'])
            for g in range(12):
                w = wst[g % 2]
                wk = f"wmst{g % 2}"
                em.dma('sp' if g % 2 == 0 else 'pool', w,
                       w_mod[l][:, g * 512:(g + 1) * 512].rearrange("(k p) n -> p k n", p=128), writes=[wk])
                for mi in range(4):
                    m = g * 4 + mi
                    for k in range(8):
                        em.mm(ps[:, m * 3:(m + 1) * 3], w[:, k, mi * 128:(mi + 1) * 128], sc[:, k, :],
                              k == 0, k == 7, [wk, 'sc'], ['psmod'])
            em.tt('dve', MOD, ps[:, 0:144].rearrange("p (m j) -> p m j", j=3),
                  PV[:, PV_BMOD:PV_BMOD + 48].unsqueeze(2).to_broadcast([128, 48, 3]), ALU.add,
                  ['psmod', 'PV'], ['MOD'])
            tmp = S(f"dertmp{l}", [128, 8, 3])

            def gain(idx, goff, mlo, plus1):
                if plus1:
                    em.ts('dve', tmp, MOD[:, mlo:mlo + 8, :], 1.0, None, ALU.add, None, ['MOD'], ['dertmp'])
                    src = tmp
                    rk = ['dertmp', 'PV']
                else:
                    src = MOD[:, mlo:mlo + 8, :]
                    rk = ['MOD', 'PV']
                em.tt('dve', DER[:, idx, :, :], src,
                      PV[:, goff:goff + 8].unsqueeze(2).to_broadcast([128, 8, 3]), ALU.mult, rk, ['DER'])
            gain(0, PV_GPRE1, 8, True)
            gain(1, PV_GPOST1, 16, False)
            gain(2, PV_GPRE2, 32, True)
            gain(3, PV_GPOST2, 40, False)
            em.act(NEGA, ROWB[:, RV_ALOG:RV_ALOG + 16], AF.Exp, ['ROWB'], ['NEGA'])
            em.ts('dve', NEGA, NEGA, -1.0, None, ALU.mult, None, ['NEGA'], ['NEGA'])
            em.ts('dve', OMMU, PV64[:, P6_KA:P6_KA + 8], -1.0, 1.0, ALU.mult, ALU.add, ['PV64'], ['OMMU'])
            em.barrier()

    def prenorm(xt, TW, h, sq, rstd, ps, jmod, gidx, sidx, kx, kh, kps, sqk='sq', rsk='rstd'):
        em.act(sq[:, :, :TW], xt[:, :, :TW], AF.Square, [kx], [sqk])
        for k in range(8):
            em.mm(ps[:, :TW], onesb, sq[:, k, :TW], k == 0, k == 7, [sqk, 'CSTB'], [kps])
        em.act(rstd[:, :TW], ps[:, :TW], AF.Sqrt, [kps], [rsk], bias=EPS, scale=1.0 / D)
        em.op('dve', lambda e: e.reciprocal(out=rstd[:, :TW], in_=rstd[:, :TW]), [rsk], [rsk])
        for k in range(8):
            em.stt(xt[:, k, :TW], xt[:, k, :TW], DER[:, gidx, k, jmod:jmod + 1], rstd[:, :TW], ALU.mult, ALU.mult,
                   [kx, rsk, 'DER'], [kx])
            em.act(h[:, k, :TW], xt[:, k, :TW], AF.Identity, [kx, 'MOD'], [kh],
                   bias=MOD[:, sidx + k, jmod:jmod + 1], scale=1.0)

    def load_wbf(es, l, wap, Kc, N, name, piece=None):
        wb = es.enter_context(nc.sbuf_tensor(f"{name}bf{l}", [128, Kc, N], BF16)).ap()
        with ExitStack() as e2:
            sts = [e2.enter_context(nc.sbuf_tensor(f"{name}st{l}_{i}", [128, N], F32)).ap() for i in range(2)]
            for k in range(Kc):
                st = sts[k % 2]
                sk = f"{name}st{k % 2}"
                em.dma('sp' if k % 2 == 0 else 'pool', st, wap[k * 128:(k + 1) * 128, :], writes=[sk])
                em.cp('act' if k % 2 == 0 else 'dve', wb[:, k, :], st, [sk], [name + 'bf'])
            em.barrier()
        return wb

    def res_src(l, s, phase):
        b = s // 2
        if phase == 0:
            if l == 0:
                return (xT[b] if s % 2 == 1 else ctxT[b]), f"in{s}"
            return RESB[s], f"resb{s}"
        return RESA[s], f"resa{s}"

    def res_dst(l, s, phase):
        b = s // 2
        if phase == 0:
            return RESA[s], f"resa{s}"
        if l == n_layers - 1 and s % 2 == 1:
            return outT[b], f"out{s}"
        return RESB[s], f"resb{s}"

    def stage_inproj(l):
        with ExitStack() as es:
            def S(name, shape, dt=F32):
                return es.enter_context(nc.sbuf_tensor(name, list(shape), dt)).ap()
            wb = load_wbf(es, l, w_in[l], 8, 3504, "win")
            w2 = S(f"ipw2{l}", [128, 8, 1952], BF16)
            with ExitStack() as e2:
                mub = e2.enter_context(nc.sbuf_tensor(f"ipmub{l}", [128, 1952], F32)).ap()
                em.dma('sp', mub, rmu[l].partition_broadcast(128), writes=['ipmub'])
                for k in range(8):
                    em.stt(w2[:, k, :], mub, 0.5, wb[:, k, 1552:3504], ALU.mult, ALU.mult, ['ipmub', 'winbf'], ['ipw2'])
                em.ts('dve', mub, mub, -1.0, 1.0, ALU.mult, ALU.add, ['ipmub'], ['ipmub'])
                for k in range(8):
                    em.tt('dve', wb[:, k, 1552:3504], wb[:, k, 1552:3504], mub, ALU.mult, ['ipmub', 'winbf', 'ipw2'], ['winbf'])
                em.barrier()
            xt = S(f"ipx{l}", [128, 8, 512])
            h = S(f"iph{l}", [128, 8, 512], BF16)
            sq = S(f"ipsq{l}", [128, 8, 512], BF16)
            hs = sq
            rstd = S(f"iprs{l}", [128, 512])
            xh = S(f"ipxh{l}", [128, 8, 2])
            hh = S(f"iphh{l}", [128, 8, 2], BF16)
            sqh = S(f"ipsqh{l}", [128, 8, 2], BF16)
            rstdh = S(f"iprsh{l}", [128, 2])
            sta = [S(f"ipsta{l}_{i}", [128, 8, 512]) for i in range(2)]
            stw = S(f"ipstw{l}", [64, 4, 512])
            stg0 = S(f"ipstg0{l}", [128, 512])
            stg1 = S(f"ipstg1{l}", [32, 512])
            stz = S(f"ipstz{l}", [128, 4, 528])
            pss = [es.enter_context(nc.psum_tensor(f"ipps{l}_{i}", [128, 512], F32)).ap() for i in range(8)]
            groups = [('xbc', 512, 128, 8), ('r', 1552, 64, 8), ('k', 2064, 64, 8), ('v', 2576, 64, 8)]
            ev = 0
            for s in range(NS):
                T = seqT(s)
                TW = min(512, T)
                jmod = 2 if s % 2 == 0 else s // 2
                src, srck = res_src(l, s, 0)
                for tt_ in range(T // TW):
                    t0 = tt_ * TW
                    em.dma('sp', xt[:, :, :TW], fm(src)[:, :, t0:t0 + TW], reads=[srck], writes=['ipx'])
                    cl = max(t0 - 1, 0)
                    cr = min(t0 + TW, T - 1)
                    em.dma('sp', xh[:, :, 0:1], fm(src)[:, :, cl:cl + 1], reads=[srck], writes=['ipxh'], allow_slow_non_contiguous=True)
                    em.dma('sp', xh[:, :, 1:2], fm(src)[:, :, cr:cr + 1], reads=[srck], writes=['ipxh'], allow_slow_non_contiguous=True)
                    prenorm(xt, TW, h, sq, rstd, pss[7], jmod, 0, 0, 'ipx', 'iph', 'ps7')
                    prenorm(xh, 2, hh, sqh, rstdh, pss[6], jmod, 0, 0, 'ipxh', 'iphh', 'ps6', sqk='ipsqh', rsk='iprsh')
                    if t0 == 0:
                        em.memset('pool', hh[:, :, 0:1], 0.0, ['iphh'])
                    if t0 + TW == T:
                        em.memset('pool', hh[:, :, 1:2], 0.0, ['iphh'])
                    em.tt('dve', hs[:, :, 1:TW - 1], h[:, :, 0:TW - 2], h[:, :, 2:TW], ALU.add, ['iph', 'sq'], ['sq'])
                    em.tt('dve', hs[:, :, 0:1], hh[:, :, 0:1], h[:, :, 1:2], ALU.add, ['iph', 'iphh', 'sq'], ['sq'])
                    em.tt('dve', hs[:, :, TW - 1:TW], h[:, :, TW - 2:TW - 1], hh[:, :, 1:2], ALU.add, ['iph', 'iphh', 'sq'], ['sq'])
                    pi = 0
                    for gi, (gname, c0, wdt, nb) in enumerate(groups):
                        st = sta[gi % 2]
                        stk = f"ipsta{gi % 2}"
                        for j in range(nb):
                            ps = pss[pi % 6]
                            pk = f"ps{pi % 6}"
                            pi += 1
                            cc = c0 + j * wdt
                            shifted = (gname != 'xbc')
                            for k in range(8):
                                em.mm(ps[:wdt, :TW], wb[:, k, cc:cc + wdt], h[:, k, :TW], k == 0, (k == 7) and not shifted,
                                      ['winbf', 'iph'], [pk])
                            if shifted:
                                for k in range(8):
                                    em.mm(ps[:wdt, :TW], w2[:, k, cc - 1552:cc - 1552 + wdt], hs[:, k, :TW], False, k == 7,
                                          ['ipw2', 'sq'], [pk])
                            em.cp('act' if ev % 2 == 0 else 'dve', st[:wdt, j, :TW], ps[:wdt, :TW], [pk], [stk])
                            ev += 1
                        if gname == 'xbc':
                            em.dma('pool', fm(XBC[s])[:, :, t0:t0 + TW], st[:, :, :TW], reads=[stk], writes=[f"xbc{s}"])
                        else:
                            jb = {'r': 0, 'k': 8, 'v': 16}[gname]
                            em.dma('pool', RKV[s][jb:jb + 8].rearrange("j p t -> p j t")[:, :, t0:t0 + TW],
                                   st[:64, :, :TW], reads=[stk], writes=[f"rkv{s}"])
                    for j in range(4):
                        ps = pss[pi % 6]
                        pk = f"ps{pi % 6}"
                        pi += 1
                        cc = 3088 + j * 64
                        for k in range(8):
                            em.mm(ps[:64, :TW], wb[:, k, cc:cc + 64], h[:, k, :TW], k == 0, False, ['winbf', 'iph'], [pk])
                        for k in range(8):
                            em.mm(ps[:64, :TW], w2[:, k, cc - 1552:cc - 1552 + 64], hs[:, k, :TW], False, k == 7, ['ipw2', 'sq'], [pk])
                        em.cp('act' if ev % 2 == 0 else 'dve', stw[:, j, :TW], ps[:64, :TW], [pk], ['ipstw'])
                        ev += 1
                    em.dma('pool', XWA[s].rearrange("j p t -> p j t")[:, :, t0:t0 + TW], stw[:, :, :TW],
                           reads=['ipstw'], writes=[f"xwa{s}"])
                    for (cc, wdt, st, stk, r0) in [(3344, 128, stg0, 'ipstg0', 0), (3472, 32, stg1, 'ipstg1', 128)]:
                        ps = pss[pi % 6]
                        pk = f"ps{pi % 6}"
                        pi += 1
                        for k in range(8):
                            em.mm(ps[:wdt, :TW], wb[:, k, cc:cc + wdt], h[:, k, :TW], k == 0, False, ['winbf', 'iph'], [pk])
                        for k in range(8):
                            em.mm(ps[:wdt, :TW], w2[:, k, cc - 1552:cc - 1552 + wdt], hs[:, k, :TW], False, k == 7, ['ipw2', 'sq'], [pk])
                        em.cp('act' if ev % 2 == 0 else 'dve', st[:wdt, :TW], ps[:wdt, :TW], [pk], [stk])
                        ev += 1
                        em.dma('pool', XG[s][r0:r0 + wdt, t0:t0 + TW], st[:wdt, :TW], reads=[stk], writes=[f"xg{s}"])
                    for i in range(TW // 128):
                        ps = pss[pi % 6]
                        pk = f"ps{pi % 6}"
                        pi += 1
                        ps2 = pss[6]
                        for k in range(8):
                            em.mm(ps[:, 0:512], h[:, k, i * 128:(i + 1) * 128], wb[:, k, 0:512], k == 0, k == 7,
                                  ['winbf', 'iph'], [pk])
                        for k in range(8):
                            em.mm(ps2[:, 0:16], h[:, k, i * 128:(i + 1) * 128], wb[:, k, 1536:1552], k == 0, k == 7,
                                  ['winbf', 'iph'], ['ps6'])
                        em.cp('act', stz[:, i, 0:512], ps[:, 0:512], [pk], ['ipstz'])
                        em.cp('dve', stz[:, i, 512:528], ps2[:, 0:16], ['ps6'], ['ipstz'])
                    em.dma('pool', ZDT[s][t0:t0 + TW, :].rearrange("(i p) c -> p i c", p=128), stz[:, :TW // 128, :],
                           reads=['ipstz'], writes=[f"zdt{s}"])
            em.barrier()

    def stage_mixer(l, need_ctx_out):
        with ExitStack() as es:
            def S(name, shape, dt=F32):
                return es.enter_context(nc.sbuf_tensor(name, list(shape), dt)).ap()
            w2b = S(f"w2b{l}", [64, 2, 512], BF16)
            a2b = S(f"a2b{l}", [64, 2, 512], BF16)
            g2b0 = S(f"g2b0{l}", [128, 512], BF16)
            g2b1 = S(f"g2b1{l}", [32, 512], BF16)
            with ExitStack() as e2:
                t1 = e2.enter_context(nc.sbuf_tensor(f"lst1{l}", [64, 2, 512], F32)).ap()
                t2 = e2.enter_context(nc.sbuf_tensor(f"lst2{l}", [64, 2, 512], F32)).ap()
                t3 = e2.enter_context(nc.sbuf_tensor(f"lst3{l}", [128, 512], F32)).ap()
                t4 = e2.enter_context(nc.sbuf_tensor(f"lst4{l}", [32, 512], F32)).ap()
                em.dma('sp', t1, r_w2[l].rearrange("d r c -> r d c"), writes=['lst1'])
                em.dma('sp', t2, r_a2[l].rearrange("d r c -> r d c"), writes=['lst2'])
                em.dma('sp', t3, r_g2[l][0:128, :], writes=['lst3'])
                em.dma('sp', t4, r_g2[l][128:160, :], writes=['lst4'])
                em.cp('dve', w2b, t1, ['lst1'], ['w2b'])
                em.cp('dve', a2b, t2, ['lst2'], ['a2b'])
                em.cp('dve', g2b0, t3, ['lst3'], ['g2b'])
                em.cp('dve', g2b1, t4, ['lst4'], ['g2b'])
                em.barrier()
            MAR = [None, None]
            MARt = S(f"mar{l}", [128, 2, 256])
            em.cp('dve', MARt[:, 0, 0:128], CST[:, C_UTS:C_UTS + 128], ['CST'], ['MAR'])
            em.cp('dve', MARt[:, 0, 128:256], CST[:, C_UTI:C_UTI + 128], ['CST'], ['MAR'])
            em.cp('dve', MARt[:, 1, 0:128], CST[:, C_LTS:C_LTS + 128], ['CST'], ['MAR'])
            em.cp('dve', MARt[:, 1, 128:256], CST[:, C_LTI:C_LTI + 128], ['CST'], ['MAR'])
            MSO = S(f"mso{l}", [128, 2, 256])
            em.cp('dve', MSO[:, 0, 0:128], CST[:, C_LTS:C_LTS + 128], ['CST'], ['MSO'])
            em.cp('dve', MSO[:, 1, 0:128], CST[:, C_UTS:C_UTS + 128], ['CST'], ['MSO'])
            em.cp('dve', MSO[:, 0, 128:256], ones, ['CST'], ['MSO'])
            em.cp('dve', MSO[:, 1, 128:256], ones, ['CST'], ['MSO'])
            RMK = S(f"rmk{l}", [64, 8, 128], BF16)
            em.memset('pool', RMK, 1.0, ['RMK'])
            em.memset('pool', RMK[:, :, 0:1], 0.0, ['RMK'])

            def MI(d):
                return CST[:, C_UTI:C_UTI + 128] if d == 0 else CST[:, C_LTI:C_LTI + 128]

            def strictTS(d):
                return CST[:, C_LTS:C_LTS + 128] if d == 0 else CST[:, C_UTS:C_UTS + 128]

            xbc = S(f"m_xbc{l}", [128, 8, 130])
            cacc = S(f"m_cacc{l}", [128, 8, 128])
            bct = S(f"m_bct{l}", [128, 4, 128], BF16)
            xtok = S(f"m_xtok{l}", [128, 512])
            xtokb = S(f"m_xtokb{l}", [128, 512], BF16)
            btokb = S(f"m_btokb{l}", [128, 256], BF16)
            zdt = S(f"m_zdt{l}", [128, 528])
            dts = S(f"m_dts{l}", [128, 8])
            dta = S(f"m_dta{l}", [128, 8])
            sm = S(f"m_sm{l}", [128, 40])
            xw = S(f"m_xw{l}", [128, 512], BF16)
            cbt = S(f"m_cbt{l}", [128, 256])
            l2 = [S(f"m_l2{l}_{i}", [128, 256]) for i in range(2)]
            Et = [S(f"m_E{l}_{i}", [128, 256]) for i in range(2)]
            LTt = [S(f"m_LT{l}_{i}", [128, 128]) for i in range(2)]
            STt = [S(f"m_ST{l}_{i}", [128, 128], BF16) for i in range(2)]
            CsT = [S(f"m_Cs{l}_{i}", [128, 128], BF16) for i in range(2)]
            hst = S(f"m_hst{l}", [128, 512])
            hstb = S(f"m_hstb{l}", [128, 512], BF16)
            yt = S(f"m_y{l}", [128, 512])
            yf = S(f"m_yf{l}", [128, 512])
            zs = S(f"m_zs{l}", [128, 512])
            ysq = S(f"m_ysq{l}", [128, 512])
            gst = S(f"m_gst{l}", [128, 4])
            mixo = S(f"mixo{l}", [128, 8, 128], BF16)
            ssum = S(f"r_ssum{l}", [64, 26, 128])
            pp = ssum
            xg0 = S(f"r_xg0{l}", [128, 130])
            xg1 = S(f"r_xg1{l}", [32, 130])
            sg0 = S(f"r_sg0{l}", [128, 128], BF16)
            sg1 = S(f"r_sg1{l}", [32, 128], BF16)
            xgt = S(f"r_xgt{l}", [128, 128])
            twb = S(f"r_twb{l}", [64, 2, 128], BF16)
            lw = S(f"r_lw{l}", [64, 8, 128])
            aa = S(f"r_aa{l}", [64, 8, 128])
            kk = S(f"r_kk{l}", [64, 8, 128])
            kd = S(f"r_kd{l}", [64, 8, 128])
            linc = S(f"r_linc{l}", [64, 8, 128])
            lex = S(f"r_lex{l}", [64, 8, 128])
            rinv = lex
            e1 = S(f"r_e1{l}", [64, 8, 128])
            e0 = S(f"r_e0{l}", [64, 8, 128])
            ei = S(f"r_ei{l}", [64, 8, 128])
            gCs = [S(f"r_gC{l}_{i}", [64, 8]) for i in range(2)]
            coefs = [S(f"r_coef{l}_{i}", [128, 8]) for i in range(2)]
            tmpk = S(f"r_tmpk{l}", [64, 8, 128])
            ARs = [S(f"r_AR{l}_{i}", [64, 8, 256], BF16) for i in range(2)]
            BKs = [S(f"r_BK{l}_{i}", [64, 8, 2, 128], BF16) for i in range(2)]
            prodb = S(f"r_prod{l}", [64, 8, 128], BF16)
            sqb = prodb
            vtokbs = [S(f"r_vtokb{l}_{i}", [128, 512], BF16) for i in range(2)]
            BKtoks = [S(f"r_BKtok{l}_{i}", [128, 8, 2, 64], BF16) for i in range(2)]
            GBs = [S(f"r_GB{l}_{i}", [128, 8, 256], BF16) for i in range(2)]
            GKs = [S(f"r_GK{l}_{i}", [128, 8, 256], BF16) for i in range(2)]
            Q0s = [S(f"r_Q0{l}_{i}", [128, 8, 128], BF16) for i in range(2)]
            Pm = [S(f"r_P{l}_{i}", [128, 8, 128], BF16) for i in range(2)]
            Qm = [S(f"r_Q{l}_{i}", [128, 8, 128], BF16) for i in range(2)]
            Ym = [S(f"r_Y{l}_{i}", [128, 8, 128], BF16) for i in range(2)]
            E1m = S(f"r_E1{l}", [128, 8, 128], BF16)
            E2m = S(f"r_E2{l}", [128, 8, 128], BF16)
            Dm = S(f"r_D{l}", [128, 8, 128], BF16)
            Zm = S(f"r_Z{l}", [128, 8, 128], BF16)
            Wt = S(f"r_W{l}", [128, 512], BF16)
            Ut = S(f"r_U{l}", [128, 512], BF16)
            Hs = S(f"r_H{l}", [64, 512])
            Hb = S(f"r_Hb{l}", [64, 512], BF16)
            ot = S(f"r_o{l}", [128, 520])
            oft = S(f"r_of{l}", [128, 520])
            osq = S(f"r_osq{l}", [128, 512])
            gn = S(f"r_gn{l}", [128, 40])
            B = [es.enter_context(nc.psum_tensor(f"mxps{l}_{i}", [128, 512], F32)).ap() for i in range(7)]
            BBp = es.enter_context(nc.psum_tensor(f"mxpsb{l}", [128, 1024], BF16)).ap()

            nbw = ROWB[:, RV_MNW:RV_MNW + 512]
            lnw = ROWB[:, RV_LNW:RV_LNW + 512]
            lnb = ROWB[:, RV_LNB:RV_LNB + 512]
            mdb = ROWB[:, RV_MD:RV_MD + 8]
            evc = [0]

            def evq():
                evc[0] += 1
                return 'act' if evc[0] % 2 == 0 else 'dve'

            def load_halo(tile, tk, srcap, srck, t0, T, nblk_dims):
                lo = max(t0 - 1, 0)
                hi = min(t0 + 129, T)
                o = lo - (t0 - 1)
                if nblk_dims:
                    em.dma('sp', tile[:, :, o:o + hi - lo], srcap[:, :, lo:hi], reads=[srck], writes=[tk])
                    if t0 == 0:
                        em.memset('pool', tile[:, :, 0:1], 0.0, [tk])
                    if t0 + 128 == T:
                        em.memset('pool', tile[:, :, 129:130], 0.0, [tk])
                else:
                    em.dma('sp', tile[:, o:o + hi - lo], srcap[:, lo:hi], reads=[srck], writes=[tk])
                    if t0 == 0:
                        em.memset('pool', tile[:, 0:1], 0.0, [tk])
                    if t0 + 128 == T:
                        em.memset('pool', tile[:, 129:130], 0.0, [tk])

            def mamba_chunk(s, t0, d, emit_out):
                T = seqT(s)
                load_halo(xbc, 'm_xbc', fm(XBC[s]), f"xbc{s}", t0, T, True)
                em.dma('pool', zdt, ZDT[s][t0:t0 + 128, :], reads=[f"zdt{s}"], writes=['m_zdt'])
                if (mixcfg or {}).get('mstop', 99) <= 1:
                    return
                for j in range(8):
                    em.ts('dve', cacc[:, j, :], xbc[:, j, 1:129], PV[:, PV_MCW + 8 + j:PV_MCW + 9 + j],
                          PV[:, PV_MCB + j:PV_MCB + j + 1], ALU.mult, ALU.add, ['m_xbc', 'PV'], ['m_cacc'])
                    em.stt(cacc[:, j, :], xbc[:, j, 0:128], PV[:, PV_MCW + j:PV_MCW + j + 1], cacc[:, j, :],
                           ALU.mult, ALU.add, ['m_xbc', 'PV', 'm_cacc'], ['m_cacc'])
                    em.stt(cacc[:, j, :], xbc[:, j, 2:130], PV[:, PV_MCW + 16 + j:PV_MCW + 17 + j], cacc[:, j, :],
                           ALU.mult, ALU.add, ['m_xbc', 'PV', 'm_cacc'], ['m_cacc'])
                em.act(cacc[:, 0:6, :], cacc[:, 0:6, :], AF.Silu, ['m_cacc'], ['m_cacc'])
                em.act(bct[:, 2:4, :], cacc[:, 6:8, :], AF.Silu, ['m_cacc'], ['m_bct'])
                em.cp('dve', bct[:, 0:2, :], cacc[:, 4:6, :], ['m_cacc'], ['m_bct'])
                if (mixcfg or {}).get('mstop', 99) <= 2:
                    return
                msub = (mixcfg or {}).get('msub', 9)
                for j in range(4):
                    em.tr(B[4][:, j * 128:(j + 1) * 128], cacc[:, j, :], ident, ['m_cacc', 'CST'], ['B4'], inc=(j == 3))
                if msub >= 1:
                    for j in range(2):
                        em.tr(B[5][:, j * 128:(j + 1) * 128], cacc[:, 4 + j, :], ident, ['m_cacc', 'CST'], ['B5'])
                if msub >= 2 and msub != 33:
                    em.cp('dve', xtok, B[4], ['B4'], ['m_xtok'])
                if msub == 30:
                    em.cp('dve', xtokb, B[4], ['B4'], ['m_xtokb'])
                elif msub == 31:
                    em.cp('act', yt, B[4], ['B4'], ['m_y'])
                elif msub == 32:
                    em.cp('act', xtokb, xtok, ['m_xtok'], ['m_xtokb'])
                elif msub >= 3:
                    em.cp('act', xtokb, B[4], ['B4'], ['m_xtokb'])
                if msub >= 4:
                    em.cp('act', btokb, B[5][:, 0:256], ['B5'], ['m_btokb'])
                if (mixcfg or {}).get('mstop', 99) <= 3:
                    return
                em.tt('dve', dts, zdt[:, 512 + d * 8:520 + d * 8], ROWB[:, RV_DTB + d * 8:RV_DTB + d * 8 + 8], ALU.add,
                      ['m_zdt', 'ROWB'], ['m_dts'])
                em.act(dts, dts, AF.Exp, ['m_dts'], ['m_dts'])
                em.act(dts, dts, AF.Ln, ['m_dts'], ['m_dts'], bias=1.0, scale=1.0)
                em.tt('dve', dta, dts, NEGA[:, d * 8:d * 8 + 8], ALU.mult, ['m_dts', 'NEGA'], ['m_dta'])
                if (mixcfg or {}).get('mstop', 99) <= 4:
                    return
                em.mm(B[5][:, 256:264], MI(d), dta, True, True, ['CST', 'm_dta'], ['B5'])
                em.mm(B[5][:, 264:272], ones, dta, True, True, ['CST', 'm_dta'], ['B5'])
                em.cp('act', sm[:, 32:40], B[5][:, 264:272], ['B5'], ['m_sm'])
                em.tt('dve', sm[:, 24:32], sm[:, 32:40], B[5][:, 256:264], ALU.subtract, ['B5', 'm_sm'], ['m_sm'])
                em.act(sm[:, 0:8], sm[:, 24:32], AF.Exp, ['m_sm'], ['m_sm'])
                em.tt('dve', sm[:, 8:16], sm[:, 0:8], dts, ALU.mult, ['m_sm', 'm_dts'], ['m_sm'])
                em.act(sm[:, 16:24], B[5][:, 264:272], AF.Exp, ['B5'], ['m_sm'])
                em.tt('dve', xw.rearrange("p (h q) -> p h q", q=64), xtok.rearrange("p (h q) -> p h q", q=64),
                      sm[:, 8:16].unsqueeze(2).to_broadcast([128, 8, 64]), ALU.mult, ['m_xtok', 'm_sm'], ['m_xw'])
                if (mixcfg or {}).get('mstop', 99) <= 5:
                    return
                for g in range(2):
                    em.mm(B[5][:, g * 128:(g + 1) * 128], bct[:, g, :], bct[:, 2 + g, :], True, True, ['m_bct'], ['B5'], inc=(g == 1))
                em.cp('act', cbt, B[5][:, 0:256], ['B5'], ['m_cbt'])
                for h in range(8):
                    g = h // 4
                    i2 = h % 2
                    pe_ = B[5][:, 256:512] if i2 == 0 else B[5][:, 0:256]
                    pek = 'B5'
                    em.ts('dve', l2[i2], MSO[:, d, :], dta[:, h:h + 1], None, ALU.mult, None,
                          ['MSO', 'm_dta'], [f"m_l2{i2}"])
                    em.mm(pe_[:, 0:128], l2[i2][:, 0:128], MI(d), True, True, [f"m_l2{i2}", 'CST'], [pek])
                    em.mm(pe_[:, 128:256], l2[i2][:, 128:256], MI(d), True, True, [f"m_l2{i2}", 'CST'], [pek])
                    em.act(Et[i2], pe_[:, 0:256], AF.Exp, [pek], [f"m_E{i2}"])
                    em.stt(LTt[i2], Et[i2][:, 0:128], dts[:, h:h + 1], MI(d), ALU.mult, ALU.mult,
                           [f"m_E{i2}", 'm_dts', 'CST'], [f"m_LT{i2}"])
                    em.tt('dve', STt[i2], cbt[:, g * 128:(g + 1) * 128], LTt[i2], ALU.mult, ['m_cbt', f"m_LT{i2}"],
                          [f"m_ST{i2}"])
                    em.tt('dve', CsT[i2], bct[:, 2 + g, :], Et[i2][:, 128:256], ALU.mult, ['m_bct', f"m_E{i2}"],
                          [f"m_Cs{i2}"])
                    em.mm(B[4][:, h * 64:(h + 1) * 64], STt[i2], xtokb[:, h * 64:(h + 1) * 64], True, False,
                          [f"m_ST{i2}", 'm_xtokb'], ['B4'])
                    em.mm(B[4][:, h * 64:(h + 1) * 64], CsT[i2], hstb[:, h * 64:(h + 1) * 64], False, True,
                          [f"m_Cs{i2}", 'm_hstb'], ['B4'])
                if (mixcfg or {}).get('mstop', 99) <= 6:
                    return
                for g in range(2):
                    em.mm(B[5][:, g * 256:(g + 1) * 256], btokb[:, g * 128:(g + 1) * 128], xw[:, g * 256:(g + 1) * 256],
                          True, True, ['m_btokb', 'm_xw'], ['B5'], inc=(g == 1))
                em.tt('dve', hst.rearrange("p (h q) -> p h q", q=64), hst.rearrange("p (h q) -> p h q", q=64),
                      sm[:, 16:24].unsqueeze(2).to_broadcast([128, 8, 64]), ALU.mult, ['m_hst', 'm_sm', 'B4'], ['m_hst'])
                em.tt('dve', hst, hst, B[5], ALU.add, ['m_hst', 'B5'], ['m_hst'])
                em.cp('act', hstb, hst, ['m_hst', 'B4'], ['m_hstb'])
                if (mixcfg or {}).get('mstop', 99) <= 7:
                    return
                if d == 0:
                    if emit_out:
                        em.cp('act', yt, B[4], ['B4'], ['m_y'])
                        em.dma('pool', YF[s][t0:t0 + 128, :], yt, reads=['m_y'], writes=[f"yf{s}"])
                elif emit_out:
                    em.dma('sp', yf, YF[s][t0:t0 + 128, :], reads=[f"yf{s}"], writes=['m_yf'])
                    em.tt('dve', yt, B[4], yf, ALU.add, ['B4', 'm_yf'], ['m_y'])
                    em.tt('dve', yf.rearrange("p (h q) -> p h q", q=64), xtok.rearrange("p (h q) -> p h q", q=64),
                          mdb.unsqueeze(2).to_broadcast([128, 8, 64]), ALU.mult, ['m_xtok', 'ROWB', 'm_yf'], ['m_yf'])
                    em.tt('dve', yt, yt, yf, ALU.add, ['m_y', 'm_yf'], ['m_y'])
                    em.act(zs, zdt[:, 0:512], AF.Silu, ['m_zdt'], ['m_zs'])
                    em.tt('dve', yt, yt, zs, ALU.mult, ['m_y', 'm_zs'], ['m_y'])
                    for g in range(2):
                        em.act(ysq[:, g * 256:(g + 1) * 256], yt[:, g * 256:(g + 1) * 256], AF.Square, ['m_y'],
                               ['m_ysq', 'm_gst'], accum=gst[:, g:g + 1])
                    em.act(gst[:, 2:4], gst[:, 0:2], AF.Sqrt, ['m_gst'], ['m_gst'], bias=EPS, scale=1.0 / 256)
                    em.op('dve', lambda e: e.reciprocal(out=gst[:, 2:4], in_=gst[:, 2:4]), ['m_gst'], ['m_gst'])
                    for g in range(2):
                        em.stt(yt[:, g * 256:(g + 1) * 256], yt[:, g * 256:(g + 1) * 256], gst[:, 2 + g:3 + g],
                               nbw[:, g * 256:(g + 1) * 256], ALU.mult, ALU.mult, ['m_y', 'm_gst', 'ROWB'], ['m_y'])
                    for j in range(4):
                        em.tr(B[5][:, j * 128:(j + 1) * 128], yt[:, j * 128:(j + 1) * 128], ident, ['m_y', 'CST'], ['B5'], inc=(j == 3))
                    em.cp('act', mixo[:, 0:4, :], B[5].rearrange("p (j t) -> p j t", t=128), ['B5'], ['mixo_m'])
                    em.dma('pool', fm(MIX[s])[:, 0:4, t0:t0 + 128], mixo[:, 0:4, :], reads=['mixo_m'], writes=[f"mixm{s}"])

            def wkv_chunk(s, t0, d, emit_out, idx=0, first_of_d=False, split=True):
                T = seqT(s)
                pb = idx % 2
                AR, BK, GB, GK, Q0, BKtok, vtokb, gC, coef = (ARs[pb], BKs[pb], GBs[pb], GKs[pb], Q0s[pb], BKtoks[pb],
                                                             vtokbs[pb], gCs[pb], coefs[pb])
                kAR, kBK, kGB, kGK, kQ0, kBKtok, kvtokb, kgC, kcoef = [f"{n}{pb}" for n in
                                                                      ('r_AR', 'r_BK', 'r_GB', 'r_GK', 'r_Q0h', 'r_BKtok',
                                                                       'r_vtokb', 'r_gC', 'r_coef')]
                if split:
                    em.stream = ('rp', idx)
                fin = (d == 1 and emit_out)
                em.dma('sp', pp[:, 0:24, :], RKV[s].rearrange("j p t -> p j t")[:, :, t0:t0 + 128], reads=[f"rkv{s}"], writes=['r_ssum'])
                em.dma('sp', pp[:, 24:25, :], XWA[s][d:d + 1].rearrange("j p t -> p j t")[:, :, t0:t0 + 128], reads=[f"xwa{s}"],
                       writes=['r_ssum'])
                em.dma('sp', pp[:, 25:26, :], XWA[s][2 + d:3 + d].rearrange("j p t -> p j t")[:, :, t0:t0 + 128], reads=[f"xwa{s}"],
                       writes=['r_ssum'])
                rr = pp[:, 0:8, :]
                kr = pp[:, 8:16, :]
                vr = pp[:, 16:24, :]
                em.act(twb[:, 0, :], pp[:, 24, :], AF.Tanh, ['r_ssum'], ['r_twb'])
                em.cp('dve', twb[:, 1, :], pp[:, 25, :], ['r_ssum'], ['r_twb'])
                for h in range(8):
                    em.mm(B[h // 4][0:64, (h % 4) * 128:(h % 4 + 1) * 128], w2b[:, d, h * 64:(h + 1) * 64], twb[:, 0, :], True, True,
                          ['w2b', 'r_twb'], [f"B{h // 4}"], inc=(h == 7))
                for h in range(8):
                    em.act(lw[:, h, :], B[h // 4][0:64, (h % 4) * 128:(h % 4 + 1) * 128], AF.Sigmoid, [f"B{h // 4}", 'PV64'],
                           ['r_lw'], bias=PV64[:, P6_W0 + d * 8 + h:P6_W0 + d * 8 + h + 1], scale=1.0)
                for h in range(8):
                    em.mm(B[h // 4][0:64, (h % 4) * 128:(h % 4 + 1) * 128], a2b[:, d, h * 64:(h + 1) * 64], twb[:, 1, :], True, True,
                          ['a2b', 'r_twb'], [f"B{h // 4}"], inc=(h == 7))
                for h in range(8):
                    em.act(aa[:, h, :], B[h // 4][0:64, (h % 4) * 128:(h % 4 + 1) * 128], AF.Sigmoid,
                           [f"B{h // 4}", 'PV64'], ['r_aa'], bias=PV64[:, P6_A0 + d * 8 + h:P6_A0 + d * 8 + h + 1], scale=1.0)
                em.ts('dve', lw, lw, -R_DECAY_SCALE, None, ALU.mult, None, ['r_lw'], ['r_lw'])
                em.tt('dve', kk, kr, PV64[:, P6_KK:P6_KK + 8].unsqueeze(2).to_broadcast([64, 8, 128]), ALU.mult,
                      ['r_ssum', 'PV64'], ['r_kk'])
                em.act(sqb, kk, AF.Square, ['r_kk'], ['r_prod'])
                for hh in range(2):
                    em.mm(B[hh][0:64, :], onesb[0:64, 0:64], sqb[:, hh * 4:(hh + 1) * 4, :], True, True,
                          ['CSTB', 'r_prod'], [f"B{hh}"])
                for hh in range(2):
                    em.act(rinv[:, hh * 4:(hh + 1) * 4, :], B[hh][0:64, :].rearrange("p (h t) -> p h t", t=128), AF.Sqrt,
                           [f"B{hh}"], ['r_lex'])
                em.ts('dve', rinv, rinv, 1e-12, None, ALU.max, None, ['r_lex'], ['r_lex'])
                em.op('dve', lambda e: e.reciprocal(out=rinv, in_=rinv), ['r_lex'], ['r_lex'])
                em.tt('dve', kk, kk, rinv, ALU.mult, ['r_kk', 'r_lex'], ['r_kk'])
                em.tt('dve', tmpk, aa, PV64[:, P6_KA:P6_KA + 8].unsqueeze(2).to_broadcast([64, 8, 128]), ALU.mult,
                      ['r_aa', 'PV64'], ['r_tmpk'])
                em.tt('dve', tmpk, tmpk, OMMU.unsqueeze(2).to_broadcast([64, 8, 128]), ALU.add, ['r_tmpk', 'OMMU'], ['r_tmpk'])
                em.tt('dve', kd, kr, tmpk, ALU.mult, ['r_ssum', 'r_tmpk'], ['r_kd'])
                em.op('dve', lambda e: e.tensor_tensor_scan(out=linc.rearrange("p h t -> p (h t)"),
                                                            data0=RMK.rearrange("p h t -> p (h t)"),
                                                            data1=lw.rearrange("p h t -> p (h t)"), initial=0.0,
                                                            op0=ALU.mult, op1=ALU.add), ['RMK', 'r_lw'], ['r_linc'])
                if d == 0:
                    tot = linc[:, :, 127:128]
                else:
                    em.tt('dve', lex, lw, linc, ALU.subtract, ['r_lw', 'r_linc'], ['r_lex'])
                    em.cp('dve', gC, linc[:, :, 127], ['r_linc'], [kgC])
                    em.tt('dve', linc, lex, gC.unsqueeze(2).to_broadcast([64, 8, 128]), ALU.add, ['r_lex', kgC, 'r_linc'],
                          ['r_linc'])
                    tot = linc[:, :, 0:1]
                em.tt('dve', lex, linc, lw, ALU.subtract, ['r_linc', 'r_lw'], ['r_lex'])
                em.act(e1, linc, AF.Exp, ['r_linc'], ['r_e1'])
                em.act(e0, lex, AF.Exp, ['r_lex'], ['r_e0'])
                em.act(ei, linc, AF.Exp, ['r_linc'], ['r_ei'], scale=-1.0)
                em.act(gC, tot.rearrange("p h o -> p (h o)"), AF.Exp, ['r_linc', kgC], [kgC])
                em.tt('dve', AR[:, :, 128:256], rr, e1, ALU.mult, ['r_ssum', 'r_e1'], [kAR])
                em.stt(AR[:, :, 0:128], kk, -1.0, e0, ALU.mult, ALU.mult, ['r_kk', 'r_e0'], [kAR])
                em.tt('dve', tmpk, kk, aa, ALU.mult, ['r_kk', 'r_aa', 'r_tmpk'], ['r_tmpk'])
                em.tt('dve', BK[:, :, 0, :], tmpk, ei, ALU.mult, ['r_tmpk', 'r_ei'], [kBK])
                em.tt('dve', BK[:, :, 1, :], kd, ei, ALU.mult, ['r_kd', 'r_ei'], [kBK])
                em.tt('dve', tmpk, rr, kd, ALU.mult, ['r_ssum', 'r_kd', 'r_tmpk'], ['r_tmpk'])
                em.tt('dve', prodb, tmpk, PV64[:, P6_RK:P6_RK + 8].unsqueeze(2).to_broadcast([64, 8, 128]), ALU.mult,
                      ['r_tmpk', 'PV64'], ['r_prod'])
                for h in range(8):
                    em.tr(B[0][:, h * 64:(h + 1) * 64], vr[:, h, :], ident[0:64, 0:64], ['r_ssum', 'CST'], ['B0'], inc=(h == 7))
                em.cp('act', vtokb, B[0], ['B0'], [kvtokb])
                for half in range(2):
                    for h4 in range(4):
                        for q in range(2):
                            em.tr(BBp[:, (h4 * 2 + q) * 64:(h4 * 2 + q + 1) * 64], BK[:, half * 4 + h4, q, :], identb[0:64, 0:64],
                                  [kBK, 'CSTB'], ['BB'], inc=(h4 == 3 and q == 1))
                    em.cp('dve', BKtok[:, half * 4:(half + 1) * 4].rearrange("p h q k -> p (h q k)"), BBp[:, 0:512], ['BB'], [kBKtok])
                for h in range(8):
                    em.mm(B[1][:, 256 + h:257 + h], prodb[:, h, :], onesb[0:64, 0:1], True, True, ['r_prod', 'CSTB'], ['B1'], inc=(h == 7))
                em.cp('act', coef, B[1][:, 256:264], ['B1'], [kcoef])
                for hp in range(4):
                    bb_ = B[0]
                    bbk = "B0"
                    bk2 = B[1]
                    bk2k = "B1"
                    for q in range(2):
                        h = hp * 2 + q
                        em.mm(bb_[:, q * 256:(q + 1) * 256], BK[:, h, 0, :], AR[:, h, :], True, True, [kBK, kAR, 'r_lw', 'r_aa'], [bbk], inc=(q == 1))
                        em.mm(bk2[:, q * 256:(q + 1) * 256], BK[:, h, 1, :], AR[:, h, :], True, True, [kBK, kAR], [bk2k], inc=(q == 1))
                    em.tt('dve', GB[:, hp * 2:hp * 2 + 2, :], bb_.rearrange("p (q c) -> p q c", c=256),
                          MARt[:, d:d + 1, :].to_broadcast([128, 2, 256]), ALU.mult, [bbk, 'MAR'], [kGB])
                    em.tt('dve', Q0[:, hp * 2:hp * 2 + 2, :], bb_.rearrange("p (q c) -> p q c", c=256)[:, :, 0:128],
                          CST[:, C_MP0 + (1 - d) * 128:C_MP0 + (2 - d) * 128].unsqueeze(1).to_broadcast([128, 2, 128]), ALU.mult,
                          [bbk, 'CST'], [kQ0])
                    em.tt('dve', GK[:, hp * 2:hp * 2 + 2, :], bk2.rearrange("p (q c) -> p q c", c=256),
                          MARt[:, d:d + 1, :].to_broadcast([128, 2, 256]), ALU.mult, [bk2k, 'MAR'], [kGK])
                if split:
                    em.stream = ('rs', idx)
                if first_of_d:
                    em.memset('pool', Hs, 0.0, ['r_H'])
                    em.memset('pool', Hb, 0.0, ['r_Hb'])
                for hh in range(2):
                    bp = B[2 + hh]
                    bpk = f"B{2 + hh}"
                    for q in range(4):
                        h = hh * 4 + q
                        em.mm(bp[:, q * 128:(q + 1) * 128], AR[:, h, 0:128], BK[:, h, 0, :], True, True, [kAR, kBK], [bpk], inc=(q == 3))
                    b3 = bp.rearrange("p (q c) -> p q c", c=128)
                    hsl = slice(hh * 4, (hh + 1) * 4)
                    em.tt('dve', Pm[0][:, hsl, :], b3, CST[:, C_MP0 + d * 128:C_MP0 + (d + 1) * 128].unsqueeze(1).to_broadcast([128, 4, 128]),
                          ALU.mult, [bpk, 'CST'], ['r_P0'])
                    em.tt('dve', E1m[:, hsl, :], b3, CST[:, C_ME1 + d * 128:C_ME1 + (d + 1) * 128].unsqueeze(1).to_broadcast([128, 4, 128]),
                          ALU.mult, [bpk, 'CST'], ['r_E1'])
                    em.tt('dve', E2m[:, hsl, :], b3, CST[:, C_ME2 + d * 128:C_ME2 + (d + 1) * 128].unsqueeze(1).to_broadcast([128, 4, 128]),
                          ALU.mult, [bpk, 'CST'], ['r_E2'])
                em.tt('dve', Ym[0], Q0, identb.unsqueeze(1).to_broadcast([128, 8, 128]), ALU.add, [kQ0, 'CSTB'], ['r_Y0'])
                em.cp('act', Qm[0], Q0, [kQ0], ['r_Q0'])
                cur = 0
                for lev in range(1, 5):
                    nxt = 1 - cur
                    for hh in range(2):
                        bp = B[2]
                        bpk = "B2"
                        bq = B[3]
                        bqk = "B3"
                        hsl = slice(hh * 4, (hh + 1) * 4)
                        for q in range(4):
                            h = hh * 4 + q
                            em.mm(bp[:, q * 128:(q + 1) * 128], Qm[cur][:, h, :], Pm[cur][:, h, :], True, True,
                                  [f"r_Q{cur}", f"r_P{cur}"], [bpk], inc=(q == 3))
                        for q in range(4):
                            h = hh * 4 + q
                            em.mm(bq[:, q * 128:(q + 1) * 128], Pm[cur][:, h, :], Qm[cur][:, h, :], True, True,
                                  [f"r_Q{cur}", f"r_P{cur}"], [bqk], inc=(q == 3))
                        em.cp('act', Pm[nxt][:, hsl, :], bp.rearrange("p (q c) -> p q c", c=128), [bpk], [f"r_P{nxt}"])
                        em.cp('dve', Qm[nxt][:, hsl, :], bq.rearrange("p (q c) -> p q c", c=128), [bqk], [f"r_Q{nxt}"])
                    for hh in range(2):
                        by = B[6]
                        byk = "B6"
                        hsl = slice(hh * 4, (hh + 1) * 4)
                        for q in range(4):
                            h = hh * 4 + q
                            em.mm(by[:, q * 128:(q + 1) * 128], Pm[nxt][:, h, :], Ym[cur][:, h, :], True, True,
                                  [f"r_P{nxt}", f"r_Y{cur}"], [byk], inc=(q == 3))
                        em.tt('dve', Ym[nxt][:, hsl, :], by.rearrange("p (q c) -> p q c", c=128), Ym[cur][:, hsl, :], ALU.add,
                              [byk, f"r_Y{cur}"], [f"r_Y{nxt}"])
                    cur = nxt
                Dt = Ym[cur]
                dtk = f"r_Y{cur}"
                for st, (Em_, ek) in enumerate([(E1m, 'r_E1'), (E2m, 'r_E2')]):
                    oth = Ym[1 - cur]
                    othk = f"r_Y{1 - cur}"
                    for half in range(2):
                        for h4 in range(4):
                            em.tr(BBp[:, 512 + h4 * 128:512 + (h4 + 1) * 128], Dt[:, half * 4 + h4, :], identb, [dtk, 'CSTB'], ['BB'], inc=(h4 == 3))
                        em.cp('act', Dm[:, half * 4:(half + 1) * 4, :], BBp[:, 512:1024].rearrange("p (h c) -> p h c", c=128),
                              ['BB'], ['r_D'])
                    for hh in range(2):
                        bz = B[2 + hh]
                        bzk = f"B{2 + hh}"
                        hsl = slice(hh * 4, (hh + 1) * 4)
                        for q in range(4):
                            h = hh * 4 + q
                            em.mm(bz[:, q * 128:(q + 1) * 128], Em_[:, h, :], Dt[:, h, :], True, True, [ek, dtk], [bzk], inc=(q == 3))
                        em.cp('act' if hh == 0 else 'dve', Zm[:, hsl, :], bz.rearrange("p (q c) -> p q c", c=128), [bzk], ['r_Z'])
                    for hh in range(2):
                        by = B[2 + hh]
                        byk = f"B{2 + hh}"
                        hsl = slice(hh * 4, (hh + 1) * 4)
                        for q in range(4):
                            h = hh * 4 + q
                            em.mm(by[:, q * 128:(q + 1) * 128], Dm[:, h, :], Zm[:, h, :], True, True, ['r_D', 'r_Z'], [byk], inc=(q == 3))
                        em.tt('dve', oth[:, hsl, :], by.rearrange("p (q c) -> p q c", c=128), Dt[:, hsl, :], ALU.add,
                              [byk, dtk], [othk])
                    cur = 1 - cur
                    Dt = Ym[cur]
                    dtk = f"r_Y{cur}"
                TT_ = Dt
                ttk = dtk
                for h in range(8):
                    hs_ = slice(h * 64, (h + 1) * 64)
                    em.mm(B[2][:, hs_], AR[:, h, 0:128], Hb[:, hs_], True, False, [kAR, 'r_Hb'], ['B2'])
                    em.mm(B[2][:, hs_], GK[:, h, 0:128], vtokb[:, hs_], False, True, [kGK, kvtokb], ['B2'], inc=(h == 7))
                em.cp('act', Wt, B[2], ['B2'], ['r_W'])
                for h in range(8):
                    hs_ = slice(h * 64, (h + 1) * 64)
                    em.mm(B[3][:, hs_], TT_[:, h, :], Wt[:, hs_], True, True, [ttk, 'r_W'], ['B3'], inc=(h == 7))
                em.cp('dve', Ut, B[3], ['B3'], ['r_U'])
                for h in range(8):
                    hs_ = slice(h * 64, (h + 1) * 64)
                    em.mm(B[2][:, hs_], AR[:, h, 128:256], Hb[:, hs_], True, False, [kAR, 'r_Hb'], ['B2'])
                    em.mm(B[2][:, hs_], GB[:, h, 128:256], Ut[:, hs_], False, False, [kGB, 'r_U'], ['B2'])
                    em.mm(B[2][:, hs_], GK[:, h, 128:256], vtokb[:, hs_], False, True, [kGK, kvtokb], ['B2'], inc=(h == 7))
                for h in range(8):
                    hs_ = slice(h * 64, (h + 1) * 64)
                    em.mm(B[3][0:64, hs_], BKtok[:, h, 0, :], Ut[:, hs_], True, False, [kBKtok, 'r_U'], ['B3'])
                    em.mm(B[3][0:64, hs_], BKtok[:, h, 1, :], vtokb[:, hs_], False, True, [kBKtok, kvtokb], ['B3'], inc=(h == 7))
                em.tt('dve', Hs, Hs, B[3][0:64, :], ALU.add, ['r_H', 'B3'], ['r_H'])
                em.tt('dve', Hs.rearrange("p (h v) -> p h v", v=64), Hs.rearrange("p (h v) -> p h v", v=64),
                      gC.unsqueeze(2).to_broadcast([64, 8, 64]), ALU.mult, ['r_H', kgC], ['r_H'])
                em.cp('act', Hb, Hs, ['r_H', 'B2'], ['r_Hb'])
                if d == 0:
                    if emit_out:
                        em.cp('act', ot[:, 0:512], B[2], ['B2'], ['r_o'])
                        em.cp('dve', ot[:, 512:520], coef, [kcoef], ['r_ocoef'])
                        em.dma('pool', OF[s][t0:t0 + 128, :], ot, reads=['r_o', 'r_ocoef'], writes=[f"of{s}"])
                elif emit_out:
                    em.dma('sp', oft, OF[s][t0:t0 + 128, :], reads=[f"of{s}"], writes=['r_of'])
                    em.tt('dve', ot[:, 0:512], B[2], oft[:, 0:512], ALU.add, ['B2', 'r_of'], ['r_o'])
                    o3 = ot[:, 0:512].rearrange("p (h v) -> p h v", v=64)
                    em.op('dve', lambda e: e.tensor_reduce(out=gn[:, 0:8], in_=o3, axis=AX.X, op=ALU.add), ['r_o'], ['r_gn'])
                    em.act(osq, ot[:, 0:512], AF.Square, ['r_o'], ['r_osq'])
                    em.op('dve', lambda e: e.tensor_reduce(out=gn[:, 8:16], in_=osq.rearrange("p (h v) -> p h v", v=64),
                                                           axis=AX.X, op=ALU.add), ['r_osq', 'r_gn'], ['r_gn'])
                    em.ts('dve', gn[:, 16:24], gn[:, 0:8], 1.0 / 64, None, ALU.mult, None, ['r_gn'], ['r_gn'])
                    em.tt('dve', gn[:, 0:8], gn[:, 16:24], gn[:, 16:24], ALU.mult, ['r_gn'], ['r_gn'])
                    em.stt(gn[:, 24:32], gn[:, 8:16], 1.0 / 64, gn[:, 0:8], ALU.mult, ALU.subtract, ['r_gn'], ['r_gn'])
                    em.act(gn[:, 24:32], gn[:, 24:32], AF.Sqrt, ['r_gn'], ['r_gn'], bias=R_LN_EPS, scale=1.0)
                    em.op('dve', lambda e: e.reciprocal(out=gn[:, 24:32], in_=gn[:, 24:32]), ['r_gn'], ['r_gn'])
                    em.tt('dve', o3, o3, gn[:, 16:24].unsqueeze(2).to_broadcast([128, 8, 64]), ALU.subtract, ['r_o', 'r_gn'], ['r_o'])
                    em.tt('dve', o3, o3, gn[:, 24:32].unsqueeze(2).to_broadcast([128, 8, 64]), ALU.mult, ['r_o', 'r_gn'], ['r_o'])
                    em.tt('dve', ot[:, 0:512], ot[:, 0:512], lnw, ALU.mult, ['r_o', 'ROWB'], ['r_o'])
                    em.tt('dve', ot[:, 0:512], ot[:, 0:512], lnb, ALU.add, ['r_o', 'ROWB'], ['r_o'])
                    em.tt('dve', gn[:, 32:40], coef, oft[:, 512:520], ALU.add, [kcoef, 'r_of', 'r_gn'], ['r_gn'])
                    em.tt('dve', osq.rearrange("p (h v) -> p h v", v=64), vtokb.rearrange("p (h v) -> p h v", v=64),
                          gn[:, 32:40].unsqueeze(2).to_broadcast([128, 8, 64]), ALU.mult, [kvtokb, 'r_gn', 'r_osq'], ['r_osq'])
                    em.tt('dve', ot[:, 0:512], ot[:, 0:512], osq, ALU.add, ['r_o', 'r_osq'], ['r_o'])
                    em.dma('sp', xg0[:, 0:128], XG[s][0:128, t0:t0 + 128], reads=[f"xg{s}"], writes=['r_xg0'])
                    em.dma('sp', xg1[:, 0:128], XG[s][128:160, t0:t0 + 128], reads=[f"xg{s}"], writes=['r_xg1'])
                    em.act(sg0, xg0[:, 0:128], AF.Sigmoid, ['r_xg0'], ['r_sg0'])
                    em.act(sg1, xg1[:, 0:128], AF.Sigmoid, ['r_xg1'], ['r_sg1'])
                    em.mm(B[3], sg0, g2b0, True, False, ['r_sg0', 'g2b'], ['B3'])
                    em.mm(B[3], sg1, g2b1, False, True, ['r_sg1', 'g2b'], ['B3'])
                    em.tt('dve', ot[:, 0:512], ot[:, 0:512], B[3], ALU.mult, ['r_o', 'B3'], ['r_o'])
                    for j in range(4):
                        em.tr(B[2][:, j * 128:(j + 1) * 128], ot[:, j * 128:(j + 1) * 128], ident, ['r_o', 'CST'], ['B2'], inc=(j == 3))
                    em.cp('act', mixo[:, 4:8, :], B[2].rearrange("p (j t) -> p j t", t=128), ['B2'], ['mixo_r'])
                    em.dma('pool', fm(MIX[s])[:, 4:8, t0:t0 + 128], mixo[:, 4:8, :], reads=['mixo_r'], writes=[f"mixr{s}"])

            mc_ = mixcfg or {}
            inter = mc_.get('interleave', True)
            for b in range(mc_.get('nb', NBL)):
                nw = 0
                for stream, fnc in (('m', mamba_chunk), ('r', wkv_chunk)):
                    if not mc_.get('mamba' if stream == 'm' else 'wkv', True):
                        continue
                    for d in range(mc_.get('nd', 2)):
                        first = True
                        if stream == 'm':
                            em.stream = 'm' if inter else None
                            em.memset('pool', hst, 0.0, ['m_hst'])
                            em.memset('pool', hstb, 0.0, ['m_hstb'])
                        for kind in range(mc_.get('nkind', 2)):
                            s = b * 2 + kind
                            T = seqT(s)
                            nch = T // CH
                            order = range(nch) if d == 0 else range(nch - 1, -1, -1)
                            emit = (kind == 1) or need_ctx_out or mc_.get('ctxout', False)
                            for c in order:
                                if stream == 'm':
                                    fnc(s, c * CH, d, emit)
                                else:
                                    fnc(s, c * CH, d, emit, idx=nw, first_of_d=first, split=inter)
                                    nw += 1
                                    first = False
                em.stream = None
                if inter:
                    em.flush_mixer(nw)
            em.barrier()

    def stage_proj_post(l, phase, wap, Kc, SRC, srcname, gidx, seqs):
        with ExitStack() as es:
            def S(name, shape, dt=F32):
                return es.enter_context(nc.sbuf_tensor(name, list(shape), dt)).ap()
            nm = f"pp{phase}"
            wb = load_wbf(es, l, wap, Kc, D, nm + "w")
            a = S(f"{nm}a{l}", [128, Kc, 512], BF16)
            xt = S(f"{nm}x{l}", [128, 8, 512])
            y = S(f"{nm}y{l}", [128, 8, 512])
            sq = S(f"{nm}sq{l}", [128, 8, 512], BF16)
            rstd = S(f"{nm}rs{l}", [128, 512])
            pss = [es.enter_context(nc.psum_tensor(f"{nm}ps{l}_{i}", [128, 512], F32)).ap() for i in range(5)]
            for s in seqs:
                T = seqT(s)
                TW = min(512, T)
                jmod = 2 if s % 2 == 0 else s // 2
                rsrc, rsk = res_src(l, s, phase)
                rdst, rdk = res_dst(l, s, phase)
                for tt_ in range(T // TW):
                    t0 = tt_ * TW
                    em.dma('sp', a[:, :, :TW], fm(SRC[s])[:, :, t0:t0 + TW], reads=([f"mixm{s}", f"mixr{s}"] if srcname == 'mix' else [f"{srcname}{s}"]), writes=[nm + 'a'])
                    em.dma('pool', xt[:, :, :TW], fm(rsrc)[:, :, t0:t0 + TW], reads=[rsk], writes=[nm + 'x'])
                    for m in range(8):
                        ps = pss[m % 4]
                        pk = f"ps{m % 4}"
                        for k in range(Kc):
                            em.mm(ps[:, :TW], wb[:, k, m * 128:(m + 1) * 128], a[:, k, :TW], k == 0, k == Kc - 1,
                                  [nm + 'wbf', nm + 'a'], [pk])
                        em.cp('dve', y[:, m, :TW], ps[:, :TW], [pk], [nm + 'y'])
                        em.act(sq[:, m, :TW], ps[:, :TW], AF.Square, [pk], [nm + 'sq'])
                    for m in range(8):
                        em.mm(pss[4][:, :TW], onesb, sq[:, m, :TW], m == 0, m == 7, ['CSTB', nm + 'sq'], ['ps4'])
                    em.act(rstd[:, :TW], pss[4][:, :TW], AF.Sqrt, ['ps4'], [nm + 'rs'], bias=EPS, scale=1.0 / D)
                    em.op('dve', lambda e: e.reciprocal(out=rstd[:, :TW], in_=rstd[:, :TW]), [nm + 'rs'], [nm + 'rs'])
                    for m in range(8):
                        em.stt(y[:, m, :TW], y[:, m, :TW], DER[:, gidx, m, jmod:jmod + 1], rstd[:, :TW], ALU.mult, ALU.mult,
                               [nm + 'y', nm + 'rs', 'DER'], [nm + 'y'])
                    em.tt('dve', xt[:, :, :TW], xt[:, :, :TW], y[:, :, :TW], ALU.add, [nm + 'x', nm + 'y'], [nm + 'x'])
                    em.dma('pool', fm(rdst)[:, :, t0:t0 + TW], xt[:, :, :TW], reads=[nm + 'x'], writes=[rdk])
            em.barrier()

    def stage_ffn_up(l, seqs):
        with ExitStack() as es:
            def S(name, shape, dt=F32):
                return es.enter_context(nc.sbuf_tensor(name, list(shape), dt)).ap()
            wb = load_wbf(es, l, f_w_up[l], 8, 2 * DFF, "wup")
            xt = S(f"fux{l}", [128, 8, 512])
            h = S(f"fuh{l}", [128, 8, 512], BF16)
            sq = S(f"fusq{l}", [128, 8, 512], BF16)
            rstd = S(f"furs{l}", [128, 512])
            stg = [S(f"fustg{l}_{i}", [128, 512]) for i in range(2)]
            stv = [S(f"fustv{l}_{i}", [128, 512], BF16) for i in range(2)]
            pss = [es.enter_context(nc.psum_tensor(f"fups{l}_{i}", [128, 512], F32)).ap() for i in range(8)]
            for s in seqs:
                T = seqT(s)
                TW = min(512, T)
                jmod = 2 if s % 2 == 0 else s // 2
                src, srck = res_src(l, s, 1)
                for tt_ in range(T // TW):
                    t0 = tt_ * TW
                    em.dma('sp', xt[:, :, :TW], fm(src)[:, :, t0:t0 + TW], reads=[srck], writes=['fux'])
                    prenorm(xt, TW, h, sq, rstd, pss[7], jmod, 2, 24, 'fux', 'fuh', 'ps7')
                    for j in range(NFF):
                        pg = pss[(2 * j) % 6]
                        pgk = f"ps{(2 * j) % 6}"
                        pv_ = pss[(2 * j + 1) % 6]
                        pvk = f"ps{(2 * j + 1) % 6}"
                        for k in range(8):
                            em.mm(pg[:, :TW], wb[:, k, j * 128:(j + 1) * 128], h[:, k, :TW], k == 0, k == 7, ['wupbf', 'fuh'], [pgk])
                        for k in range(8):
                            em.mm(pv_[:, :TW], wb[:, k, DFF + j * 128:DFF + (j + 1) * 128], h[:, k, :TW], k == 0, k == 7,
                                  ['wupbf', 'fuh'], [pvk])
                        sg_ = stg[j % 2]
                        sv_ = stv[j % 2]
                        em.cp('dve', sg_[:, :TW], pg[:, :TW], [pgk], [f"fustg{j % 2}"])
                        em.cp('act', sv_[:, :TW], pv_[:, :TW], [pvk], [f"fustv{j % 2}"])
                        em.dma('pool', GATE[s][j * 128:(j + 1) * 128, t0:t0 + TW], sg_[:, :TW], reads=[f"fustg{j % 2}"],
                               writes=[f"gate{s}"])
                        em.dma('sp', VAL[s][j * 128:(j + 1) * 128, t0:t0 + TW], sv_[:, :TW], reads=[f"fustv{j % 2}"],
                               writes=[f"val{s}"])
            em.barrier()

    def stage_ffn_conv(l, seqs):
        with ExitStack() as es:
            def S(name, shape, dt=F32):
                return es.enter_context(nc.sbuf_tensor(name, list(shape), dt)).ap()
            gflat = [S(f"fcg{l}_{i}", [128, 2048]) for i in range(2)]
            vflat = [S(f"fcv{l}_{i}", [128, 2048], BF16) for i in range(2)]
            gpx = [S(f"fcgpx{l}_{i}", [128, 34, 66], BF16) for i in range(2)]
            gpc = [S(f"fcgpc{l}_{i}", [128, 3, 258], BF16) for i in range(2)]
            dg = [S(f"fcdg{l}_{i}", [128, 9, 128], BF16) for i in range(2)]
            acc = S(f"fcacc{l}", [128, 2048])
            u = S(f"fcu{l}", [128, 2048])
            ab = [S(f"fcab{l}_{i}", [128, 2048], BF16) for i in range(2)]
            pss = [es.enter_context(nc.psum_tensor(f"fcps{l}_{i}", [128, 512], F32)).ap() for i in range(4)]
            for i in range(2):
                em.memset('pool', gpx[i], 0.0, [f"fcgpx{i}"])
                em.memset('pool', gpc[i], 0.0, [f"fcgpc{i}"])
            it = 0
            pi = 0
            for s in seqs:
                T = seqT(s)
                for j in range(NFF):
                    i2 = it % 2
                    it += 1
                    if s % 2 == 1:
                        R, Cc, gp, gpk = 32, 64, gpx[i2], f"fcgpx{i2}"
                    else:
                        R, Cc, gp, gpk = 1, 256, gpc[i2], f"fcgpc{i2}"
                    gf = gflat[i2]
                    vf = vflat[i2]
                    em.dma('sp', gf[:, :T], GATE[s][j * 128:(j + 1) * 128, :], reads=[f"gate{s}"], writes=[f"fcg{i2}"])
                    em.dma('sp', vf[:, :T], VAL[s][j * 128:(j + 1) * 128, :], reads=[f"val{s}"], writes=[f"fcv{i2}"])
                    em.cp('act', gp[:, 1:1 + R, 1:1 + Cc], gf[:, :T].rearrange("p (r c) -> p r c", c=Cc), [f"fcg{i2}"], [gpk])
                    taps = list(range(9)) if s % 2 == 1 else [3, 4, 5]
                    for tap in taps:
                        em.ts('dve', dg[i2][:, tap, :], identb, PV[:, PV_FCW + tap * NFF + j:PV_FCW + tap * NFF + j + 1], None,
                              ALU.mult, None, ['CSTB', 'PV'], [f"fcdg{i2}"])
                    nblk = T // 512 if s % 2 == 1 else 1
                    for blk in range(nblk):
                        ps = pss[pi % 4]
                        pk = f"ps{pi % 4}"
                        pi += 1
                        for ti, tap in enumerate(taps):
                            dr, dc = tap // 3 - 1, tap % 3 - 1
                            if s % 2 == 1:
                                rhs = gp[:, 1 + dr + blk * 8:1 + dr + blk * 8 + 8, 1 + dc:1 + dc + 64]
                                out = ps.rearrange("p (r c) -> p r c", c=64)
                                wdt = 512
                            else:
                                rhs = gp[:, 1, 1 + dc:1 + dc + 256]
                                out = ps[:, 0:256]
                                wdt = 256
                            em.mm(out, dg[i2][:, tap, :], rhs, ti == 0, ti == len(taps) - 1, [f"fcdg{i2}", gpk], [pk])
                        em.act(acc[:, blk * 512:blk * 512 + wdt], ps[:, 0:wdt], AF.Identity, [pk, 'PV'], ['fcacc'],
                               bias=PV[:, PV_FCB + j:PV_FCB + j + 1], scale=1.0)
                    em.act(u[:, :T], acc[:, :T], AF.Square, ['fcacc'], ['fcu'])
                    em.ts('dve', u[:, :T], u[:, :T], 0.044715, 1.0, ALU.mult, ALU.add, ['fcu'], ['fcu'])
                    em.tt('dve', u[:, :T], u[:, :T], acc[:, :T], ALU.mult, ['fcu', 'fcacc'], ['fcu'])
                    em.act(u[:, :T], u[:, :T], AF.Sigmoid, ['fcu'], ['fcu'], scale=GELU_C)
                    em.tt('dve', u[:, :T], u[:, :T], acc[:, :T], ALU.mult, ['fcu', 'fcacc'], ['fcu'])
                    em.tt('dve', ab[i2][:, :T], u[:, :T], vf[:, :T], ALU.mult, ['fcu', f"fcv{i2}"], [f"fcab{i2}"])
                    em.dma('pool', ACTV[s][j * 128:(j + 1) * 128, :], ab[i2][:, :T], reads=[f"fcab{i2}"], writes=[f"actv{s}"])
            em.barrier()

    allseq = list(range(NS))
    xseq = [s for s in range(NS) if s % 2 == 1]
    outkeys = []
    for l in range(n_layers):
        last = (l == n_layers - 1)
        stage_mod(l)
        if stop_after == 'mod':
            break
        stage_inproj(l)
        if stop_after == 'inproj':
            break
        stage_mixer(l, need_ctx_out=not last)
        if stop_after == 'mixer':
            break
        seqs = xseq if last else allseq
        stage_proj_post(l, 0, w_out[l], 8, MIX, "mix", 1, seqs)
        if stop_after == 'outproj':
            break
        stage_ffn_up(l, seqs)
        stage_ffn_conv(l, seqs)
        stage_proj_post(l, 1, f_w_down[l], NFF, ACTV, "actv", 3, seqs)
    em.barrier()
    return nc, em


def host_prep(inp):
    f = np.float32
    idx = np.arange(128)
    cstn = np.zeros((128, NCST), f)
    cstn[:, C_ID:C_ID + 128] = np.eye(128)
    cstn[:, C_UTI:C_UTI + 128] = (idx[:, None] <= idx[None, :])
    cstn[:, C_LTI:C_LTI + 128] = (idx[:, None] >= idx[None, :])
    cstn[:, C_UTS:C_UTS + 128] = (idx[:, None] < idx[None, :])
    cstn[:, C_LTS:C_LTS + 128] = (idx[:, None] > idx[None, :])
    cstn[:, C_ONE:C_ONE + 128] = 1.0
    b32 = idx // 32
    b64 = idx // 64
    same32 = b32[:, None] == b32[None, :]
    same64 = b64[:, None] == b64[None, :]
    for d in range(2):
        strict = (idx[:, None] > idx[None, :]) if d == 0 else (idx[:, None] < idx[None, :])
        cstn[:, C_MP0 + d * 128:C_MP0 + (d + 1) * 128] = strict & same32
        cstn[:, C_ME1 + d * 128:C_ME1 + (d + 1) * 128] = strict & same64 & (~same32)
        cstn[:, C_ME2 + d * 128:C_ME2 + (d + 1) * 128] = strict & (~same64)
    pvn = np.zeros((L, 128, NPV), f)
    pv6 = np.zeros((L, 64, NPV64), f)
    rwn = np.zeros((L, 1, NROW), f)
    for l in range(L):
        pvn[l, :, PV_BMOD:PV_BMOD + 48] = inp['b_mod'][l].reshape(48, 128).T
        pvn[l, :, PV_GPRE1:PV_GPRE1 + 8] = inp['g_mix_pre'][l].reshape(8, 128).T
        pvn[l, :, PV_GPOST1:PV_GPOST1 + 8] = inp['g_mix_post'][l].reshape(8, 128).T
        pvn[l, :, PV_GPRE2:PV_GPRE2 + 8] = inp['g_ffn_pre'][l].reshape(8, 128).T
        pvn[l, :, PV_GPOST2:PV_GPOST2 + 8] = inp['g_ffn_post'][l].reshape(8, 128).T
        pvn[l, :, PV_MCW:PV_MCW + 24] = inp['m_conv_w'][l].reshape(3, 8, 128).transpose(2, 0, 1).reshape(128, 24)
        pvn[l, :, PV_MCB:PV_MCB + 8] = inp['m_conv_b'][l].reshape(8, 128).T
        pvn[l, :, PV_FCW:PV_FCW + 198] = inp['f_conv_w'][l].reshape(9, NFF, 128).transpose(2, 0, 1).reshape(128, 198)
        pvn[l, :, PV_FCB:PV_FCB + NFF] = inp['f_conv_b'][l].reshape(NFF, 128).T
        mu = inp['r_mu'][l]
        pvn[l, :, PV_MUXG0] = mu[1792:1920]
        pvn[l, 0:32, PV_MUXG1] = mu[1920:1952]
        pv6[l, :, P6_MURKV:P6_MURKV + 24] = mu[0:1536].reshape(24, 64).T
        pv6[l, :, P6_MUWA:P6_MUWA + 4] = mu[1536:1792].reshape(4, 64).T
        pv6[l, :, P6_W0:P6_W0 + 16] = inp['r_w0'][l].reshape(2, 8, 64).transpose(2, 0, 1).reshape(64, 16)
        pv6[l, :, P6_A0:P6_A0 + 16] = inp['r_a0'][l].reshape(2, 8, 64).transpose(2, 0, 1).reshape(64, 16)
        pv6[l, :, P6_KK:P6_KK + 8] = inp['r_k_k'][l].reshape(8, 64).T
        pv6[l, :, P6_KA:P6_KA + 8] = inp['r_k_a'][l].reshape(8, 64).T
        pv6[l, :, P6_RK:P6_RK + 8] = inp['r_r_k'][l].T
        rwn[l, 0, RV_MNW:RV_MNW + 512] = inp['m_norm_w'][l]
        rwn[l, 0, RV_LNW:RV_LNW + 512] = inp['r_ln_w'][l]
        rwn[l, 0, RV_LNB:RV_LNB + 512] = inp['r_ln_b'][l]
        rwn[l, 0, RV_MD:RV_MD + 8] = inp['m_d'][l]
        rwn[l, 0, RV_DTB:RV_DTB + 16] = inp['m_dt_bias'][l].reshape(16)
        rwn[l, 0, RV_ALOG:RV_ALOG + 16] = inp['m_a_log'][l].reshape(16)
    return cstn, pvn, pv6, rwn


def make_in_maps(inp, cores):
    cstn, pvn, pv6, rwn = host_prep(inp)
    shared = {k: np.ascontiguousarray(np.asarray(inp[k], dtype=np.float32)) for k in
              ['w_mod', 'w_in', 'w_out', 'r_w2', 'r_a2', 'r_g2', 'f_w_up', 'f_w_down']}
    maps = []
    x = np.asarray(inp['x'], np.float32)
    ctx = np.asarray(inp['ctx'], np.float32)
    c = np.asarray(inp['c'], np.float32)
    cc = np.asarray(inp['c_ctx'], np.float32)
    for ci in cores:
        bs = [ci * NBL + i for i in range(NBL)]
        m = dict(shared)
        m['xT'] = np.ascontiguousarray(x[bs].transpose(0, 2, 1))
        m['ctxT'] = np.ascontiguousarray(ctx[bs].transpose(0, 2, 1))
        m['cT'] = np.ascontiguousarray(np.stack([c[bs[0]], c[bs[1]], cc], axis=1))
        m['cst'] = cstn
        m['pv'] = pvn
        m['pv64'] = pv6
        m['rowv'] = rwn
        m['rmu'] = np.ascontiguousarray(np.asarray(inp['r_mu'], np.float32).reshape(L, 1, 1952))
        maps.append(m)
    return maps


def kernel(**inputs):
    nc, em = build()
    cores = list(range(NCORE))
    maps = make_in_maps(inputs, cores)
    res = run_bass_kernel_spmd(nc, maps, core_ids=cores)
    out = np.empty((NCORE * NBL, TX, D), np.float32)
    for ci in cores:
        o = res.results[ci]["outT"]
        out[ci * NBL:(ci + 1) * NBL] = o.transpose(0, 2, 1)
    return out
```

```python
import numpy as np
from contextlib import ExitStack
import concourse.bass as bass
import concourse.mybir as mybir
from concourse.bass_utils import run_bass_kernel_spmd

F32 = mybir.dt.float32
BF16 = mybir.dt.bfloat16
AF = mybir.ActivationFunctionType
ALU = mybir.AluOpType
AX = mybir.AxisListType

L = 2
D = 1024
TX = 2048
TC = 256
NBL = 2
NCORE = 8
CH = 128
DFF = 2816
NFF = 22
EPS = 1e-6
R_LN_EPS = 64e-5
R_DECAY_SCALE = 0.6065306597126334
GELU_C = 1.5957691216057308

PV_BMOD = 0
PV_GPRE1 = 48
PV_GPOST1 = 56
PV_GPRE2 = 64
PV_GPOST2 = 72
PV_MCW = 80
PV_MCB = 104
PV_FCW = 112
PV_FCB = 310
PV_MUXG0 = 332
PV_MUXG1 = 333
NPV = 334
P6_MURKV = 0
P6_MUWA = 24
P6_W0 = 28
P6_A0 = 44
P6_KK = 60
P6_KA = 68
P6_RK = 76
NPV64 = 84
RV_MNW = 0
RV_LNW = 512
RV_LNB = 1024
RV_MD = 1536
RV_DTB = 1544
RV_ALOG = 1560
NROW = 1576
C_ID = 0
C_UTI = 128
C_LTI = 256
C_UTS = 384
C_LTS = 512
C_ONE = 640
C_MP0 = 768
C_ME1 = 1024
C_ME2 = 1280
NCST = 1536


class Em:
    def __init__(self, nc, ndma=16):
        self.nc = nc
        self.engs = {'pe': nc.tensor, 'act': nc.scalar, 'dve': nc.vector, 'pool': nc.gpsimd, 'sp': nc.sync}
        self.sem = {}
        self.cnt = {}
        for k in ['pe', 'act', 'dve', 'pool']:
            self.sem[k] = nc.alloc_semaphore("sem_" + k)
            self.cnt[k] = 0
        self.dq = {}
        for q in ['sp', 'pool', 'act']:
            self.dq[q] = {'n': ndma, 'next': 0}
            for i in range(ndma):
                self.sem[f"d_{q}_{i}"] = nc.alloc_semaphore(f"dsem_{q}_{i}")
                self.cnt[f"d_{q}_{i}"] = 0
        self.seen = {k: {} for k in self.engs}
        self.lastw = {}
        self.readers = {}
        self.n = 0
        self.pend = {k: False for k in self.engs}
        self.stream = None
        self.queues = {}

    def _deps(self, reads, writes):
        deps = {}

        def add(d):
            if d is None:
                return
            k, v = d
            if deps.get(k, 0) < v:
                deps[k] = v
        for b in reads:
            add(self.lastw.get(b))
        for b in writes:
            add(self.lastw.get(b))
            for r in self.readers.get(b, ()):
                add(r)
        return deps

    def _waits(self, eng, deps):
        for k, v in deps.items():
            if k == 'pe' and eng == 'pe':
                continue
            if k.startswith('d_'):
                v = self.cnt[k]
            if self.seen[eng].get(k, 0) >= v:
                continue
            self.seen[eng][k] = v
            self.engs[eng].wait_ge(self.sem[k], v)
            self.n += 1

    def _mark(self, me, reads, writes):
        for b in reads:
            self.readers.setdefault(b, []).append(me)
        for b in writes:
            self.lastw[b] = me
            self.readers[b] = []

    @staticmethod
    def _is_psum(k):
        return (k[0] == 'B' and (k[1:].isdigit() or k in ('BB', 'BBa', 'BBb'))) or k.startswith('ps')

    def flush(self):
        qs = {k: v for k, v in self.queues.items() if v}
        self.queues = {}
        pos = {k: 0 for k in qs}
        while qs:
            k = min(qs, key=lambda n: pos[n] / len(qs[n]))
            it = qs[k][pos[k]]
            pos[k] += 1
            if it[0] == 'op':
                self.op(*it[1:])
            else:
                self.dma(it[1], it[2], it[3], it[4], it[5], **it[6])
            if pos[k] >= len(qs[k]):
                del qs[k]

    @staticmethod
    def _merge(lists):
        lists = [l for l in lists if l]
        out = []
        pos = [0] * len(lists)
        live = list(range(len(lists)))
        sticky = None
        while live:
            k = sticky if sticky is not None else min(live, key=lambda n: pos[n] / len(lists[n]))
            it = lists[k][pos[k]]
            out.append(it)
            pos[k] += 1
            if it[0] == 'op' and it[1] == 'pe':
                sticky = None if it[5] else k
            if pos[k] >= len(lists[k]):
                live.remove(k)
                sticky = None
        return out

    @staticmethod
    def _ec(eng, out):
        n = max(out.free_size(), 64)
        return 0.12 + n / {'dve': 960.0, 'act': 1400.0, 'pool': 700.0}.get(eng, 960.0)

    def flush_mixer(self, nw):
        qs = self.queues
        self.queues = {}
        HOP = 0.6
        eng_free = {}
        kw_t = {}
        kr_t = {}
        out = []

        def engine_of(it):
            return it[1]

        def start_time(it):
            reads, writes = (it[3], it[4]) if it[0] == 'op' else (it[4], it[5])
            t = 0.0
            for k in reads:
                t = max(t, kw_t.get(k, 0.0))
                if self._is_psum(k):
                    t = max(t, kr_t.get(k, 0.0))
            for k in writes:
                t = max(t, kw_t.get(k, 0.0), kr_t.get(k, 0.0))
            return max(eng_free.get(engine_of(it), 0.0), t + HOP)

        def commit(it):
            st = start_time(it)
            e = engine_of(it)
            if it[0] == 'op':
                c = it[6] if it[6] is not None else 0.6
                fin = st + c
                eng_free[e] = fin
                reads, writes = it[3], it[4]
            else:
                eng_free[e] = st + 0.15
                fin = st + 2.5
                reads, writes = it[4], it[5]
            for k in reads:
                kr_t[k] = max(kr_t.get(k, 0.0), fin)
            for k in writes:
                kw_t[k] = fin
                kr_t[k] = 0.0
            out.append(it)

        m = qs.get('m', [])
        mpos = [0]
        carry = [None]

        def run_group(streams):
            pos = [0] * len(streams)
            sticky = carry[0]
            while True:
                cands = [(si, streams[si][pos[si]]) for si in range(len(streams)) if pos[si] < len(streams[si])]
                if not cands:
                    break
                if mpos[0] < len(m):
                    cands.append(('m', m[mpos[0]]))
                if sticky is not None and any(c[0] == sticky for c in cands):
                    pick = [c for c in cands if c[0] == sticky][0]
                else:
                    pick = min(cands, key=lambda c: (start_time(c[1]), 9 if c[0] == 'm' else c[0]))
                si, it = pick
                commit(it)
                if si == 'm':
                    mpos[0] += 1
                else:
                    pos[si] += 1
                if it[0] == 'op' and it[1] == 'pe':
                    sticky = None if it[5] else si
                if sticky is not None:
                    ended = (mpos[0] >= len(m)) if sticky == 'm' else (pos[sticky] >= len(streams[sticky]))
                    if ended:
                        sticky = None
            carry[0] = 'm' if sticky == 'm' else None

        if nw > 0:
            run_group([qs.get(('rp', 0), [])])
            for i in range(nw):
                run_group([qs.get(('rs', i), []), qs.get(('rp', i + 1), [])])
        while mpos[0] < len(m):
            commit(m[mpos[0]])
            mpos[0] += 1
        for it in out:
            if it[0] == 'op':
                self.op(*it[1:])
            else:
                self.dma(it[1], it[2], it[3], it[4], it[5], **it[6])

    def op(self, eng, fn, reads=(), writes=(), inc=True, cost=None):
        if self.stream is not None:
            self.queues.setdefault(self.stream, []).append(('op', eng, fn, tuple(reads), tuple(writes), inc, cost))
            return
        ex = [k for k in reads if self._is_psum(k)]
        self._waits(eng, self._deps(reads, list(writes) + ex))
        if inc:
            self.cnt[eng] += 1
            fn(self.engs[eng]).then_inc(self.sem[eng], 1)
            self._mark((eng, self.cnt[eng]), reads, writes)
            self.pend[eng] = False
        else:
            fn(self.engs[eng])
            self._mark((eng, self.cnt[eng] + 1), reads, writes)
            self.pend[eng] = True
        self.n += 1

    def dma(self, q, out, in_, reads=(), writes=(), **kw):
        if self.stream is not None:
            self.queues.setdefault(self.stream, []).append(('dma', q, out, in_, tuple(reads), tuple(writes), kw))
            return
        self._waits(q, self._deps(reads, writes))
        d = self.dq[q]
        i = d['next']
        d['next'] = (i + 1) % d['n']
        k = f"d_{q}_{i}"
        self.cnt[k] += 16
        self.engs[q].dma_start(out=out, in_=in_, **kw).then_inc(self.sem[k], 16)
        self._mark((k, self.cnt[k]), reads, writes)
        self.n += 1

    def barrier(self):
        assert not any(self.pend.values()), self.pend
        allv = {k: v for k, v in self.cnt.items() if v > 0}
        for e in self.engs:
            self._waits(e, dict(allv))

    def act(self, out, in_, func, r, w, bias=None, scale=None, accum=None):
        kw = {}
        if bias is not None:
            kw['bias'] = bias
        if scale is not None:
            kw['scale'] = scale
        if accum is not None:
            kw['accum_out'] = accum
        self.op('act', lambda e: e.activation(out=out, in_=in_, func=func, **kw), r, w, cost=self._ec('act', out))

    def tt(self, eng, out, a, b, op, r, w):
        self.op(eng, lambda e: e.tensor_tensor(out=out, in0=a, in1=b, op=op), r, w, cost=self._ec(eng, out))

    def ts(self, eng, out, a, s1, s2, op0, op1, r, w):
        if s2 is None:
            self.op(eng, lambda e: e.tensor_scalar(out=out, in0=a, scalar1=s1, scalar2=None, op0=op0), r, w,
                    cost=self._ec(eng, out))
        else:
            self.op(eng, lambda e: e.tensor_scalar(out=out, in0=a, scalar1=s1, scalar2=s2, op0=op0, op1=op1), r, w,
                    cost=self._ec(eng, out))

    def stt(self, out, a, s, b, op0, op1, r, w):
        self.op('dve', lambda e: e.scalar_tensor_tensor(out=out, in0=a, scalar=s, in1=b, op0=op0, op1=op1), r, w,
                cost=self._ec('dve', out))

    def mm(self, out, lhsT, rhs, start, stop, r, w, inc=None):
        self.op('pe', lambda e: e.matmul(out, lhsT=lhsT, rhs=rhs, start=start, stop=stop), r, w,
                inc=(stop if inc is None else inc), cost=0.06 + max(rhs.free_size(), 32) / 1200.0 * (4 if rhs.dtype == F32 else 1))

    def tr(self, out, in_, ident, r, w, inc=True):
        self.op('pe', lambda e: e.transpose(out, in_, ident), r, w, inc=inc,
                cost=0.06 + max(ident.free_size(), 32) / 1200.0 * (4 if in_.dtype == F32 else 1))

    def cp(self, eng, out, in_, r, w):
        if eng == 'act':
            self.op('act', lambda e: e.activation(out=out, in_=in_, func=AF.Identity), r, w, cost=self._ec('act', out))
        else:
            self.op(eng, lambda e: e.tensor_copy(out=out, in_=in_), r, w, cost=self._ec(eng, out))

    def memset(self, eng, ap, val, w):
        self.op(eng, lambda e: e.memset(ap, val), (), w, cost=self._ec(eng, ap))


def seqT(s):
    return TX if (s % 2) == 1 else TC


def build(debug=False, n_layers=L, stop_after=None, mixcfg=None):
    nc = bass.Bass("TRN2", target_bir_lowering=False)
    em = Em(nc)
    dbgset = debug if isinstance(debug, (set, list, tuple)) else None

    def din(name, shape, dt=F32):
        return nc.dram_tensor(name, list(shape), dt, kind="ExternalInput").ap()

    def dscr(name, shape, dt=F32):
        isdbg = (debug is True) or (dbgset is not None and name.rstrip('0123456789') in dbgset)
        return nc.dram_tensor(name, list(shape), dt, kind="ExternalOutput" if isdbg else "Internal").ap()

    xT = din("xT", [NBL, D, TX])
    ctxT = din("ctxT", [NBL, D, TC])
    cT = din("cT", [D, 3])
    w_mod = din("w_mod", [L, D, 6 * D])
    w_in = din("w_in", [L, D, 3504])
    w_out = din("w_out", [L, D, D])
    r_w2 = din("r_w2", [L, 2, 64, 512])
    r_a2 = din("r_a2", [L, 2, 64, 512])
    r_g2 = din("r_g2", [L, 160, 512])
    f_w_up = din("f_w_up", [L, D, 2 * DFF])
    f_w_down = din("f_w_down", [L, DFF, D])
    cst = din("cst", [128, NCST])
    pv = din("pv", [L, 128, NPV])
    pv64 = din("pv64", [L, 64, NPV64])
    rowv = din("rowv", [L, 1, NROW])
    rmu = din("rmu", [L, 1, 1952])
    outT = nc.dram_tensor("outT", [NBL, D, TX], F32, kind="ExternalOutput").ap()

    NS = 2 * NBL
    RESA = [dscr(f"resa{s}", [D, seqT(s)]) for s in range(NS)]
    RESB = [dscr(f"resb{s}", [D, seqT(s)]) for s in range(NS)]
    XBC = [dscr(f"xbc{s}", [D, seqT(s)]) for s in range(NS)]
    RKV = [dscr(f"rkv{s}", [24, 64, seqT(s)]) for s in range(NS)]
    XWA = [dscr(f"xwa{s}", [4, 64, seqT(s)]) for s in range(NS)]
    XG = [dscr(f"xg{s}", [160, seqT(s)]) for s in range(NS)]
    ZDT = [dscr(f"zdt{s}", [seqT(s), 528]) for s in range(NS)]
    YF = [dscr(f"yf{s}", [seqT(s), 512]) for s in range(NS)]
    OF = [dscr(f"of{s}", [seqT(s), 520]) for s in range(NS)]
    MIX = [dscr(f"mix{s}", [D, seqT(s)], BF16) for s in range(NS)]
    GATE = [dscr(f"gate{s}", [DFF, seqT(s)]) for s in range(NS)]
    VAL = [dscr(f"val{s}", [DFF, seqT(s)], BF16) for s in range(NS)]
    ACTV = [dscr(f"actv{s}", [DFF, seqT(s)], BF16) for s in range(NS)]

    def fm(ap):
        return ap.rearrange("(k p) t -> p k t", p=128)

    def sb(name, shape, dt=F32):
        return nc.alloc_sbuf_tensor(name, list(shape), dt).ap()

    CST = sb("CST", [128, NCST])
    CSTB = sb("CSTB", [128, 768], BF16)
    PV = sb("PV", [128, NPV])
    PV64 = sb("PV64", [64, NPV64])
    ROWB = sb("ROWB", [128, NROW])
    MOD = sb("MOD", [128, 48, 3])
    DER = sb("DER", [128, 4, 8, 3])
    NEGA = sb("NEGA", [128, 16])
    OMMU = sb("OMMU", [64, 8])
    em.dma('sp', CST, cst, writes=['CST'])
    em.cp('dve', CSTB, CST[:, 0:768], ['CST'], ['CSTB'])
    ident = CST[:, C_ID:C_ID + 128]
    identb = CSTB[:, C_ID:C_ID + 128]
    onesb = CSTB[:, C_ONE:C_ONE + 128]
    ones = CST[:, C_ONE:C_ONE + 128]

    def stage_scope():
        return ExitStack()

    def stage_mod(l):
        em.dma('sp', PV, pv[l], writes=['PV'])
        em.dma('sp', PV64, pv64[l], writes=['PV64'])
        em.dma('pool', ROWB, rowv[l].partition_broadcast(128), writes=['ROWB'])
        with ExitStack() as es:
            def S(name, shape, dt=F32):
                return es.enter_context(nc.sbuf_tensor(name, list(shape), dt)).ap()
            cts = S(f"cts{l}", [128, 8, 3])
            sc = S(f"sc{l}", [128, 8, 3])
            wst = [S(f"wmst{l}_{i}", [128, 8, 512]) for i in range(2)]
            ps = es.enter_context(nc.psum_tensor(f"psmod{l}", [128, 512], F32)).ap()
            em.dma('sp', cts, cT.rearrange("(k p) j -> p k j", p=128), writes=['cts'])
            em.act(sc, cts, AF.Silu, ['cts'], ['sc'])
            for g in range(12):
                w = wst[g % 2]
                wk = f"wmst{g % 2}"
                em.dma('sp' if g % 2 == 0 else 'pool', w,
                       w_mod[l][:, g * 512:(g + 1) * 512].rearrange("(k p) n -> p k n", p=128), writes=[wk])
                for mi in range(4):
                    m = g * 4 + mi
                    for k in range(8):
                        em.mm(ps[:, m * 3:(m + 1) * 3], w[:, k, mi * 128:(mi + 1) * 128], sc[:, k, :],
                              k == 0, k == 7, [wk, 'sc'], ['psmod'])
            em.tt('dve', MOD, ps[:, 0:144].rearrange("p (m j) -> p m j", j=3),
                  PV[:, PV_BMOD:PV_BMOD + 48].unsqueeze(2).to_broadcast([128, 48, 3]), ALU.add,
                  ['psmod', 'PV'], ['MOD'])
            tmp = S(f"dertmp{l}", [128, 8, 3])

            def gain(idx, goff, mlo, plus1):
                if plus1:
                    em.ts('dve', tmp, MOD[:, mlo:mlo + 8, :], 1.0, None, ALU.add, None, ['MOD'], ['dertmp'])
                    src = tmp
                    rk = ['dertmp', 'PV']
                else:
                    src = MOD[:, mlo:mlo + 8, :]
                    rk = ['MOD', 'PV']
                em.tt('dve', DER[:, idx, :, :], src,
                      PV[:, goff:goff + 8].unsqueeze(2).to_broadcast([128, 8, 3]), ALU.mult, rk, ['DER'])
            gain(0, PV_GPRE1, 8, True)
            gain(1, PV_GPOST1, 16, False)
            gain(2, PV_GPRE2, 32, True)
            gain(3, PV_GPOST2, 40, False)
            em.act(NEGA, ROWB[:, RV_ALOG:RV_ALOG + 16], AF.Exp, ['ROWB'], ['NEGA'])
            em.ts('dve', NEGA, NEGA, -1.0, None, ALU.mult, None, ['NEGA'], ['NEGA'])
            em.ts('dve', OMMU, PV64[:, P6_KA:P6_KA + 8], -1.0, 1.0, ALU.mult, ALU.add, ['PV64'], ['OMMU'])
            em.barrier()

    def prenorm(xt, TW, h, sq, rstd, ps, jmod, gidx, sidx, kx, kh, kps, sqk='sq', rsk='rstd'):
        em.act(sq[:, :, :TW], xt[:, :, :TW], AF.Square, [kx], [sqk])
        for k in range(8):
            em.mm(ps[:, :TW], onesb, sq[:, k, :TW], k == 0, k == 7, [sqk, 'CSTB'], [kps])
        em.act(rstd[:, :TW], ps[:, :TW], AF.Sqrt, [kps], [rsk], bias=EPS, scale=1.0 / D)
        em.op('dve', lambda e: e.reciprocal(out=rstd[:, :TW], in_=rstd[:, :TW]), [rsk], [rsk])
        for k in range(8):
            em.stt(xt[:, k, :TW], xt[:, k, :TW], DER[:, gidx, k, jmod:jmod + 1], rstd[:, :TW], ALU.mult, ALU.mult,
                   [kx, rsk, 'DER'], [kx])
            em.act(h[:, k, :TW], xt[:, k, :TW], AF.Identity, [kx, 'MOD'], [kh],
                   bias=MOD[:, sidx + k, jmod:jmod + 1], scale=1.0)

    def load_wbf(es, l, wap, Kc, N, name, piece=None):
        wb = es.enter_context(nc.sbuf_tensor(f"{name}bf{l}", [128, Kc, N], BF16)).ap()
        with ExitStack() as e2:
            sts = [e2.enter_context(nc.sbuf_tensor(f"{name}st{l}_{i}", [128, N], F32)).ap() for i in range(2)]
            for k in range(Kc):
                st = sts[k % 2]
                sk = f"{name}st{k % 2}"
                em.dma('sp' if k % 2 == 0 else 'pool', st, wap[k * 128:(k + 1) * 128, :], writes=[sk])
                em.cp('act' if k % 2 == 0 else 'dve', wb[:, k, :], st, [sk], [name + 'bf'])
            em.barrier()
        return wb

    def res_src(l, s, phase):
        b = s // 2
        if phase == 0:
            if l == 0:
                return (xT[b] if s % 2 == 1 else ctxT[b]), f"in{s}"
            return RESB[s], f"resb{s}"
        return RESA[s], f"resa{s}"

    def res_dst(l, s, phase):
        b = s // 2
        if phase == 0:
            return RESA[s], f"resa{s}"
        if l == n_layers - 1 and s % 2 == 1:
            return outT[b], f"out{s}"
        return RESB[s], f"resb{s}"

    def stage_inproj(l):
        with ExitStack() as es:
            def S(name, shape, dt=F32):
                return es.enter_context(nc.sbuf_tensor(name, list(shape), dt)).ap()
            wb = load_wbf(es, l, w_in[l], 8, 3504, "win")
            w2 = S(f"ipw2{l}", [128, 8, 1952], BF16)
            with ExitStack() as e2:
                mub = e2.enter_context(nc.sbuf_tensor(f"ipmub{l}", [128, 1952], F32)).ap()
                em.dma('sp', mub, rmu[l].partition_broadcast(128), writes=['ipmub'])
                for k in range(8):
                    em.stt(w2[:, k, :], mub, 0.5, wb[:, k, 1552:3504], ALU.mult, ALU.mult, ['ipmub', 'winbf'], ['ipw2'])
                em.ts('dve', mub, mub, -1.0, 1.0, ALU.mult, ALU.add, ['ipmub'], ['ipmub'])
                for k in range(8):
                    em.tt('dve', wb[:, k, 1552:3504], wb[:, k, 1552:3504], mub, ALU.mult, ['ipmub', 'winbf', 'ipw2'], ['winbf'])
                em.barrier()
            xt = S(f"ipx{l}", [128, 8, 512])
            h = S(f"iph{l}", [128, 8, 512], BF16)
            sq = S(f"ipsq{l}", [128, 8, 512], BF16)
            hs = sq
            rstd = S(f"iprs{l}", [128, 512])
            xh = S(f"ipxh{l}", [128, 8, 2])
            hh = S(f"iphh{l}", [128, 8, 2], BF16)
            sqh = S(f"ipsqh{l}", [128, 8, 2], BF16)
            rstdh = S(f"iprsh{l}", [128, 2])
            sta = [S(f"ipsta{l}_{i}", [128, 8, 512]) for i in range(2)]
            stw = S(f"ipstw{l}", [64, 4, 512])
            stg0 = S(f"ipstg0{l}", [128, 512])
            stg1 = S(f"ipstg1{l}", [32, 512])
            stz = S(f"ipstz{l}", [128, 4, 528])
            pss = [es.enter_context(nc.psum_tensor(f"ipps{l}_{i}", [128, 512], F32)).ap() for i in range(8)]
            groups = [('xbc', 512, 128, 8), ('r', 1552, 64, 8), ('k', 2064, 64, 8), ('v', 2576, 64, 8)]
            ev = 0
            for s in range(NS):
                T = seqT(s)
                TW = min(512, T)
                jmod = 2 if s % 2 == 0 else s // 2
                src, srck = res_src(l, s, 0)
                for tt_ in range(T // TW):
                    t0 = tt_ * TW
                    em.dma('sp', xt[:, :, :TW], fm(src)[:, :, t0:t0 + TW], reads=[srck], writes=['ipx'])
                    cl = max(t0 - 1, 0)
                    cr = min(t0 + TW, T - 1)
                    em.dma('sp', xh[:, :, 0:1], fm(src)[:, :, cl:cl + 1], reads=[srck], writes=['ipxh'], allow_slow_non_contiguous=True)
                    em.dma('sp', xh[:, :, 1:2], fm(src)[:, :, cr:cr + 1], reads=[srck], writes=['ipxh'], allow_slow_non_contiguous=True)
                    prenorm(xt, TW, h, sq, rstd, pss[7], jmod, 0, 0, 'ipx', 'iph', 'ps7')
                    prenorm(xh, 2, hh, sqh, rstdh, pss[6], jmod, 0, 0, 'ipxh', 'iphh', 'ps6', sqk='ipsqh', rsk='iprsh')
                    if t0 == 0:
                        em.memset('pool', hh[:, :, 0:1], 0.0, ['iphh'])
                    if t0 + TW == T:
                        em.memset('pool', hh[:, :, 1:2], 0.0, ['iphh'])
                    em.tt('dve', hs[:, :, 1:TW - 1], h[:, :, 0:TW - 2], h[:, :, 2:TW], ALU.add, ['iph', 'sq'], ['sq'])
                    em.tt('dve', hs[:, :, 0:1], hh[:, :, 0:1], h[:, :, 1:2], ALU.add, ['iph', 'iphh', 'sq'], ['sq'])
                    em.tt('dve', hs[:, :, TW - 1:TW], h[:, :, TW - 2:TW - 1], hh[:, :, 1:2], ALU.add, ['iph', 'iphh', 'sq'], ['sq'])
                    pi = 0
                    for gi, (gname, c0, wdt, nb) in enumerate(groups):
                        st = sta[gi % 2]
                        stk = f"ipsta{gi % 2}"
                        for j in range(nb):
                            ps = pss[pi % 6]
                            pk = f"ps{pi % 6}"
                            pi += 1
                            cc = c0 + j * wdt
                            shifted = (gname != 'xbc')
                            for k in range(8):
                                em.mm(ps[:wdt, :TW], wb[:, k, cc:cc + wdt], h[:, k, :TW], k == 0, (k == 7) and not shifted,
                                      ['winbf', 'iph'], [pk])
                            if shifted:
                                for k in range(8):
                                    em.mm(ps[:wdt, :TW], w2[:, k, cc - 1552:cc - 1552 + wdt], hs[:, k, :TW], False, k == 7,
                                          ['ipw2', 'sq'], [pk])
                            em.cp('act' if ev % 2 == 0 else 'dve', st[:wdt, j, :TW], ps[:wdt, :TW], [pk], [stk])
                            ev += 1
                        if gname == 'xbc':
                            em.dma('pool', fm(XBC[s])[:, :, t0:t0 + TW], st[:, :, :TW], reads=[stk], writes=[f"xbc{s}"])
                        else:
                            jb = {'r': 0, 'k': 8, 'v': 16}[gname]
                            em.dma('pool', RKV[s][jb:jb + 8].rearrange("j p t -> p j t")[:, :, t0:t0 + TW],
                                   st[:64, :, :TW], reads=[stk], writes=[f"rkv{s}"])
                    for j in range(4):
                        ps = pss[pi % 6]
                        pk = f"ps{pi % 6}"
                        pi += 1
                        cc = 3088 + j * 64
                        for k in range(8):
                            em.mm(ps[:64, :TW], wb[:, k, cc:cc + 64], h[:, k, :TW], k == 0, False, ['winbf', 'iph'], [pk])
                        for k in range(8):
                            em.mm(ps[:64, :TW], w2[:, k, cc - 1552:cc - 1552 + 64], hs[:, k, :TW], False, k == 7, ['ipw2', 'sq'], [pk])
                        em.cp('act' if ev % 2 == 0 else 'dve', stw[:, j, :TW], ps[:64, :TW], [pk], ['ipstw'])
                        ev += 1
                    em.dma('pool', XWA[s].rearrange("j p t -> p j t")[:, :, t0:t0 + TW], stw[:, :, :TW],
                           reads=['ipstw'], writes=[f"xwa{s}"])
                    for (cc, wdt, st, stk, r0) in [(3344, 128, stg0, 'ipstg0', 0), (3472, 32, stg1, 'ipstg1', 128)]:
                        ps = pss[pi % 6]
                        pk = f"ps{pi % 6}"
                        pi += 1
                        for k in range(8):
                            em.mm(ps[:wdt, :TW], wb[:, k, cc:cc + wdt], h[:, k, :TW], k == 0, False, ['winbf', 'iph'], [pk])
                        for k in range(8):
                            em.mm(ps[:wdt, :TW], w2[:, k, cc - 1552:cc - 1552 + wdt], hs[:, k, :TW], False, k == 7, ['ipw2', 'sq'], [pk])
                        em.cp('act' if ev % 2 == 0 else 'dve', st[:wdt, :TW], ps[:wdt, :TW], [pk], [stk])
                        ev += 1
                        em.dma('pool', XG[s][r0:r0 + wdt, t0:t0 + TW], st[:wdt, :TW], reads=[stk], writes=[f"xg{s}"])
                    for i in range(TW // 128):
                        ps = pss[pi % 6]
                        pk = f"ps{pi % 6}"
                        pi += 1
                        ps2 = pss[6]
                        for k in range(8):
                            em.mm(ps[:, 0:512], h[:, k, i * 128:(i + 1) * 128], wb[:, k, 0:512], k == 0, k == 7,
                                  ['winbf', 'iph'], [pk])
                        for k in range(8):
                            em.mm(ps2[:, 0:16], h[:, k, i * 128:(i + 1) * 128], wb[:, k, 1536:1552], k == 0, k == 7,
                                  ['winbf', 'iph'], ['ps6'])
                        em.cp('act', stz[:, i, 0:512], ps[:, 0:512], [pk], ['ipstz'])
                        em.cp('dve', stz[:, i, 512:528], ps2[:, 0:16], ['ps6'], ['ipstz'])
                    em.dma('pool', ZDT[s][t0:t0 + TW, :].rearrange("(i p) c -> p i c", p=128), stz[:, :TW // 128, :],
                           reads=['ipstz'], writes=[f"zdt{s}"])
            em.barrier()

    def stage_mixer(l, need_ctx_out):
        with ExitStack() as es:
            def S(name, shape, dt=F32):
                return es.enter_context(nc.sbuf_tensor(name, list(shape), dt)).ap()
            w2b = S(f"w2b{l}", [64, 2, 512], BF16)
            a2b = S(f"a2b{l}", [64, 2, 512], BF16)
            g2b0 = S(f"g2b0{l}", [128, 512], BF16)
            g2b1 = S(f"g2b1{l}", [32, 512], BF16)
            with ExitStack() as e2:
                t1 = e2.enter_context(nc.sbuf_tensor(f"lst1{l}", [64, 2, 512], F32)).ap()
                t2 = e2.enter_context(nc.sbuf_tensor(f"lst2{l}", [64, 2, 512], F32)).ap()
                t3 = e2.enter_context(nc.sbuf_tensor(f"lst3{l}", [128, 512], F32)).ap()
                t4 = e2.enter_context(nc.sbuf_tensor(f"lst4{l}", [32, 512], F32)).ap()
                em.dma('sp', t1, r_w2[l].rearrange("d r c -> r d c"), writes=['lst1'])
                em.dma('sp', t2, r_a2[l].rearrange("d r c -> r d c"), writes=['lst2'])
                em.dma('sp', t3, r_g2[l][0:128, :], writes=['lst3'])
                em.dma('sp', t4, r_g2[l][128:160, :], writes=['lst4'])
                em.cp('dve', w2b, t1, ['lst1'], ['w2b'])
                em.cp('dve', a2b, t2, ['lst2'], ['a2b'])
                em.cp('dve', g2b0, t3, ['lst3'], ['g2b'])
                em.cp('dve', g2b1, t4, ['lst4'], ['g2b'])
                em.barrier()
            MAR = [None, None]
            MARt = S(f"mar{l}", [128, 2, 256])
            em.cp('dve', MARt[:, 0, 0:128], CST[:, C_UTS:C_UTS + 128], ['CST'], ['MAR'])
            em.cp('dve', MARt[:, 0, 128:256], CST[:, C_UTI:C_UTI + 128], ['CST'], ['MAR'])
            em.cp('dve', MARt[:, 1, 0:128], CST[:, C_LTS:C_LTS + 128], ['CST'], ['MAR'])
            em.cp('dve', MARt[:, 1, 128:256], CST[:, C_LTI:C_LTI + 128], ['CST'], ['MAR'])
            MSO = S(f"mso{l}", [128, 2, 256])
            em.cp('dve', MSO[:, 0, 0:128], CST[:, C_LTS:C_LTS + 128], ['CST'], ['MSO'])
            em.cp('dve', MSO[:, 1, 0:128], CST[:, C_UTS:C_UTS + 128], ['CST'], ['MSO'])
            em.cp('dve', MSO[:, 0, 128:256], ones, ['CST'], ['MSO'])
            em.cp('dve', MSO[:, 1, 128:256], ones, ['CST'], ['MSO'])
            RMK = S(f"rmk{l}", [64, 8, 128], BF16)
            em.memset('pool', RMK, 1.0, ['RMK'])
            em.memset('pool', RMK[:, :, 0:1], 0.0, ['RMK'])

            def MI(d):
                return CST[:, C_UTI:C_UTI + 128] if d == 0 else CST[:, C_LTI:C_LTI + 128]

            def strictTS(d):
                return CST[:, C_LTS:C_LTS + 128] if d == 0 else CST[:, C_UTS:C_UTS + 128]

            xbc = S(f"m_xbc{l}", [128, 8, 130])
            cacc = S(f"m_cacc{l}", [128, 8, 128])
            bct = S(f"m_bct{l}", [128, 4, 128], BF16)
            xtok = S(f"m_xtok{l}", [128, 512])
            xtokb = S(f"m_xtokb{l}", [128, 512], BF16)
            btokb = S(f"m_btokb{l}", [128, 256], BF16)
            zdt = S(f"m_zdt{l}", [128, 528])
            dts = S(f"m_dts{l}", [128, 8])
            dta = S(f"m_dta{l}", [128, 8])
            sm = S(f"m_sm{l}", [128, 40])
            xw = S(f"m_xw{l}", [128, 512], BF16)
            cbt = S(f"m_cbt{l}", [128, 256])
            l2 = [S(f"m_l2{l}_{i}", [128, 256]) for i in range(2)]
            Et = [S(f"m_E{l}_{i}", [128, 256]) for i in range(2)]
            LTt = [S(f"m_LT{l}_{i}", [128, 128]) for i in range(2)]
            STt = [S(f"m_ST{l}_{i}", [128, 128], BF16) for i in range(2)]
            CsT = [S(f"m_Cs{l}_{i}", [128, 128], BF16) for i in range(2)]
            hst = S(f"m_hst{l}", [128, 512])
            hstb = S(f"m_hstb{l}", [128, 512], BF16)
            yt = S(f"m_y{l}", [128, 512])
            yf = S(f"m_yf{l}", [128, 512])
            zs = S(f"m_zs{l}", [128, 512])
            ysq = S(f"m_ysq{l}", [128, 512])
            gst = S(f"m_gst{l}", [128, 4])
            mixo = S(f"mixo{l}", [128, 8, 128], BF16)
            ssum = S(f"r_ssum{l}", [64, 26, 128])
            pp = ssum
            xg0 = S(f"r_xg0{l}", [128, 130])
            xg1 = S(f"r_xg1{l}", [32, 130])
            sg0 = S(f"r_sg0{l}", [128, 128], BF16)
            sg1 = S(f"r_sg1{l}", [32, 128], BF16)
            xgt = S(f"r_xgt{l}", [128, 128])
            twb = S(f"r_twb{l}", [64, 2, 128], BF16)
            lw = S(f"r_lw{l}", [64, 8, 128])
            aa = S(f"r_aa{l}", [64, 8, 128])
            kk = S(f"r_kk{l}", [64, 8, 128])
            kd = S(f"r_kd{l}", [64, 8, 128])
            linc = S(f"r_linc{l}", [64, 8, 128])
            lex = S(f"r_lex{l}", [64, 8, 128])
            rinv = lex
            e1 = S(f"r_e1{l}", [64, 8, 128])
            e0 = S(f"r_e0{l}", [64, 8, 128])
            ei = S(f"r_ei{l}", [64, 8, 128])
            gCs = [S(f"r_gC{l}_{i}", [64, 8]) for i in range(2)]
            coefs = [S(f"r_coef{l}_{i}", [128, 8]) for i in range(2)]
            tmpk = S(f"r_tmpk{l}", [64, 8, 128])
            ARs = [S(f"r_AR{l}_{i}", [64, 8, 256], BF16) for i in range(2)]
            BKs = [S(f"r_BK{l}_{i}", [64, 8, 2, 128], BF16) for i in range(2)]
            prodb = S(f"r_prod{l}", [64, 8, 128], BF16)
            sqb = prodb
            vtokbs = [S(f"r_vtokb{l}_{i}", [128, 512], BF16) for i in range(2)]
            BKtoks = [S(f"r_BKtok{l}_{i}", [128, 8, 2, 64], BF16) for i in range(2)]
            GBs = [S(f"r_GB{l}_{i}", [128, 8, 256], BF16) for i in range(2)]
            GKs = [S(f"r_GK{l}_{i}", [128, 8, 256], BF16) for i in range(2)]
            Q0s = [S(f"r_Q0{l}_{i}", [128, 8, 128], BF16) for i in range(2)]
            Pm = [S(f"r_P{l}_{i}", [128, 8, 128], BF16) for i in range(2)]
            Qm = [S(f"r_Q{l}_{i}", [128, 8, 128], BF16) for i in range(2)]
            Ym = [S(f"r_Y{l}_{i}", [128, 8, 128], BF16) for i in range(2)]
            E1m = S(f"r_E1{l}", [128, 8, 128], BF16)
            E2m = S(f"r_E2{l}", [128, 8, 128], BF16)
            Dm = S(f"r_D{l}", [128, 8, 128], BF16)
            Zm = S(f"r_Z{l}", [128, 8, 128], BF16)
            Wt = S(f"r_W{l}", [128, 512], BF16)
            Ut = S(f"r_U{l}", [128, 512], BF16)
            Hs = S(f"r_H{l}", [64, 512])
            Hb = S(f"r_Hb{l}", [64, 512], BF16)
            ot = S(f"r_o{l}", [128, 520])
            oft = S(f"r_of{l}", [128, 520])
            osq = S(f"r_osq{l}", [128, 512])
            gn = S(f"r_gn{l}", [128, 40])
            B = [es.enter_context(nc.psum_tensor(f"mxps{l}_{i}", [128, 512], F32)).ap() for i in range(7)]
            BBp = es.enter_context(nc.psum_tensor(f"mxpsb{l}", [128, 1024], BF16)).ap()

            nbw = ROWB[:, RV_MNW:RV_MNW + 512]
            lnw = ROWB[:, RV_LNW:RV_LNW + 512]
            lnb = ROWB[:, RV_LNB:RV_LNB + 512]
            mdb = ROWB[:, RV_MD:RV_MD + 8]
            evc = [0]

            def evq():
                evc[0] += 1
                return 'act' if evc[0] % 2 == 0 else 'dve'

            def load_halo(tile, tk, srcap, srck, t0, T, nblk_dims):
                lo = max(t0 - 1, 0)
                hi = min(t0 + 129, T)
                o = lo - (t0 - 1)
                if nblk_dims:
                    em.dma('sp', tile[:, :, o:o + hi - lo], srcap[:, :, lo:hi], reads=[srck], writes=[tk])
                    if t0 == 0:
                        em.memset('pool', tile[:, :, 0:1], 0.0, [tk])
                    if t0 + 128 == T:
                        em.memset('pool', tile[:, :, 129:130], 0.0, [tk])
                else:
                    em.dma('sp', tile[:, o:o + hi - lo], srcap[:, lo:hi], reads=[srck], writes=[tk])
                    if t0 == 0:
                        em.memset('pool', tile[:, 0:1], 0.0, [tk])
                    if t0 + 128 == T:
                        em.memset('pool', tile[:, 129:130], 0.0, [tk])

            def mamba_chunk(s, t0, d, emit_out):
                T = seqT(s)
                load_halo(xbc, 'm_xbc', fm(XBC[s]), f"xbc{s}", t0, T, True)
                em.dma('pool', zdt, ZDT[s][t0:t0 + 128, :], reads=[f"zdt{s}"], writes=['m_zdt'])
                if (mixcfg or {}).get('mstop', 99) <= 1:
                    return
                for j in range(8):
                    em.ts('dve', cacc[:, j, :], xbc[:, j, 1:129], PV[:, PV_MCW + 8 + j:PV_MCW + 9 + j],
                          PV[:, PV_MCB + j:PV_MCB + j + 1], ALU.mult, ALU.add, ['m_xbc', 'PV'], ['m_cacc'])
                    em.stt(cacc[:, j, :], xbc[:, j, 0:128], PV[:, PV_MCW + j:PV_MCW + j + 1], cacc[:, j, :],
                           ALU.mult, ALU.add, ['m_xbc', 'PV', 'm_cacc'], ['m_cacc'])
                    em.stt(cacc[:, j, :], xbc[:, j, 2:130], PV[:, PV_MCW + 16 + j:PV_MCW + 17 + j], cacc[:, j, :],
                           ALU.mult, ALU.add, ['m_xbc', 'PV', 'm_cacc'], ['m_cacc'])
                em.act(cacc[:, 0:6, :], cacc[:, 0:6, :], AF.Silu, ['m_cacc'], ['m_cacc'])
                em.act(bct[:, 2:4, :], cacc[:, 6:8, :], AF.Silu, ['m_cacc'], ['m_bct'])
                em.cp('dve', bct[:, 0:2, :], cacc[:, 4:6, :], ['m_cacc'], ['m_bct'])
                if (mixcfg or {}).get('mstop', 99) <= 2:
                    return
                msub = (mixcfg or {}).get('msub', 9)
                for j in range(4):
                    em.tr(B[4][:, j * 128:(j + 1) * 128], cacc[:, j, :], ident, ['m_cacc', 'CST'], ['B4'], inc=(j == 3))
                if msub >= 1:
                    for j in range(2):
                        em.tr(B[5][:, j * 128:(j + 1) * 128], cacc[:, 4 + j, :], ident, ['m_cacc', 'CST'], ['B5'])
                if msub >= 2 and msub != 33:
                    em.cp('dve', xtok, B[4], ['B4'], ['m_xtok'])
                if msub == 30:
                    em.cp('dve', xtokb, B[4], ['B4'], ['m_xtokb'])
                elif msub == 31:
                    em.cp('act', yt, B[4], ['B4'], ['m_y'])
                elif msub == 32:
                    em.cp('act', xtokb, xtok, ['m_xtok'], ['m_xtokb'])
                elif msub >= 3:
                    em.cp('act', xtokb, B[4], ['B4'], ['m_xtokb'])
                if msub >= 4:
                    em.cp('act', btokb, B[5][:, 0:256], ['B5'], ['m_btokb'])
                if (mixcfg or {}).get('mstop', 99) <= 3:
                    return
                em.tt('dve', dts, zdt[:, 512 + d * 8:520 + d * 8], ROWB[:, RV_DTB + d * 8:RV_DTB + d * 8 + 8], ALU.add,
                      ['m_zdt', 'ROWB'], ['m_dts'])
                em.act(dts, dts, AF.Exp, ['m_dts'], ['m_dts'])
                em.act(dts, dts, AF.Ln, ['m_dts'], ['m_dts'], bias=1.0, scale=1.0)
                em.tt('dve', dta, dts, NEGA[:, d * 8:d * 8 + 8], ALU.mult, ['m_dts', 'NEGA'], ['m_dta'])
                if (mixcfg or {}).get('mstop', 99) <= 4:
                    return
                em.mm(B[5][:, 256:264], MI(d), dta, True, True, ['CST', 'm_dta'], ['B5'])
                em.mm(B[5][:, 264:272], ones, dta, True, True, ['CST', 'm_dta'], ['B5'])
                em.cp('act', sm[:, 32:40], B[5][:, 264:272], ['B5'], ['m_sm'])
                em.tt('dve', sm[:, 24:32], sm[:, 32:40], B[5][:, 256:264], ALU.subtract, ['B5', 'm_sm'], ['m_sm'])
                em.act(sm[:, 0:8], sm[:, 24:32], AF.Exp, ['m_sm'], ['m_sm'])
                em.tt('dve', sm[:, 8:16], sm[:, 0:8], dts, ALU.mult, ['m_sm', 'm_dts'], ['m_sm'])
                em.act(sm[:, 16:24], B[5][:, 264:272], AF.Exp, ['B5'], ['m_sm'])
                em.tt('dve', xw.rearrange("p (h q) -> p h q", q=64), xtok.rearrange("p (h q) -> p h q", q=64),
                      sm[:, 8:16].unsqueeze(2).to_broadcast([128, 8, 64]), ALU.mult, ['m_xtok', 'm_sm'], ['m_xw'])
                if (mixcfg or {}).get('mstop', 99) <= 5:
                    return
                for g in range(2):
                    em.mm(B[5][:, g * 128:(g + 1) * 128], bct[:, g, :], bct[:, 2 + g, :], True, True, ['m_bct'], ['B5'], inc=(g == 1))
                em.cp('act', cbt, B[5][:, 0:256], ['B5'], ['m_cbt'])
                for h in range(8):
                    g = h // 4
                    i2 = h % 2
                    pe_ = B[5][:, 256:512] if i2 == 0 else B[5][:, 0:256]
                    pek = 'B5'
                    em.ts('dve', l2[i2], MSO[:, d, :], dta[:, h:h + 1], None, ALU.mult, None,
                          ['MSO', 'm_dta'], [f"m_l2{i2}"])
                    em.mm(pe_[:, 0:128], l2[i2][:, 0:128], MI(d), True, True, [f"m_l2{i2}", 'CST'], [pek])
                    em.mm(pe_[:, 128:256], l2[i2][:, 128:256], MI(d), True, True, [f"m_l2{i2}", 'CST'], [pek])
                    em.act(Et[i2], pe_[:, 0:256], AF.Exp, [pek], [f"m_E{i2}"])
                    em.stt(LTt[i2], Et[i2][:, 0:128], dts[:, h:h + 1], MI(d), ALU.mult, ALU.mult,
                           [f"m_E{i2}", 'm_dts', 'CST'], [f"m_LT{i2}"])
                    em.tt('dve', STt[i2], cbt[:, g * 128:(g + 1) * 128], LTt[i2], ALU.mult, ['m_cbt', f"m_LT{i2}"],
                          [f"m_ST{i2}"])
                    em.tt('dve', CsT[i2], bct[:, 2 + g, :], Et[i2][:, 128:256], ALU.mult, ['m_bct', f"m_E{i2}"],
                          [f"m_Cs{i2}"])
                    em.mm(B[4][:, h * 64:(h + 1) * 64], STt[i2], xtokb[:, h * 64:(h + 1) * 64], True, False,
                          [f"m_ST{i2}", 'm_xtokb'], ['B4'])
                    em.mm(B[4][:, h * 64:(h + 1) * 64], CsT[i2], hstb[:, h * 64:(h + 1) * 64], False, True,
                          [f"m_Cs{i2}", 'm_hstb'], ['B4'])
                if (mixcfg or {}).get('mstop', 99) <= 6:
                    return
                for g in range(2):
                    em.mm(B[5][:, g * 256:(g + 1) * 256], btokb[:, g * 128:(g + 1) * 128], xw[:, g * 256:(g + 1) * 256],
                          True, True, ['m_btokb', 'm_xw'], ['B5'], inc=(g == 1))
                em.tt('dve', hst.rearrange("p (h q) -> p h q", q=64), hst.rearrange("p (h q) -> p h q", q=64),
                      sm[:, 16:24].unsqueeze(2).to_broadcast([128, 8, 64]), ALU.mult, ['m_hst', 'm_sm', 'B4'], ['m_hst'])
                em.tt('dve', hst, hst, B[5], ALU.add, ['m_hst', 'B5'], ['m_hst'])
                em.cp('act', hstb, hst, ['m_hst', 'B4'], ['m_hstb'])
                if (mixcfg or {}).get('mstop', 99) <= 7:
                    return
                if d == 0:
                    if emit_out:
                        em.cp('act', yt, B[4], ['B4'], ['m_y'])
                        em.dma('pool', YF[s][t0:t0 + 128, :], yt, reads=['m_y'], writes=[f"yf{s}"])
                elif emit_out:
                    em.dma('sp', yf, YF[s][t0:t0 + 128, :], reads=[f"yf{s}"], writes=['m_yf'])
                    em.tt('dve', yt, B[4], yf, ALU.add, ['B4', 'm_yf'], ['m_y'])
                    em.tt('dve', yf.rearrange("p (h q) -> p h q", q=64), xtok.rearrange("p (h q) -> p h q", q=64),
                          mdb.unsqueeze(2).to_broadcast([128, 8, 64]), ALU.mult, ['m_xtok', 'ROWB', 'm_yf'], ['m_yf'])
                    em.tt('dve', yt, yt, yf, ALU.add, ['m_y', 'm_yf'], ['m_y'])
                    em.act(zs, zdt[:, 0:512], AF.Silu, ['m_zdt'], ['m_zs'])
                    em.tt('dve', yt, yt, zs, ALU.mult, ['m_y', 'm_zs'], ['m_y'])
                    for g in range(2):
                        em.act(ysq[:, g * 256:(g + 1) * 256], yt[:, g * 256:(g + 1) * 256], AF.Square, ['m_y'],
                               ['m_ysq', 'm_gst'], accum=gst[:, g:g + 1])
                    em.act(gst[:, 2:4], gst[:, 0:2], AF.Sqrt, ['m_gst'], ['m_gst'], bias=EPS, scale=1.0 / 256)
                    em.op('dve', lambda e: e.reciprocal(out=gst[:, 2:4], in_=gst[:, 2:4]), ['m_gst'], ['m_gst'])
                    for g in range(2):
                        em.stt(yt[:, g * 256:(g + 1) * 256], yt[:, g * 256:(g + 1) * 256], gst[:, 2 + g:3 + g],
                               nbw[:, g * 256:(g + 1) * 256], ALU.mult, ALU.mult, ['m_y', 'm_gst', 'ROWB'], ['m_y'])
                    for j in range(4):
                        em.tr(B[5][:, j * 128:(j + 1) * 128], yt[:, j * 128:(j + 1) * 128], ident, ['m_y', 'CST'], ['B5'], inc=(j == 3))
                    em.cp('act', mixo[:, 0:4, :], B[5].rearrange("p (j t) -> p j t", t=128), ['B5'], ['mixo_m'])
                    em.dma('pool', fm(MIX[s])[:, 0:4, t0:t0 + 128], mixo[:, 0:4, :], reads=['mixo_m'], writes=[f"mixm{s}"])

            def wkv_chunk(s, t0, d, emit_out, idx=0, first_of_d=False, split=True):
                T = seqT(s)
                pb = idx % 2
                AR, BK, GB, GK, Q0, BKtok, vtokb, gC, coef = (ARs[pb], BKs[pb], GBs[pb], GKs[pb], Q0s[pb], BKtoks[pb],
                                                             vtokbs[pb], gCs[pb], coefs[pb])
                kAR, kBK, kGB, kGK, kQ0, kBKtok, kvtokb, kgC, kcoef = [f"{n}{pb}" for n in
                                                                      ('r_AR', 'r_BK', 'r_GB', 'r_GK', 'r_Q0h', 'r_BKtok',
                                                                       'r_vtokb', 'r_gC', 'r_coef')]
                if split:
                    em.stream = ('rp', idx)
                fin = (d == 1 and emit_out)
                em.dma('sp', pp[:, 0:24, :], RKV[s].rearrange("j p t -> p j t")[:, :, t0:t0 + 128], reads=[f"rkv{s}"], writes=['r_ssum'])
                em.dma('sp', pp[:, 24:25, :], XWA[s][d:d + 1].rearrange("j p t -> p j t")[:, :, t0:t0 + 128], reads=[f"xwa{s}"],
                       writes=['r_ssum'])
                em.dma('sp', pp[:, 25:26, :], XWA[s][2 + d:3 + d].rearrange("j p t -> p j t")[:, :, t0:t0 + 128], reads=[f"xwa{s}"],
                       writes=['r_ssum'])
                rr = pp[:, 0:8, :]
                kr = pp[:, 8:16, :]
                vr = pp[:, 16:24, :]
                em.act(twb[:, 0, :], pp[:, 24, :], AF.Tanh, ['r_ssum'], ['r_twb'])
                em.cp('dve', twb[:, 1, :], pp[:, 25, :], ['r_ssum'], ['r_twb'])
                for h in range(8):
                    em.mm(B[h // 4][0:64, (h % 4) * 128:(h % 4 + 1) * 128], w2b[:, d, h * 64:(h + 1) * 64], twb[:, 0, :], True, True,
                          ['w2b', 'r_twb'], [f"B{h // 4}"], inc=(h == 7))
                for h in range(8):
                    em.act(lw[:, h, :], B[h // 4][0:64, (h % 4) * 128:(h % 4 + 1) * 128], AF.Sigmoid, [f"B{h // 4}", 'PV64'],
                           ['r_lw'], bias=PV64[:, P6_W0 + d * 8 + h:P6_W0 + d * 8 + h + 1], scale=1.0)
                for h in range(8):
                    em.mm(B[h // 4][0:64, (h % 4) * 128:(h % 4 + 1) * 128], a2b[:, d, h * 64:(h + 1) * 64], twb[:, 1, :], True, True,
                          ['a2b', 'r_twb'], [f"B{h // 4}"], inc=(h == 7))
                for h in range(8):
                    em.act(aa[:, h, :], B[h // 4][0:64, (h % 4) * 128:(h % 4 + 1) * 128], AF.Sigmoid,
                           [f"B{h // 4}", 'PV64'], ['r_aa'], bias=PV64[:, P6_A0 + d * 8 + h:P6_A0 + d * 8 + h + 1], scale=1.0)
                em.ts('dve', lw, lw, -R_DECAY_SCALE, None, ALU.mult, None, ['r_lw'], ['r_lw'])
                em.tt('dve', kk, kr, PV64[:, P6_KK:P6_KK + 8].unsqueeze(2).to_broadcast([64, 8, 128]), ALU.mult,
                      ['r_ssum', 'PV64'], ['r_kk'])
                em.act(sqb, kk, AF.Square, ['r_kk'], ['r_prod'])
                for hh in range(2):
                    em.mm(B[hh][0:64, :], onesb[0:64, 0:64], sqb[:, hh * 4:(hh + 1) * 4, :], True, True,
                          ['CSTB', 'r_prod'], [f"B{hh}"])
                for hh in range(2):
                    em.act(rinv[:, hh * 4:(hh + 1) * 4, :], B[hh][0:64, :].rearrange("p (h t) -> p h t", t=128), AF.Sqrt,
                           [f"B{hh}"], ['r_lex'])
                em.ts('dve', rinv, rinv, 1e-12, None, ALU.max, None, ['r_lex'], ['r_lex'])
                em.op('dve', lambda e: e.reciprocal(out=rinv, in_=rinv), ['r_lex'], ['r_lex'])
                em.tt('dve', kk, kk, rinv, ALU.mult, ['r_kk', 'r_lex'], ['r_kk'])
                em.tt('dve', tmpk, aa, PV64[:, P6_KA:P6_KA + 8].unsqueeze(2).to_broadcast([64, 8, 128]), ALU.mult,
                      ['r_aa', 'PV64'], ['r_tmpk'])
                em.tt('dve', tmpk, tmpk, OMMU.unsqueeze(2).to_broadcast([64, 8, 128]), ALU.add, ['r_tmpk', 'OMMU'], ['r_tmpk'])
                em.tt('dve', kd, kr, tmpk, ALU.mult, ['r_ssum', 'r_tmpk'], ['r_kd'])
                em.op('dve', lambda e: e.tensor_tensor_scan(out=linc.rearrange("p h t -> p (h t)"),
                                                            data0=RMK.rearrange("p h t -> p (h t)"),
                                                            data1=lw.rearrange("p h t -> p (h t)"), initial=0.0,
                                                            op0=ALU.mult, op1=ALU.add), ['RMK', 'r_lw'], ['r_linc'])
                if d == 0:
                    tot = linc[:, :, 127:128]
                else:
                    em.tt('dve', lex, lw, linc, ALU.subtract, ['r_lw', 'r_linc'], ['r_lex'])
                    em.cp('dve', gC, linc[:, :, 127], ['r_linc'], [kgC])
                    em.tt('dve', linc, lex, gC.unsqueeze(2).to_broadcast([64, 8, 128]), ALU.add, ['r_lex', kgC, 'r_linc'],
                          ['r_linc'])
                    tot = linc[:, :, 0:1]
                em.tt('dve', lex, linc, lw, ALU.subtract, ['r_linc', 'r_lw'], ['r_lex'])
                em.act(e1, linc, AF.Exp, ['r_linc'], ['r_e1'])
                em.act(e0, lex, AF.Exp, ['r_lex'], ['r_e0'])
                em.act(ei, linc, AF.Exp, ['r_linc'], ['r_ei'], scale=-1.0)
                em.act(gC, tot.rearrange("p h o -> p (h o)"), AF.Exp, ['r_linc', kgC], [kgC])
                em.tt('dve', AR[:, :, 128:256], rr, e1, ALU.mult, ['r_ssum', 'r_e1'], [kAR])
                em.stt(AR[:, :, 0:128], kk, -1.0, e0, ALU.mult, ALU.mult, ['r_kk', 'r_e0'], [kAR])
                em.tt('dve', tmpk, kk, aa, ALU.mult, ['r_kk', 'r_aa', 'r_tmpk'], ['r_tmpk'])
                em.tt('dve', BK[:, :, 0, :], tmpk, ei, ALU.mult, ['r_tmpk', 'r_ei'], [kBK])
                em.tt('dve', BK[:, :, 1, :], kd, ei, ALU.mult, ['r_kd', 'r_ei'], [kBK])
                em.tt('dve', tmpk, rr, kd, ALU.mult, ['r_ssum', 'r_kd', 'r_tmpk'], ['r_tmpk'])
                em.tt('dve', prodb, tmpk, PV64[:, P6_RK:P6_RK + 8].unsqueeze(2).to_broadcast([64, 8, 128]), ALU.mult,
                      ['r_tmpk', 'PV64'], ['r_prod'])
                for h in range(8):
                    em.tr(B[0][:, h * 64:(h + 1) * 64], vr[:, h, :], ident[0:64, 0:64], ['r_ssum', 'CST'], ['B0'], inc=(h == 7))
                em.cp('act', vtokb, B[0], ['B0'], [kvtokb])
                for half in range(2):
                    for h4 in range(4):
                        for q in range(2):
                            em.tr(BBp[:, (h4 * 2 + q) * 64:(h4 * 2 + q + 1) * 64], BK[:, half * 4 + h4, q, :], identb[0:64, 0:64],
                                  [kBK, 'CSTB'], ['BB'], inc=(h4 == 3 and q == 1))
                    em.cp('dve', BKtok[:, half * 4:(half + 1) * 4].rearrange("p h q k -> p (h q k)"), BBp[:, 0:512], ['BB'], [kBKtok])
                for h in range(8):
                    em.mm(B[1][:, 256 + h:257 + h], prodb[:, h, :], onesb[0:64, 0:1], True, True, ['r_prod', 'CSTB'], ['B1'], inc=(h == 7))
                em.cp('act', coef, B[1][:, 256:264], ['B1'], [kcoef])
                for hp in range(4):
                    bb_ = B[0]
                    bbk = "B0"
                    bk2 = B[1]
                    bk2k = "B1"
                    for q in range(2):
                        h = hp * 2 + q
                        em.mm(bb_[:, q * 256:(q + 1) * 256], BK[:, h, 0, :], AR[:, h, :], True, True, [kBK, kAR, 'r_lw', 'r_aa'], [bbk], inc=(q == 1))
                        em.mm(bk2[:, q * 256:(q + 1) * 256], BK[:, h, 1, :], AR[:, h, :], True, True, [kBK, kAR], [bk2k], inc=(q == 1))
                    em.tt('dve', GB[:, hp * 2:hp * 2 + 2, :], bb_.rearrange("p (q c) -> p q c", c=256),
                          MARt[:, d:d + 1, :].to_broadcast([128, 2, 256]), ALU.mult, [bbk, 'MAR'], [kGB])
                    em.tt('dve', Q0[:, hp * 2:hp * 2 + 2, :], bb_.rearrange("p (q c) -> p q c", c=256)[:, :, 0:128],
                          CST[:, C_MP0 + (1 - d) * 128:C_MP0 + (2 - d) * 128].unsqueeze(1).to_broadcast([128, 2, 128]), ALU.mult,
                          [bbk, 'CST'], [kQ0])
                    em.tt('dve', GK[:, hp * 2:hp * 2 + 2, :], bk2.rearrange("p (q c) -> p q c", c=256),
                          MARt[:, d:d + 1, :].to_broadcast([128, 2, 256]), ALU.mult, [bk2k, 'MAR'], [kGK])
                if split:
                    em.stream = ('rs', idx)
                if first_of_d:
                    em.memset('pool', Hs, 0.0, ['r_H'])
                    em.memset('pool', Hb, 0.0, ['r_Hb'])
                for hh in range(2):
                    bp = B[2 + hh]
                    bpk = f"B{2 + hh}"
                    for q in range(4):
                        h = hh * 4 + q
                        em.mm(bp[:, q * 128:(q + 1) * 128], AR[:, h, 0:128], BK[:, h, 0, :], True, True, [kAR, kBK], [bpk], inc=(q == 3))
                    b3 = bp.rearrange("p (q c) -> p q c", c=128)
                    hsl = slice(hh * 4, (hh + 1) * 4)
                    em.tt('dve', Pm[0][:, hsl, :], b3, CST[:, C_MP0 + d * 128:C_MP0 + (d + 1) * 128].unsqueeze(1).to_broadcast([128, 4, 128]),
                          ALU.mult, [bpk, 'CST'], ['r_P0'])
                    em.tt('dve', E1m[:, hsl, :], b3, CST[:, C_ME1 + d * 128:C_ME1 + (d + 1) * 128].unsqueeze(1).to_broadcast([128, 4, 128]),
                          ALU.mult, [bpk, 'CST'], ['r_E1'])
                    em.tt('dve', E2m[:, hsl, :], b3, CST[:, C_ME2 + d * 128:C_ME2 + (d + 1) * 128].unsqueeze(1).to_broadcast([128, 4, 128]),
                          ALU.mult, [bpk, 'CST'], ['r_E2'])
                em.tt('dve', Ym[0], Q0, identb.unsqueeze(1).to_broadcast([128, 8, 128]), ALU.add, [kQ0, 'CSTB'], ['r_Y0'])
                em.cp('act', Qm[0], Q0, [kQ0], ['r_Q0'])
                cur = 0
                for lev in range(1, 5):
                    nxt = 1 - cur
                    for hh in range(2):
                        bp = B[2]
                        bpk = "B2"
                        bq = B[3]
                        bqk = "B3"
                        hsl = slice(hh * 4, (hh + 1) * 4)
                        for q in range(4):
                            h = hh * 4 + q
                            em.mm(bp[:, q * 128:(q + 1) * 128], Qm[cur][:, h, :], Pm[cur][:, h, :], True, True,
                                  [f"r_Q{cur}", f"r_P{cur}"], [bpk], inc=(q == 3))
                        for q in range(4):
                            h = hh * 4 + q
                            em.mm(bq[:, q * 128:(q + 1) * 128], Pm[cur][:, h, :], Qm[cur][:, h, :], True, True,
                                  [f"r_Q{cur}", f"r_P{cur}"], [bqk], inc=(q == 3))
                        em.cp('act', Pm[nxt][:, hsl, :], bp.rearrange("p (q c) -> p q c", c=128), [bpk], [f"r_P{nxt}"])
                        em.cp('dve', Qm[nxt][:, hsl, :], bq.rearrange("p (q c) -> p q c", c=128), [bqk], [f"r_Q{nxt}"])
                    for hh in range(2):
                        by = B[6]
                        byk = "B6"
                        hsl = slice(hh * 4, (hh + 1) * 4)
                        for q in range(4):
                            h = hh * 4 + q
                            em.mm(by[:, q * 128:(q + 1) * 128], Pm[nxt][:, h, :], Ym[cur][:, h, :], True, True,
                                  [f"r_P{nxt}", f"r_Y{cur}"], [byk], inc=(q == 3))
                        em.tt('dve', Ym[nxt][:, hsl, :], by.rearrange("p (q c) -> p q c", c=128), Ym[cur][:, hsl, :], ALU.add,
                              [byk, f"r_Y{cur}"], [f"r_Y{nxt}"])
                    cur = nxt
                Dt = Ym[cur]
                dtk = f"r_Y{cur}"
                for st, (Em_, ek) in enumerate([(E1m, 'r_E1'), (E2m, 'r_E2')]):
                    oth = Ym[1 - cur]
                    othk = f"r_Y{1 - cur}"
                    for half in range(2):
                        for h4 in range(4):
                            em.tr(BBp[:, 512 + h4 * 128:512 + (h4 + 1) * 128], Dt[:, half * 4 + h4, :], identb, [dtk, 'CSTB'], ['BB'], inc=(h4 == 3))
                        em.cp('act', Dm[:, half * 4:(half + 1) * 4, :], BBp[:, 512:1024].rearrange("p (h c) -> p h c", c=128),
                              ['BB'], ['r_D'])
                    for hh in range(2):
                        bz = B[2 + hh]
                        bzk = f"B{2 + hh}"
                        hsl = slice(hh * 4, (hh + 1) * 4)
                        for q in range(4):
                            h = hh * 4 + q
                            em.mm(bz[:, q * 128:(q + 1) * 128], Em_[:, h, :], Dt[:, h, :], True, True, [ek, dtk], [bzk], inc=(q == 3))
                        em.cp('act' if hh == 0 else 'dve', Zm[:, hsl, :], bz.rearrange("p (q c) -> p q c", c=128), [bzk], ['r_Z'])
                    for hh in range(2):
                        by = B[2 + hh]
                        byk = f"B{2 + hh}"
                        hsl = slice(hh * 4, (hh + 1) * 4)
                        for q in range(4):
                            h = hh * 4 + q
                            em.mm(by[:, q * 128:(q + 1) * 128], Dm[:, h, :], Zm[:, h, :], True, True, ['r_D', 'r_Z'], [byk], inc=(q == 3))
                        em.tt('dve', oth[:, hsl, :], by.rearrange("p (q c) -> p q c", c=128), Dt[:, hsl, :], ALU.add,
                              [byk, dtk], [othk])
                    cur = 1 - cur
                    Dt = Ym[cur]
                    dtk = f"r_Y{cur}"
                TT_ = Dt
                ttk = dtk
                for h in range(8):
                    hs_ = slice(h * 64, (h + 1) * 64)
                    em.mm(B[2][:, hs_], AR[:, h, 0:128], Hb[:, hs_], True, False, [kAR, 'r_Hb'], ['B2'])
                    em.mm(B[2][:, hs_], GK[:, h, 0:128], vtokb[:, hs_], False, True, [kGK, kvtokb], ['B2'], inc=(h == 7))
                em.cp('act', Wt, B[2], ['B2'], ['r_W'])
                for h in range(8):
                    hs_ = slice(h * 64, (h + 1) * 64)
                    em.mm(B[3][:, hs_], TT_[:, h, :], Wt[:, hs_], True, True, [ttk, 'r_W'], ['B3'], inc=(h == 7))
                em.cp('dve', Ut, B[3], ['B3'], ['r_U'])
                for h in range(8):
                    hs_ = slice(h * 64, (h + 1) * 64)
                    em.mm(B[2][:, hs_], AR[:, h, 128:256], Hb[:, hs_], True, False, [kAR, 'r_Hb'], ['B2'])
                    em.mm(B[2][:, hs_], GB[:, h, 128:256], Ut[:, hs_], False, False, [kGB, 'r_U'], ['B2'])
                    em.mm(B[2][:, hs_], GK[:, h, 128:256], vtokb[:, hs_], False, True, [kGK, kvtokb], ['B2'], inc=(h == 7))
                for h in range(8):
                    hs_ = slice(h * 64, (h + 1) * 64)
                    em.mm(B[3][0:64, hs_], BKtok[:, h, 0, :], Ut[:, hs_], True, False, [kBKtok, 'r_U'], ['B3'])
                    em.mm(B[3][0:64, hs_], BKtok[:, h, 1, :], vtokb[:, hs_], False, True, [kBKtok, kvtokb], ['B3'], inc=(h == 7))
                em.tt('dve', Hs, Hs, B[3][0:64, :], ALU.add, ['r_H', 'B3'], ['r_H'])
                em.tt('dve', Hs.rearrange("p (h v) -> p h v", v=64), Hs.rearrange("p (h v) -> p h v", v=64),
                      gC.unsqueeze(2).to_broadcast([64, 8, 64]), ALU.mult, ['r_H', kgC], ['r_H'])
                em.cp('act', Hb, Hs, ['r_H', 'B2'], ['r_Hb'])
                if d == 0:
                    if emit_out:
                        em.cp('act', ot[:, 0:512], B[2], ['B2'], ['r_o'])
                        em.cp('dve', ot[:, 512:520], coef, [kcoef], ['r_ocoef'])
                        em.dma('pool', OF[s][t0:t0 + 128, :], ot, reads=['r_o', 'r_ocoef'], writes=[f"of{s}"])
                elif emit_out:
                    em.dma('sp', oft, OF[s][t0:t0 + 128, :], reads=[f"of{s}"], writes=['r_of'])
                    em.tt('dve', ot[:, 0:512], B[2], oft[:, 0:512], ALU.add, ['B2', 'r_of'], ['r_o'])
                    o3 = ot[:, 0:512].rearrange("p (h v) -> p h v", v=64)
                    em.op('dve', lambda e: e.tensor_reduce(out=gn[:, 0:8], in_=o3, axis=AX.X, op=ALU.add), ['r_o'], ['r_gn'])
                    em.act(osq, ot[:, 0:512], AF.Square, ['r_o'], ['r_osq'])
                    em.op('dve', lambda e: e.tensor_reduce(out=gn[:, 8:16], in_=osq.rearrange("p (h v) -> p h v", v=64),
                                                           axis=AX.X, op=ALU.add), ['r_osq', 'r_gn'], ['r_gn'])
                    em.ts('dve', gn[:, 16:24], gn[:, 0:8], 1.0 / 64, None, ALU.mult, None, ['r_gn'], ['r_gn'])
                    em.tt('dve', gn[:, 0:8], gn[:, 16:24], gn[:, 16:24], ALU.mult, ['r_gn'], ['r_gn'])
                    em.stt(gn[:, 24:32], gn[:, 8:16], 1.0 / 64, gn[:, 0:8], ALU.mult, ALU.subtract, ['r_gn'], ['r_gn'])
                    em.act(gn[:, 24:32], gn[:, 24:32], AF.Sqrt, ['r_gn'], ['r_gn'], bias=R_LN_EPS, scale=1.0)
                    em.op('dve', lambda e: e.reciprocal(out=gn[:, 24:32], in_=gn[:, 24:32]), ['r_gn'], ['r_gn'])
                    em.tt('dve', o3, o3, gn[:, 16:24].unsqueeze(2).to_broadcast([128, 8, 64]), ALU.subtract, ['r_o', 'r_gn'], ['r_o'])
                    em.tt('dve', o3, o3, gn[:, 24:32].unsqueeze(2).to_broadcast([128, 8, 64]), ALU.mult, ['r_o', 'r_gn'], ['r_o'])
                    em.tt('dve', ot[:, 0:512], ot[:, 0:512], lnw, ALU.mult, ['r_o', 'ROWB'], ['r_o'])
                    em.tt('dve', ot[:, 0:512], ot[:, 0:512], lnb, ALU.add, ['r_o', 'ROWB'], ['r_o'])
                    em.tt('dve', gn[:, 32:40], coef, oft[:, 512:520], ALU.add, [kcoef, 'r_of', 'r_gn'], ['r_gn'])
                    em.tt('dve', osq.rearrange("p (h v) -> p h v", v=64), vtokb.rearrange("p (h v) -> p h v", v=64),
                          gn[:, 32:40].unsqueeze(2).to_broadcast([128, 8, 64]), ALU.mult, [kvtokb, 'r_gn', 'r_osq'], ['r_osq'])
                    em.tt('dve', ot[:, 0:512], ot[:, 0:512], osq, ALU.add, ['r_o', 'r_osq'], ['r_o'])
                    em.dma('sp', xg0[:, 0:128], XG[s][0:128, t0:t0 + 128], reads=[f"xg{s}"], writes=['r_xg0'])
                    em.dma('sp', xg1[:, 0:128], XG[s][128:160, t0:t0 + 128], reads=[f"xg{s}"], writes=['r_xg1'])
                    em.act(sg0, xg0[:, 0:128], AF.Sigmoid, ['r_xg0'], ['r_sg0'])
                    em.act(sg1, xg1[:, 0:128], AF.Sigmoid, ['r_xg1'], ['r_sg1'])
                    em.mm(B[3], sg0, g2b0, True, False, ['r_sg0', 'g2b'], ['B3'])
                    em.mm(B[3], sg1, g2b1, False, True, ['r_sg1', 'g2b'], ['B3'])
                    em.tt('dve', ot[:, 0:512], ot[:, 0:512], B[3], ALU.mult, ['r_o', 'B3'], ['r_o'])
                    for j in range(4):
                        em.tr(B[2][:, j * 128:(j + 1) * 128], ot[:, j * 128:(j + 1) * 128], ident, ['r_o', 'CST'], ['B2'], inc=(j == 3))
                    em.cp('act', mixo[:, 4:8, :], B[2].rearrange("p (j t) -> p j t", t=128), ['B2'], ['mixo_r'])
                    em.dma('pool', fm(MIX[s])[:, 4:8, t0:t0 + 128], mixo[:, 4:8, :], reads=['mixo_r'], writes=[f"mixr{s}"])

            mc_ = mixcfg or {}
            inter = mc_.get('interleave', True)
            for b in range(mc_.get('nb', NBL)):
                nw = 0
                for stream, fnc in (('m', mamba_chunk), ('r', wkv_chunk)):
                    if not mc_.get('mamba' if stream == 'm' else 'wkv', True):
                        continue
                    for d in range(mc_.get('nd', 2)):
                        first = True
                        if stream == 'm':
                            em.stream = 'm' if inter else None
                            em.memset('pool', hst, 0.0, ['m_hst'])
                            em.memset('pool', hstb, 0.0, ['m_hstb'])
                        for kind in range(mc_.get('nkind', 2)):
                            s = b * 2 + kind
                            T = seqT(s)
                            nch = T // CH
                            order = range(nch) if d == 0 else range(nch - 1, -1, -1)
                            emit = (kind == 1) or need_ctx_out or mc_.get('ctxout', False)
                            for c in order:
                                if stream == 'm':
                                    fnc(s, c * CH, d, emit)
                                else:
                                    fnc(s, c * CH, d, emit, idx=nw, first_of_d=first, split=inter)
                                    nw += 1
                                    first = False
                em.stream = None
                if inter:
                    em.flush_mixer(nw)
            em.barrier()

    def stage_proj_post(l, phase, wap, Kc, SRC, srcname, gidx, seqs):
        with ExitStack() as es:
            def S(name, shape, dt=F32):
                return es.enter_context(nc.sbuf_tensor(name, list(shape), dt)).ap()
            nm = f"pp{phase}"
            wb = load_wbf(es, l, wap, Kc, D, nm + "w")
            a = S(f"{nm}a{l}", [128, Kc, 512], BF16)
            xt = S(f"{nm}x{l}", [128, 8, 512])
            y = S(f"{nm}y{l}", [128, 8, 512])
            sq = S(f"{nm}sq{l}", [128, 8, 512], BF16)
            rstd = S(f"{nm}rs{l}", [128, 512])
            pss = [es.enter_context(nc.psum_tensor(f"{nm}ps{l}_{i}", [128, 512], F32)).ap() for i in range(5)]
            for s in seqs:
                T = seqT(s)
                TW = min(512, T)
                jmod = 2 if s % 2 == 0 else s // 2
                rsrc, rsk = res_src(l, s, phase)
                rdst, rdk = res_dst(l, s, phase)
                for tt_ in range(T // TW):
                    t0 = tt_ * TW
                    em.dma('sp', a[:, :, :TW], fm(SRC[s])[:, :, t0:t0 + TW], reads=([f"mixm{s}", f"mixr{s}"] if srcname == 'mix' else [f"{srcname}{s}"]), writes=[nm + 'a'])
                    em.dma('pool', xt[:, :, :TW], fm(rsrc)[:, :, t0:t0 + TW], reads=[rsk], writes=[nm + 'x'])
                    for m in range(8):
                        ps = pss[m % 4]
                        pk = f"ps{m % 4}"
                        for k in range(Kc):
                            em.mm(ps[:, :TW], wb[:, k, m * 128:(m + 1) * 128], a[:, k, :TW], k == 0, k == Kc - 1,
                                  [nm + 'wbf', nm + 'a'], [pk])
                        em.cp('dve', y[:, m, :TW], ps[:, :TW], [pk], [nm + 'y'])
                        em.act(sq[:, m, :TW], ps[:, :TW], AF.Square, [pk], [nm + 'sq'])
                    for m in range(8):
                        em.mm(pss[4][:, :TW], onesb, sq[:, m, :TW], m == 0, m == 7, ['CSTB', nm + 'sq'], ['ps4'])
                    em.act(rstd[:, :TW], pss[4][:, :TW], AF.Sqrt, ['ps4'], [nm + 'rs'], bias=EPS, scale=1.0 / D)
                    em.op('dve', lambda e: e.reciprocal(out=rstd[:, :TW], in_=rstd[:, :TW]), [nm + 'rs'], [nm + 'rs'])
                    for m in range(8):
                        em.stt(y[:, m, :TW], y[:, m, :TW], DER[:, gidx, m, jmod:jmod + 1], rstd[:, :TW], ALU.mult, ALU.mult,
                               [nm + 'y', nm + 'rs', 'DER'], [nm + 'y'])
                    em.tt('dve', xt[:, :, :TW], xt[:, :, :TW], y[:, :, :TW], ALU.add, [nm + 'x', nm + 'y'], [nm + 'x'])
                    em.dma('pool', fm(rdst)[:, :, t0:t0 + TW], xt[:, :, :TW], reads=[nm + 'x'], writes=[rdk])
            em.barrier()

    def stage_ffn_up(l, seqs):
        with ExitStack() as es:
            def S(name, shape, dt=F32):
                return es.enter_context(nc.sbuf_tensor(name, list(shape), dt)).ap()
            wb = load_wbf(es, l, f_w_up[l], 8, 2 * DFF, "wup")
            xt = S(f"fux{l}", [128, 8, 512])
            h = S(f"fuh{l}", [128, 8, 512], BF16)
            sq = S(f"fusq{l}", [128, 8, 512], BF16)
            rstd = S(f"furs{l}", [128, 512])
            stg = [S(f"fustg{l}_{i}", [128, 512]) for i in range(2)]
            stv = [S(f"fustv{l}_{i}", [128, 512], BF16) for i in range(2)]
            pss = [es.enter_context(nc.psum_tensor(f"fups{l}_{i}", [128, 512], F32)).ap() for i in range(8)]
            for s in seqs:
                T = seqT(s)
                TW = min(512, T)
                jmod = 2 if s % 2 == 0 else s // 2
                src, srck = res_src(l, s, 1)
                for tt_ in range(T // TW):
                    t0 = tt_ * TW
                    em.dma('sp', xt[:, :, :TW], fm(src)[:, :, t0:t0 + TW], reads=[srck], writes=['fux'])
                    prenorm(xt, TW, h, sq, rstd, pss[7], jmod, 2, 24, 'fux', 'fuh', 'ps7')
                    for j in range(NFF):
                        pg = pss[(2 * j) % 6]
                        pgk = f"ps{(2 * j) % 6}"
                        pv_ = pss[(2 * j + 1) % 6]
                        pvk = f"ps{(2 * j + 1) % 6}"
                        for k in range(8):
                            em.mm(pg[:, :TW], wb[:, k, j * 128:(j + 1) * 128], h[:, k, :TW], k == 0, k == 7, ['wupbf', 'fuh'], [pgk])
                        for k in range(8):
                            em.mm(pv_[:, :TW], wb[:, k, DFF + j * 128:DFF + (j + 1) * 128], h[:, k, :TW], k == 0, k == 7,
                                  ['wupbf', 'fuh'], [pvk])
                        sg_ = stg[j % 2]
                        sv_ = stv[j % 2]
                        em.cp('dve', sg_[:, :TW], pg[:, :TW], [pgk], [f"fustg{j % 2}"])
                        em.cp('act', sv_[:, :TW], pv_[:, :TW], [pvk], [f"fustv{j % 2}"])
                        em.dma('pool', GATE[s][j * 128:(j + 1) * 128, t0:t0 + TW], sg_[:, :TW], reads=[f"fustg{j % 2}"],
                               writes=[f"gate{s}"])
                        em.dma('sp', VAL[s][j * 128:(j + 1) * 128, t0:t0 + TW], sv_[:, :TW], reads=[f"fustv{j % 2}"],
                               writes=[f"val{s}"])
            em.barrier()

    def stage_ffn_conv(l, seqs):
        with ExitStack() as es:
            def S(name, shape, dt=F32):
                return es.enter_context(nc.sbuf_tensor(name, list(shape), dt)).ap()
            gflat = [S(f"fcg{l}_{i}", [128, 2048]) for i in range(2)]
            vflat = [S(f"fcv{l}_{i}", [128, 2048], BF16) for i in range(2)]
            gpx = [S(f"fcgpx{l}_{i}", [128, 34, 66], BF16) for i in range(2)]
            gpc = [S(f"fcgpc{l}_{i}", [128, 3, 258], BF16) for i in range(2)]
            dg = [S(f"fcdg{l}_{i}", [128, 9, 128], BF16) for i in range(2)]
            acc = S(f"fcacc{l}", [128, 2048])
            u = S(f"fcu{l}", [128, 2048])
            ab = [S(f"fcab{l}_{i}", [128, 2048], BF16) for i in range(2)]
            pss = [es.enter_context(nc.psum_tensor(f"fcps{l}_{i}", [128, 512], F32)).ap() for i in range(4)]
            for i in range(2):
                em.memset('pool', gpx[i], 0.0, [f"fcgpx{i}"])
                em.memset('pool', gpc[i], 0.0, [f"fcgpc{i}"])
            it = 0
            pi = 0
            for s in seqs:
                T = seqT(s)
                for j in range(NFF):
                    i2 = it % 2
                    it += 1
                    if s % 2 == 1:
                        R, Cc, gp, gpk = 32, 64, gpx[i2], f"fcgpx{i2}"
                    else:
                        R, Cc, gp, gpk = 1, 256, gpc[i2], f"fcgpc{i2}"
                    gf = gflat[i2]
                    vf = vflat[i2]
                    em.dma('sp', gf[:, :T], GATE[s][j * 128:(j + 1) * 128, :], reads=[f"gate{s}"], writes=[f"fcg{i2}"])
                    em.dma('sp', vf[:, :T], VAL[s][j * 128:(j + 1) * 128, :], reads=[f"val{s}"], writes=[f"fcv{i2}"])
                    em.cp('act', gp[:, 1:1 + R, 1:1 + Cc], gf[:, :T].rearrange("p (r c) -> p r c", c=Cc), [f"fcg{i2}"], [gpk])
                    taps = list(range(9)) if s % 2 == 1 else [3, 4, 5]
                    for tap in taps:
                        em.ts('dve', dg[i2][:, tap, :], identb, PV[:, PV_FCW + tap * NFF + j:PV_FCW + tap * NFF + j + 1], None,
                              ALU.mult, None, ['CSTB', 'PV'], [f"fcdg{i2}"])
                    nblk = T // 512 if s % 2 == 1 else 1
                    for blk in range(nblk):
                        ps = pss[pi % 4]
                        pk = f"ps{pi % 4}"
                        pi += 1
                        for ti, tap in enumerate(taps):
                            dr, dc = tap // 3 - 1, tap % 3 - 1
                            if s % 2 == 1:
                                rhs = gp[:, 1 + dr + blk * 8:1 + dr + blk * 8 + 8, 1 + dc:1 + dc + 64]
                                out = ps.rearrange("p (r c) -> p r c", c=64)
                                wdt = 512
                            else:
                                rhs = gp[:, 1, 1 + dc:1 + dc + 256]
                                out = ps[:, 0:256]
                                wdt = 256
                            em.mm(out, dg[i2][:, tap, :], rhs, ti == 0, ti == len(taps) - 1, [f"fcdg{i2}", gpk], [pk])
                        em.act(acc[:, blk * 512:blk * 512 + wdt], ps[:, 0:wdt], AF.Identity, [pk, 'PV'], ['fcacc'],
                               bias=PV[:, PV_FCB + j:PV_FCB + j + 1], scale=1.0)
                    em.act(u[:, :T], acc[:, :T], AF.Square, ['fcacc'], ['fcu'])
                    em.ts('dve', u[:, :T], u[:, :T], 0.044715, 1.0, ALU.mult, ALU.add, ['fcu'], ['fcu'])
                    em.tt('dve', u[:, :T], u[:, :T], acc[:, :T], ALU.mult, ['fcu', 'fcacc'], ['fcu'])
                    em.act(u[:, :T], u[:, :T], AF.Sigmoid, ['fcu'], ['fcu'], scale=GELU_C)
                    em.tt('dve', u[:, :T], u[:, :T], acc[:, :T], ALU.mult, ['fcu', 'fcacc'], ['fcu'])
                    em.tt('dve', ab[i2][:, :T], u[:, :T], vf[:, :T], ALU.mult, ['fcu', f"fcv{i2}"], [f"fcab{i2}"])
                    em.dma('pool', ACTV[s][j * 128:(j + 1) * 128, :], ab[i2][:, :T], reads=[f"fcab{i2}"], writes=[f"actv{s}"])
            em.barrier()

    allseq = list(range(NS))
    xseq = [s for s in range(NS) if s % 2 == 1]
    outkeys = []
    for l in range(n_layers):
        last = (l == n_layers - 1)
        stage_mod(l)
        if stop_after == 'mod':
            break
        stage_inproj(l)
        if stop_after == 'inproj':
            break
        stage_mixer(l, need_ctx_out=not last)
        if stop_after == 'mixer':
            break
        seqs = xseq if last else allseq
        stage_proj_post(l, 0, w_out[l], 8, MIX, "mix", 1, seqs)
        if stop_after == 'outproj':
            break
        stage_ffn_up(l, seqs)
        stage_ffn_conv(l, seqs)
        stage_proj_post(l, 1, f_w_down[l], NFF, ACTV, "actv", 3, seqs)
    em.barrier()
    return nc, em


def host_prep(inp):
    f = np.float32
    idx = np.arange(128)
    cstn = np.zeros((128, NCST), f)
    cstn[:, C_ID:C_ID + 128] = np.eye(128)
    cstn[:, C_UTI:C_UTI + 128] = (idx[:, None] <= idx[None, :])
    cstn[:, C_LTI:C_LTI + 128] = (idx[:, None] >= idx[None, :])
    cstn[:, C_UTS:C_UTS + 128] = (idx[:, None] < idx[None, :])
    cstn[:, C_LTS:C_LTS + 128] = (idx[:, None] > idx[None, :])
    cstn[:, C_ONE:C_ONE + 128] = 1.0
    b32 = idx // 32
    b64 = idx // 64
    same32 = b32[:, None] == b32[None, :]
    same64 = b64[:, None] == b64[None, :]
    for d in range(2):
        strict = (idx[:, None] > idx[None, :]) if d == 0 else (idx[:, None] < idx[None, :])
        cstn[:, C_MP0 + d * 128:C_MP0 + (d + 1) * 128] = strict & same32
        cstn[:, C_ME1 + d * 128:C_ME1 + (d + 1) * 128] = strict & same64 & (~same32)
        cstn[:, C_ME2 + d * 128:C_ME2 + (d + 1) * 128] = strict & (~same64)
    pvn = np.zeros((L, 128, NPV), f)
    pv6 = np.zeros((L, 64, NPV64), f)
    rwn = np.zeros((L, 1, NROW), f)
    for l in range(L):
        pvn[l, :, PV_BMOD:PV_BMOD + 48] = inp['b_mod'][l].reshape(48, 128).T
        pvn[l, :, PV_GPRE1:PV_GPRE1 + 8] = inp['g_mix_pre'][l].reshape(8, 128).T
        pvn[l, :, PV_GPOST1:PV_GPOST1 + 8] = inp['g_mix_post'][l].reshape(8, 128).T
        pvn[l, :, PV_GPRE2:PV_GPRE2 + 8] = inp['g_ffn_pre'][l].reshape(8, 128).T
        pvn[l, :, PV_GPOST2:PV_GPOST2 + 8] = inp['g_ffn_post'][l].reshape(8, 128).T
        pvn[l, :, PV_MCW:PV_MCW + 24] = inp['m_conv_w'][l].reshape(3, 8, 128).transpose(2, 0, 1).reshape(128, 24)
        pvn[l, :, PV_MCB:PV_MCB + 8] = inp['m_conv_b'][l].reshape(8, 128).T
        pvn[l, :, PV_FCW:PV_FCW + 198] = inp['f_conv_w'][l].reshape(9, NFF, 128).transpose(2, 0, 1).reshape(128, 198)
        pvn[l, :, PV_FCB:PV_FCB + NFF] = inp['f_conv_b'][l].reshape(NFF, 128).T
        mu = inp['r_mu'][l]
        pvn[l, :, PV_MUXG0] = mu[1792:1920]
        pvn[l, 0:32, PV_MUXG1] = mu[1920:1952]
        pv6[l, :, P6_MURKV:P6_MURKV + 24] = mu[0:1536].reshape(24, 64).T
        pv6[l, :, P6_MUWA:P6_MUWA + 4] = mu[1536:1792].reshape(4, 64).T
        pv6[l, :, P6_W0:P6_W0 + 16] = inp['r_w0'][l].reshape(2, 8, 64).transpose(2, 0, 1).reshape(64, 16)
        pv6[l, :, P6_A0:P6_A0 + 16] = inp['r_a0'][l].reshape(2, 8, 64).transpose(2, 0, 1).reshape(64, 16)
        pv6[l, :, P6_KK:P6_KK + 8] = inp['r_k_k'][l].reshape(8, 64).T
        pv6[l, :, P6_KA:P6_KA + 8] = inp['r_k_a'][l].reshape(8, 64).T
        pv6[l, :, P6_RK:P6_RK + 8] = inp['r_r_k'][l].T
        rwn[l, 0, RV_MNW:RV_MNW + 512] = inp['m_norm_w'][l]
        rwn[l, 0, RV_LNW:RV_LNW + 512] = inp['r_ln_w'][l]
        rwn[l, 0, RV_LNB:RV_LNB + 512] = inp['r_ln_b'][l]
        rwn[l, 0, RV_MD:RV_MD + 8] = inp['m_d'][l]
        rwn[l, 0, RV_DTB:RV_DTB + 16] = inp['m_dt_bias'][l].reshape(16)
        rwn[l, 0, RV_ALOG:RV_ALOG + 16] = inp['m_a_log'][l].reshape(16)
    return cstn, pvn, pv6, rwn


def make_in_maps(inp, cores):
    cstn, pvn, pv6, rwn = host_prep(inp)
    shared = {k: np.ascontiguousarray(np.asarray(inp[k], dtype=np.float32)) for k in
              ['w_mod', 'w_in', 'w_out', 'r_w2', 'r_a2', 'r_g2', 'f_w_up', 'f_w_down']}
    maps = []
    x = np.asarray(inp['x'], np.float32)
    ctx = np.asarray(inp['ctx'], np.float32)
    c = np.asarray(inp['c'], np.float32)
    cc = np.asarray(inp['c_ctx'], np.float32)
    for ci in cores:
        bs = [ci * NBL + i for i in range(NBL)]
        m = dict(shared)
        m['xT'] = np.ascontiguousarray(x[bs].transpose(0, 2, 1))
        m['ctxT'] = np.ascontiguousarray(ctx[bs].transpose(0, 2, 1))
        m['cT'] = np.ascontiguousarray(np.stack([c[bs[0]], c[bs[1]], cc], axis=1))
        m['cst'] = cstn
        m['pv'] = pvn
        m['pv64'] = pv6
        m['rowv'] = rwn
        m['rmu'] = np.ascontiguousarray(np.asarray(inp['r_mu'], np.float32).reshape(L, 1, 1952))
        maps.append(m)
    return maps


def kernel(**inputs):
    nc, em = build()
    cores = list(range(NCORE))
    maps = make_in_maps(inputs, cores)
    res = run_bass_kernel_spmd(nc, maps, core_ids=cores)
    out = np.empty((NCORE * NBL, TX, D), np.float32)
    for ci in cores:
        o = res.results[ci]["outT"]
        out[ci * NBL:(ci + 1) * NBL] = o.transpose(0, 2, 1)
    return out
```

```python
import numpy as np
from contextlib import ExitStack
import concourse.bass as bass
import concourse.mybir as mybir
from concourse.bass_utils import run_bass_kernel_spmd

F32 = mybir.dt.float32
BF16 = mybir.dt.bfloat16
AF = mybir.ActivationFunctionType
ALU = mybir.AluOpType
AX = mybir.AxisListType

L = 2
D = 1024
TX = 2048
TC = 256
NBL = 2
NCORE = 8
CH = 128
DFF = 2816
NFF = 22
EPS = 1e-6
R_LN_EPS = 64e-5
R_DECAY_SCALE = 0.6065306597126334
GELU_C = 1.5957691216057308

PV_BMOD = 0
PV_GPRE1 = 48
PV_GPOST1 = 56
PV_GPRE2 = 64
PV_GPOST2 = 72
PV_MCW = 80
PV_MCB = 104
PV_FCW = 112
PV_FCB = 310
PV_MUXG0 = 332
PV_MUXG1 = 333
NPV = 334
P6_MURKV = 0
P6_MUWA = 24
P6_W0 = 28
P6_A0 = 44
P6_KK = 60
P6_KA = 68
P6_RK = 76
NPV64 = 84
RV_MNW = 0
RV_LNW = 512
RV_LNB = 1024
RV_MD = 1536
RV_DTB = 1544
RV_ALOG = 1560
NROW = 1576
C_ID = 0
C_UTI = 128
C_LTI = 256
C_UTS = 384
C_LTS = 512
C_ONE = 640
C_MP0 = 768
C_ME1 = 1024
C_ME2 = 1280
NCST = 1536


class Em:
    def __init__(self, nc, ndma=28):
        self.nc = nc
        self.engs = {'pe': nc.tensor, 'act': nc.scalar, 'dve': nc.vector, 'pool': nc.gpsimd, 'sp': nc.sync}
        self.sem = {}
        self.cnt = {}
        for k in ['pe', 'act', 'dve', 'pool']:
            self.sem[k] = nc.alloc_semaphore("sem_" + k)
            self.cnt[k] = 0
        self.dq = {}
        for q in ['sp', 'pool', 'act']:
            self.dq[q] = {'n': ndma, 'next': 0}
            for i in range(ndma):
                self.sem[f"d_{q}_{i}"] = nc.alloc_semaphore(f"dsem_{q}_{i}")
                self.cnt[f"d_{q}_{i}"] = 0
        self.seen = {k: {} for k in self.engs}
        self.lastw = {}
        self.readers = {}
        self.n = 0
        self.pend = {k: False for k in self.engs}
        self.stream = None
        self.queues = {}

    def _deps(self, reads, writes):
        deps = {}

        def add(d):
            if d is None:
                return
            k, v = d
            if deps.get(k, 0) < v:
                deps[k] = v
        for b in reads:
            add(self.lastw.get(b))
        for b in writes:
            add(self.lastw.get(b))
            for r in self.readers.get(b, ()):
                add(r)
        return deps

    def _waits(self, eng, deps):
        for k, v in deps.items():
            if k == 'pe' and eng == 'pe':
                continue
            if k.startswith('d_'):
                v = self.cnt[k]
            if self.seen[eng].get(k, 0) >= v:
                continue
            self.seen[eng][k] = v
            self.engs[eng].wait_ge(self.sem[k], v)
            self.n += 1

    def _mark(self, me, reads, writes):
        for b in reads:
            self.readers.setdefault(b, []).append(me)
        for b in writes:
            self.lastw[b] = me
            self.readers[b] = []

    @staticmethod
    def _is_psum(k):
        return (k[0] == 'B' and (k[1:].isdigit() or k in ('BB', 'BBa', 'BBb'))) or k.startswith('ps')

    def flush(self):
        qs = {k: v for k, v in self.queues.items() if v}
        self.queues = {}
        pos = {k: 0 for k in qs}
        while qs:
            k = min(qs, key=lambda n: pos[n] / len(qs[n]))
            it = qs[k][pos[k]]
            pos[k] += 1
            if it[0] == 'op':
                self.op(*it[1:])
            else:
                self.dma(it[1], it[2], it[3], it[4], it[5], **it[6])
            if pos[k] >= len(qs[k]):
                del qs[k]

    @staticmethod
    def _merge(lists):
        lists = [l for l in lists if l]
        out = []
        pos = [0] * len(lists)
        live = list(range(len(lists)))
        sticky = None
        while live:
            k = sticky if sticky is not None else min(live, key=lambda n: pos[n] / len(lists[n]))
            it = lists[k][pos[k]]
            out.append(it)
            pos[k] += 1
            if it[0] == 'op' and it[1] == 'pe':
                sticky = None if it[5] else k
            if pos[k] >= len(lists[k]):
                live.remove(k)
                sticky = None
        return out

    @staticmethod
    def _ec(eng, out):
        n = max(out.free_size(), 64)
        return 0.12 + n / {'dve': 960.0, 'act': 1400.0, 'pool': 700.0}.get(eng, 960.0)

    def flush_mixer(self, nw):
        qs = self.queues
        self.queues = {}
        HOP = 0.6
        eng_free = {}
        kw_t = {}
        kr_t = {}
        out = []

        def engine_of(it):
            return it[1]

        def start_time(it):
            reads, writes = (it[3], it[4]) if it[0] == 'op' else (it[4], it[5])
            t = 0.0
            for k in reads:
                t = max(t, kw_t.get(k, 0.0))
                if self._is_psum(k):
                    t = max(t, kr_t.get(k, 0.0))
            for k in writes:
                t = max(t, kw_t.get(k, 0.0), kr_t.get(k, 0.0))
            return max(eng_free.get(engine_of(it), 0.0), t + HOP)

        def commit(it):
            st = start_time(it)
            e = engine_of(it)
            if it[0] == 'op':
                c = it[6] if it[6] is not None else 0.6
                fin = st + c
                eng_free[e] = fin
                reads, writes = it[3], it[4]
            else:
                eng_free[e] = st + 0.15
                fin = st + 2.5
                reads, writes = it[4], it[5]
            for k in reads:
                kr_t[k] = max(kr_t.get(k, 0.0), fin)
            for k in writes:
                kw_t[k] = fin
                kr_t[k] = 0.0
            out.append(it)

        m = qs.get('m', [])
        mpos = [0]
        carry = [None]

        def run_group(streams):
            pos = [0] * len(streams)
            sticky = carry[0]
            while True:
                cands = [(si, streams[si][pos[si]]) for si in range(len(streams)) if pos[si] < len(streams[si])]
                if not cands:
                    break
                if mpos[0] < len(m):
                    cands.append(('m', m[mpos[0]]))
                if sticky is not None and any(c[0] == sticky for c in cands):
                    pick = [c for c in cands if c[0] == sticky][0]
                else:
                    pick = min(cands, key=lambda c: (start_time(c[1]), 9 if c[0] == 'm' else c[0]))
                si, it = pick
                commit(it)
                if si == 'm':
                    mpos[0] += 1
                else:
                    pos[si] += 1
                if it[0] == 'op' and it[1] == 'pe':
                    sticky = None if it[5] else si
                if sticky is not None:
                    ended = (mpos[0] >= len(m)) if sticky == 'm' else (pos[sticky] >= len(streams[sticky]))
                    if ended:
                        sticky = None
            carry[0] = 'm' if sticky == 'm' else None

        if nw > 0:
            run_group([qs.get(('rp', 0), [])])
            for i in range(nw):
                run_group([qs.get(('rs', i), []), qs.get(('rp', i + 1), [])])
        while mpos[0] < len(m):
            commit(m[mpos[0]])
            mpos[0] += 1
        for it in out:
            if it[0] == 'op':
                self.op(*it[1:])
            else:
                self.dma(it[1], it[2], it[3], it[4], it[5], **it[6])

    def op(self, eng, fn, reads=(), writes=(), inc=True, cost=None):
        if self.stream is not None:
            self.queues.setdefault(self.stream, []).append(('op', eng, fn, tuple(reads), tuple(writes), inc, cost))
            return
        ex = [k for k in reads if self._is_psum(k)]
        self._waits(eng, self._deps(reads, list(writes) + ex))
        if inc:
            self.cnt[eng] += 1
            fn(self.engs[eng]).then_inc(self.sem[eng], 1)
            self._mark((eng, self.cnt[eng]), reads, writes)
            self.pend[eng] = False
        else:
            fn(self.engs[eng])
            self._mark((eng, self.cnt[eng] + 1), reads, writes)
            self.pend[eng] = True
        self.n += 1

    def dma(self, q, out, in_, reads=(), writes=(), **kw):
        if self.stream is not None:
            self.queues.setdefault(self.stream, []).append(('dma', q, out, in_, tuple(reads), tuple(writes), kw))
            return
        self._waits(q, self._deps(reads, writes))
        d = self.dq[q]
        i = d['next']
        d['next'] = (i + 1) % d['n']
        k = f"d_{q}_{i}"
        self.cnt[k] += 16
        self.engs[q].dma_start(out=out, in_=in_, **kw).then_inc(self.sem[k], 16)
        self._mark((k, self.cnt[k]), reads, writes)
        self.n += 1

    def barrier(self):
        assert not any(self.pend.values()), self.pend
        allv = {k: v for k, v in self.cnt.items() if v > 0}
        for e in self.engs:
            self._waits(e, dict(allv))

    def act(self, out, in_, func, r, w, bias=None, scale=None, accum=None):
        kw = {}
        if bias is not None:
            kw['bias'] = bias
        if scale is not None:
            kw['scale'] = scale
        if accum is not None:
            kw['accum_out'] = accum
        self.op('act', lambda e: e.activation(out=out, in_=in_, func=func, **kw), r, w, cost=self._ec('act', out))

    def tt(self, eng, out, a, b, op, r, w):
        self.op(eng, lambda e: e.tensor_tensor(out=out, in0=a, in1=b, op=op), r, w, cost=self._ec(eng, out))

    def ts(self, eng, out, a, s1, s2, op0, op1, r, w):
        if s2 is None:
            self.op(eng, lambda e: e.tensor_scalar(out=out, in0=a, scalar1=s1, scalar2=None, op0=op0), r, w,
                    cost=self._ec(eng, out))
        else:
            self.op(eng, lambda e: e.tensor_scalar(out=out, in0=a, scalar1=s1, scalar2=s2, op0=op0, op1=op1), r, w,
                    cost=self._ec(eng, out))

    def stt(self, out, a, s, b, op0, op1, r, w):
        self.op('dve', lambda e: e.scalar_tensor_tensor(out=out, in0=a, scalar=s, in1=b, op0=op0, op1=op1), r, w,
                cost=self._ec('dve', out))

    def mm(self, out, lhsT, rhs, start, stop, r, w, inc=None):
        self.op('pe', lambda e: e.matmul(out, lhsT=lhsT, rhs=rhs, start=start, stop=stop), r, w,
                inc=(stop if inc is None else inc), cost=0.06 + max(rhs.free_size(), 32) / 1200.0 * (4 if rhs.dtype == F32 else 1))

    def tr(self, out, in_, ident, r, w, inc=True):
        self.op('pe', lambda e: e.transpose(out, in_, ident), r, w, inc=inc,
                cost=0.06 + max(ident.free_size(), 32) / 1200.0 * (4 if in_.dtype == F32 else 1))

    def cp(self, eng, out, in_, r, w):
        if eng == 'act':
            self.op('act', lambda e: e.activation(out=out, in_=in_, func=AF.Identity), r, w, cost=self._ec('act', out))
        else:
            self.op(eng, lambda e: e.tensor_copy(out=out, in_=in_), r, w, cost=self._ec(eng, out))

    def memset(self, eng, ap, val, w):
        self.op(eng, lambda e: e.memset(ap, val), (), w, cost=self._ec(eng, ap))


def seqT(s):
    return TX if (s % 2) == 1 else TC


def build(debug=False, n_layers=L, stop_after=None, mixcfg=None):
    nc = bass.Bass("TRN2", target_bir_lowering=False)
    em = Em(nc)
    dbgset = debug if isinstance(debug, (set, list, tuple)) else None

    def din(name, shape, dt=F32):
        return nc.dram_tensor(name, list(shape), dt, kind="ExternalInput").ap()

    def dscr(name, shape, dt=F32):
        isdbg = (debug is True) or (dbgset is not None and name.rstrip('0123456789') in dbgset)
        return nc.dram_tensor(name, list(shape), dt, kind="ExternalOutput" if isdbg else "Internal").ap()

    xT = din("xT", [NBL, D, TX])
    ctxT = din("ctxT", [NBL, D, TC])
    cT = din("cT", [D, 3])
    w_mod = din("w_mod", [L, D, 6 * D])
    w_in = din("w_in", [L, D, 3504])
    w_out = din("w_out", [L, D, D])
    r_w2 = din("r_w2", [L, 2, 64, 512])
    r_a2 = din("r_a2", [L, 2, 64, 512])
    r_g2 = din("r_g2", [L, 160, 512])
    f_w_up = din("f_w_up", [L, D, 2 * DFF])
    f_w_down = din("f_w_down", [L, DFF, D])
    cst = din("cst", [128, NCST])
    pv = din("pv", [L, 128, NPV])
    pv64 = din("pv64", [L, 64, NPV64])
    rowv = din("rowv", [L, 1, NROW])
    rmu = din("rmu", [L, 1, 1952])
    outT = nc.dram_tensor("outT", [NBL, D, TX], F32, kind="ExternalOutput").ap()

    NS = 2 * NBL
    RESA = [dscr(f"resa{s}", [D, seqT(s)]) for s in range(NS)]
    RESB = [dscr(f"resb{s}", [D, seqT(s)]) for s in range(NS)]
    XBC = [dscr(f"xbc{s}", [D, seqT(s)]) for s in range(NS)]
    RKV = [dscr(f"rkv{s}", [24, 64, seqT(s)]) for s in range(NS)]
    XWA = [dscr(f"xwa{s}", [4, 64, seqT(s)]) for s in range(NS)]
    XG = [dscr(f"xg{s}", [160, seqT(s)]) for s in range(NS)]
    ZDT = [dscr(f"zdt{s}", [seqT(s), 528]) for s in range(NS)]
    YF = [dscr(f"yf{s}", [seqT(s), 512]) for s in range(NS)]
    OF = [dscr(f"of{s}", [seqT(s), 520]) for s in range(NS)]
    MIX = [dscr(f"mix{s}", [D, seqT(s)], BF16) for s in range(NS)]
    GATE = [dscr(f"gate{s}", [DFF, seqT(s)]) for s in range(NS)]
    VAL = [dscr(f"val{s}", [DFF, seqT(s)], BF16) for s in range(NS)]
    ACTV = [dscr(f"actv{s}", [DFF, seqT(s)], BF16) for s in range(NS)]

    def fm(ap):
        return ap.rearrange("(k p) t -> p k t", p=128)

    def sb(name, shape, dt=F32):
        return nc.alloc_sbuf_tensor(name, list(shape), dt).ap()

    CST = sb("CST", [128, NCST])
    CSTB = sb("CSTB", [128, 768], BF16)
    PV = sb("PV", [128, NPV])
    PV64 = sb("PV64", [64, NPV64])
    ROWB = sb("ROWB", [128, NROW])
    MOD = sb("MOD", [128, 48, 3])
    DER = sb("DER", [128, 4, 8, 3])
    NEGA = sb("NEGA", [128, 16])
    OMMU = sb("OMMU", [64, 8])
    em.dma('sp', CST, cst, writes=['CST'])
    em.cp('dve', CSTB, CST[:, 0:768], ['CST'], ['CSTB'])
    ident = CST[:, C_ID:C_ID + 128]
    identb = CSTB[:, C_ID:C_ID + 128]
    onesb = CSTB[:, C_ONE:C_ONE + 128]
    ones = CST[:, C_ONE:C_ONE + 128]

    def stage_scope():
        return ExitStack()

    def stage_mod(l):
        em.dma('sp', PV, pv[l], writes=['PV'])
        em.dma('sp', PV64, pv64[l], writes=['PV64'])
        em.dma('pool', ROWB, rowv[l].partition_broadcast(128), writes=['ROWB'])
        with ExitStack() as es:
            def S(name, shape, dt=F32):
                return es.enter_context(nc.sbuf_tensor(name, list(shape), dt)).ap()
            cts = S(f"cts{l}", [128, 8, 3])
            sc = S(f"sc{l}", [128, 8, 3])
            wst = [S(f"wmst{l}_{i}", [128, 8, 512]) for i in range(2)]
            ps = es.enter_context(nc.psum_tensor(f"psmod{l}", [128, 512], F32)).ap()
            em.dma('sp', cts, cT.rearrange("(k p) j -> p k j", p=128), writes=['cts'])
            em.act(sc, cts, AF.Silu, ['cts'], ['sc'])
            for g in range(12):
                w = wst[g % 2]
                wk = f"wmst{g % 2}"
                em.dma('sp' if g % 2 == 0 else 'pool', w,
                       w_mod[l][:, g * 512:(g + 1) * 512].rearrange("(k p) n -> p k n", p=128), writes=[wk])
                for mi in range(4):
                    m = g * 4 + mi
                    for k in range(8):
                        em.mm(ps[:, m * 3:(m + 1) * 3], w[:, k, mi * 128:(mi + 1) * 128], sc[:, k, :],
                              k == 0, k == 7, [wk, 'sc'], ['psmod'])
            em.tt('dve', MOD, ps[:, 0:144].rearrange("p (m j) -> p m j", j=3),
                  PV[:, PV_BMOD:PV_BMOD + 48].unsqueeze(2).to_broadcast([128, 48, 3]), ALU.add,
                  ['psmod', 'PV'], ['MOD'])
            tmp = S(f"dertmp{l}", [128, 8, 3])

            def gain(idx, goff, mlo, plus1):
                if plus1:
                    em.ts('dve', tmp, MOD[:, mlo:mlo + 8, :], 1.0, None, ALU.add, None, ['MOD'], ['dertmp'])
                    src = tmp
                    rk = ['dertmp', 'PV']
                else:
                    src = MOD[:, mlo:mlo + 8, :]
                    rk = ['MOD', 'PV']
                em.tt('dve', DER[:, idx, :, :], src,
                      PV[:, goff:goff + 8].unsqueeze(2).to_broadcast([128, 8, 3]), ALU.mult, rk, ['DER'])
            gain(0, PV_GPRE1, 8, True)
            gain(1, PV_GPOST1, 16, False)
            gain(2, PV_GPRE2, 32, True)
            gain(3, PV_GPOST2, 40, False)
            em.act(NEGA, ROWB[:, RV_ALOG:RV_ALOG + 16], AF.Exp, ['ROWB'], ['NEGA'])
            em.ts('dve', NEGA, NEGA, -1.0, None, ALU.mult, None, ['NEGA'], ['NEGA'])
            em.ts('dve', OMMU, PV64[:, P6_KA:P6_KA + 8], -1.0, 1.0, ALU.mult, ALU.add, ['PV64'], ['OMMU'])
            em.barrier()

    def prenorm(xt, TW, h, sq, rstd, ps, jmod, gidx, sidx, kx, kh, kps, sqk='sq', rsk='rstd'):
        em.act(sq[:, :, :TW], xt[:, :, :TW], AF.Square, [kx], [sqk])
        for k in range(8):
            em.mm(ps[:, :TW], onesb, sq[:, k, :TW], k == 0, k == 7, [sqk, 'CSTB'], [kps])
        em.act(rstd[:, :TW], ps[:, :TW], AF.Sqrt, [kps], [rsk], bias=EPS, scale=1.0 / D)
        em.op('dve', lambda e: e.reciprocal(out=rstd[:, :TW], in_=rstd[:, :TW]), [rsk], [rsk])
        for k in range(8):
            em.stt(xt[:, k, :TW], xt[:, k, :TW], DER[:, gidx, k, jmod:jmod + 1], rstd[:, :TW], ALU.mult, ALU.mult,
                   [kx, rsk, 'DER'], [kx])
            em.act(h[:, k, :TW], xt[:, k, :TW], AF.Identity, [kx, 'MOD'], [kh],
                   bias=MOD[:, sidx + k, jmod:jmod + 1], scale=1.0)

    def load_wbf(es, l, wap, Kc, N, name, piece=None):
        wb = es.enter_context(nc.sbuf_tensor(f"{name}bf{l}", [128, Kc, N], BF16)).ap()
        with ExitStack() as e2:
            sts = [e2.enter_context(nc.sbuf_tensor(f"{name}st{l}_{i}", [128, N], F32)).ap() for i in range(2)]
            for k in range(Kc):
                st = sts[k % 2]
                sk = f"{name}st{k % 2}"
                em.dma('sp' if k % 2 == 0 else 'pool', st, wap[k * 128:(k + 1) * 128, :], writes=[sk])
                em.cp('act' if k % 2 == 0 else 'dve', wb[:, k, :], st, [sk], [name + 'bf'])
            em.barrier()
        return wb

    def res_src(l, s, phase):
        b = s // 2
        if phase == 0:
            if l == 0:
                return (xT[b] if s % 2 == 1 else ctxT[b]), f"in{s}"
            return RESB[s], f"resb{s}"
        return RESA[s], f"resa{s}"

    def res_dst(l, s, phase):
        b = s // 2
        if phase == 0:
            return RESA[s], f"resa{s}"
        if l == n_layers - 1 and s % 2 == 1:
            return outT[b], f"out{s}"
        return RESB[s], f"resb{s}"

    def stage_inproj(l):
        with ExitStack() as es:
            def S(name, shape, dt=F32):
                return es.enter_context(nc.sbuf_tensor(name, list(shape), dt)).ap()
            wb = load_wbf(es, l, w_in[l], 8, 3504, "win")
            w2 = S(f"ipw2{l}", [128, 8, 1952], BF16)
            with ExitStack() as e2:
                mub = e2.enter_context(nc.sbuf_tensor(f"ipmub{l}", [128, 1952], F32)).ap()
                em.dma('sp', mub, rmu[l].partition_broadcast(128), writes=['ipmub'])
                for k in range(8):
                    em.stt(w2[:, k, :], mub, 0.5, wb[:, k, 1552:3504], ALU.mult, ALU.mult, ['ipmub', 'winbf'], ['ipw2'])
                em.ts('dve', mub, mub, -1.0, 1.0, ALU.mult, ALU.add, ['ipmub'], ['ipmub'])
                for k in range(8):
                    em.tt('dve', wb[:, k, 1552:3504], wb[:, k, 1552:3504], mub, ALU.mult, ['ipmub', 'winbf', 'ipw2'], ['winbf'])
                em.barrier()
            xt = S(f"ipx{l}", [128, 8, 512])
            h = S(f"iph{l}", [128, 8, 512], BF16)
            sq = S(f"ipsq{l}", [128, 8, 512], BF16)
            hs = sq
            rstd = S(f"iprs{l}", [128, 512])
            xh = S(f"ipxh{l}", [128, 8, 2])
            hh = S(f"iphh{l}", [128, 8, 2], BF16)
            sqh = S(f"ipsqh{l}", [128, 8, 2], BF16)
            rstdh = S(f"iprsh{l}", [128, 2])
            sta = [S(f"ipsta{l}_{i}", [128, 8, 512]) for i in range(2)]
            stw = S(f"ipstw{l}", [64, 4, 512])
            stg0 = S(f"ipstg0{l}", [128, 512])
            stg1 = S(f"ipstg1{l}", [32, 512])
            stz = S(f"ipstz{l}", [128, 4, 528])
            pss = [es.enter_context(nc.psum_tensor(f"ipps{l}_{i}", [128, 512], F32)).ap() for i in range(8)]
            groups = [('xbc', 512, 128, 8), ('r', 1552, 64, 8), ('k', 2064, 64, 8), ('v', 2576, 64, 8)]
            ev = 0
            for s in range(NS):
                T = seqT(s)
                TW = min(512, T)
                jmod = 2 if s % 2 == 0 else s // 2
                src, srck = res_src(l, s, 0)
                for tt_ in range(T // TW):
                    t0 = tt_ * TW
                    em.dma('sp', xt[:, :, :TW], fm(src)[:, :, t0:t0 + TW], reads=[srck], writes=['ipx'])
                    cl = max(t0 - 1, 0)
                    cr = min(t0 + TW, T - 1)
                    em.dma('sp', xh[:, :, 0:1], fm(src)[:, :, cl:cl + 1], reads=[srck], writes=['ipxh'], allow_slow_non_contiguous=True)
                    em.dma('sp', xh[:, :, 1:2], fm(src)[:, :, cr:cr + 1], reads=[srck], writes=['ipxh'], allow_slow_non_contiguous=True)
                    prenorm(xt, TW, h, sq, rstd, pss[7], jmod, 0, 0, 'ipx', 'iph', 'ps7')
                    prenorm(xh, 2, hh, sqh, rstdh, pss[6], jmod, 0, 0, 'ipxh', 'iphh', 'ps6', sqk='ipsqh', rsk='iprsh')
                    if t0 == 0:
                        em.memset('pool', hh[:, :, 0:1], 0.0, ['iphh'])
                    if t0 + TW == T:
                        em.memset('pool', hh[:, :, 1:2], 0.0, ['iphh'])
                    em.tt('dve', hs[:, :, 1:TW - 1], h[:, :, 0:TW - 2], h[:, :, 2:TW], ALU.add, ['iph', 'sq'], ['sq'])
                    em.tt('dve', hs[:, :, 0:1], hh[:, :, 0:1], h[:, :, 1:2], ALU.add, ['iph', 'iphh', 'sq'], ['sq'])
                    em.tt('dve', hs[:, :, TW - 1:TW], h[:, :, TW - 2:TW - 1], hh[:, :, 1:2], ALU.add, ['iph', 'iphh', 'sq'], ['sq'])
                    pi = 0
                    for gi, (gname, c0, wdt, nb) in enumerate(groups):
                        st = sta[gi % 2]
                        stk = f"ipsta{gi % 2}"
                        for j in range(nb):
                            ps = pss[pi % 6]
                            pk = f"ps{pi % 6}"
                            pi += 1
                            cc = c0 + j * wdt
                            shifted = (gname != 'xbc')
                            for k in range(8):
                                em.mm(ps[:wdt, :TW], wb[:, k, cc:cc + wdt], h[:, k, :TW], k == 0, (k == 7) and not shifted,
                                      ['winbf', 'iph'], [pk])
                            if shifted:
                                for k in range(8):
                                    em.mm(ps[:wdt, :TW], w2[:, k, cc - 1552:cc - 1552 + wdt], hs[:, k, :TW], False, k == 7,
                                          ['ipw2', 'sq'], [pk])
                            em.cp('act' if ev % 2 == 0 else 'dve', st[:wdt, j, :TW], ps[:wdt, :TW], [pk], [stk])
                            ev += 1
                        if gname == 'xbc':
                            em.dma('pool', fm(XBC[s])[:, :, t0:t0 + TW], st[:, :, :TW], reads=[stk], writes=[f"xbc{s}"])
                        else:
                            jb = {'r': 0, 'k': 8, 'v': 16}[gname]
                            em.dma('pool', RKV[s][jb:jb + 8].rearrange("j p t -> p j t")[:, :, t0:t0 + TW],
                                   st[:64, :, :TW], reads=[stk], writes=[f"rkv{s}"])
                    for j in range(4):
                        ps = pss[pi % 6]
                        pk = f"ps{pi % 6}"
                        pi += 1
                        cc = 3088 + j * 64
                        for k in range(8):
                            em.mm(ps[:64, :TW], wb[:, k, cc:cc + 64], h[:, k, :TW], k == 0, False, ['winbf', 'iph'], [pk])
                        for k in range(8):
                            em.mm(ps[:64, :TW], w2[:, k, cc - 1552:cc - 1552 + 64], hs[:, k, :TW], False, k == 7, ['ipw2', 'sq'], [pk])
                        em.cp('act' if ev % 2 == 0 else 'dve', stw[:, j, :TW], ps[:64, :TW], [pk], ['ipstw'])
                        ev += 1
                    em.dma('pool', XWA[s].rearrange("j p t -> p j t")[:, :, t0:t0 + TW], stw[:, :, :TW],
                           reads=['ipstw'], writes=[f"xwa{s}"])
                    for (cc, wdt, st, stk, r0) in [(3344, 128, stg0, 'ipstg0', 0), (3472, 32, stg1, 'ipstg1', 128)]:
                        ps = pss[pi % 6]
                        pk = f"ps{pi % 6}"
                        pi += 1
                        for k in range(8):
                            em.mm(ps[:wdt, :TW], wb[:, k, cc:cc + wdt], h[:, k, :TW], k == 0, False, ['winbf', 'iph'], [pk])
                        for k in range(8):
                            em.mm(ps[:wdt, :TW], w2[:, k, cc - 1552:cc - 1552 + wdt], hs[:, k, :TW], False, k == 7, ['ipw2', 'sq'], [pk])
                        em.cp('act' if ev % 2 == 0 else 'dve', st[:wdt, :TW], ps[:wdt, :TW], [pk], [stk])
                        ev += 1
                        em.dma('pool', XG[s][r0:r0 + wdt, t0:t0 + TW], st[:wdt, :TW], reads=[stk], writes=[f"xg{s}"])
                    for i in range(TW // 128):
                        ps = pss[pi % 6]
                        pk = f"ps{pi % 6}"
                        pi += 1
                        ps2 = pss[6]
                        for k in range(8):
                            em.mm(ps[:, 0:512], h[:, k, i * 128:(i + 1) * 128], wb[:, k, 0:512], k == 0, k == 7,
                                  ['winbf', 'iph'], [pk])
                        for k in range(8):
                            em.mm(ps2[:, 0:16], h[:, k, i * 128:(i + 1) * 128], wb[:, k, 1536:1552], k == 0, k == 7,
                                  ['winbf', 'iph'], ['ps6'])
                        em.cp('act', stz[:, i, 0:512], ps[:, 0:512], [pk], ['ipstz'])
                        em.cp('dve', stz[:, i, 512:528], ps2[:, 0:16], ['ps6'], ['ipstz'])
                    em.dma('pool', ZDT[s][t0:t0 + TW, :].rearrange("(i p) c -> p i c", p=128), stz[:, :TW // 128, :],
                           reads=['ipstz'], writes=[f"zdt{s}"])
            em.barrier()

    def stage_mixer(l, need_ctx_out):
        with ExitStack() as es:
            def S(name, shape, dt=F32):
                return es.enter_context(nc.sbuf_tensor(name, list(shape), dt)).ap()
            w2b = S(f"w2b{l}", [64, 2, 512], BF16)
            a2b = S(f"a2b{l}", [64, 2, 512], BF16)
            g2b0 = S(f"g2b0{l}", [128, 512], BF16)
            g2b1 = S(f"g2b1{l}", [32, 512], BF16)
            with ExitStack() as e2:
                t1 = e2.enter_context(nc.sbuf_tensor(f"lst1{l}", [64, 2, 512], F32)).ap()
                t2 = e2.enter_context(nc.sbuf_tensor(f"lst2{l}", [64, 2, 512], F32)).ap()
                t3 = e2.enter_context(nc.sbuf_tensor(f"lst3{l}", [128, 512], F32)).ap()
                t4 = e2.enter_context(nc.sbuf_tensor(f"lst4{l}", [32, 512], F32)).ap()
                em.dma('sp', t1, r_w2[l].rearrange("d r c -> r d c"), writes=['lst1'])
                em.dma('sp', t2, r_a2[l].rearrange("d r c -> r d c"), writes=['lst2'])
                em.dma('sp', t3, r_g2[l][0:128, :], writes=['lst3'])
                em.dma('sp', t4, r_g2[l][128:160, :], writes=['lst4'])
                em.cp('dve', w2b, t1, ['lst1'], ['w2b'])
                em.cp('dve', a2b, t2, ['lst2'], ['a2b'])
                em.cp('dve', g2b0, t3, ['lst3'], ['g2b'])
                em.cp('dve', g2b1, t4, ['lst4'], ['g2b'])
                em.barrier()
            MAR = [None, None]
            MARt = S(f"mar{l}", [128, 2, 256])
            em.cp('dve', MARt[:, 0, 0:128], CST[:, C_UTS:C_UTS + 128], ['CST'], ['MAR'])
            em.cp('dve', MARt[:, 0, 128:256], CST[:, C_UTI:C_UTI + 128], ['CST'], ['MAR'])
            em.cp('dve', MARt[:, 1, 0:128], CST[:, C_LTS:C_LTS + 128], ['CST'], ['MAR'])
            em.cp('dve', MARt[:, 1, 128:256], CST[:, C_LTI:C_LTI + 128], ['CST'], ['MAR'])
            MSO = S(f"mso{l}", [128, 2, 256])
            em.cp('dve', MSO[:, 0, 0:128], CST[:, C_LTS:C_LTS + 128], ['CST'], ['MSO'])
            em.cp('dve', MSO[:, 1, 0:128], CST[:, C_UTS:C_UTS + 128], ['CST'], ['MSO'])
            em.cp('dve', MSO[:, 0, 128:256], ones, ['CST'], ['MSO'])
            em.cp('dve', MSO[:, 1, 128:256], ones, ['CST'], ['MSO'])
            RMK = S(f"rmk{l}", [64, 8, 128], BF16)
            em.memset('pool', RMK, 1.0, ['RMK'])
            em.memset('pool', RMK[:, :, 0:1], 0.0, ['RMK'])

            def MI(d):
                return CST[:, C_UTI:C_UTI + 128] if d == 0 else CST[:, C_LTI:C_LTI + 128]

            def strictTS(d):
                return CST[:, C_LTS:C_LTS + 128] if d == 0 else CST[:, C_UTS:C_UTS + 128]

            xbc = S(f"m_xbc{l}", [128, 8, 130])
            cacc = S(f"m_cacc{l}", [128, 8, 128])
            bct = S(f"m_bct{l}", [128, 4, 128], BF16)
            xtok = S(f"m_xtok{l}", [128, 512])
            xtokb = S(f"m_xtokb{l}", [128, 512], BF16)
            btokb = S(f"m_btokb{l}", [128, 256], BF16)
            zdt = S(f"m_zdt{l}", [128, 528])
            dts = S(f"m_dts{l}", [128, 8])
            dta = S(f"m_dta{l}", [128, 8])
            sm = S(f"m_sm{l}", [128, 40])
            xw = S(f"m_xw{l}", [128, 512], BF16)
            cbt = S(f"m_cbt{l}", [128, 256])
            l2 = [S(f"m_l2{l}_{i}", [128, 256]) for i in range(2)]
            Et = [S(f"m_E{l}_{i}", [128, 256]) for i in range(2)]
            LTt = [S(f"m_LT{l}_{i}", [128, 128]) for i in range(2)]
            STt = [S(f"m_ST{l}_{i}", [128, 128], BF16) for i in range(2)]
            CsT = [S(f"m_Cs{l}_{i}", [128, 128], BF16) for i in range(2)]
            hst = S(f"m_hst{l}", [128, 512])
            hstb = S(f"m_hstb{l}", [128, 512], BF16)
            yt = S(f"m_y{l}", [128, 512])
            yf = S(f"m_yf{l}", [128, 512])
            zs = S(f"m_zs{l}", [128, 512])
            ysq = S(f"m_ysq{l}", [128, 512])
            gst = S(f"m_gst{l}", [128, 4])
            mixo = S(f"mixo{l}", [128, 8, 128], BF16)
            ssum = S(f"r_ssum{l}", [64, 26, 128])
            pp = ssum
            xg0 = S(f"r_xg0{l}", [128, 130])
            xg1 = S(f"r_xg1{l}", [32, 130])
            sg0 = S(f"r_sg0{l}", [128, 128], BF16)
            sg1 = S(f"r_sg1{l}", [32, 128], BF16)
            xgt = S(f"r_xgt{l}", [128, 128])
            twb = S(f"r_twb{l}", [64, 2, 128], BF16)
            lw = S(f"r_lw{l}", [64, 8, 128])
            aa = S(f"r_aa{l}", [64, 8, 128])
            kk = S(f"r_kk{l}", [64, 8, 128])
            kd = S(f"r_kd{l}", [64, 8, 128])
            linc = S(f"r_linc{l}", [64, 8, 128])
            lex = S(f"r_lex{l}", [64, 8, 128])
            rinv = lex
            e1 = S(f"r_e1{l}", [64, 8, 128])
            e0 = S(f"r_e0{l}", [64, 8, 128])
            ei = S(f"r_ei{l}", [64, 8, 128])
            gCs = [S(f"r_gC{l}_{i}", [64, 8]) for i in range(2)]
            coefs = [S(f"r_coef{l}_{i}", [128, 8]) for i in range(2)]
            tmpk = S(f"r_tmpk{l}", [64, 8, 128])
            ARs = [S(f"r_AR{l}_{i}", [64, 8, 256], BF16) for i in range(2)]
            BKs = [S(f"r_BK{l}_{i}", [64, 8, 2, 128], BF16) for i in range(2)]
            prodb = S(f"r_prod{l}", [64, 8, 128], BF16)
            sqb = prodb
            vtokbs = [S(f"r_vtokb{l}_{i}", [128, 512], BF16) for i in range(2)]
            BKtoks = [S(f"r_BKtok{l}_{i}", [128, 8, 2, 64], BF16) for i in range(2)]
            GBs = [S(f"r_GB{l}_{i}", [128, 8, 256], BF16) for i in range(2)]
            GKs = [S(f"r_GK{l}_{i}", [128, 8, 256], BF16) for i in range(2)]
            Q0s = [S(f"r_Q0{l}_{i}", [128, 8, 128], BF16) for i in range(2)]
            Pm = [S(f"r_P{l}_{i}", [128, 8, 128], BF16) for i in range(2)]
            Qm = [S(f"r_Q{l}_{i}", [128, 8, 128], BF16) for i in range(2)]
            Ym = [S(f"r_Y{l}_{i}", [128, 8, 128], BF16) for i in range(2)]
            E1m = S(f"r_E1{l}", [128, 8, 128], BF16)
            E2m = S(f"r_E2{l}", [128, 8, 128], BF16)
            Dm = S(f"r_D{l}", [128, 8, 128], BF16)
            Zm = S(f"r_Z{l}", [128, 8, 128], BF16)
            Wt = S(f"r_W{l}", [128, 512], BF16)
            Ut = S(f"r_U{l}", [128, 512], BF16)
            Hs = S(f"r_H{l}", [64, 512])
            Hb = S(f"r_Hb{l}", [64, 512], BF16)
            ot = S(f"r_o{l}", [128, 520])
            oft = S(f"r_of{l}", [128, 520])
            osq = S(f"r_osq{l}", [128, 512])
            gn = S(f"r_gn{l}", [128, 40])
            B = [es.enter_context(nc.psum_tensor(f"mxps{l}_{i}", [128, 512], F32)).ap() for i in range(7)]
            BBp = es.enter_context(nc.psum_tensor(f"mxpsb{l}", [128, 1024], BF16)).ap()

            nbw = ROWB[:, RV_MNW:RV_MNW + 512]
            lnw = ROWB[:, RV_LNW:RV_LNW + 512]
            lnb = ROWB[:, RV_LNB:RV_LNB + 512]
            mdb = ROWB[:, RV_MD:RV_MD + 8]
            evc = [0]

            def evq():
                evc[0] += 1
                return 'act' if evc[0] % 2 == 0 else 'dve'

            def load_halo(tile, tk, srcap, srck, t0, T, nblk_dims):
                lo = max(t0 - 1, 0)
                hi = min(t0 + 129, T)
                o = lo - (t0 - 1)
                if nblk_dims:
                    em.dma('sp', tile[:, :, o:o + hi - lo], srcap[:, :, lo:hi], reads=[srck], writes=[tk])
                    if t0 == 0:
                        em.memset('pool', tile[:, :, 0:1], 0.0, [tk])
                    if t0 + 128 == T:
                        em.memset('pool', tile[:, :, 129:130], 0.0, [tk])
                else:
                    em.dma('sp', tile[:, o:o + hi - lo], srcap[:, lo:hi], reads=[srck], writes=[tk])
                    if t0 == 0:
                        em.memset('pool', tile[:, 0:1], 0.0, [tk])
                    if t0 + 128 == T:
                        em.memset('pool', tile[:, 129:130], 0.0, [tk])

            def mamba_chunk(s, t0, d, emit_out):
                T = seqT(s)
                load_halo(xbc, 'm_xbc', fm(XBC[s]), f"xbc{s}", t0, T, True)
                em.dma('pool', zdt, ZDT[s][t0:t0 + 128, :], reads=[f"zdt{s}"], writes=['m_zdt'])
                if (mixcfg or {}).get('mstop', 99) <= 1:
                    return
                for j in range(8):
                    em.ts('dve', cacc[:, j, :], xbc[:, j, 1:129], PV[:, PV_MCW + 8 + j:PV_MCW + 9 + j],
                          PV[:, PV_MCB + j:PV_MCB + j + 1], ALU.mult, ALU.add, ['m_xbc', 'PV'], ['m_cacc'])
                    em.stt(cacc[:, j, :], xbc[:, j, 0:128], PV[:, PV_MCW + j:PV_MCW + j + 1], cacc[:, j, :],
                           ALU.mult, ALU.add, ['m_xbc', 'PV', 'm_cacc'], ['m_cacc'])
                    em.stt(cacc[:, j, :], xbc[:, j, 2:130], PV[:, PV_MCW + 16 + j:PV_MCW + 17 + j], cacc[:, j, :],
                           ALU.mult, ALU.add, ['m_xbc', 'PV', 'm_cacc'], ['m_cacc'])
                em.act(cacc[:, 0:6, :], cacc[:, 0:6, :], AF.Silu, ['m_cacc'], ['m_cacc'])
                em.act(bct[:, 2:4, :], cacc[:, 6:8, :], AF.Silu, ['m_cacc'], ['m_bct'])
                em.cp('dve', bct[:, 0:2, :], cacc[:, 4:6, :], ['m_cacc'], ['m_bct'])
                if (mixcfg or {}).get('mstop', 99) <= 2:
                    return
                msub = (mixcfg or {}).get('msub', 9)
                for j in range(4):
                    em.tr(B[4][:, j * 128:(j + 1) * 128], cacc[:, j, :], ident, ['m_cacc', 'CST'], ['B4'], inc=(j == 3))
                if msub >= 1:
                    for j in range(2):
                        em.tr(B[5][:, j * 128:(j + 1) * 128], cacc[:, 4 + j, :], ident, ['m_cacc', 'CST'], ['B5'])
                if msub >= 2 and msub != 33:
                    em.cp('dve', xtok, B[4], ['B4'], ['m_xtok'])
                if msub == 30:
                    em.cp('dve', xtokb, B[4], ['B4'], ['m_xtokb'])
                elif msub == 31:
                    em.cp('act', yt, B[4], ['B4'], ['m_y'])
                elif msub == 32:
                    em.cp('act', xtokb, xtok, ['m_xtok'], ['m_xtokb'])
                elif msub >= 3:
                    em.cp('act', xtokb, B[4], ['B4'], ['m_xtokb'])
                if msub >= 4:
                    em.cp('act', btokb, B[5][:, 0:256], ['B5'], ['m_btokb'])
                if (mixcfg or {}).get('mstop', 99) <= 3:
                    return
                em.tt('dve', dts, zdt[:, 512 + d * 8:520 + d * 8], ROWB[:, RV_DTB + d * 8:RV_DTB + d * 8 + 8], ALU.add,
                      ['m_zdt', 'ROWB'], ['m_dts'])
                em.act(dts, dts, AF.Exp, ['m_dts'], ['m_dts'])
                em.act(dts, dts, AF.Ln, ['m_dts'], ['m_dts'], bias=1.0, scale=1.0)
                em.tt('dve', dta, dts, NEGA[:, d * 8:d * 8 + 8], ALU.mult, ['m_dts', 'NEGA'], ['m_dta'])
                if (mixcfg or {}).get('mstop', 99) <= 4:
                    return
                em.mm(B[5][:, 256:264], MI(d), dta, True, True, ['CST', 'm_dta'], ['B5'])
                em.mm(B[5][:, 264:272], ones, dta, True, True, ['CST', 'm_dta'], ['B5'])
                em.cp('act', sm[:, 32:40], B[5][:, 264:272], ['B5'], ['m_sm'])
                em.tt('dve', sm[:, 24:32], sm[:, 32:40], B[5][:, 256:264], ALU.subtract, ['B5', 'm_sm'], ['m_sm'])
                em.act(sm[:, 0:8], sm[:, 24:32], AF.Exp, ['m_sm'], ['m_sm'])
                em.tt('dve', sm[:, 8:16], sm[:, 0:8], dts, ALU.mult, ['m_sm', 'm_dts'], ['m_sm'])
                em.act(sm[:, 16:24], B[5][:, 264:272], AF.Exp, ['B5'], ['m_sm'])
                em.tt('dve', xw.rearrange("p (h q) -> p h q", q=64), xtok.rearrange("p (h q) -> p h q", q=64),
                      sm[:, 8:16].unsqueeze(2).to_broadcast([128, 8, 64]), ALU.mult, ['m_xtok', 'm_sm'], ['m_xw'])
                if (mixcfg or {}).get('mstop', 99) <= 5:
                    return
                for g in range(2):
                    em.mm(B[5][:, g * 128:(g + 1) * 128], bct[:, g, :], bct[:, 2 + g, :], True, True, ['m_bct'], ['B5'], inc=(g == 1))
                em.cp('act', cbt, B[5][:, 0:256], ['B5'], ['m_cbt'])
                for h in range(8):
                    g = h // 4
                    i2 = h % 2
                    pe_ = B[5][:, 256:512] if i2 == 0 else B[5][:, 0:256]
                    pek = 'B5'
                    em.ts('dve', l2[i2], MSO[:, d, :], dta[:, h:h + 1], None, ALU.mult, None,
                          ['MSO', 'm_dta'], [f"m_l2{i2}"])
                    em.mm(pe_[:, 0:128], l2[i2][:, 0:128], MI(d), True, True, [f"m_l2{i2}", 'CST'], [pek])
                    em.mm(pe_[:, 128:256], l2[i2][:, 128:256], MI(d), True, True, [f"m_l2{i2}", 'CST'], [pek])
                    em.act(Et[i2], pe_[:, 0:256], AF.Exp, [pek], [f"m_E{i2}"])
                    em.stt(LTt[i2], Et[i2][:, 0:128], dts[:, h:h + 1], MI(d), ALU.mult, ALU.mult,
                           [f"m_E{i2}", 'm_dts', 'CST'], [f"m_LT{i2}"])
                    em.tt('dve', STt[i2], cbt[:, g * 128:(g + 1) * 128], LTt[i2], ALU.mult, ['m_cbt', f"m_LT{i2}"],
                          [f"m_ST{i2}"])
                    em.tt('dve', CsT[i2], bct[:, 2 + g, :], Et[i2][:, 128:256], ALU.mult, ['m_bct', f"m_E{i2}"],
                          [f"m_Cs{i2}"])
                    em.mm(B[4][:, h * 64:(h + 1) * 64], STt[i2], xtokb[:, h * 64:(h + 1) * 64], True, False,
                          [f"m_ST{i2}", 'm_xtokb'], ['B4'])
                    em.mm(B[4][:, h * 64:(h + 1) * 64], CsT[i2], hstb[:, h * 64:(h + 1) * 64], False, True,
                          [f"m_Cs{i2}", 'm_hstb'], ['B4'])
                if (mixcfg or {}).get('mstop', 99) <= 6:
                    return
                for g in range(2):
                    em.mm(B[5][:, g * 256:(g + 1) * 256], btokb[:, g * 128:(g + 1) * 128], xw[:, g * 256:(g + 1) * 256],
                          True, True, ['m_btokb', 'm_xw'], ['B5'], inc=(g == 1))
                em.tt('dve', hst.rearrange("p (h q) -> p h q", q=64), hst.rearrange("p (h q) -> p h q", q=64),
                      sm[:, 16:24].unsqueeze(2).to_broadcast([128, 8, 64]), ALU.mult, ['m_hst', 'm_sm', 'B4'], ['m_hst'])
                em.tt('dve', hst, hst, B[5], ALU.add, ['m_hst', 'B5'], ['m_hst'])
                em.cp('act', hstb, hst, ['m_hst', 'B4'], ['m_hstb'])
                if (mixcfg or {}).get('mstop', 99) <= 7:
                    return
                if d == 0:
                    if emit_out:
                        em.cp('act', yt, B[4], ['B4'], ['m_y'])
                        em.dma('pool', YF[s][t0:t0 + 128, :], yt, reads=['m_y'], writes=[f"yf{s}"])
                elif emit_out:
                    em.dma('sp', yf, YF[s][t0:t0 + 128, :], reads=[f"yf{s}"], writes=['m_yf'])
                    em.tt('dve', yt, B[4], yf, ALU.add, ['B4', 'm_yf'], ['m_y'])
                    em.tt('dve', yf.rearrange("p (h q) -> p h q", q=64), xtok.rearrange("p (h q) -> p h q", q=64),
                          mdb.unsqueeze(2).to_broadcast([128, 8, 64]), ALU.mult, ['m_xtok', 'ROWB', 'm_yf'], ['m_yf'])
                    em.tt('dve', yt, yt, yf, ALU.add, ['m_y', 'm_yf'], ['m_y'])
                    em.act(zs, zdt[:, 0:512], AF.Silu, ['m_zdt'], ['m_zs'])
                    em.tt('dve', yt, yt, zs, ALU.mult, ['m_y', 'm_zs'], ['m_y'])
                    for g in range(2):
                        em.act(ysq[:, g * 256:(g + 1) * 256], yt[:, g * 256:(g + 1) * 256], AF.Square, ['m_y'],
                               ['m_ysq', 'm_gst'], accum=gst[:, g:g + 1])
                    em.act(gst[:, 2:4], gst[:, 0:2], AF.Sqrt, ['m_gst'], ['m_gst'], bias=EPS, scale=1.0 / 256)
                    em.op('dve', lambda e: e.reciprocal(out=gst[:, 2:4], in_=gst[:, 2:4]), ['m_gst'], ['m_gst'])
                    for g in range(2):
                        em.stt(yt[:, g * 256:(g + 1) * 256], yt[:, g * 256:(g + 1) * 256], gst[:, 2 + g:3 + g],
                               nbw[:, g * 256:(g + 1) * 256], ALU.mult, ALU.mult, ['m_y', 'm_gst', 'ROWB'], ['m_y'])
                    for j in range(4):
                        em.tr(B[5][:, j * 128:(j + 1) * 128], yt[:, j * 128:(j + 1) * 128], ident, ['m_y', 'CST'], ['B5'], inc=(j == 3))
                    em.cp('act', mixo[:, 0:4, :], B[5].rearrange("p (j t) -> p j t", t=128), ['B5'], ['mixo_m'])
                    em.dma('pool', fm(MIX[s])[:, 0:4, t0:t0 + 128], mixo[:, 0:4, :], reads=['mixo_m'], writes=[f"mixm{s}"])

            def wkv_chunk(s, t0, d, emit_out, idx=0, first_of_d=False, split=True):
                T = seqT(s)
                pb = idx % 2
                AR, BK, GB, GK, Q0, BKtok, vtokb, gC, coef = (ARs[pb], BKs[pb], GBs[pb], GKs[pb], Q0s[pb], BKtoks[pb],
                                                             vtokbs[pb], gCs[pb], coefs[pb])
                kAR, kBK, kGB, kGK, kQ0, kBKtok, kvtokb, kgC, kcoef = [f"{n}{pb}" for n in
                                                                      ('r_AR', 'r_BK', 'r_GB', 'r_GK', 'r_Q0h', 'r_BKtok',
                                                                       'r_vtokb', 'r_gC', 'r_coef')]
                if split:
                    em.stream = ('rp', idx)
                fin = (d == 1 and emit_out)
                em.dma('sp', pp[:, 0:24, :], RKV[s].rearrange("j p t -> p j t")[:, :, t0:t0 + 128], reads=[f"rkv{s}"], writes=['r_ssum'])
                em.dma('sp', pp[:, 24:25, :], XWA[s][d:d + 1].rearrange("j p t -> p j t")[:, :, t0:t0 + 128], reads=[f"xwa{s}"],
                       writes=['r_ssum'])
                em.dma('sp', pp[:, 25:26, :], XWA[s][2 + d:3 + d].rearrange("j p t -> p j t")[:, :, t0:t0 + 128], reads=[f"xwa{s}"],
                       writes=['r_ssum'])
                rr = pp[:, 0:8, :]
                kr = pp[:, 8:16, :]
                vr = pp[:, 16:24, :]
                em.act(twb[:, 0, :], pp[:, 24, :], AF.Tanh, ['r_ssum'], ['r_twb'])
                em.cp('dve', twb[:, 1, :], pp[:, 25, :], ['r_ssum'], ['r_twb'])
                for h in range(8):
                    em.mm(B[h // 4][0:64, (h % 4) * 128:(h % 4 + 1) * 128], w2b[:, d, h * 64:(h + 1) * 64], twb[:, 0, :], True, True,
                          ['w2b', 'r_twb'], [f"B{h // 4}"], inc=(h == 7))
                for h in range(8):
                    em.act(lw[:, h, :], B[h // 4][0:64, (h % 4) * 128:(h % 4 + 1) * 128], AF.Sigmoid, [f"B{h // 4}", 'PV64'],
                           ['r_lw'], bias=PV64[:, P6_W0 + d * 8 + h:P6_W0 + d * 8 + h + 1], scale=1.0)
                for h in range(8):
                    em.mm(B[h // 4][0:64, (h % 4) * 128:(h % 4 + 1) * 128], a2b[:, d, h * 64:(h + 1) * 64], twb[:, 1, :], True, True,
                          ['a2b', 'r_twb'], [f"B{h // 4}"], inc=(h == 7))
                for h in range(8):
                    em.act(aa[:, h, :], B[h // 4][0:64, (h % 4) * 128:(h % 4 + 1) * 128], AF.Sigmoid,
                           [f"B{h // 4}", 'PV64'], ['r_aa'], bias=PV64[:, P6_A0 + d * 8 + h:P6_A0 + d * 8 + h + 1], scale=1.0)
                em.ts('dve', lw, lw, -R_DECAY_SCALE, None, ALU.mult, None, ['r_lw'], ['r_lw'])
                em.tt('dve', kk, kr, PV64[:, P6_KK:P6_KK + 8].unsqueeze(2).to_broadcast([64, 8, 128]), ALU.mult,
                      ['r_ssum', 'PV64'], ['r_kk'])
                em.act(sqb, kk, AF.Square, ['r_kk'], ['r_prod'])
                for hh in range(2):
                    em.mm(B[hh][0:64, :], onesb[0:64, 0:64], sqb[:, hh * 4:(hh + 1) * 4, :], True, True,
                          ['CSTB', 'r_prod'], [f"B{hh}"])
                for hh in range(2):
                    em.act(rinv[:, hh * 4:(hh + 1) * 4, :], B[hh][0:64, :].rearrange("p (h t) -> p h t", t=128), AF.Sqrt,
                           [f"B{hh}"], ['r_lex'])
                em.ts('dve', rinv, rinv, 1e-12, None, ALU.max, None, ['r_lex'], ['r_lex'])
                em.op('dve', lambda e: e.reciprocal(out=rinv, in_=rinv), ['r_lex'], ['r_lex'])
                em.tt('dve', kk, kk, rinv, ALU.mult, ['r_kk', 'r_lex'], ['r_kk'])
                em.tt('dve', tmpk, aa, PV64[:, P6_KA:P6_KA + 8].unsqueeze(2).to_broadcast([64, 8, 128]), ALU.mult,
                      ['r_aa', 'PV64'], ['r_tmpk'])
                em.tt('dve', tmpk, tmpk, OMMU.unsqueeze(2).to_broadcast([64, 8, 128]), ALU.add, ['r_tmpk', 'OMMU'], ['r_tmpk'])
                em.tt('dve', kd, kr, tmpk, ALU.mult, ['r_ssum', 'r_tmpk'], ['r_kd'])
                em.op('dve', lambda e: e.tensor_tensor_scan(out=linc.rearrange("p h t -> p (h t)"),
                                                            data0=RMK.rearrange("p h t -> p (h t)"),
                                                            data1=lw.rearrange("p h t -> p (h t)"), initial=0.0,
                                                            op0=ALU.mult, op1=ALU.add), ['RMK', 'r_lw'], ['r_linc'])
                if d == 0:
                    tot = linc[:, :, 127:128]
                else:
                    em.tt('dve', lex, lw, linc, ALU.subtract, ['r_lw', 'r_linc'], ['r_lex'])
                    em.cp('dve', gC, linc[:, :, 127], ['r_linc'], [kgC])
                    em.tt('dve', linc, lex, gC.unsqueeze(2).to_broadcast([64, 8, 128]), ALU.add, ['r_lex', kgC, 'r_linc'],
                          ['r_linc'])
                    tot = linc[:, :, 0:1]
                em.tt('dve', lex, linc, lw, ALU.subtract, ['r_linc', 'r_lw'], ['r_lex'])
                em.act(e1, linc, AF.Exp, ['r_linc'], ['r_e1'])
                em.act(e0, lex, AF.Exp, ['r_lex'], ['r_e0'])
                em.act(ei, linc, AF.Exp, ['r_linc'], ['r_ei'], scale=-1.0)
                em.act(gC, tot.rearrange("p h o -> p (h o)"), AF.Exp, ['r_linc', kgC], [kgC])
                em.tt('dve', AR[:, :, 128:256], rr, e1, ALU.mult, ['r_ssum', 'r_e1'], [kAR])
                em.stt(AR[:, :, 0:128], kk, -1.0, e0, ALU.mult, ALU.mult, ['r_kk', 'r_e0'], [kAR])
                em.tt('dve', tmpk, kk, aa, ALU.mult, ['r_kk', 'r_aa', 'r_tmpk'], ['r_tmpk'])
                em.tt('dve', BK[:, :, 0, :], tmpk, ei, ALU.mult, ['r_tmpk', 'r_ei'], [kBK])
                em.tt('dve', BK[:, :, 1, :], kd, ei, ALU.mult, ['r_kd', 'r_ei'], [kBK])
                em.tt('dve', tmpk, rr, kd, ALU.mult, ['r_ssum', 'r_kd', 'r_tmpk'], ['r_tmpk'])
                em.tt('dve', prodb, tmpk, PV64[:, P6_RK:P6_RK + 8].unsqueeze(2).to_broadcast([64, 8, 128]), ALU.mult,
                      ['r_tmpk', 'PV64'], ['r_prod'])
                for h in range(8):
                    em.tr(B[0][:, h * 64:(h + 1) * 64], vr[:, h, :], ident[0:64, 0:64], ['r_ssum', 'CST'], ['B0'], inc=(h == 7))
                em.cp('act', vtokb, B[0], ['B0'], [kvtokb])
                for half in range(2):
                    for h4 in range(4):
                        for q in range(2):
                            em.tr(BBp[:, (h4 * 2 + q) * 64:(h4 * 2 + q + 1) * 64], BK[:, half * 4 + h4, q, :], identb[0:64, 0:64],
                                  [kBK, 'CSTB'], ['BB'], inc=(h4 == 3 and q == 1))
                    em.cp('dve', BKtok[:, half * 4:(half + 1) * 4].rearrange("p h q k -> p (h q k)"), BBp[:, 0:512], ['BB'], [kBKtok])
                for h in range(8):
                    em.mm(B[1][:, 256 + h:257 + h], prodb[:, h, :], onesb[0:64, 0:1], True, True, ['r_prod', 'CSTB'], ['B1'], inc=(h == 7))
                em.cp('act', coef, B[1][:, 256:264], ['B1'], [kcoef])
                for hp in range(4):
                    bb_ = B[0]
                    bbk = "B0"
                    bk2 = B[1]
                    bk2k = "B1"
                    for q in range(2):
                        h = hp * 2 + q
                        em.mm(bb_[:, q * 256:(q + 1) * 256], BK[:, h, 0, :], AR[:, h, :], True, True, [kBK, kAR, 'r_lw', 'r_aa'], [bbk], inc=(q == 1))
                        em.mm(bk2[:, q * 256:(q + 1) * 256], BK[:, h, 1, :], AR[:, h, :], True, True, [kBK, kAR], [bk2k], inc=(q == 1))
                    em.tt('dve', GB[:, hp * 2:hp * 2 + 2, :], bb_.rearrange("p (q c) -> p q c", c=256),
                          MARt[:, d:d + 1, :].to_broadcast([128, 2, 256]), ALU.mult, [bbk, 'MAR'], [kGB])
                    em.tt('dve', Q0[:, hp * 2:hp * 2 + 2, :], bb_.rearrange("p (q c) -> p q c", c=256)[:, :, 0:128],
                          CST[:, C_MP0 + (1 - d) * 128:C_MP0 + (2 - d) * 128].unsqueeze(1).to_broadcast([128, 2, 128]), ALU.mult,
                          [bbk, 'CST'], [kQ0])
                    em.tt('dve', GK[:, hp * 2:hp * 2 + 2, :], bk2.rearrange("p (q c) -> p q c", c=256),
                          MARt[:, d:d + 1, :].to_broadcast([128, 2, 256]), ALU.mult, [bk2k, 'MAR'], [kGK])
                if split:
                    em.stream = ('rs', idx)
                if first_of_d:
                    em.memset('pool', Hs, 0.0, ['r_H'])
                    em.memset('pool', Hb, 0.0, ['r_Hb'])
                for hh in range(2):
                    bp = B[2 + hh]
                    bpk = f"B{2 + hh}"
                    for q in range(4):
                        h = hh * 4 + q
                        em.mm(bp[:, q * 128:(q + 1) * 128], AR[:, h, 0:128], BK[:, h, 0, :], True, True, [kAR, kBK], [bpk], inc=(q == 3))
                    b3 = bp.rearrange("p (q c) -> p q c", c=128)
                    hsl = slice(hh * 4, (hh + 1) * 4)
                    em.tt('dve', Pm[0][:, hsl, :], b3, CST[:, C_MP0 + d * 128:C_MP0 + (d + 1) * 128].unsqueeze(1).to_broadcast([128, 4, 128]),
                          ALU.mult, [bpk, 'CST'], ['r_P0'])
                    em.tt('dve', E1m[:, hsl, :], b3, CST[:, C_ME1 + d * 128:C_ME1 + (d + 1) * 128].unsqueeze(1).to_broadcast([128, 4, 128]),
                          ALU.mult, [bpk, 'CST'], ['r_E1'])
                    em.tt('dve', E2m[:, hsl, :], b3, CST[:, C_ME2 + d * 128:C_ME2 + (d + 1) * 128].unsqueeze(1).to_broadcast([128, 4, 128]),
                          ALU.mult, [bpk, 'CST'], ['r_E2'])
                em.tt('dve', Ym[0], Q0, identb.unsqueeze(1).to_broadcast([128, 8, 128]), ALU.add, [kQ0, 'CSTB'], ['r_Y0'])
                em.cp('act', Qm[0], Q0, [kQ0], ['r_Q0'])
                cur = 0
                for lev in range(1, 5):
                    nxt = 1 - cur
                    for hh in range(2):
                        bp = B[2]
                        bpk = "B2"
                        bq = B[3]
                        bqk = "B3"
                        hsl = slice(hh * 4, (hh + 1) * 4)
                        for q in range(4):
                            h = hh * 4 + q
                            em.mm(bp[:, q * 128:(q + 1) * 128], Qm[cur][:, h, :], Pm[cur][:, h, :], True, True,
                                  [f"r_Q{cur}", f"r_P{cur}"], [bpk], inc=(q == 3))
                        for q in range(4):
                            h = hh * 4 + q
                            em.mm(bq[:, q * 128:(q + 1) * 128], Pm[cur][:, h, :], Qm[cur][:, h, :], True, True,
                                  [f"r_Q{cur}", f"r_P{cur}"], [bqk], inc=(q == 3))
                        em.cp('act', Pm[nxt][:, hsl, :], bp.rearrange("p (q c) -> p q c", c=128), [bpk], [f"r_P{nxt}"])
                        em.cp('dve', Qm[nxt][:, hsl, :], bq.rearrange("p (q c) -> p q c", c=128), [bqk], [f"r_Q{nxt}"])
                    for hh in range(2):
                        by = B[6]
                        byk = "B6"
                        hsl = slice(hh * 4, (hh + 1) * 4)
                        for q in range(4):
                            h = hh * 4 + q
                            em.mm(by[:, q * 128:(q + 1) * 128], Pm[nxt][:, h, :], Ym[cur][:, h, :], True, True,
                                  [f"r_P{nxt}", f"r_Y{cur}"], [byk], inc=(q == 3))
                        em.tt('dve', Ym[nxt][:, hsl, :], by.rearrange("p (q c) -> p q c", c=128), Ym[cur][:, hsl, :], ALU.add,
                              [byk, f"r_Y{cur}"], [f"r_Y{nxt}"])
                    cur = nxt
                Dt = Ym[cur]
                dtk = f"r_Y{cur}"
                for st, (Em_, ek) in enumerate([(E1m, 'r_E1'), (E2m, 'r_E2')]):
                    oth = Ym[1 - cur]
                    othk = f"r_Y{1 - cur}"
                    for half in range(2):
                        for h4 in range(4):
                            em.tr(BBp[:, 512 + h4 * 128:512 + (h4 + 1) * 128], Dt[:, half * 4 + h4, :], identb, [dtk, 'CSTB'], ['BB'], inc=(h4 == 3))
                        em.cp('act', Dm[:, half * 4:(half + 1) * 4, :], BBp[:, 512:1024].rearrange("p (h c) -> p h c", c=128),
                              ['BB'], ['r_D'])
                    for hh in range(2):
                        bz = B[2 + hh]
                        bzk = f"B{2 + hh}"
                        hsl = slice(hh * 4, (hh + 1) * 4)
                        for q in range(4):
                            h = hh * 4 + q
                            em.mm(bz[:, q * 128:(q + 1) * 128], Em_[:, h, :], Dt[:, h, :], True, True, [ek, dtk], [bzk], inc=(q == 3))
                        em.cp('act' if hh == 0 else 'dve', Zm[:, hsl, :], bz.rearrange("p (q c) -> p q c", c=128), [bzk], ['r_Z'])
                    for hh in range(2):
                        by = B[2 + hh]
                        byk = f"B{2 + hh}"
                        hsl = slice(hh * 4, (hh + 1) * 4)
                        for q in range(4):
                            h = hh * 4 + q
                            em.mm(by[:, q * 128:(q + 1) * 128], Dm[:, h, :], Zm[:, h, :], True, True, ['r_D', 'r_Z'], [byk], inc=(q == 3))
                        em.tt('dve', oth[:, hsl, :], by.rearrange("p (q c) -> p q c", c=128), Dt[:, hsl, :], ALU.add,
                              [byk, dtk], [othk])
                    cur = 1 - cur
                    Dt = Ym[cur]
                    dtk = f"r_Y{cur}"
                TT_ = Dt
                ttk = dtk
                for h in range(8):
                    hs_ = slice(h * 64, (h + 1) * 64)
                    em.mm(B[2][:, hs_], AR[:, h, 0:128], Hb[:, hs_], True, False, [kAR, 'r_Hb'], ['B2'])
                    em.mm(B[2][:, hs_], GK[:, h, 0:128], vtokb[:, hs_], False, True, [kGK, kvtokb], ['B2'], inc=(h == 7))
                em.cp('act', Wt, B[2], ['B2'], ['r_W'])
                for h in range(8):
                    hs_ = slice(h * 64, (h + 1) * 64)
                    em.mm(B[3][:, hs_], TT_[:, h, :], Wt[:, hs_], True, True, [ttk, 'r_W'], ['B3'], inc=(h == 7))
                em.cp('dve', Ut, B[3], ['B3'], ['r_U'])
                for h in range(8):
                    hs_ = slice(h * 64, (h + 1) * 64)
                    em.mm(B[2][:, hs_], AR[:, h, 128:256], Hb[:, hs_], True, False, [kAR, 'r_Hb'], ['B2'])
                    em.mm(B[2][:, hs_], GB[:, h, 128:256], Ut[:, hs_], False, False, [kGB, 'r_U'], ['B2'])
                    em.mm(B[2][:, hs_], GK[:, h, 128:256], vtokb[:, hs_], False, True, [kGK, kvtokb], ['B2'], inc=(h == 7))
                for h in range(8):
                    hs_ = slice(h * 64, (h + 1) * 64)
                    em.mm(B[3][0:64, hs_], BKtok[:, h, 0, :], Ut[:, hs_], True, False, [kBKtok, 'r_U'], ['B3'])
                    em.mm(B[3][0:64, hs_], BKtok[:, h, 1, :], vtokb[:, hs_], False, True, [kBKtok, kvtokb], ['B3'], inc=(h == 7))
                em.tt('dve', Hs, Hs, B[3][0:64, :], ALU.add, ['r_H', 'B3'], ['r_H'])
                em.tt('dve', Hs.rearrange("p (h v) -> p h v", v=64), Hs.rearrange("p (h v) -> p h v", v=64),
                      gC.unsqueeze(2).to_broadcast([64, 8, 64]), ALU.mult, ['r_H', kgC], ['r_H'])
                em.cp('act', Hb, Hs, ['r_H', 'B2'], ['r_Hb'])
                if d == 0:
                    if emit_out:
                        em.cp('act', ot[:, 0:512], B[2], ['B2'], ['r_o'])
                        em.cp('dve', ot[:, 512:520], coef, [kcoef], ['r_ocoef'])
                        em.dma('pool', OF[s][t0:t0 + 128, :], ot, reads=['r_o', 'r_ocoef'], writes=[f"of{s}"])
                elif emit_out:
                    em.dma('sp', oft, OF[s][t0:t0 + 128, :], reads=[f"of{s}"], writes=['r_of'])
                    em.tt('dve', ot[:, 0:512], B[2], oft[:, 0:512], ALU.add, ['B2', 'r_of'], ['r_o'])
                    o3 = ot[:, 0:512].rearrange("p (h v) -> p h v", v=64)
                    em.op('dve', lambda e: e.tensor_reduce(out=gn[:, 0:8], in_=o3, axis=AX.X, op=ALU.add), ['r_o'], ['r_gn'])
                    em.act(osq, ot[:, 0:512], AF.Square, ['r_o'], ['r_osq'])
                    em.op('dve', lambda e: e.tensor_reduce(out=gn[:, 8:16], in_=osq.rearrange("p (h v) -> p h v", v=64),
                                                           axis=AX.X, op=ALU.add), ['r_osq', 'r_gn'], ['r_gn'])
                    em.ts('dve', gn[:, 16:24], gn[:, 0:8], 1.0 / 64, None, ALU.mult, None, ['r_gn'], ['r_gn'])
                    em.tt('dve', gn[:, 0:8], gn[:, 16:24], gn[:, 16:24], ALU.mult, ['r_gn'], ['r_gn'])
                    em.stt(gn[:, 24:32], gn[:, 8:16], 1.0 / 64, gn[:, 0:8], ALU.mult, ALU.subtract, ['r_gn'], ['r_gn'])
                    em.act(gn[:, 24:32], gn[:, 24:32], AF.Sqrt, ['r_gn'], ['r_gn'], bias=R_LN_EPS, scale=1.0)
                    em.op('dve', lambda e: e.reciprocal(out=gn[:, 24:32], in_=gn[:, 24:32]), ['r_gn'], ['r_gn'])
                    em.tt('dve', o3, o3, gn[:, 16:24].unsqueeze(2).to_broadcast([128, 8, 64]), ALU.subtract, ['r_o', 'r_gn'], ['r_o'])
                    em.tt('dve', o3, o3, gn[:, 24:32].unsqueeze(2).to_broadcast([128, 8, 64]), ALU.mult, ['r_o', 'r_gn'], ['r_o'])
                    em.tt('dve', ot[:, 0:512], ot[:, 0:512], lnw, ALU.mult, ['r_o', 'ROWB'], ['r_o'])
                    em.tt('dve', ot[:, 0:512], ot[:, 0:512], lnb, ALU.add, ['r_o', 'ROWB'], ['r_o'])
                    em.tt('dve', gn[:, 32:40], coef, oft[:, 512:520], ALU.add, [kcoef, 'r_of', 'r_gn'], ['r_gn'])
                    em.tt('dve', osq.rearrange("p (h v) -> p h v", v=64), vtokb.rearrange("p (h v) -> p h v", v=64),
                          gn[:, 32:40].unsqueeze(2).to_broadcast([128, 8, 64]), ALU.mult, [kvtokb, 'r_gn', 'r_osq'], ['r_osq'])
                    em.tt('dve', ot[:, 0:512], ot[:, 0:512], osq, ALU.add, ['r_o', 'r_osq'], ['r_o'])
                    em.dma('sp', xg0[:, 0:128], XG[s][0:128, t0:t0 + 128], reads=[f"xg{s}"], writes=['r_xg0'])
                    em.dma('sp', xg1[:, 0:128], XG[s][128:160, t0:t0 + 128], reads=[f"xg{s}"], writes=['r_xg1'])
                    em.act(sg0, xg0[:, 0:128], AF.Sigmoid, ['r_xg0'], ['r_sg0'])
                    em.act(sg1, xg1[:, 0:128], AF.Sigmoid, ['r_xg1'], ['r_sg1'])
                    em.mm(B[3], sg0, g2b0, True, False, ['r_sg0', 'g2b'], ['B3'])
                    em.mm(B[3], sg1, g2b1, False, True, ['r_sg1', 'g2b'], ['B3'])
                    em.tt('dve', ot[:, 0:512], ot[:, 0:512], B[3], ALU.mult, ['r_o', 'B3'], ['r_o'])
                    for j in range(4):
                        em.tr(B[2][:, j * 128:(j + 1) * 128], ot[:, j * 128:(j + 1) * 128], ident, ['r_o', 'CST'], ['B2'], inc=(j == 3))
                    em.cp('act', mixo[:, 4:8, :], B[2].rearrange("p (j t) -> p j t", t=128), ['B2'], ['mixo_r'])
                    em.dma('pool', fm(MIX[s])[:, 4:8, t0:t0 + 128], mixo[:, 4:8, :], reads=['mixo_r'], writes=[f"mixr{s}"])

            mc_ = mixcfg or {}
            inter = mc_.get('interleave', True)
            for b in range(mc_.get('nb', NBL)):
                nw = 0
                for stream, fnc in (('m', mamba_chunk), ('r', wkv_chunk)):
                    if not mc_.get('mamba' if stream == 'm' else 'wkv', True):
                        continue
                    for d in range(mc_.get('nd', 2)):
                        first = True
                        if stream == 'm':
                            em.stream = 'm' if inter else None
                            em.memset('pool', hst, 0.0, ['m_hst'])
                            em.memset('pool', hstb, 0.0, ['m_hstb'])
                        for kind in range(mc_.get('nkind', 2)):
                            s = b * 2 + kind
                            T = seqT(s)
                            nch = T // CH
                            order = range(nch) if d == 0 else range(nch - 1, -1, -1)
                            emit = (kind == 1) or need_ctx_out or mc_.get('ctxout', False)
                            for c in order:
                                if stream == 'm':
                                    fnc(s, c * CH, d, emit)
                                else:
                                    fnc(s, c * CH, d, emit, idx=nw, first_of_d=first, split=inter)
                                    nw += 1
                                    first = False
                em.stream = None
                if inter:
                    em.flush_mixer(nw)
            em.barrier()

    def stage_proj_post(l, phase, wap, Kc, SRC, srcname, gidx, seqs):
        with ExitStack() as es:
            def S(name, shape, dt=F32):
                return es.enter_context(nc.sbuf_tensor(name, list(shape), dt)).ap()
            nm = f"pp{phase}"
            wb = load_wbf(es, l, wap, Kc, D, nm + "w")
            a = S(f"{nm}a{l}", [128, Kc, 512], BF16)
            xt = S(f"{nm}x{l}", [128, 8, 512])
            y = S(f"{nm}y{l}", [128, 8, 512])
            sq = S(f"{nm}sq{l}", [128, 8, 512], BF16)
            rstd = S(f"{nm}rs{l}", [128, 512])
            pss = [es.enter_context(nc.psum_tensor(f"{nm}ps{l}_{i}", [128, 512], F32)).ap() for i in range(5)]
            for s in seqs:
                T = seqT(s)
                TW = min(512, T)
                jmod = 2 if s % 2 == 0 else s // 2
                rsrc, rsk = res_src(l, s, phase)
                rdst, rdk = res_dst(l, s, phase)
                for tt_ in range(T // TW):
                    t0 = tt_ * TW
                    em.dma('sp', a[:, :, :TW], fm(SRC[s])[:, :, t0:t0 + TW], reads=([f"mixm{s}", f"mixr{s}"] if srcname == 'mix' else [f"{srcname}{s}"]), writes=[nm + 'a'])
                    em.dma('pool', xt[:, :, :TW], fm(rsrc)[:, :, t0:t0 + TW], reads=[rsk], writes=[nm + 'x'])
                    for m in range(8):
                        ps = pss[m % 4]
                        pk = f"ps{m % 4}"
                        for k in range(Kc):
                            em.mm(ps[:, :TW], wb[:, k, m * 128:(m + 1) * 128], a[:, k, :TW], k == 0, k == Kc - 1,
                                  [nm + 'wbf', nm + 'a'], [pk])
                        em.cp('dve', y[:, m, :TW], ps[:, :TW], [pk], [nm + 'y'])
                        em.act(sq[:, m, :TW], ps[:, :TW], AF.Square, [pk], [nm + 'sq'])
                    for m in range(8):
                        em.mm(pss[4][:, :TW], onesb, sq[:, m, :TW], m == 0, m == 7, ['CSTB', nm + 'sq'], ['ps4'])
                    em.act(rstd[:, :TW], pss[4][:, :TW], AF.Sqrt, ['ps4'], [nm + 'rs'], bias=EPS, scale=1.0 / D)
                    em.op('dve', lambda e: e.reciprocal(out=rstd[:, :TW], in_=rstd[:, :TW]), [nm + 'rs'], [nm + 'rs'])
                    for m in range(8):
                        em.stt(y[:, m, :TW], y[:, m, :TW], DER[:, gidx, m, jmod:jmod + 1], rstd[:, :TW], ALU.mult, ALU.mult,
                               [nm + 'y', nm + 'rs', 'DER'], [nm + 'y'])
                    em.tt('dve', xt[:, :, :TW], xt[:, :, :TW], y[:, :, :TW], ALU.add, [nm + 'x', nm + 'y'], [nm + 'x'])
                    em.dma('pool', fm(rdst)[:, :, t0:t0 + TW], xt[:, :, :TW], reads=[nm + 'x'], writes=[rdk])
            em.barrier()

    def stage_ffn_up(l, seqs):
        with ExitStack() as es:
            def S(name, shape, dt=F32):
                return es.enter_context(nc.sbuf_tensor(name, list(shape), dt)).ap()
            wb = load_wbf(es, l, f_w_up[l], 8, 2 * DFF, "wup")
            xt = S(f"fux{l}", [128, 8, 512])
            h = S(f"fuh{l}", [128, 8, 512], BF16)
            sq = S(f"fusq{l}", [128, 8, 512], BF16)
            rstd = S(f"furs{l}", [128, 512])
            stg = [S(f"fustg{l}_{i}", [128, 512]) for i in range(2)]
            stv = [S(f"fustv{l}_{i}", [128, 512], BF16) for i in range(2)]
            pss = [es.enter_context(nc.psum_tensor(f"fups{l}_{i}", [128, 512], F32)).ap() for i in range(8)]
            for s in seqs:
                T = seqT(s)
                TW = min(512, T)
                jmod = 2 if s % 2 == 0 else s // 2
                src, srck = res_src(l, s, 1)
                for tt_ in range(T // TW):
                    t0 = tt_ * TW
                    em.dma('sp', xt[:, :, :TW], fm(src)[:, :, t0:t0 + TW], reads=[srck], writes=['fux'])
                    prenorm(xt, TW, h, sq, rstd, pss[7], jmod, 2, 24, 'fux', 'fuh', 'ps7')
                    for j in range(NFF):
                        pg = pss[(2 * j) % 6]
                        pgk = f"ps{(2 * j) % 6}"
                        pv_ = pss[(2 * j + 1) % 6]
                        pvk = f"ps{(2 * j + 1) % 6}"
                        for k in range(8):
                            em.mm(pg[:, :TW], wb[:, k, j * 128:(j + 1) * 128], h[:, k, :TW], k == 0, k == 7, ['wupbf', 'fuh'], [pgk])
                        for k in range(8):
                            em.mm(pv_[:, :TW], wb[:, k, DFF + j * 128:DFF + (j + 1) * 128], h[:, k, :TW], k == 0, k == 7,
                                  ['wupbf', 'fuh'], [pvk])
                        sg_ = stg[j % 2]
                        sv_ = stv[j % 2]
                        em.cp('dve', sg_[:, :TW], pg[:, :TW], [pgk], [f"fustg{j % 2}"])
                        em.cp('act', sv_[:, :TW], pv_[:, :TW], [pvk], [f"fustv{j % 2}"])
                        em.dma('pool', GATE[s][j * 128:(j + 1) * 128, t0:t0 + TW], sg_[:, :TW], reads=[f"fustg{j % 2}"],
                               writes=[f"gate{s}"])
                        em.dma('sp', VAL[s][j * 128:(j + 1) * 128, t0:t0 + TW], sv_[:, :TW], reads=[f"fustv{j % 2}"],
                               writes=[f"val{s}"])
            em.barrier()

    def stage_ffn_conv(l, seqs):
        with ExitStack() as es:
            def S(name, shape, dt=F32):
                return es.enter_context(nc.sbuf_tensor(name, list(shape), dt)).ap()
            gflat = [S(f"fcg{l}_{i}", [128, 2048]) for i in range(2)]
            vflat = [S(f"fcv{l}_{i}", [128, 2048], BF16) for i in range(2)]
            gpx = [S(f"fcgpx{l}_{i}", [128, 34, 66], BF16) for i in range(2)]
            gpc = [S(f"fcgpc{l}_{i}", [128, 3, 258], BF16) for i in range(2)]
            dg = [S(f"fcdg{l}_{i}", [128, 9, 128], BF16) for i in range(2)]
            acc = S(f"fcacc{l}", [128, 2048])
            u = S(f"fcu{l}", [128, 2048])
            ab = [S(f"fcab{l}_{i}", [128, 2048], BF16) for i in range(2)]
            pss = [es.enter_context(nc.psum_tensor(f"fcps{l}_{i}", [128, 512], F32)).ap() for i in range(4)]
            for i in range(2):
                em.memset('pool', gpx[i], 0.0, [f"fcgpx{i}"])
                em.memset('pool', gpc[i], 0.0, [f"fcgpc{i}"])
            it = 0
            pi = 0
            for s in seqs:
                T = seqT(s)
                for j in range(NFF):
                    i2 = it % 2
                    it += 1
                    if s % 2 == 1:
                        R, Cc, gp, gpk = 32, 64, gpx[i2], f"fcgpx{i2}"
                    else:
                        R, Cc, gp, gpk = 1, 256, gpc[i2], f"fcgpc{i2}"
                    gf = gflat[i2]
                    vf = vflat[i2]
                    em.dma('sp', gf[:, :T], GATE[s][j * 128:(j + 1) * 128, :], reads=[f"gate{s}"], writes=[f"fcg{i2}"])
                    em.dma('sp', vf[:, :T], VAL[s][j * 128:(j + 1) * 128, :], reads=[f"val{s}"], writes=[f"fcv{i2}"])
                    em.cp('act', gp[:, 1:1 + R, 1:1 + Cc], gf[:, :T].rearrange("p (r c) -> p r c", c=Cc), [f"fcg{i2}"], [gpk])
                    taps = list(range(9)) if s % 2 == 1 else [3, 4, 5]
                    for tap in taps:
                        em.ts('dve', dg[i2][:, tap, :], identb, PV[:, PV_FCW + tap * NFF + j:PV_FCW + tap * NFF + j + 1], None,
                              ALU.mult, None, ['CSTB', 'PV'], [f"fcdg{i2}"])
                    nblk = T // 512 if s % 2 == 1 else 1
                    for blk in range(nblk):
                        ps = pss[pi % 4]
                        pk = f"ps{pi % 4}"
                        pi += 1
                        for ti, tap in enumerate(taps):
                            dr, dc = tap // 3 - 1, tap % 3 - 1
                            if s % 2 == 1:
                                rhs = gp[:, 1 + dr + blk * 8:1 + dr + blk * 8 + 8, 1 + dc:1 + dc + 64]
                                out = ps.rearrange("p (r c) -> p r c", c=64)
                                wdt = 512
                            else:
                                rhs = gp[:, 1, 1 + dc:1 + dc + 256]
                                out = ps[:, 0:256]
                                wdt = 256
                            em.mm(out, dg[i2][:, tap, :], rhs, ti == 0, ti == len(taps) - 1, [f"fcdg{i2}", gpk], [pk])
                        em.act(acc[:, blk * 512:blk * 512 + wdt], ps[:, 0:wdt], AF.Identity, [pk, 'PV'], ['fcacc'],
                               bias=PV[:, PV_FCB + j:PV_FCB + j + 1], scale=1.0)
                    em.act(u[:, :T], acc[:, :T], AF.Square, ['fcacc'], ['fcu'])
                    em.ts('dve', u[:, :T], u[:, :T], 0.044715, 1.0, ALU.mult, ALU.add, ['fcu'], ['fcu'])
                    em.tt('dve', u[:, :T], u[:, :T], acc[:, :T], ALU.mult, ['fcu', 'fcacc'], ['fcu'])
                    em.act(u[:, :T], u[:, :T], AF.Sigmoid, ['fcu'], ['fcu'], scale=GELU_C)
                    em.tt('dve', u[:, :T], u[:, :T], acc[:, :T], ALU.mult, ['fcu', 'fcacc'], ['fcu'])
                    em.tt('dve', ab[i2][:, :T], u[:, :T], vf[:, :T], ALU.mult, ['fcu', f"fcv{i2}"], [f"fcab{i2}"])
                    em.dma('pool', ACTV[s][j * 128:(j + 1) * 128, :], ab[i2][:, :T], reads=[f"fcab{i2}"], writes=[f"actv{s}"])
            em.barrier()

    allseq = list(range(NS))
    xseq = [s for s in range(NS) if s % 2 == 1]
    outkeys = []
    for l in range(n_layers):
        last = (l == n_layers - 1)
        stage_mod(l)
        if stop_after == 'mod':
            break
        stage_inproj(l)
        if stop_after == 'inproj':
            break
        stage_mixer(l, need_ctx_out=not last)
        if stop_after == 'mixer':
            break
        seqs = xseq if last else allseq
        stage_proj_post(l, 0, w_out[l], 8, MIX, "mix", 1, seqs)
        if stop_after == 'outproj':
            break
        stage_ffn_up(l, seqs)
        stage_ffn_conv(l, seqs)
        stage_proj_post(l, 1, f_w_down[l], NFF, ACTV, "actv", 3, seqs)
    em.barrier()
    return nc, em


def host_prep(inp):
    f = np.float32
    idx = np.arange(128)
    cstn = np.zeros((128, NCST), f)
    cstn[:, C_ID:C_ID + 128] = np.eye(128)
    cstn[:, C_UTI:C_UTI + 128] = (idx[:, None] <= idx[None, :])
    cstn[:, C_LTI:C_LTI + 128] = (idx[:, None] >= idx[None, :])
    cstn[:, C_UTS:C_UTS + 128] = (idx[:, None] < idx[None, :])
    cstn[:, C_LTS:C_LTS + 128] = (idx[:, None] > idx[None, :])
    cstn[:, C_ONE:C_ONE + 128] = 1.0
    b32 = idx // 32
    b64 = idx // 64
    same32 = b32[:, None] == b32[None, :]
    same64 = b64[:, None] == b64[None, :]
    for d in range(2):
        strict = (idx[:, None] > idx[None, :]) if d == 0 else (idx[:, None] < idx[None, :])
        cstn[:, C_MP0 + d * 128:C_MP0 + (d + 1) * 128] = strict & same32
        cstn[:, C_ME1 + d * 128:C_ME1 + (d + 1) * 128] = strict & same64 & (~same32)
        cstn[:, C_ME2 + d * 128:C_ME2 + (d + 1) * 128] = strict & (~same64)
    pvn = np.zeros((L, 128, NPV), f)
    pv6 = np.zeros((L, 64, NPV64), f)
    rwn = np.zeros((L, 1, NROW), f)
    for l in range(L):
        pvn[l, :, PV_BMOD:PV_BMOD + 48] = inp['b_mod'][l].reshape(48, 128).T
        pvn[l, :, PV_GPRE1:PV_GPRE1 + 8] = inp['g_mix_pre'][l].reshape(8, 128).T
        pvn[l, :, PV_GPOST1:PV_GPOST1 + 8] = inp['g_mix_post'][l].reshape(8, 128).T
        pvn[l, :, PV_GPRE2:PV_GPRE2 + 8] = inp['g_ffn_pre'][l].reshape(8, 128).T
        pvn[l, :, PV_GPOST2:PV_GPOST2 + 8] = inp['g_ffn_post'][l].reshape(8, 128).T
        pvn[l, :, PV_MCW:PV_MCW + 24] = inp['m_conv_w'][l].reshape(3, 8, 128).transpose(2, 0, 1).reshape(128, 24)
        pvn[l, :, PV_MCB:PV_MCB + 8] = inp['m_conv_b'][l].reshape(8, 128).T
        pvn[l, :, PV_FCW:PV_FCW + 198] = inp['f_conv_w'][l].reshape(9, NFF, 128).transpose(2, 0, 1).reshape(128, 198)
        pvn[l, :, PV_FCB:PV_FCB + NFF] = inp['f_conv_b'][l].reshape(NFF, 128).T
        mu = inp['r_mu'][l]
        pvn[l, :, PV_MUXG0] = mu[1792:1920]
        pvn[l, 0:32, PV_MUXG1] = mu[1920:1952]
        pv6[l, :, P6_MURKV:P6_MURKV + 24] = mu[0:1536].reshape(24, 64).T
        pv6[l, :, P6_MUWA:P6_MUWA + 4] = mu[1536:1792].reshape(4, 64).T
        pv6[l, :, P6_W0:P6_W0 + 16] = inp['r_w0'][l].reshape(2, 8, 64).transpose(2, 0, 1).reshape(64, 16)
        pv6[l, :, P6_A0:P6_A0 + 16] = inp['r_a0'][l].reshape(2, 8, 64).transpose(2, 0, 1).reshape(64, 16)
        pv6[l, :, P6_KK:P6_KK + 8] = inp['r_k_k'][l].reshape(8, 64).T
        pv6[l, :, P6_KA:P6_KA + 8] = inp['r_k_a'][l].reshape(8, 64).T
        pv6[l, :, P6_RK:P6_RK + 8] = inp['r_r_k'][l].T
        rwn[l, 0, RV_MNW:RV_MNW + 512] = inp['m_norm_w'][l]
        rwn[l, 0, RV_LNW:RV_LNW + 512] = inp['r_ln_w'][l]
        rwn[l, 0, RV_LNB:RV_LNB + 512] = inp['r_ln_b'][l]
        rwn[l, 0, RV_MD:RV_MD + 8] = inp['m_d'][l]
        rwn[l, 0, RV_DTB:RV_DTB + 16] = inp['m_dt_bias'][l].reshape(16)
        rwn[l, 0, RV_ALOG:RV_ALOG + 16] = inp['m_a_log'][l].reshape(16)
    return cstn, pvn, pv6, rwn


def make_in_maps(inp, cores):
    cstn, pvn, pv6, rwn = host_prep(inp)
    shared = {k: np.ascontiguousarray(np.asarray(inp[k], dtype=np.float32)) for k in
              ['w_mod', 'w_in', 'w_out', 'r_w2', 'r_a2', 'r_g2', 'f_w_up', 'f_w_down']}
    maps = []
    x = np.asarray(inp['x'], np.float32)
    ctx = np.asarray(inp['ctx'], np.float32)
    c = np.asarray(inp['c'], np.float32)
    cc = np.asarray(inp['c_ctx'], np.float32)
    for ci in cores:
        bs = [ci * NBL + i for i in range(NBL)]
        m = dict(shared)
        m['xT'] = np.ascontiguousarray(x[bs].transpose(0, 2, 1))
        m['ctxT'] = np.ascontiguousarray(ctx[bs].transpose(0, 2, 1))
        m['cT'] = np.ascontiguousarray(np.stack([c[bs[0]], c[bs[1]], cc], axis=1))
        m['cst'] = cstn
        m['pv'] = pvn
        m['pv64'] = pv6
        m['rowv'] = rwn
        m['rmu'] = np.ascontiguousarray(np.asarray(inp['r_mu'], np.float32).reshape(L, 1, 1952))
        maps.append(m)
    return maps


def kernel(**inputs):
    nc, em = build()
    cores = list(range(NCORE))
    maps = make_in_maps(inputs, cores)
    res = run_bass_kernel_spmd(nc, maps, core_ids=cores)
    out = np.empty((NCORE * NBL, TX, D), np.float32)
    for ci in cores:
        o = res.results[ci]["outT"]
        out[ci * NBL:(ci + 1) * NBL] = o.transpose(0, 2, 1)
    return out
```
